# Optimizing a Trainium2 kernel written in Bass

```python
import math
import jax, jax.numpy as jnp
from jax import lax
import numpy as np

D_MODEL = 1024
BATCH = 4
SEQ = 4096
DEPTH = 2

GRID_W = 64
CTX_LEN = 256
N_EVEN = (DEPTH + 1) // 2
N_ODD = DEPTH // 2
N_MOD = 6

N_Q_HEADS = 8
N_KV_HEADS = 2
Q_PER_KV = N_Q_HEADS // N_KV_HEADS
HEAD_DIM = 64
ATTN_WIDTH = N_Q_HEADS * HEAD_DIM
KV_WIDTH = N_KV_HEADS * HEAD_DIM
FOURIER_GROUPS = 8
FOURIER_GROUP_W = 64
FOURIER_WIDTH = FOURIER_GROUPS * FOURIER_GROUP_W
KV_START = FOURIER_WIDTH + ATTN_WIDTH
MIX_IN_WIDTH = KV_START + 2 * KV_WIDTH
MIX_OUT_WIDTH = FOURIER_WIDTH + ATTN_WIDTH
WINDOW = 128
BLOCK = 128
ATTN_SCALE = HEAD_DIM ** -0.5
ROPE_BASE = 10000.0
NEG_INF = -1e30

SSM_GROUP_W = 16
SSM_GROUPS = D_MODEL // SSM_GROUP_W
SSM_STATE = 64
DT_MIN = 0.001
DT_MAX = 0.1

FFN_MULT = 256
FFN_HIDDEN = -(-8 * D_MODEL // (3 * FFN_MULT)) * FFN_MULT
RMS_EPS = 1e-6

kernel_name = 'hybrid_fourier_swa_s5_dit_trunk'


def rms_norm(x, g):
    x32 = x.astype(jnp.float32)
    y = x32 * lax.rsqrt(jnp.mean(x32 * x32, axis=-1, keepdims=True) + RMS_EPS)
    return (y * g.astype(jnp.float32)).astype(x.dtype)


def adaln(cond, w, b):
    return jnp.split(jax.nn.silu(cond) @ w + b, N_MOD, axis=-1)


def swiglu(h, w_gate, w_up, w_down):
    return (jax.nn.silu(h @ w_gate) * (h @ w_up)) @ w_down


def axial_rope(rows):
    t_row = jnp.repeat(jnp.arange(rows, dtype=jnp.float32), GRID_W)
    t_col = jnp.tile(jnp.arange(GRID_W, dtype=jnp.float32), rows)
    n_freq = HEAD_DIM // 4
    inv_freq = ROPE_BASE ** (-jnp.arange(n_freq, dtype=jnp.float32) / n_freq)
    ang = jnp.concatenate([t_row[:, None] * inv_freq, t_col[:, None] * inv_freq], axis=-1)
    return jnp.cos(ang), jnp.sin(ang)


def apply_rope(x, cos, sin):
    x32 = x.astype(jnp.float32)
    x1, x2 = x32[..., 0::2], x32[..., 1::2]
    cs, sn = cos[None, :, None, :], sin[None, :, None, :]
    out = jnp.stack([x1 * cs - x2 * sn, x1 * sn + x2 * cs], axis=-1).reshape(x.shape)
    return out.astype(x.dtype)


def fourier_mix(f):
    bn, t, _ = f.shape
    g = f.astype(jnp.float32).reshape(bn, t, FOURIER_GROUPS, FOURIER_GROUP_W)
    y = jnp.fft.fft2(g, axes=(1, 3), norm='ortho').real
    return y.reshape(bn, t, FOURIER_WIDTH).astype(f.dtype)


def sink_column(sink, lead_shape):
    s = sink.astype(jnp.float32).reshape(N_KV_HEADS, Q_PER_KV, 1, 1)
    return jnp.broadcast_to(s, lead_shape + (1,))


def window_attention(q, k, v, kc, vc, sink):
    bn, n_tok = q.shape[:2]
    n_ctx = kc.shape[1]
    nb = n_tok // BLOCK
    qb = q.reshape(bn, nb, BLOCK, N_KV_HEADS, Q_PER_KV, HEAD_DIM)

    def bands(t):
        tp = jnp.pad(t, ((0, 0), (BLOCK, BLOCK), (0, 0), (0, 0)))
        tp = tp.reshape(bn, nb + 2, BLOCK, N_KV_HEADS, HEAD_DIM)
        return jnp.concatenate([tp[:, :-2], tp[:, 1:-1], tp[:, 2:]], axis=2)

    kw, vw = bands(k), bands(v)
    s_win = jnp.einsum('bnqhgd,bnkhd->bnhgqk', qb, kw).astype(jnp.float32) * ATTN_SCALE
    s_ctx = jnp.einsum('bnqhgd,bchd->bnhgqc', qb, kc).astype(jnp.float32) * ATTN_SCALE
    qi = jnp.arange(BLOCK)[:, None]
    kj = jnp.arange(3 * BLOCK)[None, :]
    in_win = jnp.abs(kj - BLOCK - qi) <= WINDOW
    kpos = jnp.arange(nb)[:, None] * BLOCK + kj - BLOCK
    valid = in_win[None] & ((kpos >= 0) & (kpos < n_tok))[:, None, :]
    s_win = jnp.where(valid[None, :, None, None], s_win, NEG_INF)
    logits = jnp.concatenate([sink_column(sink, s_ctx.shape[:-1]), s_ctx, s_win], axis=-1)
    p = jax.nn.softmax(logits, axis=-1)
    p_ctx = p[..., 1:1 + n_ctx].astype(v.dtype)
    p_win = p[..., 1 + n_ctx:].astype(v.dtype)
    o = (jnp.einsum('bnhgqc,bchd->bnqhgd', p_ctx, vc)
         + jnp.einsum('bnhgqk,bnkhd->bnqhgd', p_win, vw))
    return o.reshape(bn, n_tok, ATTN_WIDTH)


def context_attention(qc, kc, vc, sink):
    bn, n_ctx = qc.shape[:2]
    q = qc.reshape(bn, n_ctx, N_KV_HEADS, Q_PER_KV, HEAD_DIM)
    s = jnp.einsum('bqhgd,bkhd->bhgqk', q, kc).astype(jnp.float32) * ATTN_SCALE
    p = jax.nn.softmax(jnp.concatenate([sink_column(sink, s.shape[:-1]), s], axis=-1), axis=-1)
    o = jnp.einsum('bhgqk,bkhd->bqhgd', p[..., 1:].astype(vc.dtype), vc)
    return o.reshape(bn, n_ctx, ATTN_WIDTH)


def fourier_attention_mix(h_lat, h_ctx, w_in, w_out, sink, cos, sin, with_ctx_out):
    bn, n_tok, _ = h_lat.shape
    n_ctx = h_ctx.shape[1]
    p = h_lat @ w_in
    f = p[..., :FOURIER_WIDTH]
    q = p[..., FOURIER_WIDTH:KV_START].reshape(bn, n_tok, N_Q_HEADS, HEAD_DIM)
    k = p[..., KV_START:KV_START + KV_WIDTH].reshape(bn, n_tok, N_KV_HEADS, HEAD_DIM)
    v = p[..., KV_START + KV_WIDTH:].reshape(bn, n_tok, N_KV_HEADS, HEAD_DIM)
    kvc = h_ctx @ w_in[:, KV_START:]
    kc = kvc[..., :KV_WIDTH].reshape(bn, n_ctx, N_KV_HEADS, HEAD_DIM)
    vc = kvc[..., KV_WIDTH:].reshape(bn, n_ctx, N_KV_HEADS, HEAD_DIM)
    q, k = apply_rope(q, cos, sin), apply_rope(k, cos, sin)
    o_lat = jnp.concatenate([fourier_mix(f), window_attention(q, k, v, kc, vc, sink)], axis=-1) @ w_out
    o_ctx = None
    if with_ctx_out:
        pc = h_ctx @ w_in[:, :KV_START]
        qc = pc[..., FOURIER_WIDTH:].reshape(bn, n_ctx, N_Q_HEADS, HEAD_DIM)
        o_ctx = jnp.concatenate([fourier_mix(pc[..., :FOURIER_WIDTH]),
                                 context_attention(qc, kc, vc, sink)], axis=-1) @ w_out
    return o_lat, o_ctx


def s5_discretize(a_re, a_im, log_dt, b_re, b_im):
    a_re, a_im = a_re.astype(jnp.float32), a_im.astype(jnp.float32)
    b_re, b_im = b_re.astype(jnp.float32), b_im.astype(jnp.float32)
    dt = jnp.exp(log_dt.astype(jnp.float32))[:, None]
    mag = jnp.exp(a_re * dt)
    abar_re, abar_im = mag * jnp.cos(a_im * dt), mag * jnp.sin(a_im * dt)
    nr, ni = abar_re - 1.0, abar_im
    den = a_re * a_re + a_im * a_im
    f_re = (nr * a_re + ni * a_im) / den
    f_im = (ni * a_re - nr * a_im) / den
    bbar_re = f_re[..., None] * b_re - f_im[..., None] * b_im
    bbar_im = f_re[..., None] * b_im + f_im[..., None] * b_re
    return abar_re, abar_im, bbar_re, bbar_im


def complex_scan(abar_re, abar_im, bu_re, bu_im, h0):
    if h0 is not None:
        h0_re, h0_im = h0
        bu_re = bu_re.at[0].add(abar_re * h0_re - abar_im * h0_im)
        bu_im = bu_im.at[0].add(abar_re * h0_im + abar_im * h0_re)
    n_t = bu_re.shape[0]
    a_re = jnp.broadcast_to(abar_re, (n_t, 1) + abar_re.shape)
    a_im = jnp.broadcast_to(abar_im, (n_t, 1) + abar_im.shape)

    def combine(e1, e2):
        ar1, ai1, br1, bi1 = e1
        ar2, ai2, br2, bi2 = e2
        return (ar1 * ar2 - ai1 * ai2, ar1 * ai2 + ai1 * ar2,
                ar2 * br1 - ai2 * bi1 + br2, ar2 * bi1 + ai2 * br1 + bi2)

    _, _, h_re, h_im = lax.associative_scan(combine, (a_re, a_im, bu_re, bu_im), axis=0)
    return h_re, h_im


def s5_drive(bbar_re, bbar_im, u):
    return (jnp.einsum('gph,tbgh->tbgp', bbar_re, u), jnp.einsum('gph,tbgh->tbgp', bbar_im, u))


def s5_readout(c_re, c_im, h_re, h_im):
    return jnp.einsum('ghp,tbgp->tbgh', c_re, h_re) - jnp.einsum('ghp,tbgp->tbgh', c_im, h_im)


def s5_glu(y, like, glu_w):
    n_t, bn = y.shape[:2]
    y = y.reshape(n_t, bn, D_MODEL).transpose(1, 0, 2).astype(like.dtype)
    a, g = jnp.split(jax.nn.gelu(y) @ glu_w, 2, axis=-1)
    return a * jax.nn.sigmoid(g)


def s5_mix(h_lat, h_ctx, a_re, a_im, log_dt, b_re, b_im, c_re, c_im, d_skip, glu_w, with_ctx_out):
    bn = h_lat.shape[0]

    def time_major(t):
        return t.astype(jnp.float32).transpose(1, 0, 2).reshape(t.shape[1], bn, SSM_GROUPS, SSM_GROUP_W)

    u_lat, u_ctx = time_major(h_lat), time_major(h_ctx)
    d_g = d_skip.astype(jnp.float32).reshape(SSM_GROUPS, SSM_GROUP_W)
    y_lat = d_g * u_lat
    y_ctx = d_g * u_ctx if with_ctx_out else None
    for direction in range(2):
        abar_re, abar_im, bbar_re, bbar_im = s5_discretize(
            a_re[direction], a_im[direction], log_dt[direction], b_re[direction], b_im[direction])
        cr, ci = c_re[direction].astype(jnp.float32), c_im[direction].astype(jnp.float32)
        seq_ctx = u_ctx if direction == 0 else u_ctx[::-1]
        seq_lat = u_lat if direction == 0 else u_lat[::-1]
        hc_re, hc_im = complex_scan(abar_re, abar_im, *s5_drive(bbar_re, bbar_im, seq_ctx), None)
        hl_re, hl_im = complex_scan(abar_re, abar_im, *s5_drive(bbar_re, bbar_im, seq_lat),
                                    (hc_re[-1], hc_im[-1]))
        yl = s5_readout(cr, ci, hl_re, hl_im)
        y_lat = y_lat + (yl if direction == 0 else yl[::-1])
        if with_ctx_out:
            yc = s5_readout(cr, ci, hc_re, hc_im)
            y_ctx = y_ctx + (yc if direction == 0 else yc[::-1])
    o_lat = s5_glu(y_lat, h_lat, glu_w)
    o_ctx = s5_glu(y_ctx, h_ctx, glu_w) if with_ctx_out else None
    return o_lat, o_ctx


def setup_inputs(seed: int = 0) -> dict:
    key = jax.random.key(seed)
    ks = jax.random.split(key, 23)
    f32 = jnp.float32
    D, F, G, P, H = D_MODEL, FFN_HIDDEN, SSM_GROUPS, SSM_STATE, SSM_GROUP_W

    def nrm(k, shape, std):
        return std * jax.random.normal(k, shape, f32)

    return {
        'x': nrm(ks[0], (BATCH, SEQ, D), 1.0),
        'c': nrm(ks[1], (BATCH, D), 1.0),
        'ctx': nrm(ks[2], (BATCH, CTX_LEN, D), 1.0),
        'c_ctx': nrm(ks[3], (D,), 1.0),
        'mod_w': nrm(ks[4], (DEPTH, D, N_MOD * D), 0.5 * D ** -0.5),
        'mod_b': nrm(ks[5], (DEPTH, N_MOD * D), 0.02),
        'norm_g': 1.0 + nrm(ks[6], (DEPTH, 2, D), 0.02),
        'ffn_w_gate': nrm(ks[7], (DEPTH, D, F), D ** -0.5),
        'ffn_w_up': nrm(ks[8], (DEPTH, D, F), D ** -0.5),
        'ffn_w_down': nrm(ks[9], (DEPTH, F, D), F ** -0.5),
        'mix_w_in': nrm(ks[10], (N_EVEN, D, MIX_IN_WIDTH), D ** -0.5),
        'mix_w_out': nrm(ks[11], (N_EVEN, MIX_OUT_WIDTH, D), MIX_OUT_WIDTH ** -0.5),
        'attn_sink': nrm(ks[12], (N_EVEN, N_Q_HEADS), 0.5),
        'ssm_a_re': -0.5 + nrm(ks[13], (N_ODD, 2, G, P), 0.01),
        'ssm_a_im': jnp.pi * jnp.arange(P, dtype=f32) + nrm(ks[14], (N_ODD, 2, G, P), 0.01),
        'ssm_log_dt': jax.random.uniform(ks[15], (N_ODD, 2, G), f32, math.log(DT_MIN), math.log(DT_MAX)),
        'ssm_b_re': nrm(ks[16], (N_ODD, 2, G, P, H), (2 * H) ** -0.5),
        'ssm_b_im': nrm(ks[17], (N_ODD, 2, G, P, H), (2 * H) ** -0.5),
        'ssm_c_re': nrm(ks[18], (N_ODD, 2, G, H, P), 0.5),
        'ssm_c_im': nrm(ks[19], (N_ODD, 2, G, H, P), 0.5),
        'ssm_d': nrm(ks[20], (N_ODD, D), 1.0),
        'ssm_glu_w': nrm(ks[21], (N_ODD, D, 2 * D), D ** -0.5),
        'final_g': 1.0 + nrm(ks[22], (D,), 0.02),
    }


def reference(x, c, ctx, c_ctx, mod_w, mod_b, norm_g, ffn_w_gate, ffn_w_up, ffn_w_down,
              mix_w_in, mix_w_out, attn_sink, ssm_a_re, ssm_a_im, ssm_log_dt, ssm_b_re, ssm_b_im,
              ssm_c_re, ssm_c_im, ssm_d, ssm_glu_w, final_g):
    rows = x.shape[1] // GRID_W
    cos, sin = axial_rope(rows)
    h, hc = x, ctx
    for layer in range(DEPTH):
        with_ctx_out = layer < DEPTH - 1
        i = layer // 2
        sh1, sc1, g1, sh2, sc2, g2 = [m[:, None, :] for m in adaln(c, mod_w[layer], mod_b[layer])]
        csh1, csc1, cg1, csh2, csc2, cg2 = adaln(c_ctx, mod_w[layer], mod_b[layer])
        n_lat = rms_norm(h, norm_g[layer, 0]) * (1.0 + sc1) + sh1
        n_ctx = rms_norm(hc, norm_g[layer, 0]) * (1.0 + csc1) + csh1
        if layer % 2 == 0:
            o_lat, o_ctx = fourier_attention_mix(n_lat, n_ctx, mix_w_in[i], mix_w_out[i], attn_sink[i],
                                                 cos, sin, with_ctx_out)
        else:
            o_lat, o_ctx = s5_mix(n_lat, n_ctx, ssm_a_re[i], ssm_a_im[i], ssm_log_dt[i], ssm_b_re[i],
                                  ssm_b_im[i], ssm_c_re[i], ssm_c_im[i], ssm_d[i], ssm_glu_w[i],
                                  with_ctx_out)
        h = h + g1 * o_lat
        h = h + g2 * swiglu(rms_norm(h, norm_g[layer, 1]) * (1.0 + sc2) + sh2,
                            ffn_w_gate[layer], ffn_w_up[layer], ffn_w_down[layer])
        if with_ctx_out:
            hc = hc + cg1 * o_ctx
            hc = hc + cg2 * swiglu(rms_norm(hc, norm_g[layer, 1]) * (1.0 + csc2) + csh2,
                                   ffn_w_gate[layer], ffn_w_up[layer], ffn_w_down[layer])
    return rms_norm(h, final_g)
```

```python
import math
import numpy as np
import ml_dtypes
import concourse.bass as bass
import concourse.mybir as mybir
from concourse.bass_utils import run_bass_kernel_spmd
from contextlib import ExitStack

F32 = mybir.dt.float32
BF16 = mybir.dt.bfloat16
AF = mybir.ActivationFunctionType
ALU = mybir.AluOpType
NPBF = ml_dtypes.bfloat16

D = 1024
T = 4096
TC = 256
FF = 2816
NFC = 22
EPS = 1e-6


class Buf:
    def __init__(self, name=""):
        self.name = name
        self.last_w = None
        self.readers = []


class Prog:
    ENG = ['pe', 'act', 'dve', 'pool', 'sp']

    def __init__(self, nc, ndma_sems=10):
        self.nc = nc
        self.ops = {e: [] for e in self.ENG}
        self.cnt = {e: 0 for e in self.ENG}
        self.known = {e: {} for e in self.ENG}
        self.es = ExitStack()
        self.sem = {e: self.es.enter_context(nc.semaphore('s_' + e)) for e in self.ENG}
        self.dma_sems = {e: [self.es.enter_context(nc.semaphore(f'd_{e}{i}')) for i in range(ndma_sems)]
                         for e in ['sp', 'act', 'pool']}
        self.dma_val = {e: [0] * ndma_sems for e in self.dma_sems}
        self.dma_rr = {e: 0 for e in self.dma_sems}
        self.semobj = {}
        for e in self.ENG:
            self.semobj[('c', e)] = self.sem[e]
        for e in self.dma_sems:
            for i, s in enumerate(self.dma_sems[e]):
                self.semobj[('d', e, i)] = s
        self.q = 0
        self.rank_ap = None

    def _waits(self, eng, toks):
        need = {}
        for t in toks:
            if t is None:
                continue
            k, v = t
            if k == ('c', eng) and eng == 'pe':
                continue
            if self.known[eng].get(k, 0) >= v:
                continue
            if need.get(k, 0) < v:
                need[k] = v
        for k, v in need.items():
            self.known[eng][k] = v
        return list(need.items())

    def _deps(self, reads, writes):
        toks = []
        for b in reads:
            toks.append(b.last_w)
        for b in writes:
            toks.append(b.last_w)
            toks.extend(b.readers)
        return toks

    def _commit(self, tok, reads, writes):
        for b in reads:
            b.readers.append(tok)
            if len(b.readers) > 64:
                b.readers = b.readers[-64:]
        for b in writes:
            b.last_w = tok
            b.readers = []

    def op(self, eng, fn, reads=(), writes=()):
        waits = self._waits(eng, self._deps(reads, writes))
        self.cnt[eng] += 1
        tok = (('c', eng), self.cnt[eng])
        self.ops[eng].append((waits, fn, (self.sem[eng], 1)))
        self._commit(tok, reads, writes)
        return tok

    def dma(self, out, in_, reads=(), writes=(), eng=None, **kw):
        if eng is None:
            eng = ['sp', 'act', 'pool'][self.q % 2]
            self.q += 1
        toks = self._deps(reads, writes)
        i = self.dma_rr[eng]
        self.dma_rr[eng] = (i + 1) % len(self.dma_sems[eng])
        key = ('d', eng, i)
        prev = self.dma_val[eng][i]
        if prev > 0:
            toks.append((key, prev))
        waits = self._waits(eng, toks)
        self.dma_val[eng][i] = prev + 16
        tok = (key, prev + 16)
        def issue(e, out=out, in_=in_, eng=eng):
            o = out(self.dyn[eng]) if callable(out) else out
            i2 = in_(self.dyn[eng]) if callable(in_) else in_
            return e.dma_start(out=o, in_=i2, **kw)
        self.ops[eng].append((waits, issue, (self.dma_sems[eng][i], 16)))
        self._commit(tok, reads, writes)
        return tok

    def all_tokens(self):
        allt = []
        for e in self.ENG:
            if self.cnt[e]:
                allt.append((('c', e), self.cnt[e]))
        for e in self.dma_sems:
            for i, v in enumerate(self.dma_val[e]):
                if v:
                    allt.append((('d', e, i), v))
        return allt

    def barrier(self):
        allt = self.all_tokens()
        for e in self.ENG:
            w = self._waits(e, allt)
            if w:
                self.ops[e].append((w, None, None))

    def emit(self):
        nc = self.nc
        fin = self._waits('sp', self.all_tokens())
        self.ops['sp'].append((fin, None, None))
        self.dyn = {}
        with nc.Block() as block:
            def mk(eng):
                def run(e):
                    for waits, fn, inc in self.ops[eng]:
                        for k, v in waits:
                            e.wait_ge(self.semobj[k], v)
                        if fn is not None:
                            fn(e).then_inc(inc[0], inc[1])

                def body(e):
                    if eng in ('sp', 'act') and self.rank_ap is not None:
                        with e.register("rk_" + eng) as reg:
                            e.reg_load(reg, self.rank_ap)
                            self.dyn[eng] = e.snap(reg, min_val=0, max_val=2048)
                            run(e)
                    else:
                        run(e)
                return body
            block.tensor(mk('pe'))
            block.scalar(mk('act'))
            block.vector(mk('dve'))
            block.gpsimd(mk('pool'))
            block.sync(mk('sp'))
        self.es.close()


class TB:
    def __init__(self, t, name=""):
        self.t = t
        self.b = Buf(name)


def host_consts(rev=False):
    cs = {}
    cs["identf"] = np.eye(128, dtype=np.float32)
    cs["identb"] = np.eye(128, dtype=np.float32).astype(NPBF)
    t = np.arange(T)
    row = (t // 64).astype(np.float64)
    col = (t % 64).astype(np.float64)
    nf = 16
    inv = 10000.0 ** (-np.arange(nf, dtype=np.float64) / nf)
    inv = inv.astype(np.float32).astype(np.float64)
    ang = np.concatenate([(row[:, None].astype(np.float32) * inv[None].astype(np.float32)),
                          (col[:, None].astype(np.float32) * inv[None].astype(np.float32))], axis=-1).astype(np.float32)
    cosv = np.cos(ang).astype(np.float32)
    sinv = np.sin(ang).astype(np.float32)
    C = np.zeros((128, T), np.float32)
    S = np.zeros((128, T), np.float32)
    for p in range(128):
        d = p % 64
        i = d // 2
        C[p] = cosv[:, i]
        S[p] = sinv[:, i] * (-1.0 if d % 2 == 0 else 1.0)
    if rev:
        C = np.ascontiguousarray(C[:, ::-1])
        S = np.ascontiguousarray(S[:, ::-1])
    cs["ropeC"] = C
    cs["ropeS"] = S
    j = np.arange(128)[:, None]
    i = np.arange(128)[None, :]
    mp = np.where(j >= i, 0.0, -30000.0).astype(np.float32)
    mn = np.where(j <= i, 0.0, -30000.0).astype(np.float32)
    cs["maskP"] = np.tile(mp, (1, 4)).astype(NPBF)
    cs["maskN"] = np.tile(mn, (1, 4)).astype(NPBF)
    tt = np.arange(T, dtype=np.int64)
    tk = (tt[:, None] * tt[None, :]) % T
    angT = 2.0 * np.pi * tk / T
    ct = (np.cos(angT) / math.sqrt(T)).astype(np.float32)
    stt = (np.sin(angT) / math.sqrt(T)).astype(np.float32)
    if rev:
        ct = np.ascontiguousarray(ct[::-1, ::-1])
        stt = np.ascontiguousarray(stt[::-1, ::-1])
    cs["CT"] = np.ascontiguousarray(ct.reshape(32, 128, 16, 256).transpose(2, 1, 0, 3)).astype(NPBF)
    cs["ST"] = np.ascontiguousarray(stt.reshape(32, 128, 16, 256).transpose(2, 1, 0, 3)).astype(NPBF)
    t2 = np.arange(TC, dtype=np.int64)
    a2 = 2.0 * np.pi * ((t2[:, None] * t2[None, :]) % TC) / TC
    c2 = (np.cos(a2) / math.sqrt(TC)).astype(np.float32)
    s2 = (np.sin(a2) / math.sqrt(TC)).astype(np.float32)
    if rev:
        c2 = np.ascontiguousarray(c2[::-1, ::-1])
        s2 = np.ascontiguousarray(s2[::-1, ::-1])
    cs["C256"] = np.ascontiguousarray(c2.reshape(2, 128, 256).transpose(1, 0, 2)).astype(NPBF)
    cs["S256"] = np.ascontiguousarray(s2.reshape(2, 128, 256).transpose(1, 0, 2)).astype(NPBF)
    c64 = np.arange(64)
    a3 = 2.0 * np.pi * ((c64[:, None] * c64[None, :]) % 64) / 64
    cc = np.zeros((128, 128), np.float32)
    sc = np.zeros((128, 128), np.float32)
    for g in range(2):
        cc[g * 64:(g + 1) * 64, g * 64:(g + 1) * 64] = np.cos(a3) / 8.0
        sc[g * 64:(g + 1) * 64, g * 64:(g + 1) * 64] = np.sin(a3) / 8.0
    cs["Cc"] = cc.astype(NPBF)
    cs["Sc"] = sc.astype(NPBF)
    return cs


CONST_SHAPES = {
    "identf": ([128, 128], F32), "identb": ([128, 128], BF16), "ropeC": ([128, T], F32), "ropeS": ([128, T], F32),
    "maskP": ([128, 512], BF16), "maskN": ([128, 512], BF16),
    "CT": ([16, 128, 32, 256], BF16), "ST": ([16, 128, 32, 256], BF16),
    "C256": ([128, 2, 256], BF16), "S256": ([128, 2, 256], BF16), "Cc": ([128, 128], BF16), "Sc": ([128, 128], BF16),
}

IN_SHAPES = {
    "x": [T, D], "c": [8, 128], "ctx": [TC, D], "c_ctx": [8, 128],
    "mod_w": [2, D, 6 * D], "mod_b": [2, 48, 128], "norm_g": [2, 2, 8, 128],
    "ffn_w_gate": [2, D, FF], "ffn_w_up": [2, D, FF], "ffn_w_down": [2, FF, D],
    "mix_w_in": [D, 1280], "mix_w_out": [D, D], "attn_sink": [1, 8],
    "ssm_a_re": [128, 64], "ssm_a_im": [128, 64], "ssm_log_dt": [1, 128],
    "ssm_b_re": [2, 64, 64, 16], "ssm_b_im": [2, 64, 64, 16], "ssm_c_re": [2, 64, 16, 64], "ssm_c_im": [2, 64, 16, 64],
    "ssm_d": [8, 128], "ssm_glu_w": [D, 2 * D], "final_g": [1, D],
}


def build(stage=99):
    nc = bass.Bass("TRN2", target_bir_lowering=False)
    I = {n: nc.dram_tensor(n, sh, F32, kind="ExternalInput").ap() for n, sh in IN_SHAPES.items()}
    K = {n: nc.dram_tensor(n, sh, dt, kind="ExternalInput").ap() for n, (sh, dt) in CONST_SHAPES.items()}
    rk_in = nc.dram_tensor("rk", [1, 1], mybir.dt.int32, kind="ExternalInput").ap()
    HT = T // 2
    out = nc.dram_tensor("out", [T if stage < 99 else HT, D], F32, kind="ExternalOutput").ap()
    Hs = nc.dram_tensor("Hs", [T + TC, D], F32).ap()
    mod_b_flat = I["mod_b"].rearrange("l j p -> l (j p)")

    p = Prog(nc)
    p.rank_ap = None
    top = ExitStack()
    Hloc = nc.dram_tensor("Hloc", [T // 2, D], F32).ap()
    p.Hloc = Hloc

    uid = {"n": 0}

    def sbt(st, name, shape, dt):
        uid["n"] += 1
        return TB(st.enter_context(nc.sbuf_tensor(f"s{uid['n']}_{name}", shape, dt)), name)

    def pst(st, name, shape, dt=F32):
        uid["n"] += 1
        return TB(st.enter_context(nc.psum_tensor(f"p{uid['n']}_{name}", shape, dt)), name)

    identf = sbt(top, "identf", [128, 128], F32)
    identb = sbt(top, "identb", [128, 128], BF16)
    p.dma(identf.t[:], K["identf"], writes=[identf.b])
    p.dma(identb.t[:], K["identb"], writes=[identb.b])
    ones_b = sbt(top, "ones_b", [128, 128], BF16)
    p.op('pool', lambda e: e.memset(ones_b.t[:], 1.0), writes=[ones_b.b])
    AB = sbt(top, "AB", [128, 2, 2, 4, 8], F32)
    Gs = nc.dram_tensor("Gs", [8, 128, D], F32).ap()
    ATs = nc.dram_tensor("ATs", [34, 128, 4, 128], BF16).ap()

    def load_gate(st, nm, l, col, gi):
        g = sbt(st, nm, [128, D], F32)
        p.dma(g.t[:], Gs[(l * 2 + col) * 2 + gi], writes=[g.b])
        return g
    rr = {"i": 0}

    def ew(fn, reads, writes, engs=('dve', 'pool')):
        e = engs[rr["i"] % len(engs)]
        rr["i"] += 1
        return p.op(e, fn, reads=reads, writes=writes)

    with ExitStack() as st:
        gates = sbt(st, "gates", [128, 2, 2, 2, D], F32)
        crow = sbt(st, "crow", [16, 128], F32)
        p.dma(crow.t[0:8, :], I["c"], writes=[crow.b])
        p.dma(crow.t[8:16, :], I["c_ctx"], writes=[crow.b])
        pT = pst(st, "pT", [128, 96])
        scT = sbt(st, "scT", [128, 16], F32)
        p.op('pe', lambda e: e.transpose(pT.t[:, 0:16], crow.t[:, :], identf.t[0:16, 0:16]), reads=[crow.b, identf.b], writes=[pT.b])
        p.op('act', lambda e: e.activation(scT.t[:], pT.t[:, 0:16], AF.Silu), reads=[pT.b], writes=[scT.b])
        scbc = sbt(st, "scbc", [128, 16, 128], F32)
        for ck in range(16):
            ew(lambda e, ck=ck: e.tensor_copy(scbc.t[:, ck, :], scT.t[:, ck:ck + 1].to_broadcast([128, 128])), [scT.b], [scbc.b])
        mbrow = sbt(st, "mbrow", [48, 2, 128], F32)
        ngrow = sbt(st, "ngrow", [32, 128], F32)
        p.dma(mbrow.t[:, 0, :], I["mod_b"][0], writes=[mbrow.b])
        p.dma(mbrow.t[:, 1, :], I["mod_b"][1], writes=[mbrow.b])
        p.dma(ngrow.t[:], I["norm_g"].rearrange("l i k p -> (l i k) p"), writes=[ngrow.b])
        mbT = sbt(st, "mbT", [128, 2, 48], F32)
        ngT = sbt(st, "ngT", [128, 32], F32)
        for l in range(2):
            p.op('pe', lambda e, l=l: e.transpose(pT.t[:, 0:48], mbrow.t[:, l, :], identf.t[0:48, 0:48]), reads=[mbrow.b, identf.b], writes=[pT.b])
            p.op('dve', lambda e, l=l: e.tensor_copy(mbT.t[:, l, :], pT.t[:, 0:48]), reads=[pT.b], writes=[mbT.b])
        p.op('pe', lambda e: e.transpose(pT.t[:, 0:32], ngrow.t[:, :], identf.t[0:32, 0:32]), reads=[ngrow.b, identf.b], writes=[pT.b])
        p.op('dve', lambda e: e.tensor_copy(ngT.t[:], pT.t[:, 0:32]), reads=[pT.b], writes=[ngT.b])
        macc = sbt(st, "macc", [128, 2, 48, 2], F32)
        Wk = [sbt(st, f"Wk{i}", [128, 6 * D], F32) for i in range(2)]
        pg = [pst(st, f"pg{i}", [128, 512]) for i in range(2)]
        pm = pst(st, "pm", [128, 96])
        it = 0
        for l in range(2):
            for k in range(8):
                w = Wk[it % 2]
                it += 1
                for q3 in range(3):
                    p.dma(w.t[:, q3 * 2048:(q3 + 1) * 2048], I["mod_w"][l, k * 128:(k + 1) * 128, q3 * 2048:(q3 + 1) * 2048], writes=[w.b])
                for j in range(48):
                    p.op('pe', lambda e, j=j, w=w, k=k: e.matmul(pm.t[:, 2 * j:2 * j + 2], w.t[:, j * 128:(j + 1) * 128],
                                                                 scT.t[:, k:16:8], start=True, stop=True),
                         reads=[w.b, scT.b], writes=[pm.b])
                if k == 0:
                    p.op('dve', lambda e, l=l: e.tensor_copy(macc.t[:, l].rearrange("p j c -> p (j c)"), pm.t[:, :]), reads=[pm.b], writes=[macc.b])
                else:
                    p.op('dve', lambda e, l=l: e.tensor_add(macc.t[:, l].rearrange("p j c -> p (j c)"), macc.t[:, l].rearrange("p j c -> p (j c)"), pm.t[:, :]),
                         reads=[pm.b, macc.b], writes=[macc.b])
                gi = 0
                for col in range(2):
                    for g_i, which in enumerate((2, 5)):
                        for half in range(2):
                            pgt = pg[gi % 2]
                            gi += 1
                            p.op('pe', lambda e, pgt=pgt, col=col, k=k, w=w, which=which, half=half: e.matmul(
                                pgt.t[:, :], scbc.t[:, col * 8 + k, :], w.t[:, which * D + half * 512: which * D + half * 512 + 512], start=True, stop=True),
                                reads=[scbc.b, w.b], writes=[pgt.b])
                            dst = gates.t[:, l, col, g_i, half * 512:(half + 1) * 512]
                            if k == 0:
                                p.op('dve', lambda e, dst=dst, pgt=pgt: e.tensor_copy(dst, pgt.t[:, :]), reads=[pgt.b], writes=[gates.b])
                            else:
                                p.op('dve', lambda e, dst=dst, pgt=pgt: e.tensor_add(dst, dst, pgt.t[:, :]), reads=[pgt.b, gates.b], writes=[gates.b])
        gb = sbt(st, "gb", [128, D], F32)
        for l in range(2):
            for col in range(2):
                p.op('dve', lambda e, l=l, col=col: e.tensor_add(macc.t[:, l, :, col], macc.t[:, l, :, col], mbT.t[:, l, :]), reads=[macc.b, mbT.b], writes=[macc.b])
            for g_i, which in enumerate((2, 5)):
                p.dma(gb.t[:], mod_b_flat[l:l + 1, which * D:(which + 1) * D].partition_broadcast(128), writes=[gb.b])
                for col in range(2):
                    p.op('dve', lambda e, l=l, col=col, g_i=g_i: e.tensor_add(gates.t[:, l, col, g_i, :], gates.t[:, l, col, g_i, :], gb.t[:]),
                         reads=[gb.b, gates.b], writes=[gates.b])
            for col in range(2):
                for i2 in range(2):
                    sh = macc.t[:, l, (3 * i2) * 8:(3 * i2) * 8 + 8, col]
                    scl = macc.t[:, l, (3 * i2 + 1) * 8:(3 * i2 + 1) * 8 + 8, col]
                    gn = ngT.t[:, (l * 2 + i2) * 8:(l * 2 + i2) * 8 + 8]
                    p.op('dve', lambda e, l=l, col=col, i2=i2, scl=scl, gn=gn: e.scalar_tensor_tensor(
                        AB.t[:, l, col, 2 * i2, :], scl, 1.0, gn, ALU.add, ALU.mult), reads=[macc.b, ngT.b], writes=[AB.b])
                    p.op('dve', lambda e, l=l, col=col, i2=i2, sh=sh: e.tensor_copy(AB.t[:, l, col, 2 * i2 + 1, :], sh), reads=[macc.b], writes=[AB.b])
        for l in range(2):
            for col in range(2):
                for g_i in range(2):
                    p.dma(Gs[(l * 2 + col) * 2 + g_i], gates.t[:, l, col, g_i, :], reads=[gates.b])
        p.barrier()

    if stage == 0:
        with ExitStack() as st:
            g0 = load_gate(st, "g0dbg", 0, 0, 0)
            p.dma(out[0:128, :], g0.t[:], reads=[g0.b])
        p.dma(out[128:256, 0:128], AB.t[:].rearrange("p a b c d -> p (a b c d)"), reads=[AB.b])
        p.emit()
        top.close()
        return nc

    def make_norm(st, nm, nxt=2):
        ctxn = {}
        ctxn["xt"] = [sbt(st, f"{nm}xt{i}", [128, D], F32) for i in range(nxt)]
        ctxn["junk"] = sbt(st, f"{nm}junk", [128, D], BF16)
        ctxn["xn"] = [sbt(st, f"{nm}xn{i}", [128, D], BF16) for i in range(2)]
        ctxn["ss"] = [sbt(st, f"{nm}ss{i}", [128, 1], F32) for i in range(3)]
        ctxn["tp"] = [pst(st, f"{nm}tp{i}", [128, 8, 128], BF16) for i in range(2)]
        ctxn["n"] = 0
        return ctxn

    def norm_tile(cn, src, l, col, which, dst_fn, dst_buf, xt_fixed=None, ident=None):
        n = cn["n"]
        cn["n"] += 1
        xt = xt_fixed if xt_fixed is not None else cn["xt"][n % len(cn["xt"])]
        ss = cn["ss"][n % 3]
        xn = cn["xn"][n % 2]
        tp = cn["tp"][n % 2]
        junk = cn["junk"]
        idt = ident if ident is not None else identb
        p.dma(xt.t[:], src, writes=[xt.b])
        p.op('act', lambda e: e.activation(junk.t[:], xt.t[:], AF.Square, accum_out=ss.t[:]), reads=[xt.b], writes=[junk.b, ss.b])
        p.op('dve', lambda e: e.tensor_scalar(ss.t[:], ss.t[:], 1.0 / D, EPS, ALU.mult, ALU.add), reads=[ss.b], writes=[ss.b])
        p.op('act', lambda e: e.activation(ss.t[:], ss.t[:], AF.Sqrt), reads=[ss.b], writes=[ss.b])
        p.op('dve', lambda e: e.reciprocal(ss.t[:], ss.t[:]), reads=[ss.b], writes=[ss.b])
        p.op('dve', lambda e: e.tensor_scalar(xn.t[:], xt.t[:], ss.t[:, 0:1], None, ALU.mult), reads=[xt.b, ss.b], writes=[xn.b])
        for k in range(8):
            p.op('pe', lambda e, k=k: e.transpose(tp.t[:, k, :], xn.t[:, k * 128:(k + 1) * 128], idt.t[:]), reads=[xn.b, idt.b], writes=[tp.b])
        for k in range(8):
            p.op('act', lambda e, k=k: e.activation(dst_fn(k), tp.t[:, k, :], AF.Identity,
                                                    scale=AB.t[:, l, col, 2 * which, k:k + 1], bias=AB.t[:, l, col, 2 * which + 1, k:k + 1]),
                 reads=[tp.b, AB.b], writes=[dst_buf])
        return xt, ss

    def tile_src(ti):
        return I["x"][ti * 128:(ti + 1) * 128, :] if ti < 32 else I["ctx"][(ti - 32) * 128:(ti - 31) * 128, :]

    NT = 34
    L0 = ExitStack()
    Fm = sbt(L0, "Fm", [128, NT, 512], BF16)
    with ExitStack() as stBC:
        QT = sbt(stBC, "QT", [128, 4, NT * 128], BF16)
        KT = sbt(stBC, "KT", [128, NT * 128], BF16)
        Vm = sbt(stBC, "Vm", [128, NT, 128], BF16)
        with ExitStack() as st:
            wb = sbt(st, "wb", [128, 8, 1920], BF16)
            wst = [sbt(st, f"wst{i}", [128, 1280], F32) for i in range(1)]
            for k in range(8):
                s_ = wst[0]
                p.dma(s_.t[:], I["mix_w_in"][k * 128:(k + 1) * 128, :], writes=[s_.b])
                S = s_.t
                W = wb.t
                ew(lambda e, k=k, S=S, W=W: e.tensor_copy(W[:, k, 0:512], S[:, 0:512]), [s_.b], [wb.b])
                ew(lambda e, k=k, S=S, W=W: e.tensor_copy(W[:, k, 512:640], S[:, 1152:1280]), [s_.b], [wb.b])
                ew(lambda e, k=k, S=S, W=W: e.tensor_copy(W[:, k, 640:1152].rearrange("p (j h d) -> p j h d", j=4, h=2),
                                                          S[:, 512:1024].rearrange("p (h j d) -> p j h d", h=2, j=4)), [s_.b], [wb.b])
                ew(lambda e, k=k, S=S, W=W: e.tensor_copy(W[:, k, 1152:1280], S[:, 1024:1152]), [s_.b], [wb.b])
                for two in range(2):
                    ew(lambda e, k=k, S=S, W=W, two=two: e.tensor_copy(
                        W[:, k, 1280:1792].rearrange("p (j h i t) -> p j h i t", j=4, h=2, t=2)[:, :, :, :, two],
                        S[:, 512:1024].rearrange("p (h j i t) -> p j h i t", h=2, j=4, t=2)[:, :, :, :, 1 - two]), [s_.b], [wb.b])
                    ew(lambda e, k=k, S=S, W=W, two=two: e.tensor_copy(
                        W[:, k, 1792:1920].rearrange("p (i t) -> p i t", t=2)[:, :, two],
                        S[:, 1024:1152].rearrange("p (i t) -> p i t", t=2)[:, :, 1 - two]), [s_.b], [wb.b])
            ropeCb = [sbt(st, f"ropeC{i}", [128, 512], F32) for i in range(2)]
            ropeSb = [sbt(st, f"ropeS{i}", [128, 512], F32) for i in range(2)]
            cn = make_norm(st, "b")
            nTb = [sbt(st, f"nTb{i}", [128, 8, 512], BF16) for i in range(2)]
            pf = pst(st, "pf", [128, 512])
            pv = pst(st, "pv", [128, 128])
            pq = [pst(st, f"pq{i}", [128, 512]) for i in range(2)]
            pqp = [pst(st, f"pqp{i}", [128, 512]) for i in range(2)]
            t1 = [sbt(st, f"t1{i}", [128, 512], F32) for i in range(2)]
            t2 = [sbt(st, f"t2{i}", [128, 512], F32) for i in range(2)]
            qi_ = 0
            for blk in range(9):
                ntile = 4 if blk < 8 else 2
                ncol = ntile * 128
                nT = nTb[blk % 2]
                col = 0 if blk < 8 else 1
                for tt in range(ntile):
                    ti = blk * 4 + tt
                    norm_tile(cn, tile_src(ti), 0, col, 0, lambda k, tt=tt, nT=nT: nT.t[:, k, tt * 128:(tt + 1) * 128], nT.b)
                    for k in range(8):
                        p.op('pe', lambda e, k=k, tt=tt, nT=nT: e.matmul(pf.t[:, :], nT.t[:, k, tt * 128:(tt + 1) * 128], wb.t[:, k, 0:512],
                                                                      start=(k == 0), stop=(k == 7)), reads=[nT.b, wb.b], writes=[pf.b])
                    p.op('act', lambda e, ti=ti: e.activation(Fm.t[:, ti, :], pf.t[:, :], AF.Identity), reads=[pf.b], writes=[Fm.b])
                    for k in range(8):
                        p.op('pe', lambda e, k=k, tt=tt, nT=nT: e.matmul(pv.t[:, :], nT.t[:, k, tt * 128:(tt + 1) * 128], wb.t[:, k, 512:640],
                                                                      start=(k == 0), stop=(k == 7)), reads=[nT.b, wb.b], writes=[pv.b])
                    p.op('dve', lambda e, ti=ti: e.tensor_copy(Vm.t[:, ti, :], pv.t[:, :]), reads=[pv.b], writes=[Vm.b])
                c0 = blk * 512
                ropeC = ropeCb[blk % 2]
                ropeS = ropeSb[blk % 2]
                if blk < 8:
                    p.dma(ropeC.t[:], K["ropeC"][:, c0:c0 + 512], writes=[ropeC.b])
                    p.dma(ropeS.t[:], K["ropeS"][:, c0:c0 + 512], writes=[ropeS.b])
                for oc in range(5):
                    a = pq[qi_ % 2]
                    bq = pqp[qi_ % 2]
                    u1 = t1[qi_ % 2]
                    u2 = t2[qi_ % 2]
                    qi_ += 1
                    for k in range(8):
                        p.op('pe', lambda e, k=k, oc=oc, a=a, nT=nT, ncol=ncol: e.matmul(a.t[:, 0:ncol], wb.t[:, k, 640 + 128 * oc: 768 + 128 * oc], nT.t[:, k, 0:ncol],
                                                                                   start=(k == 0), stop=(k == 7)), reads=[nT.b, wb.b], writes=[a.b])
                    dstb = QT.b if oc < 4 else KT.b
                    dst = QT.t[:, oc, c0:c0 + ncol] if oc < 4 else KT.t[:, c0:c0 + ncol]
                    if blk < 8:
                        for k in range(8):
                            p.op('pe', lambda e, k=k, oc=oc, bq=bq, nT=nT, ncol=ncol: e.matmul(bq.t[:, 0:ncol], wb.t[:, k, 1280 + 128 * oc: 1408 + 128 * oc], nT.t[:, k, 0:ncol],
                                                                                        start=(k == 0), stop=(k == 7)), reads=[nT.b, wb.b], writes=[bq.b])
                        p.op('dve', lambda e, a=a, u1=u1, ropeC=ropeC: e.tensor_mul(u1.t[:, :], a.t[:, :], ropeC.t[:, :]), reads=[a.b, ropeC.b], writes=[u1.b])
                        p.op('dve', lambda e, bq=bq, u2=u2, ropeS=ropeS: e.tensor_mul(u2.t[:, :], bq.t[:, :], ropeS.t[:, :]), reads=[bq.b, ropeS.b], writes=[u2.b])
                        p.op('pool', lambda e, dst=dst, u1=u1, u2=u2: e.tensor_add(dst, u1.t[:, :], u2.t[:, :]), reads=[u1.b, u2.b], writes=[dstb])
                    else:
                        p.op('dve', lambda e, dst=dst, a=a, ncol=ncol: e.tensor_copy(dst, a.t[:, 0:ncol]), reads=[a.b], writes=[dstb])
            p.barrier()
        with ExitStack() as st:
            maskP = sbt(st, "maskP", [128, 512], BF16)
            maskN = sbt(st, "maskN", [128, 512], BF16)
            p.dma(maskP.t[:], K["maskP"], writes=[maskP.b])
            p.dma(maskN.t[:], K["maskN"], writes=[maskN.b])
            sk = sbt(st, "sk", [128, 8], F32)
            p.dma(sk.t[:], I["attn_sink"].partition_broadcast(128), writes=[sk.b])
            p.op('act', lambda e: e.activation(sk.t[:], sk.t[:], AF.Exp), reads=[sk.b], writes=[sk.b])
            skf = sbt(st, "skf", [128, 512], F32)
            for g in range(2):
                for hh in range(4):
                    p.op('dve', lambda e, g=g, hh=hh: e.tensor_copy(skf.t[64 * g:64 * g + 64, hh * 128:(hh + 1) * 128],
                                                                  sk.t[64 * g:64 * g + 64, 4 * g + hh:4 * g + hh + 1].to_broadcast([64, 128])), reads=[sk.b], writes=[skf.b])
            pS = [pst(st, f"pS{i}", [128, 512]) for i in range(3)]
            pO = [pst(st, f"pO{i}", [128, 512]) for i in range(2)]
            pZ = [pst(st, f"pZ{i}", [128, 512]) for i in range(2)]
            PT = [sbt(st, f"PT{i}", [128, 512], BF16) for i in range(3)]
            den = [sbt(st, f"den{i}", [128, 512], F32) for i in range(2)]
            ATt = [sbt(st, f"ATt{i}", [128, 4, 128], BF16) for i in range(2)]
            si = 0
            for qi in range(NT):
                AT = ATt[qi % 2]
                for g in range(2):
                    lo, hi = 64 * g, 64 * g + 64
                    keys = [(32, None), (33, None)]
                    if qi < 32:
                        if qi > 0:
                            keys.append((qi - 1, maskP))
                        keys.append((qi, None))
                        if qi < 31:
                            keys.append((qi + 1, maskN))
                    O = pO[g]
                    Z = pZ[g]
                    SP = []
                    for ki in range(len(keys)):
                        SP.append((pS[si % 3], PT[si % 3]))
                        si += 1

                    def emitS(ki):
                        kt, msk = keys[ki]
                        S_ = SP[ki][0]
                        p.op('pe', lambda e, S_=S_, kt=kt, qi=qi, lo=lo, hi=hi, msk=msk: e.matmul(
                            S_.t[:, :].rearrange("p (j q) -> p j q", j=4), KT.t[lo:hi, kt * 128:(kt + 1) * 128], QT.t[lo:hi, :, qi * 128:(qi + 1) * 128],
                            start=True, stop=(msk is None)), reads=[KT.b, QT.b], writes=[S_.b])
                        if msk is not None:
                            p.op('pe', lambda e, S_=S_, msk=msk: e.matmul(S_.t[:, :], identb.t[:, :], msk.t[:, :], start=False, stop=True),
                                 reads=[identb.b, msk.b], writes=[S_.b])
                    emitS(0)
                    if len(keys) > 1:
                        emitS(1)
                    for ki, (kt, msk) in enumerate(keys):
                        S_, P_ = SP[ki]
                        p.op('act', lambda e, S_=S_, P_=P_: e.activation(P_.t[:, :], S_.t[:, :], AF.Exp, scale=0.125), reads=[S_.b], writes=[P_.b])
                        if ki + 2 < len(keys):
                            emitS(ki + 2)
                        first = ki == 0
                        last = ki == len(keys) - 1
                        p.op('pe', lambda e, O=O, kt=kt, lo=lo, hi=hi, P_=P_, first=first, last=last: e.matmul(
                            O.t[lo:hi, :], Vm.t[:, kt, lo:hi], P_.t[:, :], start=first, stop=last), reads=[Vm.b, P_.b], writes=[O.b])
                        p.op('pe', lambda e, Z=Z, lo=lo, hi=hi, P_=P_, first=first, last=last: e.matmul(
                            Z.t[lo:hi, :], ones_b.t[:, 0:64], P_.t[:, :], start=first, stop=last), reads=[ones_b.b, P_.b], writes=[Z.b])
                    dn = den[g]
                    p.op('dve', lambda e, dn=dn, Z=Z, lo=lo, hi=hi: e.tensor_add(dn.t[lo:hi, :], Z.t[lo:hi, :], skf.t[lo:hi, :]), reads=[Z.b, skf.b], writes=[dn.b])
                    p.op('act', lambda e, dn=dn, lo=lo, hi=hi: e.activation(dn.t[lo:hi, :], dn.t[lo:hi, :], AF.Ln), reads=[dn.b], writes=[dn.b])
                    p.op('act', lambda e, dn=dn, lo=lo, hi=hi: e.activation(dn.t[lo:hi, :], dn.t[lo:hi, :], AF.Exp, scale=-1.0), reads=[dn.b], writes=[dn.b])
                    p.op('dve', lambda e, dn=dn, O=O, lo=lo, hi=hi, AT=AT: e.tensor_mul(
                        AT.t[lo:hi, :, :], O.t[lo:hi, :].rearrange("p (j q) -> p j q", j=4),
                        dn.t[lo:hi, :].rearrange("p (j q) -> p j q", j=4)), reads=[O.b, dn.b], writes=[AT.b])
                p.dma(ATs[qi], AT.t[:], reads=[AT.b])
            p.barrier()

    with ExitStack() as st:
        wob = sbt(st, "wob", [128, 8, D], BF16)
        wst = [sbt(st, f"wost{i}", [128, D], F32) for i in range(2)]
        for j in range(8):
            s_ = wst[j % 2]
            if j < 4:
                p.dma(s_.t[:], I["mix_w_out"][j * 128:(j + 1) * 128, :], writes=[s_.b])
            else:
                jj = j - 4
                p.dma(s_.t[0:64, :], I["mix_w_out"][512 + 64 * jj:512 + 64 * jj + 64, :], writes=[s_.b])
                p.dma(s_.t[64:128, :], I["mix_w_out"][512 + 64 * (4 + jj):512 + 64 * (4 + jj) + 64, :], writes=[s_.b])
            ew(lambda e, j=j, s_=s_: e.tensor_copy(wob.t[:, j, :], s_.t[:]), [s_.b], [wob.b])
        Cc = sbt(st, "Cc", [128, 128], BF16)
        Sc = sbt(st, "Sc", [128, 128], BF16)
        p.dma(Cc.t[:], K["Cc"], writes=[Cc.b])
        p.dma(Sc.t[:], K["Sc"], writes=[Sc.b])
        CTb = [sbt(st, f"CTb{i}", [128, 32, 256], BF16) for i in range(2)]
        STb = [sbt(st, f"STb{i}", [128, 32, 256], BF16) for i in range(2)]
        pP = [pst(st, f"pP{i}", [128, 256]) for i in range(2)]
        pQ = [pst(st, f"pQ{i}", [128, 256]) for i in range(2)]
        pY = pst(st, "pY", [128, 256])
        pOo = [pst(st, f"pOo{i}", [128, 512]) for i in range(2)]
        Pb = [sbt(st, f"Pb{i}", [128, 256], BF16) for i in range(2)]
        Qb = [sbt(st, f"Qb{i}", [128, 256], BF16) for i in range(2)]
        YT = [sbt(st, f"YT{i}", [128, 4, 256], BF16) for i in range(2)]
        xres = [sbt(st, f"xres{i}", [128, D], F32) for i in range(2)]
        tmpo = [sbt(st, f"tmpo{i}", [128, 512], F32) for i in range(2)]
        hn = [sbt(st, f"hn{i}", [128, D], F32) for i in range(2)]
        ATl = [sbt(st, f"ATl{i}", [128, 4, 128], BF16) for i in range(2)]
        g1l = [load_gate(st, "g1lat", 0, 0, 0), load_gate(st, "g1ctx", 0, 1, 0)]
        cnt_ = 0
        oi = 0
        for kb in range(17):
            lat = kb < 16
            if lat:
                cb = CTb[kb % 2]
                sb_ = STb[kb % 2]
                for h4 in range(4):
                    p.dma(cb.t[:, h4 * 8:(h4 + 1) * 8, :], K["CT"][kb, :, h4 * 8:(h4 + 1) * 8, :], writes=[cb.b])
                    p.dma(sb_.t[:, h4 * 8:(h4 + 1) * 8, :], K["ST"][kb, :, h4 * 8:(h4 + 1) * 8, :], writes=[sb_.b])
                nti = 32
                t0 = 0
            else:
                cb = CTb[kb % 2]
                sb_ = STb[kb % 2]
                p.dma(cb.t[:, 0:2, :], K["C256"], writes=[cb.b])
                p.dma(sb_.t[:, 0:2, :], K["S256"], writes=[sb_.b])
                nti = 2
                t0 = 32
            Y = YT[kb % 2]
            for cc in range(4):
                a = pP[cnt_ % 2]
                b = pQ[cnt_ % 2]
                ab = Pb[cnt_ % 2]
                bb = Qb[cnt_ % 2]
                cnt_ += 1
                for i in range(nti):
                    p.op('pe', lambda e, a=a, i=i, cc=cc, cb=cb, t0=t0, nti=nti: e.matmul(a.t[:, :], Fm.t[:, t0 + i, cc * 128:(cc + 1) * 128], cb.t[:, i, :],
                                                                                     start=(i == 0), stop=(i == nti - 1)), reads=[Fm.b, cb.b], writes=[a.b])
                for i in range(nti):
                    p.op('pe', lambda e, b=b, i=i, cc=cc, sb_=sb_, t0=t0, nti=nti: e.matmul(b.t[:, :], Fm.t[:, t0 + i, cc * 128:(cc + 1) * 128], sb_.t[:, i, :],
                                                                                      start=(i == 0), stop=(i == nti - 1)), reads=[Fm.b, sb_.b], writes=[b.b])
                p.op('act', lambda e, a=a, ab=ab: e.activation(ab.t[:, :], a.t[:, :], AF.Identity), reads=[a.b], writes=[ab.b])
                p.op('act', lambda e, b=b, bb=bb: e.activation(bb.t[:, :], b.t[:, :], AF.Identity, scale=-1.0), reads=[b.b], writes=[bb.b])
                p.op('pe', lambda e, ab=ab: e.matmul(pY.t[:, :], Cc.t[:, :], ab.t[:, :], start=True, stop=False), reads=[Cc.b, ab.b], writes=[pY.b])
                p.op('pe', lambda e, bb=bb: e.matmul(pY.t[:, :], Sc.t[:, :], bb.t[:, :], start=False, stop=True), reads=[Sc.b, bb.b], writes=[pY.b])
                p.op('dve', lambda e, Y=Y, cc=cc: e.tensor_copy(Y.t[:, cc, :], pY.t[:, :]), reads=[pY.b], writes=[Y.b])
            for tt in range(2):
                ti = (kb * 2 + tt) if lat else 32 + tt
                col = 0 if lat else 1
                xr = xres[ti % 2]
                h_ = hn[ti % 2]
                p.dma(xr.t[:], tile_src(ti), writes=[xr.b])
                AT = ATl[ti % 2]
                p.dma(AT.t[:], ATs[ti], writes=[AT.b])
                for half in range(2):
                    po = pOo[oi % 2]
                    tm = tmpo[oi % 2]
                    oi += 1
                    for j in range(8):
                        lhs = Y.t[:, j, tt * 128:(tt + 1) * 128] if j < 4 else AT.t[:, j - 4, :]
                        rb = Y.b if j < 4 else AT.b
                        p.op('pe', lambda e, po=po, lhs=lhs, j=j, half=half: e.matmul(po.t[:, :], lhs, wob.t[:, j, half * 512:(half + 1) * 512],
                                                                                 start=(j == 0), stop=(j == 7)), reads=[rb, wob.b], writes=[po.b])
                    p.op('dve', lambda e, po=po, tm=tm, half=half, col=col: e.tensor_mul(tm.t[:, :], po.t[:, :], g1l[col].t[:, half * 512:(half + 1) * 512]),
                         reads=[po.b, g1l[col].b], writes=[tm.b])
                    p.op('pool', lambda e, tm=tm, xr=xr, h_=h_, half=half: e.tensor_add(h_.t[:, half * 512:(half + 1) * 512], xr.t[:, half * 512:(half + 1) * 512], tm.t[:, :]),
                         reads=[tm.b, xr.b], writes=[h_.b])
                p.dma(Hs[ti * 128:(ti + 1) * 128, :], h_.t[:], reads=[h_.b])
        p.barrier()
    L0.close()

    if stage == 1:
        for ti in range(32):
            pass
        with ExitStack() as st:
            bb_ = [sbt(st, f"bo{i}", [128, D], F32) for i in range(2)]
            for ti in range(32):
                b_ = bb_[ti % 2]
                p.dma(b_.t[:], Hs[ti * 128:(ti + 1) * 128, :], writes=[b_.b])
                p.dma(out[ti * 128:(ti + 1) * 128, :], b_.t[:], reads=[b_.b])
        p.emit()
        top.close()
        return nc

    def ffn(l, tiles, final):
        with ExitStack() as st:
            Wg = sbt(st, "Wg", [128, 8, FF], BF16)
            Wu = sbt(st, "Wu", [128, 8, FF], BF16)
            Wd = sbt(st, "Wd", [128, NFC, D], BF16)
            with ExitStack() as st2:
                stg = [sbt(st2, f"fst{i}", [128, FF], F32) for i in range(2)]
                si_ = 0
                for nm, Wt in (("ffn_w_gate", Wg), ("ffn_w_up", Wu)):
                    for k in range(8):
                        s_ = stg[si_ % 2]
                        si_ += 1
                        p.dma(s_.t[:, 0:1408], I[nm][l, k * 128:(k + 1) * 128, 0:1408], writes=[s_.b])
                        p.dma(s_.t[:, 1408:FF], I[nm][l, k * 128:(k + 1) * 128, 1408:FF], writes=[s_.b])
                        ew(lambda e, Wt=Wt, k=k, s_=s_: e.tensor_copy(Wt.t[:, k, 0:1408], s_.t[:, 0:1408]), [s_.b], [Wt.b])
                        ew(lambda e, Wt=Wt, k=k, s_=s_: e.tensor_copy(Wt.t[:, k, 1408:FF], s_.t[:, 1408:FF]), [s_.b], [Wt.b])
                for fc in range(0, NFC, 2):
                    s_ = stg[si_ % 2]
                    si_ += 1
                    for q2 in range(2):
                        p.dma(s_.t[:, q2 * D:(q2 + 1) * D], I["ffn_w_down"][l, (fc + q2) * 128:(fc + q2 + 1) * 128, :], writes=[s_.b])
                    ew(lambda e, fc=fc, s_=s_: e.tensor_copy(Wd.t[:, fc:fc + 2, :].rearrange("p a d -> p (a d)"), s_.t[:, 0:2 * D]), [s_.b], [Wd.b])
                p.barrier()
            cn = make_norm(st, "f", nxt=0)
            g2l = [load_gate(st, "g2lat", l, 0, 1)] + ([load_gate(st, "g2ctx", l, 1, 1)] if not final else [])
            hx = [sbt(st, f"hx{i}", [128, D], F32) for i in range(3)]
            nTf = [sbt(st, f"nTf{i}", [128, 8, 256], BF16) for i in range(2)]
            actT = [sbt(st, f"actT{i}", [128, NFC, 256], BF16) for i in range(1)]
            sg = [sbt(st, f"sg{i}", [128, 256], F32) for i in range(2)]
            pgt = [pst(st, f"fpg{i}", [128, 256]) for i in range(2)]
            put = [pst(st, f"fpu{i}", [128, 256]) for i in range(2)]
            pdn = [pst(st, f"fpd{i}", [128, 512]) for i in range(2)]
            tm_ = [sbt(st, f"ftm{i}", [128, 512], F32) for i in range(2)]
            ho = [sbt(st, f"ho{i}", [128, D], F32) for i in range(2)]
            fss = [sbt(st, f"fss{i}", [128, 1], F32) for i in range(2)]
            fj = cn["junk"]
            fingt = None
            if final:
                fingt = sbt(st, "fing", [128, D], F32)
                p.dma(fingt.t[:], I["final_g"].partition_broadcast(128), writes=[fingt.b])
            gi_ = 0
            di_ = 0
            hi_ = 0
            for b0 in range(0, len(tiles), 2):
                tl = tiles[b0:b0 + 2]
                nT = nTf[(b0 // 2) % 2]
                aT = actT[0]
                xts = []
                for tt, ti in enumerate(tl):
                    col = 0 if ti < 32 else 1
                    hxt = hx[hi_ % 3]
                    hi_ += 1
                    src_ = Hs[ti * 128:(ti + 1) * 128, :]
                    norm_tile(cn, src_, l, col, 1, lambda k, tt=tt, nT=nT: nT.t[:, k, tt * 128:(tt + 1) * 128], nT.b, xt_fixed=hxt)
                    xts.append(hxt)
                for fc in range(NFC):
                    a = pgt[gi_ % 2]
                    b = put[gi_ % 2]
                    s2 = sg[gi_ % 2]
                    gi_ += 1
                    for k in range(8):
                        p.op('pe', lambda e, a=a, k=k, fc=fc, nT=nT: e.matmul(a.t[:, :], Wg.t[:, k, fc * 128:(fc + 1) * 128], nT.t[:, k, :], start=(k == 0), stop=(k == 7)),
                             reads=[Wg.b, nT.b], writes=[a.b])
                    for k in range(8):
                        p.op('pe', lambda e, b=b, k=k, fc=fc, nT=nT: e.matmul(b.t[:, :], Wu.t[:, k, fc * 128:(fc + 1) * 128], nT.t[:, k, :], start=(k == 0), stop=(k == 7)),
                             reads=[Wu.b, nT.b], writes=[b.b])
                    p.op('act', lambda e, a=a, s2=s2: e.activation(s2.t[:, :], a.t[:, :], AF.Silu), reads=[a.b], writes=[s2.b])
                    p.op('dve', lambda e, b=b, s2=s2, aT=aT, fc=fc: e.tensor_mul(aT.t[:, fc, :], s2.t[:, :], b.t[:, :]), reads=[b.b, s2.b], writes=[aT.b])
                for tt, ti in enumerate(tl):
                    col = 0 if ti < 32 else 1
                    hxt = xts[tt]
                    h_ = ho[ti % 2]
                    for half in range(2):
                        po = pdn[di_ % 2]
                        tm = tm_[di_ % 2]
                        di_ += 1
                        for fc in range(NFC):
                            p.op('pe', lambda e, po=po, fc=fc, tt=tt, half=half, aT=aT: e.matmul(po.t[:, :], aT.t[:, fc, tt * 128:(tt + 1) * 128], Wd.t[:, fc, half * 512:(half + 1) * 512],
                                                                                         start=(fc == 0), stop=(fc == NFC - 1)), reads=[aT.b, Wd.b], writes=[po.b])
                        p.op('dve', lambda e, po=po, tm=tm, half=half, col=col: e.tensor_mul(tm.t[:, :], po.t[:, :], g2l[col].t[:, half * 512:(half + 1) * 512]),
                             reads=[po.b, g2l[col].b], writes=[tm.b])
                        p.op('pool', lambda e, tm=tm, hxt=hxt, h_=h_, half=half: e.tensor_add(h_.t[:, half * 512:(half + 1) * 512], hxt.t[:, half * 512:(half + 1) * 512], tm.t[:, :]),
                             reads=[tm.b, hxt.b], writes=[h_.b])
                    if not final:
                        p.dma(Hs[ti * 128:(ti + 1) * 128, :], h_.t[:], reads=[h_.b])
                    else:
                        ss = fss[ti % 2]
                        p.op('act', lambda e, h_=h_, ss=ss: e.activation(fj.t[:], h_.t[:], AF.Square, accum_out=ss.t[:]), reads=[h_.b], writes=[fj.b, ss.b])
                        p.op('dve', lambda e, ss=ss: e.tensor_scalar(ss.t[:], ss.t[:], 1.0 / D, EPS, ALU.mult, ALU.add), reads=[ss.b], writes=[ss.b])
                        p.op('act', lambda e, ss=ss: e.activation(ss.t[:], ss.t[:], AF.Sqrt), reads=[ss.b], writes=[ss.b])
                        p.op('dve', lambda e, ss=ss: e.reciprocal(ss.t[:], ss.t[:]), reads=[ss.b], writes=[ss.b])
                        p.op('dve', lambda e, h_=h_, ss=ss: e.scalar_tensor_tensor(h_.t[:], h_.t[:], ss.t[:, 0:1], fingt.t[:], ALU.mult, ALU.mult),
                             reads=[h_.b, ss.b, fingt.b], writes=[h_.b])
                        p.dma(out[ti * 128:(ti + 1) * 128, :], h_.t[:], reads=[h_.b])
            p.barrier()

    ffn(0, list(range(34)), False)

    if stage == 2:
        with ExitStack() as st:
            bb_ = [sbt(st, f"bo{i}", [128, D], F32) for i in range(2)]
            for ti in range(32):
                b_ = bb_[ti % 2]
                p.dma(b_.t[:], Hs[ti * 128:(ti + 1) * 128, :], writes=[b_.b])
                p.dma(out[ti * 128:(ti + 1) * 128, :], b_.t[:], reads=[b_.b])
        p.emit()
        top.close()
        return nc

    s5_layer(nc, p, I, K, Hs, AB, load_gate, identf, identb, sbt, pst, ew, make_norm, norm_tile)
    if stage == 3:
        with ExitStack() as st:
            bb_ = [sbt(st, f"bo3{i}", [128, D], F32) for i in range(2)]
            for ti in range(32):
                b_ = bb_[ti % 2]
                p.dma(b_.t[:], Hs[ti * 128:(ti + 1) * 128, :], writes=[b_.b])
                p.dma(out[ti * 128:(ti + 1) * 128, :], b_.t[:], reads=[b_.b])
        p.emit()
        top.close()
        return nc
    ffn(1, list(range(16)), True)
    p.emit()
    top.close()
    return nc


def rev(ap):
    apl = [list(a) for a in ap.ap]
    n = apl[-1][1]
    stp = apl[-1][0]
    apl[-1][0] = -stp
    return bass.AP(ap.tensor, ap.offset + (n - 1) * stp, apl)


def bcast_last(ap, n):
    apl = [list(a) for a in ap.ap] + [[0, n]]
    return bass.AP(ap.tensor, ap.offset, apl)


def s5_layer(nc, p, I, K, Hs, AB, load_gate, identf, identb, sbt, pst, ew, make_norm, norm_tile, dbg=None):
    Hloc = Hs
    YGs = nc.dram_tensor("YGs", [8, 128, T // 2], BF16).ap()
    NTOK = T + TC
    with ExitStack() as S:
        nT = sbt(S, "s5nT", [128, 8, NTOK], BF16)
        with ExitStack() as st:
            cn = make_norm(st, "s5n", nxt=3)
            for ti in range(34):
                col = 0 if ti < 32 else 1
                norm_tile(cn, Hs[ti * 128:(ti + 1) * 128, :], 1, col, 0, lambda k, ti=ti: nT.t[:, k, ti * 128:(ti + 1) * 128], nT.b)
            p.barrier()
        PRM = sbt(S, "s5prm", [128, 3, 64], F32)
        Cw = sbt(S, "s5Cw", [128, 64, 2, 32], F32)
        Bz = sbt(S, "s5Bz", [128, 2, 64, 2, 16], F32)
        LC = 8
        PW = sbt(S, "s5PW", [128, LC + 1, 2, 64], F32)
        PRM4 = sbt(S, "s5prm4", [128, 3, 64], F32)
        dT = sbt(S, "s5dT", [128, 8], F32)
        with ExitStack() as st:
            pt = pst(st, "s5pt", [128, 128])
            def V(nm):
                return sbt(st, "s5v_" + nm, [128, 64], F32)
            arow = sbt(st, "s5arow", [64, 2, 128], F32)
            p.dma(arow.t[:, 0, :], I["ssm_a_re"].rearrange("(dq g) p -> dq (g p)", g=2), writes=[arow.b])
            p.dma(arow.t[:, 1, :], I["ssm_a_im"].rearrange("(dq g) p -> dq (g p)", g=2), writes=[arow.b])
            are, aim = V("are"), V("aim")
            for i_, dst in enumerate((are, aim)):
                p.op('pe', lambda e, i_=i_: e.transpose(pt.t[:, 0:64], arow.t[:, i_, :], identf.t[0:64, 0:64]), reads=[arow.b, identf.b], writes=[pt.b])
                p.op('dve', lambda e, dst=dst: e.tensor_copy(dst.t[:], pt.t[:, 0:64]), reads=[pt.b], writes=[dst.b])
            drow = sbt(st, "s5drow", [8, 128], F32)
            p.dma(drow.t[:], I["ssm_d"], writes=[drow.b])
            p.op('pe', lambda e: e.transpose(pt.t[:, 0:8], drow.t[:, :], identf.t[0:8, 0:8]), reads=[drow.b, identf.b], writes=[pt.b])
            p.op('dve', lambda e: e.tensor_copy(dT.t[:], pt.t[:, 0:8]), reads=[pt.b], writes=[dT.b])
            ldb = sbt(st, "s5ldb", [128, 128], F32)
            p.dma(ldb.t[:], I["ssm_log_dt"].partition_broadcast(128), writes=[ldb.b])
            dt = V("dt")
            for g2 in range(2):
                p.op('dve', lambda e, g2=g2: e.tensor_copy(dt.t[64 * g2:64 * g2 + 64, :], ldb.t[64 * g2:64 * g2 + 64, g2:128:2]), reads=[ldb.b], writes=[dt.b])
            p.op('act', lambda e: e.activation(dt.t[:], dt.t[:], AF.Exp), reads=[dt.b], writes=[dt.b])
            xr, th, mag = V("xr"), V("th"), V("mag")
            p.op('dve', lambda e: e.tensor_mul(xr.t[:], are.t[:], dt.t[:]), reads=[are.b, dt.b], writes=[xr.b])
            p.op('dve', lambda e: e.tensor_mul(th.t[:], aim.t[:], dt.t[:]), reads=[aim.b, dt.b], writes=[th.b])
            p.op('act', lambda e: e.activation(mag.t[:], xr.t[:], AF.Exp), reads=[xr.b], writes=[mag.b])
            kf = V("kf")
            ki = sbt(st, "s5ki", [128, 64], mybir.dt.int32)
            p.op('dve', lambda e: e.tensor_scalar(kf.t[:], th.t[:], 1.0 / (2 * math.pi), None, ALU.mult), reads=[th.b], writes=[kf.b])
            p.op('dve', lambda e: e.tensor_copy(ki.t[:], kf.t[:]), reads=[kf.b], writes=[ki.b])
            p.op('dve', lambda e: e.tensor_copy(kf.t[:], ki.t[:]), reads=[ki.b], writes=[kf.b])
            C1 = 6.28125
            C2 = 2 * math.pi - 6.28125
            thm = V("thm")
            p.op('dve', lambda e: e.scalar_tensor_tensor(thm.t[:], kf.t[:], -C1, th.t[:], ALU.mult, ALU.add), reads=[kf.b, th.b], writes=[thm.b])
            p.op('dve', lambda e: e.scalar_tensor_tensor(thm.t[:], kf.t[:], -C2, thm.t[:], ALU.mult, ALU.add), reads=[kf.b, thm.b], writes=[thm.b])
            xq, u2, qs, qc = V("xq"), V("u2"), V("qs"), V("qc")
            p.op('dve', lambda e: e.tensor_scalar(xq.t[:], thm.t[:], 0.25, None, ALU.mult), reads=[thm.b], writes=[xq.b])
            p.op('dve', lambda e: e.tensor_mul(u2.t[:], xq.t[:], xq.t[:]), reads=[xq.b], writes=[u2.b])
            sc_ = [(-1.0) ** k / math.factorial(2 * k + 1) for k in range(9)]
            cc_ = [(-1.0) ** k / math.factorial(2 * k) for k in range(9)]
            p.op('dve', lambda e: e.tensor_scalar(qs.t[:], u2.t[:], sc_[8], None, ALU.mult), reads=[u2.b], writes=[qs.b])
            p.op('dve', lambda e: e.tensor_scalar(qc.t[:], u2.t[:], cc_[8], None, ALU.mult), reads=[u2.b], writes=[qc.b])
            for k in range(7, 0, -1):
                p.op('dve', lambda e, k=k: e.scalar_tensor_tensor(qs.t[:], qs.t[:], sc_[k], u2.t[:], ALU.add, ALU.mult), reads=[qs.b, u2.b], writes=[qs.b])
                p.op('dve', lambda e, k=k: e.scalar_tensor_tensor(qc.t[:], qc.t[:], cc_[k], u2.t[:], ALU.add, ALU.mult), reads=[qc.b, u2.b], writes=[qc.b])
            sn, cs = V("sn"), V("cs")
            p.op('dve', lambda e: e.scalar_tensor_tensor(sn.t[:], qs.t[:], 1.0, xq.t[:], ALU.add, ALU.mult), reads=[qs.b, xq.b], writes=[sn.b])
            p.op('dve', lambda e: e.tensor_scalar(cs.t[:], qc.t[:], 1.0, None, ALU.add), reads=[qc.b], writes=[cs.b])
            ta, tb_ = V("ta"), V("tb")
            for _ in range(2):
                p.op('dve', lambda e: e.tensor_mul(ta.t[:], cs.t[:], cs.t[:]), reads=[cs.b], writes=[ta.b])
                p.op('dve', lambda e: e.tensor_mul(tb_.t[:], sn.t[:], sn.t[:]), reads=[sn.b], writes=[tb_.b])
                p.op('dve', lambda e: e.scalar_tensor_tensor(sn.t[:], cs.t[:], 2.0, sn.t[:], ALU.mult, ALU.mult), reads=[cs.b, sn.b], writes=[sn.b])
                p.op('dve', lambda e: e.tensor_sub(cs.t[:], ta.t[:], tb_.t[:]), reads=[ta.b, tb_.b], writes=[cs.b])
            p.op('dve', lambda e: e.tensor_copy(PRM.t[:, 0, :], mag.t[:]), reads=[mag.b], writes=[PRM.b])
            p.op('dve', lambda e: e.tensor_copy(PRM.t[:, 1, :], cs.t[:]), reads=[cs.b], writes=[PRM.b])
            p.op('dve', lambda e: e.tensor_copy(PRM.t[:, 2, :], sn.t[:]), reads=[sn.b], writes=[PRM.b])
            p.op('pool', lambda e: e.memset(PW.t[:, 0, 0, :], 1.0), writes=[PW.b])
            p.op('pool', lambda e: e.memset(PW.t[:, 0, 1, :], 0.0), writes=[PW.b])
            p.op('dve', lambda e: e.tensor_mul(PW.t[:, 1, 0, :], mag.t[:], cs.t[:]), reads=[mag.b, cs.b], writes=[PW.b])
            p.op('dve', lambda e: e.tensor_mul(PW.t[:, 1, 1, :], mag.t[:], sn.t[:]), reads=[mag.b, sn.b], writes=[PW.b])
            for k in range(2, LC + 1):
                p.op('dve', lambda e, k=k: e.tensor_mul(ta.t[:], PW.t[:, k - 1, 0, :], PW.t[:, 1, 0, :]), reads=[PW.b], writes=[ta.b])
                p.op('dve', lambda e, k=k: e.tensor_mul(tb_.t[:], PW.t[:, k - 1, 1, :], PW.t[:, 1, 1, :]), reads=[PW.b], writes=[tb_.b])
                p.op('dve', lambda e, k=k: e.tensor_sub(PW.t[:, k, 0, :], ta.t[:], tb_.t[:]), reads=[ta.b, tb_.b], writes=[PW.b])
                p.op('dve', lambda e, k=k: e.tensor_mul(ta.t[:], PW.t[:, k - 1, 0, :], PW.t[:, 1, 1, :]), reads=[PW.b], writes=[ta.b])
                p.op('dve', lambda e, k=k: e.tensor_mul(tb_.t[:], PW.t[:, k - 1, 1, :], PW.t[:, 1, 0, :]), reads=[PW.b], writes=[tb_.b])
                p.op('dve', lambda e, k=k: e.tensor_add(PW.t[:, k, 1, :], ta.t[:], tb_.t[:]), reads=[ta.b, tb_.b], writes=[PW.b])
            c4, s4, r4 = V("c4"), V("s4"), V("r4")
            p.op('dve', lambda e: e.tensor_copy(c4.t[:], cs.t[:]), reads=[cs.b], writes=[c4.b])
            p.op('dve', lambda e: e.tensor_copy(s4.t[:], sn.t[:]), reads=[sn.b], writes=[s4.b])
            p.op('dve', lambda e: e.tensor_copy(r4.t[:], mag.t[:]), reads=[mag.b], writes=[r4.b])
            for _ in range(3):
                p.op('dve', lambda e: e.tensor_mul(ta.t[:], c4.t[:], c4.t[:]), reads=[c4.b], writes=[ta.b])
                p.op('dve', lambda e: e.tensor_mul(tb_.t[:], s4.t[:], s4.t[:]), reads=[s4.b], writes=[tb_.b])
                p.op('dve', lambda e: e.scalar_tensor_tensor(s4.t[:], c4.t[:], 2.0, s4.t[:], ALU.mult, ALU.mult), reads=[c4.b, s4.b], writes=[s4.b])
                p.op('dve', lambda e: e.tensor_sub(c4.t[:], ta.t[:], tb_.t[:]), reads=[ta.b, tb_.b], writes=[c4.b])
                p.op('dve', lambda e: e.tensor_mul(r4.t[:], r4.t[:], r4.t[:]), reads=[r4.b], writes=[r4.b])
            p.op('dve', lambda e: e.tensor_copy(PRM4.t[:, 0, :], r4.t[:]), reads=[r4.b], writes=[PRM4.b])
            p.op('dve', lambda e: e.tensor_copy(PRM4.t[:, 1, :], c4.t[:]), reads=[c4.b], writes=[PRM4.b])
            p.op('dve', lambda e: e.tensor_copy(PRM4.t[:, 2, :], s4.t[:]), reads=[s4.b], writes=[PRM4.b])
            nr, ni, den, fre, fim = V("nr"), V("ni"), V("den"), V("fre"), V("fim")
            p.op('dve', lambda e: e.tensor_mul(nr.t[:], mag.t[:], cs.t[:]), reads=[mag.b, cs.b], writes=[nr.b])
            p.op('dve', lambda e: e.tensor_scalar(nr.t[:], nr.t[:], -1.0, None, ALU.add), reads=[nr.b], writes=[nr.b])
            p.op('dve', lambda e: e.tensor_mul(ni.t[:], mag.t[:], sn.t[:]), reads=[mag.b, sn.b], writes=[ni.b])
            p.op('dve', lambda e: e.tensor_mul(den.t[:], are.t[:], are.t[:]), reads=[are.b], writes=[den.b])
            p.op('dve', lambda e: e.tensor_mul(ta.t[:], aim.t[:], aim.t[:]), reads=[aim.b], writes=[ta.b])
            p.op('dve', lambda e: e.tensor_add(den.t[:], den.t[:], ta.t[:]), reads=[den.b, ta.b], writes=[den.b])
            p.op('dve', lambda e: e.reciprocal(den.t[:], den.t[:]), reads=[den.b], writes=[den.b])
            p.op('dve', lambda e: e.tensor_mul(ta.t[:], nr.t[:], are.t[:]), reads=[nr.b, are.b], writes=[ta.b])
            p.op('dve', lambda e: e.tensor_mul(tb_.t[:], ni.t[:], aim.t[:]), reads=[ni.b, aim.b], writes=[tb_.b])
            p.op('dve', lambda e: e.tensor_add(fre.t[:], ta.t[:], tb_.t[:]), reads=[ta.b, tb_.b], writes=[fre.b])
            p.op('dve', lambda e: e.tensor_mul(fre.t[:], fre.t[:], den.t[:]), reads=[fre.b, den.b], writes=[fre.b])
            p.op('dve', lambda e: e.tensor_mul(ta.t[:], ni.t[:], are.t[:]), reads=[ni.b, are.b], writes=[ta.b])
            p.op('dve', lambda e: e.tensor_mul(tb_.t[:], nr.t[:], aim.t[:]), reads=[nr.b, aim.b], writes=[tb_.b])
            p.op('dve', lambda e: e.tensor_sub(fim.t[:], ta.t[:], tb_.t[:]), reads=[ta.b, tb_.b], writes=[fim.b])
            p.op('dve', lambda e: e.tensor_mul(fim.t[:], fim.t[:], den.t[:]), reads=[fim.b, den.b], writes=[fim.b])
            Br = [sbt(st, f"s5Br{i}", [128, 64, 16], F32) for i in range(2)]
            for i_, nm in enumerate(("ssm_b_re", "ssm_b_im")):
                src = I[nm].rearrange("d (q g) p h -> g p (d q) h", g=2)
                for g2 in range(2):
                    p.dma(Br[i_].t[64 * g2:64 * g2 + 64, :, :], src[g2], writes=[Br[i_].b])
            p.op('pool', lambda e: e.memset(Bz.t[:].rearrange("p a b c d -> p (a b c d)"), 0.0), writes=[Bz.b])
            m1 = sbt(st, "s5m1", [128, 64, 16], F32)
            m2 = sbt(st, "s5m2", [128, 64, 16], F32)
            fre_b = bcast_last(fre.t[:], 16)
            fim_b = bcast_last(fim.t[:], 16)
            p.op('dve', lambda e: e.tensor_mul(m1.t[:], Br[0].t[:], fre_b), reads=[Br[0].b, fre.b], writes=[m1.b])
            p.op('dve', lambda e: e.tensor_mul(m2.t[:], Br[1].t[:], fim_b), reads=[Br[1].b, fim.b], writes=[m2.b])
            for g2 in range(2):
                p.op('dve', lambda e, g2=g2: e.tensor_sub(Bz.t[64 * g2:64 * g2 + 64, 0, :, g2, :], m1.t[64 * g2:64 * g2 + 64], m2.t[64 * g2:64 * g2 + 64]),
                     reads=[m1.b, m2.b], writes=[Bz.b])
            p.op('dve', lambda e: e.tensor_mul(m1.t[:], Br[1].t[:], fre_b), reads=[Br[1].b, fre.b], writes=[m1.b])
            p.op('dve', lambda e: e.tensor_mul(m2.t[:], Br[0].t[:], fim_b), reads=[Br[0].b, fim.b], writes=[m2.b])
            for g2 in range(2):
                p.op('dve', lambda e, g2=g2: e.tensor_add(Bz.t[64 * g2:64 * g2 + 64, 1, :, g2, :], m1.t[64 * g2:64 * g2 + 64], m2.t[64 * g2:64 * g2 + 64]),
                     reads=[m1.b, m2.b], writes=[Bz.b])
            pts = [pst(st, f"s5ptb{i}", [128, 128]) for i in range(2)]
            n_ = 0
            p.op('pool', lambda e: e.memset(Cw.t[:].rearrange("p a b c -> p (a b c)"), 0.0), writes=[Cw.b])
            Cn = [sbt(st, f"s5Cn{i}", [128, 2, 64], F32) for i in range(2)]
            for ri, nm in enumerate(("ssm_c_re", "ssm_c_im")):
                for dQ in range(16):
                    d_, Q_ = dQ // 8, dQ % 8
                    cnb = Cn[n_ % 2]
                    pp = pts[n_ % 2]
                    n_ += 1
                    src = I[nm][d_, Q_ * 4 * 2:(Q_ * 4 + 4) * 2].rearrange("g h p -> (g h) p")
                    p.dma(cnb.t[:, 0, :], src, writes=[cnb.b])
                    p.dma(cnb.t[:, 1, :], src, writes=[cnb.b])
                    p.op('pe', lambda e, pp=pp, cnb=cnb: e.transpose(pp.t[:, :], cnb.t[:, :, :].rearrange("p a b -> p (a b)"), identf.t[:, :]),
                         reads=[cnb.b, identf.b], writes=[pp.b])
                    sgn = 1.0 if ri == 0 else -1.0
                    for g2 in range(2):
                        p.op('act', lambda e, pp=pp, g2=g2, ri=ri, dQ=dQ, sgn=sgn: e.activation(
                            Cw.t[64 * g2:64 * g2 + 64, dQ * 4:(dQ + 1) * 4, ri, 16 * g2:16 * g2 + 16],
                            pp.t[64 * g2:64 * g2 + 64, :].rearrange("p (q g h) -> p q g h", q=4, g=2)[:, :, g2, :], AF.Identity, scale=sgn),
                            reads=[pp.b], writes=[Cw.b])
            p.barrier()

        if dbg is not None:
            dbg(PRM, Cw, nT)
            return

        with ExitStack() as st:
            yacc = sbt(st, "s5yacc", [128, T // 2], F32)
            Ec = sbt(st, "s5Ec", [128, 8, 64], F32)
            Es = sbt(st, "s5Es", [128, 8, 64], F32)
            wc = sbt(st, "s5wc", [128, 8], F32)
            ws = sbt(st, "s5ws", [128, 8], F32)
            wt1 = sbt(st, "s5wt1", [128, 8], F32)
            wt2 = sbt(st, "s5wt2", [128, 8], F32)
            et1 = sbt(st, "s5et1", [128, 8, 32], F32)
            et2 = sbt(st, "s5et2", [128, 8, 32], F32)
            hp = sbt(st, "s5hp", [128, 8, 2], F32)
            ini = sbt(st, "s5ini", [128, 8, 2], F32)
            itmp = sbt(st, "s5itmp", [128, 8, 2], F32)
            nsth = sbt(st, "s5nsth", [128, 64], F32)
            p.op('dve', lambda e: e.tensor_scalar(nsth.t[:], PRM4.t[:, 2, :], -1.0, None, ALU.mult), reads=[PRM4.b], writes=[nsth.b])
            hpb = [Buf() for _ in range(8)]
            CwP = sbt(st, "s5CwP", [128, LC + 1, 8, 2, 32], F32)
            ctm = [sbt(st, f"s5ctm{i}", [128, 4, 32], F32) for i in range(2)]
            BP = sbt(st, "s5BP", [128, LC, 2, 2, 4, 32], F32)
            Win = sbt(st, "s5Win", [128, 2, 2, LC, 128], BF16)
            Mw = sbt(st, "s5Mw", [128, 2, LC, 32], BF16)
            Wout = sbt(st, "s5Wout", [128, 8, LC, 2, 32], BF16)
            pts = [pst(st, f"s5ptq{i}", [128, 128]) for i in range(1)]
            pk = pst(st, "s5pk", [128, 16, 32])
            NS = 8
            Wk_ = [[sbt(st, f"s5w{j}_{i}", [128, 2, 64], F32) for i in range(3)] for j in range(NS)]
            Hb = [sbt(st, f"s5hb{j}", [128, 2, 68], BF16) for j in range(NS)]
            Esg = sbt(st, "s5Esg", [128, 8, 2, 64], F32)

            def swap2(tb_, ncn):
                b1 = tb_.t[:, 1, 0:ncn]
                apl = [list(a_) for a_ in b1.ap]
                return bass.AP(b1.tensor, b1.offset, [apl[0], [-64, 2], apl[-1]])

            def bc2(ap2):
                apl = [list(a_) for a_ in ap2.ap]
                return bass.AP(ap2.tensor, ap2.offset, [apl[0], [0, 2], apl[-1]])
            pS = [pst(st, f"s5pS{j}", [128, 512]) for j in range(4)]
            py = [pst(st, f"s5py{i}", [128, LC, 64]) for i in range(2)]
            ygb = [sbt(st, f"s5yg{i}", [128, 512], BF16) for i in range(2)]
            gtb = [sbt(st, f"s5gt{i}", [128, 512], F32) for i in range(2)]
            tn_ = 0
            blkc = 0
            yi_ = 0
            psi = 0
            for Q in range(8):
                for d_ in range(2):
                    us = slice(d_ * 32 + Q * 4, d_ * 32 + Q * 4 + 4)
                    ts_ = slice(d_ * 4, d_ * 4 + 4)
                    c0_ = Cw.t[:, us, 0, :]
                    c1_ = Cw.t[:, us, 1, :]
                    for k in range(LC + 1):
                        pr = bcast_last(PW.t[:, k, 0, us], 32)
                        pi_ = bcast_last(PW.t[:, k, 1, us], 32)
                        t1_, t2_ = ctm
                        p.op('dve', lambda e, t1_=t1_, c0_=c0_, pr=pr: e.tensor_mul(t1_.t[:], c0_, pr), reads=[Cw.b, PW.b], writes=[t1_.b])
                        p.op('pool', lambda e, t2_=t2_, c1_=c1_, pi_=pi_: e.tensor_mul(t2_.t[:], c1_, pi_), reads=[Cw.b, PW.b], writes=[t2_.b])
                        p.op('dve', lambda e, t1_=t1_, t2_=t2_, k=k, ts_=ts_: e.tensor_add(CwP.t[:, k, ts_, 0, :], t1_.t[:], t2_.t[:]), reads=[t1_.b, t2_.b], writes=[CwP.b])
                        p.op('dve', lambda e, t1_=t1_, c1_=c1_, pr=pr: e.tensor_mul(t1_.t[:], c1_, pr), reads=[Cw.b, PW.b], writes=[t1_.b])
                        p.op('pool', lambda e, t2_=t2_, c0_=c0_, pi_=pi_: e.tensor_mul(t2_.t[:], c0_, pi_), reads=[Cw.b, PW.b], writes=[t2_.b])
                        p.op('dve', lambda e, t1_=t1_, t2_=t2_, k=k, ts_=ts_: e.tensor_sub(CwP.t[:, k, ts_, 1, :], t1_.t[:], t2_.t[:]), reads=[t1_.b, t2_.b], writes=[CwP.b])
                    b0_ = Bz.t[:, 0, us, :, :].rearrange("p a b c -> p a (b c)")
                    b1_ = Bz.t[:, 1, us, :, :].rearrange("p a b c -> p a (b c)")
                    for s_ in range(LC):
                        k = LC - 1 - s_
                        pr = bcast_last(PW.t[:, k, 0, us], 32)
                        pi_ = bcast_last(PW.t[:, k, 1, us], 32)
                        t1_, t2_ = ctm
                        p.op('dve', lambda e, t1_=t1_, b0_=b0_, pr=pr: e.tensor_mul(t1_.t[:], b0_, pr), reads=[Bz.b, PW.b], writes=[t1_.b])
                        p.op('pool', lambda e, t2_=t2_, b1_=b1_, pi_=pi_: e.tensor_mul(t2_.t[:], b1_, pi_), reads=[Bz.b, PW.b], writes=[t2_.b])
                        p.op('dve', lambda e, t1_=t1_, t2_=t2_, s_=s_, d_=d_: e.tensor_sub(BP.t[:, s_, 0, d_, :, :], t1_.t[:], t2_.t[:]), reads=[t1_.b, t2_.b], writes=[BP.b])
                        p.op('dve', lambda e, t1_=t1_, b0_=b0_, pi_=pi_: e.tensor_mul(t1_.t[:], b0_, pi_), reads=[Bz.b, PW.b], writes=[t1_.b])
                        p.op('pool', lambda e, t2_=t2_, b1_=b1_, pr=pr: e.tensor_mul(t2_.t[:], b1_, pr), reads=[Bz.b, PW.b], writes=[t2_.b])
                        p.op('dve', lambda e, t1_=t1_, t2_=t2_, s_=s_, d_=d_: e.tensor_add(BP.t[:, s_, 1, d_, :, :], t1_.t[:], t2_.t[:]), reads=[t1_.b, t2_.b], writes=[BP.b])
                p.op('act', lambda e: e.activation(Wout.t[:].rearrange("p x t r h -> p t x r h"), CwP.t[:, 1:LC + 1, :, :, :], AF.Identity), reads=[CwP.b], writes=[Wout.b])
                for d_ in range(2):
                    for ri in range(2):
                        for s_ in range(LC):
                            pp = pts[0]
                            tn_ += 1
                            p.op('pe', lambda e, pp=pp, s_=s_, ri=ri, d_=d_: e.transpose(pp.t[:, :], BP.t[:, s_, ri, d_, :, :].rearrange("p a b -> p (a b)"), identf.t[:, :]),
                                 reads=[BP.b, identf.b], writes=[pp.b])
                            p.op('act', lambda e, pp=pp, s_=s_, ri=ri, d_=d_: e.activation(Win.t[:, d_, ri, s_, :], pp.t[:, :], AF.Identity), reads=[pp.b], writes=[Win.b])
                    for thf in range(LC // 4):
                        for tq in range(4):
                            tau = thf * 4 + tq
                            for ql in range(4):
                                tix = d_ * 4 + ql
                                for ri in range(2):
                                    p.op('pe', lambda e, d_=d_, tau=tau, tq=tq, ql=ql, tix=tix, ri=ri, us=slice(d_ * 32 + Q * 4, d_ * 32 + Q * 4 + 4): e.matmul(
                                        pk.t[:, tq * 4 + ql, :], Bz.t[:, ri, us, :, :].rearrange("p q a b -> p (q a b)"), CwP.t[:, tau, tix, ri, :],
                                        start=(ri == 0), stop=(ri == 1)), reads=[Bz.b, CwP.b], writes=[pk.b])
                        for ql in range(4):
                            p.op('dve', lambda e, d_=d_, ql=ql, thf=thf: e.tensor_copy(Mw.t[32 * ql:32 * ql + 32, d_, thf * 4:thf * 4 + 4, :], pk.t[32 * ql:32 * ql + 32, ql:16:4, :]),
                                 reads=[pk.b], writes=[Mw.b])
                for d_ in range(2):
                    cols = slice(d_ * 32 + Q * 4, d_ * 32 + Q * 4 + 4)
                    p.op('dve', lambda e, d_=d_, cols=cols: e.tensor_copy(wc.t[:, d_ * 4:d_ * 4 + 4], PRM4.t[:, 1, cols]), reads=[PRM4.b], writes=[wc.b])
                    p.op('dve', lambda e, d_=d_, cols=cols: e.tensor_copy(ws.t[:, d_ * 4:d_ * 4 + 4], PRM4.t[:, 2, cols]), reads=[PRM4.b], writes=[ws.b])
                p.op('pool', lambda e: e.memset(Ec.t[:, :, 0:1], 1.0), writes=[Ec.b])
                p.op('pool', lambda e: e.memset(Es.t[:, :, 0:1], 0.0), writes=[Es.b])
                n = 1
                while n < 64:
                    wcb = bcast_last(wc.t[:, :], n)
                    wsb = bcast_last(ws.t[:, :], n)
                    p.op('dve', lambda e, n=n, wcb=wcb: e.tensor_mul(et1.t[:, :, 0:n], Ec.t[:, :, 0:n], wcb), reads=[Ec.b, wc.b], writes=[et1.b])
                    p.op('pool', lambda e, n=n, wsb=wsb: e.tensor_mul(et2.t[:, :, 0:n], Es.t[:, :, 0:n], wsb), reads=[Es.b, ws.b], writes=[et2.b])
                    p.op('dve', lambda e, n=n: e.tensor_sub(Ec.t[:, :, n:2 * n], et1.t[:, :, 0:n], et2.t[:, :, 0:n]), reads=[et1.b, et2.b], writes=[Ec.b])
                    p.op('dve', lambda e, n=n, wcb=wcb: e.tensor_mul(et1.t[:, :, 0:n], Es.t[:, :, 0:n], wcb), reads=[Es.b, wc.b], writes=[et1.b])
                    p.op('pool', lambda e, n=n, wsb=wsb: e.tensor_mul(et2.t[:, :, 0:n], Ec.t[:, :, 0:n], wsb), reads=[Ec.b, ws.b], writes=[et2.b])
                    p.op('dve', lambda e, n=n: e.tensor_add(Es.t[:, :, n:2 * n], et1.t[:, :, 0:n], et2.t[:, :, 0:n]), reads=[et1.b, et2.b], writes=[Es.b])
                    p.op('dve', lambda e: e.tensor_mul(wt1.t[:], wc.t[:], wc.t[:]), reads=[wc.b], writes=[wt1.b])
                    p.op('dve', lambda e: e.tensor_mul(wt2.t[:], ws.t[:], ws.t[:]), reads=[ws.b], writes=[wt2.b])
                    p.op('dve', lambda e: e.scalar_tensor_tensor(ws.t[:], wc.t[:], 2.0, ws.t[:], ALU.mult, ALU.mult), reads=[wc.b, ws.b], writes=[ws.b])
                    p.op('dve', lambda e: e.tensor_sub(wc.t[:], wt1.t[:], wt2.t[:]), reads=[wt1.b, wt2.b], writes=[wc.b])
                    n *= 2
                p.op('dve', lambda e: e.tensor_copy(Esg.t[:, :, 0, :], Es.t[:, :, :]), reads=[Es.b], writes=[Esg.b])
                p.op('dve', lambda e: e.tensor_scalar(Esg.t[:, :, 1, :], Es.t[:, :, :], -1.0, None, ALU.mult), reads=[Es.b], writes=[Esg.b])
                blist = []
                for d_ in range(2):
                    if d_ == 0:
                        blocks = [(T, TC, False)] + [(b * 512, 512, True) for b in range(4)]
                    else:
                        blocks = [(T, TC, False)] + [(b * 512, 512, False) for b in (7, 6, 5, 4)] + [(b * 512, 512, True) for b in (3, 2, 1, 0)]
                    for bi_, (c0, n, islat) in enumerate(blocks):
                        blist.append((d_, bi_, c0, n, islat))
                bctx = {}

                def front_pe(k, Q=Q):
                    d_, bi_, c0, n, islat = blist[k]
                    gk = Q * 18 + k
                    ncn = n // LC
                    sets = [(gk % 2) * 4 + ql for ql in range(4)]
                    rhs_all = []
                    for ql in range(4):
                        base = nT.t[32 * ql:32 * ql + 32, Q, c0:c0 + n]
                        apl = [list(a_) for a_ in base.ap]
                        lst = []
                        for s_ in range(LC):
                            ap2 = [list(a_) for a_ in apl]
                            if d_ == 0:
                                ap2[-1] = [apl[-1][0] * LC, ncn]
                                lst.append(bass.AP(base.tensor, base.offset + s_ * apl[-1][0], ap2))
                            else:
                                ap2[-1] = [-apl[-1][0] * LC, ncn]
                                lst.append(bass.AP(base.tensor, base.offset + (n - 1 - s_) * apl[-1][0], ap2))
                        rhs_all.append(lst)
                    pss = [pS[ql] for ql in range(4)]
                    for ri in range(2):
                        for s_ in range(LC):
                            for ql in range(4):
                                ps_ = pss[ql]
                                p.op('pe', lambda e, ps_=ps_, ri=ri, s_=s_, ql=ql, d_=d_, ncn=ncn, r_=rhs_all[ql][s_]: e.matmul(
                                    ps_.t[:, ri * 128:ri * 128 + ncn], Win.t[32 * ql:32 * ql + 32, d_, ri, s_, :], r_, start=(s_ == 0), stop=(s_ == LC - 1), tile_position=(32 * ql, 0)),
                                    reads=[Win.b, nT.b], writes=[ps_.b])
                    bctx[k] = (ncn, sets, rhs_all)

                def front_ev(k, Q=Q):
                    d_, bi_, c0, n, islat = blist[k]
                    ncn, sets, rhs_all = bctx[k]
                    pss = [pS[ql] for ql in range(4)]
                    for ql in range(4):
                        ps_ = pss[ql]
                        X, P1, P2 = Wk_[sets[ql]]
                        p.op('act', lambda e, X=X, ps_=ps_, ncn=ncn: e.activation(X.t[:, :, 0:ncn], ps_.t[:, 0:256].rearrange("p (r c) -> p r c", r=2)[:, :, 0:ncn], AF.Identity),
                             reads=[ps_.b], writes=[X.b])
                    for ql in range(4):
                        X, P1, P2 = Wk_[sets[ql]]
                        tix = d_ * 4 + ql
                        ecb = bc2(Ec.t[:, tix, 0:ncn])
                        p.op('dve', lambda e, X=X, P1=P1, ecb=ecb, ncn=ncn: e.tensor_mul(P1.t[:, :, 0:ncn], X.t[:, :, 0:ncn], ecb), reads=[X.b, Ec.b], writes=[P1.b])
                        p.op('pool', lambda e, X=X, P2=P2, tix=tix, ncn=ncn: e.tensor_mul(P2.t[:, :, 0:ncn], swap2(X, ncn), Esg.t[:, tix, :, 0:ncn]), reads=[X.b, Esg.b], writes=[P2.b])
                    for ql in range(4):
                        X, P1, P2 = Wk_[sets[ql]]
                        p.op('dve', lambda e, P1=P1, P2=P2, ncn=ncn: e.tensor_add(P1.t[:, :, 0:ncn], P1.t[:, :, 0:ncn], P2.t[:, :, 0:ncn]), reads=[P1.b, P2.b], writes=[P1.b])

                def back(k, Q=Q):
                    d_, bi_, c0, n, islat = blist[k]
                    gk = Q * 18 + k
                    ncn, sets, rhs_all = bctx.pop(k)
                    pyt = py[gk % 2]
                    for ql in range(4):
                        tix = d_ * 4 + ql
                        u = d_ * 32 + Q * 4 + ql
                        hb_ = hpb[tix]
                        hbs = Hb[sets[ql]]
                        if bi_ > 0:
                            p.op('act', lambda e, tix=tix, u=u: e.activation(itmp.t[:, tix, 0:1], hp.t[:, tix, 1:2], AF.Identity, scale=nsth.t[:, u:u + 1]), reads=[hb_, nsth.b], writes=[hb_])
                            p.op('act', lambda e, tix=tix, u=u: e.activation(ini.t[:, tix, 0:1], hp.t[:, tix, 0:1], AF.Identity, scale=PRM4.t[:, 1, u:u + 1], bias=itmp.t[:, tix, 0:1]),
                                 reads=[hb_, PRM4.b], writes=[hb_])
                            p.op('act', lambda e, tix=tix, u=u: e.activation(itmp.t[:, tix, 1:2], hp.t[:, tix, 0:1], AF.Identity, scale=PRM4.t[:, 2, u:u + 1]), reads=[hb_, PRM4.b], writes=[hb_])
                            p.op('act', lambda e, tix=tix, u=u: e.activation(ini.t[:, tix, 1:2], hp.t[:, tix, 1:2], AF.Identity, scale=PRM4.t[:, 1, u:u + 1], bias=itmp.t[:, tix, 1:2]),
                                 reads=[hb_, PRM4.b], writes=[hb_])
                            if islat:
                                p.op('act', lambda e, hbs=hbs, tix=tix: e.activation(hbs.t[:, :, 0], hp.t[:, tix, :], AF.Identity), reads=[hb_], writes=[hbs.b])
                    for ql in range(4):
                        X, P1, P2 = Wk_[sets[ql]]
                        tix = d_ * 4 + ql
                        u = d_ * 32 + Q * 4 + ql
                        hb_ = hpb[tix]
                        rb = PRM4.t[:, 0, u:u + 1].to_broadcast([128, ncn])
                        for ri in range(2):
                            if bi_ == 0:
                                init_, irds = 0.0, []
                            else:
                                init_, irds = ini.t[:, tix, ri:ri + 1], [hb_]
                            p.op('dve', lambda e, X=X, P1=P1, rb=rb, init_=init_, ri=ri, ncn=ncn: e.tensor_tensor_scan(X.t[:, ri, 0:ncn], rb, P1.t[:, ri, 0:ncn], init_, ALU.mult, ALU.add),
                                 reads=[P1.b, PRM4.b] + irds, writes=[X.b])
                    for ql in range(4):
                        X, P1, P2 = Wk_[sets[ql]]
                        tix = d_ * 4 + ql
                        ecb = bc2(Ec.t[:, tix, 0:ncn])
                        p.op('dve', lambda e, X=X, P1=P1, ecb=ecb, ncn=ncn: e.tensor_mul(P1.t[:, :, 0:ncn], X.t[:, :, 0:ncn], ecb), reads=[X.b, Ec.b], writes=[P1.b])
                        p.op('pool', lambda e, X=X, P2=P2, tix=tix, ncn=ncn: e.tensor_mul(P2.t[:, :, 0:ncn], swap2(X, ncn), Esg.t[:, tix, :, 0:ncn]), reads=[X.b, Esg.b], writes=[P2.b])
                    for ql in range(4):
                        X, P1, P2 = Wk_[sets[ql]]
                        p.op('dve', lambda e, P1=P1, P2=P2, ncn=ncn: e.tensor_sub(P1.t[:, :, 0:ncn], P1.t[:, :, 0:ncn], P2.t[:, :, 0:ncn]), reads=[P1.b, P2.b], writes=[P1.b])
                    for ql in range(4):
                        X, P1, P2 = Wk_[sets[ql]]
                        tix = d_ * 4 + ql
                        hb_ = hpb[tix]
                        hbs = Hb[sets[ql]]
                        p.op('act', lambda e, tix=tix, P1=P1, ncn=ncn: e.activation(hp.t[:, tix, :], P1.t[:, :, ncn - 1], AF.Identity), reads=[P1.b], writes=[hb_])
                        if islat:
                            p.op('act', lambda e, hbs=hbs, P1=P1, ncn=ncn: e.activation(hbs.t[:, :, 1:ncn], P1.t[:, :, 0:ncn - 1], AF.Identity), reads=[P1.b], writes=[hbs.b])
                    if islat:
                        for t_ in range(LC):
                            for s_ in range(t_ + 1):
                                for ql in range(4):
                                    p.op('pe', lambda e, pyt=pyt, ql=ql, t_=t_, s_=s_, d_=d_, ncn=ncn, r_=rhs_all[ql][s_]: e.matmul(
                                        pyt.t[32 * ql:32 * ql + 32, t_, 0:ncn], Mw.t[32 * ql:32 * ql + 32, d_, t_ - s_, :], r_, start=(s_ == 0), stop=False,
                                        tile_position=(32 * ql, 32 * ql)), reads=[Mw.b, nT.b], writes=[pyt.b])
                            for ri in range(2):
                                for ql in range(4):
                                    tix = d_ * 4 + ql
                                    hh_ = Hb[sets[ql]]
                                    p.op('pe', lambda e, pyt=pyt, ql=ql, t_=t_, ri=ri, tix=tix, hh_=hh_, ncn=ncn: e.matmul(
                                        pyt.t[32 * ql:32 * ql + 32, t_, 0:ncn], Wout.t[:, tix, t_, ri, :], hh_.t[:, ri, 0:ncn], start=False, stop=(ri == 1),
                                        tile_position=(0, 32 * ql)), reads=[Wout.b, hh_.b], writes=[pyt.b])

                def yevac(k, Q=Q):
                    d_, bi_, c0, n, islat = blist[k]
                    if not islat:
                        return
                    gk = Q * 18 + k
                    ncn = n // LC
                    pyt = py[gk % 2]
                    if d_ == 0:
                        yv = yacc.t[:, c0:c0 + n].rearrange("p (c t) -> p t c", t=LC)
                        nv = nT.t[:, Q, c0:c0 + n].rearrange("p (c t) -> p t c", t=LC)
                        p.op('dve', lambda e, pyt=pyt, yv=yv, nv=nv, Q=Q: e.scalar_tensor_tensor(yv, nv, dT.t[:, Q:Q + 1], pyt.t[:, :, :], ALU.mult, ALU.add),
                             reads=[nT.b, dT.b, pyt.b], writes=[yacc.b])
                    else:
                        base = yacc.t[:, c0:c0 + n]
                        apl = [list(a_) for a_ in base.ap]
                        stp = apl[-1][0]
                        yv = bass.AP(base.tensor, base.offset + (n - 1) * stp, apl[:-1] + [[-stp, LC], [-stp * LC, ncn]])
                        p.op('dve', lambda e, pyt=pyt, yv=yv: e.tensor_add(yv, yv, pyt.t[:, :, :]), reads=[pyt.b, yacc.b], writes=[yacc.b])

                NB_ = len(blist)
                front_pe(0)
                front_ev(0)
                if NB_ > 1:
                    front_pe(1)
                for k in range(NB_):
                    if k + 1 < NB_:
                        front_ev(k + 1)
                    if k + 2 < NB_:
                        front_pe(k + 2)
                    back(k)
                    if k >= 1:
                        yevac(k - 1)
                yevac(NB_ - 1)
                for b in range(4):
                    yg = ygb[b % 2]
                    gt = gtb[b % 2]
                    ysl = yacc.t[:, b * 512:(b + 1) * 512]
                    p.op('dve', lambda e, gt=gt, ysl=ysl: e.tensor_mul(gt.t[:, :], ysl, ysl), reads=[yacc.b], writes=[gt.b])
                    p.op('dve', lambda e, gt=gt: e.tensor_scalar(gt.t[:, :], gt.t[:, :], 0.044715, 1.0, ALU.mult, ALU.add), reads=[gt.b], writes=[gt.b])
                    p.op('dve', lambda e, gt=gt, ysl=ysl: e.tensor_mul(gt.t[:, :], gt.t[:, :], ysl), reads=[gt.b, yacc.b], writes=[gt.b])
                    p.op('act', lambda e, gt=gt: e.activation(gt.t[:, :], gt.t[:, :], AF.Sigmoid, scale=2.0 * math.sqrt(2.0 / math.pi)), reads=[gt.b], writes=[gt.b])
                    p.op('dve', lambda e, gt=gt, ysl=ysl, yg=yg: e.tensor_mul(yg.t[:, :], gt.t[:, :], ysl), reads=[gt.b, yacc.b], writes=[yg.b])
                    p.dma(YGs[Q, :, b * 512:(b + 1) * 512], yg.t[:, :], reads=[yg.b])
            p.barrier()
    with ExitStack() as st:
        Wgl = sbt(st, "s5Wgl", [128, 8, 2 * D], BF16)
        with ExitStack() as st2:
            stg = [sbt(st2, f"s5gst{i}", [128, 2 * D], F32) for i in range(2)]
            for k in range(8):
                s_ = stg[k % 2]
                p.dma(s_.t[:, 0:D], I["ssm_glu_w"][k * 128:(k + 1) * 128, 0:D], writes=[s_.b])
                p.dma(s_.t[:, D:2 * D], I["ssm_glu_w"][k * 128:(k + 1) * 128, D:2 * D], writes=[s_.b])
                ew(lambda e, k=k, s_=s_: e.tensor_copy(Wgl.t[:, k, 0:D], s_.t[:, 0:D]), [s_.b], [Wgl.b])
                ew(lambda e, k=k, s_=s_: e.tensor_copy(Wgl.t[:, k, D:2 * D], s_.t[:, D:2 * D]), [s_.b], [Wgl.b])
            p.barrier()
        g1 = load_gate(st, "s5g1", 1, 0, 0)
        ygt = [sbt(st, f"s5ygt{i}", [128, 8, 128], BF16) for i in range(2)]
        h2 = [sbt(st, f"s5h2{i}", [128, D], F32) for i in range(2)]
        h3 = [sbt(st, f"s5h3{i}", [128, D], F32) for i in range(2)]
        sg_ = [sbt(st, f"s5sg{i}", [128, 512], F32) for i in range(2)]
        pa = [pst(st, f"s5pa{i}", [128, 512]) for i in range(2)]
        pgl = [pst(st, f"s5pg{i}", [128, 512]) for i in range(2)]
        c_ = 0
        YGs4 = YGs.rearrange("q p (r t) -> q p r t", r=2)
        for ti in range(16):
            yt = ygt[ti % 2]
            p.dma(yt.t[:], YGs[:, :, ti * 128:(ti + 1) * 128].rearrange("q p t -> p q t"), writes=[yt.b])
            hh = h2[ti % 2]
            ho_ = h3[ti % 2]
            p.dma(hh.t[:], Hloc[ti * 128:(ti + 1) * 128, :], writes=[hh.b])
            for half in range(2):
                a = pa[c_ % 2]
                g = pgl[c_ % 2]
                sg2 = sg_[c_ % 2]
                c_ += 1
                for k in range(8):
                    p.op('pe', lambda e, a=a, k=k, yt=yt, half=half: e.matmul(a.t[:, :], yt.t[:, k, :], Wgl.t[:, k, half * 512:(half + 1) * 512], start=(k == 0), stop=(k == 7)),
                         reads=[yt.b, Wgl.b], writes=[a.b])
                for k in range(8):
                    p.op('pe', lambda e, g=g, k=k, yt=yt, half=half: e.matmul(g.t[:, :], yt.t[:, k, :], Wgl.t[:, k, D + half * 512:D + (half + 1) * 512], start=(k == 0), stop=(k == 7)),
                         reads=[yt.b, Wgl.b], writes=[g.b])
                p.op('act', lambda e, g=g, sg2=sg2: e.activation(sg2.t[:, :], g.t[:, :], AF.Sigmoid), reads=[g.b], writes=[sg2.b])
                p.op('dve', lambda e, a=a, sg2=sg2: e.tensor_mul(sg2.t[:, :], sg2.t[:, :], a.t[:, :]), reads=[a.b, sg2.b], writes=[sg2.b])
                p.op('pool', lambda e, sg2=sg2, half=half: e.tensor_mul(sg2.t[:, :], sg2.t[:, :], g1.t[:, half * 512:(half + 1) * 512]), reads=[sg2.b, g1.b], writes=[sg2.b])
                p.op('pool', lambda e, sg2=sg2, hh=hh, ho_=ho_, half=half: e.tensor_add(ho_.t[:, half * 512:(half + 1) * 512], hh.t[:, half * 512:(half + 1) * 512], sg2.t[:, :]),
                     reads=[sg2.b, hh.b], writes=[ho_.b])
            p.dma(Hloc[ti * 128:(ti + 1) * 128, :], ho_.t[:], reads=[ho_.b])
        p.barrier()


_CACHE = {}


def kernel(**inputs):
    stage = int(inputs.pop("_stage", 99))
    if "nc" not in _CACHE or _CACHE.get("stage") != stage:
        _CACHE["nc"] = build(stage)
        _CACHE["stage"] = stage
        _CACHE["consts"] = host_consts()
    nc = _CACHE["nc"]
    if "consts_rev" not in _CACHE:
        _CACHE["consts_rev"] = host_consts(rev=True)
    f = lambda a: np.ascontiguousarray(np.asarray(a, dtype=np.float32))
    in_maps = []
    for core in range(8):
        b = core // 2
        rv = (core % 2 == 1) and stage >= 99
        cs = _CACHE["consts_rev"] if rv else _CACHE["consts"]
        sd = (lambda a: np.asarray(a)[0][::-1]) if rv else (lambda a: np.asarray(a)[0])
        xb = np.asarray(inputs["x"][b])
        cb = np.asarray(inputs["ctx"][b])
        if rv:
            xb = xb[::-1]
            cb = cb[::-1]
        m = {
            "x": f(xb), "c": f(inputs["c"][b]).reshape(8, 128), "ctx": f(cb),
            "c_ctx": f(inputs["c_ctx"]).reshape(8, 128),
            "mod_w": f(inputs["mod_w"]), "mod_b": f(inputs["mod_b"]).reshape(2, 48, 128),
            "norm_g": f(inputs["norm_g"]).reshape(2, 2, 8, 128),
            "ffn_w_gate": f(inputs["ffn_w_gate"]), "ffn_w_up": f(inputs["ffn_w_up"]), "ffn_w_down": f(inputs["ffn_w_down"]),
            "mix_w_in": f(inputs["mix_w_in"][0]), "mix_w_out": f(inputs["mix_w_out"][0]), "attn_sink": f(inputs["attn_sink"]).reshape(1, 8),
            "ssm_a_re": f(sd(inputs["ssm_a_re"])).reshape(128, 64), "ssm_a_im": f(sd(inputs["ssm_a_im"])).reshape(128, 64),
            "ssm_log_dt": f(sd(inputs["ssm_log_dt"])).reshape(1, 128),
            "ssm_b_re": f(sd(inputs["ssm_b_re"])), "ssm_b_im": f(sd(inputs["ssm_b_im"])),
            "ssm_c_re": f(sd(inputs["ssm_c_re"])), "ssm_c_im": f(sd(inputs["ssm_c_im"])),
            "ssm_d": f(inputs["ssm_d"][0]).reshape(8, 128), "ssm_glu_w": f(inputs["ssm_glu_w"][0]), "final_g": f(inputs["final_g"]).reshape(1, D),
        }
        m.update(cs)
        m["rk"] = np.array([[0]], np.int32)
        in_maps.append(m)
    res = run_bass_kernel_spmd(nc, in_maps, core_ids=list(range(8)))
    if stage < 99:
        return np.stack([np.asarray(res.results[2 * b]["out"], dtype=np.float32) for b in range(4)], axis=0)
    outp = np.stack([np.concatenate([np.asarray(res.results[2 * b]["out"], dtype=np.float32),
                                     np.asarray(res.results[2 * b + 1]["out"], dtype=np.float32)[::-1]], axis=0) for b in range(4)], axis=0)
    return outp
```

```python
import math
import numpy as np
import ml_dtypes
import concourse.bass as bass
import concourse.mybir as mybir
from concourse.bass_utils import run_bass_kernel_spmd
from contextlib import ExitStack

F32 = mybir.dt.float32
BF16 = mybir.dt.bfloat16
AF = mybir.ActivationFunctionType
ALU = mybir.AluOpType
NPBF = ml_dtypes.bfloat16

D = 1024
T = 4096
TC = 256
FF = 2816
NFC = 22
EPS = 1e-6


class Buf:
    def __init__(self, name=""):
        self.name = name
        self.last_w = None
        self.readers = []


class Prog:
    ENG = ['pe', 'act', 'dve', 'pool', 'sp']

    def __init__(self, nc, ndma_sems=10):
        self.nc = nc
        self.ops = {e: [] for e in self.ENG}
        self.cnt = {e: 0 for e in self.ENG}
        self.known = {e: {} for e in self.ENG}
        self.es = ExitStack()
        self.sem = {e: self.es.enter_context(nc.semaphore('s_' + e)) for e in self.ENG}
        self.dma_sems = {e: [self.es.enter_context(nc.semaphore(f'd_{e}{i}')) for i in range(ndma_sems)]
                         for e in ['sp', 'act', 'pool']}
        self.dma_val = {e: [0] * ndma_sems for e in self.dma_sems}
        self.dma_rr = {e: 0 for e in self.dma_sems}
        self.semobj = {}
        for e in self.ENG:
            self.semobj[('c', e)] = self.sem[e]
        for e in self.dma_sems:
            for i, s in enumerate(self.dma_sems[e]):
                self.semobj[('d', e, i)] = s
        self.q = 0
        self.rank_ap = None

    def _waits(self, eng, toks):
        need = {}
        for t in toks:
            if t is None:
                continue
            k, v = t
            if k == ('c', eng) and eng == 'pe':
                continue
            if self.known[eng].get(k, 0) >= v:
                continue
            if need.get(k, 0) < v:
                need[k] = v
        for k, v in need.items():
            self.known[eng][k] = v
        return list(need.items())

    def _deps(self, reads, writes):
        toks = []
        for b in reads:
            toks.append(b.last_w)
        for b in writes:
            toks.append(b.last_w)
            toks.extend(b.readers)
        return toks

    def _commit(self, tok, reads, writes):
        for b in reads:
            b.readers.append(tok)
            if len(b.readers) > 64:
                b.readers = b.readers[-64:]
        for b in writes:
            b.last_w = tok
            b.readers = []

    def op(self, eng, fn, reads=(), writes=()):
        waits = self._waits(eng, self._deps(reads, writes))
        self.cnt[eng] += 1
        tok = (('c', eng), self.cnt[eng])
        self.ops[eng].append((waits, fn, (self.sem[eng], 1)))
        self._commit(tok, reads, writes)
        return tok

    def dma(self, out, in_, reads=(), writes=(), eng=None, **kw):
        if eng is None:
            eng = ['sp', 'act', 'pool'][self.q % 2]
            self.q += 1
        toks = self._deps(reads, writes)
        i = self.dma_rr[eng]
        self.dma_rr[eng] = (i + 1) % len(self.dma_sems[eng])
        key = ('d', eng, i)
        prev = self.dma_val[eng][i]
        if prev > 0:
            toks.append((key, prev))
        waits = self._waits(eng, toks)
        self.dma_val[eng][i] = prev + 16
        tok = (key, prev + 16)
        def issue(e, out=out, in_=in_, eng=eng):
            o = out(self.dyn[eng]) if callable(out) else out
            i2 = in_(self.dyn[eng]) if callable(in_) else in_
            return e.dma_start(out=o, in_=i2, **kw)
        self.ops[eng].append((waits, issue, (self.dma_sems[eng][i], 16)))
        self._commit(tok, reads, writes)
        return tok

    def all_tokens(self):
        allt = []
        for e in self.ENG:
            if self.cnt[e]:
                allt.append((('c', e), self.cnt[e]))
        for e in self.dma_sems:
            for i, v in enumerate(self.dma_val[e]):
                if v:
                    allt.append((('d', e, i), v))
        return allt

    def barrier(self):
        allt = self.all_tokens()
        for e in self.ENG:
            w = self._waits(e, allt)
            if w:
                self.ops[e].append((w, None, None))

    def emit(self):
        nc = self.nc
        fin = self._waits('sp', self.all_tokens())
        self.ops['sp'].append((fin, None, None))
        self.dyn = {}
        with nc.Block() as block:
            def mk(eng):
                def run(e):
                    for waits, fn, inc in self.ops[eng]:
                        for k, v in waits:
                            e.wait_ge(self.semobj[k], v)
                        if fn is not None:
                            fn(e).then_inc(inc[0], inc[1])

                def body(e):
                    if eng in ('sp', 'act') and self.rank_ap is not None:
                        with e.register("rk_" + eng) as reg:
                            e.reg_load(reg, self.rank_ap)
                            self.dyn[eng] = e.snap(reg, min_val=0, max_val=2048)
                            run(e)
                    else:
                        run(e)
                return body
            block.tensor(mk('pe'))
            block.scalar(mk('act'))
            block.vector(mk('dve'))
            block.gpsimd(mk('pool'))
            block.sync(mk('sp'))
        self.es.close()


class TB:
    def __init__(self, t, name=""):
        self.t = t
        self.b = Buf(name)


def host_consts(rev=False):
    cs = {}
    cs["identf"] = np.eye(128, dtype=np.float32)
    cs["identb"] = np.eye(128, dtype=np.float32).astype(NPBF)
    t = np.arange(T)
    row = (t // 64).astype(np.float64)
    col = (t % 64).astype(np.float64)
    nf = 16
    inv = 10000.0 ** (-np.arange(nf, dtype=np.float64) / nf)
    inv = inv.astype(np.float32).astype(np.float64)
    ang = np.concatenate([(row[:, None].astype(np.float32) * inv[None].astype(np.float32)),
                          (col[:, None].astype(np.float32) * inv[None].astype(np.float32))], axis=-1).astype(np.float32)
    cosv = np.cos(ang).astype(np.float32)
    sinv = np.sin(ang).astype(np.float32)
    C = np.zeros((128, T), np.float32)
    S = np.zeros((128, T), np.float32)
    for p in range(128):
        d = p % 64
        i = d // 2
        C[p] = cosv[:, i]
        S[p] = sinv[:, i] * (-1.0 if d % 2 == 0 else 1.0)
    if rev:
        C = np.ascontiguousarray(C[:, ::-1])
        S = np.ascontiguousarray(S[:, ::-1])
    cs["ropeC"] = C
    cs["ropeS"] = S
    j = np.arange(128)[:, None]
    i = np.arange(128)[None, :]
    mp = np.where(j >= i, 0.0, -30000.0).astype(np.float32)
    mn = np.where(j <= i, 0.0, -30000.0).astype(np.float32)
    cs["maskP"] = np.tile(mp, (1, 4)).astype(NPBF)
    cs["maskN"] = np.tile(mn, (1, 4)).astype(NPBF)
    tt = np.arange(T, dtype=np.int64)
    tk = (tt[:, None] * tt[None, :]) % T
    angT = 2.0 * np.pi * tk / T
    ct = (np.cos(angT) / math.sqrt(T)).astype(np.float32)
    stt = (np.sin(angT) / math.sqrt(T)).astype(np.float32)
    if rev:
        ct = np.ascontiguousarray(ct[::-1, ::-1])
        stt = np.ascontiguousarray(stt[::-1, ::-1])
    cs["CT"] = np.ascontiguousarray(ct.reshape(32, 128, 16, 256).transpose(2, 1, 0, 3)).astype(NPBF)
    cs["ST"] = np.ascontiguousarray(stt.reshape(32, 128, 16, 256).transpose(2, 1, 0, 3)).astype(NPBF)
    t2 = np.arange(TC, dtype=np.int64)
    a2 = 2.0 * np.pi * ((t2[:, None] * t2[None, :]) % TC) / TC
    c2 = (np.cos(a2) / math.sqrt(TC)).astype(np.float32)
    s2 = (np.sin(a2) / math.sqrt(TC)).astype(np.float32)
    if rev:
        c2 = np.ascontiguousarray(c2[::-1, ::-1])
        s2 = np.ascontiguousarray(s2[::-1, ::-1])
    cs["C256"] = np.ascontiguousarray(c2.reshape(2, 128, 256).transpose(1, 0, 2)).astype(NPBF)
    cs["S256"] = np.ascontiguousarray(s2.reshape(2, 128, 256).transpose(1, 0, 2)).astype(NPBF)
    c64 = np.arange(64)
    a3 = 2.0 * np.pi * ((c64[:, None] * c64[None, :]) % 64) / 64
    cc = np.zeros((128, 128), np.float32)
    sc = np.zeros((128, 128), np.float32)
    for g in range(2):
        cc[g * 64:(g + 1) * 64, g * 64:(g + 1) * 64] = np.cos(a3) / 8.0
        sc[g * 64:(g + 1) * 64, g * 64:(g + 1) * 64] = np.sin(a3) / 8.0
    cs["Cc"] = cc.astype(NPBF)
    cs["Sc"] = sc.astype(NPBF)
    return cs


CONST_SHAPES = {
    "identf": ([128, 128], F32), "identb": ([128, 128], BF16), "ropeC": ([128, T], F32), "ropeS": ([128, T], F32),
    "maskP": ([128, 512], BF16), "maskN": ([128, 512], BF16),
    "CT": ([16, 128, 32, 256], BF16), "ST": ([16, 128, 32, 256], BF16),
    "C256": ([128, 2, 256], BF16), "S256": ([128, 2, 256], BF16), "Cc": ([128, 128], BF16), "Sc": ([128, 128], BF16),
}

IN_SHAPES = {
    "x": [T, D], "c": [8, 128], "ctx": [TC, D], "c_ctx": [8, 128],
    "mod_w": [2, D, 6 * D], "mod_b": [2, 48, 128], "norm_g": [2, 2, 8, 128],
    "ffn_w_gate": [2, D, FF], "ffn_w_up": [2, D, FF], "ffn_w_down": [2, FF, D],
    "mix_w_in": [D, 1280], "mix_w_out": [D, D], "attn_sink": [1, 8],
    "ssm_a_re": [128, 64], "ssm_a_im": [128, 64], "ssm_log_dt": [1, 128],
    "ssm_b_re": [2, 64, 64, 16], "ssm_b_im": [2, 64, 64, 16], "ssm_c_re": [2, 64, 16, 64], "ssm_c_im": [2, 64, 16, 64],
    "ssm_d": [8, 128], "ssm_glu_w": [D, 2 * D], "final_g": [1, D],
}


def build(stage=99):
    nc = bass.Bass("TRN2", target_bir_lowering=False)
    I = {n: nc.dram_tensor(n, sh, F32, kind="ExternalInput").ap() for n, sh in IN_SHAPES.items()}
    K = {n: nc.dram_tensor(n, sh, dt, kind="ExternalInput").ap() for n, (sh, dt) in CONST_SHAPES.items()}
    rk_in = nc.dram_tensor("rk", [1, 1], mybir.dt.int32, kind="ExternalInput").ap()
    HT = T // 2
    out = nc.dram_tensor("out", [T if stage < 99 else HT, D], F32, kind="ExternalOutput").ap()
    Hs = nc.dram_tensor("Hs", [T + TC, D], F32).ap()
    mod_b_flat = I["mod_b"].rearrange("l j p -> l (j p)")

    p = Prog(nc)
    p.rank_ap = None
    top = ExitStack()
    Hloc = nc.dram_tensor("Hloc", [T // 2, D], F32).ap()
    p.Hloc = Hloc

    uid = {"n": 0}

    def sbt(st, name, shape, dt):
        uid["n"] += 1
        return TB(st.enter_context(nc.sbuf_tensor(f"s{uid['n']}_{name}", shape, dt)), name)

    def pst(st, name, shape, dt=F32):
        uid["n"] += 1
        return TB(st.enter_context(nc.psum_tensor(f"p{uid['n']}_{name}", shape, dt)), name)

    identf = sbt(top, "identf", [128, 128], F32)
    identb = sbt(top, "identb", [128, 128], BF16)
    p.dma(identf.t[:], K["identf"], writes=[identf.b])
    p.dma(identb.t[:], K["identb"], writes=[identb.b])
    ones_b = sbt(top, "ones_b", [128, 128], BF16)
    p.op('pool', lambda e: e.memset(ones_b.t[:], 1.0), writes=[ones_b.b])
    AB = sbt(top, "AB", [128, 2, 2, 4, 8], F32)
    Gs = nc.dram_tensor("Gs", [8, 128, D], F32).ap()
    ATs = nc.dram_tensor("ATs", [34, 128, 4, 128], BF16).ap()

    def load_gate(st, nm, l, col, gi):
        g = sbt(st, nm, [128, D], F32)
        p.dma(g.t[:], Gs[(l * 2 + col) * 2 + gi], writes=[g.b])
        return g
    rr = {"i": 0}

    def ew(fn, reads, writes, engs=('dve', 'pool')):
        e = engs[rr["i"] % len(engs)]
        rr["i"] += 1
        return p.op(e, fn, reads=reads, writes=writes)

    with ExitStack() as st:
        gates = sbt(st, "gates", [128, 2, 2, 2, D], F32)
        crow = sbt(st, "crow", [16, 128], F32)
        p.dma(crow.t[0:8, :], I["c"], writes=[crow.b])
        p.dma(crow.t[8:16, :], I["c_ctx"], writes=[crow.b])
        pT = pst(st, "pT", [128, 96])
        scT = sbt(st, "scT", [128, 16], F32)
        p.op('pe', lambda e: e.transpose(pT.t[:, 0:16], crow.t[:, :], identf.t[0:16, 0:16]), reads=[crow.b, identf.b], writes=[pT.b])
        p.op('act', lambda e: e.activation(scT.t[:], pT.t[:, 0:16], AF.Silu), reads=[pT.b], writes=[scT.b])
        scbc = sbt(st, "scbc", [128, 16, 128], F32)
        for ck in range(16):
            ew(lambda e, ck=ck: e.tensor_copy(scbc.t[:, ck, :], scT.t[:, ck:ck + 1].to_broadcast([128, 128])), [scT.b], [scbc.b])
        mbrow = sbt(st, "mbrow", [48, 2, 128], F32)
        ngrow = sbt(st, "ngrow", [32, 128], F32)
        p.dma(mbrow.t[:, 0, :], I["mod_b"][0], writes=[mbrow.b])
        p.dma(mbrow.t[:, 1, :], I["mod_b"][1], writes=[mbrow.b])
        p.dma(ngrow.t[:], I["norm_g"].rearrange("l i k p -> (l i k) p"), writes=[ngrow.b])
        mbT = sbt(st, "mbT", [128, 2, 48], F32)
        ngT = sbt(st, "ngT", [128, 32], F32)
        for l in range(2):
            p.op('pe', lambda e, l=l: e.transpose(pT.t[:, 0:48], mbrow.t[:, l, :], identf.t[0:48, 0:48]), reads=[mbrow.b, identf.b], writes=[pT.b])
            p.op('dve', lambda e, l=l: e.tensor_copy(mbT.t[:, l, :], pT.t[:, 0:48]), reads=[pT.b], writes=[mbT.b])
        p.op('pe', lambda e: e.transpose(pT.t[:, 0:32], ngrow.t[:, :], identf.t[0:32, 0:32]), reads=[ngrow.b, identf.b], writes=[pT.b])
        p.op('dve', lambda e: e.tensor_copy(ngT.t[:], pT.t[:, 0:32]), reads=[pT.b], writes=[ngT.b])
        macc = sbt(st, "macc", [128, 2, 48, 2], F32)
        Wk = [sbt(st, f"Wk{i}", [128, 6 * D], F32) for i in range(2)]
        pg = [pst(st, f"pg{i}", [128, 512]) for i in range(2)]
        pm = pst(st, "pm", [128, 96])
        it = 0
        for l in range(2):
            for k in range(8):
                w = Wk[it % 2]
                it += 1
                for q3 in range(3):
                    p.dma(w.t[:, q3 * 2048:(q3 + 1) * 2048], I["mod_w"][l, k * 128:(k + 1) * 128, q3 * 2048:(q3 + 1) * 2048], writes=[w.b])
                for j in range(48):
                    p.op('pe', lambda e, j=j, w=w, k=k: e.matmul(pm.t[:, 2 * j:2 * j + 2], w.t[:, j * 128:(j + 1) * 128],
                                                                 scT.t[:, k:16:8], start=True, stop=True),
                         reads=[w.b, scT.b], writes=[pm.b])
                if k == 0:
                    p.op('dve', lambda e, l=l: e.tensor_copy(macc.t[:, l].rearrange("p j c -> p (j c)"), pm.t[:, :]), reads=[pm.b], writes=[macc.b])
                else:
                    p.op('dve', lambda e, l=l: e.tensor_add(macc.t[:, l].rearrange("p j c -> p (j c)"), macc.t[:, l].rearrange("p j c -> p (j c)"), pm.t[:, :]),
                         reads=[pm.b, macc.b], writes=[macc.b])
                gi = 0
                for col in range(2):
                    for g_i, which in enumerate((2, 5)):
                        for half in range(2):
                            pgt = pg[gi % 2]
                            gi += 1
                            p.op('pe', lambda e, pgt=pgt, col=col, k=k, w=w, which=which, half=half: e.matmul(
                                pgt.t[:, :], scbc.t[:, col * 8 + k, :], w.t[:, which * D + half * 512: which * D + half * 512 + 512], start=True, stop=True),
                                reads=[scbc.b, w.b], writes=[pgt.b])
                            dst = gates.t[:, l, col, g_i, half * 512:(half + 1) * 512]
                            if k == 0:
                                p.op('dve', lambda e, dst=dst, pgt=pgt: e.tensor_copy(dst, pgt.t[:, :]), reads=[pgt.b], writes=[gates.b])
                            else:
                                p.op('dve', lambda e, dst=dst, pgt=pgt: e.tensor_add(dst, dst, pgt.t[:, :]), reads=[pgt.b, gates.b], writes=[gates.b])
        gb = sbt(st, "gb", [128, D], F32)
        for l in range(2):
            for col in range(2):
                p.op('dve', lambda e, l=l, col=col: e.tensor_add(macc.t[:, l, :, col], macc.t[:, l, :, col], mbT.t[:, l, :]), reads=[macc.b, mbT.b], writes=[macc.b])
            for g_i, which in enumerate((2, 5)):
                p.dma(gb.t[:], mod_b_flat[l:l + 1, which * D:(which + 1) * D].partition_broadcast(128), writes=[gb.b])
                for col in range(2):
                    p.op('dve', lambda e, l=l, col=col, g_i=g_i: e.tensor_add(gates.t[:, l, col, g_i, :], gates.t[:, l, col, g_i, :], gb.t[:]),
                         reads=[gb.b, gates.b], writes=[gates.b])
            for col in range(2):
                for i2 in range(2):
                    sh = macc.t[:, l, (3 * i2) * 8:(3 * i2) * 8 + 8, col]
                    scl = macc.t[:, l, (3 * i2 + 1) * 8:(3 * i2 + 1) * 8 + 8, col]
                    gn = ngT.t[:, (l * 2 + i2) * 8:(l * 2 + i2) * 8 + 8]
                    p.op('dve', lambda e, l=l, col=col, i2=i2, scl=scl, gn=gn: e.scalar_tensor_tensor(
                        AB.t[:, l, col, 2 * i2, :], scl, 1.0, gn, ALU.add, ALU.mult), reads=[macc.b, ngT.b], writes=[AB.b])
                    p.op('dve', lambda e, l=l, col=col, i2=i2, sh=sh: e.tensor_copy(AB.t[:, l, col, 2 * i2 + 1, :], sh), reads=[macc.b], writes=[AB.b])
        for l in range(2):
            for col in range(2):
                for g_i in range(2):
                    p.dma(Gs[(l * 2 + col) * 2 + g_i], gates.t[:, l, col, g_i, :], reads=[gates.b])
        p.barrier()

    if stage == 0:
        with ExitStack() as st:
            g0 = load_gate(st, "g0dbg", 0, 0, 0)
            p.dma(out[0:128, :], g0.t[:], reads=[g0.b])
        p.dma(out[128:256, 0:128], AB.t[:].rearrange("p a b c d -> p (a b c d)"), reads=[AB.b])
        p.emit()
        top.close()
        return nc

    def make_norm(st, nm, nxt=2):
        ctxn = {}
        ctxn["xt"] = [sbt(st, f"{nm}xt{i}", [128, D], F32) for i in range(nxt)]
        ctxn["junk"] = sbt(st, f"{nm}junk", [128, D], BF16)
        ctxn["xn"] = [sbt(st, f"{nm}xn{i}", [128, D], BF16) for i in range(2)]
        ctxn["ss"] = [sbt(st, f"{nm}ss{i}", [128, 1], F32) for i in range(3)]
        ctxn["tp"] = [pst(st, f"{nm}tp{i}", [128, 8, 128], BF16) for i in range(2)]
        ctxn["n"] = 0
        return ctxn

    def norm_tile(cn, src, l, col, which, dst_fn, dst_buf, xt_fixed=None, ident=None):
        n = cn["n"]
        cn["n"] += 1
        xt = xt_fixed if xt_fixed is not None else cn["xt"][n % len(cn["xt"])]
        ss = cn["ss"][n % 3]
        xn = cn["xn"][n % 2]
        tp = cn["tp"][n % 2]
        junk = cn["junk"]
        idt = ident if ident is not None else identb
        p.dma(xt.t[:], src, writes=[xt.b])
        p.op('act', lambda e: e.activation(junk.t[:], xt.t[:], AF.Square, accum_out=ss.t[:]), reads=[xt.b], writes=[junk.b, ss.b])
        p.op('dve', lambda e: e.tensor_scalar(ss.t[:], ss.t[:], 1.0 / D, EPS, ALU.mult, ALU.add), reads=[ss.b], writes=[ss.b])
        p.op('act', lambda e: e.activation(ss.t[:], ss.t[:], AF.Sqrt), reads=[ss.b], writes=[ss.b])
        p.op('dve', lambda e: e.reciprocal(ss.t[:], ss.t[:]), reads=[ss.b], writes=[ss.b])
        p.op('dve', lambda e: e.tensor_scalar(xn.t[:], xt.t[:], ss.t[:, 0:1], None, ALU.mult), reads=[xt.b, ss.b], writes=[xn.b])
        for k in range(8):
            p.op('pe', lambda e, k=k: e.transpose(tp.t[:, k, :], xn.t[:, k * 128:(k + 1) * 128], idt.t[:]), reads=[xn.b, idt.b], writes=[tp.b])
        for k in range(8):
            p.op('act', lambda e, k=k: e.activation(dst_fn(k), tp.t[:, k, :], AF.Identity,
                                                    scale=AB.t[:, l, col, 2 * which, k:k + 1], bias=AB.t[:, l, col, 2 * which + 1, k:k + 1]),
                 reads=[tp.b, AB.b], writes=[dst_buf])
        return xt, ss

    def tile_src(ti):
        return I["x"][ti * 128:(ti + 1) * 128, :] if ti < 32 else I["ctx"][(ti - 32) * 128:(ti - 31) * 128, :]

    NT = 34
    L0 = ExitStack()
    Fm = sbt(L0, "Fm", [128, NT, 512], BF16)
    with ExitStack() as stBC:
        QT = sbt(stBC, "QT", [128, 4, NT * 128], BF16)
        KT = sbt(stBC, "KT", [128, NT * 128], BF16)
        Vm = sbt(stBC, "Vm", [128, NT, 128], BF16)
        with ExitStack() as st:
            wb = sbt(st, "wb", [128, 8, 1920], BF16)
            wst = [sbt(st, f"wst{i}", [128, 1280], F32) for i in range(1)]
            for k in range(8):
                s_ = wst[0]
                p.dma(s_.t[:], I["mix_w_in"][k * 128:(k + 1) * 128, :], writes=[s_.b])
                S = s_.t
                W = wb.t
                ew(lambda e, k=k, S=S, W=W: e.tensor_copy(W[:, k, 0:512], S[:, 0:512]), [s_.b], [wb.b])
                ew(lambda e, k=k, S=S, W=W: e.tensor_copy(W[:, k, 512:640], S[:, 1152:1280]), [s_.b], [wb.b])
                ew(lambda e, k=k, S=S, W=W: e.tensor_copy(W[:, k, 640:1152].rearrange("p (j h d) -> p j h d", j=4, h=2),
                                                          S[:, 512:1024].rearrange("p (h j d) -> p j h d", h=2, j=4)), [s_.b], [wb.b])
                ew(lambda e, k=k, S=S, W=W: e.tensor_copy(W[:, k, 1152:1280], S[:, 1024:1152]), [s_.b], [wb.b])
                for two in range(2):
                    ew(lambda e, k=k, S=S, W=W, two=two: e.tensor_copy(
                        W[:, k, 1280:1792].rearrange("p (j h i t) -> p j h i t", j=4, h=2, t=2)[:, :, :, :, two],
                        S[:, 512:1024].rearrange("p (h j i t) -> p j h i t", h=2, j=4, t=2)[:, :, :, :, 1 - two]), [s_.b], [wb.b])
                    ew(lambda e, k=k, S=S, W=W, two=two: e.tensor_copy(
                        W[:, k, 1792:1920].rearrange("p (i t) -> p i t", t=2)[:, :, two],
                        S[:, 1024:1152].rearrange("p (i t) -> p i t", t=2)[:, :, 1 - two]), [s_.b], [wb.b])
            ropeCb = [sbt(st, f"ropeC{i}", [128, 512], F32) for i in range(2)]
            ropeSb = [sbt(st, f"ropeS{i}", [128, 512], F32) for i in range(2)]
            cn = make_norm(st, "b")
            nTb = [sbt(st, f"nTb{i}", [128, 8, 512], BF16) for i in range(2)]
            pf = pst(st, "pf", [128, 512])
            pv = pst(st, "pv", [128, 128])
            pq = [pst(st, f"pq{i}", [128, 512]) for i in range(2)]
            pqp = [pst(st, f"pqp{i}", [128, 512]) for i in range(2)]
            t1 = [sbt(st, f"t1{i}", [128, 512], F32) for i in range(2)]
            t2 = [sbt(st, f"t2{i}", [128, 512], F32) for i in range(2)]
            qi_ = 0
            for blk in range(9):
                ntile = 4 if blk < 8 else 2
                ncol = ntile * 128
                nT = nTb[blk % 2]
                col = 0 if blk < 8 else 1
                for tt in range(ntile):
                    ti = blk * 4 + tt
                    norm_tile(cn, tile_src(ti), 0, col, 0, lambda k, tt=tt, nT=nT: nT.t[:, k, tt * 128:(tt + 1) * 128], nT.b)
                    for k in range(8):
                        p.op('pe', lambda e, k=k, tt=tt, nT=nT: e.matmul(pf.t[:, :], nT.t[:, k, tt * 128:(tt + 1) * 128], wb.t[:, k, 0:512],
                                                                      start=(k == 0), stop=(k == 7)), reads=[nT.b, wb.b], writes=[pf.b])
                    p.op('act', lambda e, ti=ti: e.activation(Fm.t[:, ti, :], pf.t[:, :], AF.Identity), reads=[pf.b], writes=[Fm.b])
                    for k in range(8):
                        p.op('pe', lambda e, k=k, tt=tt, nT=nT: e.matmul(pv.t[:, :], nT.t[:, k, tt * 128:(tt + 1) * 128], wb.t[:, k, 512:640],
                                                                      start=(k == 0), stop=(k == 7)), reads=[nT.b, wb.b], writes=[pv.b])
                    p.op('dve', lambda e, ti=ti: e.tensor_copy(Vm.t[:, ti, :], pv.t[:, :]), reads=[pv.b], writes=[Vm.b])
                c0 = blk * 512
                ropeC = ropeCb[blk % 2]
                ropeS = ropeSb[blk % 2]
                if blk < 8:
                    p.dma(ropeC.t[:], K["ropeC"][:, c0:c0 + 512], writes=[ropeC.b])
                    p.dma(ropeS.t[:], K["ropeS"][:, c0:c0 + 512], writes=[ropeS.b])
                for oc in range(5):
                    a = pq[qi_ % 2]
                    bq = pqp[qi_ % 2]
                    u1 = t1[qi_ % 2]
                    u2 = t2[qi_ % 2]
                    qi_ += 1
                    for k in range(8):
                        p.op('pe', lambda e, k=k, oc=oc, a=a, nT=nT, ncol=ncol: e.matmul(a.t[:, 0:ncol], wb.t[:, k, 640 + 128 * oc: 768 + 128 * oc], nT.t[:, k, 0:ncol],
                                                                                   start=(k == 0), stop=(k == 7)), reads=[nT.b, wb.b], writes=[a.b])
                    dstb = QT.b if oc < 4 else KT.b
                    dst = QT.t[:, oc, c0:c0 + ncol] if oc < 4 else KT.t[:, c0:c0 + ncol]
                    if blk < 8:
                        for k in range(8):
                            p.op('pe', lambda e, k=k, oc=oc, bq=bq, nT=nT, ncol=ncol: e.matmul(bq.t[:, 0:ncol], wb.t[:, k, 1280 + 128 * oc: 1408 + 128 * oc], nT.t[:, k, 0:ncol],
                                                                                        start=(k == 0), stop=(k == 7)), reads=[nT.b, wb.b], writes=[bq.b])
                        p.op('dve', lambda e, a=a, u1=u1, ropeC=ropeC: e.tensor_mul(u1.t[:, :], a.t[:, :], ropeC.t[:, :]), reads=[a.b, ropeC.b], writes=[u1.b])
                        p.op('dve', lambda e, bq=bq, u2=u2, ropeS=ropeS: e.tensor_mul(u2.t[:, :], bq.t[:, :], ropeS.t[:, :]), reads=[bq.b, ropeS.b], writes=[u2.b])
                        p.op('pool', lambda e, dst=dst, u1=u1, u2=u2: e.tensor_add(dst, u1.t[:, :], u2.t[:, :]), reads=[u1.b, u2.b], writes=[dstb])
                    else:
                        p.op('dve', lambda e, dst=dst, a=a, ncol=ncol: e.tensor_copy(dst, a.t[:, 0:ncol]), reads=[a.b], writes=[dstb])
            p.barrier()
        with ExitStack() as st:
            maskP = sbt(st, "maskP", [128, 512], BF16)
            maskN = sbt(st, "maskN", [128, 512], BF16)
            p.dma(maskP.t[:], K["maskP"], writes=[maskP.b])
            p.dma(maskN.t[:], K["maskN"], writes=[maskN.b])
            sk = sbt(st, "sk", [128, 8], F32)
            p.dma(sk.t[:], I["attn_sink"].partition_broadcast(128), writes=[sk.b])
            p.op('act', lambda e: e.activation(sk.t[:], sk.t[:], AF.Exp), reads=[sk.b], writes=[sk.b])
            skf = sbt(st, "skf", [128, 512], F32)
            for g in range(2):
                for hh in range(4):
                    p.op('dve', lambda e, g=g, hh=hh: e.tensor_copy(skf.t[64 * g:64 * g + 64, hh * 128:(hh + 1) * 128],
                                                                  sk.t[64 * g:64 * g + 64, 4 * g + hh:4 * g + hh + 1].to_broadcast([64, 128])), reads=[sk.b], writes=[skf.b])
            pS = [pst(st, f"pS{i}", [128, 512]) for i in range(3)]
            pO = [pst(st, f"pO{i}", [128, 512]) for i in range(2)]
            pZ = [pst(st, f"pZ{i}", [128, 512]) for i in range(2)]
            PT = [sbt(st, f"PT{i}", [128, 512], BF16) for i in range(3)]
            den = [sbt(st, f"den{i}", [128, 512], F32) for i in range(2)]
            ATt = [sbt(st, f"ATt{i}", [128, 4, 128], BF16) for i in range(2)]
            si = 0
            for qi in range(NT):
                AT = ATt[qi % 2]
                for g in range(2):
                    lo, hi = 64 * g, 64 * g + 64
                    keys = [(32, None), (33, None)]
                    if qi < 32:
                        if qi > 0:
                            keys.append((qi - 1, maskP))
                        keys.append((qi, None))
                        if qi < 31:
                            keys.append((qi + 1, maskN))
                    O = pO[g]
                    Z = pZ[g]
                    SP = []
                    for ki in range(len(keys)):
                        SP.append((pS[si % 3], PT[si % 3]))
                        si += 1

                    def emitS(ki):
                        kt, msk = keys[ki]
                        S_ = SP[ki][0]
                        p.op('pe', lambda e, S_=S_, kt=kt, qi=qi, lo=lo, hi=hi, msk=msk: e.matmul(
                            S_.t[:, :].rearrange("p (j q) -> p j q", j=4), KT.t[lo:hi, kt * 128:(kt + 1) * 128], QT.t[lo:hi, :, qi * 128:(qi + 1) * 128],
                            start=True, stop=(msk is None)), reads=[KT.b, QT.b], writes=[S_.b])
                        if msk is not None:
                            p.op('pe', lambda e, S_=S_, msk=msk: e.matmul(S_.t[:, :], identb.t[:, :], msk.t[:, :], start=False, stop=True),
                                 reads=[identb.b, msk.b], writes=[S_.b])
                    emitS(0)
                    if len(keys) > 1:
                        emitS(1)
                    for ki, (kt, msk) in enumerate(keys):
                        S_, P_ = SP[ki]
                        p.op('act', lambda e, S_=S_, P_=P_: e.activation(P_.t[:, :], S_.t[:, :], AF.Exp, scale=0.125), reads=[S_.b], writes=[P_.b])
                        if ki + 2 < len(keys):
                            emitS(ki + 2)
                        first = ki == 0
                        last = ki == len(keys) - 1
                        p.op('pe', lambda e, O=O, kt=kt, lo=lo, hi=hi, P_=P_, first=first, last=last: e.matmul(
                            O.t[lo:hi, :], Vm.t[:, kt, lo:hi], P_.t[:, :], start=first, stop=last), reads=[Vm.b, P_.b], writes=[O.b])
                        p.op('pe', lambda e, Z=Z, lo=lo, hi=hi, P_=P_, first=first, last=last: e.matmul(
                            Z.t[lo:hi, :], ones_b.t[:, 0:64], P_.t[:, :], start=first, stop=last), reads=[ones_b.b, P_.b], writes=[Z.b])
                    dn = den[g]
                    p.op('dve', lambda e, dn=dn, Z=Z, lo=lo, hi=hi: e.tensor_add(dn.t[lo:hi, :], Z.t[lo:hi, :], skf.t[lo:hi, :]), reads=[Z.b, skf.b], writes=[dn.b])
                    p.op('act', lambda e, dn=dn, lo=lo, hi=hi: e.activation(dn.t[lo:hi, :], dn.t[lo:hi, :], AF.Ln), reads=[dn.b], writes=[dn.b])
                    p.op('act', lambda e, dn=dn, lo=lo, hi=hi: e.activation(dn.t[lo:hi, :], dn.t[lo:hi, :], AF.Exp, scale=-1.0), reads=[dn.b], writes=[dn.b])
                    p.op('dve', lambda e, dn=dn, O=O, lo=lo, hi=hi, AT=AT: e.tensor_mul(
                        AT.t[lo:hi, :, :], O.t[lo:hi, :].rearrange("p (j q) -> p j q", j=4),
                        dn.t[lo:hi, :].rearrange("p (j q) -> p j q", j=4)), reads=[O.b, dn.b], writes=[AT.b])
                p.dma(ATs[qi], AT.t[:], reads=[AT.b])
            p.barrier()

    with ExitStack() as st:
        wob = sbt(st, "wob", [128, 8, D], BF16)
        wst = [sbt(st, f"wost{i}", [128, D], F32) for i in range(2)]
        for j in range(8):
            s_ = wst[j % 2]
            if j < 4:
                p.dma(s_.t[:], I["mix_w_out"][j * 128:(j + 1) * 128, :], writes=[s_.b])
            else:
                jj = j - 4
                p.dma(s_.t[0:64, :], I["mix_w_out"][512 + 64 * jj:512 + 64 * jj + 64, :], writes=[s_.b])
                p.dma(s_.t[64:128, :], I["mix_w_out"][512 + 64 * (4 + jj):512 + 64 * (4 + jj) + 64, :], writes=[s_.b])
            ew(lambda e, j=j, s_=s_: e.tensor_copy(wob.t[:, j, :], s_.t[:]), [s_.b], [wob.b])
        Cc = sbt(st, "Cc", [128, 128], BF16)
        Sc = sbt(st, "Sc", [128, 128], BF16)
        p.dma(Cc.t[:], K["Cc"], writes=[Cc.b])
        p.dma(Sc.t[:], K["Sc"], writes=[Sc.b])
        CTb = [sbt(st, f"CTb{i}", [128, 32, 256], BF16) for i in range(2)]
        STb = [sbt(st, f"STb{i}", [128, 32, 256], BF16) for i in range(2)]
        pP = [pst(st, f"pP{i}", [128, 256]) for i in range(2)]
        pQ = [pst(st, f"pQ{i}", [128, 256]) for i in range(2)]
        pY = pst(st, "pY", [128, 256])
        pOo = [pst(st, f"pOo{i}", [128, 512]) for i in range(2)]
        Pb = [sbt(st, f"Pb{i}", [128, 256], BF16) for i in range(2)]
        Qb = [sbt(st, f"Qb{i}", [128, 256], BF16) for i in range(2)]
        YT = [sbt(st, f"YT{i}", [128, 4, 256], BF16) for i in range(2)]
        xres = [sbt(st, f"xres{i}", [128, D], F32) for i in range(2)]
        tmpo = [sbt(st, f"tmpo{i}", [128, 512], F32) for i in range(2)]
        hn = [sbt(st, f"hn{i}", [128, D], F32) for i in range(2)]
        ATl = [sbt(st, f"ATl{i}", [128, 4, 128], BF16) for i in range(2)]
        g1l = [load_gate(st, "g1lat", 0, 0, 0), load_gate(st, "g1ctx", 0, 1, 0)]
        cnt_ = 0
        oi = 0
        for kb in range(17):
            lat = kb < 16
            if lat:
                cb = CTb[kb % 2]
                sb_ = STb[kb % 2]
                for h4 in range(4):
                    p.dma(cb.t[:, h4 * 8:(h4 + 1) * 8, :], K["CT"][kb, :, h4 * 8:(h4 + 1) * 8, :], writes=[cb.b])
                    p.dma(sb_.t[:, h4 * 8:(h4 + 1) * 8, :], K["ST"][kb, :, h4 * 8:(h4 + 1) * 8, :], writes=[sb_.b])
                nti = 32
                t0 = 0
            else:
                cb = CTb[kb % 2]
                sb_ = STb[kb % 2]
                p.dma(cb.t[:, 0:2, :], K["C256"], writes=[cb.b])
                p.dma(sb_.t[:, 0:2, :], K["S256"], writes=[sb_.b])
                nti = 2
                t0 = 32
            Y = YT[kb % 2]
            for cc in range(4):
                a = pP[cnt_ % 2]
                b = pQ[cnt_ % 2]
                ab = Pb[cnt_ % 2]
                bb = Qb[cnt_ % 2]
                cnt_ += 1
                for i in range(nti):
                    p.op('pe', lambda e, a=a, i=i, cc=cc, cb=cb, t0=t0, nti=nti: e.matmul(a.t[:, :], Fm.t[:, t0 + i, cc * 128:(cc + 1) * 128], cb.t[:, i, :],
                                                                                     start=(i == 0), stop=(i == nti - 1)), reads=[Fm.b, cb.b], writes=[a.b])
                for i in range(nti):
                    p.op('pe', lambda e, b=b, i=i, cc=cc, sb_=sb_, t0=t0, nti=nti: e.matmul(b.t[:, :], Fm.t[:, t0 + i, cc * 128:(cc + 1) * 128], sb_.t[:, i, :],
                                                                                      start=(i == 0), stop=(i == nti - 1)), reads=[Fm.b, sb_.b], writes=[b.b])
                p.op('act', lambda e, a=a, ab=ab: e.activation(ab.t[:, :], a.t[:, :], AF.Identity), reads=[a.b], writes=[ab.b])
                p.op('act', lambda e, b=b, bb=bb: e.activation(bb.t[:, :], b.t[:, :], AF.Identity, scale=-1.0), reads=[b.b], writes=[bb.b])
                p.op('pe', lambda e, ab=ab: e.matmul(pY.t[:, :], Cc.t[:, :], ab.t[:, :], start=True, stop=False), reads=[Cc.b, ab.b], writes=[pY.b])
                p.op('pe', lambda e, bb=bb: e.matmul(pY.t[:, :], Sc.t[:, :], bb.t[:, :], start=False, stop=True), reads=[Sc.b, bb.b], writes=[pY.b])
                p.op('dve', lambda e, Y=Y, cc=cc: e.tensor_copy(Y.t[:, cc, :], pY.t[:, :]), reads=[pY.b], writes=[Y.b])
            for tt in range(2):
                ti = (kb * 2 + tt) if lat else 32 + tt
                col = 0 if lat else 1
                xr = xres[ti % 2]
                h_ = hn[ti % 2]
                p.dma(xr.t[:], tile_src(ti), writes=[xr.b])
                AT = ATl[ti % 2]
                p.dma(AT.t[:], ATs[ti], writes=[AT.b])
                for half in range(2):
                    po = pOo[oi % 2]
                    tm = tmpo[oi % 2]
                    oi += 1
                    for j in range(8):
                        lhs = Y.t[:, j, tt * 128:(tt + 1) * 128] if j < 4 else AT.t[:, j - 4, :]
                        rb = Y.b if j < 4 else AT.b
                        p.op('pe', lambda e, po=po, lhs=lhs, j=j, half=half: e.matmul(po.t[:, :], lhs, wob.t[:, j, half * 512:(half + 1) * 512],
                                                                                 start=(j == 0), stop=(j == 7)), reads=[rb, wob.b], writes=[po.b])
                    p.op('dve', lambda e, po=po, tm=tm, half=half, col=col: e.tensor_mul(tm.t[:, :], po.t[:, :], g1l[col].t[:, half * 512:(half + 1) * 512]),
                         reads=[po.b, g1l[col].b], writes=[tm.b])
                    p.op('pool', lambda e, tm=tm, xr=xr, h_=h_, half=half: e.tensor_add(h_.t[:, half * 512:(half + 1) * 512], xr.t[:, half * 512:(half + 1) * 512], tm.t[:, :]),
                         reads=[tm.b, xr.b], writes=[h_.b])
                p.dma(Hs[ti * 128:(ti + 1) * 128, :], h_.t[:], reads=[h_.b])
        p.barrier()
    L0.close()

    if stage == 1:
        for ti in range(32):
            pass
        with ExitStack() as st:
            bb_ = [sbt(st, f"bo{i}", [128, D], F32) for i in range(2)]
            for ti in range(32):
                b_ = bb_[ti % 2]
                p.dma(b_.t[:], Hs[ti * 128:(ti + 1) * 128, :], writes=[b_.b])
                p.dma(out[ti * 128:(ti + 1) * 128, :], b_.t[:], reads=[b_.b])
        p.emit()
        top.close()
        return nc

    def ffn(l, tiles, final):
        with ExitStack() as st:
            Wg = sbt(st, "Wg", [128, 8, FF], BF16)
            Wu = sbt(st, "Wu", [128, 8, FF], BF16)
            Wd = sbt(st, "Wd", [128, NFC, D], BF16)
            with ExitStack() as st2:
                stg = [sbt(st2, f"fst{i}", [128, FF], F32) for i in range(2)]
                si_ = 0
                for nm, Wt in (("ffn_w_gate", Wg), ("ffn_w_up", Wu)):
                    for k in range(8):
                        s_ = stg[si_ % 2]
                        si_ += 1
                        p.dma(s_.t[:, 0:1408], I[nm][l, k * 128:(k + 1) * 128, 0:1408], writes=[s_.b])
                        p.dma(s_.t[:, 1408:FF], I[nm][l, k * 128:(k + 1) * 128, 1408:FF], writes=[s_.b])
                        ew(lambda e, Wt=Wt, k=k, s_=s_: e.tensor_copy(Wt.t[:, k, 0:1408], s_.t[:, 0:1408]), [s_.b], [Wt.b])
                        ew(lambda e, Wt=Wt, k=k, s_=s_: e.tensor_copy(Wt.t[:, k, 1408:FF], s_.t[:, 1408:FF]), [s_.b], [Wt.b])
                for fc in range(0, NFC, 2):
                    s_ = stg[si_ % 2]
                    si_ += 1
                    for q2 in range(2):
                        p.dma(s_.t[:, q2 * D:(q2 + 1) * D], I["ffn_w_down"][l, (fc + q2) * 128:(fc + q2 + 1) * 128, :], writes=[s_.b])
                    ew(lambda e, fc=fc, s_=s_: e.tensor_copy(Wd.t[:, fc:fc + 2, :].rearrange("p a d -> p (a d)"), s_.t[:, 0:2 * D]), [s_.b], [Wd.b])
                p.barrier()
            cn = make_norm(st, "f", nxt=0)
            g2l = [load_gate(st, "g2lat", l, 0, 1)] + ([load_gate(st, "g2ctx", l, 1, 1)] if not final else [])
            hx = [sbt(st, f"hx{i}", [128, D], F32) for i in range(3)]
            nTf = [sbt(st, f"nTf{i}", [128, 8, 256], BF16) for i in range(2)]
            actT = [sbt(st, f"actT{i}", [128, NFC, 256], BF16) for i in range(1)]
            sg = [sbt(st, f"sg{i}", [128, 256], F32) for i in range(2)]
            pgt = [pst(st, f"fpg{i}", [128, 256]) for i in range(2)]
            put = [pst(st, f"fpu{i}", [128, 256]) for i in range(2)]
            pdn = [pst(st, f"fpd{i}", [128, 512]) for i in range(2)]
            tm_ = [sbt(st, f"ftm{i}", [128, 512], F32) for i in range(2)]
            ho = [sbt(st, f"ho{i}", [128, D], F32) for i in range(2)]
            fss = [sbt(st, f"fss{i}", [128, 1], F32) for i in range(2)]
            fj = cn["junk"]
            fingt = None
            if final:
                fingt = sbt(st, "fing", [128, D], F32)
                p.dma(fingt.t[:], I["final_g"].partition_broadcast(128), writes=[fingt.b])
            gi_ = 0
            di_ = 0
            hi_ = 0
            for b0 in range(0, len(tiles), 2):
                tl = tiles[b0:b0 + 2]
                nT = nTf[(b0 // 2) % 2]
                aT = actT[0]
                xts = []
                for tt, ti in enumerate(tl):
                    col = 0 if ti < 32 else 1
                    hxt = hx[hi_ % 3]
                    hi_ += 1
                    src_ = Hs[ti * 128:(ti + 1) * 128, :]
                    norm_tile(cn, src_, l, col, 1, lambda k, tt=tt, nT=nT: nT.t[:, k, tt * 128:(tt + 1) * 128], nT.b, xt_fixed=hxt)
                    xts.append(hxt)
                for fc in range(NFC):
                    a = pgt[gi_ % 2]
                    b = put[gi_ % 2]
                    s2 = sg[gi_ % 2]
                    gi_ += 1
                    for k in range(8):
                        p.op('pe', lambda e, a=a, k=k, fc=fc, nT=nT: e.matmul(a.t[:, :], Wg.t[:, k, fc * 128:(fc + 1) * 128], nT.t[:, k, :], start=(k == 0), stop=(k == 7)),
                             reads=[Wg.b, nT.b], writes=[a.b])
                    for k in range(8):
                        p.op('pe', lambda e, b=b, k=k, fc=fc, nT=nT: e.matmul(b.t[:, :], Wu.t[:, k, fc * 128:(fc + 1) * 128], nT.t[:, k, :], start=(k == 0), stop=(k == 7)),
                             reads=[Wu.b, nT.b], writes=[b.b])
                    p.op('act', lambda e, a=a, s2=s2: e.activation(s2.t[:, :], a.t[:, :], AF.Silu), reads=[a.b], writes=[s2.b])
                    p.op('dve', lambda e, b=b, s2=s2, aT=aT, fc=fc: e.tensor_mul(aT.t[:, fc, :], s2.t[:, :], b.t[:, :]), reads=[b.b, s2.b], writes=[aT.b])
                for tt, ti in enumerate(tl):
                    col = 0 if ti < 32 else 1
                    hxt = xts[tt]
                    h_ = ho[ti % 2]
                    for half in range(2):
                        po = pdn[di_ % 2]
                        tm = tm_[di_ % 2]
                        di_ += 1
                        for fc in range(NFC):
                            p.op('pe', lambda e, po=po, fc=fc, tt=tt, half=half, aT=aT: e.matmul(po.t[:, :], aT.t[:, fc, tt * 128:(tt + 1) * 128], Wd.t[:, fc, half * 512:(half + 1) * 512],
                                                                                         start=(fc == 0), stop=(fc == NFC - 1)), reads=[aT.b, Wd.b], writes=[po.b])
                        p.op('dve', lambda e, po=po, tm=tm, half=half, col=col: e.tensor_mul(tm.t[:, :], po.t[:, :], g2l[col].t[:, half * 512:(half + 1) * 512]),
                             reads=[po.b, g2l[col].b], writes=[tm.b])
                        p.op('pool', lambda e, tm=tm, hxt=hxt, h_=h_, half=half: e.tensor_add(h_.t[:, half * 512:(half + 1) * 512], hxt.t[:, half * 512:(half + 1) * 512], tm.t[:, :]),
                             reads=[tm.b, hxt.b], writes=[h_.b])
                    if not final:
                        p.dma(Hs[ti * 128:(ti + 1) * 128, :], h_.t[:], reads=[h_.b])
                    else:
                        ss = fss[ti % 2]
                        p.op('act', lambda e, h_=h_, ss=ss: e.activation(fj.t[:], h_.t[:], AF.Square, accum_out=ss.t[:]), reads=[h_.b], writes=[fj.b, ss.b])
                        p.op('dve', lambda e, ss=ss: e.tensor_scalar(ss.t[:], ss.t[:], 1.0 / D, EPS, ALU.mult, ALU.add), reads=[ss.b], writes=[ss.b])
                        p.op('act', lambda e, ss=ss: e.activation(ss.t[:], ss.t[:], AF.Sqrt), reads=[ss.b], writes=[ss.b])
                        p.op('dve', lambda e, ss=ss: e.reciprocal(ss.t[:], ss.t[:]), reads=[ss.b], writes=[ss.b])
                        p.op('dve', lambda e, h_=h_, ss=ss: e.scalar_tensor_tensor(h_.t[:], h_.t[:], ss.t[:, 0:1], fingt.t[:], ALU.mult, ALU.mult),
                             reads=[h_.b, ss.b, fingt.b], writes=[h_.b])
                        p.dma(out[ti * 128:(ti + 1) * 128, :], h_.t[:], reads=[h_.b])
            p.barrier()

    ffn(0, list(range(34)), False)

    if stage == 2:
        with ExitStack() as st:
            bb_ = [sbt(st, f"bo{i}", [128, D], F32) for i in range(2)]
            for ti in range(32):
                b_ = bb_[ti % 2]
                p.dma(b_.t[:], Hs[ti * 128:(ti + 1) * 128, :], writes=[b_.b])
                p.dma(out[ti * 128:(ti + 1) * 128, :], b_.t[:], reads=[b_.b])
        p.emit()
        top.close()
        return nc

    s5_layer(nc, p, I, K, Hs, AB, load_gate, identf, identb, sbt, pst, ew, make_norm, norm_tile)
    if stage == 3:
        with ExitStack() as st:
            bb_ = [sbt(st, f"bo3{i}", [128, D], F32) for i in range(2)]
            for ti in range(32):
                b_ = bb_[ti % 2]
                p.dma(b_.t[:], Hs[ti * 128:(ti + 1) * 128, :], writes=[b_.b])
                p.dma(out[ti * 128:(ti + 1) * 128, :], b_.t[:], reads=[b_.b])
        p.emit()
        top.close()
        return nc
    ffn(1, list(range(16)), True)
    p.emit()
    top.close()
    return nc


def rev(ap):
    apl = [list(a) for a in ap.ap]
    n = apl[-1][1]
    stp = apl[-1][0]
    apl[-1][0] = -stp
    return bass.AP(ap.tensor, ap.offset + (n - 1) * stp, apl)


def bcast_last(ap, n):
    apl = [list(a) for a in ap.ap] + [[0, n]]
    return bass.AP(ap.tensor, ap.offset, apl)


def s5_layer(nc, p, I, K, Hs, AB, load_gate, identf, identb, sbt, pst, ew, make_norm, norm_tile, dbg=None):
    Hloc = Hs
    YGs = nc.dram_tensor("YGs", [8, 128, T // 2], BF16).ap()
    NTOK = T + TC
    with ExitStack() as S:
        nT = sbt(S, "s5nT", [128, 8, NTOK], BF16)
        with ExitStack() as st:
            cn = make_norm(st, "s5n", nxt=3)
            for ti in range(34):
                col = 0 if ti < 32 else 1
                norm_tile(cn, Hs[ti * 128:(ti + 1) * 128, :], 1, col, 0, lambda k, ti=ti: nT.t[:, k, ti * 128:(ti + 1) * 128], nT.b)
            p.barrier()
        PRM = sbt(S, "s5prm", [128, 3, 64], F32)
        Cw = sbt(S, "s5Cw", [128, 64, 2, 32], F32)
        Bz = sbt(S, "s5Bz", [128, 2, 64, 2, 16], F32)
        LC = 4
        NCM = 512 // LC
        PW = sbt(S, "s5PW", [128, LC + 1, 2, 64], F32)
        PRM4 = sbt(S, "s5prm4", [128, 3, 64], F32)
        dT = sbt(S, "s5dT", [128, 8], F32)
        with ExitStack() as st:
            pt = pst(st, "s5pt", [128, 128])
            def V(nm):
                return sbt(st, "s5v_" + nm, [128, 64], F32)
            arow = sbt(st, "s5arow", [64, 2, 128], F32)
            p.dma(arow.t[:, 0, :], I["ssm_a_re"].rearrange("(dq g) p -> dq (g p)", g=2), writes=[arow.b])
            p.dma(arow.t[:, 1, :], I["ssm_a_im"].rearrange("(dq g) p -> dq (g p)", g=2), writes=[arow.b])
            are, aim = V("are"), V("aim")
            for i_, dst in enumerate((are, aim)):
                p.op('pe', lambda e, i_=i_: e.transpose(pt.t[:, 0:64], arow.t[:, i_, :], identf.t[0:64, 0:64]), reads=[arow.b, identf.b], writes=[pt.b])
                p.op('dve', lambda e, dst=dst: e.tensor_copy(dst.t[:], pt.t[:, 0:64]), reads=[pt.b], writes=[dst.b])
            drow = sbt(st, "s5drow", [8, 128], F32)
            p.dma(drow.t[:], I["ssm_d"], writes=[drow.b])
            p.op('pe', lambda e: e.transpose(pt.t[:, 0:8], drow.t[:, :], identf.t[0:8, 0:8]), reads=[drow.b, identf.b], writes=[pt.b])
            p.op('dve', lambda e: e.tensor_copy(dT.t[:], pt.t[:, 0:8]), reads=[pt.b], writes=[dT.b])
            ldb = sbt(st, "s5ldb", [128, 128], F32)
            p.dma(ldb.t[:], I["ssm_log_dt"].partition_broadcast(128), writes=[ldb.b])
            dt = V("dt")
            for g2 in range(2):
                p.op('dve', lambda e, g2=g2: e.tensor_copy(dt.t[64 * g2:64 * g2 + 64, :], ldb.t[64 * g2:64 * g2 + 64, g2:128:2]), reads=[ldb.b], writes=[dt.b])
            p.op('act', lambda e: e.activation(dt.t[:], dt.t[:], AF.Exp), reads=[dt.b], writes=[dt.b])
            xr, th, mag = V("xr"), V("th"), V("mag")
            p.op('dve', lambda e: e.tensor_mul(xr.t[:], are.t[:], dt.t[:]), reads=[are.b, dt.b], writes=[xr.b])
            p.op('dve', lambda e: e.tensor_mul(th.t[:], aim.t[:], dt.t[:]), reads=[aim.b, dt.b], writes=[th.b])
            p.op('act', lambda e: e.activation(mag.t[:], xr.t[:], AF.Exp), reads=[xr.b], writes=[mag.b])
            kf = V("kf")
            ki = sbt(st, "s5ki", [128, 64], mybir.dt.int32)
            p.op('dve', lambda e: e.tensor_scalar(kf.t[:], th.t[:], 1.0 / (2 * math.pi), None, ALU.mult), reads=[th.b], writes=[kf.b])
            p.op('dve', lambda e: e.tensor_copy(ki.t[:], kf.t[:]), reads=[kf.b], writes=[ki.b])
            p.op('dve', lambda e: e.tensor_copy(kf.t[:], ki.t[:]), reads=[ki.b], writes=[kf.b])
            C1 = 6.28125
            C2 = 2 * math.pi - 6.28125
            thm = V("thm")
            p.op('dve', lambda e: e.scalar_tensor_tensor(thm.t[:], kf.t[:], -C1, th.t[:], ALU.mult, ALU.add), reads=[kf.b, th.b], writes=[thm.b])
            p.op('dve', lambda e: e.scalar_tensor_tensor(thm.t[:], kf.t[:], -C2, thm.t[:], ALU.mult, ALU.add), reads=[kf.b, thm.b], writes=[thm.b])
            xq, u2, qs, qc = V("xq"), V("u2"), V("qs"), V("qc")
            p.op('dve', lambda e: e.tensor_scalar(xq.t[:], thm.t[:], 0.25, None, ALU.mult), reads=[thm.b], writes=[xq.b])
            p.op('dve', lambda e: e.tensor_mul(u2.t[:], xq.t[:], xq.t[:]), reads=[xq.b], writes=[u2.b])
            sc_ = [(-1.0) ** k / math.factorial(2 * k + 1) for k in range(9)]
            cc_ = [(-1.0) ** k / math.factorial(2 * k) for k in range(9)]
            p.op('dve', lambda e: e.tensor_scalar(qs.t[:], u2.t[:], sc_[8], None, ALU.mult), reads=[u2.b], writes=[qs.b])
            p.op('dve', lambda e: e.tensor_scalar(qc.t[:], u2.t[:], cc_[8], None, ALU.mult), reads=[u2.b], writes=[qc.b])
            for k in range(7, 0, -1):
                p.op('dve', lambda e, k=k: e.scalar_tensor_tensor(qs.t[:], qs.t[:], sc_[k], u2.t[:], ALU.add, ALU.mult), reads=[qs.b, u2.b], writes=[qs.b])
                p.op('dve', lambda e, k=k: e.scalar_tensor_tensor(qc.t[:], qc.t[:], cc_[k], u2.t[:], ALU.add, ALU.mult), reads=[qc.b, u2.b], writes=[qc.b])
            sn, cs = V("sn"), V("cs")
            p.op('dve', lambda e: e.scalar_tensor_tensor(sn.t[:], qs.t[:], 1.0, xq.t[:], ALU.add, ALU.mult), reads=[qs.b, xq.b], writes=[sn.b])
            p.op('dve', lambda e: e.tensor_scalar(cs.t[:], qc.t[:], 1.0, None, ALU.add), reads=[qc.b], writes=[cs.b])
            ta, tb_ = V("ta"), V("tb")
            for _ in range(2):
                p.op('dve', lambda e: e.tensor_mul(ta.t[:], cs.t[:], cs.t[:]), reads=[cs.b], writes=[ta.b])
                p.op('dve', lambda e: e.tensor_mul(tb_.t[:], sn.t[:], sn.t[:]), reads=[sn.b], writes=[tb_.b])
                p.op('dve', lambda e: e.scalar_tensor_tensor(sn.t[:], cs.t[:], 2.0, sn.t[:], ALU.mult, ALU.mult), reads=[cs.b, sn.b], writes=[sn.b])
                p.op('dve', lambda e: e.tensor_sub(cs.t[:], ta.t[:], tb_.t[:]), reads=[ta.b, tb_.b], writes=[cs.b])
            p.op('dve', lambda e: e.tensor_copy(PRM.t[:, 0, :], mag.t[:]), reads=[mag.b], writes=[PRM.b])
            p.op('dve', lambda e: e.tensor_copy(PRM.t[:, 1, :], cs.t[:]), reads=[cs.b], writes=[PRM.b])
            p.op('dve', lambda e: e.tensor_copy(PRM.t[:, 2, :], sn.t[:]), reads=[sn.b], writes=[PRM.b])
            p.op('pool', lambda e: e.memset(PW.t[:, 0, 0, :], 1.0), writes=[PW.b])
            p.op('pool', lambda e: e.memset(PW.t[:, 0, 1, :], 0.0), writes=[PW.b])
            p.op('dve', lambda e: e.tensor_mul(PW.t[:, 1, 0, :], mag.t[:], cs.t[:]), reads=[mag.b, cs.b], writes=[PW.b])
            p.op('dve', lambda e: e.tensor_mul(PW.t[:, 1, 1, :], mag.t[:], sn.t[:]), reads=[mag.b, sn.b], writes=[PW.b])
            for k in range(2, LC + 1):
                p.op('dve', lambda e, k=k: e.tensor_mul(ta.t[:], PW.t[:, k - 1, 0, :], PW.t[:, 1, 0, :]), reads=[PW.b], writes=[ta.b])
                p.op('dve', lambda e, k=k: e.tensor_mul(tb_.t[:], PW.t[:, k - 1, 1, :], PW.t[:, 1, 1, :]), reads=[PW.b], writes=[tb_.b])
                p.op('dve', lambda e, k=k: e.tensor_sub(PW.t[:, k, 0, :], ta.t[:], tb_.t[:]), reads=[ta.b, tb_.b], writes=[PW.b])
                p.op('dve', lambda e, k=k: e.tensor_mul(ta.t[:], PW.t[:, k - 1, 0, :], PW.t[:, 1, 1, :]), reads=[PW.b], writes=[ta.b])
                p.op('dve', lambda e, k=k: e.tensor_mul(tb_.t[:], PW.t[:, k - 1, 1, :], PW.t[:, 1, 0, :]), reads=[PW.b], writes=[tb_.b])
                p.op('dve', lambda e, k=k: e.tensor_add(PW.t[:, k, 1, :], ta.t[:], tb_.t[:]), reads=[ta.b, tb_.b], writes=[PW.b])
            c4, s4, r4 = V("c4"), V("s4"), V("r4")
            p.op('dve', lambda e: e.tensor_copy(c4.t[:], cs.t[:]), reads=[cs.b], writes=[c4.b])
            p.op('dve', lambda e: e.tensor_copy(s4.t[:], sn.t[:]), reads=[sn.b], writes=[s4.b])
            p.op('dve', lambda e: e.tensor_copy(r4.t[:], mag.t[:]), reads=[mag.b], writes=[r4.b])
            for _ in range(int(round(math.log2(LC)))):
                p.op('dve', lambda e: e.tensor_mul(ta.t[:], c4.t[:], c4.t[:]), reads=[c4.b], writes=[ta.b])
                p.op('dve', lambda e: e.tensor_mul(tb_.t[:], s4.t[:], s4.t[:]), reads=[s4.b], writes=[tb_.b])
                p.op('dve', lambda e: e.scalar_tensor_tensor(s4.t[:], c4.t[:], 2.0, s4.t[:], ALU.mult, ALU.mult), reads=[c4.b, s4.b], writes=[s4.b])
                p.op('dve', lambda e: e.tensor_sub(c4.t[:], ta.t[:], tb_.t[:]), reads=[ta.b, tb_.b], writes=[c4.b])
                p.op('dve', lambda e: e.tensor_mul(r4.t[:], r4.t[:], r4.t[:]), reads=[r4.b], writes=[r4.b])
            p.op('dve', lambda e: e.tensor_copy(PRM4.t[:, 0, :], r4.t[:]), reads=[r4.b], writes=[PRM4.b])
            p.op('dve', lambda e: e.tensor_copy(PRM4.t[:, 1, :], c4.t[:]), reads=[c4.b], writes=[PRM4.b])
            p.op('dve', lambda e: e.tensor_copy(PRM4.t[:, 2, :], s4.t[:]), reads=[s4.b], writes=[PRM4.b])
            nr, ni, den, fre, fim = V("nr"), V("ni"), V("den"), V("fre"), V("fim")
            p.op('dve', lambda e: e.tensor_mul(nr.t[:], mag.t[:], cs.t[:]), reads=[mag.b, cs.b], writes=[nr.b])
            p.op('dve', lambda e: e.tensor_scalar(nr.t[:], nr.t[:], -1.0, None, ALU.add), reads=[nr.b], writes=[nr.b])
            p.op('dve', lambda e: e.tensor_mul(ni.t[:], mag.t[:], sn.t[:]), reads=[mag.b, sn.b], writes=[ni.b])
            p.op('dve', lambda e: e.tensor_mul(den.t[:], are.t[:], are.t[:]), reads=[are.b], writes=[den.b])
            p.op('dve', lambda e: e.tensor_mul(ta.t[:], aim.t[:], aim.t[:]), reads=[aim.b], writes=[ta.b])
            p.op('dve', lambda e: e.tensor_add(den.t[:], den.t[:], ta.t[:]), reads=[den.b, ta.b], writes=[den.b])
            p.op('dve', lambda e: e.reciprocal(den.t[:], den.t[:]), reads=[den.b], writes=[den.b])
            p.op('dve', lambda e: e.tensor_mul(ta.t[:], nr.t[:], are.t[:]), reads=[nr.b, are.b], writes=[ta.b])
            p.op('dve', lambda e: e.tensor_mul(tb_.t[:], ni.t[:], aim.t[:]), reads=[ni.b, aim.b], writes=[tb_.b])
            p.op('dve', lambda e: e.tensor_add(fre.t[:], ta.t[:], tb_.t[:]), reads=[ta.b, tb_.b], writes=[fre.b])
            p.op('dve', lambda e: e.tensor_mul(fre.t[:], fre.t[:], den.t[:]), reads=[fre.b, den.b], writes=[fre.b])
            p.op('dve', lambda e: e.tensor_mul(ta.t[:], ni.t[:], are.t[:]), reads=[ni.b, are.b], writes=[ta.b])
            p.op('dve', lambda e: e.tensor_mul(tb_.t[:], nr.t[:], aim.t[:]), reads=[nr.b, aim.b], writes=[tb_.b])
            p.op('dve', lambda e: e.tensor_sub(fim.t[:], ta.t[:], tb_.t[:]), reads=[ta.b, tb_.b], writes=[fim.b])
            p.op('dve', lambda e: e.tensor_mul(fim.t[:], fim.t[:], den.t[:]), reads=[fim.b, den.b], writes=[fim.b])
            Br = [sbt(st, f"s5Br{i}", [128, 64, 16], F32) for i in range(2)]
            for i_, nm in enumerate(("ssm_b_re", "ssm_b_im")):
                src = I[nm].rearrange("d (q g) p h -> g p (d q) h", g=2)
                for g2 in range(2):
                    p.dma(Br[i_].t[64 * g2:64 * g2 + 64, :, :], src[g2], writes=[Br[i_].b])
            p.op('pool', lambda e: e.memset(Bz.t[:].rearrange("p a b c d -> p (a b c d)"), 0.0), writes=[Bz.b])
            m1 = sbt(st, "s5m1", [128, 64, 16], F32)
            m2 = sbt(st, "s5m2", [128, 64, 16], F32)
            fre_b = bcast_last(fre.t[:], 16)
            fim_b = bcast_last(fim.t[:], 16)
            p.op('dve', lambda e: e.tensor_mul(m1.t[:], Br[0].t[:], fre_b), reads=[Br[0].b, fre.b], writes=[m1.b])
            p.op('dve', lambda e: e.tensor_mul(m2.t[:], Br[1].t[:], fim_b), reads=[Br[1].b, fim.b], writes=[m2.b])
            for g2 in range(2):
                p.op('dve', lambda e, g2=g2: e.tensor_sub(Bz.t[64 * g2:64 * g2 + 64, 0, :, g2, :], m1.t[64 * g2:64 * g2 + 64], m2.t[64 * g2:64 * g2 + 64]),
                     reads=[m1.b, m2.b], writes=[Bz.b])
            p.op('dve', lambda e: e.tensor_mul(m1.t[:], Br[1].t[:], fre_b), reads=[Br[1].b, fre.b], writes=[m1.b])
            p.op('dve', lambda e: e.tensor_mul(m2.t[:], Br[0].t[:], fim_b), reads=[Br[0].b, fim.b], writes=[m2.b])
            for g2 in range(2):
                p.op('dve', lambda e, g2=g2: e.tensor_add(Bz.t[64 * g2:64 * g2 + 64, 1, :, g2, :], m1.t[64 * g2:64 * g2 + 64], m2.t[64 * g2:64 * g2 + 64]),
                     reads=[m1.b, m2.b], writes=[Bz.b])
            pts = [pst(st, f"s5ptb{i}", [128, 128]) for i in range(2)]
            n_ = 0
            p.op('pool', lambda e: e.memset(Cw.t[:].rearrange("p a b c -> p (a b c)"), 0.0), writes=[Cw.b])
            Cn = [sbt(st, f"s5Cn{i}", [128, 2, 64], F32) for i in range(2)]
            for ri, nm in enumerate(("ssm_c_re", "ssm_c_im")):
                for dQ in range(16):
                    d_, Q_ = dQ // 8, dQ % 8
                    cnb = Cn[n_ % 2]
                    pp = pts[n_ % 2]
                    n_ += 1
                    src = I[nm][d_, Q_ * 4 * 2:(Q_ * 4 + 4) * 2].rearrange("g h p -> (g h) p")
                    p.dma(cnb.t[:, 0, :], src, writes=[cnb.b])
                    p.dma(cnb.t[:, 1, :], src, writes=[cnb.b])
                    p.op('pe', lambda e, pp=pp, cnb=cnb: e.transpose(pp.t[:, :], cnb.t[:, :, :].rearrange("p a b -> p (a b)"), identf.t[:, :]),
                         reads=[cnb.b, identf.b], writes=[pp.b])
                    sgn = 1.0 if ri == 0 else -1.0
                    for g2 in range(2):
                        p.op('act', lambda e, pp=pp, g2=g2, ri=ri, dQ=dQ, sgn=sgn: e.activation(
                            Cw.t[64 * g2:64 * g2 + 64, dQ * 4:(dQ + 1) * 4, ri, 16 * g2:16 * g2 + 16],
                            pp.t[64 * g2:64 * g2 + 64, :].rearrange("p (q g h) -> p q g h", q=4, g=2)[:, :, g2, :], AF.Identity, scale=sgn),
                            reads=[pp.b], writes=[Cw.b])
            p.barrier()

        if dbg is not None:
            dbg(PRM, Cw, nT)
            return

        with ExitStack() as st:
            yacc = sbt(st, "s5yacc", [128, T // 2], F32)
            Ec = sbt(st, "s5Ec", [128, 8, NCM], F32)
            Es = sbt(st, "s5Es", [128, 8, NCM], F32)
            wc = sbt(st, "s5wc", [128, 8], F32)
            ws = sbt(st, "s5ws", [128, 8], F32)
            wt1 = sbt(st, "s5wt1", [128, 8], F32)
            wt2 = sbt(st, "s5wt2", [128, 8], F32)
            et1 = sbt(st, "s5et1", [128, 8, NCM // 2], F32)
            et2 = sbt(st, "s5et2", [128, 8, NCM // 2], F32)
            hp = sbt(st, "s5hp", [128, 8, 2], F32)
            ini = sbt(st, "s5ini", [128, 8, 2], F32)
            itmp = sbt(st, "s5itmp", [128, 8, 2], F32)
            nsth = sbt(st, "s5nsth", [128, 64], F32)
            p.op('dve', lambda e: e.tensor_scalar(nsth.t[:], PRM4.t[:, 2, :], -1.0, None, ALU.mult), reads=[PRM4.b], writes=[nsth.b])
            hpb = [Buf() for _ in range(8)]
            CwP = sbt(st, "s5CwP", [128, LC + 1, 8, 2, 32], F32)
            ctm = [sbt(st, f"s5ctm{i}", [128, 4, 32], F32) for i in range(2)]
            BP = sbt(st, "s5BP", [128, LC, 2, 2, 4, 32], F32)
            Win = sbt(st, "s5Win", [128, 2, 2, LC, 128], BF16)
            Mw = sbt(st, "s5Mw", [128, 2, LC, 32], BF16)
            Wout = sbt(st, "s5Wout", [128, 8, LC, 2, 32], BF16)
            pts = [pst(st, f"s5ptq{i}", [128, 128]) for i in range(1)]
            pk = pst(st, "s5pk", [128, 16, 32])
            NS = 8
            Wk_ = [[sbt(st, f"s5w{j}_{i}", [128, 2, NCM], F32) for i in range(3)] for j in range(NS)]
            Hb = [sbt(st, f"s5hb{j}", [128, 2, NCM + 4], BF16) for j in range(NS)]
            Esg = sbt(st, "s5Esg", [128, 8, 2, NCM], F32)

            def swap2(tb_, ncn):
                b1 = tb_.t[:, 1, 0:ncn]
                apl = [list(a_) for a_ in b1.ap]
                return bass.AP(b1.tensor, b1.offset, [apl[0], [-NCM, 2], apl[-1]])

            def bc2(ap2):
                apl = [list(a_) for a_ in ap2.ap]
                return bass.AP(ap2.tensor, ap2.offset, [apl[0], [0, 2], apl[-1]])
            pS = [pst(st, f"s5pS{j}", [128, 512]) for j in range(4)]
            py = [pst(st, f"s5py{i}", [128, LC, NCM]) for i in range(2)]
            ygb = [sbt(st, f"s5yg{i}", [128, 512], BF16) for i in range(2)]
            gtb = [sbt(st, f"s5gt{i}", [128, 512], F32) for i in range(2)]
            tn_ = 0
            blkc = 0
            yi_ = 0
            psi = 0
            for Q in range(8):
                for d_ in range(2):
                    us = slice(d_ * 32 + Q * 4, d_ * 32 + Q * 4 + 4)
                    ts_ = slice(d_ * 4, d_ * 4 + 4)
                    c0_ = Cw.t[:, us, 0, :]
                    c1_ = Cw.t[:, us, 1, :]
                    for k in range(LC + 1):
                        pr = bcast_last(PW.t[:, k, 0, us], 32)
                        pi_ = bcast_last(PW.t[:, k, 1, us], 32)
                        t1_, t2_ = ctm
                        p.op('dve', lambda e, t1_=t1_, c0_=c0_, pr=pr: e.tensor_mul(t1_.t[:], c0_, pr), reads=[Cw.b, PW.b], writes=[t1_.b])
                        p.op('pool', lambda e, t2_=t2_, c1_=c1_, pi_=pi_: e.tensor_mul(t2_.t[:], c1_, pi_), reads=[Cw.b, PW.b], writes=[t2_.b])
                        p.op('dve', lambda e, t1_=t1_, t2_=t2_, k=k, ts_=ts_: e.tensor_add(CwP.t[:, k, ts_, 0, :], t1_.t[:], t2_.t[:]), reads=[t1_.b, t2_.b], writes=[CwP.b])
                        p.op('dve', lambda e, t1_=t1_, c1_=c1_, pr=pr: e.tensor_mul(t1_.t[:], c1_, pr), reads=[Cw.b, PW.b], writes=[t1_.b])
                        p.op('pool', lambda e, t2_=t2_, c0_=c0_, pi_=pi_: e.tensor_mul(t2_.t[:], c0_, pi_), reads=[Cw.b, PW.b], writes=[t2_.b])
                        p.op('dve', lambda e, t1_=t1_, t2_=t2_, k=k, ts_=ts_: e.tensor_sub(CwP.t[:, k, ts_, 1, :], t1_.t[:], t2_.t[:]), reads=[t1_.b, t2_.b], writes=[CwP.b])
                    b0_ = Bz.t[:, 0, us, :, :].rearrange("p a b c -> p a (b c)")
                    b1_ = Bz.t[:, 1, us, :, :].rearrange("p a b c -> p a (b c)")
                    for s_ in range(LC):
                        k = LC - 1 - s_
                        pr = bcast_last(PW.t[:, k, 0, us], 32)
                        pi_ = bcast_last(PW.t[:, k, 1, us], 32)
                        t1_, t2_ = ctm
                        p.op('dve', lambda e, t1_=t1_, b0_=b0_, pr=pr: e.tensor_mul(t1_.t[:], b0_, pr), reads=[Bz.b, PW.b], writes=[t1_.b])
                        p.op('pool', lambda e, t2_=t2_, b1_=b1_, pi_=pi_: e.tensor_mul(t2_.t[:], b1_, pi_), reads=[Bz.b, PW.b], writes=[t2_.b])
                        p.op('dve', lambda e, t1_=t1_, t2_=t2_, s_=s_, d_=d_: e.tensor_sub(BP.t[:, s_, 0, d_, :, :], t1_.t[:], t2_.t[:]), reads=[t1_.b, t2_.b], writes=[BP.b])
                        p.op('dve', lambda e, t1_=t1_, b0_=b0_, pi_=pi_: e.tensor_mul(t1_.t[:], b0_, pi_), reads=[Bz.b, PW.b], writes=[t1_.b])
                        p.op('pool', lambda e, t2_=t2_, b1_=b1_, pr=pr: e.tensor_mul(t2_.t[:], b1_, pr), reads=[Bz.b, PW.b], writes=[t2_.b])
                        p.op('dve', lambda e, t1_=t1_, t2_=t2_, s_=s_, d_=d_: e.tensor_add(BP.t[:, s_, 1, d_, :, :], t1_.t[:], t2_.t[:]), reads=[t1_.b, t2_.b], writes=[BP.b])
                p.op('act', lambda e: e.activation(Wout.t[:].rearrange("p x t r h -> p t x r h"), CwP.t[:, 1:LC + 1, :, :, :], AF.Identity), reads=[CwP.b], writes=[Wout.b])
                for d_ in range(2):
                    for ri in range(2):
                        for s_ in range(LC):
                            pp = pts[0]
                            tn_ += 1
                            p.op('pe', lambda e, pp=pp, s_=s_, ri=ri, d_=d_: e.transpose(pp.t[:, :], BP.t[:, s_, ri, d_, :, :].rearrange("p a b -> p (a b)"), identf.t[:, :]),
                                 reads=[BP.b, identf.b], writes=[pp.b])
                            p.op('act', lambda e, pp=pp, s_=s_, ri=ri, d_=d_: e.activation(Win.t[:, d_, ri, s_, :], pp.t[:, :], AF.Identity), reads=[pp.b], writes=[Win.b])
                    for thf in range(LC // 4):
                        for tq in range(4):
                            tau = thf * 4 + tq
                            for ql in range(4):
                                tix = d_ * 4 + ql
                                for ri in range(2):
                                    p.op('pe', lambda e, d_=d_, tau=tau, tq=tq, ql=ql, tix=tix, ri=ri, us=slice(d_ * 32 + Q * 4, d_ * 32 + Q * 4 + 4): e.matmul(
                                        pk.t[:, tq * 4 + ql, :], Bz.t[:, ri, us, :, :].rearrange("p q a b -> p (q a b)"), CwP.t[:, tau, tix, ri, :],
                                        start=(ri == 0), stop=(ri == 1)), reads=[Bz.b, CwP.b], writes=[pk.b])
                        for ql in range(4):
                            p.op('dve', lambda e, d_=d_, ql=ql, thf=thf: e.tensor_copy(Mw.t[32 * ql:32 * ql + 32, d_, thf * 4:thf * 4 + 4, :], pk.t[32 * ql:32 * ql + 32, ql:16:4, :]),
                                 reads=[pk.b], writes=[Mw.b])
                for d_ in range(2):
                    cols = slice(d_ * 32 + Q * 4, d_ * 32 + Q * 4 + 4)
                    p.op('dve', lambda e, d_=d_, cols=cols: e.tensor_copy(wc.t[:, d_ * 4:d_ * 4 + 4], PRM4.t[:, 1, cols]), reads=[PRM4.b], writes=[wc.b])
                    p.op('dve', lambda e, d_=d_, cols=cols: e.tensor_copy(ws.t[:, d_ * 4:d_ * 4 + 4], PRM4.t[:, 2, cols]), reads=[PRM4.b], writes=[ws.b])
                p.op('pool', lambda e: e.memset(Ec.t[:, :, 0:1], 1.0), writes=[Ec.b])
                p.op('pool', lambda e: e.memset(Es.t[:, :, 0:1], 0.0), writes=[Es.b])
                n = 1
                while n < NCM:
                    wcb = bcast_last(wc.t[:, :], n)
                    wsb = bcast_last(ws.t[:, :], n)
                    p.op('dve', lambda e, n=n, wcb=wcb: e.tensor_mul(et1.t[:, :, 0:n], Ec.t[:, :, 0:n], wcb), reads=[Ec.b, wc.b], writes=[et1.b])
                    p.op('pool', lambda e, n=n, wsb=wsb: e.tensor_mul(et2.t[:, :, 0:n], Es.t[:, :, 0:n], wsb), reads=[Es.b, ws.b], writes=[et2.b])
                    p.op('dve', lambda e, n=n: e.tensor_sub(Ec.t[:, :, n:2 * n], et1.t[:, :, 0:n], et2.t[:, :, 0:n]), reads=[et1.b, et2.b], writes=[Ec.b])
                    p.op('dve', lambda e, n=n, wcb=wcb: e.tensor_mul(et1.t[:, :, 0:n], Es.t[:, :, 0:n], wcb), reads=[Es.b, wc.b], writes=[et1.b])
                    p.op('pool', lambda e, n=n, wsb=wsb: e.tensor_mul(et2.t[:, :, 0:n], Ec.t[:, :, 0:n], wsb), reads=[Ec.b, ws.b], writes=[et2.b])
                    p.op('dve', lambda e, n=n: e.tensor_add(Es.t[:, :, n:2 * n], et1.t[:, :, 0:n], et2.t[:, :, 0:n]), reads=[et1.b, et2.b], writes=[Es.b])
                    p.op('dve', lambda e: e.tensor_mul(wt1.t[:], wc.t[:], wc.t[:]), reads=[wc.b], writes=[wt1.b])
                    p.op('dve', lambda e: e.tensor_mul(wt2.t[:], ws.t[:], ws.t[:]), reads=[ws.b], writes=[wt2.b])
                    p.op('dve', lambda e: e.scalar_tensor_tensor(ws.t[:], wc.t[:], 2.0, ws.t[:], ALU.mult, ALU.mult), reads=[wc.b, ws.b], writes=[ws.b])
                    p.op('dve', lambda e: e.tensor_sub(wc.t[:], wt1.t[:], wt2.t[:]), reads=[wt1.b, wt2.b], writes=[wc.b])
                    n *= 2
                p.op('dve', lambda e: e.tensor_copy(Esg.t[:, :, 0, :], Es.t[:, :, :]), reads=[Es.b], writes=[Esg.b])
                p.op('dve', lambda e: e.tensor_scalar(Esg.t[:, :, 1, :], Es.t[:, :, :], -1.0, None, ALU.mult), reads=[Es.b], writes=[Esg.b])
                blist = []
                for d_ in range(2):
                    if d_ == 0:
                        blocks = [(T, TC, False)] + [(b * 512, 512, True) for b in range(4)]
                    else:
                        blocks = [(T, TC, False)] + [(b * 512, 512, False) for b in (7, 6, 5, 4)] + [(b * 512, 512, True) for b in (3, 2, 1, 0)]
                    for bi_, (c0, n, islat) in enumerate(blocks):
                        blist.append((d_, bi_, c0, n, islat))
                bctx = {}

                def front_pe(k, Q=Q):
                    d_, bi_, c0, n, islat = blist[k]
                    gk = Q * 18 + k
                    ncn = n // LC
                    sets = [(gk % 2) * 4 + ql for ql in range(4)]
                    rhs_all = []
                    for ql in range(4):
                        base = nT.t[32 * ql:32 * ql + 32, Q, c0:c0 + n]
                        apl = [list(a_) for a_ in base.ap]
                        lst = []
                        for s_ in range(LC):
                            ap2 = [list(a_) for a_ in apl]
                            if d_ == 0:
                                ap2[-1] = [apl[-1][0] * LC, ncn]
                                lst.append(bass.AP(base.tensor, base.offset + s_ * apl[-1][0], ap2))
                            else:
                                ap2[-1] = [-apl[-1][0] * LC, ncn]
                                lst.append(bass.AP(base.tensor, base.offset + (n - 1 - s_) * apl[-1][0], ap2))
                        rhs_all.append(lst)
                    pss = [pS[ql] for ql in range(4)]
                    for ri in range(2):
                        for s_ in range(LC):
                            for ql in range(4):
                                ps_ = pss[ql]
                                p.op('pe', lambda e, ps_=ps_, ri=ri, s_=s_, ql=ql, d_=d_, ncn=ncn, r_=rhs_all[ql][s_]: e.matmul(
                                    ps_.t[:, ri * 128:ri * 128 + ncn], Win.t[32 * ql:32 * ql + 32, d_, ri, s_, :], r_, start=(s_ == 0), stop=(s_ == LC - 1), tile_position=(32 * ql, 0)),
                                    reads=[Win.b, nT.b], writes=[ps_.b])
                    bctx[k] = (ncn, sets, rhs_all)

                def front_ev(k, Q=Q):
                    d_, bi_, c0, n, islat = blist[k]
                    ncn, sets, rhs_all = bctx[k]
                    pss = [pS[ql] for ql in range(4)]
                    for ql in range(4):
                        ps_ = pss[ql]
                        X, P1, P2 = Wk_[sets[ql]]
                        p.op('act', lambda e, X=X, ps_=ps_, ncn=ncn: e.activation(X.t[:, :, 0:ncn], ps_.t[:, 0:256].rearrange("p (r c) -> p r c", r=2)[:, :, 0:ncn], AF.Identity),
                             reads=[ps_.b], writes=[X.b])
                    for ql in range(4):
                        X, P1, P2 = Wk_[sets[ql]]
                        tix = d_ * 4 + ql
                        ecb = bc2(Ec.t[:, tix, 0:ncn])
                        p.op('dve', lambda e, X=X, P1=P1, ecb=ecb, ncn=ncn: e.tensor_mul(P1.t[:, :, 0:ncn], X.t[:, :, 0:ncn], ecb), reads=[X.b, Ec.b], writes=[P1.b])
                        p.op('pool', lambda e, X=X, P2=P2, tix=tix, ncn=ncn: e.tensor_mul(P2.t[:, :, 0:ncn], swap2(X, ncn), Esg.t[:, tix, :, 0:ncn]), reads=[X.b, Esg.b], writes=[P2.b])
                    for ql in range(4):
                        X, P1, P2 = Wk_[sets[ql]]
                        p.op('dve', lambda e, P1=P1, P2=P2, ncn=ncn: e.tensor_add(P1.t[:, :, 0:ncn], P1.t[:, :, 0:ncn], P2.t[:, :, 0:ncn]), reads=[P1.b, P2.b], writes=[P1.b])

                def back(k, Q=Q):
                    d_, bi_, c0, n, islat = blist[k]
                    gk = Q * 18 + k
                    ncn, sets, rhs_all = bctx.pop(k)
                    pyt = py[gk % 2]
                    for ql in range(4):
                        tix = d_ * 4 + ql
                        u = d_ * 32 + Q * 4 + ql
                        hb_ = hpb[tix]
                        hbs = Hb[sets[ql]]
                        if bi_ > 0:
                            p.op('act', lambda e, tix=tix, u=u: e.activation(itmp.t[:, tix, 0:1], hp.t[:, tix, 1:2], AF.Identity, scale=nsth.t[:, u:u + 1]), reads=[hb_, nsth.b], writes=[hb_])
                            p.op('act', lambda e, tix=tix, u=u: e.activation(ini.t[:, tix, 0:1], hp.t[:, tix, 0:1], AF.Identity, scale=PRM4.t[:, 1, u:u + 1], bias=itmp.t[:, tix, 0:1]),
                                 reads=[hb_, PRM4.b], writes=[hb_])
                            p.op('act', lambda e, tix=tix, u=u: e.activation(itmp.t[:, tix, 1:2], hp.t[:, tix, 0:1], AF.Identity, scale=PRM4.t[:, 2, u:u + 1]), reads=[hb_, PRM4.b], writes=[hb_])
                            p.op('act', lambda e, tix=tix, u=u: e.activation(ini.t[:, tix, 1:2], hp.t[:, tix, 1:2], AF.Identity, scale=PRM4.t[:, 1, u:u + 1], bias=itmp.t[:, tix, 1:2]),
                                 reads=[hb_, PRM4.b], writes=[hb_])
                            if islat:
                                p.op('act', lambda e, hbs=hbs, tix=tix: e.activation(hbs.t[:, :, 0], hp.t[:, tix, :], AF.Identity), reads=[hb_], writes=[hbs.b])
                    for ql in range(4):
                        X, P1, P2 = Wk_[sets[ql]]
                        tix = d_ * 4 + ql
                        u = d_ * 32 + Q * 4 + ql
                        hb_ = hpb[tix]
                        rb = PRM4.t[:, 0, u:u + 1].to_broadcast([128, ncn])
                        for ri in range(2):
                            if bi_ == 0:
                                init_, irds = 0.0, []
                            else:
                                init_, irds = ini.t[:, tix, ri:ri + 1], [hb_]
                            p.op('dve', lambda e, X=X, P1=P1, rb=rb, init_=init_, ri=ri, ncn=ncn: e.tensor_tensor_scan(X.t[:, ri, 0:ncn], rb, P1.t[:, ri, 0:ncn], init_, ALU.mult, ALU.add),
                                 reads=[P1.b, PRM4.b] + irds, writes=[X.b])
                    for ql in range(4):
                        X, P1, P2 = Wk_[sets[ql]]
                        tix = d_ * 4 + ql
                        ecb = bc2(Ec.t[:, tix, 0:ncn])
                        p.op('dve', lambda e, X=X, P1=P1, ecb=ecb, ncn=ncn: e.tensor_mul(P1.t[:, :, 0:ncn], X.t[:, :, 0:ncn], ecb), reads=[X.b, Ec.b], writes=[P1.b])
                        p.op('pool', lambda e, X=X, P2=P2, tix=tix, ncn=ncn: e.tensor_mul(P2.t[:, :, 0:ncn], swap2(X, ncn), Esg.t[:, tix, :, 0:ncn]), reads=[X.b, Esg.b], writes=[P2.b])
                    for ql in range(4):
                        X, P1, P2 = Wk_[sets[ql]]
                        p.op('dve', lambda e, P1=P1, P2=P2, ncn=ncn: e.tensor_sub(P1.t[:, :, 0:ncn], P1.t[:, :, 0:ncn], P2.t[:, :, 0:ncn]), reads=[P1.b, P2.b], writes=[P1.b])
                    for ql in range(4):
                        X, P1, P2 = Wk_[sets[ql]]
                        tix = d_ * 4 + ql
                        hb_ = hpb[tix]
                        hbs = Hb[sets[ql]]
                        p.op('act', lambda e, tix=tix, P1=P1, ncn=ncn: e.activation(hp.t[:, tix, :], P1.t[:, :, ncn - 1], AF.Identity), reads=[P1.b], writes=[hb_])
                        if islat:
                            p.op('act', lambda e, hbs=hbs, P1=P1, ncn=ncn: e.activation(hbs.t[:, :, 1:ncn], P1.t[:, :, 0:ncn - 1], AF.Identity), reads=[P1.b], writes=[hbs.b])
                    if islat:
                        for t_ in range(LC):
                            for s_ in range(t_ + 1):
                                for ql in range(4):
                                    p.op('pe', lambda e, pyt=pyt, ql=ql, t_=t_, s_=s_, d_=d_, ncn=ncn, r_=rhs_all[ql][s_]: e.matmul(
                                        pyt.t[32 * ql:32 * ql + 32, t_, 0:ncn], Mw.t[32 * ql:32 * ql + 32, d_, t_ - s_, :], r_, start=(s_ == 0 and t_ == 0), stop=False,
                                        tile_position=(32 * ql, 32 * ql)), reads=[Mw.b, nT.b], writes=[pyt.b])
                        for t_ in range(LC):
                            for ri in range(2):
                                for ql in range(4):
                                    tix = d_ * 4 + ql
                                    hh_ = Hb[sets[ql]]
                                    p.op('pe', lambda e, pyt=pyt, ql=ql, t_=t_, ri=ri, tix=tix, hh_=hh_, ncn=ncn: e.matmul(
                                        pyt.t[32 * ql:32 * ql + 32, t_, 0:ncn], Wout.t[:, tix, t_, ri, :], hh_.t[:, ri, 0:ncn], start=False, stop=(ri == 1 and t_ == LC - 1),
                                        tile_position=(0, 32 * ql)), reads=[Wout.b, hh_.b], writes=[pyt.b])

                def yevac(k, Q=Q):
                    d_, bi_, c0, n, islat = blist[k]
                    if not islat:
                        return
                    gk = Q * 18 + k
                    ncn = n // LC
                    pyt = py[gk % 2]
                    if d_ == 0:
                        yv = yacc.t[:, c0:c0 + n].rearrange("p (c t) -> p t c", t=LC)
                        nv = nT.t[:, Q, c0:c0 + n].rearrange("p (c t) -> p t c", t=LC)
                        p.op('dve', lambda e, pyt=pyt, yv=yv, nv=nv, Q=Q: e.scalar_tensor_tensor(yv, nv, dT.t[:, Q:Q + 1], pyt.t[:, :, :], ALU.mult, ALU.add),
                             reads=[nT.b, dT.b, pyt.b], writes=[yacc.b])
                    else:
                        base = yacc.t[:, c0:c0 + n]
                        apl = [list(a_) for a_ in base.ap]
                        stp = apl[-1][0]
                        yv = bass.AP(base.tensor, base.offset + (n - 1) * stp, apl[:-1] + [[-stp, LC], [-stp * LC, ncn]])
                        p.op('dve', lambda e, pyt=pyt, yv=yv: e.tensor_add(yv, yv, pyt.t[:, :, :]), reads=[pyt.b, yacc.b], writes=[yacc.b])

                NB_ = len(blist)
                front_pe(0)
                front_ev(0)
                if NB_ > 1:
                    front_pe(1)
                for k in range(NB_):
                    if k + 1 < NB_:
                        front_ev(k + 1)
                    if k + 2 < NB_:
                        front_pe(k + 2)
                    back(k)
                    if k >= 1:
                        yevac(k - 1)
                yevac(NB_ - 1)
                for b in range(4):
                    yg = ygb[b % 2]
                    gt = gtb[b % 2]
                    ysl = yacc.t[:, b * 512:(b + 1) * 512]
                    p.op('dve', lambda e, gt=gt, ysl=ysl: e.tensor_mul(gt.t[:, :], ysl, ysl), reads=[yacc.b], writes=[gt.b])
                    p.op('dve', lambda e, gt=gt: e.tensor_scalar(gt.t[:, :], gt.t[:, :], 0.044715, 1.0, ALU.mult, ALU.add), reads=[gt.b], writes=[gt.b])
                    p.op('dve', lambda e, gt=gt, ysl=ysl: e.tensor_mul(gt.t[:, :], gt.t[:, :], ysl), reads=[gt.b, yacc.b], writes=[gt.b])
                    p.op('act', lambda e, gt=gt: e.activation(gt.t[:, :], gt.t[:, :], AF.Sigmoid, scale=2.0 * math.sqrt(2.0 / math.pi)), reads=[gt.b], writes=[gt.b])
                    p.op('dve', lambda e, gt=gt, ysl=ysl, yg=yg: e.tensor_mul(yg.t[:, :], gt.t[:, :], ysl), reads=[gt.b, yacc.b], writes=[yg.b])
                    p.dma(YGs[Q, :, b * 512:(b + 1) * 512], yg.t[:, :], reads=[yg.b])
            p.barrier()
    with ExitStack() as st:
        Wgl = sbt(st, "s5Wgl", [128, 8, 2 * D], BF16)
        with ExitStack() as st2:
            stg = [sbt(st2, f"s5gst{i}", [128, 2 * D], F32) for i in range(2)]
            for k in range(8):
                s_ = stg[k % 2]
                p.dma(s_.t[:, 0:D], I["ssm_glu_w"][k * 128:(k + 1) * 128, 0:D], writes=[s_.b])
                p.dma(s_.t[:, D:2 * D], I["ssm_glu_w"][k * 128:(k + 1) * 128, D:2 * D], writes=[s_.b])
                ew(lambda e, k=k, s_=s_: e.tensor_copy(Wgl.t[:, k, 0:D], s_.t[:, 0:D]), [s_.b], [Wgl.b])
                ew(lambda e, k=k, s_=s_: e.tensor_copy(Wgl.t[:, k, D:2 * D], s_.t[:, D:2 * D]), [s_.b], [Wgl.b])
            p.barrier()
        g1 = load_gate(st, "s5g1", 1, 0, 0)
        ygt = [sbt(st, f"s5ygt{i}", [128, 8, 128], BF16) for i in range(2)]
        h2 = [sbt(st, f"s5h2{i}", [128, D], F32) for i in range(2)]
        h3 = [sbt(st, f"s5h3{i}", [128, D], F32) for i in range(2)]
        sg_ = [sbt(st, f"s5sg{i}", [128, 512], F32) for i in range(2)]
        pa = [pst(st, f"s5pa{i}", [128, 512]) for i in range(2)]
        pgl = [pst(st, f"s5pg{i}", [128, 512]) for i in range(2)]
        c_ = 0
        YGs4 = YGs.rearrange("q p (r t) -> q p r t", r=2)
        for ti in range(16):
            yt = ygt[ti % 2]
            p.dma(yt.t[:], YGs[:, :, ti * 128:(ti + 1) * 128].rearrange("q p t -> p q t"), writes=[yt.b])
            hh = h2[ti % 2]
            ho_ = h3[ti % 2]
            p.dma(hh.t[:], Hloc[ti * 128:(ti + 1) * 128, :], writes=[hh.b])
            for half in range(2):
                a = pa[c_ % 2]
                g = pgl[c_ % 2]
                sg2 = sg_[c_ % 2]
                c_ += 1
                for k in range(8):
                    p.op('pe', lambda e, a=a, k=k, yt=yt, half=half: e.matmul(a.t[:, :], yt.t[:, k, :], Wgl.t[:, k, half * 512:(half + 1) * 512], start=(k == 0), stop=(k == 7)),
                         reads=[yt.b, Wgl.b], writes=[a.b])
                for k in range(8):
                    p.op('pe', lambda e, g=g, k=k, yt=yt, half=half: e.matmul(g.t[:, :], yt.t[:, k, :], Wgl.t[:, k, D + half * 512:D + (half + 1) * 512], start=(k == 0), stop=(k == 7)),
                         reads=[yt.b, Wgl.b], writes=[g.b])
                p.op('act', lambda e, g=g, sg2=sg2: e.activation(sg2.t[:, :], g.t[:, :], AF.Sigmoid), reads=[g.b], writes=[sg2.b])
                p.op('dve', lambda e, a=a, sg2=sg2: e.tensor_mul(sg2.t[:, :], sg2.t[:, :], a.t[:, :]), reads=[a.b, sg2.b], writes=[sg2.b])
                p.op('pool', lambda e, sg2=sg2, half=half: e.tensor_mul(sg2.t[:, :], sg2.t[:, :], g1.t[:, half * 512:(half + 1) * 512]), reads=[sg2.b, g1.b], writes=[sg2.b])
                p.op('pool', lambda e, sg2=sg2, hh=hh, ho_=ho_, half=half: e.tensor_add(ho_.t[:, half * 512:(half + 1) * 512], hh.t[:, half * 512:(half + 1) * 512], sg2.t[:, :]),
                     reads=[sg2.b, hh.b], writes=[ho_.b])
            p.dma(Hloc[ti * 128:(ti + 1) * 128, :], ho_.t[:], reads=[ho_.b])
        p.barrier()


_CACHE = {}


def kernel(**inputs):
    stage = int(inputs.pop("_stage", 99))
    if "nc" not in _CACHE or _CACHE.get("stage") != stage:
        _CACHE["nc"] = build(stage)
        _CACHE["stage"] = stage
        _CACHE["consts"] = host_consts()
    nc = _CACHE["nc"]
    if "consts_rev" not in _CACHE:
        _CACHE["consts_rev"] = host_consts(rev=True)
    f = lambda a: np.ascontiguousarray(np.asarray(a, dtype=np.float32))
    in_maps = []
    for core in range(8):
        b = core // 2
        rv = (core % 2 == 1) and stage >= 99
        cs = _CACHE["consts_rev"] if rv else _CACHE["consts"]
        sd = (lambda a: np.asarray(a)[0][::-1]) if rv else (lambda a: np.asarray(a)[0])
        xb = np.asarray(inputs["x"][b])
        cb = np.asarray(inputs["ctx"][b])
        if rv:
            xb = xb[::-1]
            cb = cb[::-1]
        m = {
            "x": f(xb), "c": f(inputs["c"][b]).reshape(8, 128), "ctx": f(cb),
            "c_ctx": f(inputs["c_ctx"]).reshape(8, 128),
            "mod_w": f(inputs["mod_w"]), "mod_b": f(inputs["mod_b"]).reshape(2, 48, 128),
            "norm_g": f(inputs["norm_g"]).reshape(2, 2, 8, 128),
            "ffn_w_gate": f(inputs["ffn_w_gate"]), "ffn_w_up": f(inputs["ffn_w_up"]), "ffn_w_down": f(inputs["ffn_w_down"]),
            "mix_w_in": f(inputs["mix_w_in"][0]), "mix_w_out": f(inputs["mix_w_out"][0]), "attn_sink": f(inputs["attn_sink"]).reshape(1, 8),
            "ssm_a_re": f(sd(inputs["ssm_a_re"])).reshape(128, 64), "ssm_a_im": f(sd(inputs["ssm_a_im"])).reshape(128, 64),
            "ssm_log_dt": f(sd(inputs["ssm_log_dt"])).reshape(1, 128),
            "ssm_b_re": f(sd(inputs["ssm_b_re"])), "ssm_b_im": f(sd(inputs["ssm_b_im"])),
            "ssm_c_re": f(sd(inputs["ssm_c_re"])), "ssm_c_im": f(sd(inputs["ssm_c_im"])),
            "ssm_d": f(inputs["ssm_d"][0]).reshape(8, 128), "ssm_glu_w": f(inputs["ssm_glu_w"][0]), "final_g": f(inputs["final_g"]).reshape(1, D),
        }
        m.update(cs)
        m["rk"] = np.array([[0]], np.int32)
        in_maps.append(m)
    res = run_bass_kernel_spmd(nc, in_maps, core_ids=list(range(8)))
    if stage < 99:
        return np.stack([np.asarray(res.results[2 * b]["out"], dtype=np.float32) for b in range(4)], axis=0)
    outp = np.stack([np.concatenate([np.asarray(res.results[2 * b]["out"], dtype=np.float32),
                                     np.asarray(res.results[2 * b + 1]["out"], dtype=np.float32)[::-1]], axis=0) for b in range(4)], axis=0)
    return outp
```

```python
import math
import numpy as np
import ml_dtypes
import concourse.bass as bass
import concourse.mybir as mybir
from concourse.bass_utils import run_bass_kernel_spmd
from contextlib import ExitStack

F32 = mybir.dt.float32
BF16 = mybir.dt.bfloat16
AF = mybir.ActivationFunctionType
ALU = mybir.AluOpType
NPBF = ml_dtypes.bfloat16

D = 1024
T = 4096
TC = 256
FF = 2816
NFC = 22
EPS = 1e-6


class Buf:
    def __init__(self, name=""):
        self.name = name
        self.last_w = None
        self.readers = []


class Prog:
    ENG = ['pe', 'act', 'dve', 'pool', 'sp']

    def __init__(self, nc, ndma_sems=10):
        self.nc = nc
        self.ops = {e: [] for e in self.ENG}
        self.cnt = {e: 0 for e in self.ENG}
        self.known = {e: {} for e in self.ENG}
        self.es = ExitStack()
        self.sem = {e: self.es.enter_context(nc.semaphore('s_' + e)) for e in self.ENG}
        self.dma_sems = {e: [self.es.enter_context(nc.semaphore(f'd_{e}{i}')) for i in range(ndma_sems)]
                         for e in ['sp', 'act', 'pool']}
        self.dma_val = {e: [0] * ndma_sems for e in self.dma_sems}
        self.dma_rr = {e: 0 for e in self.dma_sems}
        self.semobj = {}
        for e in self.ENG:
            self.semobj[('c', e)] = self.sem[e]
        for e in self.dma_sems:
            for i, s in enumerate(self.dma_sems[e]):
                self.semobj[('d', e, i)] = s
        self.q = 0
        self.rank_ap = None

    def _waits(self, eng, toks):
        need = {}
        for t in toks:
            if t is None:
                continue
            k, v = t
            if k == ('c', eng) and eng == 'pe':
                continue
            if self.known[eng].get(k, 0) >= v:
                continue
            if need.get(k, 0) < v:
                need[k] = v
        for k, v in need.items():
            self.known[eng][k] = v
        return list(need.items())

    def _deps(self, reads, writes):
        toks = []
        for b in reads:
            toks.append(b.last_w)
        for b in writes:
            toks.append(b.last_w)
            toks.extend(b.readers)
        return toks

    def _commit(self, tok, reads, writes):
        for b in reads:
            b.readers.append(tok)
            if len(b.readers) > 64:
                b.readers = b.readers[-64:]
        for b in writes:
            b.last_w = tok
            b.readers = []

    def op(self, eng, fn, reads=(), writes=()):
        waits = self._waits(eng, self._deps(reads, writes))
        self.cnt[eng] += 1
        tok = (('c', eng), self.cnt[eng])
        self.ops[eng].append((waits, fn, (self.sem[eng], 1)))
        self._commit(tok, reads, writes)
        return tok

    def dma(self, out, in_, reads=(), writes=(), eng=None, **kw):
        if eng is None:
            eng = ['sp', 'act', 'pool'][self.q % 2]
            self.q += 1
        toks = self._deps(reads, writes)
        i = self.dma_rr[eng]
        self.dma_rr[eng] = (i + 1) % len(self.dma_sems[eng])
        key = ('d', eng, i)
        prev = self.dma_val[eng][i]
        if prev > 0:
            toks.append((key, prev))
        waits = self._waits(eng, toks)
        self.dma_val[eng][i] = prev + 16
        tok = (key, prev + 16)
        def issue(e, out=out, in_=in_, eng=eng):
            o = out(self.dyn[eng]) if callable(out) else out
            i2 = in_(self.dyn[eng]) if callable(in_) else in_
            return e.dma_start(out=o, in_=i2, **kw)
        self.ops[eng].append((waits, issue, (self.dma_sems[eng][i], 16)))
        self._commit(tok, reads, writes)
        return tok

    def all_tokens(self):
        allt = []
        for e in self.ENG:
            if self.cnt[e]:
                allt.append((('c', e), self.cnt[e]))
        for e in self.dma_sems:
            for i, v in enumerate(self.dma_val[e]):
                if v:
                    allt.append((('d', e, i), v))
        return allt

    def barrier(self):
        allt = self.all_tokens()
        for e in self.ENG:
            w = self._waits(e, allt)
            if w:
                self.ops[e].append((w, None, None))

    def emit(self):
        nc = self.nc
        fin = self._waits('sp', self.all_tokens())
        self.ops['sp'].append((fin, None, None))
        self.dyn = {}
        with nc.Block() as block:
            def mk(eng):
                def run(e):
                    for waits, fn, inc in self.ops[eng]:
                        for k, v in waits:
                            e.wait_ge(self.semobj[k], v)
                        if fn is not None:
                            fn(e).then_inc(inc[0], inc[1])

                def body(e):
                    if eng in ('sp', 'act') and self.rank_ap is not None:
                        with e.register("rk_" + eng) as reg:
                            e.reg_load(reg, self.rank_ap)
                            self.dyn[eng] = e.snap(reg, min_val=0, max_val=2048)
                            run(e)
                    else:
                        run(e)
                return body
            block.tensor(mk('pe'))
            block.scalar(mk('act'))
            block.vector(mk('dve'))
            block.gpsimd(mk('pool'))
            block.sync(mk('sp'))
        self.es.close()


class TB:
    def __init__(self, t, name=""):
        self.t = t
        self.b = Buf(name)


def host_consts(rev=False):
    cs = {}
    cs["identf"] = np.eye(128, dtype=np.float32)
    cs["identb"] = np.eye(128, dtype=np.float32).astype(NPBF)
    t = np.arange(T)
    row = (t // 64).astype(np.float64)
    col = (t % 64).astype(np.float64)
    nf = 16
    inv = 10000.0 ** (-np.arange(nf, dtype=np.float64) / nf)
    inv = inv.astype(np.float32).astype(np.float64)
    ang = np.concatenate([(row[:, None].astype(np.float32) * inv[None].astype(np.float32)),
                          (col[:, None].astype(np.float32) * inv[None].astype(np.float32))], axis=-1).astype(np.float32)
    cosv = np.cos(ang).astype(np.float32)
    sinv = np.sin(ang).astype(np.float32)
    C = np.zeros((128, T), np.float32)
    S = np.zeros((128, T), np.float32)
    for p in range(128):
        d = p % 64
        i = d // 2
        C[p] = cosv[:, i]
        S[p] = sinv[:, i] * (-1.0 if d % 2 == 0 else 1.0)
    if rev:
        C = np.ascontiguousarray(C[:, ::-1])
        S = np.ascontiguousarray(S[:, ::-1])
    cs["ropeC"] = C
    cs["ropeS"] = S
    j = np.arange(128)[:, None]
    i = np.arange(128)[None, :]
    mp = np.where(j >= i, 0.0, -30000.0).astype(np.float32)
    mn = np.where(j <= i, 0.0, -30000.0).astype(np.float32)
    cs["maskP"] = np.tile(mp, (1, 4)).astype(NPBF)
    cs["maskN"] = np.tile(mn, (1, 4)).astype(NPBF)
    tt = np.arange(T, dtype=np.int64)
    tk = (tt[:, None] * tt[None, :]) % T
    angT = 2.0 * np.pi * tk / T
    ct = (np.cos(angT) / math.sqrt(T)).astype(np.float32)
    stt = (np.sin(angT) / math.sqrt(T)).astype(np.float32)
    if rev:
        ct = np.ascontiguousarray(ct[::-1, ::-1])
        stt = np.ascontiguousarray(stt[::-1, ::-1])
    cs["CT"] = np.ascontiguousarray(ct.reshape(32, 128, 16, 256).transpose(2, 1, 0, 3)).astype(NPBF)
    cs["ST"] = np.ascontiguousarray(stt.reshape(32, 128, 16, 256).transpose(2, 1, 0, 3)).astype(NPBF)
    t2 = np.arange(TC, dtype=np.int64)
    a2 = 2.0 * np.pi * ((t2[:, None] * t2[None, :]) % TC) / TC
    c2 = (np.cos(a2) / math.sqrt(TC)).astype(np.float32)
    s2 = (np.sin(a2) / math.sqrt(TC)).astype(np.float32)
    if rev:
        c2 = np.ascontiguousarray(c2[::-1, ::-1])
        s2 = np.ascontiguousarray(s2[::-1, ::-1])
    cs["C256"] = np.ascontiguousarray(c2.reshape(2, 128, 256).transpose(1, 0, 2)).astype(NPBF)
    cs["S256"] = np.ascontiguousarray(s2.reshape(2, 128, 256).transpose(1, 0, 2)).astype(NPBF)
    c64 = np.arange(64)
    a3 = 2.0 * np.pi * ((c64[:, None] * c64[None, :]) % 64) / 64
    cc = np.zeros((128, 128), np.float32)
    sc = np.zeros((128, 128), np.float32)
    for g in range(2):
        cc[g * 64:(g + 1) * 64, g * 64:(g + 1) * 64] = np.cos(a3) / 8.0
        sc[g * 64:(g + 1) * 64, g * 64:(g + 1) * 64] = np.sin(a3) / 8.0
    cs["Cc"] = cc.astype(NPBF)
    cs["Sc"] = sc.astype(NPBF)
    return cs


CONST_SHAPES = {
    "identf": ([128, 128], F32), "identb": ([128, 128], BF16), "ropeC": ([128, T], F32), "ropeS": ([128, T], F32),
    "maskP": ([128, 512], BF16), "maskN": ([128, 512], BF16),
    "CT": ([16, 128, 32, 256], BF16), "ST": ([16, 128, 32, 256], BF16),
    "C256": ([128, 2, 256], BF16), "S256": ([128, 2, 256], BF16), "Cc": ([128, 128], BF16), "Sc": ([128, 128], BF16),
}

IN_SHAPES = {
    "x": [T, D], "c": [8, 128], "ctx": [TC, D], "c_ctx": [8, 128],
    "mod_w": [2, D, 6 * D], "mod_b": [2, 48, 128], "norm_g": [2, 2, 8, 128],
    "ffn_w_gate": [2, D, FF], "ffn_w_up": [2, D, FF], "ffn_w_down": [2, FF, D],
    "mix_w_in": [D, 1280], "mix_w_out": [D, D], "attn_sink": [1, 8],
    "ssm_a_re": [128, 64], "ssm_a_im": [128, 64], "ssm_log_dt": [1, 128],
    "ssm_b_re": [2, 64, 64, 16], "ssm_b_im": [2, 64, 64, 16], "ssm_c_re": [2, 64, 16, 64], "ssm_c_im": [2, 64, 16, 64],
    "ssm_d": [8, 128], "ssm_glu_w": [D, 2 * D], "final_g": [1, D],
}


def build(stage=99):
    nc = bass.Bass("TRN2", target_bir_lowering=False)
    I = {n: nc.dram_tensor(n, sh, F32, kind="ExternalInput").ap() for n, sh in IN_SHAPES.items()}
    K = {n: nc.dram_tensor(n, sh, dt, kind="ExternalInput").ap() for n, (sh, dt) in CONST_SHAPES.items()}
    rk_in = nc.dram_tensor("rk", [1, 1], mybir.dt.int32, kind="ExternalInput").ap()
    HT = T // 2
    out = nc.dram_tensor("out", [T if stage < 99 else HT, D], F32, kind="ExternalOutput").ap()
    Hs = nc.dram_tensor("Hs", [T + TC, D], F32).ap()
    mod_b_flat = I["mod_b"].rearrange("l j p -> l (j p)")

    p = Prog(nc)
    p.rank_ap = None
    top = ExitStack()
    Hloc = nc.dram_tensor("Hloc", [T // 2, D], F32).ap()
    p.Hloc = Hloc

    uid = {"n": 0}

    def sbt(st, name, shape, dt):
        uid["n"] += 1
        return TB(st.enter_context(nc.sbuf_tensor(f"s{uid['n']}_{name}", shape, dt)), name)

    def pst(st, name, shape, dt=F32):
        uid["n"] += 1
        return TB(st.enter_context(nc.psum_tensor(f"p{uid['n']}_{name}", shape, dt)), name)

    identf = sbt(top, "identf", [128, 128], F32)
    identb = sbt(top, "identb", [128, 128], BF16)
    p.dma(identf.t[:], K["identf"], writes=[identf.b])
    p.dma(identb.t[:], K["identb"], writes=[identb.b])
    ones_b = sbt(top, "ones_b", [128, 128], BF16)
    p.op('pool', lambda e: e.memset(ones_b.t[:], 1.0), writes=[ones_b.b])
    AB = sbt(top, "AB", [128, 2, 2, 4, 8], F32)
    Gs = nc.dram_tensor("Gs", [8, 128, D], F32).ap()
    ATs = nc.dram_tensor("ATs", [34, 128, 4, 128], BF16).ap()

    def load_gate(st, nm, l, col, gi):
        g = sbt(st, nm, [128, D], F32)
        p.dma(g.t[:], Gs[(l * 2 + col) * 2 + gi], writes=[g.b])
        return g
    rr = {"i": 0}

    def ew(fn, reads, writes, engs=('dve', 'pool')):
        e = engs[rr["i"] % len(engs)]
        rr["i"] += 1
        return p.op(e, fn, reads=reads, writes=writes)

    with ExitStack() as st:
        gates = sbt(st, "gates", [128, 2, 2, 2, D], F32)
        crow = sbt(st, "crow", [16, 128], F32)
        p.dma(crow.t[0:8, :], I["c"], writes=[crow.b])
        p.dma(crow.t[8:16, :], I["c_ctx"], writes=[crow.b])
        pT = pst(st, "pT", [128, 96])
        scT = sbt(st, "scT", [128, 16], F32)
        p.op('pe', lambda e: e.transpose(pT.t[:, 0:16], crow.t[:, :], identf.t[0:16, 0:16]), reads=[crow.b, identf.b], writes=[pT.b])
        p.op('act', lambda e: e.activation(scT.t[:], pT.t[:, 0:16], AF.Silu), reads=[pT.b], writes=[scT.b])
        scbc = sbt(st, "scbc", [128, 16, 128], F32)
        for ck in range(16):
            ew(lambda e, ck=ck: e.tensor_copy(scbc.t[:, ck, :], scT.t[:, ck:ck + 1].to_broadcast([128, 128])), [scT.b], [scbc.b])
        mbrow = sbt(st, "mbrow", [48, 2, 128], F32)
        ngrow = sbt(st, "ngrow", [32, 128], F32)
        p.dma(mbrow.t[:, 0, :], I["mod_b"][0], writes=[mbrow.b])
        p.dma(mbrow.t[:, 1, :], I["mod_b"][1], writes=[mbrow.b])
        p.dma(ngrow.t[:], I["norm_g"].rearrange("l i k p -> (l i k) p"), writes=[ngrow.b])
        mbT = sbt(st, "mbT", [128, 2, 48], F32)
        ngT = sbt(st, "ngT", [128, 32], F32)
        for l in range(2):
            p.op('pe', lambda e, l=l: e.transpose(pT.t[:, 0:48], mbrow.t[:, l, :], identf.t[0:48, 0:48]), reads=[mbrow.b, identf.b], writes=[pT.b])
            p.op('dve', lambda e, l=l: e.tensor_copy(mbT.t[:, l, :], pT.t[:, 0:48]), reads=[pT.b], writes=[mbT.b])
        p.op('pe', lambda e: e.transpose(pT.t[:, 0:32], ngrow.t[:, :], identf.t[0:32, 0:32]), reads=[ngrow.b, identf.b], writes=[pT.b])
        p.op('dve', lambda e: e.tensor_copy(ngT.t[:], pT.t[:, 0:32]), reads=[pT.b], writes=[ngT.b])
        macc = sbt(st, "macc", [128, 2, 48, 2], F32)
        Wk = [sbt(st, f"Wk{i}", [128, 6 * D], F32) for i in range(2)]
        pg = [pst(st, f"pg{i}", [128, 512]) for i in range(2)]
        pm = pst(st, "pm", [128, 96])
        it = 0
        for l in range(2):
            for k in range(8):
                w = Wk[it % 2]
                it += 1
                for q3 in range(3):
                    p.dma(w.t[:, q3 * 2048:(q3 + 1) * 2048], I["mod_w"][l, k * 128:(k + 1) * 128, q3 * 2048:(q3 + 1) * 2048], writes=[w.b])
                for j in range(48):
                    p.op('pe', lambda e, j=j, w=w, k=k: e.matmul(pm.t[:, 2 * j:2 * j + 2], w.t[:, j * 128:(j + 1) * 128],
                                                                 scT.t[:, k:16:8], start=True, stop=True),
                         reads=[w.b, scT.b], writes=[pm.b])
                if k == 0:
                    p.op('dve', lambda e, l=l: e.tensor_copy(macc.t[:, l].rearrange("p j c -> p (j c)"), pm.t[:, :]), reads=[pm.b], writes=[macc.b])
                else:
                    p.op('dve', lambda e, l=l: e.tensor_add(macc.t[:, l].rearrange("p j c -> p (j c)"), macc.t[:, l].rearrange("p j c -> p (j c)"), pm.t[:, :]),
                         reads=[pm.b, macc.b], writes=[macc.b])
                gi = 0
                for col in range(2):
                    for g_i, which in enumerate((2, 5)):
                        for half in range(2):
                            pgt = pg[gi % 2]
                            gi += 1
                            p.op('pe', lambda e, pgt=pgt, col=col, k=k, w=w, which=which, half=half: e.matmul(
                                pgt.t[:, :], scbc.t[:, col * 8 + k, :], w.t[:, which * D + half * 512: which * D + half * 512 + 512], start=True, stop=True),
                                reads=[scbc.b, w.b], writes=[pgt.b])
                            dst = gates.t[:, l, col, g_i, half * 512:(half + 1) * 512]
                            if k == 0:
                                p.op('dve', lambda e, dst=dst, pgt=pgt: e.tensor_copy(dst, pgt.t[:, :]), reads=[pgt.b], writes=[gates.b])
                            else:
                                p.op('dve', lambda e, dst=dst, pgt=pgt: e.tensor_add(dst, dst, pgt.t[:, :]), reads=[pgt.b, gates.b], writes=[gates.b])
        gb = sbt(st, "gb", [128, D], F32)
        for l in range(2):
            for col in range(2):
                p.op('dve', lambda e, l=l, col=col: e.tensor_add(macc.t[:, l, :, col], macc.t[:, l, :, col], mbT.t[:, l, :]), reads=[macc.b, mbT.b], writes=[macc.b])
            for g_i, which in enumerate((2, 5)):
                p.dma(gb.t[:], mod_b_flat[l:l + 1, which * D:(which + 1) * D].partition_broadcast(128), writes=[gb.b])
                for col in range(2):
                    p.op('dve', lambda e, l=l, col=col, g_i=g_i: e.tensor_add(gates.t[:, l, col, g_i, :], gates.t[:, l, col, g_i, :], gb.t[:]),
                         reads=[gb.b, gates.b], writes=[gates.b])
            for col in range(2):
                for i2 in range(2):
                    sh = macc.t[:, l, (3 * i2) * 8:(3 * i2) * 8 + 8, col]
                    scl = macc.t[:, l, (3 * i2 + 1) * 8:(3 * i2 + 1) * 8 + 8, col]
                    gn = ngT.t[:, (l * 2 + i2) * 8:(l * 2 + i2) * 8 + 8]
                    p.op('dve', lambda e, l=l, col=col, i2=i2, scl=scl, gn=gn: e.scalar_tensor_tensor(
                        AB.t[:, l, col, 2 * i2, :], scl, 1.0, gn, ALU.add, ALU.mult), reads=[macc.b, ngT.b], writes=[AB.b])
                    p.op('dve', lambda e, l=l, col=col, i2=i2, sh=sh: e.tensor_copy(AB.t[:, l, col, 2 * i2 + 1, :], sh), reads=[macc.b], writes=[AB.b])
        for l in range(2):
            for col in range(2):
                for g_i in range(2):
                    p.dma(Gs[(l * 2 + col) * 2 + g_i], gates.t[:, l, col, g_i, :], reads=[gates.b])
        p.barrier()

    if stage == 0:
        with ExitStack() as st:
            g0 = load_gate(st, "g0dbg", 0, 0, 0)
            p.dma(out[0:128, :], g0.t[:], reads=[g0.b])
        p.dma(out[128:256, 0:128], AB.t[:].rearrange("p a b c d -> p (a b c d)"), reads=[AB.b])
        p.emit()
        top.close()
        return nc

    def make_norm(st, nm, nxt=2):
        ctxn = {}
        ctxn["xt"] = [sbt(st, f"{nm}xt{i}", [128, D], F32) for i in range(nxt)]
        ctxn["junk"] = sbt(st, f"{nm}junk", [128, D], BF16)
        ctxn["xn"] = [sbt(st, f"{nm}xn{i}", [128, D], BF16) for i in range(2)]
        ctxn["ss"] = [sbt(st, f"{nm}ss{i}", [128, 1], F32) for i in range(3)]
        ctxn["tp"] = [pst(st, f"{nm}tp{i}", [128, 8, 128], BF16) for i in range(2)]
        ctxn["n"] = 0
        return ctxn

    def norm_tile(cn, src, l, col, which, dst_fn, dst_buf, xt_fixed=None, ident=None):
        n = cn["n"]
        cn["n"] += 1
        xt = xt_fixed if xt_fixed is not None else cn["xt"][n % len(cn["xt"])]
        ss = cn["ss"][n % 3]
        xn = cn["xn"][n % 2]
        tp = cn["tp"][n % 2]
        junk = cn["junk"]
        idt = ident if ident is not None else identb
        p.dma(xt.t[:], src, writes=[xt.b])
        p.op('act', lambda e: e.activation(junk.t[:], xt.t[:], AF.Square, accum_out=ss.t[:]), reads=[xt.b], writes=[junk.b, ss.b])
        p.op('dve', lambda e: e.tensor_scalar(ss.t[:], ss.t[:], 1.0 / D, EPS, ALU.mult, ALU.add), reads=[ss.b], writes=[ss.b])
        p.op('act', lambda e: e.activation(ss.t[:], ss.t[:], AF.Sqrt), reads=[ss.b], writes=[ss.b])
        p.op('dve', lambda e: e.reciprocal(ss.t[:], ss.t[:]), reads=[ss.b], writes=[ss.b])
        p.op('dve', lambda e: e.tensor_scalar(xn.t[:], xt.t[:], ss.t[:, 0:1], None, ALU.mult), reads=[xt.b, ss.b], writes=[xn.b])
        for k in range(8):
            p.op('pe', lambda e, k=k: e.transpose(tp.t[:, k, :], xn.t[:, k * 128:(k + 1) * 128], idt.t[:]), reads=[xn.b, idt.b], writes=[tp.b])
        for k in range(8):
            p.op('act', lambda e, k=k: e.activation(dst_fn(k), tp.t[:, k, :], AF.Identity,
                                                    scale=AB.t[:, l, col, 2 * which, k:k + 1], bias=AB.t[:, l, col, 2 * which + 1, k:k + 1]),
                 reads=[tp.b, AB.b], writes=[dst_buf])
        return xt, ss

    def tile_src(ti):
        return I["x"][ti * 128:(ti + 1) * 128, :] if ti < 32 else I["ctx"][(ti - 32) * 128:(ti - 31) * 128, :]

    NT = 34
    L0 = ExitStack()
    Fm = sbt(L0, "Fm", [128, NT, 512], BF16)
    with ExitStack() as stBC:
        QT = sbt(stBC, "QT", [128, 4, NT * 128], BF16)
        KT = sbt(stBC, "KT", [128, NT * 128], BF16)
        Vm = sbt(stBC, "Vm", [128, NT, 128], BF16)
        with ExitStack() as st:
            wb = sbt(st, "wb", [128, 8, 1920], BF16)
            wst = [sbt(st, f"wst{i}", [128, 1280], F32) for i in range(1)]
            for k in range(8):
                s_ = wst[0]
                p.dma(s_.t[:], I["mix_w_in"][k * 128:(k + 1) * 128, :], writes=[s_.b])
                S = s_.t
                W = wb.t
                ew(lambda e, k=k, S=S, W=W: e.tensor_copy(W[:, k, 0:512], S[:, 0:512]), [s_.b], [wb.b])
                ew(lambda e, k=k, S=S, W=W: e.tensor_copy(W[:, k, 512:640], S[:, 1152:1280]), [s_.b], [wb.b])
                ew(lambda e, k=k, S=S, W=W: e.tensor_copy(W[:, k, 640:1152].rearrange("p (j h d) -> p j h d", j=4, h=2),
                                                          S[:, 512:1024].rearrange("p (h j d) -> p j h d", h=2, j=4)), [s_.b], [wb.b])
                ew(lambda e, k=k, S=S, W=W: e.tensor_copy(W[:, k, 1152:1280], S[:, 1024:1152]), [s_.b], [wb.b])
                for two in range(2):
                    ew(lambda e, k=k, S=S, W=W, two=two: e.tensor_copy(
                        W[:, k, 1280:1792].rearrange("p (j h i t) -> p j h i t", j=4, h=2, t=2)[:, :, :, :, two],
                        S[:, 512:1024].rearrange("p (h j i t) -> p j h i t", h=2, j=4, t=2)[:, :, :, :, 1 - two]), [s_.b], [wb.b])
                    ew(lambda e, k=k, S=S, W=W, two=two: e.tensor_copy(
                        W[:, k, 1792:1920].rearrange("p (i t) -> p i t", t=2)[:, :, two],
                        S[:, 1024:1152].rearrange("p (i t) -> p i t", t=2)[:, :, 1 - two]), [s_.b], [wb.b])
            ropeCb = [sbt(st, f"ropeC{i}", [128, 512], F32) for i in range(2)]
            ropeSb = [sbt(st, f"ropeS{i}", [128, 512], F32) for i in range(2)]
            cn = make_norm(st, "b")
            nTb = [sbt(st, f"nTb{i}", [128, 8, 512], BF16) for i in range(2)]
            pf = pst(st, "pf", [128, 512])
            pv = pst(st, "pv", [128, 128])
            pq = [pst(st, f"pq{i}", [128, 512]) for i in range(2)]
            pqp = [pst(st, f"pqp{i}", [128, 512]) for i in range(2)]
            t1 = [sbt(st, f"t1{i}", [128, 512], F32) for i in range(2)]
            t2 = [sbt(st, f"t2{i}", [128, 512], F32) for i in range(2)]
            qi_ = 0
            for blk in range(9):
                ntile = 4 if blk < 8 else 2
                ncol = ntile * 128
                nT = nTb[blk % 2]
                col = 0 if blk < 8 else 1
                for tt in range(ntile):
                    ti = blk * 4 + tt
                    norm_tile(cn, tile_src(ti), 0, col, 0, lambda k, tt=tt, nT=nT: nT.t[:, k, tt * 128:(tt + 1) * 128], nT.b)
                    for k in range(8):
                        p.op('pe', lambda e, k=k, tt=tt, nT=nT: e.matmul(pf.t[:, :], nT.t[:, k, tt * 128:(tt + 1) * 128], wb.t[:, k, 0:512],
                                                                      start=(k == 0), stop=(k == 7)), reads=[nT.b, wb.b], writes=[pf.b])
                    p.op('act', lambda e, ti=ti: e.activation(Fm.t[:, ti, :], pf.t[:, :], AF.Identity), reads=[pf.b], writes=[Fm.b])
                    for k in range(8):
                        p.op('pe', lambda e, k=k, tt=tt, nT=nT: e.matmul(pv.t[:, :], nT.t[:, k, tt * 128:(tt + 1) * 128], wb.t[:, k, 512:640],
                                                                      start=(k == 0), stop=(k == 7)), reads=[nT.b, wb.b], writes=[pv.b])
                    p.op('dve', lambda e, ti=ti: e.tensor_copy(Vm.t[:, ti, :], pv.t[:, :]), reads=[pv.b], writes=[Vm.b])
                c0 = blk * 512
                ropeC = ropeCb[blk % 2]
                ropeS = ropeSb[blk % 2]
                if blk < 8:
                    p.dma(ropeC.t[:], K["ropeC"][:, c0:c0 + 512], writes=[ropeC.b])
                    p.dma(ropeS.t[:], K["ropeS"][:, c0:c0 + 512], writes=[ropeS.b])
                for oc in range(5):
                    a = pq[qi_ % 2]
                    bq = pqp[qi_ % 2]
                    u1 = t1[qi_ % 2]
                    u2 = t2[qi_ % 2]
                    qi_ += 1
                    for k in range(8):
                        p.op('pe', lambda e, k=k, oc=oc, a=a, nT=nT, ncol=ncol: e.matmul(a.t[:, 0:ncol], wb.t[:, k, 640 + 128 * oc: 768 + 128 * oc], nT.t[:, k, 0:ncol],
                                                                                   start=(k == 0), stop=(k == 7)), reads=[nT.b, wb.b], writes=[a.b])
                    dstb = QT.b if oc < 4 else KT.b
                    dst = QT.t[:, oc, c0:c0 + ncol] if oc < 4 else KT.t[:, c0:c0 + ncol]
                    if blk < 8:
                        for k in range(8):
                            p.op('pe', lambda e, k=k, oc=oc, bq=bq, nT=nT, ncol=ncol: e.matmul(bq.t[:, 0:ncol], wb.t[:, k, 1280 + 128 * oc: 1408 + 128 * oc], nT.t[:, k, 0:ncol],
                                                                                        start=(k == 0), stop=(k == 7)), reads=[nT.b, wb.b], writes=[bq.b])
                        p.op('dve', lambda e, a=a, u1=u1, ropeC=ropeC: e.tensor_mul(u1.t[:, :], a.t[:, :], ropeC.t[:, :]), reads=[a.b, ropeC.b], writes=[u1.b])
                        p.op('dve', lambda e, bq=bq, u2=u2, ropeS=ropeS: e.tensor_mul(u2.t[:, :], bq.t[:, :], ropeS.t[:, :]), reads=[bq.b, ropeS.b], writes=[u2.b])
                        p.op('pool', lambda e, dst=dst, u1=u1, u2=u2: e.tensor_add(dst, u1.t[:, :], u2.t[:, :]), reads=[u1.b, u2.b], writes=[dstb])
                    else:
                        p.op('dve', lambda e, dst=dst, a=a, ncol=ncol: e.tensor_copy(dst, a.t[:, 0:ncol]), reads=[a.b], writes=[dstb])
            p.barrier()
        with ExitStack() as st:
            maskP = sbt(st, "maskP", [128, 512], BF16)
            maskN = sbt(st, "maskN", [128, 512], BF16)
            p.dma(maskP.t[:], K["maskP"], writes=[maskP.b])
            p.dma(maskN.t[:], K["maskN"], writes=[maskN.b])
            sk = sbt(st, "sk", [128, 8], F32)
            p.dma(sk.t[:], I["attn_sink"].partition_broadcast(128), writes=[sk.b])
            p.op('act', lambda e: e.activation(sk.t[:], sk.t[:], AF.Exp), reads=[sk.b], writes=[sk.b])
            skf = sbt(st, "skf", [128, 512], F32)
            for g in range(2):
                for hh in range(4):
                    p.op('dve', lambda e, g=g, hh=hh: e.tensor_copy(skf.t[64 * g:64 * g + 64, hh * 128:(hh + 1) * 128],
                                                                  sk.t[64 * g:64 * g + 64, 4 * g + hh:4 * g + hh + 1].to_broadcast([64, 128])), reads=[sk.b], writes=[skf.b])
            pS = [pst(st, f"pS{i}", [128, 512]) for i in range(3)]
            pO = [pst(st, f"pO{i}", [128, 512]) for i in range(2)]
            pZ = [pst(st, f"pZ{i}", [128, 512]) for i in range(2)]
            PT = [sbt(st, f"PT{i}", [128, 512], BF16) for i in range(3)]
            den = [sbt(st, f"den{i}", [128, 512], F32) for i in range(2)]
            ATt = [sbt(st, f"ATt{i}", [128, 4, 128], BF16) for i in range(2)]
            OB = [[Buf() for _ in range(2)] for _ in range(2)]
            ZB = [[Buf() for _ in range(2)] for _ in range(2)]
            DB = [[Buf() for _ in range(2)] for _ in range(2)]
            si = 0
            for qi in range(NT):
                AT = ATt[qi % 2]
                for g in range(2):
                    lo, hi = 64 * g, 64 * g + 64
                    keys = [(32, None), (33, None)]
                    if qi < 32:
                        if qi > 0:
                            keys.append((qi - 1, maskP))
                        keys.append((qi, None))
                        if qi < 31:
                            keys.append((qi + 1, maskN))
                    O = pO[qi % 2]
                    Z = pZ[qi % 2]
                    Ob = OB[qi % 2][g]
                    Zb = ZB[qi % 2][g]
                    Db = DB[qi % 2][g]
                    SP = []
                    for ki in range(len(keys)):
                        SP.append((pS[si % 3], PT[si % 3]))
                        si += 1

                    def emitS(ki):
                        kt, msk = keys[ki]
                        S_ = SP[ki][0]
                        p.op('pe', lambda e, S_=S_, kt=kt, qi=qi, lo=lo, hi=hi, msk=msk: e.matmul(
                            S_.t[:, :].rearrange("p (j q) -> p j q", j=4), KT.t[lo:hi, kt * 128:(kt + 1) * 128], QT.t[lo:hi, :, qi * 128:(qi + 1) * 128],
                            start=True, stop=(msk is None)), reads=[KT.b, QT.b], writes=[S_.b])
                        if msk is not None:
                            p.op('pe', lambda e, S_=S_, msk=msk: e.matmul(S_.t[:, :], identb.t[:, :], msk.t[:, :], start=False, stop=True),
                                 reads=[identb.b, msk.b], writes=[S_.b])
                    emitS(0)
                    if len(keys) > 1:
                        emitS(1)
                    for ki, (kt, msk) in enumerate(keys):
                        S_, P_ = SP[ki]
                        p.op('act', lambda e, S_=S_, P_=P_: e.activation(P_.t[:, :], S_.t[:, :], AF.Exp, scale=0.125), reads=[S_.b], writes=[P_.b])
                        if ki + 2 < len(keys):
                            emitS(ki + 2)
                        first = ki == 0
                        last = ki == len(keys) - 1
                        p.op('pe', lambda e, O=O, kt=kt, lo=lo, hi=hi, P_=P_, first=first, last=last: e.matmul(
                            O.t[lo:hi, :], Vm.t[:, kt, lo:hi], P_.t[:, :], start=first, stop=last), reads=[Vm.b, P_.b], writes=[Ob])
                        p.op('pe', lambda e, Z=Z, lo=lo, hi=hi, P_=P_, first=first, last=last: e.matmul(
                            Z.t[lo:hi, :], ones_b.t[:, 0:64], P_.t[:, :], start=first, stop=last), reads=[ones_b.b, P_.b], writes=[Zb])
                    dn = den[qi % 2]
                    p.op('dve', lambda e, dn=dn, Z=Z, lo=lo, hi=hi: e.tensor_add(dn.t[lo:hi, :], Z.t[lo:hi, :], skf.t[lo:hi, :]), reads=[Zb, skf.b], writes=[Db])
                    p.op('act', lambda e, dn=dn, lo=lo, hi=hi: e.activation(dn.t[lo:hi, :], dn.t[lo:hi, :], AF.Ln), reads=[Db], writes=[Db])
                    p.op('act', lambda e, dn=dn, lo=lo, hi=hi: e.activation(dn.t[lo:hi, :], dn.t[lo:hi, :], AF.Exp, scale=-1.0), reads=[Db], writes=[Db])
                    p.op('dve', lambda e, dn=dn, O=O, lo=lo, hi=hi, AT=AT: e.tensor_mul(
                        AT.t[lo:hi, :, :], O.t[lo:hi, :].rearrange("p (j q) -> p j q", j=4),
                        dn.t[lo:hi, :].rearrange("p (j q) -> p j q", j=4)), reads=[Ob, Db], writes=[AT.b])
                p.dma(ATs[qi], AT.t[:], reads=[AT.b])
            p.barrier()

    with ExitStack() as st:
        wob = sbt(st, "wob", [128, 8, D], BF16)
        wst = [sbt(st, f"wost{i}", [128, D], F32) for i in range(2)]
        for j in range(8):
            s_ = wst[j % 2]
            if j < 4:
                p.dma(s_.t[:], I["mix_w_out"][j * 128:(j + 1) * 128, :], writes=[s_.b])
            else:
                jj = j - 4
                p.dma(s_.t[0:64, :], I["mix_w_out"][512 + 64 * jj:512 + 64 * jj + 64, :], writes=[s_.b])
                p.dma(s_.t[64:128, :], I["mix_w_out"][512 + 64 * (4 + jj):512 + 64 * (4 + jj) + 64, :], writes=[s_.b])
            ew(lambda e, j=j, s_=s_: e.tensor_copy(wob.t[:, j, :], s_.t[:]), [s_.b], [wob.b])
        Cc = sbt(st, "Cc", [128, 128], BF16)
        Sc = sbt(st, "Sc", [128, 128], BF16)
        p.dma(Cc.t[:], K["Cc"], writes=[Cc.b])
        p.dma(Sc.t[:], K["Sc"], writes=[Sc.b])
        CTb = [sbt(st, f"CTb{i}", [128, 32, 256], BF16) for i in range(2)]
        STb = [sbt(st, f"STb{i}", [128, 32, 256], BF16) for i in range(2)]
        pP = [pst(st, f"pP{i}", [128, 256]) for i in range(2)]
        pQ = [pst(st, f"pQ{i}", [128, 256]) for i in range(2)]
        pY = pst(st, "pY", [128, 256])
        pOo = [pst(st, f"pOo{i}", [128, 512]) for i in range(2)]
        Pb = [sbt(st, f"Pb{i}", [128, 256], BF16) for i in range(2)]
        Qb = [sbt(st, f"Qb{i}", [128, 256], BF16) for i in range(2)]
        YT = [sbt(st, f"YT{i}", [128, 4, 256], BF16) for i in range(2)]
        xres = [sbt(st, f"xres{i}", [128, D], F32) for i in range(2)]
        tmpo = [sbt(st, f"tmpo{i}", [128, 512], F32) for i in range(2)]
        hn = [sbt(st, f"hn{i}", [128, D], F32) for i in range(2)]
        ATl = [sbt(st, f"ATl{i}", [128, 4, 128], BF16) for i in range(2)]
        g1l = [load_gate(st, "g1lat", 0, 0, 0), load_gate(st, "g1ctx", 0, 1, 0)]
        cnt_ = 0
        oi = 0
        for kb in range(17):
            lat = kb < 16
            if lat:
                cb = CTb[kb % 2]
                sb_ = STb[kb % 2]
                for h4 in range(4):
                    p.dma(cb.t[:, h4 * 8:(h4 + 1) * 8, :], K["CT"][kb, :, h4 * 8:(h4 + 1) * 8, :], writes=[cb.b])
                    p.dma(sb_.t[:, h4 * 8:(h4 + 1) * 8, :], K["ST"][kb, :, h4 * 8:(h4 + 1) * 8, :], writes=[sb_.b])
                nti = 32
                t0 = 0
            else:
                cb = CTb[kb % 2]
                sb_ = STb[kb % 2]
                p.dma(cb.t[:, 0:2, :], K["C256"], writes=[cb.b])
                p.dma(sb_.t[:, 0:2, :], K["S256"], writes=[sb_.b])
                nti = 2
                t0 = 32
            Y = YT[kb % 2]
            for cc in range(4):
                a = pP[cnt_ % 2]
                b = pQ[cnt_ % 2]
                ab = Pb[cnt_ % 2]
                bb = Qb[cnt_ % 2]
                cnt_ += 1
                for i in range(nti):
                    p.op('pe', lambda e, a=a, i=i, cc=cc, cb=cb, t0=t0, nti=nti: e.matmul(a.t[:, :], Fm.t[:, t0 + i, cc * 128:(cc + 1) * 128], cb.t[:, i, :],
                                                                                     start=(i == 0), stop=(i == nti - 1)), reads=[Fm.b, cb.b], writes=[a.b])
                for i in range(nti):
                    p.op('pe', lambda e, b=b, i=i, cc=cc, sb_=sb_, t0=t0, nti=nti: e.matmul(b.t[:, :], Fm.t[:, t0 + i, cc * 128:(cc + 1) * 128], sb_.t[:, i, :],
                                                                                      start=(i == 0), stop=(i == nti - 1)), reads=[Fm.b, sb_.b], writes=[b.b])
                p.op('act', lambda e, a=a, ab=ab: e.activation(ab.t[:, :], a.t[:, :], AF.Identity), reads=[a.b], writes=[ab.b])
                p.op('act', lambda e, b=b, bb=bb: e.activation(bb.t[:, :], b.t[:, :], AF.Identity, scale=-1.0), reads=[b.b], writes=[bb.b])
                p.op('pe', lambda e, ab=ab: e.matmul(pY.t[:, :], Cc.t[:, :], ab.t[:, :], start=True, stop=False), reads=[Cc.b, ab.b], writes=[pY.b])
                p.op('pe', lambda e, bb=bb: e.matmul(pY.t[:, :], Sc.t[:, :], bb.t[:, :], start=False, stop=True), reads=[Sc.b, bb.b], writes=[pY.b])
                p.op('dve', lambda e, Y=Y, cc=cc: e.tensor_copy(Y.t[:, cc, :], pY.t[:, :]), reads=[pY.b], writes=[Y.b])
            for tt in range(2):
                ti = (kb * 2 + tt) if lat else 32 + tt
                col = 0 if lat else 1
                xr = xres[ti % 2]
                h_ = hn[ti % 2]
                p.dma(xr.t[:], tile_src(ti), writes=[xr.b])
                AT = ATl[ti % 2]
                p.dma(AT.t[:], ATs[ti], writes=[AT.b])
                for half in range(2):
                    po = pOo[oi % 2]
                    tm = tmpo[oi % 2]
                    oi += 1
                    for j in range(8):
                        lhs = Y.t[:, j, tt * 128:(tt + 1) * 128] if j < 4 else AT.t[:, j - 4, :]
                        rb = Y.b if j < 4 else AT.b
                        p.op('pe', lambda e, po=po, lhs=lhs, j=j, half=half: e.matmul(po.t[:, :], lhs, wob.t[:, j, half * 512:(half + 1) * 512],
                                                                                 start=(j == 0), stop=(j == 7)), reads=[rb, wob.b], writes=[po.b])
                    p.op('dve', lambda e, po=po, tm=tm, half=half, col=col: e.tensor_mul(tm.t[:, :], po.t[:, :], g1l[col].t[:, half * 512:(half + 1) * 512]),
                         reads=[po.b, g1l[col].b], writes=[tm.b])
                    p.op('pool', lambda e, tm=tm, xr=xr, h_=h_, half=half: e.tensor_add(h_.t[:, half * 512:(half + 1) * 512], xr.t[:, half * 512:(half + 1) * 512], tm.t[:, :]),
                         reads=[tm.b, xr.b], writes=[h_.b])
                p.dma(Hs[ti * 128:(ti + 1) * 128, :], h_.t[:], reads=[h_.b])
        p.barrier()
    L0.close()

    if stage == 1:
        for ti in range(32):
            pass
        with ExitStack() as st:
            bb_ = [sbt(st, f"bo{i}", [128, D], F32) for i in range(2)]
            for ti in range(32):
                b_ = bb_[ti % 2]
                p.dma(b_.t[:], Hs[ti * 128:(ti + 1) * 128, :], writes=[b_.b])
                p.dma(out[ti * 128:(ti + 1) * 128, :], b_.t[:], reads=[b_.b])
        p.emit()
        top.close()
        return nc

    def ffn(l, tiles, final):
        with ExitStack() as st:
            Wg = sbt(st, "Wg", [128, 8, FF], BF16)
            Wu = sbt(st, "Wu", [128, 8, FF], BF16)
            Wd = sbt(st, "Wd", [128, NFC, D], BF16)
            with ExitStack() as st2:
                stg = [sbt(st2, f"fst{i}", [128, FF], F32) for i in range(4)]
                si_ = 0
                for nm, Wt in (("ffn_w_gate", Wg), ("ffn_w_up", Wu)):
                    for k in range(8):
                        s_ = stg[si_ % 4]
                        si_ += 1
                        p.dma(s_.t[:, 0:1408], I[nm][l, k * 128:(k + 1) * 128, 0:1408], writes=[s_.b])
                        p.dma(s_.t[:, 1408:FF], I[nm][l, k * 128:(k + 1) * 128, 1408:FF], writes=[s_.b])
                        p.op('dve', lambda e, Wt=Wt, k=k, s_=s_: e.tensor_copy(Wt.t[:, k, 0:1408], s_.t[:, 0:1408]), reads=[s_.b], writes=[Wt.b])
                        p.op('act', lambda e, Wt=Wt, k=k, s_=s_: e.activation(Wt.t[:, k, 1408:FF], s_.t[:, 1408:FF], AF.Identity), reads=[s_.b], writes=[Wt.b])
                for fc in range(0, NFC, 2):
                    s_ = stg[si_ % 4]
                    si_ += 1
                    for q2 in range(2):
                        p.dma(s_.t[:, q2 * D:(q2 + 1) * D], I["ffn_w_down"][l, (fc + q2) * 128:(fc + q2 + 1) * 128, :], writes=[s_.b])
                    p.op('dve' if (fc // 2) % 2 == 0 else 'act', (lambda e, fc=fc, s_=s_: e.tensor_copy(Wd.t[:, fc:fc + 2, :].rearrange("p a d -> p (a d)"), s_.t[:, 0:2 * D])) if (fc // 2) % 2 == 0 else (lambda e, fc=fc, s_=s_: e.activation(Wd.t[:, fc:fc + 2, :].rearrange("p a d -> p (a d)"), s_.t[:, 0:2 * D], AF.Identity)), reads=[s_.b], writes=[Wd.b])
                p.barrier()
            cn = make_norm(st, "f", nxt=0)
            g2l = [load_gate(st, "g2lat", l, 0, 1)] + ([load_gate(st, "g2ctx", l, 1, 1)] if not final else [])
            hx = [sbt(st, f"hx{i}", [128, D], F32) for i in range(3)]
            nTf = [sbt(st, f"nTf{i}", [128, 8, 256], BF16) for i in range(2)]
            actT = [sbt(st, f"actT{i}", [128, NFC, 256], BF16) for i in range(1)]
            sg = [sbt(st, f"sg{i}", [128, 256], F32) for i in range(2)]
            pgt = [pst(st, f"fpg{i}", [128, 256]) for i in range(2)]
            put = [pst(st, f"fpu{i}", [128, 256]) for i in range(2)]
            pdn = [pst(st, f"fpd{i}", [128, 512]) for i in range(2)]
            tm_ = [sbt(st, f"ftm{i}", [128, 512], F32) for i in range(2)]
            ho = [sbt(st, f"ho{i}", [128, D], F32) for i in range(2)]
            fss = [sbt(st, f"fss{i}", [128, 1], F32) for i in range(2)]
            fj = cn["junk"]
            fingt = None
            if final:
                fingt = sbt(st, "fing", [128, D], F32)
                p.dma(fingt.t[:], I["final_g"].partition_broadcast(128), writes=[fingt.b])
            gi_ = 0
            di_ = 0
            hi_ = 0
            for b0 in range(0, len(tiles), 2):
                tl = tiles[b0:b0 + 2]
                nT = nTf[(b0 // 2) % 2]
                aT = actT[0]
                xts = []
                for tt, ti in enumerate(tl):
                    col = 0 if ti < 32 else 1
                    hxt = hx[hi_ % 3]
                    hi_ += 1
                    src_ = Hs[ti * 128:(ti + 1) * 128, :]
                    norm_tile(cn, src_, l, col, 1, lambda k, tt=tt, nT=nT: nT.t[:, k, tt * 128:(tt + 1) * 128], nT.b, xt_fixed=hxt)
                    xts.append(hxt)
                for fc in range(NFC):
                    a = pgt[gi_ % 2]
                    b = put[gi_ % 2]
                    s2 = sg[gi_ % 2]
                    gi_ += 1
                    for k in range(8):
                        p.op('pe', lambda e, a=a, k=k, fc=fc, nT=nT: e.matmul(a.t[:, :], Wg.t[:, k, fc * 128:(fc + 1) * 128], nT.t[:, k, :], start=(k == 0), stop=(k == 7)),
                             reads=[Wg.b, nT.b], writes=[a.b])
                    for k in range(8):
                        p.op('pe', lambda e, b=b, k=k, fc=fc, nT=nT: e.matmul(b.t[:, :], Wu.t[:, k, fc * 128:(fc + 1) * 128], nT.t[:, k, :], start=(k == 0), stop=(k == 7)),
                             reads=[Wu.b, nT.b], writes=[b.b])
                    p.op('act', lambda e, a=a, s2=s2: e.activation(s2.t[:, :], a.t[:, :], AF.Silu), reads=[a.b], writes=[s2.b])
                    p.op('dve', lambda e, b=b, s2=s2, aT=aT, fc=fc: e.tensor_mul(aT.t[:, fc, :], s2.t[:, :], b.t[:, :]), reads=[b.b, s2.b], writes=[aT.b])
                for tt, ti in enumerate(tl):
                    col = 0 if ti < 32 else 1
                    hxt = xts[tt]
                    h_ = ho[ti % 2]
                    for half in range(2):
                        po = pdn[di_ % 2]
                        tm = tm_[di_ % 2]
                        di_ += 1
                        for fc in range(NFC):
                            p.op('pe', lambda e, po=po, fc=fc, tt=tt, half=half, aT=aT: e.matmul(po.t[:, :], aT.t[:, fc, tt * 128:(tt + 1) * 128], Wd.t[:, fc, half * 512:(half + 1) * 512],
                                                                                         start=(fc == 0), stop=(fc == NFC - 1)), reads=[aT.b, Wd.b], writes=[po.b])
                        p.op('dve', lambda e, po=po, tm=tm, half=half, col=col: e.tensor_mul(tm.t[:, :], po.t[:, :], g2l[col].t[:, half * 512:(half + 1) * 512]),
                             reads=[po.b, g2l[col].b], writes=[tm.b])
                        p.op('pool', lambda e, tm=tm, hxt=hxt, h_=h_, half=half: e.tensor_add(h_.t[:, half * 512:(half + 1) * 512], hxt.t[:, half * 512:(half + 1) * 512], tm.t[:, :]),
                             reads=[tm.b, hxt.b], writes=[h_.b])
                    if not final:
                        p.dma(Hs[ti * 128:(ti + 1) * 128, :], h_.t[:], reads=[h_.b])
                    else:
                        ss = fss[ti % 2]
                        p.op('act', lambda e, h_=h_, ss=ss: e.activation(fj.t[:], h_.t[:], AF.Square, accum_out=ss.t[:]), reads=[h_.b], writes=[fj.b, ss.b])
                        p.op('dve', lambda e, ss=ss: e.tensor_scalar(ss.t[:], ss.t[:], 1.0 / D, EPS, ALU.mult, ALU.add), reads=[ss.b], writes=[ss.b])
                        p.op('act', lambda e, ss=ss: e.activation(ss.t[:], ss.t[:], AF.Sqrt), reads=[ss.b], writes=[ss.b])
                        p.op('dve', lambda e, ss=ss: e.reciprocal(ss.t[:], ss.t[:]), reads=[ss.b], writes=[ss.b])
                        p.op('dve', lambda e, h_=h_, ss=ss: e.scalar_tensor_tensor(h_.t[:], h_.t[:], ss.t[:, 0:1], fingt.t[:], ALU.mult, ALU.mult),
                             reads=[h_.b, ss.b, fingt.b], writes=[h_.b])
                        p.dma(out[ti * 128:(ti + 1) * 128, :], h_.t[:], reads=[h_.b])
            p.barrier()

    ffn(0, list(range(34)), False)

    if stage == 2:
        with ExitStack() as st:
            bb_ = [sbt(st, f"bo{i}", [128, D], F32) for i in range(2)]
            for ti in range(32):
                b_ = bb_[ti % 2]
                p.dma(b_.t[:], Hs[ti * 128:(ti + 1) * 128, :], writes=[b_.b])
                p.dma(out[ti * 128:(ti + 1) * 128, :], b_.t[:], reads=[b_.b])
        p.emit()
        top.close()
        return nc

    s5_layer(nc, p, I, K, Hs, AB, load_gate, identf, identb, sbt, pst, ew, make_norm, norm_tile)
    if stage == 3:
        with ExitStack() as st:
            bb_ = [sbt(st, f"bo3{i}", [128, D], F32) for i in range(2)]
            for ti in range(32):
                b_ = bb_[ti % 2]
                p.dma(b_.t[:], Hs[ti * 128:(ti + 1) * 128, :], writes=[b_.b])
                p.dma(out[ti * 128:(ti + 1) * 128, :], b_.t[:], reads=[b_.b])
        p.emit()
        top.close()
        return nc
    ffn(1, list(range(16)), True)
    p.emit()
    top.close()
    return nc


def rev(ap):
    apl = [list(a) for a in ap.ap]
    n = apl[-1][1]
    stp = apl[-1][0]
    apl[-1][0] = -stp
    return bass.AP(ap.tensor, ap.offset + (n - 1) * stp, apl)


def bcast_last(ap, n):
    apl = [list(a) for a in ap.ap] + [[0, n]]
    return bass.AP(ap.tensor, ap.offset, apl)


def s5_layer(nc, p, I, K, Hs, AB, load_gate, identf, identb, sbt, pst, ew, make_norm, norm_tile, dbg=None):
    Hloc = Hs
    YGs = nc.dram_tensor("YGs", [8, 128, T // 2], BF16).ap()
    NTOK = T + TC
    with ExitStack() as S:
        nT = sbt(S, "s5nT", [128, 8, NTOK], BF16)
        with ExitStack() as st:
            cn = make_norm(st, "s5n", nxt=3)
            for ti in range(34):
                col = 0 if ti < 32 else 1
                norm_tile(cn, Hs[ti * 128:(ti + 1) * 128, :], 1, col, 0, lambda k, ti=ti: nT.t[:, k, ti * 128:(ti + 1) * 128], nT.b)
            p.barrier()
        PRM = sbt(S, "s5prm", [128, 3, 64], F32)
        Cw = sbt(S, "s5Cw", [128, 64, 2, 32], F32)
        Bz = sbt(S, "s5Bz", [128, 2, 64, 2, 16], F32)
        LC = 4
        NCM = 512 // LC
        PW = sbt(S, "s5PW", [128, LC + 1, 2, 64], F32)
        PRM4 = sbt(S, "s5prm4", [128, 3, 64], F32)
        dT = sbt(S, "s5dT", [128, 8], F32)
        with ExitStack() as st:
            pt = pst(st, "s5pt", [128, 128])
            def V(nm):
                return sbt(st, "s5v_" + nm, [128, 64], F32)
            arow = sbt(st, "s5arow", [64, 2, 128], F32)
            p.dma(arow.t[:, 0, :], I["ssm_a_re"].rearrange("(dq g) p -> dq (g p)", g=2), writes=[arow.b])
            p.dma(arow.t[:, 1, :], I["ssm_a_im"].rearrange("(dq g) p -> dq (g p)", g=2), writes=[arow.b])
            are, aim = V("are"), V("aim")
            for i_, dst in enumerate((are, aim)):
                p.op('pe', lambda e, i_=i_: e.transpose(pt.t[:, 0:64], arow.t[:, i_, :], identf.t[0:64, 0:64]), reads=[arow.b, identf.b], writes=[pt.b])
                p.op('dve', lambda e, dst=dst: e.tensor_copy(dst.t[:], pt.t[:, 0:64]), reads=[pt.b], writes=[dst.b])
            drow = sbt(st, "s5drow", [8, 128], F32)
            p.dma(drow.t[:], I["ssm_d"], writes=[drow.b])
            p.op('pe', lambda e: e.transpose(pt.t[:, 0:8], drow.t[:, :], identf.t[0:8, 0:8]), reads=[drow.b, identf.b], writes=[pt.b])
            p.op('dve', lambda e: e.tensor_copy(dT.t[:], pt.t[:, 0:8]), reads=[pt.b], writes=[dT.b])
            ldb = sbt(st, "s5ldb", [128, 128], F32)
            p.dma(ldb.t[:], I["ssm_log_dt"].partition_broadcast(128), writes=[ldb.b])
            dt = V("dt")
            for g2 in range(2):
                p.op('dve', lambda e, g2=g2: e.tensor_copy(dt.t[64 * g2:64 * g2 + 64, :], ldb.t[64 * g2:64 * g2 + 64, g2:128:2]), reads=[ldb.b], writes=[dt.b])
            p.op('act', lambda e: e.activation(dt.t[:], dt.t[:], AF.Exp), reads=[dt.b], writes=[dt.b])
            xr, th, mag = V("xr"), V("th"), V("mag")
            p.op('dve', lambda e: e.tensor_mul(xr.t[:], are.t[:], dt.t[:]), reads=[are.b, dt.b], writes=[xr.b])
            p.op('dve', lambda e: e.tensor_mul(th.t[:], aim.t[:], dt.t[:]), reads=[aim.b, dt.b], writes=[th.b])
            p.op('act', lambda e: e.activation(mag.t[:], xr.t[:], AF.Exp), reads=[xr.b], writes=[mag.b])
            kf = V("kf")
            ki = sbt(st, "s5ki", [128, 64], mybir.dt.int32)
            p.op('dve', lambda e: e.tensor_scalar(kf.t[:], th.t[:], 1.0 / (2 * math.pi), None, ALU.mult), reads=[th.b], writes=[kf.b])
            p.op('dve', lambda e: e.tensor_copy(ki.t[:], kf.t[:]), reads=[kf.b], writes=[ki.b])
            p.op('dve', lambda e: e.tensor_copy(kf.t[:], ki.t[:]), reads=[ki.b], writes=[kf.b])
            C1 = 6.28125
            C2 = 2 * math.pi - 6.28125
            thm = V("thm")
            p.op('dve', lambda e: e.scalar_tensor_tensor(thm.t[:], kf.t[:], -C1, th.t[:], ALU.mult, ALU.add), reads=[kf.b, th.b], writes=[thm.b])
            p.op('dve', lambda e: e.scalar_tensor_tensor(thm.t[:], kf.t[:], -C2, thm.t[:], ALU.mult, ALU.add), reads=[kf.b, thm.b], writes=[thm.b])
            xq, u2, qs, qc = V("xq"), V("u2"), V("qs"), V("qc")
            p.op('dve', lambda e: e.tensor_scalar(xq.t[:], thm.t[:], 0.25, None, ALU.mult), reads=[thm.b], writes=[xq.b])
            p.op('dve', lambda e: e.tensor_mul(u2.t[:], xq.t[:], xq.t[:]), reads=[xq.b], writes=[u2.b])
            sc_ = [(-1.0) ** k / math.factorial(2 * k + 1) for k in range(9)]
            cc_ = [(-1.0) ** k / math.factorial(2 * k) for k in range(9)]
            p.op('dve', lambda e: e.tensor_scalar(qs.t[:], u2.t[:], sc_[8], None, ALU.mult), reads=[u2.b], writes=[qs.b])
            p.op('dve', lambda e: e.tensor_scalar(qc.t[:], u2.t[:], cc_[8], None, ALU.mult), reads=[u2.b], writes=[qc.b])
            for k in range(7, 0, -1):
                p.op('dve', lambda e, k=k: e.scalar_tensor_tensor(qs.t[:], qs.t[:], sc_[k], u2.t[:], ALU.add, ALU.mult), reads=[qs.b, u2.b], writes=[qs.b])
                p.op('dve', lambda e, k=k: e.scalar_tensor_tensor(qc.t[:], qc.t[:], cc_[k], u2.t[:], ALU.add, ALU.mult), reads=[qc.b, u2.b], writes=[qc.b])
            sn, cs = V("sn"), V("cs")
            p.op('dve', lambda e: e.scalar_tensor_tensor(sn.t[:], qs.t[:], 1.0, xq.t[:], ALU.add, ALU.mult), reads=[qs.b, xq.b], writes=[sn.b])
            p.op('dve', lambda e: e.tensor_scalar(cs.t[:], qc.t[:], 1.0, None, ALU.add), reads=[qc.b], writes=[cs.b])
            ta, tb_ = V("ta"), V("tb")
            for _ in range(2):
                p.op('dve', lambda e: e.tensor_mul(ta.t[:], cs.t[:], cs.t[:]), reads=[cs.b], writes=[ta.b])
                p.op('dve', lambda e: e.tensor_mul(tb_.t[:], sn.t[:], sn.t[:]), reads=[sn.b], writes=[tb_.b])
                p.op('dve', lambda e: e.scalar_tensor_tensor(sn.t[:], cs.t[:], 2.0, sn.t[:], ALU.mult, ALU.mult), reads=[cs.b, sn.b], writes=[sn.b])
                p.op('dve', lambda e: e.tensor_sub(cs.t[:], ta.t[:], tb_.t[:]), reads=[ta.b, tb_.b], writes=[cs.b])
            p.op('dve', lambda e: e.tensor_copy(PRM.t[:, 0, :], mag.t[:]), reads=[mag.b], writes=[PRM.b])
            p.op('dve', lambda e: e.tensor_copy(PRM.t[:, 1, :], cs.t[:]), reads=[cs.b], writes=[PRM.b])
            p.op('dve', lambda e: e.tensor_copy(PRM.t[:, 2, :], sn.t[:]), reads=[sn.b], writes=[PRM.b])
            p.op('pool', lambda e: e.memset(PW.t[:, 0, 0, :], 1.0), writes=[PW.b])
            p.op('pool', lambda e: e.memset(PW.t[:, 0, 1, :], 0.0), writes=[PW.b])
            p.op('dve', lambda e: e.tensor_mul(PW.t[:, 1, 0, :], mag.t[:], cs.t[:]), reads=[mag.b, cs.b], writes=[PW.b])
            p.op('dve', lambda e: e.tensor_mul(PW.t[:, 1, 1, :], mag.t[:], sn.t[:]), reads=[mag.b, sn.b], writes=[PW.b])
            for k in range(2, LC + 1):
                p.op('dve', lambda e, k=k: e.tensor_mul(ta.t[:], PW.t[:, k - 1, 0, :], PW.t[:, 1, 0, :]), reads=[PW.b], writes=[ta.b])
                p.op('dve', lambda e, k=k: e.tensor_mul(tb_.t[:], PW.t[:, k - 1, 1, :], PW.t[:, 1, 1, :]), reads=[PW.b], writes=[tb_.b])
                p.op('dve', lambda e, k=k: e.tensor_sub(PW.t[:, k, 0, :], ta.t[:], tb_.t[:]), reads=[ta.b, tb_.b], writes=[PW.b])
                p.op('dve', lambda e, k=k: e.tensor_mul(ta.t[:], PW.t[:, k - 1, 0, :], PW.t[:, 1, 1, :]), reads=[PW.b], writes=[ta.b])
                p.op('dve', lambda e, k=k: e.tensor_mul(tb_.t[:], PW.t[:, k - 1, 1, :], PW.t[:, 1, 0, :]), reads=[PW.b], writes=[tb_.b])
                p.op('dve', lambda e, k=k: e.tensor_add(PW.t[:, k, 1, :], ta.t[:], tb_.t[:]), reads=[ta.b, tb_.b], writes=[PW.b])
            c4, s4, r4 = V("c4"), V("s4"), V("r4")
            p.op('dve', lambda e: e.tensor_copy(c4.t[:], cs.t[:]), reads=[cs.b], writes=[c4.b])
            p.op('dve', lambda e: e.tensor_copy(s4.t[:], sn.t[:]), reads=[sn.b], writes=[s4.b])
            p.op('dve', lambda e: e.tensor_copy(r4.t[:], mag.t[:]), reads=[mag.b], writes=[r4.b])
            for _ in range(int(round(math.log2(LC)))):
                p.op('dve', lambda e: e.tensor_mul(ta.t[:], c4.t[:], c4.t[:]), reads=[c4.b], writes=[ta.b])
                p.op('dve', lambda e: e.tensor_mul(tb_.t[:], s4.t[:], s4.t[:]), reads=[s4.b], writes=[tb_.b])
                p.op('dve', lambda e: e.scalar_tensor_tensor(s4.t[:], c4.t[:], 2.0, s4.t[:], ALU.mult, ALU.mult), reads=[c4.b, s4.b], writes=[s4.b])
                p.op('dve', lambda e: e.tensor_sub(c4.t[:], ta.t[:], tb_.t[:]), reads=[ta.b, tb_.b], writes=[c4.b])
                p.op('dve', lambda e: e.tensor_mul(r4.t[:], r4.t[:], r4.t[:]), reads=[r4.b], writes=[r4.b])
            p.op('dve', lambda e: e.tensor_copy(PRM4.t[:, 0, :], r4.t[:]), reads=[r4.b], writes=[PRM4.b])
            p.op('dve', lambda e: e.tensor_copy(PRM4.t[:, 1, :], c4.t[:]), reads=[c4.b], writes=[PRM4.b])
            p.op('dve', lambda e: e.tensor_copy(PRM4.t[:, 2, :], s4.t[:]), reads=[s4.b], writes=[PRM4.b])
            nr, ni, den, fre, fim = V("nr"), V("ni"), V("den"), V("fre"), V("fim")
            p.op('dve', lambda e: e.tensor_mul(nr.t[:], mag.t[:], cs.t[:]), reads=[mag.b, cs.b], writes=[nr.b])
            p.op('dve', lambda e: e.tensor_scalar(nr.t[:], nr.t[:], -1.0, None, ALU.add), reads=[nr.b], writes=[nr.b])
            p.op('dve', lambda e: e.tensor_mul(ni.t[:], mag.t[:], sn.t[:]), reads=[mag.b, sn.b], writes=[ni.b])
            p.op('dve', lambda e: e.tensor_mul(den.t[:], are.t[:], are.t[:]), reads=[are.b], writes=[den.b])
            p.op('dve', lambda e: e.tensor_mul(ta.t[:], aim.t[:], aim.t[:]), reads=[aim.b], writes=[ta.b])
            p.op('dve', lambda e: e.tensor_add(den.t[:], den.t[:], ta.t[:]), reads=[den.b, ta.b], writes=[den.b])
            p.op('dve', lambda e: e.reciprocal(den.t[:], den.t[:]), reads=[den.b], writes=[den.b])
            p.op('dve', lambda e: e.tensor_mul(ta.t[:], nr.t[:], are.t[:]), reads=[nr.b, are.b], writes=[ta.b])
            p.op('dve', lambda e: e.tensor_mul(tb_.t[:], ni.t[:], aim.t[:]), reads=[ni.b, aim.b], writes=[tb_.b])
            p.op('dve', lambda e: e.tensor_add(fre.t[:], ta.t[:], tb_.t[:]), reads=[ta.b, tb_.b], writes=[fre.b])
            p.op('dve', lambda e: e.tensor_mul(fre.t[:], fre.t[:], den.t[:]), reads=[fre.b, den.b], writes=[fre.b])
            p.op('dve', lambda e: e.tensor_mul(ta.t[:], ni.t[:], are.t[:]), reads=[ni.b, are.b], writes=[ta.b])
            p.op('dve', lambda e: e.tensor_mul(tb_.t[:], nr.t[:], aim.t[:]), reads=[nr.b, aim.b], writes=[tb_.b])
            p.op('dve', lambda e: e.tensor_sub(fim.t[:], ta.t[:], tb_.t[:]), reads=[ta.b, tb_.b], writes=[fim.b])
            p.op('dve', lambda e: e.tensor_mul(fim.t[:], fim.t[:], den.t[:]), reads=[fim.b, den.b], writes=[fim.b])
            Br = [sbt(st, f"s5Br{i}", [128, 64, 16], F32) for i in range(2)]
            for i_, nm in enumerate(("ssm_b_re", "ssm_b_im")):
                src = I[nm].rearrange("d (q g) p h -> g p (d q) h", g=2)
                for g2 in range(2):
                    p.dma(Br[i_].t[64 * g2:64 * g2 + 64, :, :], src[g2], writes=[Br[i_].b])
            p.op('pool', lambda e: e.memset(Bz.t[:].rearrange("p a b c d -> p (a b c d)"), 0.0), writes=[Bz.b])
            m1 = sbt(st, "s5m1", [128, 64, 16], F32)
            m2 = sbt(st, "s5m2", [128, 64, 16], F32)
            fre_b = bcast_last(fre.t[:], 16)
            fim_b = bcast_last(fim.t[:], 16)
            p.op('dve', lambda e: e.tensor_mul(m1.t[:], Br[0].t[:], fre_b), reads=[Br[0].b, fre.b], writes=[m1.b])
            p.op('dve', lambda e: e.tensor_mul(m2.t[:], Br[1].t[:], fim_b), reads=[Br[1].b, fim.b], writes=[m2.b])
            for g2 in range(2):
                p.op('dve', lambda e, g2=g2: e.tensor_sub(Bz.t[64 * g2:64 * g2 + 64, 0, :, g2, :], m1.t[64 * g2:64 * g2 + 64], m2.t[64 * g2:64 * g2 + 64]),
                     reads=[m1.b, m2.b], writes=[Bz.b])
            p.op('dve', lambda e: e.tensor_mul(m1.t[:], Br[1].t[:], fre_b), reads=[Br[1].b, fre.b], writes=[m1.b])
            p.op('dve', lambda e: e.tensor_mul(m2.t[:], Br[0].t[:], fim_b), reads=[Br[0].b, fim.b], writes=[m2.b])
            for g2 in range(2):
                p.op('dve', lambda e, g2=g2: e.tensor_add(Bz.t[64 * g2:64 * g2 + 64, 1, :, g2, :], m1.t[64 * g2:64 * g2 + 64], m2.t[64 * g2:64 * g2 + 64]),
                     reads=[m1.b, m2.b], writes=[Bz.b])
            pts = [pst(st, f"s5ptb{i}", [128, 128]) for i in range(2)]
            n_ = 0
            p.op('pool', lambda e: e.memset(Cw.t[:].rearrange("p a b c -> p (a b c)"), 0.0), writes=[Cw.b])
            Cn = [sbt(st, f"s5Cn{i}", [128, 2, 64], F32) for i in range(2)]
            for ri, nm in enumerate(("ssm_c_re", "ssm_c_im")):
                for dQ in range(16):
                    d_, Q_ = dQ // 8, dQ % 8
                    cnb = Cn[n_ % 2]
                    pp = pts[n_ % 2]
                    n_ += 1
                    src = I[nm][d_, Q_ * 4 * 2:(Q_ * 4 + 4) * 2].rearrange("g h p -> (g h) p")
                    p.dma(cnb.t[:, 0, :], src, writes=[cnb.b])
                    p.dma(cnb.t[:, 1, :], src, writes=[cnb.b])
                    p.op('pe', lambda e, pp=pp, cnb=cnb: e.transpose(pp.t[:, :], cnb.t[:, :, :].rearrange("p a b -> p (a b)"), identf.t[:, :]),
                         reads=[cnb.b, identf.b], writes=[pp.b])
                    sgn = 1.0 if ri == 0 else -1.0
                    for g2 in range(2):
                        p.op('act', lambda e, pp=pp, g2=g2, ri=ri, dQ=dQ, sgn=sgn: e.activation(
                            Cw.t[64 * g2:64 * g2 + 64, dQ * 4:(dQ + 1) * 4, ri, 16 * g2:16 * g2 + 16],
                            pp.t[64 * g2:64 * g2 + 64, :].rearrange("p (q g h) -> p q g h", q=4, g=2)[:, :, g2, :], AF.Identity, scale=sgn),
                            reads=[pp.b], writes=[Cw.b])
            p.barrier()

        if dbg is not None:
            dbg(PRM, Cw, nT)
            return

        with ExitStack() as st:
            yacc = sbt(st, "s5yacc", [128, T // 2], F32)
            Ec = sbt(st, "s5Ec", [128, 8, NCM], F32)
            Es = sbt(st, "s5Es", [128, 8, NCM], F32)
            wc = sbt(st, "s5wc", [128, 8], F32)
            ws = sbt(st, "s5ws", [128, 8], F32)
            wt1 = sbt(st, "s5wt1", [128, 8], F32)
            wt2 = sbt(st, "s5wt2", [128, 8], F32)
            et1 = sbt(st, "s5et1", [128, 8, NCM // 2], F32)
            et2 = sbt(st, "s5et2", [128, 8, NCM // 2], F32)
            hp = sbt(st, "s5hp", [128, 8, 2], F32)
            ini = sbt(st, "s5ini", [128, 8, 2], F32)
            itmp = sbt(st, "s5itmp", [128, 8, 2], F32)
            nsth = sbt(st, "s5nsth", [128, 64], F32)
            p.op('dve', lambda e: e.tensor_scalar(nsth.t[:], PRM4.t[:, 2, :], -1.0, None, ALU.mult), reads=[PRM4.b], writes=[nsth.b])
            hpb = [Buf() for _ in range(8)]
            CwP = sbt(st, "s5CwP", [128, LC + 1, 8, 2, 32], F32)
            ctm = [sbt(st, f"s5ctm{i}", [128, 4, 32], F32) for i in range(2)]
            BP = sbt(st, "s5BP", [128, LC, 2, 2, 4, 32], F32)
            Win = sbt(st, "s5Win", [128, 2, 2, LC, 128], BF16)
            Mw = sbt(st, "s5Mw", [128, 2, LC, 32], BF16)
            Wout = sbt(st, "s5Wout", [128, 8, LC, 2, 32], BF16)
            pts = [pst(st, f"s5ptq{i}", [128, 128]) for i in range(1)]
            pk = pst(st, "s5pk", [128, 16, 32])
            NS = 8
            Wk_ = [[sbt(st, f"s5w{j}_{i}", [128, 2, NCM], F32) for i in range(3)] for j in range(NS)]
            Hb = [sbt(st, f"s5hb{j}", [128, 2, NCM + 4], BF16) for j in range(NS)]
            Esg = sbt(st, "s5Esg", [128, 8, 2, NCM], F32)

            def swap2(tb_, ncn):
                b1 = tb_.t[:, 1, 0:ncn]
                apl = [list(a_) for a_ in b1.ap]
                return bass.AP(b1.tensor, b1.offset, [apl[0], [-NCM, 2], apl[-1]])

            def bc2(ap2):
                apl = [list(a_) for a_ in ap2.ap]
                return bass.AP(ap2.tensor, ap2.offset, [apl[0], [0, 2], apl[-1]])
            pS = [pst(st, f"s5pS{j}", [128, 512]) for j in range(4)]
            py = [pst(st, f"s5py{i}", [128, LC, NCM]) for i in range(2)]
            ygb = [sbt(st, f"s5yg{i}", [128, 512], BF16) for i in range(2)]
            gtb = [sbt(st, f"s5gt{i}", [128, 512], F32) for i in range(2)]
            tn_ = 0
            blkc = 0
            yi_ = 0
            psi = 0
            for Q in range(8):
                for d_ in range(2):
                    us = slice(d_ * 32 + Q * 4, d_ * 32 + Q * 4 + 4)
                    ts_ = slice(d_ * 4, d_ * 4 + 4)
                    c0_ = Cw.t[:, us, 0, :]
                    c1_ = Cw.t[:, us, 1, :]
                    for k in range(LC + 1):
                        pr = bcast_last(PW.t[:, k, 0, us], 32)
                        pi_ = bcast_last(PW.t[:, k, 1, us], 32)
                        t1_, t2_ = ctm
                        p.op('dve', lambda e, t1_=t1_, c0_=c0_, pr=pr: e.tensor_mul(t1_.t[:], c0_, pr), reads=[Cw.b, PW.b], writes=[t1_.b])
                        p.op('pool', lambda e, t2_=t2_, c1_=c1_, pi_=pi_: e.tensor_mul(t2_.t[:], c1_, pi_), reads=[Cw.b, PW.b], writes=[t2_.b])
                        p.op('dve', lambda e, t1_=t1_, t2_=t2_, k=k, ts_=ts_: e.tensor_add(CwP.t[:, k, ts_, 0, :], t1_.t[:], t2_.t[:]), reads=[t1_.b, t2_.b], writes=[CwP.b])
                        p.op('dve', lambda e, t1_=t1_, c1_=c1_, pr=pr: e.tensor_mul(t1_.t[:], c1_, pr), reads=[Cw.b, PW.b], writes=[t1_.b])
                        p.op('pool', lambda e, t2_=t2_, c0_=c0_, pi_=pi_: e.tensor_mul(t2_.t[:], c0_, pi_), reads=[Cw.b, PW.b], writes=[t2_.b])
                        p.op('dve', lambda e, t1_=t1_, t2_=t2_, k=k, ts_=ts_: e.tensor_sub(CwP.t[:, k, ts_, 1, :], t1_.t[:], t2_.t[:]), reads=[t1_.b, t2_.b], writes=[CwP.b])
                    b0_ = Bz.t[:, 0, us, :, :].rearrange("p a b c -> p a (b c)")
                    b1_ = Bz.t[:, 1, us, :, :].rearrange("p a b c -> p a (b c)")
                    for s_ in range(LC):
                        k = LC - 1 - s_
                        pr = bcast_last(PW.t[:, k, 0, us], 32)
                        pi_ = bcast_last(PW.t[:, k, 1, us], 32)
                        t1_, t2_ = ctm
                        p.op('dve', lambda e, t1_=t1_, b0_=b0_, pr=pr: e.tensor_mul(t1_.t[:], b0_, pr), reads=[Bz.b, PW.b], writes=[t1_.b])
                        p.op('pool', lambda e, t2_=t2_, b1_=b1_, pi_=pi_: e.tensor_mul(t2_.t[:], b1_, pi_), reads=[Bz.b, PW.b], writes=[t2_.b])
                        p.op('dve', lambda e, t1_=t1_, t2_=t2_, s_=s_, d_=d_: e.tensor_sub(BP.t[:, s_, 0, d_, :, :], t1_.t[:], t2_.t[:]), reads=[t1_.b, t2_.b], writes=[BP.b])
                        p.op('dve', lambda e, t1_=t1_, b0_=b0_, pi_=pi_: e.tensor_mul(t1_.t[:], b0_, pi_), reads=[Bz.b, PW.b], writes=[t1_.b])
                        p.op('pool', lambda e, t2_=t2_, b1_=b1_, pr=pr: e.tensor_mul(t2_.t[:], b1_, pr), reads=[Bz.b, PW.b], writes=[t2_.b])
                        p.op('dve', lambda e, t1_=t1_, t2_=t2_, s_=s_, d_=d_: e.tensor_add(BP.t[:, s_, 1, d_, :, :], t1_.t[:], t2_.t[:]), reads=[t1_.b, t2_.b], writes=[BP.b])
                p.op('act', lambda e: e.activation(Wout.t[:].rearrange("p x t r h -> p t x r h"), CwP.t[:, 1:LC + 1, :, :, :], AF.Identity), reads=[CwP.b], writes=[Wout.b])
                for d_ in range(2):
                    for ri in range(2):
                        for s_ in range(LC):
                            pp = pts[0]
                            tn_ += 1
                            p.op('pe', lambda e, pp=pp, s_=s_, ri=ri, d_=d_: e.transpose(pp.t[:, :], BP.t[:, s_, ri, d_, :, :].rearrange("p a b -> p (a b)"), identf.t[:, :]),
                                 reads=[BP.b, identf.b], writes=[pp.b])
                            p.op('act', lambda e, pp=pp, s_=s_, ri=ri, d_=d_: e.activation(Win.t[:, d_, ri, s_, :], pp.t[:, :], AF.Identity), reads=[pp.b], writes=[Win.b])
                    for thf in range(LC // 4):
                        for tq in range(4):
                            tau = thf * 4 + tq
                            for ql in range(4):
                                tix = d_ * 4 + ql
                                for ri in range(2):
                                    p.op('pe', lambda e, d_=d_, tau=tau, tq=tq, ql=ql, tix=tix, ri=ri, us=slice(d_ * 32 + Q * 4, d_ * 32 + Q * 4 + 4): e.matmul(
                                        pk.t[:, tq * 4 + ql, :], Bz.t[:, ri, us, :, :].rearrange("p q a b -> p (q a b)"), CwP.t[:, tau, tix, ri, :],
                                        start=(ri == 0), stop=(ri == 1)), reads=[Bz.b, CwP.b], writes=[pk.b])
                        for ql in range(4):
                            p.op('dve', lambda e, d_=d_, ql=ql, thf=thf: e.tensor_copy(Mw.t[32 * ql:32 * ql + 32, d_, thf * 4:thf * 4 + 4, :], pk.t[32 * ql:32 * ql + 32, ql:16:4, :]),
                                 reads=[pk.b], writes=[Mw.b])
                for d_ in range(2):
                    cols = slice(d_ * 32 + Q * 4, d_ * 32 + Q * 4 + 4)
                    p.op('dve', lambda e, d_=d_, cols=cols: e.tensor_copy(wc.t[:, d_ * 4:d_ * 4 + 4], PRM4.t[:, 1, cols]), reads=[PRM4.b], writes=[wc.b])
                    p.op('dve', lambda e, d_=d_, cols=cols: e.tensor_copy(ws.t[:, d_ * 4:d_ * 4 + 4], PRM4.t[:, 2, cols]), reads=[PRM4.b], writes=[ws.b])
                p.op('pool', lambda e: e.memset(Ec.t[:, :, 0:1], 1.0), writes=[Ec.b])
                p.op('pool', lambda e: e.memset(Es.t[:, :, 0:1], 0.0), writes=[Es.b])
                n = 1
                while n < NCM:
                    wcb = bcast_last(wc.t[:, :], n)
                    wsb = bcast_last(ws.t[:, :], n)
                    p.op('dve', lambda e, n=n, wcb=wcb: e.tensor_mul(et1.t[:, :, 0:n], Ec.t[:, :, 0:n], wcb), reads=[Ec.b, wc.b], writes=[et1.b])
                    p.op('pool', lambda e, n=n, wsb=wsb: e.tensor_mul(et2.t[:, :, 0:n], Es.t[:, :, 0:n], wsb), reads=[Es.b, ws.b], writes=[et2.b])
                    p.op('dve', lambda e, n=n: e.tensor_sub(Ec.t[:, :, n:2 * n], et1.t[:, :, 0:n], et2.t[:, :, 0:n]), reads=[et1.b, et2.b], writes=[Ec.b])
                    p.op('dve', lambda e, n=n, wcb=wcb: e.tensor_mul(et1.t[:, :, 0:n], Es.t[:, :, 0:n], wcb), reads=[Es.b, wc.b], writes=[et1.b])
                    p.op('pool', lambda e, n=n, wsb=wsb: e.tensor_mul(et2.t[:, :, 0:n], Ec.t[:, :, 0:n], wsb), reads=[Ec.b, ws.b], writes=[et2.b])
                    p.op('dve', lambda e, n=n: e.tensor_add(Es.t[:, :, n:2 * n], et1.t[:, :, 0:n], et2.t[:, :, 0:n]), reads=[et1.b, et2.b], writes=[Es.b])
                    p.op('dve', lambda e: e.tensor_mul(wt1.t[:], wc.t[:], wc.t[:]), reads=[wc.b], writes=[wt1.b])
                    p.op('dve', lambda e: e.tensor_mul(wt2.t[:], ws.t[:], ws.t[:]), reads=[ws.b], writes=[wt2.b])
                    p.op('dve', lambda e: e.scalar_tensor_tensor(ws.t[:], wc.t[:], 2.0, ws.t[:], ALU.mult, ALU.mult), reads=[wc.b, ws.b], writes=[ws.b])
                    p.op('dve', lambda e: e.tensor_sub(wc.t[:], wt1.t[:], wt2.t[:]), reads=[wt1.b, wt2.b], writes=[wc.b])
                    n *= 2
                p.op('dve', lambda e: e.tensor_copy(Esg.t[:, :, 0, :], Es.t[:, :, :]), reads=[Es.b], writes=[Esg.b])
                p.op('dve', lambda e: e.tensor_scalar(Esg.t[:, :, 1, :], Es.t[:, :, :], -1.0, None, ALU.mult), reads=[Es.b], writes=[Esg.b])
                blist = []
                for d_ in range(2):
                    if d_ == 0:
                        blocks = [(T, TC, False)] + [(b * 512, 512, True) for b in range(4)]
                    else:
                        blocks = [(T, TC, False)] + [(b * 512, 512, False) for b in (7, 6, 5, 4)] + [(b * 512, 512, True) for b in (3, 2, 1, 0)]
                    for bi_, (c0, n, islat) in enumerate(blocks):
                        blist.append((d_, bi_, c0, n, islat))
                bctx = {}

                def front_pe(k, Q=Q):
                    d_, bi_, c0, n, islat = blist[k]
                    gk = Q * 18 + k
                    ncn = n // LC
                    sets = [(gk % 2) * 4 + ql for ql in range(4)]
                    rhs_all = []
                    for ql in range(4):
                        base = nT.t[32 * ql:32 * ql + 32, Q, c0:c0 + n]
                        apl = [list(a_) for a_ in base.ap]
                        lst = []
                        for s_ in range(LC):
                            ap2 = [list(a_) for a_ in apl]
                            if d_ == 0:
                                ap2[-1] = [apl[-1][0] * LC, ncn]
                                lst.append(bass.AP(base.tensor, base.offset + s_ * apl[-1][0], ap2))
                            else:
                                ap2[-1] = [-apl[-1][0] * LC, ncn]
                                lst.append(bass.AP(base.tensor, base.offset + (n - 1 - s_) * apl[-1][0], ap2))
                        rhs_all.append(lst)
                    pss = [pS[ql] for ql in range(4)]
                    for ri in range(2):
                        for s_ in range(LC):
                            for ql in range(4):
                                ps_ = pss[ql]
                                p.op('pe', lambda e, ps_=ps_, ri=ri, s_=s_, ql=ql, d_=d_, ncn=ncn, r_=rhs_all[ql][s_]: e.matmul(
                                    ps_.t[:, ri * 128:ri * 128 + ncn], Win.t[32 * ql:32 * ql + 32, d_, ri, s_, :], r_, start=(s_ == 0), stop=(s_ == LC - 1), tile_position=(32 * ql, 0)),
                                    reads=[Win.b, nT.b], writes=[ps_.b])
                    bctx[k] = (ncn, sets, rhs_all)

                def front_ev(k, Q=Q):
                    d_, bi_, c0, n, islat = blist[k]
                    ncn, sets, rhs_all = bctx[k]
                    pss = [pS[ql] for ql in range(4)]
                    for ql in range(4):
                        ps_ = pss[ql]
                        X, P1, P2 = Wk_[sets[ql]]
                        p.op('act', lambda e, X=X, ps_=ps_, ncn=ncn: e.activation(X.t[:, :, 0:ncn], ps_.t[:, 0:256].rearrange("p (r c) -> p r c", r=2)[:, :, 0:ncn], AF.Identity),
                             reads=[ps_.b], writes=[X.b])
                    for ql in range(4):
                        X, P1, P2 = Wk_[sets[ql]]
                        tix = d_ * 4 + ql
                        ecb = bc2(Ec.t[:, tix, 0:ncn])
                        p.op('dve', lambda e, X=X, P1=P1, ecb=ecb, ncn=ncn: e.tensor_mul(P1.t[:, :, 0:ncn], X.t[:, :, 0:ncn], ecb), reads=[X.b, Ec.b], writes=[P1.b])
                        p.op('pool', lambda e, X=X, P2=P2, tix=tix, ncn=ncn: e.tensor_mul(P2.t[:, :, 0:ncn], swap2(X, ncn), Esg.t[:, tix, :, 0:ncn]), reads=[X.b, Esg.b], writes=[P2.b])
                    for ql in range(4):
                        X, P1, P2 = Wk_[sets[ql]]
                        p.op('dve', lambda e, P1=P1, P2=P2, ncn=ncn: e.tensor_add(P1.t[:, :, 0:ncn], P1.t[:, :, 0:ncn], P2.t[:, :, 0:ncn]), reads=[P1.b, P2.b], writes=[P1.b])

                def back(k, Q=Q):
                    d_, bi_, c0, n, islat = blist[k]
                    gk = Q * 18 + k
                    ncn, sets, rhs_all = bctx.pop(k)
                    pyt = py[gk % 2]
                    for ql in range(4):
                        tix = d_ * 4 + ql
                        u = d_ * 32 + Q * 4 + ql
                        hb_ = hpb[tix]
                        hbs = Hb[sets[ql]]
                        if bi_ > 0:
                            p.op('act', lambda e, tix=tix, u=u: e.activation(itmp.t[:, tix, 0:1], hp.t[:, tix, 1:2], AF.Identity, scale=nsth.t[:, u:u + 1]), reads=[hb_, nsth.b], writes=[hb_])
                            p.op('act', lambda e, tix=tix, u=u: e.activation(ini.t[:, tix, 0:1], hp.t[:, tix, 0:1], AF.Identity, scale=PRM4.t[:, 1, u:u + 1], bias=itmp.t[:, tix, 0:1]),
                                 reads=[hb_, PRM4.b], writes=[hb_])
                            p.op('act', lambda e, tix=tix, u=u: e.activation(itmp.t[:, tix, 1:2], hp.t[:, tix, 0:1], AF.Identity, scale=PRM4.t[:, 2, u:u + 1]), reads=[hb_, PRM4.b], writes=[hb_])
                            p.op('act', lambda e, tix=tix, u=u: e.activation(ini.t[:, tix, 1:2], hp.t[:, tix, 1:2], AF.Identity, scale=PRM4.t[:, 1, u:u + 1], bias=itmp.t[:, tix, 1:2]),
                                 reads=[hb_, PRM4.b], writes=[hb_])
                            if islat:
                                p.op('act', lambda e, hbs=hbs, tix=tix: e.activation(hbs.t[:, :, 0], hp.t[:, tix, :], AF.Identity), reads=[hb_], writes=[hbs.b])
                    for ql in range(4):
                        X, P1, P2 = Wk_[sets[ql]]
                        tix = d_ * 4 + ql
                        u = d_ * 32 + Q * 4 + ql
                        hb_ = hpb[tix]
                        rb = PRM4.t[:, 0, u:u + 1].to_broadcast([128, ncn])
                        for ri in range(2):
                            if bi_ == 0:
                                init_, irds = 0.0, []
                            else:
                                init_, irds = ini.t[:, tix, ri:ri + 1], [hb_]
                            p.op('dve', lambda e, X=X, P1=P1, rb=rb, init_=init_, ri=ri, ncn=ncn: e.tensor_tensor_scan(X.t[:, ri, 0:ncn], rb, P1.t[:, ri, 0:ncn], init_, ALU.mult, ALU.add),
                                 reads=[P1.b, PRM4.b] + irds, writes=[X.b])
                    for ql in range(4):
                        X, P1, P2 = Wk_[sets[ql]]
                        tix = d_ * 4 + ql
                        ecb = bc2(Ec.t[:, tix, 0:ncn])
                        p.op('dve', lambda e, X=X, P1=P1, ecb=ecb, ncn=ncn: e.tensor_mul(P1.t[:, :, 0:ncn], X.t[:, :, 0:ncn], ecb), reads=[X.b, Ec.b], writes=[P1.b])
                        p.op('pool', lambda e, X=X, P2=P2, tix=tix, ncn=ncn: e.tensor_mul(P2.t[:, :, 0:ncn], swap2(X, ncn), Esg.t[:, tix, :, 0:ncn]), reads=[X.b, Esg.b], writes=[P2.b])
                    for ql in range(4):
                        X, P1, P2 = Wk_[sets[ql]]
                        p.op('dve', lambda e, P1=P1, P2=P2, ncn=ncn: e.tensor_sub(P1.t[:, :, 0:ncn], P1.t[:, :, 0:ncn], P2.t[:, :, 0:ncn]), reads=[P1.b, P2.b], writes=[P1.b])
                    for ql in range(4):
                        X, P1, P2 = Wk_[sets[ql]]
                        tix = d_ * 4 + ql
                        hb_ = hpb[tix]
                        hbs = Hb[sets[ql]]
                        p.op('act', lambda e, tix=tix, P1=P1, ncn=ncn: e.activation(hp.t[:, tix, :], P1.t[:, :, ncn - 1], AF.Identity), reads=[P1.b], writes=[hb_])
                        if islat:
                            p.op('act', lambda e, hbs=hbs, P1=P1, ncn=ncn: e.activation(hbs.t[:, :, 1:ncn], P1.t[:, :, 0:ncn - 1], AF.Identity), reads=[P1.b], writes=[hbs.b])
                    if islat:
                        for t_ in range(LC):
                            for s_ in range(t_ + 1):
                                for ql in range(4):
                                    p.op('pe', lambda e, pyt=pyt, ql=ql, t_=t_, s_=s_, d_=d_, ncn=ncn, r_=rhs_all[ql][s_]: e.matmul(
                                        pyt.t[32 * ql:32 * ql + 32, t_, 0:ncn], Mw.t[32 * ql:32 * ql + 32, d_, t_ - s_, :], r_, start=(s_ == 0 and t_ == 0), stop=False,
                                        tile_position=(32 * ql, 32 * ql)), reads=[Mw.b, nT.b], writes=[pyt.b])
                        for t_ in range(LC):
                            for ri in range(2):
                                for ql in range(4):
                                    tix = d_ * 4 + ql
                                    hh_ = Hb[sets[ql]]
                                    p.op('pe', lambda e, pyt=pyt, ql=ql, t_=t_, ri=ri, tix=tix, hh_=hh_, ncn=ncn: e.matmul(
                                        pyt.t[32 * ql:32 * ql + 32, t_, 0:ncn], Wout.t[:, tix, t_, ri, :], hh_.t[:, ri, 0:ncn], start=False, stop=(ri == 1 and t_ == LC - 1),
                                        tile_position=(0, 32 * ql)), reads=[Wout.b, hh_.b], writes=[pyt.b])

                def yevac(k, Q=Q):
                    d_, bi_, c0, n, islat = blist[k]
                    if not islat:
                        return
                    gk = Q * 18 + k
                    ncn = n // LC
                    pyt = py[gk % 2]
                    if d_ == 0:
                        yv = yacc.t[:, c0:c0 + n].rearrange("p (c t) -> p t c", t=LC)
                        nv = nT.t[:, Q, c0:c0 + n].rearrange("p (c t) -> p t c", t=LC)
                        p.op('dve', lambda e, pyt=pyt, yv=yv, nv=nv, Q=Q: e.scalar_tensor_tensor(yv, nv, dT.t[:, Q:Q + 1], pyt.t[:, :, :], ALU.mult, ALU.add),
                             reads=[nT.b, dT.b, pyt.b], writes=[yacc.b])
                    else:
                        base = yacc.t[:, c0:c0 + n]
                        apl = [list(a_) for a_ in base.ap]
                        stp = apl[-1][0]
                        yv = bass.AP(base.tensor, base.offset + (n - 1) * stp, apl[:-1] + [[-stp, LC], [-stp * LC, ncn]])
                        p.op('dve', lambda e, pyt=pyt, yv=yv: e.tensor_add(yv, yv, pyt.t[:, :, :]), reads=[pyt.b, yacc.b], writes=[yacc.b])

                NB_ = len(blist)
                front_pe(0)
                front_ev(0)
                if NB_ > 1:
                    front_pe(1)
                for k in range(NB_):
                    if k + 1 < NB_:
                        front_ev(k + 1)
                    if k + 2 < NB_:
                        front_pe(k + 2)
                    back(k)
                    if k >= 1:
                        yevac(k - 1)
                yevac(NB_ - 1)
                for b in range(4):
                    yg = ygb[b % 2]
                    gt = gtb[b % 2]
                    ysl = yacc.t[:, b * 512:(b + 1) * 512]
                    p.op('dve', lambda e, gt=gt, ysl=ysl: e.tensor_mul(gt.t[:, :], ysl, ysl), reads=[yacc.b], writes=[gt.b])
                    p.op('dve', lambda e, gt=gt: e.tensor_scalar(gt.t[:, :], gt.t[:, :], 0.044715, 1.0, ALU.mult, ALU.add), reads=[gt.b], writes=[gt.b])
                    p.op('dve', lambda e, gt=gt, ysl=ysl: e.tensor_mul(gt.t[:, :], gt.t[:, :], ysl), reads=[gt.b, yacc.b], writes=[gt.b])
                    p.op('act', lambda e, gt=gt: e.activation(gt.t[:, :], gt.t[:, :], AF.Sigmoid, scale=2.0 * math.sqrt(2.0 / math.pi)), reads=[gt.b], writes=[gt.b])
                    p.op('dve', lambda e, gt=gt, ysl=ysl, yg=yg: e.tensor_mul(yg.t[:, :], gt.t[:, :], ysl), reads=[gt.b, yacc.b], writes=[yg.b])
                    p.dma(YGs[Q, :, b * 512:(b + 1) * 512], yg.t[:, :], reads=[yg.b])
            p.barrier()
    with ExitStack() as st:
        Wgl = sbt(st, "s5Wgl", [128, 8, 2 * D], BF16)
        with ExitStack() as st2:
            stg = [sbt(st2, f"s5gst{i}", [128, 2 * D], F32) for i in range(2)]
            for k in range(8):
                s_ = stg[k % 2]
                p.dma(s_.t[:, 0:D], I["ssm_glu_w"][k * 128:(k + 1) * 128, 0:D], writes=[s_.b])
                p.dma(s_.t[:, D:2 * D], I["ssm_glu_w"][k * 128:(k + 1) * 128, D:2 * D], writes=[s_.b])
                ew(lambda e, k=k, s_=s_: e.tensor_copy(Wgl.t[:, k, 0:D], s_.t[:, 0:D]), [s_.b], [Wgl.b])
                ew(lambda e, k=k, s_=s_: e.tensor_copy(Wgl.t[:, k, D:2 * D], s_.t[:, D:2 * D]), [s_.b], [Wgl.b])
            p.barrier()
        g1 = load_gate(st, "s5g1", 1, 0, 0)
        ygt = [sbt(st, f"s5ygt{i}", [128, 8, 128], BF16) for i in range(2)]
        h2 = [sbt(st, f"s5h2{i}", [128, D], F32) for i in range(2)]
        h3 = [sbt(st, f"s5h3{i}", [128, D], F32) for i in range(2)]
        sg_ = [sbt(st, f"s5sg{i}", [128, 512], F32) for i in range(2)]
        pa = [pst(st, f"s5pa{i}", [128, 512]) for i in range(2)]
        pgl = [pst(st, f"s5pg{i}", [128, 512]) for i in range(2)]
        c_ = 0
        YGs4 = YGs.rearrange("q p (r t) -> q p r t", r=2)
        for ti in range(16):
            yt = ygt[ti % 2]
            p.dma(yt.t[:], YGs[:, :, ti * 128:(ti + 1) * 128].rearrange("q p t -> p q t"), writes=[yt.b])
            hh = h2[ti % 2]
            ho_ = h3[ti % 2]
            p.dma(hh.t[:], Hloc[ti * 128:(ti + 1) * 128, :], writes=[hh.b])
            for half in range(2):
                a = pa[c_ % 2]
                g = pgl[c_ % 2]
                sg2 = sg_[c_ % 2]
                c_ += 1
                for k in range(8):
                    p.op('pe', lambda e, a=a, k=k, yt=yt, half=half: e.matmul(a.t[:, :], yt.t[:, k, :], Wgl.t[:, k, half * 512:(half + 1) * 512], start=(k == 0), stop=(k == 7)),
                         reads=[yt.b, Wgl.b], writes=[a.b])
                for k in range(8):
                    p.op('pe', lambda e, g=g, k=k, yt=yt, half=half: e.matmul(g.t[:, :], yt.t[:, k, :], Wgl.t[:, k, D + half * 512:D + (half + 1) * 512], start=(k == 0), stop=(k == 7)),
                         reads=[yt.b, Wgl.b], writes=[g.b])
                p.op('act', lambda e, g=g, sg2=sg2: e.activation(sg2.t[:, :], g.t[:, :], AF.Sigmoid), reads=[g.b], writes=[sg2.b])
                p.op('dve', lambda e, a=a, sg2=sg2: e.tensor_mul(sg2.t[:, :], sg2.t[:, :], a.t[:, :]), reads=[a.b, sg2.b], writes=[sg2.b])
                p.op('pool', lambda e, sg2=sg2, half=half: e.tensor_mul(sg2.t[:, :], sg2.t[:, :], g1.t[:, half * 512:(half + 1) * 512]), reads=[sg2.b, g1.b], writes=[sg2.b])
                p.op('pool', lambda e, sg2=sg2, hh=hh, ho_=ho_, half=half: e.tensor_add(ho_.t[:, half * 512:(half + 1) * 512], hh.t[:, half * 512:(half + 1) * 512], sg2.t[:, :]),
                     reads=[sg2.b, hh.b], writes=[ho_.b])
            p.dma(Hloc[ti * 128:(ti + 1) * 128, :], ho_.t[:], reads=[ho_.b])
        p.barrier()


_CACHE = {}


def kernel(**inputs):
    stage = int(inputs.pop("_stage", 99))
    if "nc" not in _CACHE or _CACHE.get("stage") != stage:
        _CACHE["nc"] = build(stage)
        _CACHE["stage"] = stage
        _CACHE["consts"] = host_consts()
    nc = _CACHE["nc"]
    if "consts_rev" not in _CACHE:
        _CACHE["consts_rev"] = host_consts(rev=True)
    f = lambda a: np.ascontiguousarray(np.asarray(a, dtype=np.float32))
    in_maps = []
    for core in range(8):
        b = core // 2
        rv = (core % 2 == 1) and stage >= 99
        cs = _CACHE["consts_rev"] if rv else _CACHE["consts"]
        sd = (lambda a: np.asarray(a)[0][::-1]) if rv else (lambda a: np.asarray(a)[0])
        xb = np.asarray(inputs["x"][b])
        cb = np.asarray(inputs["ctx"][b])
        if rv:
            xb = xb[::-1]
            cb = cb[::-1]
        m = {
            "x": f(xb), "c": f(inputs["c"][b]).reshape(8, 128), "ctx": f(cb),
            "c_ctx": f(inputs["c_ctx"]).reshape(8, 128),
            "mod_w": f(inputs["mod_w"]), "mod_b": f(inputs["mod_b"]).reshape(2, 48, 128),
            "norm_g": f(inputs["norm_g"]).reshape(2, 2, 8, 128),
            "ffn_w_gate": f(inputs["ffn_w_gate"]), "ffn_w_up": f(inputs["ffn_w_up"]), "ffn_w_down": f(inputs["ffn_w_down"]),
            "mix_w_in": f(inputs["mix_w_in"][0]), "mix_w_out": f(inputs["mix_w_out"][0]), "attn_sink": f(inputs["attn_sink"]).reshape(1, 8),
            "ssm_a_re": f(sd(inputs["ssm_a_re"])).reshape(128, 64), "ssm_a_im": f(sd(inputs["ssm_a_im"])).reshape(128, 64),
            "ssm_log_dt": f(sd(inputs["ssm_log_dt"])).reshape(1, 128),
            "ssm_b_re": f(sd(inputs["ssm_b_re"])), "ssm_b_im": f(sd(inputs["ssm_b_im"])),
            "ssm_c_re": f(sd(inputs["ssm_c_re"])), "ssm_c_im": f(sd(inputs["ssm_c_im"])),
            "ssm_d": f(inputs["ssm_d"][0]).reshape(8, 128), "ssm_glu_w": f(inputs["ssm_glu_w"][0]), "final_g": f(inputs["final_g"]).reshape(1, D),
        }
        m.update(cs)
        m["rk"] = np.array([[0]], np.int32)
        in_maps.append(m)
    res = run_bass_kernel_spmd(nc, in_maps, core_ids=list(range(8)))
    if stage < 99:
        return np.stack([np.asarray(res.results[2 * b]["out"], dtype=np.float32) for b in range(4)], axis=0)
    outp = np.stack([np.concatenate([np.asarray(res.results[2 * b]["out"], dtype=np.float32),
                                     np.asarray(res.results[2 * b + 1]["out"], dtype=np.float32)[::-1]], axis=0) for b in range(4)], axis=0)
    return outp
```

```python
import math
import numpy as np
import ml_dtypes
import concourse.bass as bass
import concourse.mybir as mybir
from concourse.bass_utils import run_bass_kernel_spmd
from contextlib import ExitStack

F32 = mybir.dt.float32
BF16 = mybir.dt.bfloat16
AF = mybir.ActivationFunctionType
ALU = mybir.AluOpType
NPBF = ml_dtypes.bfloat16

D = 1024
T = 4096
TC = 256
FF = 2816
NFC = 22
EPS = 1e-6


class Buf:
    def __init__(self, name=""):
        self.name = name
        self.last_w = None
        self.readers = []


class Prog:
    ENG = ['pe', 'act', 'dve', 'pool', 'sp']

    def __init__(self, nc, ndma_sems=10):
        self.nc = nc
        self.ops = {e: [] for e in self.ENG}
        self.cnt = {e: 0 for e in self.ENG}
        self.known = {e: {} for e in self.ENG}
        self.es = ExitStack()
        self.sem = {e: self.es.enter_context(nc.semaphore('s_' + e)) for e in self.ENG}
        self.dma_sems = {e: [self.es.enter_context(nc.semaphore(f'd_{e}{i}')) for i in range(ndma_sems)]
                         for e in ['sp', 'act', 'pool']}
        self.dma_val = {e: [0] * ndma_sems for e in self.dma_sems}
        self.dma_rr = {e: 0 for e in self.dma_sems}
        self.semobj = {}
        for e in self.ENG:
            self.semobj[('c', e)] = self.sem[e]
        for e in self.dma_sems:
            for i, s in enumerate(self.dma_sems[e]):
                self.semobj[('d', e, i)] = s
        self.q = 0
        self.rank_ap = None

    def _waits(self, eng, toks):
        need = {}
        for t in toks:
            if t is None:
                continue
            k, v = t
            if k == ('c', eng) and eng == 'pe':
                continue
            if self.known[eng].get(k, 0) >= v:
                continue
            if need.get(k, 0) < v:
                need[k] = v
        for k, v in need.items():
            self.known[eng][k] = v
        return list(need.items())

    def _deps(self, reads, writes):
        toks = []
        for b in reads:
            toks.append(b.last_w)
        for b in writes:
            toks.append(b.last_w)
            toks.extend(b.readers)
        return toks

    def _commit(self, tok, reads, writes):
        for b in reads:
            b.readers.append(tok)
            if len(b.readers) > 64:
                b.readers = b.readers[-64:]
        for b in writes:
            b.last_w = tok
            b.readers = []

    def op(self, eng, fn, reads=(), writes=()):
        waits = self._waits(eng, self._deps(reads, writes))
        self.cnt[eng] += 1
        tok = (('c', eng), self.cnt[eng])
        self.ops[eng].append((waits, fn, (self.sem[eng], 1)))
        self._commit(tok, reads, writes)
        return tok

    def dma(self, out, in_, reads=(), writes=(), eng=None, **kw):
        if eng is None:
            eng = ['sp', 'act', 'pool'][self.q % 2]
            self.q += 1
        toks = self._deps(reads, writes)
        i = self.dma_rr[eng]
        self.dma_rr[eng] = (i + 1) % len(self.dma_sems[eng])
        key = ('d', eng, i)
        prev = self.dma_val[eng][i]
        if prev > 0:
            toks.append((key, prev))
        waits = self._waits(eng, toks)
        self.dma_val[eng][i] = prev + 16
        tok = (key, prev + 16)
        def issue(e, out=out, in_=in_, eng=eng):
            o = out(self.dyn[eng]) if callable(out) else out
            i2 = in_(self.dyn[eng]) if callable(in_) else in_
            return e.dma_start(out=o, in_=i2, **kw)
        self.ops[eng].append((waits, issue, (self.dma_sems[eng][i], 16)))
        self._commit(tok, reads, writes)
        return tok

    def all_tokens(self):
        allt = []
        for e in self.ENG:
            if self.cnt[e]:
                allt.append((('c', e), self.cnt[e]))
        for e in self.dma_sems:
            for i, v in enumerate(self.dma_val[e]):
                if v:
                    allt.append((('d', e, i), v))
        return allt

    def barrier(self):
        allt = self.all_tokens()
        for e in self.ENG:
            w = self._waits(e, allt)
            if w:
                self.ops[e].append((w, None, None))

    def emit(self):
        nc = self.nc
        fin = self._waits('sp', self.all_tokens())
        self.ops['sp'].append((fin, None, None))
        self.dyn = {}
        with nc.Block() as block:
            def mk(eng):
                def run(e):
                    for waits, fn, inc in self.ops[eng]:
                        for k, v in waits:
                            e.wait_ge(self.semobj[k], v)
                        if fn is not None:
                            fn(e).then_inc(inc[0], inc[1])

                def body(e):
                    if eng in ('sp', 'act') and self.rank_ap is not None:
                        with e.register("rk_" + eng) as reg:
                            e.reg_load(reg, self.rank_ap)
                            self.dyn[eng] = e.snap(reg, min_val=0, max_val=2048)
                            run(e)
                    else:
                        run(e)
                return body
            block.tensor(mk('pe'))
            block.scalar(mk('act'))
            block.vector(mk('dve'))
            block.gpsimd(mk('pool'))
            block.sync(mk('sp'))
        self.es.close()


class TB:
    def __init__(self, t, name=""):
        self.t = t
        self.b = Buf(name)


def host_consts(rev=False):
    cs = {}
    cs["identf"] = np.eye(128, dtype=np.float32)
    cs["identb"] = np.eye(128, dtype=np.float32).astype(NPBF)
    t = np.arange(T)
    row = (t // 64).astype(np.float64)
    col = (t % 64).astype(np.float64)
    nf = 16
    inv = 10000.0 ** (-np.arange(nf, dtype=np.float64) / nf)
    inv = inv.astype(np.float32).astype(np.float64)
    ang = np.concatenate([(row[:, None].astype(np.float32) * inv[None].astype(np.float32)),
                          (col[:, None].astype(np.float32) * inv[None].astype(np.float32))], axis=-1).astype(np.float32)
    cosv = np.cos(ang).astype(np.float32)
    sinv = np.sin(ang).astype(np.float32)
    C = np.zeros((128, T), np.float32)
    S = np.zeros((128, T), np.float32)
    for p in range(128):
        d = p % 64
        i = d // 2
        C[p] = cosv[:, i]
        S[p] = sinv[:, i] * (-1.0 if d % 2 == 0 else 1.0)
    if rev:
        C = np.ascontiguousarray(C[:, ::-1])
        S = np.ascontiguousarray(S[:, ::-1])
    cs["ropeC"] = C
    cs["ropeS"] = S
    j = np.arange(128)[:, None]
    i = np.arange(128)[None, :]
    mp = np.where(j >= i, 0.0, -30000.0).astype(np.float32)
    mn = np.where(j <= i, 0.0, -30000.0).astype(np.float32)
    cs["maskP"] = np.tile(mp, (1, 4)).astype(NPBF)
    cs["maskN"] = np.tile(mn, (1, 4)).astype(NPBF)
    tt = np.arange(T, dtype=np.int64)
    tk = (tt[:, None] * tt[None, :]) % T
    angT = 2.0 * np.pi * tk / T
    ct = (np.cos(angT) / math.sqrt(T)).astype(np.float32)
    stt = (np.sin(angT) / math.sqrt(T)).astype(np.float32)
    if rev:
        ct = np.ascontiguousarray(ct[::-1, ::-1])
        stt = np.ascontiguousarray(stt[::-1, ::-1])
    cs["CT"] = np.ascontiguousarray(ct.reshape(32, 128, 16, 256).transpose(2, 1, 0, 3)).astype(NPBF)
    cs["ST"] = np.ascontiguousarray(stt.reshape(32, 128, 16, 256).transpose(2, 1, 0, 3)).astype(NPBF)
    t2 = np.arange(TC, dtype=np.int64)
    a2 = 2.0 * np.pi * ((t2[:, None] * t2[None, :]) % TC) / TC
    c2 = (np.cos(a2) / math.sqrt(TC)).astype(np.float32)
    s2 = (np.sin(a2) / math.sqrt(TC)).astype(np.float32)
    if rev:
        c2 = np.ascontiguousarray(c2[::-1, ::-1])
        s2 = np.ascontiguousarray(s2[::-1, ::-1])
    cs["C256"] = np.ascontiguousarray(c2.reshape(2, 128, 256).transpose(1, 0, 2)).astype(NPBF)
    cs["S256"] = np.ascontiguousarray(s2.reshape(2, 128, 256).transpose(1, 0, 2)).astype(NPBF)
    c64 = np.arange(64)
    a3 = 2.0 * np.pi * ((c64[:, None] * c64[None, :]) % 64) / 64
    cc = np.zeros((128, 128), np.float32)
    sc = np.zeros((128, 128), np.float32)
    for g in range(2):
        cc[g * 64:(g + 1) * 64, g * 64:(g + 1) * 64] = np.cos(a3) / 8.0
        sc[g * 64:(g + 1) * 64, g * 64:(g + 1) * 64] = np.sin(a3) / 8.0
    cs["Cc"] = cc.astype(NPBF)
    cs["Sc"] = sc.astype(NPBF)
    return cs


CONST_SHAPES = {
    "identf": ([128, 128], F32), "identb": ([128, 128], BF16), "ropeC": ([128, T], F32), "ropeS": ([128, T], F32),
    "maskP": ([128, 512], BF16), "maskN": ([128, 512], BF16),
    "CT": ([16, 128, 32, 256], BF16), "ST": ([16, 128, 32, 256], BF16),
    "C256": ([128, 2, 256], BF16), "S256": ([128, 2, 256], BF16), "Cc": ([128, 128], BF16), "Sc": ([128, 128], BF16),
}

IN_SHAPES = {
    "x": [T, D], "c": [8, 128], "ctx": [TC, D], "c_ctx": [8, 128],
    "mod_w": [2, D, 6 * D], "mod_b": [2, 48, 128], "norm_g": [2, 2, 8, 128],
    "ffn_w_gate": [2, D, FF], "ffn_w_up": [2, D, FF], "ffn_w_down": [2, FF, D],
    "mix_w_in": [D, 1280], "mix_w_out": [D, D], "attn_sink": [1, 8],
    "ssm_a_re": [128, 64], "ssm_a_im": [128, 64], "ssm_log_dt": [1, 128],
    "ssm_b_re": [2, 64, 64, 16], "ssm_b_im": [2, 64, 64, 16], "ssm_c_re": [2, 64, 16, 64], "ssm_c_im": [2, 64, 16, 64],
    "ssm_d": [8, 128], "ssm_glu_w": [D, 2 * D], "final_g": [1, D],
}


def build(stage=99):
    nc = bass.Bass("TRN2", target_bir_lowering=False)
    I = {n: nc.dram_tensor(n, sh, F32, kind="ExternalInput").ap() for n, sh in IN_SHAPES.items()}
    K = {n: nc.dram_tensor(n, sh, dt, kind="ExternalInput").ap() for n, (sh, dt) in CONST_SHAPES.items()}
    rk_in = nc.dram_tensor("rk", [1, 1], mybir.dt.int32, kind="ExternalInput").ap()
    HT = T // 2
    out = nc.dram_tensor("out", [T if stage < 99 else HT, D], F32, kind="ExternalOutput").ap()
    Hs = nc.dram_tensor("Hs", [T + TC, D], F32).ap()
    mod_b_flat = I["mod_b"].rearrange("l j p -> l (j p)")

    p = Prog(nc)
    p.rank_ap = None
    top = ExitStack()
    Hloc = nc.dram_tensor("Hloc", [T // 2, D], F32).ap()
    p.Hloc = Hloc

    uid = {"n": 0}

    def sbt(st, name, shape, dt):
        uid["n"] += 1
        return TB(st.enter_context(nc.sbuf_tensor(f"s{uid['n']}_{name}", shape, dt)), name)

    def pst(st, name, shape, dt=F32):
        uid["n"] += 1
        return TB(st.enter_context(nc.psum_tensor(f"p{uid['n']}_{name}", shape, dt)), name)

    identf = sbt(top, "identf", [128, 128], F32)
    identb = sbt(top, "identb", [128, 128], BF16)
    p.dma(identf.t[:], K["identf"], writes=[identf.b])
    p.dma(identb.t[:], K["identb"], writes=[identb.b])
    ones_b = sbt(top, "ones_b", [128, 128], BF16)
    p.op('pool', lambda e: e.memset(ones_b.t[:], 1.0), writes=[ones_b.b])
    AB = sbt(top, "AB", [128, 2, 2, 4, 8], F32)
    Gs = nc.dram_tensor("Gs", [8, 128, D], F32).ap()
    ATs = nc.dram_tensor("ATs", [34, 128, 4, 128], BF16).ap()

    def load_gate(st, nm, l, col, gi):
        g = sbt(st, nm, [128, D], F32)
        p.dma(g.t[:], Gs[(l * 2 + col) * 2 + gi], writes=[g.b])
        return g
    rr = {"i": 0}

    def ew(fn, reads, writes, engs=('dve', 'pool')):
        e = engs[rr["i"] % len(engs)]
        rr["i"] += 1
        return p.op(e, fn, reads=reads, writes=writes)

    with ExitStack() as st:
        gates = sbt(st, "gates", [128, 2, 2, 2, D], F32)
        crow = sbt(st, "crow", [16, 128], F32)
        p.dma(crow.t[0:8, :], I["c"], writes=[crow.b])
        p.dma(crow.t[8:16, :], I["c_ctx"], writes=[crow.b])
        pT = pst(st, "pT", [128, 96])
        scT = sbt(st, "scT", [128, 16], F32)
        p.op('pe', lambda e: e.transpose(pT.t[:, 0:16], crow.t[:, :], identf.t[0:16, 0:16]), reads=[crow.b, identf.b], writes=[pT.b])
        p.op('act', lambda e: e.activation(scT.t[:], pT.t[:, 0:16], AF.Silu), reads=[pT.b], writes=[scT.b])
        scbc = sbt(st, "scbc", [128, 16, 128], F32)
        for ck in range(16):
            ew(lambda e, ck=ck: e.tensor_copy(scbc.t[:, ck, :], scT.t[:, ck:ck + 1].to_broadcast([128, 128])), [scT.b], [scbc.b])
        mbrow = sbt(st, "mbrow", [48, 2, 128], F32)
        ngrow = sbt(st, "ngrow", [32, 128], F32)
        p.dma(mbrow.t[:, 0, :], I["mod_b"][0], writes=[mbrow.b])
        p.dma(mbrow.t[:, 1, :], I["mod_b"][1], writes=[mbrow.b])
        p.dma(ngrow.t[:], I["norm_g"].rearrange("l i k p -> (l i k) p"), writes=[ngrow.b])
        mbT = sbt(st, "mbT", [128, 2, 48], F32)
        ngT = sbt(st, "ngT", [128, 32], F32)
        for l in range(2):
            p.op('pe', lambda e, l=l: e.transpose(pT.t[:, 0:48], mbrow.t[:, l, :], identf.t[0:48, 0:48]), reads=[mbrow.b, identf.b], writes=[pT.b])
            p.op('dve', lambda e, l=l: e.tensor_copy(mbT.t[:, l, :], pT.t[:, 0:48]), reads=[pT.b], writes=[mbT.b])
        p.op('pe', lambda e: e.transpose(pT.t[:, 0:32], ngrow.t[:, :], identf.t[0:32, 0:32]), reads=[ngrow.b, identf.b], writes=[pT.b])
        p.op('dve', lambda e: e.tensor_copy(ngT.t[:], pT.t[:, 0:32]), reads=[pT.b], writes=[ngT.b])
        macc = sbt(st, "macc", [128, 2, 48, 2], F32)
        Wk = [sbt(st, f"Wk{i}", [128, 6 * D], F32) for i in range(2)]
        pg = [pst(st, f"pg{i}", [128, 512]) for i in range(2)]
        pm = pst(st, "pm", [128, 96])
        it = 0
        for l in range(2):
            for k in range(8):
                w = Wk[it % 2]
                it += 1
                for q3 in range(3):
                    p.dma(w.t[:, q3 * 2048:(q3 + 1) * 2048], I["mod_w"][l, k * 128:(k + 1) * 128, q3 * 2048:(q3 + 1) * 2048], writes=[w.b])
                for j in range(48):
                    p.op('pe', lambda e, j=j, w=w, k=k: e.matmul(pm.t[:, 2 * j:2 * j + 2], w.t[:, j * 128:(j + 1) * 128],
                                                                 scT.t[:, k:16:8], start=True, stop=True),
                         reads=[w.b, scT.b], writes=[pm.b])
                if k == 0:
                    p.op('dve', lambda e, l=l: e.tensor_copy(macc.t[:, l].rearrange("p j c -> p (j c)"), pm.t[:, :]), reads=[pm.b], writes=[macc.b])
                else:
                    p.op('dve', lambda e, l=l: e.tensor_add(macc.t[:, l].rearrange("p j c -> p (j c)"), macc.t[:, l].rearrange("p j c -> p (j c)"), pm.t[:, :]),
                         reads=[pm.b, macc.b], writes=[macc.b])
                gi = 0
                for col in range(2):
                    for g_i, which in enumerate((2, 5)):
                        for half in range(2):
                            pgt = pg[gi % 2]
                            gi += 1
                            p.op('pe', lambda e, pgt=pgt, col=col, k=k, w=w, which=which, half=half: e.matmul(
                                pgt.t[:, :], scbc.t[:, col * 8 + k, :], w.t[:, which * D + half * 512: which * D + half * 512 + 512], start=True, stop=True),
                                reads=[scbc.b, w.b], writes=[pgt.b])
                            dst = gates.t[:, l, col, g_i, half * 512:(half + 1) * 512]
                            if k == 0:
                                p.op('dve', lambda e, dst=dst, pgt=pgt: e.tensor_copy(dst, pgt.t[:, :]), reads=[pgt.b], writes=[gates.b])
                            else:
                                p.op('dve', lambda e, dst=dst, pgt=pgt: e.tensor_add(dst, dst, pgt.t[:, :]), reads=[pgt.b, gates.b], writes=[gates.b])
        gb = sbt(st, "gb", [128, D], F32)
        for l in range(2):
            for col in range(2):
                p.op('dve', lambda e, l=l, col=col: e.tensor_add(macc.t[:, l, :, col], macc.t[:, l, :, col], mbT.t[:, l, :]), reads=[macc.b, mbT.b], writes=[macc.b])
            for g_i, which in enumerate((2, 5)):
                p.dma(gb.t[:], mod_b_flat[l:l + 1, which * D:(which + 1) * D].partition_broadcast(128), writes=[gb.b])
                for col in range(2):
                    p.op('dve', lambda e, l=l, col=col, g_i=g_i: e.tensor_add(gates.t[:, l, col, g_i, :], gates.t[:, l, col, g_i, :], gb.t[:]),
                         reads=[gb.b, gates.b], writes=[gates.b])
            for col in range(2):
                for i2 in range(2):
                    sh = macc.t[:, l, (3 * i2) * 8:(3 * i2) * 8 + 8, col]
                    scl = macc.t[:, l, (3 * i2 + 1) * 8:(3 * i2 + 1) * 8 + 8, col]
                    gn = ngT.t[:, (l * 2 + i2) * 8:(l * 2 + i2) * 8 + 8]
                    p.op('dve', lambda e, l=l, col=col, i2=i2, scl=scl, gn=gn: e.scalar_tensor_tensor(
                        AB.t[:, l, col, 2 * i2, :], scl, 1.0, gn, ALU.add, ALU.mult), reads=[macc.b, ngT.b], writes=[AB.b])
                    p.op('dve', lambda e, l=l, col=col, i2=i2, sh=sh: e.tensor_copy(AB.t[:, l, col, 2 * i2 + 1, :], sh), reads=[macc.b], writes=[AB.b])
        for l in range(2):
            for col in range(2):
                for g_i in range(2):
                    p.dma(Gs[(l * 2 + col) * 2 + g_i], gates.t[:, l, col, g_i, :], reads=[gates.b])
        p.barrier()

    if stage == 0:
        with ExitStack() as st:
            g0 = load_gate(st, "g0dbg", 0, 0, 0)
            p.dma(out[0:128, :], g0.t[:], reads=[g0.b])
        p.dma(out[128:256, 0:128], AB.t[:].rearrange("p a b c d -> p (a b c d)"), reads=[AB.b])
        p.emit()
        top.close()
        return nc

    def make_norm(st, nm, nxt=2):
        ctxn = {}
        ctxn["xt"] = [sbt(st, f"{nm}xt{i}", [128, D], F32) for i in range(nxt)]
        ctxn["junk"] = sbt(st, f"{nm}junk", [128, D], BF16)
        ctxn["xn"] = [sbt(st, f"{nm}xn{i}", [128, D], BF16) for i in range(2)]
        ctxn["ss"] = [sbt(st, f"{nm}ss{i}", [128, 1], F32) for i in range(3)]
        ctxn["tp"] = [pst(st, f"{nm}tp{i}", [128, 8, 128], BF16) for i in range(2)]
        ctxn["n"] = 0
        return ctxn

    def norm_tile(cn, src, l, col, which, dst_fn, dst_buf, xt_fixed=None, ident=None):
        n = cn["n"]
        cn["n"] += 1
        xt = xt_fixed if xt_fixed is not None else cn["xt"][n % len(cn["xt"])]
        ss = cn["ss"][n % 3]
        xn = cn["xn"][n % 2]
        tp = cn["tp"][n % 2]
        junk = cn["junk"]
        idt = ident if ident is not None else identb
        p.dma(xt.t[:], src, writes=[xt.b])
        p.op('act', lambda e: e.activation(junk.t[:], xt.t[:], AF.Square, accum_out=ss.t[:]), reads=[xt.b], writes=[junk.b, ss.b])
        p.op('dve', lambda e: e.tensor_scalar(ss.t[:], ss.t[:], 1.0 / D, EPS, ALU.mult, ALU.add), reads=[ss.b], writes=[ss.b])
        p.op('act', lambda e: e.activation(ss.t[:], ss.t[:], AF.Sqrt), reads=[ss.b], writes=[ss.b])
        p.op('dve', lambda e: e.reciprocal(ss.t[:], ss.t[:]), reads=[ss.b], writes=[ss.b])
        p.op('dve', lambda e: e.tensor_scalar(xn.t[:], xt.t[:], ss.t[:, 0:1], None, ALU.mult), reads=[xt.b, ss.b], writes=[xn.b])
        for k in range(8):
            p.op('pe', lambda e, k=k: e.transpose(tp.t[:, k, :], xn.t[:, k * 128:(k + 1) * 128], idt.t[:]), reads=[xn.b, idt.b], writes=[tp.b])
        for k in range(8):
            p.op('act', lambda e, k=k: e.activation(dst_fn(k), tp.t[:, k, :], AF.Identity,
                                                    scale=AB.t[:, l, col, 2 * which, k:k + 1], bias=AB.t[:, l, col, 2 * which + 1, k:k + 1]),
                 reads=[tp.b, AB.b], writes=[dst_buf])
        return xt, ss

    def tile_src(ti):
        return I["x"][ti * 128:(ti + 1) * 128, :] if ti < 32 else I["ctx"][(ti - 32) * 128:(ti - 31) * 128, :]

    NT = 34
    L0 = ExitStack()
    Fm = sbt(L0, "Fm", [128, NT, 512], BF16)
    with ExitStack() as stBC:
        QT = sbt(stBC, "QT", [128, 4, NT * 128], BF16)
        KT = sbt(stBC, "KT", [128, NT * 128], BF16)
        Vm = sbt(stBC, "Vm", [128, NT, 128], BF16)
        with ExitStack() as st:
            wb = sbt(st, "wb", [128, 8, 1920], BF16)
            wst = [sbt(st, f"wst{i}", [128, 1280], F32) for i in range(1)]
            for k in range(8):
                s_ = wst[0]
                p.dma(s_.t[:], I["mix_w_in"][k * 128:(k + 1) * 128, :], writes=[s_.b])
                S = s_.t
                W = wb.t
                ew(lambda e, k=k, S=S, W=W: e.tensor_copy(W[:, k, 0:512], S[:, 0:512]), [s_.b], [wb.b])
                ew(lambda e, k=k, S=S, W=W: e.tensor_copy(W[:, k, 512:640], S[:, 1152:1280]), [s_.b], [wb.b])
                ew(lambda e, k=k, S=S, W=W: e.tensor_copy(W[:, k, 640:1152].rearrange("p (j h d) -> p j h d", j=4, h=2),
                                                          S[:, 512:1024].rearrange("p (h j d) -> p j h d", h=2, j=4)), [s_.b], [wb.b])
                ew(lambda e, k=k, S=S, W=W: e.tensor_copy(W[:, k, 1152:1280], S[:, 1024:1152]), [s_.b], [wb.b])
                for two in range(2):
                    ew(lambda e, k=k, S=S, W=W, two=two: e.tensor_copy(
                        W[:, k, 1280:1792].rearrange("p (j h i t) -> p j h i t", j=4, h=2, t=2)[:, :, :, :, two],
                        S[:, 512:1024].rearrange("p (h j i t) -> p j h i t", h=2, j=4, t=2)[:, :, :, :, 1 - two]), [s_.b], [wb.b])
                    ew(lambda e, k=k, S=S, W=W, two=two: e.tensor_copy(
                        W[:, k, 1792:1920].rearrange("p (i t) -> p i t", t=2)[:, :, two],
                        S[:, 1024:1152].rearrange("p (i t) -> p i t", t=2)[:, :, 1 - two]), [s_.b], [wb.b])
            ropeCb = [sbt(st, f"ropeC{i}", [128, 512], F32) for i in range(2)]
            ropeSb = [sbt(st, f"ropeS{i}", [128, 512], F32) for i in range(2)]
            cn = make_norm(st, "b")
            nTb = [sbt(st, f"nTb{i}", [128, 8, 512], BF16) for i in range(2)]
            pf = pst(st, "pf", [128, 512])
            pv = pst(st, "pv", [128, 128])
            pq = [pst(st, f"pq{i}", [128, 512]) for i in range(2)]
            pqp = [pst(st, f"pqp{i}", [128, 512]) for i in range(2)]
            t1 = [sbt(st, f"t1{i}", [128, 512], F32) for i in range(2)]
            t2 = [sbt(st, f"t2{i}", [128, 512], F32) for i in range(2)]
            qi_ = 0
            for blk in range(9):
                ntile = 4 if blk < 8 else 2
                ncol = ntile * 128
                nT = nTb[blk % 2]
                col = 0 if blk < 8 else 1
                for tt in range(ntile):
                    ti = blk * 4 + tt
                    norm_tile(cn, tile_src(ti), 0, col, 0, lambda k, tt=tt, nT=nT: nT.t[:, k, tt * 128:(tt + 1) * 128], nT.b)
                    for k in range(8):
                        p.op('pe', lambda e, k=k, tt=tt, nT=nT: e.matmul(pf.t[:, :], nT.t[:, k, tt * 128:(tt + 1) * 128], wb.t[:, k, 0:512],
                                                                      start=(k == 0), stop=(k == 7)), reads=[nT.b, wb.b], writes=[pf.b])
                    p.op('act', lambda e, ti=ti: e.activation(Fm.t[:, ti, :], pf.t[:, :], AF.Identity), reads=[pf.b], writes=[Fm.b])
                    for k in range(8):
                        p.op('pe', lambda e, k=k, tt=tt, nT=nT: e.matmul(pv.t[:, :], nT.t[:, k, tt * 128:(tt + 1) * 128], wb.t[:, k, 512:640],
                                                                      start=(k == 0), stop=(k == 7)), reads=[nT.b, wb.b], writes=[pv.b])
                    p.op('dve', lambda e, ti=ti: e.tensor_copy(Vm.t[:, ti, :], pv.t[:, :]), reads=[pv.b], writes=[Vm.b])
                c0 = blk * 512
                ropeC = ropeCb[blk % 2]
                ropeS = ropeSb[blk % 2]
                if blk < 8:
                    p.dma(ropeC.t[:], K["ropeC"][:, c0:c0 + 512], writes=[ropeC.b])
                    p.dma(ropeS.t[:], K["ropeS"][:, c0:c0 + 512], writes=[ropeS.b])
                for oc in range(5):
                    a = pq[qi_ % 2]
                    bq = pqp[qi_ % 2]
                    u1 = t1[qi_ % 2]
                    u2 = t2[qi_ % 2]
                    qi_ += 1
                    for k in range(8):
                        p.op('pe', lambda e, k=k, oc=oc, a=a, nT=nT, ncol=ncol: e.matmul(a.t[:, 0:ncol], wb.t[:, k, 640 + 128 * oc: 768 + 128 * oc], nT.t[:, k, 0:ncol],
                                                                                   start=(k == 0), stop=(k == 7)), reads=[nT.b, wb.b], writes=[a.b])
                    dstb = QT.b if oc < 4 else KT.b
                    dst = QT.t[:, oc, c0:c0 + ncol] if oc < 4 else KT.t[:, c0:c0 + ncol]
                    if blk < 8:
                        for k in range(8):
                            p.op('pe', lambda e, k=k, oc=oc, bq=bq, nT=nT, ncol=ncol: e.matmul(bq.t[:, 0:ncol], wb.t[:, k, 1280 + 128 * oc: 1408 + 128 * oc], nT.t[:, k, 0:ncol],
                                                                                        start=(k == 0), stop=(k == 7)), reads=[nT.b, wb.b], writes=[bq.b])
                        p.op('dve', lambda e, a=a, u1=u1, ropeC=ropeC: e.tensor_mul(u1.t[:, :], a.t[:, :], ropeC.t[:, :]), reads=[a.b, ropeC.b], writes=[u1.b])
                        p.op('dve', lambda e, bq=bq, u2=u2, ropeS=ropeS: e.tensor_mul(u2.t[:, :], bq.t[:, :], ropeS.t[:, :]), reads=[bq.b, ropeS.b], writes=[u2.b])
                        p.op('pool', lambda e, dst=dst, u1=u1, u2=u2: e.tensor_add(dst, u1.t[:, :], u2.t[:, :]), reads=[u1.b, u2.b], writes=[dstb])
                    else:
                        p.op('dve', lambda e, dst=dst, a=a, ncol=ncol: e.tensor_copy(dst, a.t[:, 0:ncol]), reads=[a.b], writes=[dstb])
            p.barrier()
        with ExitStack() as st:
            maskP = sbt(st, "maskP", [128, 512], BF16)
            maskN = sbt(st, "maskN", [128, 512], BF16)
            p.dma(maskP.t[:], K["maskP"], writes=[maskP.b])
            p.dma(maskN.t[:], K["maskN"], writes=[maskN.b])
            sk = sbt(st, "sk", [128, 8], F32)
            p.dma(sk.t[:], I["attn_sink"].partition_broadcast(128), writes=[sk.b])
            p.op('act', lambda e: e.activation(sk.t[:], sk.t[:], AF.Exp), reads=[sk.b], writes=[sk.b])
            skf = sbt(st, "skf", [128, 512], F32)
            for g in range(2):
                for hh in range(4):
                    p.op('dve', lambda e, g=g, hh=hh: e.tensor_copy(skf.t[64 * g:64 * g + 64, hh * 128:(hh + 1) * 128],
                                                                  sk.t[64 * g:64 * g + 64, 4 * g + hh:4 * g + hh + 1].to_broadcast([64, 128])), reads=[sk.b], writes=[skf.b])
            pS = [pst(st, f"pS{i}", [128, 512]) for i in range(3)]
            pO = [pst(st, f"pO{i}", [128, 512]) for i in range(2)]
            pZ = [pst(st, f"pZ{i}", [128, 512]) for i in range(2)]
            PT = [sbt(st, f"PT{i}", [128, 512], BF16) for i in range(3)]
            den = [sbt(st, f"den{i}", [128, 512], F32) for i in range(2)]
            ATt = [sbt(st, f"ATt{i}", [128, 4, 128], BF16) for i in range(2)]
            OB = [[Buf() for _ in range(2)] for _ in range(2)]
            ZB = [[Buf() for _ in range(2)] for _ in range(2)]
            DB = [[Buf() for _ in range(2)] for _ in range(2)]
            si = 0
            pend = {"t": None}
            for qi in range(NT):
                AT = ATt[qi % 2]
                for g in range(2):
                    lo, hi = 64 * g, 64 * g + 64
                    keys = [(32, None), (33, None)]
                    if qi < 32:
                        if qi > 0:
                            keys.append((qi - 1, maskP))
                        keys.append((qi, None))
                        if qi < 31:
                            keys.append((qi + 1, maskN))
                    O = pO[qi % 2]
                    Z = pZ[qi % 2]
                    Ob = OB[qi % 2][g]
                    Zb = ZB[qi % 2][g]
                    Db = DB[qi % 2][g]
                    SP = []
                    for ki in range(len(keys)):
                        SP.append((pS[si % 3], PT[si % 3]))
                        si += 1

                    def emitS(ki):
                        kt, msk = keys[ki]
                        S_ = SP[ki][0]
                        p.op('pe', lambda e, S_=S_, kt=kt, qi=qi, lo=lo, hi=hi, msk=msk: e.matmul(
                            S_.t[:, :].rearrange("p (j q) -> p j q", j=4), KT.t[lo:hi, kt * 128:(kt + 1) * 128], QT.t[lo:hi, :, qi * 128:(qi + 1) * 128],
                            start=True, stop=(msk is None)), reads=[KT.b, QT.b], writes=[S_.b])
                        if msk is not None:
                            p.op('pe', lambda e, S_=S_, msk=msk: e.matmul(S_.t[:, :], identb.t[:, :], msk.t[:, :], start=False, stop=True),
                                 reads=[identb.b, msk.b], writes=[S_.b])
                    emitS(0)
                    if len(keys) > 1:
                        emitS(1)
                    for ki, (kt, msk) in enumerate(keys):
                        S_, P_ = SP[ki]
                        p.op('act', lambda e, S_=S_, P_=P_: e.activation(P_.t[:, :], S_.t[:, :], AF.Exp, scale=0.125), reads=[S_.b], writes=[P_.b])
                        if ki == 1 and pend["t"] is not None:
                            pend["t"]()
                            pend["t"] = None
                        if ki + 2 < len(keys):
                            emitS(ki + 2)
                        first = ki == 0
                        last = ki == len(keys) - 1
                        p.op('pe', lambda e, O=O, kt=kt, lo=lo, hi=hi, P_=P_, first=first, last=last: e.matmul(
                            O.t[lo:hi, :], Vm.t[:, kt, lo:hi], P_.t[:, :], start=first, stop=last), reads=[Vm.b, P_.b], writes=[Ob])
                        p.op('pe', lambda e, Z=Z, lo=lo, hi=hi, P_=P_, first=first, last=last: e.matmul(
                            Z.t[lo:hi, :], ones_b.t[:, 0:64], P_.t[:, :], start=first, stop=last), reads=[ones_b.b, P_.b], writes=[Zb])
                    dn = den[qi % 2]

                    def tail(dn=dn, Z=Z, O=O, lo=lo, hi=hi, AT=AT, Zb=Zb, Ob=Ob, Db=Db, g=g, qi=qi):
                        p.op('dve', lambda e: e.tensor_add(dn.t[lo:hi, :], Z.t[lo:hi, :], skf.t[lo:hi, :]), reads=[Zb, skf.b], writes=[Db])
                        p.op('act', lambda e: e.activation(dn.t[lo:hi, :], dn.t[lo:hi, :], AF.Ln), reads=[Db], writes=[Db])
                        p.op('act', lambda e: e.activation(dn.t[lo:hi, :], dn.t[lo:hi, :], AF.Exp, scale=-1.0), reads=[Db], writes=[Db])
                        p.op('dve', lambda e: e.tensor_mul(
                            AT.t[lo:hi, :, :], O.t[lo:hi, :].rearrange("p (j q) -> p j q", j=4),
                            dn.t[lo:hi, :].rearrange("p (j q) -> p j q", j=4)), reads=[Ob, Db], writes=[AT.b])
                        if g == 1:
                            p.dma(ATs[qi], AT.t[:], reads=[AT.b])
                    pend["t"] = tail
            if pend["t"] is not None:
                pend["t"]()
                pend["t"] = None
            p.barrier()

    with ExitStack() as st:
        wob = sbt(st, "wob", [128, 8, D], BF16)
        wst = [sbt(st, f"wost{i}", [128, D], F32) for i in range(2)]
        for j in range(8):
            s_ = wst[j % 2]
            if j < 4:
                p.dma(s_.t[:], I["mix_w_out"][j * 128:(j + 1) * 128, :], writes=[s_.b])
            else:
                jj = j - 4
                p.dma(s_.t[0:64, :], I["mix_w_out"][512 + 64 * jj:512 + 64 * jj + 64, :], writes=[s_.b])
                p.dma(s_.t[64:128, :], I["mix_w_out"][512 + 64 * (4 + jj):512 + 64 * (4 + jj) + 64, :], writes=[s_.b])
            ew(lambda e, j=j, s_=s_: e.tensor_copy(wob.t[:, j, :], s_.t[:]), [s_.b], [wob.b])
        Cc = sbt(st, "Cc", [128, 128], BF16)
        Sc = sbt(st, "Sc", [128, 128], BF16)
        p.dma(Cc.t[:], K["Cc"], writes=[Cc.b])
        p.dma(Sc.t[:], K["Sc"], writes=[Sc.b])
        CTb = [sbt(st, f"CTb{i}", [128, 32, 256], BF16) for i in range(2)]
        STb = [sbt(st, f"STb{i}", [128, 32, 256], BF16) for i in range(2)]
        pP = [pst(st, f"pP{i}", [128, 256]) for i in range(2)]
        pQ = [pst(st, f"pQ{i}", [128, 256]) for i in range(2)]
        pY = pst(st, "pY", [128, 256])
        pOo = [pst(st, f"pOo{i}", [128, 512]) for i in range(2)]
        Pb = [sbt(st, f"Pb{i}", [128, 256], BF16) for i in range(2)]
        Qb = [sbt(st, f"Qb{i}", [128, 256], BF16) for i in range(2)]
        YT = [sbt(st, f"YT{i}", [128, 4, 256], BF16) for i in range(2)]
        xres = [sbt(st, f"xres{i}", [128, D], F32) for i in range(2)]
        tmpo = [sbt(st, f"tmpo{i}", [128, 512], F32) for i in range(2)]
        hn = [sbt(st, f"hn{i}", [128, D], F32) for i in range(2)]
        ATl = [sbt(st, f"ATl{i}", [128, 4, 128], BF16) for i in range(2)]
        g1l = [load_gate(st, "g1lat", 0, 0, 0), load_gate(st, "g1ctx", 0, 1, 0)]
        cnt_ = 0
        oi = 0
        for kb in range(17):
            lat = kb < 16
            if lat:
                cb = CTb[kb % 2]
                sb_ = STb[kb % 2]
                for h4 in range(4):
                    p.dma(cb.t[:, h4 * 8:(h4 + 1) * 8, :], K["CT"][kb, :, h4 * 8:(h4 + 1) * 8, :], writes=[cb.b])
                    p.dma(sb_.t[:, h4 * 8:(h4 + 1) * 8, :], K["ST"][kb, :, h4 * 8:(h4 + 1) * 8, :], writes=[sb_.b])
                nti = 32
                t0 = 0
            else:
                cb = CTb[kb % 2]
                sb_ = STb[kb % 2]
                p.dma(cb.t[:, 0:2, :], K["C256"], writes=[cb.b])
                p.dma(sb_.t[:, 0:2, :], K["S256"], writes=[sb_.b])
                nti = 2
                t0 = 32
            Y = YT[kb % 2]
            for cc in range(4):
                a = pP[cnt_ % 2]
                b = pQ[cnt_ % 2]
                ab = Pb[cnt_ % 2]
                bb = Qb[cnt_ % 2]
                cnt_ += 1
                for i in range(nti):
                    p.op('pe', lambda e, a=a, i=i, cc=cc, cb=cb, t0=t0, nti=nti: e.matmul(a.t[:, :], Fm.t[:, t0 + i, cc * 128:(cc + 1) * 128], cb.t[:, i, :],
                                                                                     start=(i == 0), stop=(i == nti - 1)), reads=[Fm.b, cb.b], writes=[a.b])
                for i in range(nti):
                    p.op('pe', lambda e, b=b, i=i, cc=cc, sb_=sb_, t0=t0, nti=nti: e.matmul(b.t[:, :], Fm.t[:, t0 + i, cc * 128:(cc + 1) * 128], sb_.t[:, i, :],
                                                                                      start=(i == 0), stop=(i == nti - 1)), reads=[Fm.b, sb_.b], writes=[b.b])
                p.op('act', lambda e, a=a, ab=ab: e.activation(ab.t[:, :], a.t[:, :], AF.Identity), reads=[a.b], writes=[ab.b])
                p.op('act', lambda e, b=b, bb=bb: e.activation(bb.t[:, :], b.t[:, :], AF.Identity, scale=-1.0), reads=[b.b], writes=[bb.b])
                p.op('pe', lambda e, ab=ab: e.matmul(pY.t[:, :], Cc.t[:, :], ab.t[:, :], start=True, stop=False), reads=[Cc.b, ab.b], writes=[pY.b])
                p.op('pe', lambda e, bb=bb: e.matmul(pY.t[:, :], Sc.t[:, :], bb.t[:, :], start=False, stop=True), reads=[Sc.b, bb.b], writes=[pY.b])
                p.op('dve', lambda e, Y=Y, cc=cc: e.tensor_copy(Y.t[:, cc, :], pY.t[:, :]), reads=[pY.b], writes=[Y.b])
            for tt in range(2):
                ti = (kb * 2 + tt) if lat else 32 + tt
                col = 0 if lat else 1
                xr = xres[ti % 2]
                h_ = hn[ti % 2]
                p.dma(xr.t[:], tile_src(ti), writes=[xr.b])
                AT = ATl[ti % 2]
                p.dma(AT.t[:], ATs[ti], writes=[AT.b])
                for half in range(2):
                    po = pOo[oi % 2]
                    tm = tmpo[oi % 2]
                    oi += 1
                    for j in range(8):
                        lhs = Y.t[:, j, tt * 128:(tt + 1) * 128] if j < 4 else AT.t[:, j - 4, :]
                        rb = Y.b if j < 4 else AT.b
                        p.op('pe', lambda e, po=po, lhs=lhs, j=j, half=half: e.matmul(po.t[:, :], lhs, wob.t[:, j, half * 512:(half + 1) * 512],
                                                                                 start=(j == 0), stop=(j == 7)), reads=[rb, wob.b], writes=[po.b])
                    p.op('dve', lambda e, po=po, tm=tm, half=half, col=col: e.tensor_mul(tm.t[:, :], po.t[:, :], g1l[col].t[:, half * 512:(half + 1) * 512]),
                         reads=[po.b, g1l[col].b], writes=[tm.b])
                    p.op('pool', lambda e, tm=tm, xr=xr, h_=h_, half=half: e.tensor_add(h_.t[:, half * 512:(half + 1) * 512], xr.t[:, half * 512:(half + 1) * 512], tm.t[:, :]),
                         reads=[tm.b, xr.b], writes=[h_.b])
                p.dma(Hs[ti * 128:(ti + 1) * 128, :], h_.t[:], reads=[h_.b])
        p.barrier()
    L0.close()

    if stage == 1:
        for ti in range(32):
            pass
        with ExitStack() as st:
            bb_ = [sbt(st, f"bo{i}", [128, D], F32) for i in range(2)]
            for ti in range(32):
                b_ = bb_[ti % 2]
                p.dma(b_.t[:], Hs[ti * 128:(ti + 1) * 128, :], writes=[b_.b])
                p.dma(out[ti * 128:(ti + 1) * 128, :], b_.t[:], reads=[b_.b])
        p.emit()
        top.close()
        return nc

    def ffn(l, tiles, final):
        with ExitStack() as st:
            Wg = sbt(st, "Wg", [128, 8, FF], BF16)
            Wu = sbt(st, "Wu", [128, 8, FF], BF16)
            Wd = sbt(st, "Wd", [128, NFC, D], BF16)
            with ExitStack() as st2:
                stg = [sbt(st2, f"fst{i}", [128, FF], F32) for i in range(4)]
                si_ = 0
                for nm, Wt in (("ffn_w_gate", Wg), ("ffn_w_up", Wu)):
                    for k in range(8):
                        s_ = stg[si_ % 4]
                        si_ += 1
                        p.dma(s_.t[:, 0:1408], I[nm][l, k * 128:(k + 1) * 128, 0:1408], writes=[s_.b])
                        p.dma(s_.t[:, 1408:FF], I[nm][l, k * 128:(k + 1) * 128, 1408:FF], writes=[s_.b])
                        p.op('dve', lambda e, Wt=Wt, k=k, s_=s_: e.tensor_copy(Wt.t[:, k, 0:1408], s_.t[:, 0:1408]), reads=[s_.b], writes=[Wt.b])
                        p.op('act', lambda e, Wt=Wt, k=k, s_=s_: e.activation(Wt.t[:, k, 1408:FF], s_.t[:, 1408:FF], AF.Identity), reads=[s_.b], writes=[Wt.b])
                for fc in range(0, NFC, 2):
                    s_ = stg[si_ % 4]
                    si_ += 1
                    for q2 in range(2):
                        p.dma(s_.t[:, q2 * D:(q2 + 1) * D], I["ffn_w_down"][l, (fc + q2) * 128:(fc + q2 + 1) * 128, :], writes=[s_.b])
                    p.op('dve' if (fc // 2) % 2 == 0 else 'act', (lambda e, fc=fc, s_=s_: e.tensor_copy(Wd.t[:, fc:fc + 2, :].rearrange("p a d -> p (a d)"), s_.t[:, 0:2 * D])) if (fc // 2) % 2 == 0 else (lambda e, fc=fc, s_=s_: e.activation(Wd.t[:, fc:fc + 2, :].rearrange("p a d -> p (a d)"), s_.t[:, 0:2 * D], AF.Identity)), reads=[s_.b], writes=[Wd.b])
                p.barrier()
            cn = make_norm(st, "f", nxt=0)
            g2l = [load_gate(st, "g2lat", l, 0, 1)] + ([load_gate(st, "g2ctx", l, 1, 1)] if not final else [])
            hx = [sbt(st, f"hx{i}", [128, D], F32) for i in range(3)]
            nTf = [sbt(st, f"nTf{i}", [128, 8, 256], BF16) for i in range(2)]
            actT = [sbt(st, f"actT{i}", [128, NFC, 256], BF16) for i in range(1)]
            sg = [sbt(st, f"sg{i}", [128, 256], F32) for i in range(2)]
            pgt = [pst(st, f"fpg{i}", [128, 256]) for i in range(2)]
            put = [pst(st, f"fpu{i}", [128, 256]) for i in range(2)]
            pdn = [pst(st, f"fpd{i}", [128, 512]) for i in range(2)]
            tm_ = [sbt(st, f"ftm{i}", [128, 512], F32) for i in range(2)]
            ho = [sbt(st, f"ho{i}", [128, D], F32) for i in range(2)]
            fss = [sbt(st, f"fss{i}", [128, 1], F32) for i in range(2)]
            fj = cn["junk"]
            fingt = None
            if final:
                fingt = sbt(st, "fing", [128, D], F32)
                p.dma(fingt.t[:], I["final_g"].partition_broadcast(128), writes=[fingt.b])
            gi_ = 0
            di_ = 0
            hi_ = 0
            for b0 in range(0, len(tiles), 2):
                tl = tiles[b0:b0 + 2]
                nT = nTf[(b0 // 2) % 2]
                aT = actT[0]
                xts = []
                for tt, ti in enumerate(tl):
                    col = 0 if ti < 32 else 1
                    hxt = hx[hi_ % 3]
                    hi_ += 1
                    src_ = Hs[ti * 128:(ti + 1) * 128, :]
                    norm_tile(cn, src_, l, col, 1, lambda k, tt=tt, nT=nT: nT.t[:, k, tt * 128:(tt + 1) * 128], nT.b, xt_fixed=hxt)
                    xts.append(hxt)
                for fc in range(NFC):
                    a = pgt[gi_ % 2]
                    b = put[gi_ % 2]
                    s2 = sg[gi_ % 2]
                    gi_ += 1
                    for k in range(8):
                        p.op('pe', lambda e, a=a, k=k, fc=fc, nT=nT: e.matmul(a.t[:, :], Wg.t[:, k, fc * 128:(fc + 1) * 128], nT.t[:, k, :], start=(k == 0), stop=(k == 7)),
                             reads=[Wg.b, nT.b], writes=[a.b])
                    for k in range(8):
                        p.op('pe', lambda e, b=b, k=k, fc=fc, nT=nT: e.matmul(b.t[:, :], Wu.t[:, k, fc * 128:(fc + 1) * 128], nT.t[:, k, :], start=(k == 0), stop=(k == 7)),
                             reads=[Wu.b, nT.b], writes=[b.b])
                    p.op('act', lambda e, a=a, s2=s2: e.activation(s2.t[:, :], a.t[:, :], AF.Silu), reads=[a.b], writes=[s2.b])
                    p.op('dve', lambda e, b=b, s2=s2, aT=aT, fc=fc: e.tensor_mul(aT.t[:, fc, :], s2.t[:, :], b.t[:, :]), reads=[b.b, s2.b], writes=[aT.b])
                for tt, ti in enumerate(tl):
                    col = 0 if ti < 32 else 1
                    hxt = xts[tt]
                    h_ = ho[ti % 2]
                    for half in range(2):
                        po = pdn[di_ % 2]
                        tm = tm_[di_ % 2]
                        di_ += 1
                        for fc in range(NFC):
                            p.op('pe', lambda e, po=po, fc=fc, tt=tt, half=half, aT=aT: e.matmul(po.t[:, :], aT.t[:, fc, tt * 128:(tt + 1) * 128], Wd.t[:, fc, half * 512:(half + 1) * 512],
                                                                                         start=(fc == 0), stop=(fc == NFC - 1)), reads=[aT.b, Wd.b], writes=[po.b])
                        p.op('dve', lambda e, po=po, tm=tm, half=half, col=col: e.tensor_mul(tm.t[:, :], po.t[:, :], g2l[col].t[:, half * 512:(half + 1) * 512]),
                             reads=[po.b, g2l[col].b], writes=[tm.b])
                        p.op('pool', lambda e, tm=tm, hxt=hxt, h_=h_, half=half: e.tensor_add(h_.t[:, half * 512:(half + 1) * 512], hxt.t[:, half * 512:(half + 1) * 512], tm.t[:, :]),
                             reads=[tm.b, hxt.b], writes=[h_.b])
                    if not final:
                        p.dma(Hs[ti * 128:(ti + 1) * 128, :], h_.t[:], reads=[h_.b])
                    else:
                        ss = fss[ti % 2]
                        p.op('act', lambda e, h_=h_, ss=ss: e.activation(fj.t[:], h_.t[:], AF.Square, accum_out=ss.t[:]), reads=[h_.b], writes=[fj.b, ss.b])
                        p.op('dve', lambda e, ss=ss: e.tensor_scalar(ss.t[:], ss.t[:], 1.0 / D, EPS, ALU.mult, ALU.add), reads=[ss.b], writes=[ss.b])
                        p.op('act', lambda e, ss=ss: e.activation(ss.t[:], ss.t[:], AF.Sqrt), reads=[ss.b], writes=[ss.b])
                        p.op('dve', lambda e, ss=ss: e.reciprocal(ss.t[:], ss.t[:]), reads=[ss.b], writes=[ss.b])
                        p.op('dve', lambda e, h_=h_, ss=ss: e.scalar_tensor_tensor(h_.t[:], h_.t[:], ss.t[:, 0:1], fingt.t[:], ALU.mult, ALU.mult),
                             reads=[h_.b, ss.b, fingt.b], writes=[h_.b])
                        p.dma(out[ti * 128:(ti + 1) * 128, :], h_.t[:], reads=[h_.b])
            p.barrier()

    ffn(0, list(range(34)), False)

    if stage == 2:
        with ExitStack() as st:
            bb_ = [sbt(st, f"bo{i}", [128, D], F32) for i in range(2)]
            for ti in range(32):
                b_ = bb_[ti % 2]
                p.dma(b_.t[:], Hs[ti * 128:(ti + 1) * 128, :], writes=[b_.b])
                p.dma(out[ti * 128:(ti + 1) * 128, :], b_.t[:], reads=[b_.b])
        p.emit()
        top.close()
        return nc

    s5_layer(nc, p, I, K, Hs, AB, load_gate, identf, identb, sbt, pst, ew, make_norm, norm_tile)
    if stage == 3:
        with ExitStack() as st:
            bb_ = [sbt(st, f"bo3{i}", [128, D], F32) for i in range(2)]
            for ti in range(32):
                b_ = bb_[ti % 2]
                p.dma(b_.t[:], Hs[ti * 128:(ti + 1) * 128, :], writes=[b_.b])
                p.dma(out[ti * 128:(ti + 1) * 128, :], b_.t[:], reads=[b_.b])
        p.emit()
        top.close()
        return nc
    ffn(1, list(range(16)), True)
    p.emit()
    top.close()
    return nc


def rev(ap):
    apl = [list(a) for a in ap.ap]
    n = apl[-1][1]
    stp = apl[-1][0]
    apl[-1][0] = -stp
    return bass.AP(ap.tensor, ap.offset + (n - 1) * stp, apl)


def bcast_last(ap, n):
    apl = [list(a) for a in ap.ap] + [[0, n]]
    return bass.AP(ap.tensor, ap.offset, apl)


def s5_layer(nc, p, I, K, Hs, AB, load_gate, identf, identb, sbt, pst, ew, make_norm, norm_tile, dbg=None):
    Hloc = Hs
    YGs = nc.dram_tensor("YGs", [8, 128, T // 2], BF16).ap()
    NTOK = T + TC
    with ExitStack() as S:
        nT = sbt(S, "s5nT", [128, 8, NTOK], BF16)
        with ExitStack() as st:
            cn = make_norm(st, "s5n", nxt=3)
            for ti in range(34):
                col = 0 if ti < 32 else 1
                norm_tile(cn, Hs[ti * 128:(ti + 1) * 128, :], 1, col, 0, lambda k, ti=ti: nT.t[:, k, ti * 128:(ti + 1) * 128], nT.b)
            p.barrier()
        PRM = sbt(S, "s5prm", [128, 3, 64], F32)
        Cw = sbt(S, "s5Cw", [128, 64, 2, 32], F32)
        Bz = sbt(S, "s5Bz", [128, 2, 64, 2, 16], F32)
        LC = 4
        NCM = 512 // LC
        PW = sbt(S, "s5PW", [128, LC + 1, 2, 64], F32)
        PRM4 = sbt(S, "s5prm4", [128, 3, 64], F32)
        dT = sbt(S, "s5dT", [128, 8], F32)
        with ExitStack() as st:
            pt = pst(st, "s5pt", [128, 128])
            def V(nm):
                return sbt(st, "s5v_" + nm, [128, 64], F32)
            arow = sbt(st, "s5arow", [64, 2, 128], F32)
            p.dma(arow.t[:, 0, :], I["ssm_a_re"].rearrange("(dq g) p -> dq (g p)", g=2), writes=[arow.b])
            p.dma(arow.t[:, 1, :], I["ssm_a_im"].rearrange("(dq g) p -> dq (g p)", g=2), writes=[arow.b])
            are, aim = V("are"), V("aim")
            for i_, dst in enumerate((are, aim)):
                p.op('pe', lambda e, i_=i_: e.transpose(pt.t[:, 0:64], arow.t[:, i_, :], identf.t[0:64, 0:64]), reads=[arow.b, identf.b], writes=[pt.b])
                p.op('dve', lambda e, dst=dst: e.tensor_copy(dst.t[:], pt.t[:, 0:64]), reads=[pt.b], writes=[dst.b])
            drow = sbt(st, "s5drow", [8, 128], F32)
            p.dma(drow.t[:], I["ssm_d"], writes=[drow.b])
            p.op('pe', lambda e: e.transpose(pt.t[:, 0:8], drow.t[:, :], identf.t[0:8, 0:8]), reads=[drow.b, identf.b], writes=[pt.b])
            p.op('dve', lambda e: e.tensor_copy(dT.t[:], pt.t[:, 0:8]), reads=[pt.b], writes=[dT.b])
            ldb = sbt(st, "s5ldb", [128, 128], F32)
            p.dma(ldb.t[:], I["ssm_log_dt"].partition_broadcast(128), writes=[ldb.b])
            dt = V("dt")
            for g2 in range(2):
                p.op('dve', lambda e, g2=g2: e.tensor_copy(dt.t[64 * g2:64 * g2 + 64, :], ldb.t[64 * g2:64 * g2 + 64, g2:128:2]), reads=[ldb.b], writes=[dt.b])
            p.op('act', lambda e: e.activation(dt.t[:], dt.t[:], AF.Exp), reads=[dt.b], writes=[dt.b])
            xr, th, mag = V("xr"), V("th"), V("mag")
            p.op('dve', lambda e: e.tensor_mul(xr.t[:], are.t[:], dt.t[:]), reads=[are.b, dt.b], writes=[xr.b])
            p.op('dve', lambda e: e.tensor_mul(th.t[:], aim.t[:], dt.t[:]), reads=[aim.b, dt.b], writes=[th.b])
            p.op('act', lambda e: e.activation(mag.t[:], xr.t[:], AF.Exp), reads=[xr.b], writes=[mag.b])
            kf = V("kf")
            ki = sbt(st, "s5ki", [128, 64], mybir.dt.int32)
            p.op('dve', lambda e: e.tensor_scalar(kf.t[:], th.t[:], 1.0 / (2 * math.pi), None, ALU.mult), reads=[th.b], writes=[kf.b])
            p.op('dve', lambda e: e.tensor_copy(ki.t[:], kf.t[:]), reads=[kf.b], writes=[ki.b])
            p.op('dve', lambda e: e.tensor_copy(kf.t[:], ki.t[:]), reads=[ki.b], writes=[kf.b])
            C1 = 6.28125
            C2 = 2 * math.pi - 6.28125
            thm = V("thm")
            p.op('dve', lambda e: e.scalar_tensor_tensor(thm.t[:], kf.t[:], -C1, th.t[:], ALU.mult, ALU.add), reads=[kf.b, th.b], writes=[thm.b])
            p.op('dve', lambda e: e.scalar_tensor_tensor(thm.t[:], kf.t[:], -C2, thm.t[:], ALU.mult, ALU.add), reads=[kf.b, thm.b], writes=[thm.b])
            xq, u2, qs, qc = V("xq"), V("u2"), V("qs"), V("qc")
            p.op('dve', lambda e: e.tensor_scalar(xq.t[:], thm.t[:], 0.25, None, ALU.mult), reads=[thm.b], writes=[xq.b])
            p.op('dve', lambda e: e.tensor_mul(u2.t[:], xq.t[:], xq.t[:]), reads=[xq.b], writes=[u2.b])
            sc_ = [(-1.0) ** k / math.factorial(2 * k + 1) for k in range(9)]
            cc_ = [(-1.0) ** k / math.factorial(2 * k) for k in range(9)]
            p.op('dve', lambda e: e.tensor_scalar(qs.t[:], u2.t[:], sc_[8], None, ALU.mult), reads=[u2.b], writes=[qs.b])
            p.op('dve', lambda e: e.tensor_scalar(qc.t[:], u2.t[:], cc_[8], None, ALU.mult), reads=[u2.b], writes=[qc.b])
            for k in range(7, 0, -1):
                p.op('dve', lambda e, k=k: e.scalar_tensor_tensor(qs.t[:], qs.t[:], sc_[k], u2.t[:], ALU.add, ALU.mult), reads=[qs.b, u2.b], writes=[qs.b])
                p.op('dve', lambda e, k=k: e.scalar_tensor_tensor(qc.t[:], qc.t[:], cc_[k], u2.t[:], ALU.add, ALU.mult), reads=[qc.b, u2.b], writes=[qc.b])
            sn, cs = V("sn"), V("cs")
            p.op('dve', lambda e: e.scalar_tensor_tensor(sn.t[:], qs.t[:], 1.0, xq.t[:], ALU.add, ALU.mult), reads=[qs.b, xq.b], writes=[sn.b])
            p.op('dve', lambda e: e.tensor_scalar(cs.t[:], qc.t[:], 1.0, None, ALU.add), reads=[qc.b], writes=[cs.b])
            ta, tb_ = V("ta"), V("tb")
            for _ in range(2):
                p.op('dve', lambda e: e.tensor_mul(ta.t[:], cs.t[:], cs.t[:]), reads=[cs.b], writes=[ta.b])
                p.op('dve', lambda e: e.tensor_mul(tb_.t[:], sn.t[:], sn.t[:]), reads=[sn.b], writes=[tb_.b])
                p.op('dve', lambda e: e.scalar_tensor_tensor(sn.t[:], cs.t[:], 2.0, sn.t[:], ALU.mult, ALU.mult), reads=[cs.b, sn.b], writes=[sn.b])
                p.op('dve', lambda e: e.tensor_sub(cs.t[:], ta.t[:], tb_.t[:]), reads=[ta.b, tb_.b], writes=[cs.b])
            p.op('dve', lambda e: e.tensor_copy(PRM.t[:, 0, :], mag.t[:]), reads=[mag.b], writes=[PRM.b])
            p.op('dve', lambda e: e.tensor_copy(PRM.t[:, 1, :], cs.t[:]), reads=[cs.b], writes=[PRM.b])
            p.op('dve', lambda e: e.tensor_copy(PRM.t[:, 2, :], sn.t[:]), reads=[sn.b], writes=[PRM.b])
            p.op('pool', lambda e: e.memset(PW.t[:, 0, 0, :], 1.0), writes=[PW.b])
            p.op('pool', lambda e: e.memset(PW.t[:, 0, 1, :], 0.0), writes=[PW.b])
            p.op('dve', lambda e: e.tensor_mul(PW.t[:, 1, 0, :], mag.t[:], cs.t[:]), reads=[mag.b, cs.b], writes=[PW.b])
            p.op('dve', lambda e: e.tensor_mul(PW.t[:, 1, 1, :], mag.t[:], sn.t[:]), reads=[mag.b, sn.b], writes=[PW.b])
            for k in range(2, LC + 1):
                p.op('dve', lambda e, k=k: e.tensor_mul(ta.t[:], PW.t[:, k - 1, 0, :], PW.t[:, 1, 0, :]), reads=[PW.b], writes=[ta.b])
                p.op('dve', lambda e, k=k: e.tensor_mul(tb_.t[:], PW.t[:, k - 1, 1, :], PW.t[:, 1, 1, :]), reads=[PW.b], writes=[tb_.b])
                p.op('dve', lambda e, k=k: e.tensor_sub(PW.t[:, k, 0, :], ta.t[:], tb_.t[:]), reads=[ta.b, tb_.b], writes=[PW.b])
                p.op('dve', lambda e, k=k: e.tensor_mul(ta.t[:], PW.t[:, k - 1, 0, :], PW.t[:, 1, 1, :]), reads=[PW.b], writes=[ta.b])
                p.op('dve', lambda e, k=k: e.tensor_mul(tb_.t[:], PW.t[:, k - 1, 1, :], PW.t[:, 1, 0, :]), reads=[PW.b], writes=[tb_.b])
                p.op('dve', lambda e, k=k: e.tensor_add(PW.t[:, k, 1, :], ta.t[:], tb_.t[:]), reads=[ta.b, tb_.b], writes=[PW.b])
            c4, s4, r4 = V("c4"), V("s4"), V("r4")
            p.op('dve', lambda e: e.tensor_copy(c4.t[:], cs.t[:]), reads=[cs.b], writes=[c4.b])
            p.op('dve', lambda e: e.tensor_copy(s4.t[:], sn.t[:]), reads=[sn.b], writes=[s4.b])
            p.op('dve', lambda e: e.tensor_copy(r4.t[:], mag.t[:]), reads=[mag.b], writes=[r4.b])
            for _ in range(int(round(math.log2(LC)))):
                p.op('dve', lambda e: e.tensor_mul(ta.t[:], c4.t[:], c4.t[:]), reads=[c4.b], writes=[ta.b])
                p.op('dve', lambda e: e.tensor_mul(tb_.t[:], s4.t[:], s4.t[:]), reads=[s4.b], writes=[tb_.b])
                p.op('dve', lambda e: e.scalar_tensor_tensor(s4.t[:], c4.t[:], 2.0, s4.t[:], ALU.mult, ALU.mult), reads=[c4.b, s4.b], writes=[s4.b])
                p.op('dve', lambda e: e.tensor_sub(c4.t[:], ta.t[:], tb_.t[:]), reads=[ta.b, tb_.b], writes=[c4.b])
                p.op('dve', lambda e: e.tensor_mul(r4.t[:], r4.t[:], r4.t[:]), reads=[r4.b], writes=[r4.b])
            p.op('dve', lambda e: e.tensor_copy(PRM4.t[:, 0, :], r4.t[:]), reads=[r4.b], writes=[PRM4.b])
            p.op('dve', lambda e: e.tensor_copy(PRM4.t[:, 1, :], c4.t[:]), reads=[c4.b], writes=[PRM4.b])
            p.op('dve', lambda e: e.tensor_copy(PRM4.t[:, 2, :], s4.t[:]), reads=[s4.b], writes=[PRM4.b])
            nr, ni, den, fre, fim = V("nr"), V("ni"), V("den"), V("fre"), V("fim")
            p.op('dve', lambda e: e.tensor_mul(nr.t[:], mag.t[:], cs.t[:]), reads=[mag.b, cs.b], writes=[nr.b])
            p.op('dve', lambda e: e.tensor_scalar(nr.t[:], nr.t[:], -1.0, None, ALU.add), reads=[nr.b], writes=[nr.b])
            p.op('dve', lambda e: e.tensor_mul(ni.t[:], mag.t[:], sn.t[:]), reads=[mag.b, sn.b], writes=[ni.b])
            p.op('dve', lambda e: e.tensor_mul(den.t[:], are.t[:], are.t[:]), reads=[are.b], writes=[den.b])
            p.op('dve', lambda e: e.tensor_mul(ta.t[:], aim.t[:], aim.t[:]), reads=[aim.b], writes=[ta.b])
            p.op('dve', lambda e: e.tensor_add(den.t[:], den.t[:], ta.t[:]), reads=[den.b, ta.b], writes=[den.b])
            p.op('dve', lambda e: e.reciprocal(den.t[:], den.t[:]), reads=[den.b], writes=[den.b])
            p.op('dve', lambda e: e.tensor_mul(ta.t[:], nr.t[:], are.t[:]), reads=[nr.b, are.b], writes=[ta.b])
            p.op('dve', lambda e: e.tensor_mul(tb_.t[:], ni.t[:], aim.t[:]), reads=[ni.b, aim.b], writes=[tb_.b])
            p.op('dve', lambda e: e.tensor_add(fre.t[:], ta.t[:], tb_.t[:]), reads=[ta.b, tb_.b], writes=[fre.b])
            p.op('dve', lambda e: e.tensor_mul(fre.t[:], fre.t[:], den.t[:]), reads=[fre.b, den.b], writes=[fre.b])
            p.op('dve', lambda e: e.tensor_mul(ta.t[:], ni.t[:], are.t[:]), reads=[ni.b, are.b], writes=[ta.b])
            p.op('dve', lambda e: e.tensor_mul(tb_.t[:], nr.t[:], aim.t[:]), reads=[nr.b, aim.b], writes=[tb_.b])
            p.op('dve', lambda e: e.tensor_sub(fim.t[:], ta.t[:], tb_.t[:]), reads=[ta.b, tb_.b], writes=[fim.b])
            p.op('dve', lambda e: e.tensor_mul(fim.t[:], fim.t[:], den.t[:]), reads=[fim.b, den.b], writes=[fim.b])
            Br = [sbt(st, f"s5Br{i}", [128, 64, 16], F32) for i in range(2)]
            for i_, nm in enumerate(("ssm_b_re", "ssm_b_im")):
                src = I[nm].rearrange("d (q g) p h -> g p (d q) h", g=2)
                for g2 in range(2):
                    p.dma(Br[i_].t[64 * g2:64 * g2 + 64, :, :], src[g2], writes=[Br[i_].b])
            p.op('pool', lambda e: e.memset(Bz.t[:].rearrange("p a b c d -> p (a b c d)"), 0.0), writes=[Bz.b])
            m1 = sbt(st, "s5m1", [128, 64, 16], F32)
            m2 = sbt(st, "s5m2", [128, 64, 16], F32)
            fre_b = bcast_last(fre.t[:], 16)
            fim_b = bcast_last(fim.t[:], 16)
            p.op('dve', lambda e: e.tensor_mul(m1.t[:], Br[0].t[:], fre_b), reads=[Br[0].b, fre.b], writes=[m1.b])
            p.op('dve', lambda e: e.tensor_mul(m2.t[:], Br[1].t[:], fim_b), reads=[Br[1].b, fim.b], writes=[m2.b])
            for g2 in range(2):
                p.op('dve', lambda e, g2=g2: e.tensor_sub(Bz.t[64 * g2:64 * g2 + 64, 0, :, g2, :], m1.t[64 * g2:64 * g2 + 64], m2.t[64 * g2:64 * g2 + 64]),
                     reads=[m1.b, m2.b], writes=[Bz.b])
            p.op('dve', lambda e: e.tensor_mul(m1.t[:], Br[1].t[:], fre_b), reads=[Br[1].b, fre.b], writes=[m1.b])
            p.op('dve', lambda e: e.tensor_mul(m2.t[:], Br[0].t[:], fim_b), reads=[Br[0].b, fim.b], writes=[m2.b])
            for g2 in range(2):
                p.op('dve', lambda e, g2=g2: e.tensor_add(Bz.t[64 * g2:64 * g2 + 64, 1, :, g2, :], m1.t[64 * g2:64 * g2 + 64], m2.t[64 * g2:64 * g2 + 64]),
                     reads=[m1.b, m2.b], writes=[Bz.b])
            pts = [pst(st, f"s5ptb{i}", [128, 128]) for i in range(2)]
            n_ = 0
            p.op('pool', lambda e: e.memset(Cw.t[:].rearrange("p a b c -> p (a b c)"), 0.0), writes=[Cw.b])
            Cn = [sbt(st, f"s5Cn{i}", [128, 2, 64], F32) for i in range(2)]
            for ri, nm in enumerate(("ssm_c_re", "ssm_c_im")):
                for dQ in range(16):
                    d_, Q_ = dQ // 8, dQ % 8
                    cnb = Cn[n_ % 2]
                    pp = pts[n_ % 2]
                    n_ += 1
                    src = I[nm][d_, Q_ * 4 * 2:(Q_ * 4 + 4) * 2].rearrange("g h p -> (g h) p")
                    p.dma(cnb.t[:, 0, :], src, writes=[cnb.b])
                    p.dma(cnb.t[:, 1, :], src, writes=[cnb.b])
                    p.op('pe', lambda e, pp=pp, cnb=cnb: e.transpose(pp.t[:, :], cnb.t[:, :, :].rearrange("p a b -> p (a b)"), identf.t[:, :]),
                         reads=[cnb.b, identf.b], writes=[pp.b])
                    sgn = 1.0 if ri == 0 else -1.0
                    for g2 in range(2):
                        p.op('act', lambda e, pp=pp, g2=g2, ri=ri, dQ=dQ, sgn=sgn: e.activation(
                            Cw.t[64 * g2:64 * g2 + 64, dQ * 4:(dQ + 1) * 4, ri, 16 * g2:16 * g2 + 16],
                            pp.t[64 * g2:64 * g2 + 64, :].rearrange("p (q g h) -> p q g h", q=4, g=2)[:, :, g2, :], AF.Identity, scale=sgn),
                            reads=[pp.b], writes=[Cw.b])
            p.barrier()

        if dbg is not None:
            dbg(PRM, Cw, nT)
            return

        with ExitStack() as st:
            yacc = sbt(st, "s5yacc", [128, T // 2], F32)
            Ec = sbt(st, "s5Ec", [128, 8, NCM], F32)
            Es = sbt(st, "s5Es", [128, 8, NCM], F32)
            wc = sbt(st, "s5wc", [128, 8], F32)
            ws = sbt(st, "s5ws", [128, 8], F32)
            wt1 = sbt(st, "s5wt1", [128, 8], F32)
            wt2 = sbt(st, "s5wt2", [128, 8], F32)
            et1 = sbt(st, "s5et1", [128, 8, NCM // 2], F32)
            et2 = sbt(st, "s5et2", [128, 8, NCM // 2], F32)
            hp = sbt(st, "s5hp", [128, 8, 2], F32)
            ini = sbt(st, "s5ini", [128, 8, 2], F32)
            itmp = sbt(st, "s5itmp", [128, 8, 2], F32)
            nsth = sbt(st, "s5nsth", [128, 64], F32)
            p.op('dve', lambda e: e.tensor_scalar(nsth.t[:], PRM4.t[:, 2, :], -1.0, None, ALU.mult), reads=[PRM4.b], writes=[nsth.b])
            hpb = [Buf() for _ in range(8)]
            CwP = sbt(st, "s5CwP", [128, LC + 1, 8, 2, 32], F32)
            ctm = [sbt(st, f"s5ctm{i}", [128, 4, 32], F32) for i in range(2)]
            BP = sbt(st, "s5BP", [128, LC, 2, 2, 4, 32], F32)
            Win = sbt(st, "s5Win", [128, 2, 2, LC, 128], BF16)
            Mw = sbt(st, "s5Mw", [128, 2, LC, 32], BF16)
            Wout = sbt(st, "s5Wout", [128, 8, LC, 2, 32], BF16)
            pts = [pst(st, f"s5ptq{i}", [128, 128]) for i in range(1)]
            pk = pst(st, "s5pk", [128, 16, 32])
            NS = 8
            Wk_ = [[sbt(st, f"s5w{j}_{i}", [128, 2, NCM], F32) for i in range(3)] for j in range(NS)]
            Hb = [sbt(st, f"s5hb{j}", [128, 2, NCM + 4], BF16) for j in range(NS)]
            Esg = sbt(st, "s5Esg", [128, 8, 2, NCM], F32)

            def swap2(tb_, ncn):
                b1 = tb_.t[:, 1, 0:ncn]
                apl = [list(a_) for a_ in b1.ap]
                return bass.AP(b1.tensor, b1.offset, [apl[0], [-NCM, 2], apl[-1]])

            def bc2(ap2):
                apl = [list(a_) for a_ in ap2.ap]
                return bass.AP(ap2.tensor, ap2.offset, [apl[0], [0, 2], apl[-1]])
            pS = [pst(st, f"s5pS{j}", [128, 512]) for j in range(4)]
            py = [pst(st, f"s5py{i}", [128, LC, NCM]) for i in range(2)]
            ygb = [sbt(st, f"s5yg{i}", [128, 512], BF16) for i in range(2)]
            gtb = [sbt(st, f"s5gt{i}", [128, 512], F32) for i in range(2)]
            tn_ = 0
            blkc = 0
            yi_ = 0
            psi = 0
            for Q in range(8):
                for d_ in range(2):
                    us = slice(d_ * 32 + Q * 4, d_ * 32 + Q * 4 + 4)
                    ts_ = slice(d_ * 4, d_ * 4 + 4)
                    c0_ = Cw.t[:, us, 0, :]
                    c1_ = Cw.t[:, us, 1, :]
                    for k in range(LC + 1):
                        pr = bcast_last(PW.t[:, k, 0, us], 32)
                        pi_ = bcast_last(PW.t[:, k, 1, us], 32)
                        t1_, t2_ = ctm
                        p.op('dve', lambda e, t1_=t1_, c0_=c0_, pr=pr: e.tensor_mul(t1_.t[:], c0_, pr), reads=[Cw.b, PW.b], writes=[t1_.b])
                        p.op('pool', lambda e, t2_=t2_, c1_=c1_, pi_=pi_: e.tensor_mul(t2_.t[:], c1_, pi_), reads=[Cw.b, PW.b], writes=[t2_.b])
                        p.op('dve', lambda e, t1_=t1_, t2_=t2_, k=k, ts_=ts_: e.tensor_add(CwP.t[:, k, ts_, 0, :], t1_.t[:], t2_.t[:]), reads=[t1_.b, t2_.b], writes=[CwP.b])
                        p.op('dve', lambda e, t1_=t1_, c1_=c1_, pr=pr: e.tensor_mul(t1_.t[:], c1_, pr), reads=[Cw.b, PW.b], writes=[t1_.b])
                        p.op('pool', lambda e, t2_=t2_, c0_=c0_, pi_=pi_: e.tensor_mul(t2_.t[:], c0_, pi_), reads=[Cw.b, PW.b], writes=[t2_.b])
                        p.op('dve', lambda e, t1_=t1_, t2_=t2_, k=k, ts_=ts_: e.tensor_sub(CwP.t[:, k, ts_, 1, :], t1_.t[:], t2_.t[:]), reads=[t1_.b, t2_.b], writes=[CwP.b])
                    b0_ = Bz.t[:, 0, us, :, :].rearrange("p a b c -> p a (b c)")
                    b1_ = Bz.t[:, 1, us, :, :].rearrange("p a b c -> p a (b c)")
                    for s_ in range(LC):
                        k = LC - 1 - s_
                        pr = bcast_last(PW.t[:, k, 0, us], 32)
                        pi_ = bcast_last(PW.t[:, k, 1, us], 32)
                        t1_, t2_ = ctm
                        p.op('dve', lambda e, t1_=t1_, b0_=b0_, pr=pr: e.tensor_mul(t1_.t[:], b0_, pr), reads=[Bz.b, PW.b], writes=[t1_.b])
                        p.op('pool', lambda e, t2_=t2_, b1_=b1_, pi_=pi_: e.tensor_mul(t2_.t[:], b1_, pi_), reads=[Bz.b, PW.b], writes=[t2_.b])
                        p.op('dve', lambda e, t1_=t1_, t2_=t2_, s_=s_, d_=d_: e.tensor_sub(BP.t[:, s_, 0, d_, :, :], t1_.t[:], t2_.t[:]), reads=[t1_.b, t2_.b], writes=[BP.b])
                        p.op('dve', lambda e, t1_=t1_, b0_=b0_, pi_=pi_: e.tensor_mul(t1_.t[:], b0_, pi_), reads=[Bz.b, PW.b], writes=[t1_.b])
                        p.op('pool', lambda e, t2_=t2_, b1_=b1_, pr=pr: e.tensor_mul(t2_.t[:], b1_, pr), reads=[Bz.b, PW.b], writes=[t2_.b])
                        p.op('dve', lambda e, t1_=t1_, t2_=t2_, s_=s_, d_=d_: e.tensor_add(BP.t[:, s_, 1, d_, :, :], t1_.t[:], t2_.t[:]), reads=[t1_.b, t2_.b], writes=[BP.b])
                p.op('act', lambda e: e.activation(Wout.t[:].rearrange("p x t r h -> p t x r h"), CwP.t[:, 1:LC + 1, :, :, :], AF.Identity), reads=[CwP.b], writes=[Wout.b])
                for d_ in range(2):
                    for ri in range(2):
                        for s_ in range(LC):
                            pp = pts[0]
                            tn_ += 1
                            p.op('pe', lambda e, pp=pp, s_=s_, ri=ri, d_=d_: e.transpose(pp.t[:, :], BP.t[:, s_, ri, d_, :, :].rearrange("p a b -> p (a b)"), identf.t[:, :]),
                                 reads=[BP.b, identf.b], writes=[pp.b])
                            p.op('act', lambda e, pp=pp, s_=s_, ri=ri, d_=d_: e.activation(Win.t[:, d_, ri, s_, :], pp.t[:, :], AF.Identity), reads=[pp.b], writes=[Win.b])
                    for thf in range(LC // 4):
                        for tq in range(4):
                            tau = thf * 4 + tq
                            for ql in range(4):
                                tix = d_ * 4 + ql
                                for ri in range(2):
                                    p.op('pe', lambda e, d_=d_, tau=tau, tq=tq, ql=ql, tix=tix, ri=ri, us=slice(d_ * 32 + Q * 4, d_ * 32 + Q * 4 + 4): e.matmul(
                                        pk.t[:, tq * 4 + ql, :], Bz.t[:, ri, us, :, :].rearrange("p q a b -> p (q a b)"), CwP.t[:, tau, tix, ri, :],
                                        start=(ri == 0), stop=(ri == 1)), reads=[Bz.b, CwP.b], writes=[pk.b])
                        for ql in range(4):
                            p.op('dve', lambda e, d_=d_, ql=ql, thf=thf: e.tensor_copy(Mw.t[32 * ql:32 * ql + 32, d_, thf * 4:thf * 4 + 4, :], pk.t[32 * ql:32 * ql + 32, ql:16:4, :]),
                                 reads=[pk.b], writes=[Mw.b])
                for d_ in range(2):
                    cols = slice(d_ * 32 + Q * 4, d_ * 32 + Q * 4 + 4)
                    p.op('dve', lambda e, d_=d_, cols=cols: e.tensor_copy(wc.t[:, d_ * 4:d_ * 4 + 4], PRM4.t[:, 1, cols]), reads=[PRM4.b], writes=[wc.b])
                    p.op('dve', lambda e, d_=d_, cols=cols: e.tensor_copy(ws.t[:, d_ * 4:d_ * 4 + 4], PRM4.t[:, 2, cols]), reads=[PRM4.b], writes=[ws.b])
                p.op('pool', lambda e: e.memset(Ec.t[:, :, 0:1], 1.0), writes=[Ec.b])
                p.op('pool', lambda e: e.memset(Es.t[:, :, 0:1], 0.0), writes=[Es.b])
                n = 1
                while n < NCM:
                    wcb = bcast_last(wc.t[:, :], n)
                    wsb = bcast_last(ws.t[:, :], n)
                    p.op('dve', lambda e, n=n, wcb=wcb: e.tensor_mul(et1.t[:, :, 0:n], Ec.t[:, :, 0:n], wcb), reads=[Ec.b, wc.b], writes=[et1.b])
                    p.op('pool', lambda e, n=n, wsb=wsb: e.tensor_mul(et2.t[:, :, 0:n], Es.t[:, :, 0:n], wsb), reads=[Es.b, ws.b], writes=[et2.b])
                    p.op('dve', lambda e, n=n: e.tensor_sub(Ec.t[:, :, n:2 * n], et1.t[:, :, 0:n], et2.t[:, :, 0:n]), reads=[et1.b, et2.b], writes=[Ec.b])
                    p.op('dve', lambda e, n=n, wcb=wcb: e.tensor_mul(et1.t[:, :, 0:n], Es.t[:, :, 0:n], wcb), reads=[Es.b, wc.b], writes=[et1.b])
                    p.op('pool', lambda e, n=n, wsb=wsb: e.tensor_mul(et2.t[:, :, 0:n], Ec.t[:, :, 0:n], wsb), reads=[Ec.b, ws.b], writes=[et2.b])
                    p.op('dve', lambda e, n=n: e.tensor_add(Es.t[:, :, n:2 * n], et1.t[:, :, 0:n], et2.t[:, :, 0:n]), reads=[et1.b, et2.b], writes=[Es.b])
                    p.op('dve', lambda e: e.tensor_mul(wt1.t[:], wc.t[:], wc.t[:]), reads=[wc.b], writes=[wt1.b])
                    p.op('dve', lambda e: e.tensor_mul(wt2.t[:], ws.t[:], ws.t[:]), reads=[ws.b], writes=[wt2.b])
                    p.op('dve', lambda e: e.scalar_tensor_tensor(ws.t[:], wc.t[:], 2.0, ws.t[:], ALU.mult, ALU.mult), reads=[wc.b, ws.b], writes=[ws.b])
                    p.op('dve', lambda e: e.tensor_sub(wc.t[:], wt1.t[:], wt2.t[:]), reads=[wt1.b, wt2.b], writes=[wc.b])
                    n *= 2
                p.op('dve', lambda e: e.tensor_copy(Esg.t[:, :, 0, :], Es.t[:, :, :]), reads=[Es.b], writes=[Esg.b])
                p.op('dve', lambda e: e.tensor_scalar(Esg.t[:, :, 1, :], Es.t[:, :, :], -1.0, None, ALU.mult), reads=[Es.b], writes=[Esg.b])
                blist = []
                for d_ in range(2):
                    if d_ == 0:
                        blocks = [(T, TC, False)] + [(b * 512, 512, True) for b in range(4)]
                    else:
                        blocks = [(T, TC, False)] + [(b * 512, 512, False) for b in (7, 6, 5, 4)] + [(b * 512, 512, True) for b in (3, 2, 1, 0)]
                    for bi_, (c0, n, islat) in enumerate(blocks):
                        blist.append((d_, bi_, c0, n, islat))
                bctx = {}

                def front_pe(k, Q=Q):
                    d_, bi_, c0, n, islat = blist[k]
                    gk = Q * 18 + k
                    ncn = n // LC
                    sets = [(gk % 2) * 4 + ql for ql in range(4)]
                    rhs_all = []
                    for ql in range(4):
                        base = nT.t[32 * ql:32 * ql + 32, Q, c0:c0 + n]
                        apl = [list(a_) for a_ in base.ap]
                        lst = []
                        for s_ in range(LC):
                            ap2 = [list(a_) for a_ in apl]
                            if d_ == 0:
                                ap2[-1] = [apl[-1][0] * LC, ncn]
                                lst.append(bass.AP(base.tensor, base.offset + s_ * apl[-1][0], ap2))
                            else:
                                ap2[-1] = [-apl[-1][0] * LC, ncn]
                                lst.append(bass.AP(base.tensor, base.offset + (n - 1 - s_) * apl[-1][0], ap2))
                        rhs_all.append(lst)
                    pss = [pS[ql] for ql in range(4)]
                    for ri in range(2):
                        for s_ in range(LC):
                            for ql in range(4):
                                ps_ = pss[ql]
                                p.op('pe', lambda e, ps_=ps_, ri=ri, s_=s_, ql=ql, d_=d_, ncn=ncn, r_=rhs_all[ql][s_]: e.matmul(
                                    ps_.t[:, ri * 128:ri * 128 + ncn], Win.t[32 * ql:32 * ql + 32, d_, ri, s_, :], r_, start=(s_ == 0), stop=(s_ == LC - 1), tile_position=(32 * ql, 0)),
                                    reads=[Win.b, nT.b], writes=[ps_.b])
                    bctx[k] = (ncn, sets, rhs_all)

                def front_ev(k, Q=Q):
                    d_, bi_, c0, n, islat = blist[k]
                    ncn, sets, rhs_all = bctx[k]
                    pss = [pS[ql] for ql in range(4)]
                    for ql in range(4):
                        ps_ = pss[ql]
                        X, P1, P2 = Wk_[sets[ql]]
                        p.op('act', lambda e, X=X, ps_=ps_, ncn=ncn: e.activation(X.t[:, :, 0:ncn], ps_.t[:, 0:256].rearrange("p (r c) -> p r c", r=2)[:, :, 0:ncn], AF.Identity),
                             reads=[ps_.b], writes=[X.b])
                    for ql in range(4):
                        X, P1, P2 = Wk_[sets[ql]]
                        tix = d_ * 4 + ql
                        ecb = bc2(Ec.t[:, tix, 0:ncn])
                        p.op('dve', lambda e, X=X, P1=P1, ecb=ecb, ncn=ncn: e.tensor_mul(P1.t[:, :, 0:ncn], X.t[:, :, 0:ncn], ecb), reads=[X.b, Ec.b], writes=[P1.b])
                        p.op('pool', lambda e, X=X, P2=P2, tix=tix, ncn=ncn: e.tensor_mul(P2.t[:, :, 0:ncn], swap2(X, ncn), Esg.t[:, tix, :, 0:ncn]), reads=[X.b, Esg.b], writes=[P2.b])
                    for ql in range(4):
                        X, P1, P2 = Wk_[sets[ql]]
                        p.op('dve', lambda e, P1=P1, P2=P2, ncn=ncn: e.tensor_add(P1.t[:, :, 0:ncn], P1.t[:, :, 0:ncn], P2.t[:, :, 0:ncn]), reads=[P1.b, P2.b], writes=[P1.b])

                def back(k, Q=Q):
                    d_, bi_, c0, n, islat = blist[k]
                    gk = Q * 18 + k
                    ncn, sets, rhs_all = bctx.pop(k)
                    pyt = py[gk % 2]
                    for ql in range(4):
                        tix = d_ * 4 + ql
                        u = d_ * 32 + Q * 4 + ql
                        hb_ = hpb[tix]
                        hbs = Hb[sets[ql]]
                        if bi_ > 0:
                            p.op('act', lambda e, tix=tix, u=u: e.activation(itmp.t[:, tix, 0:1], hp.t[:, tix, 1:2], AF.Identity, scale=nsth.t[:, u:u + 1]), reads=[hb_, nsth.b], writes=[hb_])
                            p.op('act', lambda e, tix=tix, u=u: e.activation(ini.t[:, tix, 0:1], hp.t[:, tix, 0:1], AF.Identity, scale=PRM4.t[:, 1, u:u + 1], bias=itmp.t[:, tix, 0:1]),
                                 reads=[hb_, PRM4.b], writes=[hb_])
                            p.op('act', lambda e, tix=tix, u=u: e.activation(itmp.t[:, tix, 1:2], hp.t[:, tix, 0:1], AF.Identity, scale=PRM4.t[:, 2, u:u + 1]), reads=[hb_, PRM4.b], writes=[hb_])
                            p.op('act', lambda e, tix=tix, u=u: e.activation(ini.t[:, tix, 1:2], hp.t[:, tix, 1:2], AF.Identity, scale=PRM4.t[:, 1, u:u + 1], bias=itmp.t[:, tix, 1:2]),
                                 reads=[hb_, PRM4.b], writes=[hb_])
                            if islat:
                                p.op('act', lambda e, hbs=hbs, tix=tix: e.activation(hbs.t[:, :, 0], hp.t[:, tix, :], AF.Identity), reads=[hb_], writes=[hbs.b])
                    for ql in range(4):
                        X, P1, P2 = Wk_[sets[ql]]
                        tix = d_ * 4 + ql
                        u = d_ * 32 + Q * 4 + ql
                        hb_ = hpb[tix]
                        rb = PRM4.t[:, 0, u:u + 1].to_broadcast([128, ncn])
                        for ri in range(2):
                            if bi_ == 0:
                                init_, irds = 0.0, []
                            else:
                                init_, irds = ini.t[:, tix, ri:ri + 1], [hb_]
                            p.op('dve', lambda e, X=X, P1=P1, rb=rb, init_=init_, ri=ri, ncn=ncn: e.tensor_tensor_scan(X.t[:, ri, 0:ncn], rb, P1.t[:, ri, 0:ncn], init_, ALU.mult, ALU.add),
                                 reads=[P1.b, PRM4.b] + irds, writes=[X.b])
                    for ql in range(4):
                        X, P1, P2 = Wk_[sets[ql]]
                        tix = d_ * 4 + ql
                        ecb = bc2(Ec.t[:, tix, 0:ncn])
                        p.op('dve', lambda e, X=X, P1=P1, ecb=ecb, ncn=ncn: e.tensor_mul(P1.t[:, :, 0:ncn], X.t[:, :, 0:ncn], ecb), reads=[X.b, Ec.b], writes=[P1.b])
                        p.op('pool', lambda e, X=X, P2=P2, tix=tix, ncn=ncn: e.tensor_mul(P2.t[:, :, 0:ncn], swap2(X, ncn), Esg.t[:, tix, :, 0:ncn]), reads=[X.b, Esg.b], writes=[P2.b])
                    for ql in range(4):
                        X, P1, P2 = Wk_[sets[ql]]
                        p.op('dve', lambda e, P1=P1, P2=P2, ncn=ncn: e.tensor_sub(P1.t[:, :, 0:ncn], P1.t[:, :, 0:ncn], P2.t[:, :, 0:ncn]), reads=[P1.b, P2.b], writes=[P1.b])
                    for ql in range(4):
                        X, P1, P2 = Wk_[sets[ql]]
                        tix = d_ * 4 + ql
                        hb_ = hpb[tix]
                        hbs = Hb[sets[ql]]
                        p.op('act', lambda e, tix=tix, P1=P1, ncn=ncn: e.activation(hp.t[:, tix, :], P1.t[:, :, ncn - 1], AF.Identity), reads=[P1.b], writes=[hb_])
                        if islat:
                            p.op('act', lambda e, hbs=hbs, P1=P1, ncn=ncn: e.activation(hbs.t[:, :, 1:ncn], P1.t[:, :, 0:ncn - 1], AF.Identity), reads=[P1.b], writes=[hbs.b])
                    if islat:
                        for t_ in range(LC):
                            for s_ in range(t_ + 1):
                                for ql in range(4):
                                    p.op('pe', lambda e, pyt=pyt, ql=ql, t_=t_, s_=s_, d_=d_, ncn=ncn, r_=rhs_all[ql][s_]: e.matmul(
                                        pyt.t[32 * ql:32 * ql + 32, t_, 0:ncn], Mw.t[32 * ql:32 * ql + 32, d_, t_ - s_, :], r_, start=(s_ == 0 and t_ == 0), stop=False,
                                        tile_position=(32 * ql, 32 * ql)), reads=[Mw.b, nT.b], writes=[pyt.b])
                        for t_ in range(LC):
                            for ri in range(2):
                                for ql in range(4):
                                    tix = d_ * 4 + ql
                                    hh_ = Hb[sets[ql]]
                                    p.op('pe', lambda e, pyt=pyt, ql=ql, t_=t_, ri=ri, tix=tix, hh_=hh_, ncn=ncn: e.matmul(
                                        pyt.t[32 * ql:32 * ql + 32, t_, 0:ncn], Wout.t[:, tix, t_, ri, :], hh_.t[:, ri, 0:ncn], start=False, stop=(ri == 1 and t_ == LC - 1),
                                        tile_position=(0, 32 * ql)), reads=[Wout.b, hh_.b], writes=[pyt.b])

                def yevac(k, Q=Q):
                    d_, bi_, c0, n, islat = blist[k]
                    if not islat:
                        return
                    gk = Q * 18 + k
                    ncn = n // LC
                    pyt = py[gk % 2]
                    if d_ == 0:
                        yv = yacc.t[:, c0:c0 + n].rearrange("p (c t) -> p t c", t=LC)
                        nv = nT.t[:, Q, c0:c0 + n].rearrange("p (c t) -> p t c", t=LC)
                        p.op('dve', lambda e, pyt=pyt, yv=yv, nv=nv, Q=Q: e.scalar_tensor_tensor(yv, nv, dT.t[:, Q:Q + 1], pyt.t[:, :, :], ALU.mult, ALU.add),
                             reads=[nT.b, dT.b, pyt.b], writes=[yacc.b])
                    else:
                        base = yacc.t[:, c0:c0 + n]
                        apl = [list(a_) for a_ in base.ap]
                        stp = apl[-1][0]
                        yv = bass.AP(base.tensor, base.offset + (n - 1) * stp, apl[:-1] + [[-stp, LC], [-stp * LC, ncn]])
                        p.op('dve', lambda e, pyt=pyt, yv=yv: e.tensor_add(yv, yv, pyt.t[:, :, :]), reads=[pyt.b, yacc.b], writes=[yacc.b])

                NB_ = len(blist)
                front_pe(0)
                front_ev(0)
                if NB_ > 1:
                    front_pe(1)
                for k in range(NB_):
                    if k + 1 < NB_:
                        front_ev(k + 1)
                    if k + 2 < NB_:
                        front_pe(k + 2)
                    back(k)
                    if k >= 1:
                        yevac(k - 1)
                yevac(NB_ - 1)
                for b in range(4):
                    yg = ygb[b % 2]
                    gt = gtb[b % 2]
                    ysl = yacc.t[:, b * 512:(b + 1) * 512]
                    p.op('dve', lambda e, gt=gt, ysl=ysl: e.tensor_mul(gt.t[:, :], ysl, ysl), reads=[yacc.b], writes=[gt.b])
                    p.op('dve', lambda e, gt=gt: e.tensor_scalar(gt.t[:, :], gt.t[:, :], 0.044715, 1.0, ALU.mult, ALU.add), reads=[gt.b], writes=[gt.b])
                    p.op('dve', lambda e, gt=gt, ysl=ysl: e.tensor_mul(gt.t[:, :], gt.t[:, :], ysl), reads=[gt.b, yacc.b], writes=[gt.b])
                    p.op('act', lambda e, gt=gt: e.activation(gt.t[:, :], gt.t[:, :], AF.Sigmoid, scale=2.0 * math.sqrt(2.0 / math.pi)), reads=[gt.b], writes=[gt.b])
                    p.op('dve', lambda e, gt=gt, ysl=ysl, yg=yg: e.tensor_mul(yg.t[:, :], gt.t[:, :], ysl), reads=[gt.b, yacc.b], writes=[yg.b])
                    p.dma(YGs[Q, :, b * 512:(b + 1) * 512], yg.t[:, :], reads=[yg.b])
            p.barrier()
    with ExitStack() as st:
        Wgl = sbt(st, "s5Wgl", [128, 8, 2 * D], BF16)
        with ExitStack() as st2:
            stg = [sbt(st2, f"s5gst{i}", [128, 2 * D], F32) for i in range(2)]
            for k in range(8):
                s_ = stg[k % 2]
                p.dma(s_.t[:, 0:D], I["ssm_glu_w"][k * 128:(k + 1) * 128, 0:D], writes=[s_.b])
                p.dma(s_.t[:, D:2 * D], I["ssm_glu_w"][k * 128:(k + 1) * 128, D:2 * D], writes=[s_.b])
                ew(lambda e, k=k, s_=s_: e.tensor_copy(Wgl.t[:, k, 0:D], s_.t[:, 0:D]), [s_.b], [Wgl.b])
                ew(lambda e, k=k, s_=s_: e.tensor_copy(Wgl.t[:, k, D:2 * D], s_.t[:, D:2 * D]), [s_.b], [Wgl.b])
            p.barrier()
        g1 = load_gate(st, "s5g1", 1, 0, 0)
        ygt = [sbt(st, f"s5ygt{i}", [128, 8, 128], BF16) for i in range(2)]
        h2 = [sbt(st, f"s5h2{i}", [128, D], F32) for i in range(2)]
        h3 = [sbt(st, f"s5h3{i}", [128, D], F32) for i in range(2)]
        sg_ = [sbt(st, f"s5sg{i}", [128, 512], F32) for i in range(2)]
        pa = [pst(st, f"s5pa{i}", [128, 512]) for i in range(2)]
        pgl = [pst(st, f"s5pg{i}", [128, 512]) for i in range(2)]
        c_ = 0
        YGs4 = YGs.rearrange("q p (r t) -> q p r t", r=2)
        for ti in range(16):
            yt = ygt[ti % 2]
            p.dma(yt.t[:], YGs[:, :, ti * 128:(ti + 1) * 128].rearrange("q p t -> p q t"), writes=[yt.b])
            hh = h2[ti % 2]
            ho_ = h3[ti % 2]
            p.dma(hh.t[:], Hloc[ti * 128:(ti + 1) * 128, :], writes=[hh.b])
            for half in range(2):
                a = pa[c_ % 2]
                g = pgl[c_ % 2]
                sg2 = sg_[c_ % 2]
                c_ += 1
                for k in range(8):
                    p.op('pe', lambda e, a=a, k=k, yt=yt, half=half: e.matmul(a.t[:, :], yt.t[:, k, :], Wgl.t[:, k, half * 512:(half + 1) * 512], start=(k == 0), stop=(k == 7)),
                         reads=[yt.b, Wgl.b], writes=[a.b])
                for k in range(8):
                    p.op('pe', lambda e, g=g, k=k, yt=yt, half=half: e.matmul(g.t[:, :], yt.t[:, k, :], Wgl.t[:, k, D + half * 512:D + (half + 1) * 512], start=(k == 0), stop=(k == 7)),
                         reads=[yt.b, Wgl.b], writes=[g.b])
                p.op('act', lambda e, g=g, sg2=sg2: e.activation(sg2.t[:, :], g.t[:, :], AF.Sigmoid), reads=[g.b], writes=[sg2.b])
                p.op('dve', lambda e, a=a, sg2=sg2: e.tensor_mul(sg2.t[:, :], sg2.t[:, :], a.t[:, :]), reads=[a.b, sg2.b], writes=[sg2.b])
                p.op('pool', lambda e, sg2=sg2, half=half: e.tensor_mul(sg2.t[:, :], sg2.t[:, :], g1.t[:, half * 512:(half + 1) * 512]), reads=[sg2.b, g1.b], writes=[sg2.b])
                p.op('pool', lambda e, sg2=sg2, hh=hh, ho_=ho_, half=half: e.tensor_add(ho_.t[:, half * 512:(half + 1) * 512], hh.t[:, half * 512:(half + 1) * 512], sg2.t[:, :]),
                     reads=[sg2.b, hh.b], writes=[ho_.b])
            p.dma(Hloc[ti * 128:(ti + 1) * 128, :], ho_.t[:], reads=[ho_.b])
        p.barrier()


_CACHE = {}


def kernel(**inputs):
    stage = int(inputs.pop("_stage", 99))
    if "nc" not in _CACHE or _CACHE.get("stage") != stage:
        _CACHE["nc"] = build(stage)
        _CACHE["stage"] = stage
        _CACHE["consts"] = host_consts()
    nc = _CACHE["nc"]
    if "consts_rev" not in _CACHE:
        _CACHE["consts_rev"] = host_consts(rev=True)
    f = lambda a: np.ascontiguousarray(np.asarray(a, dtype=np.float32))
    in_maps = []
    for core in range(8):
        b = core // 2
        rv = (core % 2 == 1) and stage >= 99
        cs = _CACHE["consts_rev"] if rv else _CACHE["consts"]
        sd = (lambda a: np.asarray(a)[0][::-1]) if rv else (lambda a: np.asarray(a)[0])
        xb = np.asarray(inputs["x"][b])
        cb = np.asarray(inputs["ctx"][b])
        if rv:
            xb = xb[::-1]
            cb = cb[::-1]
        m = {
            "x": f(xb), "c": f(inputs["c"][b]).reshape(8, 128), "ctx": f(cb),
            "c_ctx": f(inputs["c_ctx"]).reshape(8, 128),
            "mod_w": f(inputs["mod_w"]), "mod_b": f(inputs["mod_b"]).reshape(2, 48, 128),
            "norm_g": f(inputs["norm_g"]).reshape(2, 2, 8, 128),
            "ffn_w_gate": f(inputs["ffn_w_gate"]), "ffn_w_up": f(inputs["ffn_w_up"]), "ffn_w_down": f(inputs["ffn_w_down"]),
            "mix_w_in": f(inputs["mix_w_in"][0]), "mix_w_out": f(inputs["mix_w_out"][0]), "attn_sink": f(inputs["attn_sink"]).reshape(1, 8),
            "ssm_a_re": f(sd(inputs["ssm_a_re"])).reshape(128, 64), "ssm_a_im": f(sd(inputs["ssm_a_im"])).reshape(128, 64),
            "ssm_log_dt": f(sd(inputs["ssm_log_dt"])).reshape(1, 128),
            "ssm_b_re": f(sd(inputs["ssm_b_re"])), "ssm_b_im": f(sd(inputs["ssm_b_im"])),
            "ssm_c_re": f(sd(inputs["ssm_c_re"])), "ssm_c_im": f(sd(inputs["ssm_c_im"])),
            "ssm_d": f(inputs["ssm_d"][0]).reshape(8, 128), "ssm_glu_w": f(inputs["ssm_glu_w"][0]), "final_g": f(inputs["final_g"]).reshape(1, D),
        }
        m.update(cs)
        m["rk"] = np.array([[0]], np.int32)
        in_maps.append(m)
    res = run_bass_kernel_spmd(nc, in_maps, core_ids=list(range(8)))
    if stage < 99:
        return np.stack([np.asarray(res.results[2 * b]["out"], dtype=np.float32) for b in range(4)], axis=0)
    outp = np.stack([np.concatenate([np.asarray(res.results[2 * b]["out"], dtype=np.float32),
                                     np.asarray(res.results[2 * b + 1]["out"], dtype=np.float32)[::-1]], axis=0) for b in range(4)], axis=0)
    return outp
```

```python
import math
import numpy as np
import ml_dtypes
import concourse.bass as bass
import concourse.mybir as mybir
from concourse.bass_utils import run_bass_kernel_spmd
from contextlib import ExitStack

F32 = mybir.dt.float32
BF16 = mybir.dt.bfloat16
AF = mybir.ActivationFunctionType
ALU = mybir.AluOpType
NPBF = ml_dtypes.bfloat16

D = 1024
T = 4096
TC = 256
FF = 2816
NFC = 22
EPS = 1e-6


class Buf:
    def __init__(self, name=""):
        self.name = name
        self.last_w = None
        self.readers = []


class Prog:
    ENG = ['pe', 'act', 'dve', 'pool', 'sp']

    def __init__(self, nc, ndma_sems=10):
        self.nc = nc
        self.ops = {e: [] for e in self.ENG}
        self.cnt = {e: 0 for e in self.ENG}
        self.known = {e: {} for e in self.ENG}
        self.es = ExitStack()
        self.sem = {e: self.es.enter_context(nc.semaphore('s_' + e)) for e in self.ENG}
        self.dma_sems = {e: [self.es.enter_context(nc.semaphore(f'd_{e}{i}')) for i in range(ndma_sems)]
                         for e in ['sp', 'act', 'pool']}
        self.dma_val = {e: [0] * ndma_sems for e in self.dma_sems}
        self.dma_rr = {e: 0 for e in self.dma_sems}
        self.semobj = {}
        for e in self.ENG:
            self.semobj[('c', e)] = self.sem[e]
        for e in self.dma_sems:
            for i, s in enumerate(self.dma_sems[e]):
                self.semobj[('d', e, i)] = s
        self.q = 0
        self.rank_ap = None

    def _waits(self, eng, toks):
        need = {}
        for t in toks:
            if t is None:
                continue
            k, v = t
            if k == ('c', eng) and eng == 'pe':
                continue
            if self.known[eng].get(k, 0) >= v:
                continue
            if need.get(k, 0) < v:
                need[k] = v
        for k, v in need.items():
            self.known[eng][k] = v
        return list(need.items())

    def _deps(self, reads, writes):
        toks = []
        for b in reads:
            toks.append(b.last_w)
        for b in writes:
            toks.append(b.last_w)
            toks.extend(b.readers)
        return toks

    def _commit(self, tok, reads, writes):
        for b in reads:
            b.readers.append(tok)
            if len(b.readers) > 64:
                b.readers = b.readers[-64:]
        for b in writes:
            b.last_w = tok
            b.readers = []

    def op(self, eng, fn, reads=(), writes=()):
        waits = self._waits(eng, self._deps(reads, writes))
        self.cnt[eng] += 1
        tok = (('c', eng), self.cnt[eng])
        self.ops[eng].append((waits, fn, (self.sem[eng], 1)))
        self._commit(tok, reads, writes)
        return tok

    def dma(self, out, in_, reads=(), writes=(), eng=None, **kw):
        if eng is None:
            eng = ['sp', 'act', 'pool'][self.q % 2]
            self.q += 1
        toks = self._deps(reads, writes)
        i = self.dma_rr[eng]
        self.dma_rr[eng] = (i + 1) % len(self.dma_sems[eng])
        key = ('d', eng, i)
        prev = self.dma_val[eng][i]
        if prev > 0:
            toks.append((key, prev))
        waits = self._waits(eng, toks)
        self.dma_val[eng][i] = prev + 16
        tok = (key, prev + 16)
        def issue(e, out=out, in_=in_, eng=eng):
            o = out(self.dyn[eng]) if callable(out) else out
            i2 = in_(self.dyn[eng]) if callable(in_) else in_
            return e.dma_start(out=o, in_=i2, **kw)
        self.ops[eng].append((waits, issue, (self.dma_sems[eng][i], 16)))
        self._commit(tok, reads, writes)
        return tok

    def all_tokens(self):
        allt = []
        for e in self.ENG:
            if self.cnt[e]:
                allt.append((('c', e), self.cnt[e]))
        for e in self.dma_sems:
            for i, v in enumerate(self.dma_val[e]):
                if v:
                    allt.append((('d', e, i), v))
        return allt

    def barrier(self):
        allt = self.all_tokens()
        for e in self.ENG:
            w = self._waits(e, allt)
            if w:
                self.ops[e].append((w, None, None))

    def emit(self):
        nc = self.nc
        fin = self._waits('sp', self.all_tokens())
        self.ops['sp'].append((fin, None, None))
        self.dyn = {}
        with nc.Block() as block:
            def mk(eng):
                def run(e):
                    for waits, fn, inc in self.ops[eng]:
                        for k, v in waits:
                            e.wait_ge(self.semobj[k], v)
                        if fn is not None:
                            fn(e).then_inc(inc[0], inc[1])

                def body(e):
                    if eng in ('sp', 'act') and self.rank_ap is not None:
                        with e.register("rk_" + eng) as reg:
                            e.reg_load(reg, self.rank_ap)
                            self.dyn[eng] = e.snap(reg, min_val=0, max_val=2048)
                            run(e)
                    else:
                        run(e)
                return body
            block.tensor(mk('pe'))
            block.scalar(mk('act'))
            block.vector(mk('dve'))
            block.gpsimd(mk('pool'))
            block.sync(mk('sp'))
        self.es.close()


class TB:
    def __init__(self, t, name=""):
        self.t = t
        self.b = Buf(name)


def host_consts(rev=False):
    cs = {}
    cs["identf"] = np.eye(128, dtype=np.float32)
    cs["identb"] = np.eye(128, dtype=np.float32).astype(NPBF)
    t = np.arange(T)
    row = (t // 64).astype(np.float64)
    col = (t % 64).astype(np.float64)
    nf = 16
    inv = 10000.0 ** (-np.arange(nf, dtype=np.float64) / nf)
    inv = inv.astype(np.float32).astype(np.float64)
    ang = np.concatenate([(row[:, None].astype(np.float32) * inv[None].astype(np.float32)),
                          (col[:, None].astype(np.float32) * inv[None].astype(np.float32))], axis=-1).astype(np.float32)
    cosv = np.cos(ang).astype(np.float32)
    sinv = np.sin(ang).astype(np.float32)
    C = np.zeros((128, T), np.float32)
    S = np.zeros((128, T), np.float32)
    for p in range(128):
        d = p % 64
        i = d // 2
        C[p] = cosv[:, i]
        S[p] = sinv[:, i] * (-1.0 if d % 2 == 0 else 1.0)
    if rev:
        C = np.ascontiguousarray(C[:, ::-1])
        S = np.ascontiguousarray(S[:, ::-1])
    cs["ropeC"] = C
    cs["ropeS"] = S
    j = np.arange(128)[:, None]
    i = np.arange(128)[None, :]
    mp = np.where(j >= i, 0.0, -30000.0).astype(np.float32)
    mn = np.where(j <= i, 0.0, -30000.0).astype(np.float32)
    cs["maskP"] = np.tile(mp, (1, 4)).astype(NPBF)
    cs["maskN"] = np.tile(mn, (1, 4)).astype(NPBF)
    tt = np.arange(T, dtype=np.int64)
    tk = (tt[:, None] * tt[None, :]) % T
    angT = 2.0 * np.pi * tk / T
    ct = (np.cos(angT) / math.sqrt(T)).astype(np.float32)
    stt = (np.sin(angT) / math.sqrt(T)).astype(np.float32)
    if rev:
        ct = np.ascontiguousarray(ct[::-1, ::-1])
        stt = np.ascontiguousarray(stt[::-1, ::-1])
    cs["CT"] = np.ascontiguousarray(ct.reshape(32, 128, 16, 256).transpose(2, 1, 0, 3)).astype(NPBF)
    cs["ST"] = np.ascontiguousarray(stt.reshape(32, 128, 16, 256).transpose(2, 1, 0, 3)).astype(NPBF)
    t2 = np.arange(TC, dtype=np.int64)
    a2 = 2.0 * np.pi * ((t2[:, None] * t2[None, :]) % TC) / TC
    c2 = (np.cos(a2) / math.sqrt(TC)).astype(np.float32)
    s2 = (np.sin(a2) / math.sqrt(TC)).astype(np.float32)
    if rev:
        c2 = np.ascontiguousarray(c2[::-1, ::-1])
        s2 = np.ascontiguousarray(s2[::-1, ::-1])
    cs["C256"] = np.ascontiguousarray(c2.reshape(2, 128, 256).transpose(1, 0, 2)).astype(NPBF)
    cs["S256"] = np.ascontiguousarray(s2.reshape(2, 128, 256).transpose(1, 0, 2)).astype(NPBF)
    c64 = np.arange(64)
    a3 = 2.0 * np.pi * ((c64[:, None] * c64[None, :]) % 64) / 64
    cc = np.zeros((128, 128), np.float32)
    sc = np.zeros((128, 128), np.float32)
    for g in range(2):
        cc[g * 64:(g + 1) * 64, g * 64:(g + 1) * 64] = np.cos(a3) / 8.0
        sc[g * 64:(g + 1) * 64, g * 64:(g + 1) * 64] = np.sin(a3) / 8.0
    cs["Cc"] = cc.astype(NPBF)
    cs["Sc"] = sc.astype(NPBF)
    return cs


CONST_SHAPES = {
    "identf": ([128, 128], F32), "identb": ([128, 128], BF16), "ropeC": ([128, T], F32), "ropeS": ([128, T], F32),
    "maskP": ([128, 512], BF16), "maskN": ([128, 512], BF16),
    "CT": ([16, 128, 32, 256], BF16), "ST": ([16, 128, 32, 256], BF16),
    "C256": ([128, 2, 256], BF16), "S256": ([128, 2, 256], BF16), "Cc": ([128, 128], BF16), "Sc": ([128, 128], BF16),
}

IN_SHAPES = {
    "x": [T, D], "c": [8, 128], "ctx": [TC, D], "c_ctx": [8, 128],
    "mod_w": [2, D, 6 * D], "mod_b": [2, 48, 128], "norm_g": [2, 2, 8, 128],
    "ffn_w_gate": [2, D, FF], "ffn_w_up": [2, D, FF], "ffn_w_down": [2, FF, D],
    "mix_w_in": [D, 1280], "mix_w_out": [D, D], "attn_sink": [1, 8],
    "ssm_a_re": [128, 64], "ssm_a_im": [128, 64], "ssm_log_dt": [1, 128],
    "ssm_b_re": [2, 64, 64, 16], "ssm_b_im": [2, 64, 64, 16], "ssm_c_re": [2, 64, 16, 64], "ssm_c_im": [2, 64, 16, 64],
    "ssm_d": [8, 128], "ssm_glu_w": [D, 2 * D], "final_g": [1, D],
}


def build(stage=99):
    nc = bass.Bass("TRN2", target_bir_lowering=False)
    I = {n: nc.dram_tensor(n, sh, F32, kind="ExternalInput").ap() for n, sh in IN_SHAPES.items()}
    K = {n: nc.dram_tensor(n, sh, dt, kind="ExternalInput").ap() for n, (sh, dt) in CONST_SHAPES.items()}
    rk_in = nc.dram_tensor("rk", [1, 1], mybir.dt.int32, kind="ExternalInput").ap()
    HT = T // 2
    out = nc.dram_tensor("out", [T if stage < 99 else HT, D], F32, kind="ExternalOutput").ap()
    Hs = nc.dram_tensor("Hs", [T + TC, D], F32).ap()
    mod_b_flat = I["mod_b"].rearrange("l j p -> l (j p)")

    p = Prog(nc)
    p.rank_ap = None
    top = ExitStack()
    Hloc = nc.dram_tensor("Hloc", [T // 2, D], F32).ap()
    p.Hloc = Hloc

    uid = {"n": 0}

    def sbt(st, name, shape, dt):
        uid["n"] += 1
        return TB(st.enter_context(nc.sbuf_tensor(f"s{uid['n']}_{name}", shape, dt)), name)

    def pst(st, name, shape, dt=F32):
        uid["n"] += 1
        return TB(st.enter_context(nc.psum_tensor(f"p{uid['n']}_{name}", shape, dt)), name)

    identf = sbt(top, "identf", [128, 128], F32)
    identb = sbt(top, "identb", [128, 128], BF16)
    p.dma(identf.t[:], K["identf"], writes=[identf.b])
    p.dma(identb.t[:], K["identb"], writes=[identb.b])
    ones_b = sbt(top, "ones_b", [128, 128], BF16)
    p.op('pool', lambda e: e.memset(ones_b.t[:], 1.0), writes=[ones_b.b])
    AB = sbt(top, "AB", [128, 2, 2, 4, 8], F32)
    Gs = nc.dram_tensor("Gs", [8, 128, D], F32).ap()
    ATs = nc.dram_tensor("ATs", [34, 128, 4, 128], BF16).ap()

    def load_gate(st, nm, l, col, gi):
        g = sbt(st, nm, [128, D], F32)
        p.dma(g.t[:], Gs[(l * 2 + col) * 2 + gi], writes=[g.b])
        return g
    rr = {"i": 0}

    def ew(fn, reads, writes, engs=('dve', 'pool')):
        e = engs[rr["i"] % len(engs)]
        rr["i"] += 1
        return p.op(e, fn, reads=reads, writes=writes)

    with ExitStack() as st:
        gates = sbt(st, "gates", [128, 2, 2, 2, D], F32)
        crow = sbt(st, "crow", [16, 128], F32)
        p.dma(crow.t[0:8, :], I["c"], writes=[crow.b])
        p.dma(crow.t[8:16, :], I["c_ctx"], writes=[crow.b])
        pT = pst(st, "pT", [128, 96])
        scT = sbt(st, "scT", [128, 16], F32)
        p.op('pe', lambda e: e.transpose(pT.t[:, 0:16], crow.t[:, :], identf.t[0:16, 0:16]), reads=[crow.b, identf.b], writes=[pT.b])
        p.op('act', lambda e: e.activation(scT.t[:], pT.t[:, 0:16], AF.Silu), reads=[pT.b], writes=[scT.b])
        scbc = sbt(st, "scbc", [128, 16, 128], F32)
        for ck in range(16):
            ew(lambda e, ck=ck: e.tensor_copy(scbc.t[:, ck, :], scT.t[:, ck:ck + 1].to_broadcast([128, 128])), [scT.b], [scbc.b])
        mbrow = sbt(st, "mbrow", [48, 2, 128], F32)
        ngrow = sbt(st, "ngrow", [32, 128], F32)
        p.dma(mbrow.t[:, 0, :], I["mod_b"][0], writes=[mbrow.b])
        p.dma(mbrow.t[:, 1, :], I["mod_b"][1], writes=[mbrow.b])
        p.dma(ngrow.t[:], I["norm_g"].rearrange("l i k p -> (l i k) p"), writes=[ngrow.b])
        mbT = sbt(st, "mbT", [128, 2, 48], F32)
        ngT = sbt(st, "ngT", [128, 32], F32)
        for l in range(2):
            p.op('pe', lambda e, l=l: e.transpose(pT.t[:, 0:48], mbrow.t[:, l, :], identf.t[0:48, 0:48]), reads=[mbrow.b, identf.b], writes=[pT.b])
            p.op('dve', lambda e, l=l: e.tensor_copy(mbT.t[:, l, :], pT.t[:, 0:48]), reads=[pT.b], writes=[mbT.b])
        p.op('pe', lambda e: e.transpose(pT.t[:, 0:32], ngrow.t[:, :], identf.t[0:32, 0:32]), reads=[ngrow.b, identf.b], writes=[pT.b])
        p.op('dve', lambda e: e.tensor_copy(ngT.t[:], pT.t[:, 0:32]), reads=[pT.b], writes=[ngT.b])
        macc = sbt(st, "macc", [128, 2, 48, 2], F32)
        Wk = [sbt(st, f"Wk{i}", [128, 6 * D], F32) for i in range(2)]
        pg = [pst(st, f"pg{i}", [128, 512]) for i in range(2)]
        pm = pst(st, "pm", [128, 96])
        it = 0
        for l in range(2):
            for k in range(8):
                w = Wk[it % 2]
                it += 1
                for q3 in range(3):
                    p.dma(w.t[:, q3 * 2048:(q3 + 1) * 2048], I["mod_w"][l, k * 128:(k + 1) * 128, q3 * 2048:(q3 + 1) * 2048], writes=[w.b])
                for j in range(48):
                    p.op('pe', lambda e, j=j, w=w, k=k: e.matmul(pm.t[:, 2 * j:2 * j + 2], w.t[:, j * 128:(j + 1) * 128],
                                                                 scT.t[:, k:16:8], start=True, stop=True),
                         reads=[w.b, scT.b], writes=[pm.b])
                if k == 0:
                    p.op('dve', lambda e, l=l: e.tensor_copy(macc.t[:, l].rearrange("p j c -> p (j c)"), pm.t[:, :]), reads=[pm.b], writes=[macc.b])
                else:
                    p.op('dve', lambda e, l=l: e.tensor_add(macc.t[:, l].rearrange("p j c -> p (j c)"), macc.t[:, l].rearrange("p j c -> p (j c)"), pm.t[:, :]),
                         reads=[pm.b, macc.b], writes=[macc.b])
                gi = 0
                for col in range(2):
                    for g_i, which in enumerate((2, 5)):
                        for half in range(2):
                            pgt = pg[gi % 2]
                            gi += 1
                            p.op('pe', lambda e, pgt=pgt, col=col, k=k, w=w, which=which, half=half: e.matmul(
                                pgt.t[:, :], scbc.t[:, col * 8 + k, :], w.t[:, which * D + half * 512: which * D + half * 512 + 512], start=True, stop=True),
                                reads=[scbc.b, w.b], writes=[pgt.b])
                            dst = gates.t[:, l, col, g_i, half * 512:(half + 1) * 512]
                            if k == 0:
                                p.op('dve', lambda e, dst=dst, pgt=pgt: e.tensor_copy(dst, pgt.t[:, :]), reads=[pgt.b], writes=[gates.b])
                            else:
                                p.op('dve', lambda e, dst=dst, pgt=pgt: e.tensor_add(dst, dst, pgt.t[:, :]), reads=[pgt.b, gates.b], writes=[gates.b])
        gb = sbt(st, "gb", [128, D], F32)
        for l in range(2):
            for col in range(2):
                p.op('dve', lambda e, l=l, col=col: e.tensor_add(macc.t[:, l, :, col], macc.t[:, l, :, col], mbT.t[:, l, :]), reads=[macc.b, mbT.b], writes=[macc.b])
            for g_i, which in enumerate((2, 5)):
                p.dma(gb.t[:], mod_b_flat[l:l + 1, which * D:(which + 1) * D].partition_broadcast(128), writes=[gb.b])
                for col in range(2):
                    p.op('dve', lambda e, l=l, col=col, g_i=g_i: e.tensor_add(gates.t[:, l, col, g_i, :], gates.t[:, l, col, g_i, :], gb.t[:]),
                         reads=[gb.b, gates.b], writes=[gates.b])
            for col in range(2):
                for i2 in range(2):
                    sh = macc.t[:, l, (3 * i2) * 8:(3 * i2) * 8 + 8, col]
                    scl = macc.t[:, l, (3 * i2 + 1) * 8:(3 * i2 + 1) * 8 + 8, col]
                    gn = ngT.t[:, (l * 2 + i2) * 8:(l * 2 + i2) * 8 + 8]
                    p.op('dve', lambda e, l=l, col=col, i2=i2, scl=scl, gn=gn: e.scalar_tensor_tensor(
                        AB.t[:, l, col, 2 * i2, :], scl, 1.0, gn, ALU.add, ALU.mult), reads=[macc.b, ngT.b], writes=[AB.b])
                    p.op('dve', lambda e, l=l, col=col, i2=i2, sh=sh: e.tensor_copy(AB.t[:, l, col, 2 * i2 + 1, :], sh), reads=[macc.b], writes=[AB.b])
        for l in range(2):
            for col in range(2):
                for g_i in range(2):
                    p.dma(Gs[(l * 2 + col) * 2 + g_i], gates.t[:, l, col, g_i, :], reads=[gates.b])
        p.barrier()

    if stage == 0:
        with ExitStack() as st:
            g0 = load_gate(st, "g0dbg", 0, 0, 0)
            p.dma(out[0:128, :], g0.t[:], reads=[g0.b])
        p.dma(out[128:256, 0:128], AB.t[:].rearrange("p a b c d -> p (a b c d)"), reads=[AB.b])
        p.emit()
        top.close()
        return nc

    def make_norm(st, nm, nxt=2):
        ctxn = {}
        ctxn["xt"] = [sbt(st, f"{nm}xt{i}", [128, D], F32) for i in range(nxt)]
        ctxn["junk"] = sbt(st, f"{nm}junk", [128, D], BF16)
        ctxn["xn"] = [sbt(st, f"{nm}xn{i}", [128, D], BF16) for i in range(2)]
        ctxn["ss"] = [sbt(st, f"{nm}ss{i}", [128, 1], F32) for i in range(3)]
        ctxn["tp"] = [pst(st, f"{nm}tp{i}", [128, 8, 128], BF16) for i in range(2)]
        ctxn["n"] = 0
        return ctxn

    def norm_tile(cn, src, l, col, which, dst_fn, dst_buf, xt_fixed=None, ident=None):
        n = cn["n"]
        cn["n"] += 1
        xt = xt_fixed if xt_fixed is not None else cn["xt"][n % len(cn["xt"])]
        ss = cn["ss"][n % 3]
        xn = cn["xn"][n % 2]
        tp = cn["tp"][n % 2]
        junk = cn["junk"]
        idt = ident if ident is not None else identb
        p.dma(xt.t[:], src, writes=[xt.b])
        p.op('act', lambda e: e.activation(junk.t[:], xt.t[:], AF.Square, accum_out=ss.t[:]), reads=[xt.b], writes=[junk.b, ss.b])
        p.op('dve', lambda e: e.tensor_scalar(ss.t[:], ss.t[:], 1.0 / D, EPS, ALU.mult, ALU.add), reads=[ss.b], writes=[ss.b])
        p.op('act', lambda e: e.activation(ss.t[:], ss.t[:], AF.Sqrt), reads=[ss.b], writes=[ss.b])
        p.op('dve', lambda e: e.reciprocal(ss.t[:], ss.t[:]), reads=[ss.b], writes=[ss.b])
        p.op('dve', lambda e: e.tensor_scalar(xn.t[:], xt.t[:], ss.t[:, 0:1], None, ALU.mult), reads=[xt.b, ss.b], writes=[xn.b])
        for k in range(8):
            p.op('pe', lambda e, k=k: e.transpose(tp.t[:, k, :], xn.t[:, k * 128:(k + 1) * 128], idt.t[:]), reads=[xn.b, idt.b], writes=[tp.b])
        for k in range(8):
            p.op('act', lambda e, k=k: e.activation(dst_fn(k), tp.t[:, k, :], AF.Identity,
                                                    scale=AB.t[:, l, col, 2 * which, k:k + 1], bias=AB.t[:, l, col, 2 * which + 1, k:k + 1]),
                 reads=[tp.b, AB.b], writes=[dst_buf])
        return xt, ss

    def tile_src(ti):
        return I["x"][ti * 128:(ti + 1) * 128, :] if ti < 32 else I["ctx"][(ti - 32) * 128:(ti - 31) * 128, :]

    NT = 34
    L0 = ExitStack()
    Fm = sbt(L0, "Fm", [128, NT, 512], BF16)
    with ExitStack() as stBC:
        QT = sbt(stBC, "QT", [128, 4, NT * 128], BF16)
        KT = sbt(stBC, "KT", [128, NT * 128], BF16)
        Vm = sbt(stBC, "Vm", [128, NT, 128], BF16)
        with ExitStack() as st:
            wb = sbt(st, "wb", [128, 8, 1920], BF16)
            wst = [sbt(st, f"wst{i}", [128, 1280], F32) for i in range(1)]
            for k in range(8):
                s_ = wst[0]
                p.dma(s_.t[:], I["mix_w_in"][k * 128:(k + 1) * 128, :], writes=[s_.b])
                S = s_.t
                W = wb.t
                ew(lambda e, k=k, S=S, W=W: e.tensor_copy(W[:, k, 0:512], S[:, 0:512]), [s_.b], [wb.b])
                ew(lambda e, k=k, S=S, W=W: e.tensor_copy(W[:, k, 512:640], S[:, 1152:1280]), [s_.b], [wb.b])
                ew(lambda e, k=k, S=S, W=W: e.tensor_copy(W[:, k, 640:1152].rearrange("p (j h d) -> p j h d", j=4, h=2),
                                                          S[:, 512:1024].rearrange("p (h j d) -> p j h d", h=2, j=4)), [s_.b], [wb.b])
                ew(lambda e, k=k, S=S, W=W: e.tensor_copy(W[:, k, 1152:1280], S[:, 1024:1152]), [s_.b], [wb.b])
                for two in range(2):
                    ew(lambda e, k=k, S=S, W=W, two=two: e.tensor_copy(
                        W[:, k, 1280:1792].rearrange("p (j h i t) -> p j h i t", j=4, h=2, t=2)[:, :, :, :, two],
                        S[:, 512:1024].rearrange("p (h j i t) -> p j h i t", h=2, j=4, t=2)[:, :, :, :, 1 - two]), [s_.b], [wb.b])
                    ew(lambda e, k=k, S=S, W=W, two=two: e.tensor_copy(
                        W[:, k, 1792:1920].rearrange("p (i t) -> p i t", t=2)[:, :, two],
                        S[:, 1024:1152].rearrange("p (i t) -> p i t", t=2)[:, :, 1 - two]), [s_.b], [wb.b])
            ropeCb = [sbt(st, f"ropeC{i}", [128, 512], F32) for i in range(2)]
            ropeSb = [sbt(st, f"ropeS{i}", [128, 512], F32) for i in range(2)]
            cn = make_norm(st, "b")
            nTb = [sbt(st, f"nTb{i}", [128, 8, 512], BF16) for i in range(2)]
            pf = pst(st, "pf", [128, 512])
            pv = pst(st, "pv", [128, 128])
            pq = [pst(st, f"pq{i}", [128, 512]) for i in range(2)]
            pqp = [pst(st, f"pqp{i}", [128, 512]) for i in range(2)]
            t1 = [sbt(st, f"t1{i}", [128, 512], F32) for i in range(2)]
            t2 = [sbt(st, f"t2{i}", [128, 512], F32) for i in range(2)]
            qi_ = 0
            for blk in range(9):
                ntile = 4 if blk < 8 else 2
                ncol = ntile * 128
                nT = nTb[blk % 2]
                col = 0 if blk < 8 else 1
                for tt in range(ntile):
                    ti = blk * 4 + tt
                    norm_tile(cn, tile_src(ti), 0, col, 0, lambda k, tt=tt, nT=nT: nT.t[:, k, tt * 128:(tt + 1) * 128], nT.b)
                    for k in range(8):
                        p.op('pe', lambda e, k=k, tt=tt, nT=nT: e.matmul(pf.t[:, :], nT.t[:, k, tt * 128:(tt + 1) * 128], wb.t[:, k, 0:512],
                                                                      start=(k == 0), stop=(k == 7)), reads=[nT.b, wb.b], writes=[pf.b])
                    p.op('act', lambda e, ti=ti: e.activation(Fm.t[:, ti, :], pf.t[:, :], AF.Identity), reads=[pf.b], writes=[Fm.b])
                    for k in range(8):
                        p.op('pe', lambda e, k=k, tt=tt, nT=nT: e.matmul(pv.t[:, :], nT.t[:, k, tt * 128:(tt + 1) * 128], wb.t[:, k, 512:640],
                                                                      start=(k == 0), stop=(k == 7)), reads=[nT.b, wb.b], writes=[pv.b])
                    p.op('dve', lambda e, ti=ti: e.tensor_copy(Vm.t[:, ti, :], pv.t[:, :]), reads=[pv.b], writes=[Vm.b])
                c0 = blk * 512
                ropeC = ropeCb[blk % 2]
                ropeS = ropeSb[blk % 2]
                if blk < 8:
                    p.dma(ropeC.t[:], K["ropeC"][:, c0:c0 + 512], writes=[ropeC.b])
                    p.dma(ropeS.t[:], K["ropeS"][:, c0:c0 + 512], writes=[ropeS.b])
                for oc in range(5):
                    a = pq[qi_ % 2]
                    bq = pqp[qi_ % 2]
                    u1 = t1[qi_ % 2]
                    u2 = t2[qi_ % 2]
                    qi_ += 1
                    for k in range(8):
                        p.op('pe', lambda e, k=k, oc=oc, a=a, nT=nT, ncol=ncol: e.matmul(a.t[:, 0:ncol], wb.t[:, k, 640 + 128 * oc: 768 + 128 * oc], nT.t[:, k, 0:ncol],
                                                                                   start=(k == 0), stop=(k == 7)), reads=[nT.b, wb.b], writes=[a.b])
                    dstb = QT.b if oc < 4 else KT.b
                    dst = QT.t[:, oc, c0:c0 + ncol] if oc < 4 else KT.t[:, c0:c0 + ncol]
                    if blk < 8:
                        for k in range(8):
                            p.op('pe', lambda e, k=k, oc=oc, bq=bq, nT=nT, ncol=ncol: e.matmul(bq.t[:, 0:ncol], wb.t[:, k, 1280 + 128 * oc: 1408 + 128 * oc], nT.t[:, k, 0:ncol],
                                                                                        start=(k == 0), stop=(k == 7)), reads=[nT.b, wb.b], writes=[bq.b])
                        p.op('dve', lambda e, a=a, u1=u1, ropeC=ropeC: e.tensor_mul(u1.t[:, :], a.t[:, :], ropeC.t[:, :]), reads=[a.b, ropeC.b], writes=[u1.b])
                        p.op('dve', lambda e, bq=bq, u2=u2, ropeS=ropeS: e.tensor_mul(u2.t[:, :], bq.t[:, :], ropeS.t[:, :]), reads=[bq.b, ropeS.b], writes=[u2.b])
                        p.op('pool', lambda e, dst=dst, u1=u1, u2=u2: e.tensor_add(dst, u1.t[:, :], u2.t[:, :]), reads=[u1.b, u2.b], writes=[dstb])
                    else:
                        p.op('dve', lambda e, dst=dst, a=a, ncol=ncol: e.tensor_copy(dst, a.t[:, 0:ncol]), reads=[a.b], writes=[dstb])
            p.barrier()
        with ExitStack() as st:
            maskP = sbt(st, "maskP", [128, 512], BF16)
            maskN = sbt(st, "maskN", [128, 512], BF16)
            p.dma(maskP.t[:], K["maskP"], writes=[maskP.b])
            p.dma(maskN.t[:], K["maskN"], writes=[maskN.b])
            sk = sbt(st, "sk", [128, 8], F32)
            p.dma(sk.t[:], I["attn_sink"].partition_broadcast(128), writes=[sk.b])
            p.op('act', lambda e: e.activation(sk.t[:], sk.t[:], AF.Exp), reads=[sk.b], writes=[sk.b])
            skf = sbt(st, "skf", [128, 512], F32)
            for g in range(2):
                for hh in range(4):
                    p.op('dve', lambda e, g=g, hh=hh: e.tensor_copy(skf.t[64 * g:64 * g + 64, hh * 128:(hh + 1) * 128],
                                                                  sk.t[64 * g:64 * g + 64, 4 * g + hh:4 * g + hh + 1].to_broadcast([64, 128])), reads=[sk.b], writes=[skf.b])
            pS = [pst(st, f"pS{i}", [128, 512]) for i in range(3)]
            pO = [pst(st, f"pO{i}", [128, 512]) for i in range(2)]
            pZ = [pst(st, f"pZ{i}", [128, 512]) for i in range(2)]
            PT = [sbt(st, f"PT{i}", [128, 512], BF16) for i in range(3)]
            den = [sbt(st, f"den{i}", [128, 512], F32) for i in range(2)]
            ATt = [sbt(st, f"ATt{i}", [128, 4, 128], BF16) for i in range(2)]
            OB = [[Buf() for _ in range(2)] for _ in range(2)]
            ZB = [[Buf() for _ in range(2)] for _ in range(2)]
            DB = [[Buf() for _ in range(2)] for _ in range(2)]
            si = 0
            pend = {"t": None}
            for qi in range(NT):
                AT = ATt[qi % 2]
                for g in range(2):
                    lo, hi = 64 * g, 64 * g + 64
                    keys = [(32, None), (33, None)]
                    if qi < 32:
                        if qi > 0:
                            keys.append((qi - 1, maskP))
                        keys.append((qi, None))
                        if qi < 31:
                            keys.append((qi + 1, maskN))
                    O = pO[qi % 2]
                    Z = pZ[qi % 2]
                    Ob = OB[qi % 2][g]
                    Zb = ZB[qi % 2][g]
                    Db = DB[qi % 2][g]
                    SP = []
                    for ki in range(len(keys)):
                        SP.append((pS[si % 3], PT[si % 3]))
                        si += 1

                    def emitS(ki):
                        kt, msk = keys[ki]
                        S_ = SP[ki][0]
                        p.op('pe', lambda e, S_=S_, kt=kt, qi=qi, lo=lo, hi=hi, msk=msk: e.matmul(
                            S_.t[:, :].rearrange("p (j q) -> p j q", j=4), KT.t[lo:hi, kt * 128:(kt + 1) * 128], QT.t[lo:hi, :, qi * 128:(qi + 1) * 128],
                            start=True, stop=(msk is None)), reads=[KT.b, QT.b], writes=[S_.b])
                        if msk is not None:
                            p.op('pe', lambda e, S_=S_, msk=msk: e.matmul(S_.t[:, :], identb.t[:, :], msk.t[:, :], start=False, stop=True),
                                 reads=[identb.b, msk.b], writes=[S_.b])
                    emitS(0)
                    if len(keys) > 1:
                        emitS(1)
                    for ki, (kt, msk) in enumerate(keys):
                        S_, P_ = SP[ki]
                        p.op('act', lambda e, S_=S_, P_=P_: e.activation(P_.t[:, :], S_.t[:, :], AF.Exp, scale=0.125), reads=[S_.b], writes=[P_.b])
                        if ki == 1 and pend["t"] is not None:
                            pend["t"]()
                            pend["t"] = None
                        if ki + 2 < len(keys):
                            emitS(ki + 2)
                        first = ki == 0
                        last = ki == len(keys) - 1
                        p.op('pe', lambda e, O=O, kt=kt, lo=lo, hi=hi, P_=P_, first=first, last=last: e.matmul(
                            O.t[lo:hi, :], Vm.t[:, kt, lo:hi], P_.t[:, :], start=first, stop=last), reads=[Vm.b, P_.b], writes=[Ob])
                        p.op('pe', lambda e, Z=Z, lo=lo, hi=hi, P_=P_, first=first, last=last: e.matmul(
                            Z.t[lo:hi, :], ones_b.t[:, 0:64], P_.t[:, :], start=first, stop=last), reads=[ones_b.b, P_.b], writes=[Zb])
                    dn = den[qi % 2]

                    def tail(dn=dn, Z=Z, O=O, lo=lo, hi=hi, AT=AT, Zb=Zb, Ob=Ob, Db=Db, g=g, qi=qi):
                        p.op('dve', lambda e: e.tensor_add(dn.t[lo:hi, :], Z.t[lo:hi, :], skf.t[lo:hi, :]), reads=[Zb, skf.b], writes=[Db])
                        p.op('act', lambda e: e.activation(dn.t[lo:hi, :], dn.t[lo:hi, :], AF.Ln), reads=[Db], writes=[Db])
                        p.op('act', lambda e: e.activation(dn.t[lo:hi, :], dn.t[lo:hi, :], AF.Exp, scale=-1.0), reads=[Db], writes=[Db])
                        p.op('dve', lambda e: e.tensor_mul(
                            AT.t[lo:hi, :, :], O.t[lo:hi, :].rearrange("p (j q) -> p j q", j=4),
                            dn.t[lo:hi, :].rearrange("p (j q) -> p j q", j=4)), reads=[Ob, Db], writes=[AT.b])
                        if g == 1:
                            p.dma(ATs[qi], AT.t[:], reads=[AT.b])
                    pend["t"] = tail
            if pend["t"] is not None:
                pend["t"]()
                pend["t"] = None
            p.barrier()

    with ExitStack() as st:
        wob = sbt(st, "wob", [128, 8, D], BF16)
        wst = [sbt(st, f"wost{i}", [128, D], F32) for i in range(2)]
        for j in range(8):
            s_ = wst[j % 2]
            if j < 4:
                p.dma(s_.t[:], I["mix_w_out"][j * 128:(j + 1) * 128, :], writes=[s_.b])
            else:
                jj = j - 4
                p.dma(s_.t[0:64, :], I["mix_w_out"][512 + 64 * jj:512 + 64 * jj + 64, :], writes=[s_.b])
                p.dma(s_.t[64:128, :], I["mix_w_out"][512 + 64 * (4 + jj):512 + 64 * (4 + jj) + 64, :], writes=[s_.b])
            ew(lambda e, j=j, s_=s_: e.tensor_copy(wob.t[:, j, :], s_.t[:]), [s_.b], [wob.b])
        Cc = sbt(st, "Cc", [128, 128], BF16)
        Sc = sbt(st, "Sc", [128, 128], BF16)
        p.dma(Cc.t[:], K["Cc"], writes=[Cc.b])
        p.dma(Sc.t[:], K["Sc"], writes=[Sc.b])
        CTb = [sbt(st, f"CTb{i}", [128, 32, 256], BF16) for i in range(2)]
        STb = [sbt(st, f"STb{i}", [128, 32, 256], BF16) for i in range(2)]
        pP = [pst(st, f"pP{i}", [128, 256]) for i in range(2)]
        pQ = [pst(st, f"pQ{i}", [128, 256]) for i in range(2)]
        pY = pst(st, "pY", [128, 256])
        pOo = [pst(st, f"pOo{i}", [128, 512]) for i in range(2)]
        Pb = [sbt(st, f"Pb{i}", [128, 256], BF16) for i in range(2)]
        Qb = [sbt(st, f"Qb{i}", [128, 256], BF16) for i in range(2)]
        YT = [sbt(st, f"YT{i}", [128, 4, 256], BF16) for i in range(2)]
        xres = [sbt(st, f"xres{i}", [128, D], F32) for i in range(2)]
        tmpo = [sbt(st, f"tmpo{i}", [128, 512], F32) for i in range(2)]
        hn = [sbt(st, f"hn{i}", [128, D], F32) for i in range(2)]
        ATl = [sbt(st, f"ATl{i}", [128, 4, 128], BF16) for i in range(2)]
        g1l = [load_gate(st, "g1lat", 0, 0, 0), load_gate(st, "g1ctx", 0, 1, 0)]
        cnt_ = 0
        oi = 0
        for kb in range(17):
            lat = kb < 16
            if lat:
                cb = CTb[kb % 2]
                sb_ = STb[kb % 2]
                for h4 in range(4):
                    p.dma(cb.t[:, h4 * 8:(h4 + 1) * 8, :], K["CT"][kb, :, h4 * 8:(h4 + 1) * 8, :], writes=[cb.b])
                    p.dma(sb_.t[:, h4 * 8:(h4 + 1) * 8, :], K["ST"][kb, :, h4 * 8:(h4 + 1) * 8, :], writes=[sb_.b])
                nti = 32
                t0 = 0
            else:
                cb = CTb[kb % 2]
                sb_ = STb[kb % 2]
                p.dma(cb.t[:, 0:2, :], K["C256"], writes=[cb.b])
                p.dma(sb_.t[:, 0:2, :], K["S256"], writes=[sb_.b])
                nti = 2
                t0 = 32
            Y = YT[kb % 2]
            for cc in range(4):
                a = pP[cnt_ % 2]
                b = pQ[cnt_ % 2]
                ab = Pb[cnt_ % 2]
                bb = Qb[cnt_ % 2]
                cnt_ += 1
                for i in range(nti):
                    p.op('pe', lambda e, a=a, i=i, cc=cc, cb=cb, t0=t0, nti=nti: e.matmul(a.t[:, :], Fm.t[:, t0 + i, cc * 128:(cc + 1) * 128], cb.t[:, i, :],
                                                                                     start=(i == 0), stop=(i == nti - 1)), reads=[Fm.b, cb.b], writes=[a.b])
                for i in range(nti):
                    p.op('pe', lambda e, b=b, i=i, cc=cc, sb_=sb_, t0=t0, nti=nti: e.matmul(b.t[:, :], Fm.t[:, t0 + i, cc * 128:(cc + 1) * 128], sb_.t[:, i, :],
                                                                                      start=(i == 0), stop=(i == nti - 1)), reads=[Fm.b, sb_.b], writes=[b.b])
                p.op('act', lambda e, a=a, ab=ab: e.activation(ab.t[:, :], a.t[:, :], AF.Identity), reads=[a.b], writes=[ab.b])
                p.op('act', lambda e, b=b, bb=bb: e.activation(bb.t[:, :], b.t[:, :], AF.Identity, scale=-1.0), reads=[b.b], writes=[bb.b])
                p.op('pe', lambda e, ab=ab: e.matmul(pY.t[:, :], Cc.t[:, :], ab.t[:, :], start=True, stop=False), reads=[Cc.b, ab.b], writes=[pY.b])
                p.op('pe', lambda e, bb=bb: e.matmul(pY.t[:, :], Sc.t[:, :], bb.t[:, :], start=False, stop=True), reads=[Sc.b, bb.b], writes=[pY.b])
                p.op('dve', lambda e, Y=Y, cc=cc: e.tensor_copy(Y.t[:, cc, :], pY.t[:, :]), reads=[pY.b], writes=[Y.b])
            for tt in range(2):
                ti = (kb * 2 + tt) if lat else 32 + tt
                col = 0 if lat else 1
                xr = xres[ti % 2]
                h_ = hn[ti % 2]
                p.dma(xr.t[:], tile_src(ti), writes=[xr.b])
                AT = ATl[ti % 2]
                p.dma(AT.t[:], ATs[ti], writes=[AT.b])
                for half in range(2):
                    po = pOo[oi % 2]
                    tm = tmpo[oi % 2]
                    oi += 1
                    for j in range(8):
                        lhs = Y.t[:, j, tt * 128:(tt + 1) * 128] if j < 4 else AT.t[:, j - 4, :]
                        rb = Y.b if j < 4 else AT.b
                        p.op('pe', lambda e, po=po, lhs=lhs, j=j, half=half: e.matmul(po.t[:, :], lhs, wob.t[:, j, half * 512:(half + 1) * 512],
                                                                                 start=(j == 0), stop=(j == 7)), reads=[rb, wob.b], writes=[po.b])
                    p.op('dve', lambda e, po=po, tm=tm, half=half, col=col: e.tensor_mul(tm.t[:, :], po.t[:, :], g1l[col].t[:, half * 512:(half + 1) * 512]),
                         reads=[po.b, g1l[col].b], writes=[tm.b])
                    p.op('pool', lambda e, tm=tm, xr=xr, h_=h_, half=half: e.tensor_add(h_.t[:, half * 512:(half + 1) * 512], xr.t[:, half * 512:(half + 1) * 512], tm.t[:, :]),
                         reads=[tm.b, xr.b], writes=[h_.b])
                p.dma(Hs[ti * 128:(ti + 1) * 128, :], h_.t[:], reads=[h_.b])
        p.barrier()
    L0.close()

    if stage == 1:
        for ti in range(32):
            pass
        with ExitStack() as st:
            bb_ = [sbt(st, f"bo{i}", [128, D], F32) for i in range(2)]
            for ti in range(32):
                b_ = bb_[ti % 2]
                p.dma(b_.t[:], Hs[ti * 128:(ti + 1) * 128, :], writes=[b_.b])
                p.dma(out[ti * 128:(ti + 1) * 128, :], b_.t[:], reads=[b_.b])
        p.emit()
        top.close()
        return nc

    def ffn(l, tiles, final):
        with ExitStack() as st:
            Wg = sbt(st, "Wg", [128, 8, FF], BF16)
            Wu = sbt(st, "Wu", [128, 8, FF], BF16)
            Wd = sbt(st, "Wd", [128, NFC, D], BF16)
            with ExitStack() as st2:
                stg = [sbt(st2, f"fst{i}", [128, FF], F32) for i in range(4)]
                si_ = 0
                for nm, Wt in (("ffn_w_gate", Wg), ("ffn_w_up", Wu)):
                    for k in range(8):
                        s_ = stg[si_ % 4]
                        si_ += 1
                        p.dma(s_.t[:, 0:1408], I[nm][l, k * 128:(k + 1) * 128, 0:1408], writes=[s_.b])
                        p.dma(s_.t[:, 1408:FF], I[nm][l, k * 128:(k + 1) * 128, 1408:FF], writes=[s_.b])
                        p.op('dve', lambda e, Wt=Wt, k=k, s_=s_: e.tensor_copy(Wt.t[:, k, 0:1408], s_.t[:, 0:1408]), reads=[s_.b], writes=[Wt.b])
                        p.op('act', lambda e, Wt=Wt, k=k, s_=s_: e.activation(Wt.t[:, k, 1408:FF], s_.t[:, 1408:FF], AF.Identity), reads=[s_.b], writes=[Wt.b])
                for fc in range(0, NFC, 2):
                    s_ = stg[si_ % 4]
                    si_ += 1
                    for q2 in range(2):
                        p.dma(s_.t[:, q2 * D:(q2 + 1) * D], I["ffn_w_down"][l, (fc + q2) * 128:(fc + q2 + 1) * 128, :], writes=[s_.b])
                    p.op('dve' if (fc // 2) % 2 == 0 else 'act', (lambda e, fc=fc, s_=s_: e.tensor_copy(Wd.t[:, fc:fc + 2, :].rearrange("p a d -> p (a d)"), s_.t[:, 0:2 * D])) if (fc // 2) % 2 == 0 else (lambda e, fc=fc, s_=s_: e.activation(Wd.t[:, fc:fc + 2, :].rearrange("p a d -> p (a d)"), s_.t[:, 0:2 * D], AF.Identity)), reads=[s_.b], writes=[Wd.b])
                p.barrier()
            cn = make_norm(st, "f", nxt=0)
            g2l = [load_gate(st, "g2lat", l, 0, 1)] + ([load_gate(st, "g2ctx", l, 1, 1)] if not final else [])
            hx = [sbt(st, f"hx{i}", [128, D], F32) for i in range(3)]
            nTf = [sbt(st, f"nTf{i}", [128, 8, 256], BF16) for i in range(2)]
            actT = [sbt(st, f"actT{i}", [128, NFC, 256], BF16) for i in range(1)]
            sg = [sbt(st, f"sg{i}", [128, 256], F32) for i in range(2)]
            pgt = [pst(st, f"fpg{i}", [128, 256]) for i in range(2)]
            put = [pst(st, f"fpu{i}", [128, 256]) for i in range(2)]
            pdn = [pst(st, f"fpd{i}", [128, 512]) for i in range(2)]
            tm_ = [sbt(st, f"ftm{i}", [128, 512], F32) for i in range(2)]
            ho = [sbt(st, f"ho{i}", [128, D], F32) for i in range(2)]
            fss = [sbt(st, f"fss{i}", [128, 1], F32) for i in range(2)]
            fj = cn["junk"]
            fingt = None
            if final:
                fingt = sbt(st, "fing", [128, D], F32)
                p.dma(fingt.t[:], I["final_g"].partition_broadcast(128), writes=[fingt.b])
            gi_ = 0
            di_ = 0
            hi_ = 0
            for b0 in range(0, len(tiles), 2):
                tl = tiles[b0:b0 + 2]
                nT = nTf[(b0 // 2) % 2]
                aT = actT[0]
                xts = []
                for tt, ti in enumerate(tl):
                    col = 0 if ti < 32 else 1
                    hxt = hx[hi_ % 3]
                    hi_ += 1
                    src_ = Hs[ti * 128:(ti + 1) * 128, :]
                    norm_tile(cn, src_, l, col, 1, lambda k, tt=tt, nT=nT: nT.t[:, k, tt * 128:(tt + 1) * 128], nT.b, xt_fixed=hxt)
                    xts.append(hxt)
                for fc in range(NFC):
                    a = pgt[gi_ % 2]
                    b = put[gi_ % 2]
                    s2 = sg[gi_ % 2]
                    gi_ += 1
                    for k in range(8):
                        p.op('pe', lambda e, a=a, k=k, fc=fc, nT=nT: e.matmul(a.t[:, :], Wg.t[:, k, fc * 128:(fc + 1) * 128], nT.t[:, k, :], start=(k == 0), stop=(k == 7)),
                             reads=[Wg.b, nT.b], writes=[a.b])
                    for k in range(8):
                        p.op('pe', lambda e, b=b, k=k, fc=fc, nT=nT: e.matmul(b.t[:, :], Wu.t[:, k, fc * 128:(fc + 1) * 128], nT.t[:, k, :], start=(k == 0), stop=(k == 7)),
                             reads=[Wu.b, nT.b], writes=[b.b])
                    p.op('act', lambda e, a=a, s2=s2: e.activation(s2.t[:, :], a.t[:, :], AF.Silu), reads=[a.b], writes=[s2.b])
                    p.op('dve', lambda e, b=b, s2=s2, aT=aT, fc=fc: e.tensor_mul(aT.t[:, fc, :], s2.t[:, :], b.t[:, :]), reads=[b.b, s2.b], writes=[aT.b])
                for tt, ti in enumerate(tl):
                    col = 0 if ti < 32 else 1
                    hxt = xts[tt]
                    h_ = ho[ti % 2]
                    for half in range(2):
                        po = pdn[di_ % 2]
                        tm = tm_[di_ % 2]
                        di_ += 1
                        for fc in range(NFC):
                            p.op('pe', lambda e, po=po, fc=fc, tt=tt, half=half, aT=aT: e.matmul(po.t[:, :], aT.t[:, fc, tt * 128:(tt + 1) * 128], Wd.t[:, fc, half * 512:(half + 1) * 512],
                                                                                         start=(fc == 0), stop=(fc == NFC - 1)), reads=[aT.b, Wd.b], writes=[po.b])
                        p.op('dve', lambda e, po=po, tm=tm, half=half, col=col: e.tensor_mul(tm.t[:, :], po.t[:, :], g2l[col].t[:, half * 512:(half + 1) * 512]),
                             reads=[po.b, g2l[col].b], writes=[tm.b])
                        p.op('pool', lambda e, tm=tm, hxt=hxt, h_=h_, half=half: e.tensor_add(h_.t[:, half * 512:(half + 1) * 512], hxt.t[:, half * 512:(half + 1) * 512], tm.t[:, :]),
                             reads=[tm.b, hxt.b], writes=[h_.b])
                    if not final:
                        p.dma(Hs[ti * 128:(ti + 1) * 128, :], h_.t[:], reads=[h_.b])
                    else:
                        ss = fss[ti % 2]
                        p.op('act', lambda e, h_=h_, ss=ss: e.activation(fj.t[:], h_.t[:], AF.Square, accum_out=ss.t[:]), reads=[h_.b], writes=[fj.b, ss.b])
                        p.op('dve', lambda e, ss=ss: e.tensor_scalar(ss.t[:], ss.t[:], 1.0 / D, EPS, ALU.mult, ALU.add), reads=[ss.b], writes=[ss.b])
                        p.op('act', lambda e, ss=ss: e.activation(ss.t[:], ss.t[:], AF.Sqrt), reads=[ss.b], writes=[ss.b])
                        p.op('dve', lambda e, ss=ss: e.reciprocal(ss.t[:], ss.t[:]), reads=[ss.b], writes=[ss.b])
                        p.op('dve', lambda e, h_=h_, ss=ss: e.scalar_tensor_tensor(h_.t[:], h_.t[:], ss.t[:, 0:1], fingt.t[:], ALU.mult, ALU.mult),
                             reads=[h_.b, ss.b, fingt.b], writes=[h_.b])
                        p.dma(out[ti * 128:(ti + 1) * 128, :], h_.t[:], reads=[h_.b])
            p.barrier()

    ffn(0, list(range(34)), False)

    if stage == 2:
        with ExitStack() as st:
            bb_ = [sbt(st, f"bo{i}", [128, D], F32) for i in range(2)]
            for ti in range(32):
                b_ = bb_[ti % 2]
                p.dma(b_.t[:], Hs[ti * 128:(ti + 1) * 128, :], writes=[b_.b])
                p.dma(out[ti * 128:(ti + 1) * 128, :], b_.t[:], reads=[b_.b])
        p.emit()
        top.close()
        return nc

    s5_layer(nc, p, I, K, Hs, AB, load_gate, identf, identb, sbt, pst, ew, make_norm, norm_tile)
    if stage == 3:
        with ExitStack() as st:
            bb_ = [sbt(st, f"bo3{i}", [128, D], F32) for i in range(2)]
            for ti in range(32):
                b_ = bb_[ti % 2]
                p.dma(b_.t[:], Hs[ti * 128:(ti + 1) * 128, :], writes=[b_.b])
                p.dma(out[ti * 128:(ti + 1) * 128, :], b_.t[:], reads=[b_.b])
        p.emit()
        top.close()
        return nc
    ffn(1, list(range(16)), True)
    p.emit()
    top.close()
    return nc


def rev(ap):
    apl = [list(a) for a in ap.ap]
    n = apl[-1][1]
    stp = apl[-1][0]
    apl[-1][0] = -stp
    return bass.AP(ap.tensor, ap.offset + (n - 1) * stp, apl)


def bcast_last(ap, n):
    apl = [list(a) for a in ap.ap] + [[0, n]]
    return bass.AP(ap.tensor, ap.offset, apl)


def s5_layer(nc, p, I, K, Hs, AB, load_gate, identf, identb, sbt, pst, ew, make_norm, norm_tile, dbg=None):
    Hloc = Hs
    YGs = nc.dram_tensor("YGs", [8, 128, T // 2], BF16).ap()
    NTOK = T + TC
    with ExitStack() as S:
        nT = sbt(S, "s5nT", [128, 8, NTOK], BF16)
        with ExitStack() as st:
            cn = make_norm(st, "s5n", nxt=3)
            for ti in range(34):
                col = 0 if ti < 32 else 1
                norm_tile(cn, Hs[ti * 128:(ti + 1) * 128, :], 1, col, 0, lambda k, ti=ti: nT.t[:, k, ti * 128:(ti + 1) * 128], nT.b)
            p.barrier()
        PRM = sbt(S, "s5prm", [128, 3, 64], F32)
        Cw = sbt(S, "s5Cw", [128, 64, 2, 32], F32)
        Bz = sbt(S, "s5Bz", [128, 2, 64, 2, 16], F32)
        LC = 4
        NCM = 512 // LC
        PW = sbt(S, "s5PW", [128, LC + 1, 2, 64], F32)
        PRM4 = sbt(S, "s5prm4", [128, 3, 64], F32)
        dT = sbt(S, "s5dT", [128, 8], F32)
        with ExitStack() as st:
            pt = pst(st, "s5pt", [128, 128])
            def V(nm):
                return sbt(st, "s5v_" + nm, [128, 64], F32)
            arow = sbt(st, "s5arow", [64, 2, 128], F32)
            p.dma(arow.t[:, 0, :], I["ssm_a_re"].rearrange("(dq g) p -> dq (g p)", g=2), writes=[arow.b])
            p.dma(arow.t[:, 1, :], I["ssm_a_im"].rearrange("(dq g) p -> dq (g p)", g=2), writes=[arow.b])
            are, aim = V("are"), V("aim")
            for i_, dst in enumerate((are, aim)):
                p.op('pe', lambda e, i_=i_: e.transpose(pt.t[:, 0:64], arow.t[:, i_, :], identf.t[0:64, 0:64]), reads=[arow.b, identf.b], writes=[pt.b])
                p.op('dve', lambda e, dst=dst: e.tensor_copy(dst.t[:], pt.t[:, 0:64]), reads=[pt.b], writes=[dst.b])
            drow = sbt(st, "s5drow", [8, 128], F32)
            p.dma(drow.t[:], I["ssm_d"], writes=[drow.b])
            p.op('pe', lambda e: e.transpose(pt.t[:, 0:8], drow.t[:, :], identf.t[0:8, 0:8]), reads=[drow.b, identf.b], writes=[pt.b])
            p.op('dve', lambda e: e.tensor_copy(dT.t[:], pt.t[:, 0:8]), reads=[pt.b], writes=[dT.b])
            ldb = sbt(st, "s5ldb", [128, 128], F32)
            p.dma(ldb.t[:], I["ssm_log_dt"].partition_broadcast(128), writes=[ldb.b])
            dt = V("dt")
            for g2 in range(2):
                p.op('dve', lambda e, g2=g2: e.tensor_copy(dt.t[64 * g2:64 * g2 + 64, :], ldb.t[64 * g2:64 * g2 + 64, g2:128:2]), reads=[ldb.b], writes=[dt.b])
            p.op('act', lambda e: e.activation(dt.t[:], dt.t[:], AF.Exp), reads=[dt.b], writes=[dt.b])
            xr, th, mag = V("xr"), V("th"), V("mag")
            p.op('dve', lambda e: e.tensor_mul(xr.t[:], are.t[:], dt.t[:]), reads=[are.b, dt.b], writes=[xr.b])
            p.op('dve', lambda e: e.tensor_mul(th.t[:], aim.t[:], dt.t[:]), reads=[aim.b, dt.b], writes=[th.b])
            p.op('act', lambda e: e.activation(mag.t[:], xr.t[:], AF.Exp), reads=[xr.b], writes=[mag.b])
            kf = V("kf")
            ki = sbt(st, "s5ki", [128, 64], mybir.dt.int32)
            p.op('dve', lambda e: e.tensor_scalar(kf.t[:], th.t[:], 1.0 / (2 * math.pi), None, ALU.mult), reads=[th.b], writes=[kf.b])
            p.op('dve', lambda e: e.tensor_copy(ki.t[:], kf.t[:]), reads=[kf.b], writes=[ki.b])
            p.op('dve', lambda e: e.tensor_copy(kf.t[:], ki.t[:]), reads=[ki.b], writes=[kf.b])
            C1 = 6.28125
            C2 = 2 * math.pi - 6.28125
            thm = V("thm")
            p.op('dve', lambda e: e.scalar_tensor_tensor(thm.t[:], kf.t[:], -C1, th.t[:], ALU.mult, ALU.add), reads=[kf.b, th.b], writes=[thm.b])
            p.op('dve', lambda e: e.scalar_tensor_tensor(thm.t[:], kf.t[:], -C2, thm.t[:], ALU.mult, ALU.add), reads=[kf.b, thm.b], writes=[thm.b])
            xq, u2, qs, qc = V("xq"), V("u2"), V("qs"), V("qc")
            p.op('dve', lambda e: e.tensor_scalar(xq.t[:], thm.t[:], 0.25, None, ALU.mult), reads=[thm.b], writes=[xq.b])
            p.op('dve', lambda e: e.tensor_mul(u2.t[:], xq.t[:], xq.t[:]), reads=[xq.b], writes=[u2.b])
            sc_ = [(-1.0) ** k / math.factorial(2 * k + 1) for k in range(9)]
            cc_ = [(-1.0) ** k / math.factorial(2 * k) for k in range(9)]
            p.op('dve', lambda e: e.tensor_scalar(qs.t[:], u2.t[:], sc_[8], None, ALU.mult), reads=[u2.b], writes=[qs.b])
            p.op('dve', lambda e: e.tensor_scalar(qc.t[:], u2.t[:], cc_[8], None, ALU.mult), reads=[u2.b], writes=[qc.b])
            for k in range(7, 0, -1):
                p.op('dve', lambda e, k=k: e.scalar_tensor_tensor(qs.t[:], qs.t[:], sc_[k], u2.t[:], ALU.add, ALU.mult), reads=[qs.b, u2.b], writes=[qs.b])
                p.op('dve', lambda e, k=k: e.scalar_tensor_tensor(qc.t[:], qc.t[:], cc_[k], u2.t[:], ALU.add, ALU.mult), reads=[qc.b, u2.b], writes=[qc.b])
            sn, cs = V("sn"), V("cs")
            p.op('dve', lambda e: e.scalar_tensor_tensor(sn.t[:], qs.t[:], 1.0, xq.t[:], ALU.add, ALU.mult), reads=[qs.b, xq.b], writes=[sn.b])
            p.op('dve', lambda e: e.tensor_scalar(cs.t[:], qc.t[:], 1.0, None, ALU.add), reads=[qc.b], writes=[cs.b])
            ta, tb_ = V("ta"), V("tb")
            for _ in range(2):
                p.op('dve', lambda e: e.tensor_mul(ta.t[:], cs.t[:], cs.t[:]), reads=[cs.b], writes=[ta.b])
                p.op('dve', lambda e: e.tensor_mul(tb_.t[:], sn.t[:], sn.t[:]), reads=[sn.b], writes=[tb_.b])
                p.op('dve', lambda e: e.scalar_tensor_tensor(sn.t[:], cs.t[:], 2.0, sn.t[:], ALU.mult, ALU.mult), reads=[cs.b, sn.b], writes=[sn.b])
                p.op('dve', lambda e: e.tensor_sub(cs.t[:], ta.t[:], tb_.t[:]), reads=[ta.b, tb_.b], writes=[cs.b])
            p.op('dve', lambda e: e.tensor_copy(PRM.t[:, 0, :], mag.t[:]), reads=[mag.b], writes=[PRM.b])
            p.op('dve', lambda e: e.tensor_copy(PRM.t[:, 1, :], cs.t[:]), reads=[cs.b], writes=[PRM.b])
            p.op('dve', lambda e: e.tensor_copy(PRM.t[:, 2, :], sn.t[:]), reads=[sn.b], writes=[PRM.b])
            p.op('pool', lambda e: e.memset(PW.t[:, 0, 0, :], 1.0), writes=[PW.b])
            p.op('pool', lambda e: e.memset(PW.t[:, 0, 1, :], 0.0), writes=[PW.b])
            p.op('dve', lambda e: e.tensor_mul(PW.t[:, 1, 0, :], mag.t[:], cs.t[:]), reads=[mag.b, cs.b], writes=[PW.b])
            p.op('dve', lambda e: e.tensor_mul(PW.t[:, 1, 1, :], mag.t[:], sn.t[:]), reads=[mag.b, sn.b], writes=[PW.b])
            for k in range(2, LC + 1):
                p.op('dve', lambda e, k=k: e.tensor_mul(ta.t[:], PW.t[:, k - 1, 0, :], PW.t[:, 1, 0, :]), reads=[PW.b], writes=[ta.b])
                p.op('dve', lambda e, k=k: e.tensor_mul(tb_.t[:], PW.t[:, k - 1, 1, :], PW.t[:, 1, 1, :]), reads=[PW.b], writes=[tb_.b])
                p.op('dve', lambda e, k=k: e.tensor_sub(PW.t[:, k, 0, :], ta.t[:], tb_.t[:]), reads=[ta.b, tb_.b], writes=[PW.b])
                p.op('dve', lambda e, k=k: e.tensor_mul(ta.t[:], PW.t[:, k - 1, 0, :], PW.t[:, 1, 1, :]), reads=[PW.b], writes=[ta.b])
                p.op('dve', lambda e, k=k: e.tensor_mul(tb_.t[:], PW.t[:, k - 1, 1, :], PW.t[:, 1, 0, :]), reads=[PW.b], writes=[tb_.b])
                p.op('dve', lambda e, k=k: e.tensor_add(PW.t[:, k, 1, :], ta.t[:], tb_.t[:]), reads=[ta.b, tb_.b], writes=[PW.b])
            c4, s4, r4 = V("c4"), V("s4"), V("r4")
            p.op('dve', lambda e: e.tensor_copy(c4.t[:], cs.t[:]), reads=[cs.b], writes=[c4.b])
            p.op('dve', lambda e: e.tensor_copy(s4.t[:], sn.t[:]), reads=[sn.b], writes=[s4.b])
            p.op('dve', lambda e: e.tensor_copy(r4.t[:], mag.t[:]), reads=[mag.b], writes=[r4.b])
            for _ in range(int(round(math.log2(LC)))):
                p.op('dve', lambda e: e.tensor_mul(ta.t[:], c4.t[:], c4.t[:]), reads=[c4.b], writes=[ta.b])
                p.op('dve', lambda e: e.tensor_mul(tb_.t[:], s4.t[:], s4.t[:]), reads=[s4.b], writes=[tb_.b])
                p.op('dve', lambda e: e.scalar_tensor_tensor(s4.t[:], c4.t[:], 2.0, s4.t[:], ALU.mult, ALU.mult), reads=[c4.b, s4.b], writes=[s4.b])
                p.op('dve', lambda e: e.tensor_sub(c4.t[:], ta.t[:], tb_.t[:]), reads=[ta.b, tb_.b], writes=[c4.b])
                p.op('dve', lambda e: e.tensor_mul(r4.t[:], r4.t[:], r4.t[:]), reads=[r4.b], writes=[r4.b])
            p.op('dve', lambda e: e.tensor_copy(PRM4.t[:, 0, :], r4.t[:]), reads=[r4.b], writes=[PRM4.b])
            p.op('dve', lambda e: e.tensor_copy(PRM4.t[:, 1, :], c4.t[:]), reads=[c4.b], writes=[PRM4.b])
            p.op('dve', lambda e: e.tensor_copy(PRM4.t[:, 2, :], s4.t[:]), reads=[s4.b], writes=[PRM4.b])
            nr, ni, den, fre, fim = V("nr"), V("ni"), V("den"), V("fre"), V("fim")
            p.op('dve', lambda e: e.tensor_mul(nr.t[:], mag.t[:], cs.t[:]), reads=[mag.b, cs.b], writes=[nr.b])
            p.op('dve', lambda e: e.tensor_scalar(nr.t[:], nr.t[:], -1.0, None, ALU.add), reads=[nr.b], writes=[nr.b])
            p.op('dve', lambda e: e.tensor_mul(ni.t[:], mag.t[:], sn.t[:]), reads=[mag.b, sn.b], writes=[ni.b])
            p.op('dve', lambda e: e.tensor_mul(den.t[:], are.t[:], are.t[:]), reads=[are.b], writes=[den.b])
            p.op('dve', lambda e: e.tensor_mul(ta.t[:], aim.t[:], aim.t[:]), reads=[aim.b], writes=[ta.b])
            p.op('dve', lambda e: e.tensor_add(den.t[:], den.t[:], ta.t[:]), reads=[den.b, ta.b], writes=[den.b])
            p.op('dve', lambda e: e.reciprocal(den.t[:], den.t[:]), reads=[den.b], writes=[den.b])
            p.op('dve', lambda e: e.tensor_mul(ta.t[:], nr.t[:], are.t[:]), reads=[nr.b, are.b], writes=[ta.b])
            p.op('dve', lambda e: e.tensor_mul(tb_.t[:], ni.t[:], aim.t[:]), reads=[ni.b, aim.b], writes=[tb_.b])
            p.op('dve', lambda e: e.tensor_add(fre.t[:], ta.t[:], tb_.t[:]), reads=[ta.b, tb_.b], writes=[fre.b])
            p.op('dve', lambda e: e.tensor_mul(fre.t[:], fre.t[:], den.t[:]), reads=[fre.b, den.b], writes=[fre.b])
            p.op('dve', lambda e: e.tensor_mul(ta.t[:], ni.t[:], are.t[:]), reads=[ni.b, are.b], writes=[ta.b])
            p.op('dve', lambda e: e.tensor_mul(tb_.t[:], nr.t[:], aim.t[:]), reads=[nr.b, aim.b], writes=[tb_.b])
            p.op('dve', lambda e: e.tensor_sub(fim.t[:], ta.t[:], tb_.t[:]), reads=[ta.b, tb_.b], writes=[fim.b])
            p.op('dve', lambda e: e.tensor_mul(fim.t[:], fim.t[:], den.t[:]), reads=[fim.b, den.b], writes=[fim.b])
            Br = [sbt(st, f"s5Br{i}", [128, 64, 16], F32) for i in range(2)]
            for i_, nm in enumerate(("ssm_b_re", "ssm_b_im")):
                src = I[nm].rearrange("d (q g) p h -> g p (d q) h", g=2)
                for g2 in range(2):
                    p.dma(Br[i_].t[64 * g2:64 * g2 + 64, :, :], src[g2], writes=[Br[i_].b])
            p.op('pool', lambda e: e.memset(Bz.t[:].rearrange("p a b c d -> p (a b c d)"), 0.0), writes=[Bz.b])
            m1 = sbt(st, "s5m1", [128, 64, 16], F32)
            m2 = sbt(st, "s5m2", [128, 64, 16], F32)
            fre_b = bcast_last(fre.t[:], 16)
            fim_b = bcast_last(fim.t[:], 16)
            p.op('dve', lambda e: e.tensor_mul(m1.t[:], Br[0].t[:], fre_b), reads=[Br[0].b, fre.b], writes=[m1.b])
            p.op('dve', lambda e: e.tensor_mul(m2.t[:], Br[1].t[:], fim_b), reads=[Br[1].b, fim.b], writes=[m2.b])
            for g2 in range(2):
                p.op('dve', lambda e, g2=g2: e.tensor_sub(Bz.t[64 * g2:64 * g2 + 64, 0, :, g2, :], m1.t[64 * g2:64 * g2 + 64], m2.t[64 * g2:64 * g2 + 64]),
                     reads=[m1.b, m2.b], writes=[Bz.b])
            p.op('dve', lambda e: e.tensor_mul(m1.t[:], Br[1].t[:], fre_b), reads=[Br[1].b, fre.b], writes=[m1.b])
            p.op('dve', lambda e: e.tensor_mul(m2.t[:], Br[0].t[:], fim_b), reads=[Br[0].b, fim.b], writes=[m2.b])
            for g2 in range(2):
                p.op('dve', lambda e, g2=g2: e.tensor_add(Bz.t[64 * g2:64 * g2 + 64, 1, :, g2, :], m1.t[64 * g2:64 * g2 + 64], m2.t[64 * g2:64 * g2 + 64]),
                     reads=[m1.b, m2.b], writes=[Bz.b])
            pts = [pst(st, f"s5ptb{i}", [128, 128]) for i in range(2)]
            n_ = 0
            p.op('pool', lambda e: e.memset(Cw.t[:].rearrange("p a b c -> p (a b c)"), 0.0), writes=[Cw.b])
            Cn = [sbt(st, f"s5Cn{i}", [128, 2, 64], F32) for i in range(2)]
            for ri, nm in enumerate(("ssm_c_re", "ssm_c_im")):
                for dQ in range(16):
                    d_, Q_ = dQ // 8, dQ % 8
                    cnb = Cn[n_ % 2]
                    pp = pts[n_ % 2]
                    n_ += 1
                    src = I[nm][d_, Q_ * 4 * 2:(Q_ * 4 + 4) * 2].rearrange("g h p -> (g h) p")
                    p.dma(cnb.t[:, 0, :], src, writes=[cnb.b])
                    p.dma(cnb.t[:, 1, :], src, writes=[cnb.b])
                    p.op('pe', lambda e, pp=pp, cnb=cnb: e.transpose(pp.t[:, :], cnb.t[:, :, :].rearrange("p a b -> p (a b)"), identf.t[:, :]),
                         reads=[cnb.b, identf.b], writes=[pp.b])
                    sgn = 1.0 if ri == 0 else -1.0
                    for g2 in range(2):
                        p.op('act', lambda e, pp=pp, g2=g2, ri=ri, dQ=dQ, sgn=sgn: e.activation(
                            Cw.t[64 * g2:64 * g2 + 64, dQ * 4:(dQ + 1) * 4, ri, 16 * g2:16 * g2 + 16],
                            pp.t[64 * g2:64 * g2 + 64, :].rearrange("p (q g h) -> p q g h", q=4, g=2)[:, :, g2, :], AF.Identity, scale=sgn),
                            reads=[pp.b], writes=[Cw.b])
            p.barrier()

        if dbg is not None:
            dbg(PRM, Cw, nT)
            return

        with ExitStack() as st:
            yacc = sbt(st, "s5yacc", [128, T // 2], F32)
            Ec = sbt(st, "s5Ec", [128, 8, NCM], F32)
            Es = sbt(st, "s5Es", [128, 8, NCM], F32)
            wc = sbt(st, "s5wc", [128, 8], F32)
            ws = sbt(st, "s5ws", [128, 8], F32)
            wt1 = sbt(st, "s5wt1", [128, 8], F32)
            wt2 = sbt(st, "s5wt2", [128, 8], F32)
            et1 = sbt(st, "s5et1", [128, 8, NCM // 2], F32)
            et2 = sbt(st, "s5et2", [128, 8, NCM // 2], F32)
            hp = sbt(st, "s5hp", [128, 8, 2], F32)
            ini = sbt(st, "s5ini", [128, 8, 2], F32)
            itmp = sbt(st, "s5itmp", [128, 8, 2], F32)
            nsth = sbt(st, "s5nsth", [128, 64], F32)
            p.op('dve', lambda e: e.tensor_scalar(nsth.t[:], PRM4.t[:, 2, :], -1.0, None, ALU.mult), reads=[PRM4.b], writes=[nsth.b])
            hpb = [Buf() for _ in range(8)]
            CwP = sbt(st, "s5CwP", [128, LC + 1, 8, 2, 32], F32)
            ctm = [sbt(st, f"s5ctm{i}", [128, 4, 32], F32) for i in range(2)]
            BP = sbt(st, "s5BP", [128, LC, 2, 2, 4, 32], F32)
            Win = sbt(st, "s5Win", [128, 2, 2, LC, 128], BF16)
            Mw = sbt(st, "s5Mw", [128, 2, LC, 32], BF16)
            Wout = sbt(st, "s5Wout", [128, 8, LC, 2, 32], BF16)
            pts = [pst(st, f"s5ptq{i}", [128, 128]) for i in range(1)]
            pk = pst(st, "s5pk", [128, 16, 32])
            NS = 8
            Wk_ = [[sbt(st, f"s5w{j}_{i}", [128, 2, NCM], F32) for i in range(3)] for j in range(NS)]
            Hb = [sbt(st, f"s5hb{j}", [128, 2, NCM + 4], BF16) for j in range(NS)]
            Esg = sbt(st, "s5Esg", [128, 8, 2, NCM], F32)

            def swap2(tb_, ncn):
                b1 = tb_.t[:, 1, 0:ncn]
                apl = [list(a_) for a_ in b1.ap]
                return bass.AP(b1.tensor, b1.offset, [apl[0], [-NCM, 2], apl[-1]])

            def bc2(ap2):
                apl = [list(a_) for a_ in ap2.ap]
                return bass.AP(ap2.tensor, ap2.offset, [apl[0], [0, 2], apl[-1]])
            pS = [pst(st, f"s5pS{j}", [128, 512]) for j in range(4)]
            py = [pst(st, f"s5py{i}", [128, LC, NCM]) for i in range(2)]
            ygb = [sbt(st, f"s5yg{i}", [128, 512], BF16) for i in range(2)]
            gtb = [sbt(st, f"s5gt{i}", [128, 512], F32) for i in range(2)]
            tn_ = 0
            blkc = 0
            yi_ = 0
            psi = 0
            for Q in range(8):
                for d_ in range(2):
                    us = slice(d_ * 32 + Q * 4, d_ * 32 + Q * 4 + 4)
                    ts_ = slice(d_ * 4, d_ * 4 + 4)
                    c0_ = Cw.t[:, us, 0, :]
                    c1_ = Cw.t[:, us, 1, :]
                    for k in range(LC + 1):
                        pr = bcast_last(PW.t[:, k, 0, us], 32)
                        pi_ = bcast_last(PW.t[:, k, 1, us], 32)
                        t1_, t2_ = ctm
                        p.op('dve', lambda e, t1_=t1_, c0_=c0_, pr=pr: e.tensor_mul(t1_.t[:], c0_, pr), reads=[Cw.b, PW.b], writes=[t1_.b])
                        p.op('pool', lambda e, t2_=t2_, c1_=c1_, pi_=pi_: e.tensor_mul(t2_.t[:], c1_, pi_), reads=[Cw.b, PW.b], writes=[t2_.b])
                        p.op('dve', lambda e, t1_=t1_, t2_=t2_, k=k, ts_=ts_: e.tensor_add(CwP.t[:, k, ts_, 0, :], t1_.t[:], t2_.t[:]), reads=[t1_.b, t2_.b], writes=[CwP.b])
                        p.op('dve', lambda e, t1_=t1_, c1_=c1_, pr=pr: e.tensor_mul(t1_.t[:], c1_, pr), reads=[Cw.b, PW.b], writes=[t1_.b])
                        p.op('pool', lambda e, t2_=t2_, c0_=c0_, pi_=pi_: e.tensor_mul(t2_.t[:], c0_, pi_), reads=[Cw.b, PW.b], writes=[t2_.b])
                        p.op('dve', lambda e, t1_=t1_, t2_=t2_, k=k, ts_=ts_: e.tensor_sub(CwP.t[:, k, ts_, 1, :], t1_.t[:], t2_.t[:]), reads=[t1_.b, t2_.b], writes=[CwP.b])
                    b0_ = Bz.t[:, 0, us, :, :].rearrange("p a b c -> p a (b c)")
                    b1_ = Bz.t[:, 1, us, :, :].rearrange("p a b c -> p a (b c)")
                    for s_ in range(LC):
                        k = LC - 1 - s_
                        pr = bcast_last(PW.t[:, k, 0, us], 32)
                        pi_ = bcast_last(PW.t[:, k, 1, us], 32)
                        t1_, t2_ = ctm
                        p.op('dve', lambda e, t1_=t1_, b0_=b0_, pr=pr: e.tensor_mul(t1_.t[:], b0_, pr), reads=[Bz.b, PW.b], writes=[t1_.b])
                        p.op('pool', lambda e, t2_=t2_, b1_=b1_, pi_=pi_: e.tensor_mul(t2_.t[:], b1_, pi_), reads=[Bz.b, PW.b], writes=[t2_.b])
                        p.op('dve', lambda e, t1_=t1_, t2_=t2_, s_=s_, d_=d_: e.tensor_sub(BP.t[:, s_, 0, d_, :, :], t1_.t[:], t2_.t[:]), reads=[t1_.b, t2_.b], writes=[BP.b])
                        p.op('dve', lambda e, t1_=t1_, b0_=b0_, pi_=pi_: e.tensor_mul(t1_.t[:], b0_, pi_), reads=[Bz.b, PW.b], writes=[t1_.b])
                        p.op('pool', lambda e, t2_=t2_, b1_=b1_, pr=pr: e.tensor_mul(t2_.t[:], b1_, pr), reads=[Bz.b, PW.b], writes=[t2_.b])
                        p.op('dve', lambda e, t1_=t1_, t2_=t2_, s_=s_, d_=d_: e.tensor_add(BP.t[:, s_, 1, d_, :, :], t1_.t[:], t2_.t[:]), reads=[t1_.b, t2_.b], writes=[BP.b])
                p.op('act', lambda e: e.activation(Wout.t[:].rearrange("p x t r h -> p t x r h"), CwP.t[:, 1:LC + 1, :, :, :], AF.Identity), reads=[CwP.b], writes=[Wout.b])
                for d_ in range(2):
                    for ri in range(2):
                        for s_ in range(LC):
                            pp = pts[0]
                            tn_ += 1
                            p.op('pe', lambda e, pp=pp, s_=s_, ri=ri, d_=d_: e.transpose(pp.t[:, :], BP.t[:, s_, ri, d_, :, :].rearrange("p a b -> p (a b)"), identf.t[:, :]),
                                 reads=[BP.b, identf.b], writes=[pp.b])
                            p.op('act', lambda e, pp=pp, s_=s_, ri=ri, d_=d_: e.activation(Win.t[:, d_, ri, s_, :], pp.t[:, :], AF.Identity), reads=[pp.b], writes=[Win.b])
                    for thf in range(LC // 4):
                        for tq in range(4):
                            tau = thf * 4 + tq
                            for ql in range(4):
                                tix = d_ * 4 + ql
                                for ri in range(2):
                                    p.op('pe', lambda e, d_=d_, tau=tau, tq=tq, ql=ql, tix=tix, ri=ri, us=slice(d_ * 32 + Q * 4, d_ * 32 + Q * 4 + 4): e.matmul(
                                        pk.t[:, tq * 4 + ql, :], Bz.t[:, ri, us, :, :].rearrange("p q a b -> p (q a b)"), CwP.t[:, tau, tix, ri, :],
                                        start=(ri == 0), stop=(ri == 1)), reads=[Bz.b, CwP.b], writes=[pk.b])
                        for ql in range(4):
                            p.op('dve', lambda e, d_=d_, ql=ql, thf=thf: e.tensor_copy(Mw.t[32 * ql:32 * ql + 32, d_, thf * 4:thf * 4 + 4, :], pk.t[32 * ql:32 * ql + 32, ql:16:4, :]),
                                 reads=[pk.b], writes=[Mw.b])
                for d_ in range(2):
                    cols = slice(d_ * 32 + Q * 4, d_ * 32 + Q * 4 + 4)
                    p.op('dve', lambda e, d_=d_, cols=cols: e.tensor_copy(wc.t[:, d_ * 4:d_ * 4 + 4], PRM4.t[:, 1, cols]), reads=[PRM4.b], writes=[wc.b])
                    p.op('dve', lambda e, d_=d_, cols=cols: e.tensor_copy(ws.t[:, d_ * 4:d_ * 4 + 4], PRM4.t[:, 2, cols]), reads=[PRM4.b], writes=[ws.b])
                p.op('pool', lambda e: e.memset(Ec.t[:, :, 0:1], 1.0), writes=[Ec.b])
                p.op('pool', lambda e: e.memset(Es.t[:, :, 0:1], 0.0), writes=[Es.b])
                n = 1
                while n < NCM:
                    wcb = bcast_last(wc.t[:, :], n)
                    wsb = bcast_last(ws.t[:, :], n)
                    p.op('dve', lambda e, n=n, wcb=wcb: e.tensor_mul(et1.t[:, :, 0:n], Ec.t[:, :, 0:n], wcb), reads=[Ec.b, wc.b], writes=[et1.b])
                    p.op('pool', lambda e, n=n, wsb=wsb: e.tensor_mul(et2.t[:, :, 0:n], Es.t[:, :, 0:n], wsb), reads=[Es.b, ws.b], writes=[et2.b])
                    p.op('dve', lambda e, n=n: e.tensor_sub(Ec.t[:, :, n:2 * n], et1.t[:, :, 0:n], et2.t[:, :, 0:n]), reads=[et1.b, et2.b], writes=[Ec.b])
                    p.op('dve', lambda e, n=n, wcb=wcb: e.tensor_mul(et1.t[:, :, 0:n], Es.t[:, :, 0:n], wcb), reads=[Es.b, wc.b], writes=[et1.b])
                    p.op('pool', lambda e, n=n, wsb=wsb: e.tensor_mul(et2.t[:, :, 0:n], Ec.t[:, :, 0:n], wsb), reads=[Ec.b, ws.b], writes=[et2.b])
                    p.op('dve', lambda e, n=n: e.tensor_add(Es.t[:, :, n:2 * n], et1.t[:, :, 0:n], et2.t[:, :, 0:n]), reads=[et1.b, et2.b], writes=[Es.b])
                    p.op('dve', lambda e: e.tensor_mul(wt1.t[:], wc.t[:], wc.t[:]), reads=[wc.b], writes=[wt1.b])
                    p.op('dve', lambda e: e.tensor_mul(wt2.t[:], ws.t[:], ws.t[:]), reads=[ws.b], writes=[wt2.b])
                    p.op('dve', lambda e: e.scalar_tensor_tensor(ws.t[:], wc.t[:], 2.0, ws.t[:], ALU.mult, ALU.mult), reads=[wc.b, ws.b], writes=[ws.b])
                    p.op('dve', lambda e: e.tensor_sub(wc.t[:], wt1.t[:], wt2.t[:]), reads=[wt1.b, wt2.b], writes=[wc.b])
                    n *= 2
                p.op('dve', lambda e: e.tensor_copy(Esg.t[:, :, 0, :], Es.t[:, :, :]), reads=[Es.b], writes=[Esg.b])
                p.op('dve', lambda e: e.tensor_scalar(Esg.t[:, :, 1, :], Es.t[:, :, :], -1.0, None, ALU.mult), reads=[Es.b], writes=[Esg.b])
                blist = []
                for d_ in range(2):
                    if d_ == 0:
                        blocks = [(T, TC, False)] + [(b * 512, 512, True) for b in range(4)]
                    else:
                        blocks = [(T, TC, False)] + [(b * 512, 512, False) for b in (7, 6, 5, 4)] + [(b * 512, 512, True) for b in (3, 2, 1, 0)]
                    for bi_, (c0, n, islat) in enumerate(blocks):
                        blist.append((d_, bi_, c0, n, islat))
                bctx = {}

                def front_pe(k, Q=Q):
                    d_, bi_, c0, n, islat = blist[k]
                    gk = Q * 18 + k
                    ncn = n // LC
                    sets = [(gk % 2) * 4 + ql for ql in range(4)]
                    rhs_all = []
                    for ql in range(4):
                        base = nT.t[32 * ql:32 * ql + 32, Q, c0:c0 + n]
                        apl = [list(a_) for a_ in base.ap]
                        lst = []
                        for s_ in range(LC):
                            ap2 = [list(a_) for a_ in apl]
                            if d_ == 0:
                                ap2[-1] = [apl[-1][0] * LC, ncn]
                                lst.append(bass.AP(base.tensor, base.offset + s_ * apl[-1][0], ap2))
                            else:
                                ap2[-1] = [-apl[-1][0] * LC, ncn]
                                lst.append(bass.AP(base.tensor, base.offset + (n - 1 - s_) * apl[-1][0], ap2))
                        rhs_all.append(lst)
                    pss = [pS[ql] for ql in range(4)]
                    for ri in range(2):
                        for s_ in range(LC):
                            for ql in range(4):
                                ps_ = pss[ql]
                                p.op('pe', lambda e, ps_=ps_, ri=ri, s_=s_, ql=ql, d_=d_, ncn=ncn, r_=rhs_all[ql][s_]: e.matmul(
                                    ps_.t[:, ri * 128:ri * 128 + ncn], Win.t[32 * ql:32 * ql + 32, d_, ri, s_, :], r_, start=(s_ == 0), stop=(s_ == LC - 1), tile_position=(32 * ql, 0)),
                                    reads=[Win.b, nT.b], writes=[ps_.b])
                    bctx[k] = (ncn, sets, rhs_all)

                def front_ev(k, Q=Q):
                    d_, bi_, c0, n, islat = blist[k]
                    ncn, sets, rhs_all = bctx[k]
                    pss = [pS[ql] for ql in range(4)]
                    for ql in range(4):
                        ps_ = pss[ql]
                        X, P1, P2 = Wk_[sets[ql]]
                        p.op('act', lambda e, X=X, ps_=ps_, ncn=ncn: e.activation(X.t[:, :, 0:ncn], ps_.t[:, 0:256].rearrange("p (r c) -> p r c", r=2)[:, :, 0:ncn], AF.Identity),
                             reads=[ps_.b], writes=[X.b])
                    for ql in range(4):
                        X, P1, P2 = Wk_[sets[ql]]
                        tix = d_ * 4 + ql
                        ecb = bc2(Ec.t[:, tix, 0:ncn])
                        p.op('dve', lambda e, X=X, P1=P1, ecb=ecb, ncn=ncn: e.tensor_mul(P1.t[:, :, 0:ncn], X.t[:, :, 0:ncn], ecb), reads=[X.b, Ec.b], writes=[P1.b])
                        p.op('pool', lambda e, X=X, P2=P2, tix=tix, ncn=ncn: e.tensor_mul(P2.t[:, :, 0:ncn], swap2(X, ncn), Esg.t[:, tix, :, 0:ncn]), reads=[X.b, Esg.b], writes=[P2.b])
                    for ql in range(4):
                        X, P1, P2 = Wk_[sets[ql]]
                        p.op('pool', lambda e, P1=P1, P2=P2, ncn=ncn: e.tensor_add(P1.t[:, :, 0:ncn], P1.t[:, :, 0:ncn], P2.t[:, :, 0:ncn]), reads=[P1.b, P2.b], writes=[P1.b])

                def back(k, Q=Q):
                    d_, bi_, c0, n, islat = blist[k]
                    gk = Q * 18 + k
                    ncn, sets, rhs_all = bctx.pop(k)
                    pyt = py[gk % 2]
                    for ql in range(4):
                        tix = d_ * 4 + ql
                        u = d_ * 32 + Q * 4 + ql
                        hb_ = hpb[tix]
                        hbs = Hb[sets[ql]]
                        if bi_ > 0:
                            p.op('act', lambda e, tix=tix, u=u: e.activation(itmp.t[:, tix, 0:1], hp.t[:, tix, 1:2], AF.Identity, scale=nsth.t[:, u:u + 1]), reads=[hb_, nsth.b], writes=[hb_])
                            p.op('act', lambda e, tix=tix, u=u: e.activation(ini.t[:, tix, 0:1], hp.t[:, tix, 0:1], AF.Identity, scale=PRM4.t[:, 1, u:u + 1], bias=itmp.t[:, tix, 0:1]),
                                 reads=[hb_, PRM4.b], writes=[hb_])
                            p.op('act', lambda e, tix=tix, u=u: e.activation(itmp.t[:, tix, 1:2], hp.t[:, tix, 0:1], AF.Identity, scale=PRM4.t[:, 2, u:u + 1]), reads=[hb_, PRM4.b], writes=[hb_])
                            p.op('act', lambda e, tix=tix, u=u: e.activation(ini.t[:, tix, 1:2], hp.t[:, tix, 1:2], AF.Identity, scale=PRM4.t[:, 1, u:u + 1], bias=itmp.t[:, tix, 1:2]),
                                 reads=[hb_, PRM4.b], writes=[hb_])
                            if islat:
                                p.op('act', lambda e, hbs=hbs, tix=tix: e.activation(hbs.t[:, :, 0], hp.t[:, tix, :], AF.Identity), reads=[hb_], writes=[hbs.b])
                    for ql in range(4):
                        X, P1, P2 = Wk_[sets[ql]]
                        tix = d_ * 4 + ql
                        u = d_ * 32 + Q * 4 + ql
                        hb_ = hpb[tix]
                        rb = PRM4.t[:, 0, u:u + 1].to_broadcast([128, ncn])
                        for ri in range(2):
                            if bi_ == 0:
                                init_, irds = 0.0, []
                            else:
                                init_, irds = ini.t[:, tix, ri:ri + 1], [hb_]
                            p.op('dve', lambda e, X=X, P1=P1, rb=rb, init_=init_, ri=ri, ncn=ncn: e.tensor_tensor_scan(X.t[:, ri, 0:ncn], rb, P1.t[:, ri, 0:ncn], init_, ALU.mult, ALU.add),
                                 reads=[P1.b, PRM4.b] + irds, writes=[X.b])
                    for ql in range(4):
                        X, P1, P2 = Wk_[sets[ql]]
                        tix = d_ * 4 + ql
                        ecb = bc2(Ec.t[:, tix, 0:ncn])
                        p.op('dve', lambda e, X=X, P1=P1, ecb=ecb, ncn=ncn: e.tensor_mul(P1.t[:, :, 0:ncn], X.t[:, :, 0:ncn], ecb), reads=[X.b, Ec.b], writes=[P1.b])
                        p.op('pool', lambda e, X=X, P2=P2, tix=tix, ncn=ncn: e.tensor_mul(P2.t[:, :, 0:ncn], swap2(X, ncn), Esg.t[:, tix, :, 0:ncn]), reads=[X.b, Esg.b], writes=[P2.b])
                    for ql in range(4):
                        X, P1, P2 = Wk_[sets[ql]]
                        p.op('dve', lambda e, P1=P1, P2=P2, ncn=ncn: e.tensor_sub(P1.t[:, :, 0:ncn], P1.t[:, :, 0:ncn], P2.t[:, :, 0:ncn]), reads=[P1.b, P2.b], writes=[P1.b])
                    for ql in range(4):
                        X, P1, P2 = Wk_[sets[ql]]
                        tix = d_ * 4 + ql
                        hb_ = hpb[tix]
                        hbs = Hb[sets[ql]]
                        p.op('act', lambda e, tix=tix, P1=P1, ncn=ncn: e.activation(hp.t[:, tix, :], P1.t[:, :, ncn - 1], AF.Identity), reads=[P1.b], writes=[hb_])
                        if islat:
                            p.op('act', lambda e, hbs=hbs, P1=P1, ncn=ncn: e.activation(hbs.t[:, :, 1:ncn], P1.t[:, :, 0:ncn - 1], AF.Identity), reads=[P1.b], writes=[hbs.b])
                    if islat:
                        for t_ in range(LC):
                            for s_ in range(t_ + 1):
                                for ql in range(4):
                                    p.op('pe', lambda e, pyt=pyt, ql=ql, t_=t_, s_=s_, d_=d_, ncn=ncn, r_=rhs_all[ql][s_]: e.matmul(
                                        pyt.t[32 * ql:32 * ql + 32, t_, 0:ncn], Mw.t[32 * ql:32 * ql + 32, d_, t_ - s_, :], r_, start=(s_ == 0 and t_ == 0), stop=False,
                                        tile_position=(32 * ql, 32 * ql)), reads=[Mw.b, nT.b], writes=[pyt.b])
                        for t_ in range(LC):
                            for ri in range(2):
                                for ql in range(4):
                                    tix = d_ * 4 + ql
                                    hh_ = Hb[sets[ql]]
                                    p.op('pe', lambda e, pyt=pyt, ql=ql, t_=t_, ri=ri, tix=tix, hh_=hh_, ncn=ncn: e.matmul(
                                        pyt.t[32 * ql:32 * ql + 32, t_, 0:ncn], Wout.t[:, tix, t_, ri, :], hh_.t[:, ri, 0:ncn], start=False, stop=(ri == 1 and t_ == LC - 1),
                                        tile_position=(0, 32 * ql)), reads=[Wout.b, hh_.b], writes=[pyt.b])

                def yevac(k, Q=Q):
                    d_, bi_, c0, n, islat = blist[k]
                    if not islat:
                        return
                    gk = Q * 18 + k
                    ncn = n // LC
                    pyt = py[gk % 2]
                    if d_ == 0:
                        yv = yacc.t[:, c0:c0 + n].rearrange("p (c t) -> p t c", t=LC)
                        nv = nT.t[:, Q, c0:c0 + n].rearrange("p (c t) -> p t c", t=LC)
                        p.op('dve', lambda e, pyt=pyt, yv=yv, nv=nv, Q=Q: e.scalar_tensor_tensor(yv, nv, dT.t[:, Q:Q + 1], pyt.t[:, :, :], ALU.mult, ALU.add),
                             reads=[nT.b, dT.b, pyt.b], writes=[yacc.b])
                    else:
                        base = yacc.t[:, c0:c0 + n]
                        apl = [list(a_) for a_ in base.ap]
                        stp = apl[-1][0]
                        yv = bass.AP(base.tensor, base.offset + (n - 1) * stp, apl[:-1] + [[-stp, LC], [-stp * LC, ncn]])
                        p.op('dve', lambda e, pyt=pyt, yv=yv: e.tensor_add(yv, yv, pyt.t[:, :, :]), reads=[pyt.b, yacc.b], writes=[yacc.b])

                NB_ = len(blist)
                front_pe(0)
                front_ev(0)
                if NB_ > 1:
                    front_pe(1)
                for k in range(NB_):
                    if k + 1 < NB_:
                        front_ev(k + 1)
                    if k + 2 < NB_:
                        front_pe(k + 2)
                    back(k)
                    if k >= 1:
                        yevac(k - 1)
                yevac(NB_ - 1)
                for b in range(4):
                    yg = ygb[b % 2]
                    gt = gtb[b % 2]
                    ysl = yacc.t[:, b * 512:(b + 1) * 512]
                    p.op('dve', lambda e, gt=gt, ysl=ysl: e.tensor_mul(gt.t[:, :], ysl, ysl), reads=[yacc.b], writes=[gt.b])
                    p.op('dve', lambda e, gt=gt: e.tensor_scalar(gt.t[:, :], gt.t[:, :], 0.044715, 1.0, ALU.mult, ALU.add), reads=[gt.b], writes=[gt.b])
                    p.op('dve', lambda e, gt=gt, ysl=ysl: e.tensor_mul(gt.t[:, :], gt.t[:, :], ysl), reads=[gt.b, yacc.b], writes=[gt.b])
                    p.op('act', lambda e, gt=gt: e.activation(gt.t[:, :], gt.t[:, :], AF.Sigmoid, scale=2.0 * math.sqrt(2.0 / math.pi)), reads=[gt.b], writes=[gt.b])
                    p.op('dve', lambda e, gt=gt, ysl=ysl, yg=yg: e.tensor_mul(yg.t[:, :], gt.t[:, :], ysl), reads=[gt.b, yacc.b], writes=[yg.b])
                    p.dma(YGs[Q, :, b * 512:(b + 1) * 512], yg.t[:, :], reads=[yg.b])
            p.barrier()
    with ExitStack() as st:
        Wgl = sbt(st, "s5Wgl", [128, 8, 2 * D], BF16)
        with ExitStack() as st2:
            stg = [sbt(st2, f"s5gst{i}", [128, 2 * D], F32) for i in range(2)]
            for k in range(8):
                s_ = stg[k % 2]
                p.dma(s_.t[:, 0:D], I["ssm_glu_w"][k * 128:(k + 1) * 128, 0:D], writes=[s_.b])
                p.dma(s_.t[:, D:2 * D], I["ssm_glu_w"][k * 128:(k + 1) * 128, D:2 * D], writes=[s_.b])
                ew(lambda e, k=k, s_=s_: e.tensor_copy(Wgl.t[:, k, 0:D], s_.t[:, 0:D]), [s_.b], [Wgl.b])
                ew(lambda e, k=k, s_=s_: e.tensor_copy(Wgl.t[:, k, D:2 * D], s_.t[:, D:2 * D]), [s_.b], [Wgl.b])
            p.barrier()
        g1 = load_gate(st, "s5g1", 1, 0, 0)
        ygt = [sbt(st, f"s5ygt{i}", [128, 8, 128], BF16) for i in range(2)]
        h2 = [sbt(st, f"s5h2{i}", [128, D], F32) for i in range(2)]
        h3 = [sbt(st, f"s5h3{i}", [128, D], F32) for i in range(2)]
        sg_ = [sbt(st, f"s5sg{i}", [128, 512], F32) for i in range(2)]
        pa = [pst(st, f"s5pa{i}", [128, 512]) for i in range(2)]
        pgl = [pst(st, f"s5pg{i}", [128, 512]) for i in range(2)]
        c_ = 0
        YGs4 = YGs.rearrange("q p (r t) -> q p r t", r=2)
        for ti in range(16):
            yt = ygt[ti % 2]
            p.dma(yt.t[:], YGs[:, :, ti * 128:(ti + 1) * 128].rearrange("q p t -> p q t"), writes=[yt.b])
            hh = h2[ti % 2]
            ho_ = h3[ti % 2]
            p.dma(hh.t[:], Hloc[ti * 128:(ti + 1) * 128, :], writes=[hh.b])
            for half in range(2):
                a = pa[c_ % 2]
                g = pgl[c_ % 2]
                sg2 = sg_[c_ % 2]
                c_ += 1
                for k in range(8):
                    p.op('pe', lambda e, a=a, k=k, yt=yt, half=half: e.matmul(a.t[:, :], yt.t[:, k, :], Wgl.t[:, k, half * 512:(half + 1) * 512], start=(k == 0), stop=(k == 7)),
                         reads=[yt.b, Wgl.b], writes=[a.b])
                for k in range(8):
                    p.op('pe', lambda e, g=g, k=k, yt=yt, half=half: e.matmul(g.t[:, :], yt.t[:, k, :], Wgl.t[:, k, D + half * 512:D + (half + 1) * 512], start=(k == 0), stop=(k == 7)),
                         reads=[yt.b, Wgl.b], writes=[g.b])
                p.op('act', lambda e, g=g, sg2=sg2: e.activation(sg2.t[:, :], g.t[:, :], AF.Sigmoid), reads=[g.b], writes=[sg2.b])
                p.op('dve', lambda e, a=a, sg2=sg2: e.tensor_mul(sg2.t[:, :], sg2.t[:, :], a.t[:, :]), reads=[a.b, sg2.b], writes=[sg2.b])
                p.op('pool', lambda e, sg2=sg2, half=half: e.tensor_mul(sg2.t[:, :], sg2.t[:, :], g1.t[:, half * 512:(half + 1) * 512]), reads=[sg2.b, g1.b], writes=[sg2.b])
                p.op('pool', lambda e, sg2=sg2, hh=hh, ho_=ho_, half=half: e.tensor_add(ho_.t[:, half * 512:(half + 1) * 512], hh.t[:, half * 512:(half + 1) * 512], sg2.t[:, :]),
                     reads=[sg2.b, hh.b], writes=[ho_.b])
            p.dma(Hloc[ti * 128:(ti + 1) * 128, :], ho_.t[:], reads=[ho_.b])
        p.barrier()


_CACHE = {}


def kernel(**inputs):
    stage = int(inputs.pop("_stage", 99))
    if "nc" not in _CACHE or _CACHE.get("stage") != stage:
        _CACHE["nc"] = build(stage)
        _CACHE["stage"] = stage
        _CACHE["consts"] = host_consts()
    nc = _CACHE["nc"]
    if "consts_rev" not in _CACHE:
        _CACHE["consts_rev"] = host_consts(rev=True)
    f = lambda a: np.ascontiguousarray(np.asarray(a, dtype=np.float32))
    in_maps = []
    for core in range(8):
        b = core // 2
        rv = (core % 2 == 1) and stage >= 99
        cs = _CACHE["consts_rev"] if rv else _CACHE["consts"]
        sd = (lambda a: np.asarray(a)[0][::-1]) if rv else (lambda a: np.asarray(a)[0])
        xb = np.asarray(inputs["x"][b])
        cb = np.asarray(inputs["ctx"][b])
        if rv:
            xb = xb[::-1]
            cb = cb[::-1]
        m = {
            "x": f(xb), "c": f(inputs["c"][b]).reshape(8, 128), "ctx": f(cb),
            "c_ctx": f(inputs["c_ctx"]).reshape(8, 128),
            "mod_w": f(inputs["mod_w"]), "mod_b": f(inputs["mod_b"]).reshape(2, 48, 128),
            "norm_g": f(inputs["norm_g"]).reshape(2, 2, 8, 128),
            "ffn_w_gate": f(inputs["ffn_w_gate"]), "ffn_w_up": f(inputs["ffn_w_up"]), "ffn_w_down": f(inputs["ffn_w_down"]),
            "mix_w_in": f(inputs["mix_w_in"][0]), "mix_w_out": f(inputs["mix_w_out"][0]), "attn_sink": f(inputs["attn_sink"]).reshape(1, 8),
            "ssm_a_re": f(sd(inputs["ssm_a_re"])).reshape(128, 64), "ssm_a_im": f(sd(inputs["ssm_a_im"])).reshape(128, 64),
            "ssm_log_dt": f(sd(inputs["ssm_log_dt"])).reshape(1, 128),
            "ssm_b_re": f(sd(inputs["ssm_b_re"])), "ssm_b_im": f(sd(inputs["ssm_b_im"])),
            "ssm_c_re": f(sd(inputs["ssm_c_re"])), "ssm_c_im": f(sd(inputs["ssm_c_im"])),
            "ssm_d": f(inputs["ssm_d"][0]).reshape(8, 128), "ssm_glu_w": f(inputs["ssm_glu_w"][0]), "final_g": f(inputs["final_g"]).reshape(1, D),
        }
        m.update(cs)
        m["rk"] = np.array([[0]], np.int32)
        in_maps.append(m)
    res = run_bass_kernel_spmd(nc, in_maps, core_ids=list(range(8)))
    if stage < 99:
        return np.stack([np.asarray(res.results[2 * b]["out"], dtype=np.float32) for b in range(4)], axis=0)
    outp = np.stack([np.concatenate([np.asarray(res.results[2 * b]["out"], dtype=np.float32),
                                     np.asarray(res.results[2 * b + 1]["out"], dtype=np.float32)[::-1]], axis=0) for b in range(4)], axis=0)
    return outp
```

```python
import math
import numpy as np
import ml_dtypes
import concourse.bass as bass
import concourse.mybir as mybir
from concourse.bass_utils import run_bass_kernel_spmd
from contextlib import ExitStack

F32 = mybir.dt.float32
BF16 = mybir.dt.bfloat16
AF = mybir.ActivationFunctionType
ALU = mybir.AluOpType
NPBF = ml_dtypes.bfloat16

D = 1024
T = 4096
TC = 256
FF = 2816
NFC = 22
EPS = 1e-6


class Buf:
    def __init__(self, name=""):
        self.name = name
        self.last_w = None
        self.readers = []


class Prog:
    ENG = ['pe', 'act', 'dve', 'pool', 'sp']

    def __init__(self, nc, ndma_sems=10):
        self.nc = nc
        self.ops = {e: [] for e in self.ENG}
        self.cnt = {e: 0 for e in self.ENG}
        self.known = {e: {} for e in self.ENG}
        self.es = ExitStack()
        self.sem = {e: self.es.enter_context(nc.semaphore('s_' + e)) for e in self.ENG}
        self.dma_sems = {e: [self.es.enter_context(nc.semaphore(f'd_{e}{i}')) for i in range(ndma_sems)]
                         for e in ['sp', 'act', 'pool']}
        self.dma_val = {e: [0] * ndma_sems for e in self.dma_sems}
        self.dma_rr = {e: 0 for e in self.dma_sems}
        self.semobj = {}
        for e in self.ENG:
            self.semobj[('c', e)] = self.sem[e]
        for e in self.dma_sems:
            for i, s in enumerate(self.dma_sems[e]):
                self.semobj[('d', e, i)] = s
        self.q = 0
        self.rank_ap = None

    def _waits(self, eng, toks):
        need = {}
        for t in toks:
            if t is None:
                continue
            k, v = t
            if k == ('c', eng) and eng == 'pe':
                continue
            if self.known[eng].get(k, 0) >= v:
                continue
            if need.get(k, 0) < v:
                need[k] = v
        for k, v in need.items():
            self.known[eng][k] = v
        return list(need.items())

    def _deps(self, reads, writes):
        toks = []
        for b in reads:
            toks.append(b.last_w)
        for b in writes:
            toks.append(b.last_w)
            toks.extend(b.readers)
        return toks

    def _commit(self, tok, reads, writes):
        for b in reads:
            b.readers.append(tok)
            if len(b.readers) > 64:
                b.readers = b.readers[-64:]
        for b in writes:
            b.last_w = tok
            b.readers = []

    def op(self, eng, fn, reads=(), writes=()):
        waits = self._waits(eng, self._deps(reads, writes))
        self.cnt[eng] += 1
        tok = (('c', eng), self.cnt[eng])
        self.ops[eng].append((waits, fn, (self.sem[eng], 1)))
        self._commit(tok, reads, writes)
        return tok

    def dma(self, out, in_, reads=(), writes=(), eng=None, **kw):
        if eng is None:
            eng = ['sp', 'act', 'pool'][self.q % 2]
            self.q += 1
        toks = self._deps(reads, writes)
        i = self.dma_rr[eng]
        self.dma_rr[eng] = (i + 1) % len(self.dma_sems[eng])
        key = ('d', eng, i)
        prev = self.dma_val[eng][i]
        if prev > 0:
            toks.append((key, prev))
        waits = self._waits(eng, toks)
        self.dma_val[eng][i] = prev + 16
        tok = (key, prev + 16)
        def issue(e, out=out, in_=in_, eng=eng):
            o = out(self.dyn[eng]) if callable(out) else out
            i2 = in_(self.dyn[eng]) if callable(in_) else in_
            return e.dma_start(out=o, in_=i2, **kw)
        self.ops[eng].append((waits, issue, (self.dma_sems[eng][i], 16)))
        self._commit(tok, reads, writes)
        return tok

    def all_tokens(self):
        allt = []
        for e in self.ENG:
            if self.cnt[e]:
                allt.append((('c', e), self.cnt[e]))
        for e in self.dma_sems:
            for i, v in enumerate(self.dma_val[e]):
                if v:
                    allt.append((('d', e, i), v))
        return allt

    def barrier(self):
        allt = self.all_tokens()
        for e in self.ENG:
            w = self._waits(e, allt)
            if w:
                self.ops[e].append((w, None, None))

    def emit(self):
        nc = self.nc
        fin = self._waits('sp', self.all_tokens())
        self.ops['sp'].append((fin, None, None))
        self.dyn = {}
        with nc.Block() as block:
            def mk(eng):
                def run(e):
                    for waits, fn, inc in self.ops[eng]:
                        for k, v in waits:
                            e.wait_ge(self.semobj[k], v)
                        if fn is not None:
                            fn(e).then_inc(inc[0], inc[1])

                def body(e):
                    if eng in ('sp', 'act') and self.rank_ap is not None:
                        with e.register("rk_" + eng) as reg:
                            e.reg_load(reg, self.rank_ap)
                            self.dyn[eng] = e.snap(reg, min_val=0, max_val=2048)
                            run(e)
                    else:
                        run(e)
                return body
            block.tensor(mk('pe'))
            block.scalar(mk('act'))
            block.vector(mk('dve'))
            block.gpsimd(mk('pool'))
            block.sync(mk('sp'))
        self.es.close()


class TB:
    def __init__(self, t, name=""):
        self.t = t
        self.b = Buf(name)


def host_consts(rev=False):
    cs = {}
    cs["identf"] = np.eye(128, dtype=np.float32)
    cs["identb"] = np.eye(128, dtype=np.float32).astype(NPBF)
    t = np.arange(T)
    row = (t // 64).astype(np.float64)
    col = (t % 64).astype(np.float64)
    nf = 16
    inv = 10000.0 ** (-np.arange(nf, dtype=np.float64) / nf)
    inv = inv.astype(np.float32).astype(np.float64)
    ang = np.concatenate([(row[:, None].astype(np.float32) * inv[None].astype(np.float32)),
                          (col[:, None].astype(np.float32) * inv[None].astype(np.float32))], axis=-1).astype(np.float32)
    cosv = np.cos(ang).astype(np.float32)
    sinv = np.sin(ang).astype(np.float32)
    C = np.zeros((128, T), np.float32)
    S = np.zeros((128, T), np.float32)
    for p in range(128):
        d = p % 64
        i = d // 2
        C[p] = cosv[:, i]
        S[p] = sinv[:, i] * (-1.0 if d % 2 == 0 else 1.0)
    if rev:
        C = np.ascontiguousarray(C[:, ::-1])
        S = np.ascontiguousarray(S[:, ::-1])
    cs["ropeC"] = C
    cs["ropeS"] = S
    j = np.arange(128)[:, None]
    i = np.arange(128)[None, :]
    mp = np.where(j >= i, 0.0, -30000.0).astype(np.float32)
    mn = np.where(j <= i, 0.0, -30000.0).astype(np.float32)
    cs["maskP"] = np.tile(mp, (1, 4)).astype(NPBF)
    cs["maskN"] = np.tile(mn, (1, 4)).astype(NPBF)
    tt = np.arange(T, dtype=np.int64)
    tk = (tt[:, None] * tt[None, :]) % T
    angT = 2.0 * np.pi * tk / T
    ct = (np.cos(angT) / math.sqrt(T)).astype(np.float32)
    stt = (np.sin(angT) / math.sqrt(T)).astype(np.float32)
    if rev:
        ct = np.ascontiguousarray(ct[::-1, ::-1])
        stt = np.ascontiguousarray(stt[::-1, ::-1])
    cs["CT"] = np.ascontiguousarray(ct.reshape(32, 128, 16, 256).transpose(2, 1, 0, 3)).astype(NPBF)
    cs["ST"] = np.ascontiguousarray(stt.reshape(32, 128, 16, 256).transpose(2, 1, 0, 3)).astype(NPBF)
    t2 = np.arange(TC, dtype=np.int64)
    a2 = 2.0 * np.pi * ((t2[:, None] * t2[None, :]) % TC) / TC
    c2 = (np.cos(a2) / math.sqrt(TC)).astype(np.float32)
    s2 = (np.sin(a2) / math.sqrt(TC)).astype(np.float32)
    if rev:
        c2 = np.ascontiguousarray(c2[::-1, ::-1])
        s2 = np.ascontiguousarray(s2[::-1, ::-1])
    cs["C256"] = np.ascontiguousarray(c2.reshape(2, 128, 256).transpose(1, 0, 2)).astype(NPBF)
    cs["S256"] = np.ascontiguousarray(s2.reshape(2, 128, 256).transpose(1, 0, 2)).astype(NPBF)
    c64 = np.arange(64)
    a3 = 2.0 * np.pi * ((c64[:, None] * c64[None, :]) % 64) / 64
    cc = np.zeros((128, 128), np.float32)
    sc = np.zeros((128, 128), np.float32)
    for g in range(2):
        cc[g * 64:(g + 1) * 64, g * 64:(g + 1) * 64] = np.cos(a3) / 8.0
        sc[g * 64:(g + 1) * 64, g * 64:(g + 1) * 64] = np.sin(a3) / 8.0
    cs["Cc"] = cc.astype(NPBF)
    cs["Sc"] = sc.astype(NPBF)
    return cs


CONST_SHAPES = {
    "identf": ([128, 128], F32), "identb": ([128, 128], BF16), "ropeC": ([128, T], F32), "ropeS": ([128, T], F32),
    "maskP": ([128, 512], BF16), "maskN": ([128, 512], BF16),
    "CT": ([16, 128, 32, 256], BF16), "ST": ([16, 128, 32, 256], BF16),
    "C256": ([128, 2, 256], BF16), "S256": ([128, 2, 256], BF16), "Cc": ([128, 128], BF16), "Sc": ([128, 128], BF16),
}

IN_SHAPES = {
    "x": [T, D], "c": [8, 128], "ctx": [TC, D], "c_ctx": [8, 128],
    "mod_w": [2, D, 6 * D], "mod_b": [2, 48, 128], "norm_g": [2, 2, 8, 128],
    "ffn_w_gate": [2, D, FF], "ffn_w_up": [2, D, FF], "ffn_w_down": [2, FF, D],
    "mix_w_in": [D, 1280], "mix_w_out": [D, D], "attn_sink": [1, 8],
    "ssm_a_re": [128, 64], "ssm_a_im": [128, 64], "ssm_log_dt": [1, 128],
    "ssm_b_re": [2, 64, 64, 16], "ssm_b_im": [2, 64, 64, 16], "ssm_c_re": [2, 64, 16, 64], "ssm_c_im": [2, 64, 16, 64],
    "ssm_d": [8, 128], "ssm_glu_w": [D, 2 * D], "final_g": [1, D],
}


def build(stage=99):
    nc = bass.Bass("TRN2", target_bir_lowering=False)
    I = {n: nc.dram_tensor(n, sh, F32, kind="ExternalInput").ap() for n, sh in IN_SHAPES.items()}
    K = {n: nc.dram_tensor(n, sh, dt, kind="ExternalInput").ap() for n, (sh, dt) in CONST_SHAPES.items()}
    rk_in = nc.dram_tensor("rk", [1, 1], mybir.dt.int32, kind="ExternalInput").ap()
    HT = T // 2
    out = nc.dram_tensor("out", [T if stage < 99 else HT, D], F32, kind="ExternalOutput").ap()
    Hs = nc.dram_tensor("Hs", [T + TC, D], F32).ap()
    mod_b_flat = I["mod_b"].rearrange("l j p -> l (j p)")

    p = Prog(nc)
    p.rank_ap = None
    top = ExitStack()
    Hloc = nc.dram_tensor("Hloc", [T // 2, D], F32).ap()
    p.Hloc = Hloc

    uid = {"n": 0}

    def sbt(st, name, shape, dt):
        uid["n"] += 1
        return TB(st.enter_context(nc.sbuf_tensor(f"s{uid['n']}_{name}", shape, dt)), name)

    def pst(st, name, shape, dt=F32):
        uid["n"] += 1
        return TB(st.enter_context(nc.psum_tensor(f"p{uid['n']}_{name}", shape, dt)), name)

    identf = sbt(top, "identf", [128, 128], F32)
    identb = sbt(top, "identb", [128, 128], BF16)
    p.dma(identf.t[:], K["identf"], writes=[identf.b])
    p.dma(identb.t[:], K["identb"], writes=[identb.b])
    ones_b = sbt(top, "ones_b", [128, 128], BF16)
    p.op('pool', lambda e: e.memset(ones_b.t[:], 1.0), writes=[ones_b.b])
    AB = sbt(top, "AB", [128, 2, 2, 4, 8], F32)
    Gs = nc.dram_tensor("Gs", [8, 128, D], F32).ap()
    ATs = nc.dram_tensor("ATs", [34, 128, 4, 128], BF16).ap()

    def load_gate(st, nm, l, col, gi):
        g = sbt(st, nm, [128, D], F32)
        p.dma(g.t[:], Gs[(l * 2 + col) * 2 + gi], writes=[g.b])
        return g
    rr = {"i": 0}

    def ew(fn, reads, writes, engs=('dve', 'pool')):
        e = engs[rr["i"] % len(engs)]
        rr["i"] += 1
        return p.op(e, fn, reads=reads, writes=writes)

    with ExitStack() as st:
        gates = sbt(st, "gates", [128, 2, 2, 2, D], F32)
        crow = sbt(st, "crow", [16, 128], F32)
        p.dma(crow.t[0:8, :], I["c"], writes=[crow.b])
        p.dma(crow.t[8:16, :], I["c_ctx"], writes=[crow.b])
        pT = pst(st, "pT", [128, 96])
        scT = sbt(st, "scT", [128, 16], F32)
        p.op('pe', lambda e: e.transpose(pT.t[:, 0:16], crow.t[:, :], identf.t[0:16, 0:16]), reads=[crow.b, identf.b], writes=[pT.b])
        p.op('act', lambda e: e.activation(scT.t[:], pT.t[:, 0:16], AF.Silu), reads=[pT.b], writes=[scT.b])
        scbc = sbt(st, "scbc", [128, 16, 128], F32)
        for ck in range(16):
            ew(lambda e, ck=ck: e.tensor_copy(scbc.t[:, ck, :], scT.t[:, ck:ck + 1].to_broadcast([128, 128])), [scT.b], [scbc.b])
        mbrow = sbt(st, "mbrow", [48, 2, 128], F32)
        ngrow = sbt(st, "ngrow", [32, 128], F32)
        p.dma(mbrow.t[:, 0, :], I["mod_b"][0], writes=[mbrow.b])
        p.dma(mbrow.t[:, 1, :], I["mod_b"][1], writes=[mbrow.b])
        p.dma(ngrow.t[:], I["norm_g"].rearrange("l i k p -> (l i k) p"), writes=[ngrow.b])
        mbT = sbt(st, "mbT", [128, 2, 48], F32)
        ngT = sbt(st, "ngT", [128, 32], F32)
        for l in range(2):
            p.op('pe', lambda e, l=l: e.transpose(pT.t[:, 0:48], mbrow.t[:, l, :], identf.t[0:48, 0:48]), reads=[mbrow.b, identf.b], writes=[pT.b])
            p.op('dve', lambda e, l=l: e.tensor_copy(mbT.t[:, l, :], pT.t[:, 0:48]), reads=[pT.b], writes=[mbT.b])
        p.op('pe', lambda e: e.transpose(pT.t[:, 0:32], ngrow.t[:, :], identf.t[0:32, 0:32]), reads=[ngrow.b, identf.b], writes=[pT.b])
        p.op('dve', lambda e: e.tensor_copy(ngT.t[:], pT.t[:, 0:32]), reads=[pT.b], writes=[ngT.b])
        macc = sbt(st, "macc", [128, 2, 48, 2], F32)
        Wk = [sbt(st, f"Wk{i}", [128, 6 * D], F32) for i in range(2)]
        pg = [pst(st, f"pg{i}", [128, 512]) for i in range(2)]
        pm = pst(st, "pm", [128, 96])
        it = 0
        for l in range(2):
            for k in range(8):
                w = Wk[it % 2]
                it += 1
                for q3 in range(3):
                    p.dma(w.t[:, q3 * 2048:(q3 + 1) * 2048], I["mod_w"][l, k * 128:(k + 1) * 128, q3 * 2048:(q3 + 1) * 2048], writes=[w.b])
                for j in range(48):
                    p.op('pe', lambda e, j=j, w=w, k=k: e.matmul(pm.t[:, 2 * j:2 * j + 2], w.t[:, j * 128:(j + 1) * 128],
                                                                 scT.t[:, k:16:8], start=True, stop=True),
                         reads=[w.b, scT.b], writes=[pm.b])
                if k == 0:
                    p.op('dve', lambda e, l=l: e.tensor_copy(macc.t[:, l].rearrange("p j c -> p (j c)"), pm.t[:, :]), reads=[pm.b], writes=[macc.b])
                else:
                    p.op('dve', lambda e, l=l: e.tensor_add(macc.t[:, l].rearrange("p j c -> p (j c)"), macc.t[:, l].rearrange("p j c -> p (j c)"), pm.t[:, :]),
                         reads=[pm.b, macc.b], writes=[macc.b])
                gi = 0
                for col in range(2):
                    for g_i, which in enumerate((2, 5)):
                        for half in range(2):
                            pgt = pg[gi % 2]
                            gi += 1
                            p.op('pe', lambda e, pgt=pgt, col=col, k=k, w=w, which=which, half=half: e.matmul(
                                pgt.t[:, :], scbc.t[:, col * 8 + k, :], w.t[:, which * D + half * 512: which * D + half * 512 + 512], start=True, stop=True),
                                reads=[scbc.b, w.b], writes=[pgt.b])
                            dst = gates.t[:, l, col, g_i, half * 512:(half + 1) * 512]
                            if k == 0:
                                p.op('dve', lambda e, dst=dst, pgt=pgt: e.tensor_copy(dst, pgt.t[:, :]), reads=[pgt.b], writes=[gates.b])
                            else:
                                p.op('dve', lambda e, dst=dst, pgt=pgt: e.tensor_add(dst, dst, pgt.t[:, :]), reads=[pgt.b, gates.b], writes=[gates.b])
        gb = sbt(st, "gb", [128, D], F32)
        for l in range(2):
            for col in range(2):
                p.op('dve', lambda e, l=l, col=col: e.tensor_add(macc.t[:, l, :, col], macc.t[:, l, :, col], mbT.t[:, l, :]), reads=[macc.b, mbT.b], writes=[macc.b])
            for g_i, which in enumerate((2, 5)):
                p.dma(gb.t[:], mod_b_flat[l:l + 1, which * D:(which + 1) * D].partition_broadcast(128), writes=[gb.b])
                for col in range(2):
                    p.op('dve', lambda e, l=l, col=col, g_i=g_i: e.tensor_add(gates.t[:, l, col, g_i, :], gates.t[:, l, col, g_i, :], gb.t[:]),
                         reads=[gb.b, gates.b], writes=[gates.b])
            for col in range(2):
                for i2 in range(2):
                    sh = macc.t[:, l, (3 * i2) * 8:(3 * i2) * 8 + 8, col]
                    scl = macc.t[:, l, (3 * i2 + 1) * 8:(3 * i2 + 1) * 8 + 8, col]
                    gn = ngT.t[:, (l * 2 + i2) * 8:(l * 2 + i2) * 8 + 8]
                    p.op('dve', lambda e, l=l, col=col, i2=i2, scl=scl, gn=gn: e.scalar_tensor_tensor(
                        AB.t[:, l, col, 2 * i2, :], scl, 1.0, gn, ALU.add, ALU.mult), reads=[macc.b, ngT.b], writes=[AB.b])
                    p.op('dve', lambda e, l=l, col=col, i2=i2, sh=sh: e.tensor_copy(AB.t[:, l, col, 2 * i2 + 1, :], sh), reads=[macc.b], writes=[AB.b])
        for l in range(2):
            for col in range(2):
                for g_i in range(2):
                    p.dma(Gs[(l * 2 + col) * 2 + g_i], gates.t[:, l, col, g_i, :], reads=[gates.b])
        p.barrier()

    if stage == 0:
        with ExitStack() as st:
            g0 = load_gate(st, "g0dbg", 0, 0, 0)
            p.dma(out[0:128, :], g0.t[:], reads=[g0.b])
        p.dma(out[128:256, 0:128], AB.t[:].rearrange("p a b c d -> p (a b c d)"), reads=[AB.b])
        p.emit()
        top.close()
        return nc

    def make_norm(st, nm, nxt=2):
        ctxn = {}
        ctxn["xt"] = [sbt(st, f"{nm}xt{i}", [128, D], F32) for i in range(nxt)]
        ctxn["junk"] = sbt(st, f"{nm}junk", [128, D], BF16)
        ctxn["xn"] = [sbt(st, f"{nm}xn{i}", [128, D], BF16) for i in range(2)]
        ctxn["ss"] = [sbt(st, f"{nm}ss{i}", [128, 1], F32) for i in range(3)]
        ctxn["tp"] = [pst(st, f"{nm}tp{i}", [128, 8, 128], BF16) for i in range(2)]
        ctxn["n"] = 0
        return ctxn

    def norm_tile(cn, src, l, col, which, dst_fn, dst_buf, xt_fixed=None, ident=None):
        n = cn["n"]
        cn["n"] += 1
        xt = xt_fixed if xt_fixed is not None else cn["xt"][n % len(cn["xt"])]
        ss = cn["ss"][n % 3]
        xn = cn["xn"][n % 2]
        tp = cn["tp"][n % 2]
        junk = cn["junk"]
        idt = ident if ident is not None else identb
        p.dma(xt.t[:], src, writes=[xt.b])
        p.op('act', lambda e: e.activation(junk.t[:], xt.t[:], AF.Square, accum_out=ss.t[:]), reads=[xt.b], writes=[junk.b, ss.b])
        p.op('dve', lambda e: e.tensor_scalar(ss.t[:], ss.t[:], 1.0 / D, EPS, ALU.mult, ALU.add), reads=[ss.b], writes=[ss.b])
        p.op('act', lambda e: e.activation(ss.t[:], ss.t[:], AF.Sqrt), reads=[ss.b], writes=[ss.b])
        p.op('dve', lambda e: e.reciprocal(ss.t[:], ss.t[:]), reads=[ss.b], writes=[ss.b])
        p.op('dve', lambda e: e.tensor_scalar(xn.t[:], xt.t[:], ss.t[:, 0:1], None, ALU.mult), reads=[xt.b, ss.b], writes=[xn.b])
        for k in range(8):
            p.op('pe', lambda e, k=k: e.transpose(tp.t[:, k, :], xn.t[:, k * 128:(k + 1) * 128], idt.t[:]), reads=[xn.b, idt.b], writes=[tp.b])
        for k in range(8):
            p.op('act', lambda e, k=k: e.activation(dst_fn(k), tp.t[:, k, :], AF.Identity,
                                                    scale=AB.t[:, l, col, 2 * which, k:k + 1], bias=AB.t[:, l, col, 2 * which + 1, k:k + 1]),
                 reads=[tp.b, AB.b], writes=[dst_buf])
        return xt, ss

    def tile_src(ti):
        return I["x"][ti * 128:(ti + 1) * 128, :] if ti < 32 else I["ctx"][(ti - 32) * 128:(ti - 31) * 128, :]

    NT = 34
    L0 = ExitStack()
    Fm = sbt(L0, "Fm", [128, NT, 512], BF16)
    with ExitStack() as stBC:
        QT = sbt(stBC, "QT", [128, 4, NT * 128], BF16)
        KT = sbt(stBC, "KT", [128, NT * 128], BF16)
        Vm = sbt(stBC, "Vm", [128, NT, 128], BF16)
        with ExitStack() as st:
            wb = sbt(st, "wb", [128, 8, 1920], BF16)
            wst = [sbt(st, f"wst{i}", [128, 1280], F32) for i in range(1)]
            for k in range(8):
                s_ = wst[0]
                p.dma(s_.t[:], I["mix_w_in"][k * 128:(k + 1) * 128, :], writes=[s_.b])
                S = s_.t
                W = wb.t
                ew(lambda e, k=k, S=S, W=W: e.tensor_copy(W[:, k, 0:512], S[:, 0:512]), [s_.b], [wb.b])
                ew(lambda e, k=k, S=S, W=W: e.tensor_copy(W[:, k, 512:640], S[:, 1152:1280]), [s_.b], [wb.b])
                ew(lambda e, k=k, S=S, W=W: e.tensor_copy(W[:, k, 640:1152].rearrange("p (j h d) -> p j h d", j=4, h=2),
                                                          S[:, 512:1024].rearrange("p (h j d) -> p j h d", h=2, j=4)), [s_.b], [wb.b])
                ew(lambda e, k=k, S=S, W=W: e.tensor_copy(W[:, k, 1152:1280], S[:, 1024:1152]), [s_.b], [wb.b])
                for two in range(2):
                    ew(lambda e, k=k, S=S, W=W, two=two: e.tensor_copy(
                        W[:, k, 1280:1792].rearrange("p (j h i t) -> p j h i t", j=4, h=2, t=2)[:, :, :, :, two],
                        S[:, 512:1024].rearrange("p (h j i t) -> p j h i t", h=2, j=4, t=2)[:, :, :, :, 1 - two]), [s_.b], [wb.b])
                    ew(lambda e, k=k, S=S, W=W, two=two: e.tensor_copy(
                        W[:, k, 1792:1920].rearrange("p (i t) -> p i t", t=2)[:, :, two],
                        S[:, 1024:1152].rearrange("p (i t) -> p i t", t=2)[:, :, 1 - two]), [s_.b], [wb.b])
            ropeCb = [sbt(st, f"ropeC{i}", [128, 512], F32) for i in range(2)]
            ropeSb = [sbt(st, f"ropeS{i}", [128, 512], F32) for i in range(2)]
            cn = make_norm(st, "b")
            nTb = [sbt(st, f"nTb{i}", [128, 8, 512], BF16) for i in range(2)]
            pf = pst(st, "pf", [128, 512])
            pv = pst(st, "pv", [128, 128])
            pq = [pst(st, f"pq{i}", [128, 512]) for i in range(2)]
            pqp = [pst(st, f"pqp{i}", [128, 512]) for i in range(2)]
            t1 = [sbt(st, f"t1{i}", [128, 512], F32) for i in range(2)]
            t2 = [sbt(st, f"t2{i}", [128, 512], F32) for i in range(2)]
            qi_ = 0
            for blk in range(9):
                ntile = 4 if blk < 8 else 2
                ncol = ntile * 128
                nT = nTb[blk % 2]
                col = 0 if blk < 8 else 1
                for tt in range(ntile):
                    ti = blk * 4 + tt
                    norm_tile(cn, tile_src(ti), 0, col, 0, lambda k, tt=tt, nT=nT: nT.t[:, k, tt * 128:(tt + 1) * 128], nT.b)
                    for k in range(8):
                        p.op('pe', lambda e, k=k, tt=tt, nT=nT: e.matmul(pf.t[:, :], nT.t[:, k, tt * 128:(tt + 1) * 128], wb.t[:, k, 0:512],
                                                                      start=(k == 0), stop=(k == 7)), reads=[nT.b, wb.b], writes=[pf.b])
                    p.op('act', lambda e, ti=ti: e.activation(Fm.t[:, ti, :], pf.t[:, :], AF.Identity), reads=[pf.b], writes=[Fm.b])
                    for k in range(8):
                        p.op('pe', lambda e, k=k, tt=tt, nT=nT: e.matmul(pv.t[:, :], nT.t[:, k, tt * 128:(tt + 1) * 128], wb.t[:, k, 512:640],
                                                                      start=(k == 0), stop=(k == 7)), reads=[nT.b, wb.b], writes=[pv.b])
                    p.op('dve', lambda e, ti=ti: e.tensor_copy(Vm.t[:, ti, :], pv.t[:, :]), reads=[pv.b], writes=[Vm.b])
                c0 = blk * 512
                ropeC = ropeCb[blk % 2]
                ropeS = ropeSb[blk % 2]
                if blk < 8:
                    p.dma(ropeC.t[:], K["ropeC"][:, c0:c0 + 512], writes=[ropeC.b])
                    p.dma(ropeS.t[:], K["ropeS"][:, c0:c0 + 512], writes=[ropeS.b])
                for oc in range(5):
                    a = pq[qi_ % 2]
                    bq = pqp[qi_ % 2]
                    u1 = t1[qi_ % 2]
                    u2 = t2[qi_ % 2]
                    qi_ += 1
                    for k in range(8):
                        p.op('pe', lambda e, k=k, oc=oc, a=a, nT=nT, ncol=ncol: e.matmul(a.t[:, 0:ncol], wb.t[:, k, 640 + 128 * oc: 768 + 128 * oc], nT.t[:, k, 0:ncol],
                                                                                   start=(k == 0), stop=(k == 7)), reads=[nT.b, wb.b], writes=[a.b])
                    dstb = QT.b if oc < 4 else KT.b
                    dst = QT.t[:, oc, c0:c0 + ncol] if oc < 4 else KT.t[:, c0:c0 + ncol]
                    if blk < 8:
                        for k in range(8):
                            p.op('pe', lambda e, k=k, oc=oc, bq=bq, nT=nT, ncol=ncol: e.matmul(bq.t[:, 0:ncol], wb.t[:, k, 1280 + 128 * oc: 1408 + 128 * oc], nT.t[:, k, 0:ncol],
                                                                                        start=(k == 0), stop=(k == 7)), reads=[nT.b, wb.b], writes=[bq.b])
                        p.op('dve', lambda e, a=a, u1=u1, ropeC=ropeC: e.tensor_mul(u1.t[:, :], a.t[:, :], ropeC.t[:, :]), reads=[a.b, ropeC.b], writes=[u1.b])
                        p.op('dve', lambda e, bq=bq, u2=u2, ropeS=ropeS: e.tensor_mul(u2.t[:, :], bq.t[:, :], ropeS.t[:, :]), reads=[bq.b, ropeS.b], writes=[u2.b])
                        p.op('pool', lambda e, dst=dst, u1=u1, u2=u2: e.tensor_add(dst, u1.t[:, :], u2.t[:, :]), reads=[u1.b, u2.b], writes=[dstb])
                    else:
                        p.op('dve', lambda e, dst=dst, a=a, ncol=ncol: e.tensor_copy(dst, a.t[:, 0:ncol]), reads=[a.b], writes=[dstb])
            p.barrier()
        with ExitStack() as st:
            maskP = sbt(st, "maskP", [128, 512], BF16)
            maskN = sbt(st, "maskN", [128, 512], BF16)
            p.dma(maskP.t[:], K["maskP"], writes=[maskP.b])
            p.dma(maskN.t[:], K["maskN"], writes=[maskN.b])
            sk = sbt(st, "sk", [128, 8], F32)
            p.dma(sk.t[:], I["attn_sink"].partition_broadcast(128), writes=[sk.b])
            p.op('act', lambda e: e.activation(sk.t[:], sk.t[:], AF.Exp), reads=[sk.b], writes=[sk.b])
            skf = sbt(st, "skf", [128, 512], F32)
            for g in range(2):
                for hh in range(4):
                    p.op('dve', lambda e, g=g, hh=hh: e.tensor_copy(skf.t[64 * g:64 * g + 64, hh * 128:(hh + 1) * 128],
                                                                  sk.t[64 * g:64 * g + 64, 4 * g + hh:4 * g + hh + 1].to_broadcast([64, 128])), reads=[sk.b], writes=[skf.b])
            pS = [pst(st, f"pS{i}", [128, 512]) for i in range(3)]
            pO = [pst(st, f"pO{i}", [128, 512]) for i in range(2)]
            pZ = [pst(st, f"pZ{i}", [128, 512]) for i in range(2)]
            PT = [sbt(st, f"PT{i}", [128, 512], BF16) for i in range(3)]
            den = [sbt(st, f"den{i}", [128, 512], F32) for i in range(2)]
            ATt = [sbt(st, f"ATt{i}", [128, 4, 128], BF16) for i in range(2)]
            OB = [[Buf() for _ in range(2)] for _ in range(2)]
            ZB = [[Buf() for _ in range(2)] for _ in range(2)]
            DB = [[Buf() for _ in range(2)] for _ in range(2)]
            si = 0
            pend = {"t": None}
            for qi in range(NT):
                AT = ATt[qi % 2]
                for g in range(2):
                    lo, hi = 64 * g, 64 * g + 64
                    keys = [(32, None), (33, None)]
                    if qi < 32:
                        if qi > 0:
                            keys.append((qi - 1, maskP))
                        keys.append((qi, None))
                        if qi < 31:
                            keys.append((qi + 1, maskN))
                    O = pO[qi % 2]
                    Z = pZ[qi % 2]
                    Ob = OB[qi % 2][g]
                    Zb = ZB[qi % 2][g]
                    Db = DB[qi % 2][g]
                    SP = []
                    for ki in range(len(keys)):
                        SP.append((pS[si % 3], PT[si % 3]))
                        si += 1

                    def emitS(ki):
                        kt, msk = keys[ki]
                        S_ = SP[ki][0]
                        p.op('pe', lambda e, S_=S_, kt=kt, qi=qi, lo=lo, hi=hi, msk=msk: e.matmul(
                            S_.t[:, :].rearrange("p (j q) -> p j q", j=4), KT.t[lo:hi, kt * 128:(kt + 1) * 128], QT.t[lo:hi, :, qi * 128:(qi + 1) * 128],
                            start=True, stop=(msk is None)), reads=[KT.b, QT.b], writes=[S_.b])
                        if msk is not None:
                            p.op('pe', lambda e, S_=S_, msk=msk: e.matmul(S_.t[:, :], identb.t[:, :], msk.t[:, :], start=False, stop=True),
                                 reads=[identb.b, msk.b], writes=[S_.b])
                    emitS(0)
                    if len(keys) > 1:
                        emitS(1)
                    for ki, (kt, msk) in enumerate(keys):
                        S_, P_ = SP[ki]
                        p.op('act', lambda e, S_=S_, P_=P_: e.activation(P_.t[:, :], S_.t[:, :], AF.Exp, scale=0.125), reads=[S_.b], writes=[P_.b])
                        if ki == 1 and pend["t"] is not None:
                            pend["t"]()
                            pend["t"] = None
                        if ki + 2 < len(keys):
                            emitS(ki + 2)
                        first = ki == 0
                        last = ki == len(keys) - 1
                        p.op('pe', lambda e, O=O, kt=kt, lo=lo, hi=hi, P_=P_, first=first, last=last: e.matmul(
                            O.t[lo:hi, :], Vm.t[:, kt, lo:hi], P_.t[:, :], start=first, stop=last), reads=[Vm.b, P_.b], writes=[Ob])
                        p.op('pe', lambda e, Z=Z, lo=lo, hi=hi, P_=P_, first=first, last=last: e.matmul(
                            Z.t[lo:hi, :], ones_b.t[:, 0:64], P_.t[:, :], start=first, stop=last), reads=[ones_b.b, P_.b], writes=[Zb])
                    dn = den[qi % 2]

                    def tail(dn=dn, Z=Z, O=O, lo=lo, hi=hi, AT=AT, Zb=Zb, Ob=Ob, Db=Db, g=g, qi=qi):
                        p.op('dve', lambda e: e.tensor_add(dn.t[lo:hi, :], Z.t[lo:hi, :], skf.t[lo:hi, :]), reads=[Zb, skf.b], writes=[Db])
                        p.op('act', lambda e: e.activation(dn.t[lo:hi, :], dn.t[lo:hi, :], AF.Ln), reads=[Db], writes=[Db])
                        p.op('act', lambda e: e.activation(dn.t[lo:hi, :], dn.t[lo:hi, :], AF.Exp, scale=-1.0), reads=[Db], writes=[Db])
                        p.op('dve', lambda e: e.tensor_mul(
                            AT.t[lo:hi, :, :], O.t[lo:hi, :].rearrange("p (j q) -> p j q", j=4),
                            dn.t[lo:hi, :].rearrange("p (j q) -> p j q", j=4)), reads=[Ob, Db], writes=[AT.b])
                        if g == 1:
                            p.dma(ATs[qi], AT.t[:], reads=[AT.b])
                    pend["t"] = tail
            if pend["t"] is not None:
                pend["t"]()
                pend["t"] = None
            p.barrier()

    with ExitStack() as st:
        wob = sbt(st, "wob", [128, 8, D], BF16)
        wst = [sbt(st, f"wost{i}", [128, D], F32) for i in range(2)]
        for j in range(8):
            s_ = wst[j % 2]
            if j < 4:
                p.dma(s_.t[:], I["mix_w_out"][j * 128:(j + 1) * 128, :], writes=[s_.b])
            else:
                jj = j - 4
                p.dma(s_.t[0:64, :], I["mix_w_out"][512 + 64 * jj:512 + 64 * jj + 64, :], writes=[s_.b])
                p.dma(s_.t[64:128, :], I["mix_w_out"][512 + 64 * (4 + jj):512 + 64 * (4 + jj) + 64, :], writes=[s_.b])
            ew(lambda e, j=j, s_=s_: e.tensor_copy(wob.t[:, j, :], s_.t[:]), [s_.b], [wob.b])
        Cc = sbt(st, "Cc", [128, 128], BF16)
        Sc = sbt(st, "Sc", [128, 128], BF16)
        p.dma(Cc.t[:], K["Cc"], writes=[Cc.b])
        p.dma(Sc.t[:], K["Sc"], writes=[Sc.b])
        CTb = [sbt(st, f"CTb{i}", [128, 32, 256], BF16) for i in range(2)]
        STb = [sbt(st, f"STb{i}", [128, 32, 256], BF16) for i in range(2)]
        pP = [pst(st, f"pP{i}", [128, 256]) for i in range(2)]
        pQ = [pst(st, f"pQ{i}", [128, 256]) for i in range(2)]
        pY = pst(st, "pY", [128, 256])
        pOo = [pst(st, f"pOo{i}", [128, 512]) for i in range(2)]
        Pb = [sbt(st, f"Pb{i}", [128, 256], BF16) for i in range(2)]
        Qb = [sbt(st, f"Qb{i}", [128, 256], BF16) for i in range(2)]
        YT = [sbt(st, f"YT{i}", [128, 4, 256], BF16) for i in range(2)]
        xres = [sbt(st, f"xres{i}", [128, D], F32) for i in range(2)]
        tmpo = [sbt(st, f"tmpo{i}", [128, 512], F32) for i in range(2)]
        hn = [sbt(st, f"hn{i}", [128, D], F32) for i in range(2)]
        ATl = [sbt(st, f"ATl{i}", [128, 4, 128], BF16) for i in range(2)]
        g1l = [load_gate(st, "g1lat", 0, 0, 0), load_gate(st, "g1ctx", 0, 1, 0)]
        cnt_ = 0
        oi = 0
        for kb in range(17):
            lat = kb < 16
            if lat:
                cb = CTb[kb % 2]
                sb_ = STb[kb % 2]
                for h4 in range(4):
                    p.dma(cb.t[:, h4 * 8:(h4 + 1) * 8, :], K["CT"][kb, :, h4 * 8:(h4 + 1) * 8, :], writes=[cb.b])
                    p.dma(sb_.t[:, h4 * 8:(h4 + 1) * 8, :], K["ST"][kb, :, h4 * 8:(h4 + 1) * 8, :], writes=[sb_.b])
                nti = 32
                t0 = 0
            else:
                cb = CTb[kb % 2]
                sb_ = STb[kb % 2]
                p.dma(cb.t[:, 0:2, :], K["C256"], writes=[cb.b])
                p.dma(sb_.t[:, 0:2, :], K["S256"], writes=[sb_.b])
                nti = 2
                t0 = 32
            Y = YT[kb % 2]
            for cc in range(4):
                a = pP[cnt_ % 2]
                b = pQ[cnt_ % 2]
                ab = Pb[cnt_ % 2]
                bb = Qb[cnt_ % 2]
                cnt_ += 1
                for i in range(nti):
                    p.op('pe', lambda e, a=a, i=i, cc=cc, cb=cb, t0=t0, nti=nti: e.matmul(a.t[:, :], Fm.t[:, t0 + i, cc * 128:(cc + 1) * 128], cb.t[:, i, :],
                                                                                     start=(i == 0), stop=(i == nti - 1)), reads=[Fm.b, cb.b], writes=[a.b])
                for i in range(nti):
                    p.op('pe', lambda e, b=b, i=i, cc=cc, sb_=sb_, t0=t0, nti=nti: e.matmul(b.t[:, :], Fm.t[:, t0 + i, cc * 128:(cc + 1) * 128], sb_.t[:, i, :],
                                                                                      start=(i == 0), stop=(i == nti - 1)), reads=[Fm.b, sb_.b], writes=[b.b])
                p.op('act', lambda e, a=a, ab=ab: e.activation(ab.t[:, :], a.t[:, :], AF.Identity), reads=[a.b], writes=[ab.b])
                p.op('act', lambda e, b=b, bb=bb: e.activation(bb.t[:, :], b.t[:, :], AF.Identity, scale=-1.0), reads=[b.b], writes=[bb.b])
                p.op('pe', lambda e, ab=ab: e.matmul(pY.t[:, :], Cc.t[:, :], ab.t[:, :], start=True, stop=False), reads=[Cc.b, ab.b], writes=[pY.b])
                p.op('pe', lambda e, bb=bb: e.matmul(pY.t[:, :], Sc.t[:, :], bb.t[:, :], start=False, stop=True), reads=[Sc.b, bb.b], writes=[pY.b])
                p.op('dve', lambda e, Y=Y, cc=cc: e.tensor_copy(Y.t[:, cc, :], pY.t[:, :]), reads=[pY.b], writes=[Y.b])
            for tt in range(2):
                ti = (kb * 2 + tt) if lat else 32 + tt
                col = 0 if lat else 1
                xr = xres[ti % 2]
                h_ = hn[ti % 2]
                p.dma(xr.t[:], tile_src(ti), writes=[xr.b])
                AT = ATl[ti % 2]
                p.dma(AT.t[:], ATs[ti], writes=[AT.b])
                for half in range(2):
                    po = pOo[oi % 2]
                    tm = tmpo[oi % 2]
                    oi += 1
                    for j in range(8):
                        lhs = Y.t[:, j, tt * 128:(tt + 1) * 128] if j < 4 else AT.t[:, j - 4, :]
                        rb = Y.b if j < 4 else AT.b
                        p.op('pe', lambda e, po=po, lhs=lhs, j=j, half=half: e.matmul(po.t[:, :], lhs, wob.t[:, j, half * 512:(half + 1) * 512],
                                                                                 start=(j == 0), stop=(j == 7)), reads=[rb, wob.b], writes=[po.b])
                    p.op('dve', lambda e, po=po, tm=tm, half=half, col=col: e.tensor_mul(tm.t[:, :], po.t[:, :], g1l[col].t[:, half * 512:(half + 1) * 512]),
                         reads=[po.b, g1l[col].b], writes=[tm.b])
                    p.op('pool', lambda e, tm=tm, xr=xr, h_=h_, half=half: e.tensor_add(h_.t[:, half * 512:(half + 1) * 512], xr.t[:, half * 512:(half + 1) * 512], tm.t[:, :]),
                         reads=[tm.b, xr.b], writes=[h_.b])
                p.dma(Hs[ti * 128:(ti + 1) * 128, :], h_.t[:], reads=[h_.b])
        p.barrier()
    L0.close()

    if stage == 1:
        for ti in range(32):
            pass
        with ExitStack() as st:
            bb_ = [sbt(st, f"bo{i}", [128, D], F32) for i in range(2)]
            for ti in range(32):
                b_ = bb_[ti % 2]
                p.dma(b_.t[:], Hs[ti * 128:(ti + 1) * 128, :], writes=[b_.b])
                p.dma(out[ti * 128:(ti + 1) * 128, :], b_.t[:], reads=[b_.b])
        p.emit()
        top.close()
        return nc

    def ffn(l, tiles, final):
        with ExitStack() as st:
            Wg = sbt(st, "Wg", [128, 8, FF], BF16)
            Wu = sbt(st, "Wu", [128, 8, FF], BF16)
            Wd = sbt(st, "Wd", [128, NFC, D], BF16)
            with ExitStack() as st2:
                stg = [sbt(st2, f"fst{i}", [128, FF], F32) for i in range(4)]
                si_ = 0
                for nm, Wt in (("ffn_w_gate", Wg), ("ffn_w_up", Wu)):
                    for k in range(8):
                        s_ = stg[si_ % 4]
                        si_ += 1
                        p.dma(s_.t[:, 0:1408], I[nm][l, k * 128:(k + 1) * 128, 0:1408], writes=[s_.b])
                        p.dma(s_.t[:, 1408:FF], I[nm][l, k * 128:(k + 1) * 128, 1408:FF], writes=[s_.b])
                        p.op('dve', lambda e, Wt=Wt, k=k, s_=s_: e.tensor_copy(Wt.t[:, k, 0:1408], s_.t[:, 0:1408]), reads=[s_.b], writes=[Wt.b])
                        p.op('act', lambda e, Wt=Wt, k=k, s_=s_: e.activation(Wt.t[:, k, 1408:FF], s_.t[:, 1408:FF], AF.Identity), reads=[s_.b], writes=[Wt.b])
                for fc in range(0, NFC, 2):
                    s_ = stg[si_ % 4]
                    si_ += 1
                    for q2 in range(2):
                        p.dma(s_.t[:, q2 * D:(q2 + 1) * D], I["ffn_w_down"][l, (fc + q2) * 128:(fc + q2 + 1) * 128, :], writes=[s_.b])
                    p.op('dve' if (fc // 2) % 2 == 0 else 'act', (lambda e, fc=fc, s_=s_: e.tensor_copy(Wd.t[:, fc:fc + 2, :].rearrange("p a d -> p (a d)"), s_.t[:, 0:2 * D])) if (fc // 2) % 2 == 0 else (lambda e, fc=fc, s_=s_: e.activation(Wd.t[:, fc:fc + 2, :].rearrange("p a d -> p (a d)"), s_.t[:, 0:2 * D], AF.Identity)), reads=[s_.b], writes=[Wd.b])
                p.barrier()
            cn = make_norm(st, "f", nxt=0)
            g2l = [load_gate(st, "g2lat", l, 0, 1)] + ([load_gate(st, "g2ctx", l, 1, 1)] if not final else [])
            hx = [sbt(st, f"hx{i}", [128, D], F32) for i in range(3)]
            nTf = [sbt(st, f"nTf{i}", [128, 8, 256], BF16) for i in range(2)]
            actT = [sbt(st, f"actT{i}", [128, NFC, 256], BF16) for i in range(1)]
            sg = [sbt(st, f"sg{i}", [128, 256], F32) for i in range(2)]
            pgt = [pst(st, f"fpg{i}", [128, 256]) for i in range(2)]
            put = [pst(st, f"fpu{i}", [128, 256]) for i in range(2)]
            pdn = [pst(st, f"fpd{i}", [128, 512]) for i in range(2)]
            tm_ = [sbt(st, f"ftm{i}", [128, 512], F32) for i in range(2)]
            ho = [sbt(st, f"ho{i}", [128, D], F32) for i in range(2)]
            fss = [sbt(st, f"fss{i}", [128, 1], F32) for i in range(2)]
            fj = cn["junk"]
            fingt = None
            if final:
                fingt = sbt(st, "fing", [128, D], F32)
                p.dma(fingt.t[:], I["final_g"].partition_broadcast(128), writes=[fingt.b])
            gi_ = 0
            di_ = 0
            hi_ = 0
            for b0 in range(0, len(tiles), 2):
                tl = tiles[b0:b0 + 2]
                nT = nTf[(b0 // 2) % 2]
                aT = actT[0]
                xts = []
                for tt, ti in enumerate(tl):
                    col = 0 if ti < 32 else 1
                    hxt = hx[hi_ % 3]
                    hi_ += 1
                    src_ = Hs[ti * 128:(ti + 1) * 128, :]
                    norm_tile(cn, src_, l, col, 1, lambda k, tt=tt, nT=nT: nT.t[:, k, tt * 128:(tt + 1) * 128], nT.b, xt_fixed=hxt)
                    xts.append(hxt)
                for fc in range(NFC):
                    a = pgt[gi_ % 2]
                    b = put[gi_ % 2]
                    s2 = sg[gi_ % 2]
                    gi_ += 1
                    for k in range(8):
                        p.op('pe', lambda e, a=a, k=k, fc=fc, nT=nT: e.matmul(a.t[:, :], Wg.t[:, k, fc * 128:(fc + 1) * 128], nT.t[:, k, :], start=(k == 0), stop=(k == 7)),
                             reads=[Wg.b, nT.b], writes=[a.b])
                    for k in range(8):
                        p.op('pe', lambda e, b=b, k=k, fc=fc, nT=nT: e.matmul(b.t[:, :], Wu.t[:, k, fc * 128:(fc + 1) * 128], nT.t[:, k, :], start=(k == 0), stop=(k == 7)),
                             reads=[Wu.b, nT.b], writes=[b.b])
                    p.op('act', lambda e, a=a, s2=s2: e.activation(s2.t[:, :], a.t[:, :], AF.Silu), reads=[a.b], writes=[s2.b])
                    p.op('dve', lambda e, b=b, s2=s2, aT=aT, fc=fc: e.tensor_mul(aT.t[:, fc, :], s2.t[:, :], b.t[:, :]), reads=[b.b, s2.b], writes=[aT.b])
                for tt, ti in enumerate(tl):
                    col = 0 if ti < 32 else 1
                    hxt = xts[tt]
                    h_ = ho[ti % 2]
                    for half in range(2):
                        po = pdn[di_ % 2]
                        tm = tm_[di_ % 2]
                        di_ += 1
                        for fc in range(NFC):
                            p.op('pe', lambda e, po=po, fc=fc, tt=tt, half=half, aT=aT: e.matmul(po.t[:, :], aT.t[:, fc, tt * 128:(tt + 1) * 128], Wd.t[:, fc, half * 512:(half + 1) * 512],
                                                                                         start=(fc == 0), stop=(fc == NFC - 1)), reads=[aT.b, Wd.b], writes=[po.b])
                        p.op('dve', lambda e, po=po, tm=tm, half=half, col=col: e.tensor_mul(tm.t[:, :], po.t[:, :], g2l[col].t[:, half * 512:(half + 1) * 512]),
                             reads=[po.b, g2l[col].b], writes=[tm.b])
                        p.op('pool', lambda e, tm=tm, hxt=hxt, h_=h_, half=half: e.tensor_add(h_.t[:, half * 512:(half + 1) * 512], hxt.t[:, half * 512:(half + 1) * 512], tm.t[:, :]),
                             reads=[tm.b, hxt.b], writes=[h_.b])
                    if not final:
                        p.dma(Hs[ti * 128:(ti + 1) * 128, :], h_.t[:], reads=[h_.b])
                    else:
                        ss = fss[ti % 2]
                        p.op('act', lambda e, h_=h_, ss=ss: e.activation(fj.t[:], h_.t[:], AF.Square, accum_out=ss.t[:]), reads=[h_.b], writes=[fj.b, ss.b])
                        p.op('dve', lambda e, ss=ss: e.tensor_scalar(ss.t[:], ss.t[:], 1.0 / D, EPS, ALU.mult, ALU.add), reads=[ss.b], writes=[ss.b])
                        p.op('act', lambda e, ss=ss: e.activation(ss.t[:], ss.t[:], AF.Sqrt), reads=[ss.b], writes=[ss.b])
                        p.op('dve', lambda e, ss=ss: e.reciprocal(ss.t[:], ss.t[:]), reads=[ss.b], writes=[ss.b])
                        p.op('dve', lambda e, h_=h_, ss=ss: e.scalar_tensor_tensor(h_.t[:], h_.t[:], ss.t[:, 0:1], fingt.t[:], ALU.mult, ALU.mult),
                             reads=[h_.b, ss.b, fingt.b], writes=[h_.b])
                        p.dma(out[ti * 128:(ti + 1) * 128, :], h_.t[:], reads=[h_.b])
            p.barrier()

    ffn(0, list(range(34)), False)

    if stage == 2:
        with ExitStack() as st:
            bb_ = [sbt(st, f"bo{i}", [128, D], F32) for i in range(2)]
            for ti in range(32):
                b_ = bb_[ti % 2]
                p.dma(b_.t[:], Hs[ti * 128:(ti + 1) * 128, :], writes=[b_.b])
                p.dma(out[ti * 128:(ti + 1) * 128, :], b_.t[:], reads=[b_.b])
        p.emit()
        top.close()
        return nc

    s5_layer(nc, p, I, K, Hs, AB, load_gate, identf, identb, sbt, pst, ew, make_norm, norm_tile)
    if stage == 3:
        with ExitStack() as st:
            bb_ = [sbt(st, f"bo3{i}", [128, D], F32) for i in range(2)]
            for ti in range(32):
                b_ = bb_[ti % 2]
                p.dma(b_.t[:], Hs[ti * 128:(ti + 1) * 128, :], writes=[b_.b])
                p.dma(out[ti * 128:(ti + 1) * 128, :], b_.t[:], reads=[b_.b])
        p.emit()
        top.close()
        return nc
    ffn(1, list(range(16)), True)
    p.emit()
    top.close()
    return nc


def rev(ap):
    apl = [list(a) for a in ap.ap]
    n = apl[-1][1]
    stp = apl[-1][0]
    apl[-1][0] = -stp
    return bass.AP(ap.tensor, ap.offset + (n - 1) * stp, apl)


def bcast_last(ap, n):
    apl = [list(a) for a in ap.ap] + [[0, n]]
    return bass.AP(ap.tensor, ap.offset, apl)


def s5_layer(nc, p, I, K, Hs, AB, load_gate, identf, identb, sbt, pst, ew, make_norm, norm_tile, dbg=None):
    Hloc = Hs
    YGs = nc.dram_tensor("YGs", [8, 128, T // 2], BF16).ap()
    NTOK = T + TC
    with ExitStack() as S:
        nT = sbt(S, "s5nT", [128, 8, NTOK], BF16)
        with ExitStack() as st:
            cn = make_norm(st, "s5n", nxt=3)
            for ti in range(34):
                col = 0 if ti < 32 else 1
                norm_tile(cn, Hs[ti * 128:(ti + 1) * 128, :], 1, col, 0, lambda k, ti=ti: nT.t[:, k, ti * 128:(ti + 1) * 128], nT.b)
            p.barrier()
        PRM = sbt(S, "s5prm", [128, 3, 64], F32)
        Cw = sbt(S, "s5Cw", [128, 64, 2, 32], F32)
        Bz = sbt(S, "s5Bz", [128, 2, 64, 2, 16], F32)
        LC = 4
        NCM = 512 // LC
        PW = sbt(S, "s5PW", [128, LC + 1, 2, 64], F32)
        PRM4 = sbt(S, "s5prm4", [128, 3, 64], F32)
        dT = sbt(S, "s5dT", [128, 8], F32)
        with ExitStack() as st:
            pt = pst(st, "s5pt", [128, 128])
            def V(nm):
                return sbt(st, "s5v_" + nm, [128, 64], F32)
            arow = sbt(st, "s5arow", [64, 2, 128], F32)
            p.dma(arow.t[:, 0, :], I["ssm_a_re"].rearrange("(dq g) p -> dq (g p)", g=2), writes=[arow.b])
            p.dma(arow.t[:, 1, :], I["ssm_a_im"].rearrange("(dq g) p -> dq (g p)", g=2), writes=[arow.b])
            are, aim = V("are"), V("aim")
            for i_, dst in enumerate((are, aim)):
                p.op('pe', lambda e, i_=i_: e.transpose(pt.t[:, 0:64], arow.t[:, i_, :], identf.t[0:64, 0:64]), reads=[arow.b, identf.b], writes=[pt.b])
                p.op('dve', lambda e, dst=dst: e.tensor_copy(dst.t[:], pt.t[:, 0:64]), reads=[pt.b], writes=[dst.b])
            drow = sbt(st, "s5drow", [8, 128], F32)
            p.dma(drow.t[:], I["ssm_d"], writes=[drow.b])
            p.op('pe', lambda e: e.transpose(pt.t[:, 0:8], drow.t[:, :], identf.t[0:8, 0:8]), reads=[drow.b, identf.b], writes=[pt.b])
            p.op('dve', lambda e: e.tensor_copy(dT.t[:], pt.t[:, 0:8]), reads=[pt.b], writes=[dT.b])
            ldb = sbt(st, "s5ldb", [128, 128], F32)
            p.dma(ldb.t[:], I["ssm_log_dt"].partition_broadcast(128), writes=[ldb.b])
            dt = V("dt")
            for g2 in range(2):
                p.op('dve', lambda e, g2=g2: e.tensor_copy(dt.t[64 * g2:64 * g2 + 64, :], ldb.t[64 * g2:64 * g2 + 64, g2:128:2]), reads=[ldb.b], writes=[dt.b])
            p.op('act', lambda e: e.activation(dt.t[:], dt.t[:], AF.Exp), reads=[dt.b], writes=[dt.b])
            xr, th, mag = V("xr"), V("th"), V("mag")
            p.op('dve', lambda e: e.tensor_mul(xr.t[:], are.t[:], dt.t[:]), reads=[are.b, dt.b], writes=[xr.b])
            p.op('dve', lambda e: e.tensor_mul(th.t[:], aim.t[:], dt.t[:]), reads=[aim.b, dt.b], writes=[th.b])
            p.op('act', lambda e: e.activation(mag.t[:], xr.t[:], AF.Exp), reads=[xr.b], writes=[mag.b])
            kf = V("kf")
            ki = sbt(st, "s5ki", [128, 64], mybir.dt.int32)
            p.op('dve', lambda e: e.tensor_scalar(kf.t[:], th.t[:], 1.0 / (2 * math.pi), None, ALU.mult), reads=[th.b], writes=[kf.b])
            p.op('dve', lambda e: e.tensor_copy(ki.t[:], kf.t[:]), reads=[kf.b], writes=[ki.b])
            p.op('dve', lambda e: e.tensor_copy(kf.t[:], ki.t[:]), reads=[ki.b], writes=[kf.b])
            C1 = 6.28125
            C2 = 2 * math.pi - 6.28125
            thm = V("thm")
            p.op('dve', lambda e: e.scalar_tensor_tensor(thm.t[:], kf.t[:], -C1, th.t[:], ALU.mult, ALU.add), reads=[kf.b, th.b], writes=[thm.b])
            p.op('dve', lambda e: e.scalar_tensor_tensor(thm.t[:], kf.t[:], -C2, thm.t[:], ALU.mult, ALU.add), reads=[kf.b, thm.b], writes=[thm.b])
            xq, u2, qs, qc = V("xq"), V("u2"), V("qs"), V("qc")
            p.op('dve', lambda e: e.tensor_scalar(xq.t[:], thm.t[:], 0.25, None, ALU.mult), reads=[thm.b], writes=[xq.b])
            p.op('dve', lambda e: e.tensor_mul(u2.t[:], xq.t[:], xq.t[:]), reads=[xq.b], writes=[u2.b])
            sc_ = [(-1.0) ** k / math.factorial(2 * k + 1) for k in range(9)]
            cc_ = [(-1.0) ** k / math.factorial(2 * k) for k in range(9)]
            p.op('dve', lambda e: e.tensor_scalar(qs.t[:], u2.t[:], sc_[8], None, ALU.mult), reads=[u2.b], writes=[qs.b])
            p.op('dve', lambda e: e.tensor_scalar(qc.t[:], u2.t[:], cc_[8], None, ALU.mult), reads=[u2.b], writes=[qc.b])
            for k in range(7, 0, -1):
                p.op('dve', lambda e, k=k: e.scalar_tensor_tensor(qs.t[:], qs.t[:], sc_[k], u2.t[:], ALU.add, ALU.mult), reads=[qs.b, u2.b], writes=[qs.b])
                p.op('dve', lambda e, k=k: e.scalar_tensor_tensor(qc.t[:], qc.t[:], cc_[k], u2.t[:], ALU.add, ALU.mult), reads=[qc.b, u2.b], writes=[qc.b])
            sn, cs = V("sn"), V("cs")
            p.op('dve', lambda e: e.scalar_tensor_tensor(sn.t[:], qs.t[:], 1.0, xq.t[:], ALU.add, ALU.mult), reads=[qs.b, xq.b], writes=[sn.b])
            p.op('dve', lambda e: e.tensor_scalar(cs.t[:], qc.t[:], 1.0, None, ALU.add), reads=[qc.b], writes=[cs.b])
            ta, tb_ = V("ta"), V("tb")
            for _ in range(2):
                p.op('dve', lambda e: e.tensor_mul(ta.t[:], cs.t[:], cs.t[:]), reads=[cs.b], writes=[ta.b])
                p.op('dve', lambda e: e.tensor_mul(tb_.t[:], sn.t[:], sn.t[:]), reads=[sn.b], writes=[tb_.b])
                p.op('dve', lambda e: e.scalar_tensor_tensor(sn.t[:], cs.t[:], 2.0, sn.t[:], ALU.mult, ALU.mult), reads=[cs.b, sn.b], writes=[sn.b])
                p.op('dve', lambda e: e.tensor_sub(cs.t[:], ta.t[:], tb_.t[:]), reads=[ta.b, tb_.b], writes=[cs.b])
            p.op('dve', lambda e: e.tensor_copy(PRM.t[:, 0, :], mag.t[:]), reads=[mag.b], writes=[PRM.b])
            p.op('dve', lambda e: e.tensor_copy(PRM.t[:, 1, :], cs.t[:]), reads=[cs.b], writes=[PRM.b])
            p.op('dve', lambda e: e.tensor_copy(PRM.t[:, 2, :], sn.t[:]), reads=[sn.b], writes=[PRM.b])
            p.op('pool', lambda e: e.memset(PW.t[:, 0, 0, :], 1.0), writes=[PW.b])
            p.op('pool', lambda e: e.memset(PW.t[:, 0, 1, :], 0.0), writes=[PW.b])
            p.op('dve', lambda e: e.tensor_mul(PW.t[:, 1, 0, :], mag.t[:], cs.t[:]), reads=[mag.b, cs.b], writes=[PW.b])
            p.op('dve', lambda e: e.tensor_mul(PW.t[:, 1, 1, :], mag.t[:], sn.t[:]), reads=[mag.b, sn.b], writes=[PW.b])
            for k in range(2, LC + 1):
                p.op('dve', lambda e, k=k: e.tensor_mul(ta.t[:], PW.t[:, k - 1, 0, :], PW.t[:, 1, 0, :]), reads=[PW.b], writes=[ta.b])
                p.op('dve', lambda e, k=k: e.tensor_mul(tb_.t[:], PW.t[:, k - 1, 1, :], PW.t[:, 1, 1, :]), reads=[PW.b], writes=[tb_.b])
                p.op('dve', lambda e, k=k: e.tensor_sub(PW.t[:, k, 0, :], ta.t[:], tb_.t[:]), reads=[ta.b, tb_.b], writes=[PW.b])
                p.op('dve', lambda e, k=k: e.tensor_mul(ta.t[:], PW.t[:, k - 1, 0, :], PW.t[:, 1, 1, :]), reads=[PW.b], writes=[ta.b])
                p.op('dve', lambda e, k=k: e.tensor_mul(tb_.t[:], PW.t[:, k - 1, 1, :], PW.t[:, 1, 0, :]), reads=[PW.b], writes=[tb_.b])
                p.op('dve', lambda e, k=k: e.tensor_add(PW.t[:, k, 1, :], ta.t[:], tb_.t[:]), reads=[ta.b, tb_.b], writes=[PW.b])
            c4, s4, r4 = V("c4"), V("s4"), V("r4")
            p.op('dve', lambda e: e.tensor_copy(c4.t[:], cs.t[:]), reads=[cs.b], writes=[c4.b])
            p.op('dve', lambda e: e.tensor_copy(s4.t[:], sn.t[:]), reads=[sn.b], writes=[s4.b])
            p.op('dve', lambda e: e.tensor_copy(r4.t[:], mag.t[:]), reads=[mag.b], writes=[r4.b])
            for _ in range(int(round(math.log2(LC)))):
                p.op('dve', lambda e: e.tensor_mul(ta.t[:], c4.t[:], c4.t[:]), reads=[c4.b], writes=[ta.b])
                p.op('dve', lambda e: e.tensor_mul(tb_.t[:], s4.t[:], s4.t[:]), reads=[s4.b], writes=[tb_.b])
                p.op('dve', lambda e: e.scalar_tensor_tensor(s4.t[:], c4.t[:], 2.0, s4.t[:], ALU.mult, ALU.mult), reads=[c4.b, s4.b], writes=[s4.b])
                p.op('dve', lambda e: e.tensor_sub(c4.t[:], ta.t[:], tb_.t[:]), reads=[ta.b, tb_.b], writes=[c4.b])
                p.op('dve', lambda e: e.tensor_mul(r4.t[:], r4.t[:], r4.t[:]), reads=[r4.b], writes=[r4.b])
            p.op('dve', lambda e: e.tensor_copy(PRM4.t[:, 0, :], r4.t[:]), reads=[r4.b], writes=[PRM4.b])
            p.op('dve', lambda e: e.tensor_copy(PRM4.t[:, 1, :], c4.t[:]), reads=[c4.b], writes=[PRM4.b])
            p.op('dve', lambda e: e.tensor_copy(PRM4.t[:, 2, :], s4.t[:]), reads=[s4.b], writes=[PRM4.b])
            nr, ni, den, fre, fim = V("nr"), V("ni"), V("den"), V("fre"), V("fim")
            p.op('dve', lambda e: e.tensor_mul(nr.t[:], mag.t[:], cs.t[:]), reads=[mag.b, cs.b], writes=[nr.b])
            p.op('dve', lambda e: e.tensor_scalar(nr.t[:], nr.t[:], -1.0, None, ALU.add), reads=[nr.b], writes=[nr.b])
            p.op('dve', lambda e: e.tensor_mul(ni.t[:], mag.t[:], sn.t[:]), reads=[mag.b, sn.b], writes=[ni.b])
            p.op('dve', lambda e: e.tensor_mul(den.t[:], are.t[:], are.t[:]), reads=[are.b], writes=[den.b])
            p.op('dve', lambda e: e.tensor_mul(ta.t[:], aim.t[:], aim.t[:]), reads=[aim.b], writes=[ta.b])
            p.op('dve', lambda e: e.tensor_add(den.t[:], den.t[:], ta.t[:]), reads=[den.b, ta.b], writes=[den.b])
            p.op('dve', lambda e: e.reciprocal(den.t[:], den.t[:]), reads=[den.b], writes=[den.b])
            p.op('dve', lambda e: e.tensor_mul(ta.t[:], nr.t[:], are.t[:]), reads=[nr.b, are.b], writes=[ta.b])
            p.op('dve', lambda e: e.tensor_mul(tb_.t[:], ni.t[:], aim.t[:]), reads=[ni.b, aim.b], writes=[tb_.b])
            p.op('dve', lambda e: e.tensor_add(fre.t[:], ta.t[:], tb_.t[:]), reads=[ta.b, tb_.b], writes=[fre.b])
            p.op('dve', lambda e: e.tensor_mul(fre.t[:], fre.t[:], den.t[:]), reads=[fre.b, den.b], writes=[fre.b])
            p.op('dve', lambda e: e.tensor_mul(ta.t[:], ni.t[:], are.t[:]), reads=[ni.b, are.b], writes=[ta.b])
            p.op('dve', lambda e: e.tensor_mul(tb_.t[:], nr.t[:], aim.t[:]), reads=[nr.b, aim.b], writes=[tb_.b])
            p.op('dve', lambda e: e.tensor_sub(fim.t[:], ta.t[:], tb_.t[:]), reads=[ta.b, tb_.b], writes=[fim.b])
            p.op('dve', lambda e: e.tensor_mul(fim.t[:], fim.t[:], den.t[:]), reads=[fim.b, den.b], writes=[fim.b])
            Br = [sbt(st, f"s5Br{i}", [128, 64, 16], F32) for i in range(2)]
            for i_, nm in enumerate(("ssm_b_re", "ssm_b_im")):
                src = I[nm].rearrange("d (q g) p h -> g p (d q) h", g=2)
                for g2 in range(2):
                    p.dma(Br[i_].t[64 * g2:64 * g2 + 64, :, :], src[g2], writes=[Br[i_].b])
            p.op('pool', lambda e: e.memset(Bz.t[:].rearrange("p a b c d -> p (a b c d)"), 0.0), writes=[Bz.b])
            m1 = sbt(st, "s5m1", [128, 64, 16], F32)
            m2 = sbt(st, "s5m2", [128, 64, 16], F32)
            fre_b = bcast_last(fre.t[:], 16)
            fim_b = bcast_last(fim.t[:], 16)
            p.op('dve', lambda e: e.tensor_mul(m1.t[:], Br[0].t[:], fre_b), reads=[Br[0].b, fre.b], writes=[m1.b])
            p.op('dve', lambda e: e.tensor_mul(m2.t[:], Br[1].t[:], fim_b), reads=[Br[1].b, fim.b], writes=[m2.b])
            for g2 in range(2):
                p.op('dve', lambda e, g2=g2: e.tensor_sub(Bz.t[64 * g2:64 * g2 + 64, 0, :, g2, :], m1.t[64 * g2:64 * g2 + 64], m2.t[64 * g2:64 * g2 + 64]),
                     reads=[m1.b, m2.b], writes=[Bz.b])
            p.op('dve', lambda e: e.tensor_mul(m1.t[:], Br[1].t[:], fre_b), reads=[Br[1].b, fre.b], writes=[m1.b])
            p.op('dve', lambda e: e.tensor_mul(m2.t[:], Br[0].t[:], fim_b), reads=[Br[0].b, fim.b], writes=[m2.b])
            for g2 in range(2):
                p.op('dve', lambda e, g2=g2: e.tensor_add(Bz.t[64 * g2:64 * g2 + 64, 1, :, g2, :], m1.t[64 * g2:64 * g2 + 64], m2.t[64 * g2:64 * g2 + 64]),
                     reads=[m1.b, m2.b], writes=[Bz.b])
            pts = [pst(st, f"s5ptb{i}", [128, 128]) for i in range(2)]
            n_ = 0
            p.op('pool', lambda e: e.memset(Cw.t[:].rearrange("p a b c -> p (a b c)"), 0.0), writes=[Cw.b])
            Cn = [sbt(st, f"s5Cn{i}", [128, 2, 64], F32) for i in range(2)]
            for ri, nm in enumerate(("ssm_c_re", "ssm_c_im")):
                for dQ in range(16):
                    d_, Q_ = dQ // 8, dQ % 8
                    cnb = Cn[n_ % 2]
                    pp = pts[n_ % 2]
                    n_ += 1
                    src = I[nm][d_, Q_ * 4 * 2:(Q_ * 4 + 4) * 2].rearrange("g h p -> (g h) p")
                    p.dma(cnb.t[:, 0, :], src, writes=[cnb.b])
                    p.dma(cnb.t[:, 1, :], src, writes=[cnb.b])
                    p.op('pe', lambda e, pp=pp, cnb=cnb: e.transpose(pp.t[:, :], cnb.t[:, :, :].rearrange("p a b -> p (a b)"), identf.t[:, :]),
                         reads=[cnb.b, identf.b], writes=[pp.b])
                    sgn = 1.0 if ri == 0 else -1.0
                    for g2 in range(2):
                        p.op('act', lambda e, pp=pp, g2=g2, ri=ri, dQ=dQ, sgn=sgn: e.activation(
                            Cw.t[64 * g2:64 * g2 + 64, dQ * 4:(dQ + 1) * 4, ri, 16 * g2:16 * g2 + 16],
                            pp.t[64 * g2:64 * g2 + 64, :].rearrange("p (q g h) -> p q g h", q=4, g=2)[:, :, g2, :], AF.Identity, scale=sgn),
                            reads=[pp.b], writes=[Cw.b])
            p.barrier()

        if dbg is not None:
            dbg(PRM, Cw, nT)
            return

        with ExitStack() as st:
            yacc = sbt(st, "s5yacc", [128, T // 2], F32)
            Ec = sbt(st, "s5Ec", [128, 8, NCM], F32)
            Es = sbt(st, "s5Es", [128, 8, NCM], F32)
            wc = sbt(st, "s5wc", [128, 8], F32)
            ws = sbt(st, "s5ws", [128, 8], F32)
            wt1 = sbt(st, "s5wt1", [128, 8], F32)
            wt2 = sbt(st, "s5wt2", [128, 8], F32)
            et1 = sbt(st, "s5et1", [128, 8, NCM // 2], F32)
            et2 = sbt(st, "s5et2", [128, 8, NCM // 2], F32)
            hp = sbt(st, "s5hp", [128, 8, 2], F32)
            ini = sbt(st, "s5ini", [128, 8, 2], F32)
            itmp = sbt(st, "s5itmp", [128, 8, 2], F32)
            nsth = sbt(st, "s5nsth", [128, 64], F32)
            p.op('dve', lambda e: e.tensor_scalar(nsth.t[:], PRM4.t[:, 2, :], -1.0, None, ALU.mult), reads=[PRM4.b], writes=[nsth.b])
            hpb = [Buf() for _ in range(8)]
            CwP = sbt(st, "s5CwP", [128, LC + 1, 8, 2, 32], F32)
            ctm = [sbt(st, f"s5ctm{i}", [128, 4, 32], F32) for i in range(2)]
            BP = sbt(st, "s5BP", [128, LC, 2, 2, 4, 32], F32)
            WinL = [sbt(st, f"s5Win{i}", [128, 2, 2, LC, 128], BF16) for i in range(2)]
            MwL = [sbt(st, f"s5Mw{i}", [128, 2, LC, 32], BF16) for i in range(2)]
            WoutL = [sbt(st, f"s5Wout{i}", [128, 8, LC, 2, 32], BF16) for i in range(2)]
            pts = [pst(st, f"s5ptq{i}", [128, 128]) for i in range(1)]
            pk = pst(st, "s5pk", [128, 16, 32])
            NS = 8
            Wk_ = [[sbt(st, f"s5w{j}_{i}", [128, 2, NCM], F32) for i in range(3)] for j in range(NS)]
            Hb = [sbt(st, f"s5hb{j}", [128, 2, NCM + 4], BF16) for j in range(NS)]
            Esg = sbt(st, "s5Esg", [128, 8, 2, NCM], F32)

            def swap2(tb_, ncn):
                b1 = tb_.t[:, 1, 0:ncn]
                apl = [list(a_) for a_ in b1.ap]
                return bass.AP(b1.tensor, b1.offset, [apl[0], [-NCM, 2], apl[-1]])

            def bc2(ap2):
                apl = [list(a_) for a_ in ap2.ap]
                return bass.AP(ap2.tensor, ap2.offset, [apl[0], [0, 2], apl[-1]])
            pS = [pst(st, f"s5pS{j}", [128, 512]) for j in range(4)]
            py = [pst(st, f"s5py{i}", [128, LC, NCM]) for i in range(2)]
            ygb = [sbt(st, f"s5yg{i}", [128, 512], BF16) for i in range(1)]
            gtb = [sbt(st, f"s5gt{i}", [128, 512], F32) for i in range(1)]
            tn_ = 0
            blkc = 0
            yi_ = 0
            psi = 0
            def gen_weights(Q):
                Win, Mw, Wout = WinL[Q % 2], MwL[Q % 2], WoutL[Q % 2]
                for d_ in range(2):
                    us = slice(d_ * 32 + Q * 4, d_ * 32 + Q * 4 + 4)
                    ts_ = slice(d_ * 4, d_ * 4 + 4)
                    c0_ = Cw.t[:, us, 0, :]
                    c1_ = Cw.t[:, us, 1, :]
                    for k in range(LC + 1):
                        pr = bcast_last(PW.t[:, k, 0, us], 32)
                        pi_ = bcast_last(PW.t[:, k, 1, us], 32)
                        t1_, t2_ = ctm
                        p.op('dve', lambda e, t1_=t1_, c0_=c0_, pr=pr: e.tensor_mul(t1_.t[:], c0_, pr), reads=[Cw.b, PW.b], writes=[t1_.b])
                        p.op('pool', lambda e, t2_=t2_, c1_=c1_, pi_=pi_: e.tensor_mul(t2_.t[:], c1_, pi_), reads=[Cw.b, PW.b], writes=[t2_.b])
                        p.op('dve', lambda e, t1_=t1_, t2_=t2_, k=k, ts_=ts_: e.tensor_add(CwP.t[:, k, ts_, 0, :], t1_.t[:], t2_.t[:]), reads=[t1_.b, t2_.b], writes=[CwP.b])
                        p.op('dve', lambda e, t1_=t1_, c1_=c1_, pr=pr: e.tensor_mul(t1_.t[:], c1_, pr), reads=[Cw.b, PW.b], writes=[t1_.b])
                        p.op('pool', lambda e, t2_=t2_, c0_=c0_, pi_=pi_: e.tensor_mul(t2_.t[:], c0_, pi_), reads=[Cw.b, PW.b], writes=[t2_.b])
                        p.op('dve', lambda e, t1_=t1_, t2_=t2_, k=k, ts_=ts_: e.tensor_sub(CwP.t[:, k, ts_, 1, :], t1_.t[:], t2_.t[:]), reads=[t1_.b, t2_.b], writes=[CwP.b])
                    b0_ = Bz.t[:, 0, us, :, :].rearrange("p a b c -> p a (b c)")
                    b1_ = Bz.t[:, 1, us, :, :].rearrange("p a b c -> p a (b c)")
                    for s_ in range(LC):
                        k = LC - 1 - s_
                        pr = bcast_last(PW.t[:, k, 0, us], 32)
                        pi_ = bcast_last(PW.t[:, k, 1, us], 32)
                        t1_, t2_ = ctm
                        p.op('dve', lambda e, t1_=t1_, b0_=b0_, pr=pr: e.tensor_mul(t1_.t[:], b0_, pr), reads=[Bz.b, PW.b], writes=[t1_.b])
                        p.op('pool', lambda e, t2_=t2_, b1_=b1_, pi_=pi_: e.tensor_mul(t2_.t[:], b1_, pi_), reads=[Bz.b, PW.b], writes=[t2_.b])
                        p.op('dve', lambda e, t1_=t1_, t2_=t2_, s_=s_, d_=d_: e.tensor_sub(BP.t[:, s_, 0, d_, :, :], t1_.t[:], t2_.t[:]), reads=[t1_.b, t2_.b], writes=[BP.b])
                        p.op('dve', lambda e, t1_=t1_, b0_=b0_, pi_=pi_: e.tensor_mul(t1_.t[:], b0_, pi_), reads=[Bz.b, PW.b], writes=[t1_.b])
                        p.op('pool', lambda e, t2_=t2_, b1_=b1_, pr=pr: e.tensor_mul(t2_.t[:], b1_, pr), reads=[Bz.b, PW.b], writes=[t2_.b])
                        p.op('dve', lambda e, t1_=t1_, t2_=t2_, s_=s_, d_=d_: e.tensor_add(BP.t[:, s_, 1, d_, :, :], t1_.t[:], t2_.t[:]), reads=[t1_.b, t2_.b], writes=[BP.b])
                p.op('act', lambda e: e.activation(Wout.t[:].rearrange("p x t r h -> p t x r h"), CwP.t[:, 1:LC + 1, :, :, :], AF.Identity), reads=[CwP.b], writes=[Wout.b])
                for d_ in range(2):
                    for ri in range(2):
                        for s_ in range(LC):
                            pp = pts[0]
                            p.op('pe', lambda e, pp=pp, s_=s_, ri=ri, d_=d_: e.transpose(pp.t[:, :], BP.t[:, s_, ri, d_, :, :].rearrange("p a b -> p (a b)"), identf.t[:, :]),
                                 reads=[BP.b, identf.b], writes=[pp.b])
                            p.op('act', lambda e, pp=pp, s_=s_, ri=ri, d_=d_: e.activation(Win.t[:, d_, ri, s_, :], pp.t[:, :], AF.Identity), reads=[pp.b], writes=[Win.b])
                    for thf in range(LC // 4):
                        for tq in range(4):
                            tau = thf * 4 + tq
                            for ql in range(4):
                                tix = d_ * 4 + ql
                                for ri in range(2):
                                    p.op('pe', lambda e, d_=d_, tau=tau, tq=tq, ql=ql, tix=tix, ri=ri, us=slice(d_ * 32 + Q * 4, d_ * 32 + Q * 4 + 4): e.matmul(
                                        pk.t[:, tq * 4 + ql, :], Bz.t[:, ri, us, :, :].rearrange("p q a b -> p (q a b)"), CwP.t[:, tau, tix, ri, :],
                                        start=(ri == 0), stop=(ri == 1)), reads=[Bz.b, CwP.b], writes=[pk.b])
                        for ql in range(4):
                            p.op('dve', lambda e, d_=d_, ql=ql, thf=thf: e.tensor_copy(Mw.t[32 * ql:32 * ql + 32, d_, thf * 4:thf * 4 + 4, :], pk.t[32 * ql:32 * ql + 32, ql:16:4, :]),
                                 reads=[pk.b], writes=[Mw.b])

            gen_weights(0)
            for Q in range(8):
                Win, Mw, Wout = WinL[Q % 2], MwL[Q % 2], WoutL[Q % 2]
                for d_ in range(2):
                    cols = slice(d_ * 32 + Q * 4, d_ * 32 + Q * 4 + 4)
                    p.op('dve', lambda e, d_=d_, cols=cols: e.tensor_copy(wc.t[:, d_ * 4:d_ * 4 + 4], PRM4.t[:, 1, cols]), reads=[PRM4.b], writes=[wc.b])
                    p.op('dve', lambda e, d_=d_, cols=cols: e.tensor_copy(ws.t[:, d_ * 4:d_ * 4 + 4], PRM4.t[:, 2, cols]), reads=[PRM4.b], writes=[ws.b])
                p.op('pool', lambda e: e.memset(Ec.t[:, :, 0:1], 1.0), writes=[Ec.b])
                p.op('pool', lambda e: e.memset(Es.t[:, :, 0:1], 0.0), writes=[Es.b])
                n = 1
                while n < NCM:
                    wcb = bcast_last(wc.t[:, :], n)
                    wsb = bcast_last(ws.t[:, :], n)
                    p.op('dve', lambda e, n=n, wcb=wcb: e.tensor_mul(et1.t[:, :, 0:n], Ec.t[:, :, 0:n], wcb), reads=[Ec.b, wc.b], writes=[et1.b])
                    p.op('pool', lambda e, n=n, wsb=wsb: e.tensor_mul(et2.t[:, :, 0:n], Es.t[:, :, 0:n], wsb), reads=[Es.b, ws.b], writes=[et2.b])
                    p.op('dve', lambda e, n=n: e.tensor_sub(Ec.t[:, :, n:2 * n], et1.t[:, :, 0:n], et2.t[:, :, 0:n]), reads=[et1.b, et2.b], writes=[Ec.b])
                    p.op('dve', lambda e, n=n, wcb=wcb: e.tensor_mul(et1.t[:, :, 0:n], Es.t[:, :, 0:n], wcb), reads=[Es.b, wc.b], writes=[et1.b])
                    p.op('pool', lambda e, n=n, wsb=wsb: e.tensor_mul(et2.t[:, :, 0:n], Ec.t[:, :, 0:n], wsb), reads=[Ec.b, ws.b], writes=[et2.b])
                    p.op('dve', lambda e, n=n: e.tensor_add(Es.t[:, :, n:2 * n], et1.t[:, :, 0:n], et2.t[:, :, 0:n]), reads=[et1.b, et2.b], writes=[Es.b])
                    p.op('dve', lambda e: e.tensor_mul(wt1.t[:], wc.t[:], wc.t[:]), reads=[wc.b], writes=[wt1.b])
                    p.op('dve', lambda e: e.tensor_mul(wt2.t[:], ws.t[:], ws.t[:]), reads=[ws.b], writes=[wt2.b])
                    p.op('dve', lambda e: e.scalar_tensor_tensor(ws.t[:], wc.t[:], 2.0, ws.t[:], ALU.mult, ALU.mult), reads=[wc.b, ws.b], writes=[ws.b])
                    p.op('dve', lambda e: e.tensor_sub(wc.t[:], wt1.t[:], wt2.t[:]), reads=[wt1.b, wt2.b], writes=[wc.b])
                    n *= 2
                p.op('dve', lambda e: e.tensor_copy(Esg.t[:, :, 0, :], Es.t[:, :, :]), reads=[Es.b], writes=[Esg.b])
                p.op('dve', lambda e: e.tensor_scalar(Esg.t[:, :, 1, :], Es.t[:, :, :], -1.0, None, ALU.mult), reads=[Es.b], writes=[Esg.b])
                blist = []
                for d_ in range(2):
                    if d_ == 0:
                        blocks = [(T, TC, False)] + [(b * 512, 512, True) for b in range(4)]
                    else:
                        blocks = [(T, TC, False)] + [(b * 512, 512, False) for b in (7, 6, 5, 4)] + [(b * 512, 512, True) for b in (3, 2, 1, 0)]
                    for bi_, (c0, n, islat) in enumerate(blocks):
                        blist.append((d_, bi_, c0, n, islat))
                bctx = {}

                def front_pe(k, Q=Q, Win=Win):
                    d_, bi_, c0, n, islat = blist[k]
                    gk = Q * 18 + k
                    ncn = n // LC
                    sets = [(gk % 2) * 4 + ql for ql in range(4)]
                    rhs_all = []
                    for ql in range(4):
                        base = nT.t[32 * ql:32 * ql + 32, Q, c0:c0 + n]
                        apl = [list(a_) for a_ in base.ap]
                        lst = []
                        for s_ in range(LC):
                            ap2 = [list(a_) for a_ in apl]
                            if d_ == 0:
                                ap2[-1] = [apl[-1][0] * LC, ncn]
                                lst.append(bass.AP(base.tensor, base.offset + s_ * apl[-1][0], ap2))
                            else:
                                ap2[-1] = [-apl[-1][0] * LC, ncn]
                                lst.append(bass.AP(base.tensor, base.offset + (n - 1 - s_) * apl[-1][0], ap2))
                        rhs_all.append(lst)
                    pss = [pS[ql] for ql in range(4)]
                    for ri in range(2):
                        for s_ in range(LC):
                            for ql in range(4):
                                ps_ = pss[ql]
                                p.op('pe', lambda e, ps_=ps_, ri=ri, s_=s_, ql=ql, d_=d_, ncn=ncn, r_=rhs_all[ql][s_]: e.matmul(
                                    ps_.t[:, ri * 128:ri * 128 + ncn], Win.t[32 * ql:32 * ql + 32, d_, ri, s_, :], r_, start=(s_ == 0), stop=(s_ == LC - 1), tile_position=(32 * ql, 0)),
                                    reads=[Win.b, nT.b], writes=[ps_.b])
                    bctx[k] = (ncn, sets, rhs_all)

                def front_ev(k, Q=Q):
                    d_, bi_, c0, n, islat = blist[k]
                    ncn, sets, rhs_all = bctx[k]
                    pss = [pS[ql] for ql in range(4)]
                    for ql in range(4):
                        ps_ = pss[ql]
                        X, P1, P2 = Wk_[sets[ql]]
                        p.op('act', lambda e, X=X, ps_=ps_, ncn=ncn: e.activation(X.t[:, :, 0:ncn], ps_.t[:, 0:256].rearrange("p (r c) -> p r c", r=2)[:, :, 0:ncn], AF.Identity),
                             reads=[ps_.b], writes=[X.b])
                    for ql in range(4):
                        X, P1, P2 = Wk_[sets[ql]]
                        tix = d_ * 4 + ql
                        ecb = bc2(Ec.t[:, tix, 0:ncn])
                        p.op('dve', lambda e, X=X, P1=P1, ecb=ecb, ncn=ncn: e.tensor_mul(P1.t[:, :, 0:ncn], X.t[:, :, 0:ncn], ecb), reads=[X.b, Ec.b], writes=[P1.b])
                        p.op('pool', lambda e, X=X, P2=P2, tix=tix, ncn=ncn: e.tensor_mul(P2.t[:, :, 0:ncn], swap2(X, ncn), Esg.t[:, tix, :, 0:ncn]), reads=[X.b, Esg.b], writes=[P2.b])
                    for ql in range(4):
                        X, P1, P2 = Wk_[sets[ql]]
                        p.op('dve', lambda e, P1=P1, P2=P2, ncn=ncn: e.tensor_add(P1.t[:, :, 0:ncn], P1.t[:, :, 0:ncn], P2.t[:, :, 0:ncn]), reads=[P1.b, P2.b], writes=[P1.b])

                def back(k, Q=Q, Mw=Mw, Wout=Wout):
                    d_, bi_, c0, n, islat = blist[k]
                    gk = Q * 18 + k
                    ncn, sets, rhs_all = bctx.pop(k)
                    pyt = py[gk % 2]
                    for ql in range(4):
                        tix = d_ * 4 + ql
                        u = d_ * 32 + Q * 4 + ql
                        hb_ = hpb[tix]
                        hbs = Hb[sets[ql]]
                        if bi_ > 0:
                            p.op('act', lambda e, tix=tix, u=u: e.activation(itmp.t[:, tix, 0:1], hp.t[:, tix, 1:2], AF.Identity, scale=nsth.t[:, u:u + 1]), reads=[hb_, nsth.b], writes=[hb_])
                            p.op('act', lambda e, tix=tix, u=u: e.activation(ini.t[:, tix, 0:1], hp.t[:, tix, 0:1], AF.Identity, scale=PRM4.t[:, 1, u:u + 1], bias=itmp.t[:, tix, 0:1]),
                                 reads=[hb_, PRM4.b], writes=[hb_])
                            p.op('act', lambda e, tix=tix, u=u: e.activation(itmp.t[:, tix, 1:2], hp.t[:, tix, 0:1], AF.Identity, scale=PRM4.t[:, 2, u:u + 1]), reads=[hb_, PRM4.b], writes=[hb_])
                            p.op('act', lambda e, tix=tix, u=u: e.activation(ini.t[:, tix, 1:2], hp.t[:, tix, 1:2], AF.Identity, scale=PRM4.t[:, 1, u:u + 1], bias=itmp.t[:, tix, 1:2]),
                                 reads=[hb_, PRM4.b], writes=[hb_])
                            if islat:
                                p.op('act', lambda e, hbs=hbs, tix=tix: e.activation(hbs.t[:, :, 0], hp.t[:, tix, :], AF.Identity), reads=[hb_], writes=[hbs.b])
                    for ql in range(4):
                        X, P1, P2 = Wk_[sets[ql]]
                        tix = d_ * 4 + ql
                        u = d_ * 32 + Q * 4 + ql
                        hb_ = hpb[tix]
                        rb = PRM4.t[:, 0, u:u + 1].to_broadcast([128, ncn])
                        for ri in range(2):
                            if bi_ == 0:
                                init_, irds = 0.0, []
                            else:
                                init_, irds = ini.t[:, tix, ri:ri + 1], [hb_]
                            p.op('dve', lambda e, X=X, P1=P1, rb=rb, init_=init_, ri=ri, ncn=ncn: e.tensor_tensor_scan(X.t[:, ri, 0:ncn], rb, P1.t[:, ri, 0:ncn], init_, ALU.mult, ALU.add),
                                 reads=[P1.b, PRM4.b] + irds, writes=[X.b])
                    for ql in range(4):
                        X, P1, P2 = Wk_[sets[ql]]
                        tix = d_ * 4 + ql
                        ecb = bc2(Ec.t[:, tix, 0:ncn])
                        p.op('dve', lambda e, X=X, P1=P1, ecb=ecb, ncn=ncn: e.tensor_mul(P1.t[:, :, 0:ncn], X.t[:, :, 0:ncn], ecb), reads=[X.b, Ec.b], writes=[P1.b])
                        p.op('pool', lambda e, X=X, P2=P2, tix=tix, ncn=ncn: e.tensor_mul(P2.t[:, :, 0:ncn], swap2(X, ncn), Esg.t[:, tix, :, 0:ncn]), reads=[X.b, Esg.b], writes=[P2.b])
                    for ql in range(4):
                        X, P1, P2 = Wk_[sets[ql]]
                        p.op('dve', lambda e, P1=P1, P2=P2, ncn=ncn: e.tensor_sub(P1.t[:, :, 0:ncn], P1.t[:, :, 0:ncn], P2.t[:, :, 0:ncn]), reads=[P1.b, P2.b], writes=[P1.b])
                    for ql in range(4):
                        X, P1, P2 = Wk_[sets[ql]]
                        tix = d_ * 4 + ql
                        hb_ = hpb[tix]
                        hbs = Hb[sets[ql]]
                        p.op('act', lambda e, tix=tix, P1=P1, ncn=ncn: e.activation(hp.t[:, tix, :], P1.t[:, :, ncn - 1], AF.Identity), reads=[P1.b], writes=[hb_])
                        if islat:
                            p.op('act', lambda e, hbs=hbs, P1=P1, ncn=ncn: e.activation(hbs.t[:, :, 1:ncn], P1.t[:, :, 0:ncn - 1], AF.Identity), reads=[P1.b], writes=[hbs.b])
                    if islat:
                        for t_ in range(LC):
                            for s_ in range(t_ + 1):
                                for ql in range(4):
                                    p.op('pe', lambda e, pyt=pyt, ql=ql, t_=t_, s_=s_, d_=d_, ncn=ncn, r_=rhs_all[ql][s_]: e.matmul(
                                        pyt.t[32 * ql:32 * ql + 32, t_, 0:ncn], Mw.t[32 * ql:32 * ql + 32, d_, t_ - s_, :], r_, start=(s_ == 0 and t_ == 0), stop=False,
                                        tile_position=(32 * ql, 32 * ql)), reads=[Mw.b, nT.b], writes=[pyt.b])
                        for t_ in range(LC):
                            for ri in range(2):
                                for ql in range(4):
                                    tix = d_ * 4 + ql
                                    hh_ = Hb[sets[ql]]
                                    p.op('pe', lambda e, pyt=pyt, ql=ql, t_=t_, ri=ri, tix=tix, hh_=hh_, ncn=ncn: e.matmul(
                                        pyt.t[32 * ql:32 * ql + 32, t_, 0:ncn], Wout.t[:, tix, t_, ri, :], hh_.t[:, ri, 0:ncn], start=False, stop=(ri == 1 and t_ == LC - 1),
                                        tile_position=(0, 32 * ql)), reads=[Wout.b, hh_.b], writes=[pyt.b])

                def yevac(k, Q=Q):
                    d_, bi_, c0, n, islat = blist[k]
                    if not islat:
                        return
                    gk = Q * 18 + k
                    ncn = n // LC
                    pyt = py[gk % 2]
                    if d_ == 0:
                        yv = yacc.t[:, c0:c0 + n].rearrange("p (c t) -> p t c", t=LC)
                        nv = nT.t[:, Q, c0:c0 + n].rearrange("p (c t) -> p t c", t=LC)
                        p.op('dve', lambda e, pyt=pyt, yv=yv, nv=nv, Q=Q: e.scalar_tensor_tensor(yv, nv, dT.t[:, Q:Q + 1], pyt.t[:, :, :], ALU.mult, ALU.add),
                             reads=[nT.b, dT.b, pyt.b], writes=[yacc.b])
                    else:
                        base = yacc.t[:, c0:c0 + n]
                        apl = [list(a_) for a_ in base.ap]
                        stp = apl[-1][0]
                        yv = bass.AP(base.tensor, base.offset + (n - 1) * stp, apl[:-1] + [[-stp, LC], [-stp * LC, ncn]])
                        p.op('dve', lambda e, pyt=pyt, yv=yv: e.tensor_add(yv, yv, pyt.t[:, :, :]), reads=[pyt.b, yacc.b], writes=[yacc.b])

                NB_ = len(blist)
                front_pe(0)
                front_ev(0)
                if NB_ > 1:
                    front_pe(1)
                for k in range(NB_):
                    if k + 1 < NB_:
                        front_ev(k + 1)
                    if k + 2 < NB_:
                        front_pe(k + 2)
                    back(k)
                    if k >= 1:
                        yevac(k - 1)
                    if k == 4 and Q + 1 < 8:
                        gen_weights(Q + 1)
                yevac(NB_ - 1)
                for b in range(4):
                    yg = ygb[0]
                    gt = gtb[0]
                    ysl = yacc.t[:, b * 512:(b + 1) * 512]
                    p.op('dve', lambda e, gt=gt, ysl=ysl: e.tensor_mul(gt.t[:, :], ysl, ysl), reads=[yacc.b], writes=[gt.b])
                    p.op('dve', lambda e, gt=gt: e.tensor_scalar(gt.t[:, :], gt.t[:, :], 0.044715, 1.0, ALU.mult, ALU.add), reads=[gt.b], writes=[gt.b])
                    p.op('dve', lambda e, gt=gt, ysl=ysl: e.tensor_mul(gt.t[:, :], gt.t[:, :], ysl), reads=[gt.b, yacc.b], writes=[gt.b])
                    p.op('act', lambda e, gt=gt: e.activation(gt.t[:, :], gt.t[:, :], AF.Sigmoid, scale=2.0 * math.sqrt(2.0 / math.pi)), reads=[gt.b], writes=[gt.b])
                    p.op('dve', lambda e, gt=gt, ysl=ysl, yg=yg: e.tensor_mul(yg.t[:, :], gt.t[:, :], ysl), reads=[gt.b, yacc.b], writes=[yg.b])
                    p.dma(YGs[Q, :, b * 512:(b + 1) * 512], yg.t[:, :], reads=[yg.b])
            p.barrier()
    with ExitStack() as st:
        Wgl = sbt(st, "s5Wgl", [128, 8, 2 * D], BF16)
        with ExitStack() as st2:
            stg = [sbt(st2, f"s5gst{i}", [128, 2 * D], F32) for i in range(2)]
            for k in range(8):
                s_ = stg[k % 2]
                p.dma(s_.t[:, 0:D], I["ssm_glu_w"][k * 128:(k + 1) * 128, 0:D], writes=[s_.b])
                p.dma(s_.t[:, D:2 * D], I["ssm_glu_w"][k * 128:(k + 1) * 128, D:2 * D], writes=[s_.b])
                ew(lambda e, k=k, s_=s_: e.tensor_copy(Wgl.t[:, k, 0:D], s_.t[:, 0:D]), [s_.b], [Wgl.b])
                ew(lambda e, k=k, s_=s_: e.tensor_copy(Wgl.t[:, k, D:2 * D], s_.t[:, D:2 * D]), [s_.b], [Wgl.b])
            p.barrier()
        g1 = load_gate(st, "s5g1", 1, 0, 0)
        ygt = [sbt(st, f"s5ygt{i}", [128, 8, 128], BF16) for i in range(2)]
        h2 = [sbt(st, f"s5h2{i}", [128, D], F32) for i in range(2)]
        h3 = [sbt(st, f"s5h3{i}", [128, D], F32) for i in range(2)]
        sg_ = [sbt(st, f"s5sg{i}", [128, 512], F32) for i in range(2)]
        pa = [pst(st, f"s5pa{i}", [128, 512]) for i in range(2)]
        pgl = [pst(st, f"s5pg{i}", [128, 512]) for i in range(2)]
        c_ = 0
        YGs4 = YGs.rearrange("q p (r t) -> q p r t", r=2)
        for ti in range(16):
            yt = ygt[ti % 2]
            p.dma(yt.t[:], YGs[:, :, ti * 128:(ti + 1) * 128].rearrange("q p t -> p q t"), writes=[yt.b])
            hh = h2[ti % 2]
            ho_ = h3[ti % 2]
            p.dma(hh.t[:], Hloc[ti * 128:(ti + 1) * 128, :], writes=[hh.b])
            for half in range(2):
                a = pa[c_ % 2]
                g = pgl[c_ % 2]
                sg2 = sg_[c_ % 2]
                c_ += 1
                for k in range(8):
                    p.op('pe', lambda e, a=a, k=k, yt=yt, half=half: e.matmul(a.t[:, :], yt.t[:, k, :], Wgl.t[:, k, half * 512:(half + 1) * 512], start=(k == 0), stop=(k == 7)),
                         reads=[yt.b, Wgl.b], writes=[a.b])
                for k in range(8):
                    p.op('pe', lambda e, g=g, k=k, yt=yt, half=half: e.matmul(g.t[:, :], yt.t[:, k, :], Wgl.t[:, k, D + half * 512:D + (half + 1) * 512], start=(k == 0), stop=(k == 7)),
                         reads=[yt.b, Wgl.b], writes=[g.b])
                p.op('act', lambda e, g=g, sg2=sg2: e.activation(sg2.t[:, :], g.t[:, :], AF.Sigmoid), reads=[g.b], writes=[sg2.b])
                p.op('dve', lambda e, a=a, sg2=sg2: e.tensor_mul(sg2.t[:, :], sg2.t[:, :], a.t[:, :]), reads=[a.b, sg2.b], writes=[sg2.b])
                p.op('pool', lambda e, sg2=sg2, half=half: e.tensor_mul(sg2.t[:, :], sg2.t[:, :], g1.t[:, half * 512:(half + 1) * 512]), reads=[sg2.b, g1.b], writes=[sg2.b])
                p.op('pool', lambda e, sg2=sg2, hh=hh, ho_=ho_, half=half: e.tensor_add(ho_.t[:, half * 512:(half + 1) * 512], hh.t[:, half * 512:(half + 1) * 512], sg2.t[:, :]),
                     reads=[sg2.b, hh.b], writes=[ho_.b])
            p.dma(Hloc[ti * 128:(ti + 1) * 128, :], ho_.t[:], reads=[ho_.b])
        p.barrier()


_CACHE = {}


def kernel(**inputs):
    stage = int(inputs.pop("_stage", 99))
    if "nc" not in _CACHE or _CACHE.get("stage") != stage:
        _CACHE["nc"] = build(stage)
        _CACHE["stage"] = stage
        _CACHE["consts"] = host_consts()
    nc = _CACHE["nc"]
    if "consts_rev" not in _CACHE:
        _CACHE["consts_rev"] = host_consts(rev=True)
    f = lambda a: np.ascontiguousarray(np.asarray(a, dtype=np.float32))
    in_maps = []
    for core in range(8):
        b = core // 2
        rv = (core % 2 == 1) and stage >= 99
        cs = _CACHE["consts_rev"] if rv else _CACHE["consts"]
        sd = (lambda a: np.asarray(a)[0][::-1]) if rv else (lambda a: np.asarray(a)[0])
        xb = np.asarray(inputs["x"][b])
        cb = np.asarray(inputs["ctx"][b])
        if rv:
            xb = xb[::-1]
            cb = cb[::-1]
        m = {
            "x": f(xb), "c": f(inputs["c"][b]).reshape(8, 128), "ctx": f(cb),
            "c_ctx": f(inputs["c_ctx"]).reshape(8, 128),
            "mod_w": f(inputs["mod_w"]), "mod_b": f(inputs["mod_b"]).reshape(2, 48, 128),
            "norm_g": f(inputs["norm_g"]).reshape(2, 2, 8, 128),
            "ffn_w_gate": f(inputs["ffn_w_gate"]), "ffn_w_up": f(inputs["ffn_w_up"]), "ffn_w_down": f(inputs["ffn_w_down"]),
            "mix_w_in": f(inputs["mix_w_in"][0]), "mix_w_out": f(inputs["mix_w_out"][0]), "attn_sink": f(inputs["attn_sink"]).reshape(1, 8),
            "ssm_a_re": f(sd(inputs["ssm_a_re"])).reshape(128, 64), "ssm_a_im": f(sd(inputs["ssm_a_im"])).reshape(128, 64),
            "ssm_log_dt": f(sd(inputs["ssm_log_dt"])).reshape(1, 128),
            "ssm_b_re": f(sd(inputs["ssm_b_re"])), "ssm_b_im": f(sd(inputs["ssm_b_im"])),
            "ssm_c_re": f(sd(inputs["ssm_c_re"])), "ssm_c_im": f(sd(inputs["ssm_c_im"])),
            "ssm_d": f(inputs["ssm_d"][0]).reshape(8, 128), "ssm_glu_w": f(inputs["ssm_glu_w"][0]), "final_g": f(inputs["final_g"]).reshape(1, D),
        }
        m.update(cs)
        m["rk"] = np.array([[0]], np.int32)
        in_maps.append(m)
    res = run_bass_kernel_spmd(nc, in_maps, core_ids=list(range(8)))
    if stage < 99:
        return np.stack([np.asarray(res.results[2 * b]["out"], dtype=np.float32) for b in range(4)], axis=0)
    outp = np.stack([np.concatenate([np.asarray(res.results[2 * b]["out"], dtype=np.float32),
                                     np.asarray(res.results[2 * b + 1]["out"], dtype=np.float32)[::-1]], axis=0) for b in range(4)], axis=0)
    return outp
```

```python
import math
import numpy as np
import ml_dtypes
import concourse.bass as bass
import concourse.mybir as mybir
from concourse.bass_utils import run_bass_kernel_spmd
from contextlib import ExitStack

F32 = mybir.dt.float32
BF16 = mybir.dt.bfloat16
AF = mybir.ActivationFunctionType
ALU = mybir.AluOpType
NPBF = ml_dtypes.bfloat16

D = 1024
T = 4096
TC = 256
FF = 2816
NFC = 22
EPS = 1e-6


class Buf:
    def __init__(self, name=""):
        self.name = name
        self.last_w = None
        self.readers = []


class Prog:
    ENG = ['pe', 'act', 'dve', 'pool', 'sp']

    def __init__(self, nc, ndma_sems=10):
        self.nc = nc
        self.ops = {e: [] for e in self.ENG}
        self.cnt = {e: 0 for e in self.ENG}
        self.known = {e: {} for e in self.ENG}
        self.es = ExitStack()
        self.sem = {e: self.es.enter_context(nc.semaphore('s_' + e)) for e in self.ENG}
        self.dma_sems = {e: [self.es.enter_context(nc.semaphore(f'd_{e}{i}')) for i in range(ndma_sems)]
                         for e in ['sp', 'act', 'pool']}
        self.dma_val = {e: [0] * ndma_sems for e in self.dma_sems}
        self.dma_rr = {e: 0 for e in self.dma_sems}
        self.semobj = {}
        for e in self.ENG:
            self.semobj[('c', e)] = self.sem[e]
        for e in self.dma_sems:
            for i, s in enumerate(self.dma_sems[e]):
                self.semobj[('d', e, i)] = s
        self.q = 0
        self.rank_ap = None

    def _waits(self, eng, toks):
        need = {}
        for t in toks:
            if t is None:
                continue
            k, v = t
            if k == ('c', eng) and eng == 'pe':
                continue
            if self.known[eng].get(k, 0) >= v:
                continue
            if need.get(k, 0) < v:
                need[k] = v
        for k, v in need.items():
            self.known[eng][k] = v
        return list(need.items())

    def _deps(self, reads, writes):
        toks = []
        for b in reads:
            toks.append(b.last_w)
        for b in writes:
            toks.append(b.last_w)
            toks.extend(b.readers)
        return toks

    def _commit(self, tok, reads, writes):
        for b in reads:
            b.readers.append(tok)
            if len(b.readers) > 64:
                b.readers = b.readers[-64:]
        for b in writes:
            b.last_w = tok
            b.readers = []

    def op(self, eng, fn, reads=(), writes=()):
        waits = self._waits(eng, self._deps(reads, writes))
        self.cnt[eng] += 1
        tok = (('c', eng), self.cnt[eng])
        self.ops[eng].append((waits, fn, (self.sem[eng], 1)))
        self._commit(tok, reads, writes)
        return tok

    def dma(self, out, in_, reads=(), writes=(), eng=None, **kw):
        if eng is None:
            eng = ['sp', 'act', 'pool'][self.q % 2]
            self.q += 1
        toks = self._deps(reads, writes)
        i = self.dma_rr[eng]
        self.dma_rr[eng] = (i + 1) % len(self.dma_sems[eng])
        key = ('d', eng, i)
        prev = self.dma_val[eng][i]
        if prev > 0:
            toks.append((key, prev))
        waits = self._waits(eng, toks)
        self.dma_val[eng][i] = prev + 16
        tok = (key, prev + 16)
        def issue(e, out=out, in_=in_, eng=eng):
            o = out(self.dyn[eng]) if callable(out) else out
            i2 = in_(self.dyn[eng]) if callable(in_) else in_
            return e.dma_start(out=o, in_=i2, **kw)
        self.ops[eng].append((waits, issue, (self.dma_sems[eng][i], 16)))
        self._commit(tok, reads, writes)
        return tok

    def all_tokens(self):
        allt = []
        for e in self.ENG:
            if self.cnt[e]:
                allt.append((('c', e), self.cnt[e]))
        for e in self.dma_sems:
            for i, v in enumerate(self.dma_val[e]):
                if v:
                    allt.append((('d', e, i), v))
        return allt

    def barrier(self):
        allt = self.all_tokens()
        for e in self.ENG:
            w = self._waits(e, allt)
            if w:
                self.ops[e].append((w, None, None))

    def emit(self):
        nc = self.nc
        fin = self._waits('sp', self.all_tokens())
        self.ops['sp'].append((fin, None, None))
        self.dyn = {}
        with nc.Block() as block:
            def mk(eng):
                def run(e):
                    for waits, fn, inc in self.ops[eng]:
                        for k, v in waits:
                            e.wait_ge(self.semobj[k], v)
                        if fn is not None:
                            fn(e).then_inc(inc[0], inc[1])

                def body(e):
                    if eng in ('sp', 'act') and self.rank_ap is not None:
                        with e.register("rk_" + eng) as reg:
                            e.reg_load(reg, self.rank_ap)
                            self.dyn[eng] = e.snap(reg, min_val=0, max_val=2048)
                            run(e)
                    else:
                        run(e)
                return body
            block.tensor(mk('pe'))
            block.scalar(mk('act'))
            block.vector(mk('dve'))
            block.gpsimd(mk('pool'))
            block.sync(mk('sp'))
        self.es.close()


class TB:
    def __init__(self, t, name=""):
        self.t = t
        self.b = Buf(name)


def host_consts(rev=False):
    cs = {}
    cs["identf"] = np.eye(128, dtype=np.float32)
    cs["identb"] = np.eye(128, dtype=np.float32).astype(NPBF)
    t = np.arange(T)
    row = (t // 64).astype(np.float64)
    col = (t % 64).astype(np.float64)
    nf = 16
    inv = 10000.0 ** (-np.arange(nf, dtype=np.float64) / nf)
    inv = inv.astype(np.float32).astype(np.float64)
    ang = np.concatenate([(row[:, None].astype(np.float32) * inv[None].astype(np.float32)),
                          (col[:, None].astype(np.float32) * inv[None].astype(np.float32))], axis=-1).astype(np.float32)
    cosv = np.cos(ang).astype(np.float32)
    sinv = np.sin(ang).astype(np.float32)
    C = np.zeros((128, T), np.float32)
    S = np.zeros((128, T), np.float32)
    for p in range(128):
        d = p % 64
        i = d // 2
        C[p] = cosv[:, i]
        S[p] = sinv[:, i] * (-1.0 if d % 2 == 0 else 1.0)
    if rev:
        C = np.ascontiguousarray(C[:, ::-1])
        S = np.ascontiguousarray(S[:, ::-1])
    cs["ropeC"] = C
    cs["ropeS"] = S
    j = np.arange(128)[:, None]
    i = np.arange(128)[None, :]
    mp = np.where(j >= i, 0.0, -30000.0).astype(np.float32)
    mn = np.where(j <= i, 0.0, -30000.0).astype(np.float32)
    cs["maskP"] = np.tile(mp, (1, 4)).astype(NPBF)
    cs["maskN"] = np.tile(mn, (1, 4)).astype(NPBF)
    tt = np.arange(T, dtype=np.int64)
    tk = (tt[:, None] * tt[None, :]) % T
    angT = 2.0 * np.pi * tk / T
    ct = (np.cos(angT) / math.sqrt(T)).astype(np.float32)
    stt = (np.sin(angT) / math.sqrt(T)).astype(np.float32)
    if rev:
        ct = np.ascontiguousarray(ct[::-1, ::-1])
        stt = np.ascontiguousarray(stt[::-1, ::-1])
    cs["CT"] = np.ascontiguousarray(ct.reshape(32, 128, 16, 256).transpose(2, 1, 0, 3)).astype(NPBF)
    cs["ST"] = np.ascontiguousarray(stt.reshape(32, 128, 16, 256).transpose(2, 1, 0, 3)).astype(NPBF)
    t2 = np.arange(TC, dtype=np.int64)
    a2 = 2.0 * np.pi * ((t2[:, None] * t2[None, :]) % TC) / TC
    c2 = (np.cos(a2) / math.sqrt(TC)).astype(np.float32)
    s2 = (np.sin(a2) / math.sqrt(TC)).astype(np.float32)
    if rev:
        c2 = np.ascontiguousarray(c2[::-1, ::-1])
        s2 = np.ascontiguousarray(s2[::-1, ::-1])
    cs["C256"] = np.ascontiguousarray(c2.reshape(2, 128, 256).transpose(1, 0, 2)).astype(NPBF)
    cs["S256"] = np.ascontiguousarray(s2.reshape(2, 128, 256).transpose(1, 0, 2)).astype(NPBF)
    c64 = np.arange(64)
    a3 = 2.0 * np.pi * ((c64[:, None] * c64[None, :]) % 64) / 64
    cc = np.zeros((128, 128), np.float32)
    sc = np.zeros((128, 128), np.float32)
    for g in range(2):
        cc[g * 64:(g + 1) * 64, g * 64:(g + 1) * 64] = np.cos(a3) / 8.0
        sc[g * 64:(g + 1) * 64, g * 64:(g + 1) * 64] = np.sin(a3) / 8.0
    cs["Cc"] = cc.astype(NPBF)
    cs["Sc"] = sc.astype(NPBF)
    return cs


CONST_SHAPES = {
    "identf": ([128, 128], F32), "identb": ([128, 128], BF16), "ropeC": ([128, T], F32), "ropeS": ([128, T], F32),
    "maskP": ([128, 512], BF16), "maskN": ([128, 512], BF16),
    "CT": ([16, 128, 32, 256], BF16), "ST": ([16, 128, 32, 256], BF16),
    "C256": ([128, 2, 256], BF16), "S256": ([128, 2, 256], BF16), "Cc": ([128, 128], BF16), "Sc": ([128, 128], BF16),
}

IN_SHAPES = {
    "x": [T, D], "c": [8, 128], "ctx": [TC, D], "c_ctx": [8, 128],
    "mod_w": [2, D, 6 * D], "mod_b": [2, 48, 128], "norm_g": [2, 2, 8, 128],
    "ffn_w_gate": [2, D, FF], "ffn_w_up": [2, D, FF], "ffn_w_down": [2, FF, D],
    "mix_w_in": [D, 1280], "mix_w_out": [D, D], "attn_sink": [1, 8],
    "ssm_a_re": [128, 64], "ssm_a_im": [128, 64], "ssm_log_dt": [1, 128],
    "ssm_b_re": [2, 64, 64, 16], "ssm_b_im": [2, 64, 64, 16], "ssm_c_re": [2, 64, 16, 64], "ssm_c_im": [2, 64, 16, 64],
    "ssm_d": [8, 128], "ssm_glu_w": [D, 2 * D], "final_g": [1, D],
}


def build(stage=99):
    nc = bass.Bass("TRN2", target_bir_lowering=False)
    I = {n: nc.dram_tensor(n, sh, F32, kind="ExternalInput").ap() for n, sh in IN_SHAPES.items()}
    K = {n: nc.dram_tensor(n, sh, dt, kind="ExternalInput").ap() for n, (sh, dt) in CONST_SHAPES.items()}
    rk_in = nc.dram_tensor("rk", [1, 1], mybir.dt.int32, kind="ExternalInput").ap()
    HT = T // 2
    out = nc.dram_tensor("out", [T if stage < 99 else HT, D], F32, kind="ExternalOutput").ap()
    Hs = nc.dram_tensor("Hs", [T + TC, D], F32).ap()
    mod_b_flat = I["mod_b"].rearrange("l j p -> l (j p)")

    p = Prog(nc)
    p.rank_ap = None
    top = ExitStack()
    Hloc = nc.dram_tensor("Hloc", [T // 2, D], F32).ap()
    p.Hloc = Hloc

    uid = {"n": 0}

    def sbt(st, name, shape, dt):
        uid["n"] += 1
        return TB(st.enter_context(nc.sbuf_tensor(f"s{uid['n']}_{name}", shape, dt)), name)

    def pst(st, name, shape, dt=F32):
        uid["n"] += 1
        return TB(st.enter_context(nc.psum_tensor(f"p{uid['n']}_{name}", shape, dt)), name)

    identf = sbt(top, "identf", [128, 128], F32)
    identb = sbt(top, "identb", [128, 128], BF16)
    p.dma(identf.t[:], K["identf"], writes=[identf.b])
    p.dma(identb.t[:], K["identb"], writes=[identb.b])
    ones_b = sbt(top, "ones_b", [128, 128], BF16)
    p.op('pool', lambda e: e.memset(ones_b.t[:], 1.0), writes=[ones_b.b])
    AB = sbt(top, "AB", [128, 2, 2, 4, 8], F32)
    Gs = nc.dram_tensor("Gs", [8, 128, D], F32).ap()
    ATs = nc.dram_tensor("ATs", [34, 128, 4, 128], BF16).ap()

    def load_gate(st, nm, l, col, gi):
        g = sbt(st, nm, [128, D], F32)
        p.dma(g.t[:], Gs[(l * 2 + col) * 2 + gi], writes=[g.b])
        return g
    rr = {"i": 0}

    def ew(fn, reads, writes, engs=('dve', 'pool')):
        e = engs[rr["i"] % len(engs)]
        rr["i"] += 1
        return p.op(e, fn, reads=reads, writes=writes)

    with ExitStack() as st:
        gates = sbt(st, "gates", [128, 2, 2, 2, D], F32)
        crow = sbt(st, "crow", [16, 128], F32)
        p.dma(crow.t[0:8, :], I["c"], writes=[crow.b])
        p.dma(crow.t[8:16, :], I["c_ctx"], writes=[crow.b])
        pT = pst(st, "pT", [128, 96])
        scT = sbt(st, "scT", [128, 16], F32)
        p.op('pe', lambda e: e.transpose(pT.t[:, 0:16], crow.t[:, :], identf.t[0:16, 0:16]), reads=[crow.b, identf.b], writes=[pT.b])
        p.op('act', lambda e: e.activation(scT.t[:], pT.t[:, 0:16], AF.Silu), reads=[pT.b], writes=[scT.b])
        scbc = sbt(st, "scbc", [128, 16, 128], F32)
        for ck in range(16):
            ew(lambda e, ck=ck: e.tensor_copy(scbc.t[:, ck, :], scT.t[:, ck:ck + 1].to_broadcast([128, 128])), [scT.b], [scbc.b])
        mbrow = sbt(st, "mbrow", [48, 2, 128], F32)
        ngrow = sbt(st, "ngrow", [32, 128], F32)
        p.dma(mbrow.t[:, 0, :], I["mod_b"][0], writes=[mbrow.b])
        p.dma(mbrow.t[:, 1, :], I["mod_b"][1], writes=[mbrow.b])
        p.dma(ngrow.t[:], I["norm_g"].rearrange("l i k p -> (l i k) p"), writes=[ngrow.b])
        mbT = sbt(st, "mbT", [128, 2, 48], F32)
        ngT = sbt(st, "ngT", [128, 32], F32)
        for l in range(2):
            p.op('pe', lambda e, l=l: e.transpose(pT.t[:, 0:48], mbrow.t[:, l, :], identf.t[0:48, 0:48]), reads=[mbrow.b, identf.b], writes=[pT.b])
            p.op('dve', lambda e, l=l: e.tensor_copy(mbT.t[:, l, :], pT.t[:, 0:48]), reads=[pT.b], writes=[mbT.b])
        p.op('pe', lambda e: e.transpose(pT.t[:, 0:32], ngrow.t[:, :], identf.t[0:32, 0:32]), reads=[ngrow.b, identf.b], writes=[pT.b])
        p.op('dve', lambda e: e.tensor_copy(ngT.t[:], pT.t[:, 0:32]), reads=[pT.b], writes=[ngT.b])
        macc = sbt(st, "macc", [128, 2, 48, 2], F32)
        Wk = [sbt(st, f"Wk{i}", [128, 6 * D], F32) for i in range(2)]
        pg = [pst(st, f"pg{i}", [128, 512]) for i in range(2)]
        pm = pst(st, "pm", [128, 96])
        it = 0
        for l in range(2):
            for k in range(8):
                w = Wk[it % 2]
                it += 1
                for q3 in range(3):
                    p.dma(w.t[:, q3 * 2048:(q3 + 1) * 2048], I["mod_w"][l, k * 128:(k + 1) * 128, q3 * 2048:(q3 + 1) * 2048], writes=[w.b])
                for j in range(48):
                    p.op('pe', lambda e, j=j, w=w, k=k: e.matmul(pm.t[:, 2 * j:2 * j + 2], w.t[:, j * 128:(j + 1) * 128],
                                                                 scT.t[:, k:16:8], start=True, stop=True),
                         reads=[w.b, scT.b], writes=[pm.b])
                if k == 0:
                    p.op('dve', lambda e, l=l: e.tensor_copy(macc.t[:, l].rearrange("p j c -> p (j c)"), pm.t[:, :]), reads=[pm.b], writes=[macc.b])
                else:
                    p.op('dve', lambda e, l=l: e.tensor_add(macc.t[:, l].rearrange("p j c -> p (j c)"), macc.t[:, l].rearrange("p j c -> p (j c)"), pm.t[:, :]),
                         reads=[pm.b, macc.b], writes=[macc.b])
                gi = 0
                for col in range(2):
                    for g_i, which in enumerate((2, 5)):
                        for half in range(2):
                            pgt = pg[gi % 2]
                            gi += 1
                            p.op('pe', lambda e, pgt=pgt, col=col, k=k, w=w, which=which, half=half: e.matmul(
                                pgt.t[:, :], scbc.t[:, col * 8 + k, :], w.t[:, which * D + half * 512: which * D + half * 512 + 512], start=True, stop=True),
                                reads=[scbc.b, w.b], writes=[pgt.b])
                            dst = gates.t[:, l, col, g_i, half * 512:(half + 1) * 512]
                            if k == 0:
                                p.op('dve', lambda e, dst=dst, pgt=pgt: e.tensor_copy(dst, pgt.t[:, :]), reads=[pgt.b], writes=[gates.b])
                            else:
                                p.op('dve', lambda e, dst=dst, pgt=pgt: e.tensor_add(dst, dst, pgt.t[:, :]), reads=[pgt.b, gates.b], writes=[gates.b])
        gb = sbt(st, "gb", [128, D], F32)
        for l in range(2):
            for col in range(2):
                p.op('dve', lambda e, l=l, col=col: e.tensor_add(macc.t[:, l, :, col], macc.t[:, l, :, col], mbT.t[:, l, :]), reads=[macc.b, mbT.b], writes=[macc.b])
            for g_i, which in enumerate((2, 5)):
                p.dma(gb.t[:], mod_b_flat[l:l + 1, which * D:(which + 1) * D].partition_broadcast(128), writes=[gb.b])
                for col in range(2):
                    p.op('dve', lambda e, l=l, col=col, g_i=g_i: e.tensor_add(gates.t[:, l, col, g_i, :], gates.t[:, l, col, g_i, :], gb.t[:]),
                         reads=[gb.b, gates.b], writes=[gates.b])
            for col in range(2):
                for i2 in range(2):
                    sh = macc.t[:, l, (3 * i2) * 8:(3 * i2) * 8 + 8, col]
                    scl = macc.t[:, l, (3 * i2 + 1) * 8:(3 * i2 + 1) * 8 + 8, col]
                    gn = ngT.t[:, (l * 2 + i2) * 8:(l * 2 + i2) * 8 + 8]
                    p.op('dve', lambda e, l=l, col=col, i2=i2, scl=scl, gn=gn: e.scalar_tensor_tensor(
                        AB.t[:, l, col, 2 * i2, :], scl, 1.0, gn, ALU.add, ALU.mult), reads=[macc.b, ngT.b], writes=[AB.b])
                    p.op('dve', lambda e, l=l, col=col, i2=i2, sh=sh: e.tensor_copy(AB.t[:, l, col, 2 * i2 + 1, :], sh), reads=[macc.b], writes=[AB.b])
        for l in range(2):
            for col in range(2):
                for g_i in range(2):
                    p.dma(Gs[(l * 2 + col) * 2 + g_i], gates.t[:, l, col, g_i, :], reads=[gates.b])
        p.barrier()

    if stage == 0:
        with ExitStack() as st:
            g0 = load_gate(st, "g0dbg", 0, 0, 0)
            p.dma(out[0:128, :], g0.t[:], reads=[g0.b])
        p.dma(out[128:256, 0:128], AB.t[:].rearrange("p a b c d -> p (a b c d)"), reads=[AB.b])
        p.emit()
        top.close()
        return nc

    def make_norm(st, nm, nxt=2):
        ctxn = {}
        ctxn["xt"] = [sbt(st, f"{nm}xt{i}", [128, D], F32) for i in range(nxt)]
        ctxn["junk"] = sbt(st, f"{nm}junk", [128, D], BF16)
        ctxn["xn"] = [sbt(st, f"{nm}xn{i}", [128, D], BF16) for i in range(2)]
        ctxn["ss"] = [sbt(st, f"{nm}ss{i}", [128, 1], F32) for i in range(3)]
        ctxn["tp"] = [pst(st, f"{nm}tp{i}", [128, 8, 128], BF16) for i in range(2)]
        ctxn["n"] = 0
        return ctxn

    def norm_tile(cn, src, l, col, which, dst_fn, dst_buf, xt_fixed=None, ident=None):
        n = cn["n"]
        cn["n"] += 1
        xt = xt_fixed if xt_fixed is not None else cn["xt"][n % len(cn["xt"])]
        ss = cn["ss"][n % 3]
        xn = cn["xn"][n % 2]
        tp = cn["tp"][n % 2]
        junk = cn["junk"]
        idt = ident if ident is not None else identb
        p.dma(xt.t[:], src, writes=[xt.b])
        p.op('act', lambda e: e.activation(junk.t[:], xt.t[:], AF.Square, accum_out=ss.t[:]), reads=[xt.b], writes=[junk.b, ss.b])
        p.op('dve', lambda e: e.tensor_scalar(ss.t[:], ss.t[:], 1.0 / D, EPS, ALU.mult, ALU.add), reads=[ss.b], writes=[ss.b])
        p.op('act', lambda e: e.activation(ss.t[:], ss.t[:], AF.Sqrt), reads=[ss.b], writes=[ss.b])
        p.op('dve', lambda e: e.reciprocal(ss.t[:], ss.t[:]), reads=[ss.b], writes=[ss.b])
        p.op('dve', lambda e: e.tensor_scalar(xn.t[:], xt.t[:], ss.t[:, 0:1], None, ALU.mult), reads=[xt.b, ss.b], writes=[xn.b])
        for k in range(8):
            p.op('pe', lambda e, k=k: e.transpose(tp.t[:, k, :], xn.t[:, k * 128:(k + 1) * 128], idt.t[:]), reads=[xn.b, idt.b], writes=[tp.b])
        for k in range(8):
            p.op('act', lambda e, k=k: e.activation(dst_fn(k), tp.t[:, k, :], AF.Identity,
                                                    scale=AB.t[:, l, col, 2 * which, k:k + 1], bias=AB.t[:, l, col, 2 * which + 1, k:k + 1]),
                 reads=[tp.b, AB.b], writes=[dst_buf])
        return xt, ss

    def tile_src(ti):
        return I["x"][ti * 128:(ti + 1) * 128, :] if ti < 32 else I["ctx"][(ti - 32) * 128:(ti - 31) * 128, :]

    NT = 34
    L0 = ExitStack()
    Fm = sbt(L0, "Fm", [128, NT, 512], BF16)
    with ExitStack() as stBC:
        QT = sbt(stBC, "QT", [128, 4, NT * 128], BF16)
        KT = sbt(stBC, "KT", [128, NT * 128], BF16)
        Vm = sbt(stBC, "Vm", [128, NT, 128], BF16)
        with ExitStack() as st:
            wb = sbt(st, "wb", [128, 8, 1920], BF16)
            wst = [sbt(st, f"wst{i}", [128, 1280], F32) for i in range(1)]
            for k in range(8):
                s_ = wst[0]
                p.dma(s_.t[:], I["mix_w_in"][k * 128:(k + 1) * 128, :], writes=[s_.b])
                S = s_.t
                W = wb.t
                ew(lambda e, k=k, S=S, W=W: e.tensor_copy(W[:, k, 0:512], S[:, 0:512]), [s_.b], [wb.b])
                ew(lambda e, k=k, S=S, W=W: e.tensor_copy(W[:, k, 512:640], S[:, 1152:1280]), [s_.b], [wb.b])
                ew(lambda e, k=k, S=S, W=W: e.tensor_copy(W[:, k, 640:1152].rearrange("p (j h d) -> p j h d", j=4, h=2),
                                                          S[:, 512:1024].rearrange("p (h j d) -> p j h d", h=2, j=4)), [s_.b], [wb.b])
                ew(lambda e, k=k, S=S, W=W: e.tensor_copy(W[:, k, 1152:1280], S[:, 1024:1152]), [s_.b], [wb.b])
                for two in range(2):
                    ew(lambda e, k=k, S=S, W=W, two=two: e.tensor_copy(
                        W[:, k, 1280:1792].rearrange("p (j h i t) -> p j h i t", j=4, h=2, t=2)[:, :, :, :, two],
                        S[:, 512:1024].rearrange("p (h j i t) -> p j h i t", h=2, j=4, t=2)[:, :, :, :, 1 - two]), [s_.b], [wb.b])
                    ew(lambda e, k=k, S=S, W=W, two=two: e.tensor_copy(
                        W[:, k, 1792:1920].rearrange("p (i t) -> p i t", t=2)[:, :, two],
                        S[:, 1024:1152].rearrange("p (i t) -> p i t", t=2)[:, :, 1 - two]), [s_.b], [wb.b])
            ropeCb = [sbt(st, f"ropeC{i}", [128, 512], F32) for i in range(2)]
            ropeSb = [sbt(st, f"ropeS{i}", [128, 512], F32) for i in range(2)]
            cn = make_norm(st, "b")
            nTb = [sbt(st, f"nTb{i}", [128, 8, 512], BF16) for i in range(2)]
            pf = pst(st, "pf", [128, 512])
            pv = pst(st, "pv", [128, 128])
            pq = [pst(st, f"pq{i}", [128, 512]) for i in range(2)]
            pqp = [pst(st, f"pqp{i}", [128, 512]) for i in range(2)]
            t1 = [sbt(st, f"t1{i}", [128, 512], F32) for i in range(2)]
            t2 = [sbt(st, f"t2{i}", [128, 512], F32) for i in range(2)]
            qi_ = 0
            for blk in range(9):
                ntile = 4 if blk < 8 else 2
                ncol = ntile * 128
                nT = nTb[blk % 2]
                col = 0 if blk < 8 else 1
                for tt in range(ntile):
                    ti = blk * 4 + tt
                    norm_tile(cn, tile_src(ti), 0, col, 0, lambda k, tt=tt, nT=nT: nT.t[:, k, tt * 128:(tt + 1) * 128], nT.b)
                    for k in range(8):
                        p.op('pe', lambda e, k=k, tt=tt, nT=nT: e.matmul(pf.t[:, :], nT.t[:, k, tt * 128:(tt + 1) * 128], wb.t[:, k, 0:512],
                                                                      start=(k == 0), stop=(k == 7)), reads=[nT.b, wb.b], writes=[pf.b])
                    p.op('act', lambda e, ti=ti: e.activation(Fm.t[:, ti, :], pf.t[:, :], AF.Identity), reads=[pf.b], writes=[Fm.b])
                    for k in range(8):
                        p.op('pe', lambda e, k=k, tt=tt, nT=nT: e.matmul(pv.t[:, :], nT.t[:, k, tt * 128:(tt + 1) * 128], wb.t[:, k, 512:640],
                                                                      start=(k == 0), stop=(k == 7)), reads=[nT.b, wb.b], writes=[pv.b])
                    p.op('dve', lambda e, ti=ti: e.tensor_copy(Vm.t[:, ti, :], pv.t[:, :]), reads=[pv.b], writes=[Vm.b])
                c0 = blk * 512
                ropeC = ropeCb[blk % 2]
                ropeS = ropeSb[blk % 2]
                if blk < 8:
                    p.dma(ropeC.t[:], K["ropeC"][:, c0:c0 + 512], writes=[ropeC.b])
                    p.dma(ropeS.t[:], K["ropeS"][:, c0:c0 + 512], writes=[ropeS.b])
                for oc in range(5):
                    a = pq[qi_ % 2]
                    bq = pqp[qi_ % 2]
                    u1 = t1[qi_ % 2]
                    u2 = t2[qi_ % 2]
                    qi_ += 1
                    for k in range(8):
                        p.op('pe', lambda e, k=k, oc=oc, a=a, nT=nT, ncol=ncol: e.matmul(a.t[:, 0:ncol], wb.t[:, k, 640 + 128 * oc: 768 + 128 * oc], nT.t[:, k, 0:ncol],
                                                                                   start=(k == 0), stop=(k == 7)), reads=[nT.b, wb.b], writes=[a.b])
                    dstb = QT.b if oc < 4 else KT.b
                    dst = QT.t[:, oc, c0:c0 + ncol] if oc < 4 else KT.t[:, c0:c0 + ncol]
                    if blk < 8:
                        for k in range(8):
                            p.op('pe', lambda e, k=k, oc=oc, bq=bq, nT=nT, ncol=ncol: e.matmul(bq.t[:, 0:ncol], wb.t[:, k, 1280 + 128 * oc: 1408 + 128 * oc], nT.t[:, k, 0:ncol],
                                                                                        start=(k == 0), stop=(k == 7)), reads=[nT.b, wb.b], writes=[bq.b])
                        p.op('dve', lambda e, a=a, u1=u1, ropeC=ropeC: e.tensor_mul(u1.t[:, :], a.t[:, :], ropeC.t[:, :]), reads=[a.b, ropeC.b], writes=[u1.b])
                        p.op('dve', lambda e, bq=bq, u2=u2, ropeS=ropeS: e.tensor_mul(u2.t[:, :], bq.t[:, :], ropeS.t[:, :]), reads=[bq.b, ropeS.b], writes=[u2.b])
                        p.op('pool', lambda e, dst=dst, u1=u1, u2=u2: e.tensor_add(dst, u1.t[:, :], u2.t[:, :]), reads=[u1.b, u2.b], writes=[dstb])
                    else:
                        p.op('dve', lambda e, dst=dst, a=a, ncol=ncol: e.tensor_copy(dst, a.t[:, 0:ncol]), reads=[a.b], writes=[dstb])
            p.barrier()
        with ExitStack() as st:
            maskP = sbt(st, "maskP", [128, 512], BF16)
            maskN = sbt(st, "maskN", [128, 512], BF16)
            p.dma(maskP.t[:], K["maskP"], writes=[maskP.b])
            p.dma(maskN.t[:], K["maskN"], writes=[maskN.b])
            sk = sbt(st, "sk", [128, 8], F32)
            p.dma(sk.t[:], I["attn_sink"].partition_broadcast(128), writes=[sk.b])
            p.op('act', lambda e: e.activation(sk.t[:], sk.t[:], AF.Exp), reads=[sk.b], writes=[sk.b])
            skf = sbt(st, "skf", [128, 512], F32)
            for g in range(2):
                for hh in range(4):
                    p.op('dve', lambda e, g=g, hh=hh: e.tensor_copy(skf.t[64 * g:64 * g + 64, hh * 128:(hh + 1) * 128],
                                                                  sk.t[64 * g:64 * g + 64, 4 * g + hh:4 * g + hh + 1].to_broadcast([64, 128])), reads=[sk.b], writes=[skf.b])
            pS = [pst(st, f"pS{i}", [128, 512]) for i in range(3)]
            pO = [pst(st, f"pO{i}", [128, 512]) for i in range(2)]
            pZ = [pst(st, f"pZ{i}", [128, 512]) for i in range(2)]
            PT = [sbt(st, f"PT{i}", [128, 512], BF16) for i in range(3)]
            den = [sbt(st, f"den{i}", [128, 512], F32) for i in range(2)]
            ATt = [sbt(st, f"ATt{i}", [128, 4, 128], BF16) for i in range(2)]
            OB = [[Buf() for _ in range(2)] for _ in range(2)]
            ZB = [[Buf() for _ in range(2)] for _ in range(2)]
            DB = [[Buf() for _ in range(2)] for _ in range(2)]
            si = 0
            pend = {"t": None}
            for qi in range(NT):
                AT = ATt[qi % 2]
                for g in range(2):
                    lo, hi = 64 * g, 64 * g + 64
                    keys = [(32, None), (33, None)]
                    if qi < 32:
                        if qi > 0:
                            keys.append((qi - 1, maskP))
                        keys.append((qi, None))
                        if qi < 31:
                            keys.append((qi + 1, maskN))
                    O = pO[qi % 2]
                    Z = pZ[qi % 2]
                    Ob = OB[qi % 2][g]
                    Zb = ZB[qi % 2][g]
                    Db = DB[qi % 2][g]
                    SP = []
                    for ki in range(len(keys)):
                        SP.append((pS[si % 3], PT[si % 3]))
                        si += 1

                    def emitS(ki):
                        kt, msk = keys[ki]
                        S_ = SP[ki][0]
                        p.op('pe', lambda e, S_=S_, kt=kt, qi=qi, lo=lo, hi=hi, msk=msk: e.matmul(
                            S_.t[:, :].rearrange("p (j q) -> p j q", j=4), KT.t[lo:hi, kt * 128:(kt + 1) * 128], QT.t[lo:hi, :, qi * 128:(qi + 1) * 128],
                            start=True, stop=(msk is None)), reads=[KT.b, QT.b], writes=[S_.b])
                        if msk is not None:
                            p.op('pe', lambda e, S_=S_, msk=msk: e.matmul(S_.t[:, :], identb.t[:, :], msk.t[:, :], start=False, stop=True),
                                 reads=[identb.b, msk.b], writes=[S_.b])
                    emitS(0)
                    if len(keys) > 1:
                        emitS(1)
                    for ki, (kt, msk) in enumerate(keys):
                        S_, P_ = SP[ki]
                        p.op('act', lambda e, S_=S_, P_=P_: e.activation(P_.t[:, :], S_.t[:, :], AF.Exp, scale=0.125), reads=[S_.b], writes=[P_.b])
                        if ki == 1 and pend["t"] is not None:
                            pend["t"]()
                            pend["t"] = None
                        if ki + 2 < len(keys):
                            emitS(ki + 2)
                        first = ki == 0
                        last = ki == len(keys) - 1
                        p.op('pe', lambda e, O=O, kt=kt, lo=lo, hi=hi, P_=P_, first=first, last=last: e.matmul(
                            O.t[lo:hi, :], Vm.t[:, kt, lo:hi], P_.t[:, :], start=first, stop=last), reads=[Vm.b, P_.b], writes=[Ob])
                        p.op('pe', lambda e, Z=Z, lo=lo, hi=hi, P_=P_, first=first, last=last: e.matmul(
                            Z.t[lo:hi, :], ones_b.t[:, 0:64], P_.t[:, :], start=first, stop=last), reads=[ones_b.b, P_.b], writes=[Zb])
                    dn = den[qi % 2]

                    def tail(dn=dn, Z=Z, O=O, lo=lo, hi=hi, AT=AT, Zb=Zb, Ob=Ob, Db=Db, g=g, qi=qi):
                        p.op('dve', lambda e: e.tensor_add(dn.t[lo:hi, :], Z.t[lo:hi, :], skf.t[lo:hi, :]), reads=[Zb, skf.b], writes=[Db])
                        p.op('act', lambda e: e.activation(dn.t[lo:hi, :], dn.t[lo:hi, :], AF.Ln), reads=[Db], writes=[Db])
                        p.op('act', lambda e: e.activation(dn.t[lo:hi, :], dn.t[lo:hi, :], AF.Exp, scale=-1.0), reads=[Db], writes=[Db])
                        p.op('dve', lambda e: e.tensor_mul(
                            AT.t[lo:hi, :, :], O.t[lo:hi, :].rearrange("p (j q) -> p j q", j=4),
                            dn.t[lo:hi, :].rearrange("p (j q) -> p j q", j=4)), reads=[Ob, Db], writes=[AT.b])
                        if g == 1:
                            p.dma(ATs[qi], AT.t[:], reads=[AT.b])
                    pend["t"] = tail
            if pend["t"] is not None:
                pend["t"]()
                pend["t"] = None
            p.barrier()

    with ExitStack() as st:
        wob = sbt(st, "wob", [128, 8, D], BF16)
        wst = [sbt(st, f"wost{i}", [128, D], F32) for i in range(2)]
        for j in range(8):
            s_ = wst[j % 2]
            if j < 4:
                p.dma(s_.t[:], I["mix_w_out"][j * 128:(j + 1) * 128, :], writes=[s_.b])
            else:
                jj = j - 4
                p.dma(s_.t[0:64, :], I["mix_w_out"][512 + 64 * jj:512 + 64 * jj + 64, :], writes=[s_.b])
                p.dma(s_.t[64:128, :], I["mix_w_out"][512 + 64 * (4 + jj):512 + 64 * (4 + jj) + 64, :], writes=[s_.b])
            ew(lambda e, j=j, s_=s_: e.tensor_copy(wob.t[:, j, :], s_.t[:]), [s_.b], [wob.b])
        Cc = sbt(st, "Cc", [128, 128], BF16)
        Sc = sbt(st, "Sc", [128, 128], BF16)
        p.dma(Cc.t[:], K["Cc"], writes=[Cc.b])
        p.dma(Sc.t[:], K["Sc"], writes=[Sc.b])
        CTb = [sbt(st, f"CTb{i}", [128, 32, 256], BF16) for i in range(2)]
        STb = [sbt(st, f"STb{i}", [128, 32, 256], BF16) for i in range(2)]
        pP = [pst(st, f"pP{i}", [128, 256]) for i in range(2)]
        pQ = [pst(st, f"pQ{i}", [128, 256]) for i in range(2)]
        pY = pst(st, "pY", [128, 256])
        pOo = [pst(st, f"pOo{i}", [128, 512]) for i in range(2)]
        Pb = [sbt(st, f"Pb{i}", [128, 256], BF16) for i in range(2)]
        Qb = [sbt(st, f"Qb{i}", [128, 256], BF16) for i in range(2)]
        YT = [sbt(st, f"YT{i}", [128, 4, 256], BF16) for i in range(2)]
        xres = [sbt(st, f"xres{i}", [128, D], F32) for i in range(2)]
        tmpo = [sbt(st, f"tmpo{i}", [128, 512], F32) for i in range(2)]
        hn = [sbt(st, f"hn{i}", [128, D], F32) for i in range(2)]
        ATl = [sbt(st, f"ATl{i}", [128, 4, 128], BF16) for i in range(2)]
        g1l = [load_gate(st, "g1lat", 0, 0, 0), load_gate(st, "g1ctx", 0, 1, 0)]
        cnt_ = 0
        oi = 0
        for kb in range(17):
            lat = kb < 16
            if lat:
                cb = CTb[kb % 2]
                sb_ = STb[kb % 2]
                for h4 in range(4):
                    p.dma(cb.t[:, h4 * 8:(h4 + 1) * 8, :], K["CT"][kb, :, h4 * 8:(h4 + 1) * 8, :], writes=[cb.b])
                    p.dma(sb_.t[:, h4 * 8:(h4 + 1) * 8, :], K["ST"][kb, :, h4 * 8:(h4 + 1) * 8, :], writes=[sb_.b])
                nti = 32
                t0 = 0
            else:
                cb = CTb[kb % 2]
                sb_ = STb[kb % 2]
                p.dma(cb.t[:, 0:2, :], K["C256"], writes=[cb.b])
                p.dma(sb_.t[:, 0:2, :], K["S256"], writes=[sb_.b])
                nti = 2
                t0 = 32
            Y = YT[kb % 2]
            for cc in range(4):
                a = pP[cnt_ % 2]
                b = pQ[cnt_ % 2]
                ab = Pb[cnt_ % 2]
                bb = Qb[cnt_ % 2]
                cnt_ += 1
                for i in range(nti):
                    p.op('pe', lambda e, a=a, i=i, cc=cc, cb=cb, t0=t0, nti=nti: e.matmul(a.t[:, :], Fm.t[:, t0 + i, cc * 128:(cc + 1) * 128], cb.t[:, i, :],
                                                                                     start=(i == 0), stop=(i == nti - 1)), reads=[Fm.b, cb.b], writes=[a.b])
                for i in range(nti):
                    p.op('pe', lambda e, b=b, i=i, cc=cc, sb_=sb_, t0=t0, nti=nti: e.matmul(b.t[:, :], Fm.t[:, t0 + i, cc * 128:(cc + 1) * 128], sb_.t[:, i, :],
                                                                                      start=(i == 0), stop=(i == nti - 1)), reads=[Fm.b, sb_.b], writes=[b.b])
                p.op('act', lambda e, a=a, ab=ab: e.activation(ab.t[:, :], a.t[:, :], AF.Identity), reads=[a.b], writes=[ab.b])
                p.op('act', lambda e, b=b, bb=bb: e.activation(bb.t[:, :], b.t[:, :], AF.Identity, scale=-1.0), reads=[b.b], writes=[bb.b])
                p.op('pe', lambda e, ab=ab: e.matmul(pY.t[:, :], Cc.t[:, :], ab.t[:, :], start=True, stop=False), reads=[Cc.b, ab.b], writes=[pY.b])
                p.op('pe', lambda e, bb=bb: e.matmul(pY.t[:, :], Sc.t[:, :], bb.t[:, :], start=False, stop=True), reads=[Sc.b, bb.b], writes=[pY.b])
                p.op('dve', lambda e, Y=Y, cc=cc: e.tensor_copy(Y.t[:, cc, :], pY.t[:, :]), reads=[pY.b], writes=[Y.b])
            for tt in range(2):
                ti = (kb * 2 + tt) if lat else 32 + tt
                col = 0 if lat else 1
                xr = xres[ti % 2]
                h_ = hn[ti % 2]
                p.dma(xr.t[:], tile_src(ti), writes=[xr.b])
                AT = ATl[ti % 2]
                p.dma(AT.t[:], ATs[ti], writes=[AT.b])
                for half in range(2):
                    po = pOo[oi % 2]
                    tm = tmpo[oi % 2]
                    oi += 1
                    for j in range(8):
                        lhs = Y.t[:, j, tt * 128:(tt + 1) * 128] if j < 4 else AT.t[:, j - 4, :]
                        rb = Y.b if j < 4 else AT.b
                        p.op('pe', lambda e, po=po, lhs=lhs, j=j, half=half: e.matmul(po.t[:, :], lhs, wob.t[:, j, half * 512:(half + 1) * 512],
                                                                                 start=(j == 0), stop=(j == 7)), reads=[rb, wob.b], writes=[po.b])
                    p.op('dve', lambda e, po=po, tm=tm, half=half, col=col: e.tensor_mul(tm.t[:, :], po.t[:, :], g1l[col].t[:, half * 512:(half + 1) * 512]),
                         reads=[po.b, g1l[col].b], writes=[tm.b])
                    p.op('pool', lambda e, tm=tm, xr=xr, h_=h_, half=half: e.tensor_add(h_.t[:, half * 512:(half + 1) * 512], xr.t[:, half * 512:(half + 1) * 512], tm.t[:, :]),
                         reads=[tm.b, xr.b], writes=[h_.b])
                p.dma(Hs[ti * 128:(ti + 1) * 128, :], h_.t[:], reads=[h_.b])
        p.barrier()
    L0.close()

    if stage == 1:
        for ti in range(32):
            pass
        with ExitStack() as st:
            bb_ = [sbt(st, f"bo{i}", [128, D], F32) for i in range(2)]
            for ti in range(32):
                b_ = bb_[ti % 2]
                p.dma(b_.t[:], Hs[ti * 128:(ti + 1) * 128, :], writes=[b_.b])
                p.dma(out[ti * 128:(ti + 1) * 128, :], b_.t[:], reads=[b_.b])
        p.emit()
        top.close()
        return nc

    def ffn(l, tiles, final):
        with ExitStack() as st:
            Wg = sbt(st, "Wg", [128, 8, FF], BF16)
            Wu = sbt(st, "Wu", [128, 8, FF], BF16)
            Wd = sbt(st, "Wd", [128, NFC, D], BF16)
            with ExitStack() as st2:
                stg = [sbt(st2, f"fst{i}", [128, FF], F32) for i in range(4)]
                si_ = 0
                for nm, Wt in (("ffn_w_gate", Wg), ("ffn_w_up", Wu)):
                    for k in range(8):
                        s_ = stg[si_ % 4]
                        si_ += 1
                        p.dma(s_.t[:, 0:1408], I[nm][l, k * 128:(k + 1) * 128, 0:1408], writes=[s_.b])
                        p.dma(s_.t[:, 1408:FF], I[nm][l, k * 128:(k + 1) * 128, 1408:FF], writes=[s_.b])
                        p.op('dve', lambda e, Wt=Wt, k=k, s_=s_: e.tensor_copy(Wt.t[:, k, 0:1408], s_.t[:, 0:1408]), reads=[s_.b], writes=[Wt.b])
                        p.op('act', lambda e, Wt=Wt, k=k, s_=s_: e.activation(Wt.t[:, k, 1408:FF], s_.t[:, 1408:FF], AF.Identity), reads=[s_.b], writes=[Wt.b])
                for fc in range(0, NFC, 2):
                    s_ = stg[si_ % 4]
                    si_ += 1
                    for q2 in range(2):
                        p.dma(s_.t[:, q2 * D:(q2 + 1) * D], I["ffn_w_down"][l, (fc + q2) * 128:(fc + q2 + 1) * 128, :], writes=[s_.b])
                    p.op('dve' if (fc // 2) % 2 == 0 else 'act', (lambda e, fc=fc, s_=s_: e.tensor_copy(Wd.t[:, fc:fc + 2, :].rearrange("p a d -> p (a d)"), s_.t[:, 0:2 * D])) if (fc // 2) % 2 == 0 else (lambda e, fc=fc, s_=s_: e.activation(Wd.t[:, fc:fc + 2, :].rearrange("p a d -> p (a d)"), s_.t[:, 0:2 * D], AF.Identity)), reads=[s_.b], writes=[Wd.b])
                p.barrier()
            cn = make_norm(st, "f", nxt=0)
            g2l = [load_gate(st, "g2lat", l, 0, 1)] + ([load_gate(st, "g2ctx", l, 1, 1)] if not final else [])
            hx = [sbt(st, f"hx{i}", [128, D], F32) for i in range(4)]
            nTf = [sbt(st, f"nTf{i}", [128, 8, 256], BF16) for i in range(2)]
            actT = [sbt(st, f"actT{i}", [128, NFC, 256], BF16) for i in range(1)]
            sg = [sbt(st, f"sg{i}", [128, 256], F32) for i in range(2)]
            pgt = [pst(st, f"fpg{i}", [128, 256]) for i in range(2)]
            put = [pst(st, f"fpu{i}", [128, 256]) for i in range(2)]
            pdn = [pst(st, f"fpd{i}", [128, 512]) for i in range(2)]
            tm_ = [sbt(st, f"ftm{i}", [128, 512], F32) for i in range(2)]
            ho = [sbt(st, f"ho{i}", [128, D], F32) for i in range(2)]
            fss = [sbt(st, f"fss{i}", [128, 1], F32) for i in range(2)]
            fj = cn["junk"]
            fingt = None
            if final:
                fingt = sbt(st, "fing", [128, D], F32)
                p.dma(fingt.t[:], I["final_g"].partition_broadcast(128), writes=[fingt.b])
            cnt = {"gi": 0, "di": 0, "hi": 0}
            blks = [tiles[b0:b0 + 2] for b0 in range(0, len(tiles), 2)]

            def do_norm(bi):
                tl = blks[bi]
                nT = nTf[bi % 2]
                xts = []
                for tt, ti in enumerate(tl):
                    col = 0 if ti < 32 else 1
                    hxt = hx[cnt['hi'] % 4]
                    cnt['hi'] += 1
                    src_ = Hs[ti * 128:(ti + 1) * 128, :]
                    norm_tile(cn, src_, l, col, 1, lambda k, tt=tt, nT=nT: nT.t[:, k, tt * 128:(tt + 1) * 128], nT.b, xt_fixed=hxt)
                    xts.append(hxt)

                return xts

            def gate_up(bi):
                nT = nTf[bi % 2]
                aT = actT[0]
                for fc in range(NFC):
                    a = pgt[cnt['gi'] % 2]
                    b = put[cnt['gi'] % 2]
                    s2 = sg[cnt['gi'] % 2]
                    cnt['gi'] += 1
                    for k in range(8):
                        p.op('pe', lambda e, a=a, k=k, fc=fc, nT=nT: e.matmul(a.t[:, :], Wg.t[:, k, fc * 128:(fc + 1) * 128], nT.t[:, k, :], start=(k == 0), stop=(k == 7)),
                             reads=[Wg.b, nT.b], writes=[a.b])
                    for k in range(8):
                        p.op('pe', lambda e, b=b, k=k, fc=fc, nT=nT: e.matmul(b.t[:, :], Wu.t[:, k, fc * 128:(fc + 1) * 128], nT.t[:, k, :], start=(k == 0), stop=(k == 7)),
                             reads=[Wu.b, nT.b], writes=[b.b])
                    p.op('act', lambda e, a=a, s2=s2: e.activation(s2.t[:, :], a.t[:, :], AF.Silu), reads=[a.b], writes=[s2.b])
                    p.op('dve', lambda e, b=b, s2=s2, aT=aT, fc=fc: e.tensor_mul(aT.t[:, fc, :], s2.t[:, :], b.t[:, :]), reads=[b.b, s2.b], writes=[aT.b])


            def down(bi, xts):
                tl = blks[bi]
                aT = actT[0]
                for tt, ti in enumerate(tl):
                    col = 0 if ti < 32 else 1
                    hxt = xts[tt]
                    h_ = ho[ti % 2]
                    for half in range(2):
                        po = pdn[cnt['di'] % 2]
                        tm = tm_[cnt['di'] % 2]
                        cnt['di'] += 1
                        for fc in range(NFC):
                            p.op('pe', lambda e, po=po, fc=fc, tt=tt, half=half, aT=aT: e.matmul(po.t[:, :], aT.t[:, fc, tt * 128:(tt + 1) * 128], Wd.t[:, fc, half * 512:(half + 1) * 512],
                                                                                         start=(fc == 0), stop=(fc == NFC - 1)), reads=[aT.b, Wd.b], writes=[po.b])
                        p.op('dve', lambda e, po=po, tm=tm, half=half, col=col: e.tensor_mul(tm.t[:, :], po.t[:, :], g2l[col].t[:, half * 512:(half + 1) * 512]),
                             reads=[po.b, g2l[col].b], writes=[tm.b])
                        p.op('pool', lambda e, tm=tm, hxt=hxt, h_=h_, half=half: e.tensor_add(h_.t[:, half * 512:(half + 1) * 512], hxt.t[:, half * 512:(half + 1) * 512], tm.t[:, :]),
                             reads=[tm.b, hxt.b], writes=[h_.b])
                    if not final:
                        p.dma(Hs[ti * 128:(ti + 1) * 128, :], h_.t[:], reads=[h_.b])
                    else:
                        ss = fss[ti % 2]
                        p.op('act', lambda e, h_=h_, ss=ss: e.activation(fj.t[:], h_.t[:], AF.Square, accum_out=ss.t[:]), reads=[h_.b], writes=[fj.b, ss.b])
                        p.op('dve', lambda e, ss=ss: e.tensor_scalar(ss.t[:], ss.t[:], 1.0 / D, EPS, ALU.mult, ALU.add), reads=[ss.b], writes=[ss.b])
                        p.op('act', lambda e, ss=ss: e.activation(ss.t[:], ss.t[:], AF.Sqrt), reads=[ss.b], writes=[ss.b])
                        p.op('dve', lambda e, ss=ss: e.reciprocal(ss.t[:], ss.t[:]), reads=[ss.b], writes=[ss.b])
                        p.op('dve', lambda e, h_=h_, ss=ss: e.scalar_tensor_tensor(h_.t[:], h_.t[:], ss.t[:, 0:1], fingt.t[:], ALU.mult, ALU.mult),
                             reads=[h_.b, ss.b, fingt.b], writes=[h_.b])
                        p.dma(out[ti * 128:(ti + 1) * 128, :], h_.t[:], reads=[h_.b])

            xts_next = do_norm(0)
            for bi in range(len(blks)):
                xts_cur = xts_next
                gate_up(bi)
                if bi + 1 < len(blks):
                    xts_next = do_norm(bi + 1)
                down(bi, xts_cur)
            p.barrier()

    ffn(0, list(range(34)), False)

    if stage == 2:
        with ExitStack() as st:
            bb_ = [sbt(st, f"bo{i}", [128, D], F32) for i in range(2)]
            for ti in range(32):
                b_ = bb_[ti % 2]
                p.dma(b_.t[:], Hs[ti * 128:(ti + 1) * 128, :], writes=[b_.b])
                p.dma(out[ti * 128:(ti + 1) * 128, :], b_.t[:], reads=[b_.b])
        p.emit()
        top.close()
        return nc

    s5_layer(nc, p, I, K, Hs, AB, load_gate, identf, identb, sbt, pst, ew, make_norm, norm_tile)
    if stage == 3:
        with ExitStack() as st:
            bb_ = [sbt(st, f"bo3{i}", [128, D], F32) for i in range(2)]
            for ti in range(32):
                b_ = bb_[ti % 2]
                p.dma(b_.t[:], Hs[ti * 128:(ti + 1) * 128, :], writes=[b_.b])
                p.dma(out[ti * 128:(ti + 1) * 128, :], b_.t[:], reads=[b_.b])
        p.emit()
        top.close()
        return nc
    ffn(1, list(range(16)), True)
    p.emit()
    top.close()
    return nc


def rev(ap):
    apl = [list(a) for a in ap.ap]
    n = apl[-1][1]
    stp = apl[-1][0]
    apl[-1][0] = -stp
    return bass.AP(ap.tensor, ap.offset + (n - 1) * stp, apl)


def bcast_last(ap, n):
    apl = [list(a) for a in ap.ap] + [[0, n]]
    return bass.AP(ap.tensor, ap.offset, apl)


def s5_layer(nc, p, I, K, Hs, AB, load_gate, identf, identb, sbt, pst, ew, make_norm, norm_tile, dbg=None):
    Hloc = Hs
    YGs = nc.dram_tensor("YGs", [8, 128, T // 2], BF16).ap()
    NTOK = T + TC
    with ExitStack() as S:
        nT = sbt(S, "s5nT", [128, 8, NTOK], BF16)
        with ExitStack() as st:
            cn = make_norm(st, "s5n", nxt=3)
            for ti in range(34):
                col = 0 if ti < 32 else 1
                norm_tile(cn, Hs[ti * 128:(ti + 1) * 128, :], 1, col, 0, lambda k, ti=ti: nT.t[:, k, ti * 128:(ti + 1) * 128], nT.b)
            p.barrier()
        PRM = sbt(S, "s5prm", [128, 3, 64], F32)
        Cw = sbt(S, "s5Cw", [128, 64, 2, 32], F32)
        Bz = sbt(S, "s5Bz", [128, 2, 64, 2, 16], F32)
        LC = 4
        NCM = 512 // LC
        PW = sbt(S, "s5PW", [128, LC + 1, 2, 64], F32)
        PRM4 = sbt(S, "s5prm4", [128, 3, 64], F32)
        dT = sbt(S, "s5dT", [128, 8], F32)
        with ExitStack() as st:
            pt = pst(st, "s5pt", [128, 128])
            def V(nm):
                return sbt(st, "s5v_" + nm, [128, 64], F32)
            arow = sbt(st, "s5arow", [64, 2, 128], F32)
            p.dma(arow.t[:, 0, :], I["ssm_a_re"].rearrange("(dq g) p -> dq (g p)", g=2), writes=[arow.b])
            p.dma(arow.t[:, 1, :], I["ssm_a_im"].rearrange("(dq g) p -> dq (g p)", g=2), writes=[arow.b])
            are, aim = V("are"), V("aim")
            for i_, dst in enumerate((are, aim)):
                p.op('pe', lambda e, i_=i_: e.transpose(pt.t[:, 0:64], arow.t[:, i_, :], identf.t[0:64, 0:64]), reads=[arow.b, identf.b], writes=[pt.b])
                p.op('dve', lambda e, dst=dst: e.tensor_copy(dst.t[:], pt.t[:, 0:64]), reads=[pt.b], writes=[dst.b])
            drow = sbt(st, "s5drow", [8, 128], F32)
            p.dma(drow.t[:], I["ssm_d"], writes=[drow.b])
            p.op('pe', lambda e: e.transpose(pt.t[:, 0:8], drow.t[:, :], identf.t[0:8, 0:8]), reads=[drow.b, identf.b], writes=[pt.b])
            p.op('dve', lambda e: e.tensor_copy(dT.t[:], pt.t[:, 0:8]), reads=[pt.b], writes=[dT.b])
            ldb = sbt(st, "s5ldb", [128, 128], F32)
            p.dma(ldb.t[:], I["ssm_log_dt"].partition_broadcast(128), writes=[ldb.b])
            dt = V("dt")
            for g2 in range(2):
                p.op('dve', lambda e, g2=g2: e.tensor_copy(dt.t[64 * g2:64 * g2 + 64, :], ldb.t[64 * g2:64 * g2 + 64, g2:128:2]), reads=[ldb.b], writes=[dt.b])
            p.op('act', lambda e: e.activation(dt.t[:], dt.t[:], AF.Exp), reads=[dt.b], writes=[dt.b])
            xr, th, mag = V("xr"), V("th"), V("mag")
            p.op('dve', lambda e: e.tensor_mul(xr.t[:], are.t[:], dt.t[:]), reads=[are.b, dt.b], writes=[xr.b])
            p.op('dve', lambda e: e.tensor_mul(th.t[:], aim.t[:], dt.t[:]), reads=[aim.b, dt.b], writes=[th.b])
            p.op('act', lambda e: e.activation(mag.t[:], xr.t[:], AF.Exp), reads=[xr.b], writes=[mag.b])
            kf = V("kf")
            ki = sbt(st, "s5ki", [128, 64], mybir.dt.int32)
            p.op('dve', lambda e: e.tensor_scalar(kf.t[:], th.t[:], 1.0 / (2 * math.pi), None, ALU.mult), reads=[th.b], writes=[kf.b])
            p.op('dve', lambda e: e.tensor_copy(ki.t[:], kf.t[:]), reads=[kf.b], writes=[ki.b])
            p.op('dve', lambda e: e.tensor_copy(kf.t[:], ki.t[:]), reads=[ki.b], writes=[kf.b])
            C1 = 6.28125
            C2 = 2 * math.pi - 6.28125
            thm = V("thm")
            p.op('dve', lambda e: e.scalar_tensor_tensor(thm.t[:], kf.t[:], -C1, th.t[:], ALU.mult, ALU.add), reads=[kf.b, th.b], writes=[thm.b])
            p.op('dve', lambda e: e.scalar_tensor_tensor(thm.t[:], kf.t[:], -C2, thm.t[:], ALU.mult, ALU.add), reads=[kf.b, thm.b], writes=[thm.b])
            xq, u2, qs, qc = V("xq"), V("u2"), V("qs"), V("qc")
            p.op('dve', lambda e: e.tensor_scalar(xq.t[:], thm.t[:], 0.25, None, ALU.mult), reads=[thm.b], writes=[xq.b])
            p.op('dve', lambda e: e.tensor_mul(u2.t[:], xq.t[:], xq.t[:]), reads=[xq.b], writes=[u2.b])
            sc_ = [(-1.0) ** k / math.factorial(2 * k + 1) for k in range(9)]
            cc_ = [(-1.0) ** k / math.factorial(2 * k) for k in range(9)]
            p.op('dve', lambda e: e.tensor_scalar(qs.t[:], u2.t[:], sc_[8], None, ALU.mult), reads=[u2.b], writes=[qs.b])
            p.op('dve', lambda e: e.tensor_scalar(qc.t[:], u2.t[:], cc_[8], None, ALU.mult), reads=[u2.b], writes=[qc.b])
            for k in range(7, 0, -1):
                p.op('dve', lambda e, k=k: e.scalar_tensor_tensor(qs.t[:], qs.t[:], sc_[k], u2.t[:], ALU.add, ALU.mult), reads=[qs.b, u2.b], writes=[qs.b])
                p.op('dve', lambda e, k=k: e.scalar_tensor_tensor(qc.t[:], qc.t[:], cc_[k], u2.t[:], ALU.add, ALU.mult), reads=[qc.b, u2.b], writes=[qc.b])
            sn, cs = V("sn"), V("cs")
            p.op('dve', lambda e: e.scalar_tensor_tensor(sn.t[:], qs.t[:], 1.0, xq.t[:], ALU.add, ALU.mult), reads=[qs.b, xq.b], writes=[sn.b])
            p.op('dve', lambda e: e.tensor_scalar(cs.t[:], qc.t[:], 1.0, None, ALU.add), reads=[qc.b], writes=[cs.b])
            ta, tb_ = V("ta"), V("tb")
            for _ in range(2):
                p.op('dve', lambda e: e.tensor_mul(ta.t[:], cs.t[:], cs.t[:]), reads=[cs.b], writes=[ta.b])
                p.op('dve', lambda e: e.tensor_mul(tb_.t[:], sn.t[:], sn.t[:]), reads=[sn.b], writes=[tb_.b])
                p.op('dve', lambda e: e.scalar_tensor_tensor(sn.t[:], cs.t[:], 2.0, sn.t[:], ALU.mult, ALU.mult), reads=[cs.b, sn.b], writes=[sn.b])
                p.op('dve', lambda e: e.tensor_sub(cs.t[:], ta.t[:], tb_.t[:]), reads=[ta.b, tb_.b], writes=[cs.b])
            p.op('dve', lambda e: e.tensor_copy(PRM.t[:, 0, :], mag.t[:]), reads=[mag.b], writes=[PRM.b])
            p.op('dve', lambda e: e.tensor_copy(PRM.t[:, 1, :], cs.t[:]), reads=[cs.b], writes=[PRM.b])
            p.op('dve', lambda e: e.tensor_copy(PRM.t[:, 2, :], sn.t[:]), reads=[sn.b], writes=[PRM.b])
            p.op('pool', lambda e: e.memset(PW.t[:, 0, 0, :], 1.0), writes=[PW.b])
            p.op('pool', lambda e: e.memset(PW.t[:, 0, 1, :], 0.0), writes=[PW.b])
            p.op('dve', lambda e: e.tensor_mul(PW.t[:, 1, 0, :], mag.t[:], cs.t[:]), reads=[mag.b, cs.b], writes=[PW.b])
            p.op('dve', lambda e: e.tensor_mul(PW.t[:, 1, 1, :], mag.t[:], sn.t[:]), reads=[mag.b, sn.b], writes=[PW.b])
            for k in range(2, LC + 1):
                p.op('dve', lambda e, k=k: e.tensor_mul(ta.t[:], PW.t[:, k - 1, 0, :], PW.t[:, 1, 0, :]), reads=[PW.b], writes=[ta.b])
                p.op('dve', lambda e, k=k: e.tensor_mul(tb_.t[:], PW.t[:, k - 1, 1, :], PW.t[:, 1, 1, :]), reads=[PW.b], writes=[tb_.b])
                p.op('dve', lambda e, k=k: e.tensor_sub(PW.t[:, k, 0, :], ta.t[:], tb_.t[:]), reads=[ta.b, tb_.b], writes=[PW.b])
                p.op('dve', lambda e, k=k: e.tensor_mul(ta.t[:], PW.t[:, k - 1, 0, :], PW.t[:, 1, 1, :]), reads=[PW.b], writes=[ta.b])
                p.op('dve', lambda e, k=k: e.tensor_mul(tb_.t[:], PW.t[:, k - 1, 1, :], PW.t[:, 1, 0, :]), reads=[PW.b], writes=[tb_.b])
                p.op('dve', lambda e, k=k: e.tensor_add(PW.t[:, k, 1, :], ta.t[:], tb_.t[:]), reads=[ta.b, tb_.b], writes=[PW.b])
            c4, s4, r4 = V("c4"), V("s4"), V("r4")
            p.op('dve', lambda e: e.tensor_copy(c4.t[:], cs.t[:]), reads=[cs.b], writes=[c4.b])
            p.op('dve', lambda e: e.tensor_copy(s4.t[:], sn.t[:]), reads=[sn.b], writes=[s4.b])
            p.op('dve', lambda e: e.tensor_copy(r4.t[:], mag.t[:]), reads=[mag.b], writes=[r4.b])
            for _ in range(int(round(math.log2(LC)))):
                p.op('dve', lambda e: e.tensor_mul(ta.t[:], c4.t[:], c4.t[:]), reads=[c4.b], writes=[ta.b])
                p.op('dve', lambda e: e.tensor_mul(tb_.t[:], s4.t[:], s4.t[:]), reads=[s4.b], writes=[tb_.b])
                p.op('dve', lambda e: e.scalar_tensor_tensor(s4.t[:], c4.t[:], 2.0, s4.t[:], ALU.mult, ALU.mult), reads=[c4.b, s4.b], writes=[s4.b])
                p.op('dve', lambda e: e.tensor_sub(c4.t[:], ta.t[:], tb_.t[:]), reads=[ta.b, tb_.b], writes=[c4.b])
                p.op('dve', lambda e: e.tensor_mul(r4.t[:], r4.t[:], r4.t[:]), reads=[r4.b], writes=[r4.b])
            p.op('dve', lambda e: e.tensor_copy(PRM4.t[:, 0, :], r4.t[:]), reads=[r4.b], writes=[PRM4.b])
            p.op('dve', lambda e: e.tensor_copy(PRM4.t[:, 1, :], c4.t[:]), reads=[c4.b], writes=[PRM4.b])
            p.op('dve', lambda e: e.tensor_copy(PRM4.t[:, 2, :], s4.t[:]), reads=[s4.b], writes=[PRM4.b])
            nr, ni, den, fre, fim = V("nr"), V("ni"), V("den"), V("fre"), V("fim")
            p.op('dve', lambda e: e.tensor_mul(nr.t[:], mag.t[:], cs.t[:]), reads=[mag.b, cs.b], writes=[nr.b])
            p.op('dve', lambda e: e.tensor_scalar(nr.t[:], nr.t[:], -1.0, None, ALU.add), reads=[nr.b], writes=[nr.b])
            p.op('dve', lambda e: e.tensor_mul(ni.t[:], mag.t[:], sn.t[:]), reads=[mag.b, sn.b], writes=[ni.b])
            p.op('dve', lambda e: e.tensor_mul(den.t[:], are.t[:], are.t[:]), reads=[are.b], writes=[den.b])
            p.op('dve', lambda e: e.tensor_mul(ta.t[:], aim.t[:], aim.t[:]), reads=[aim.b], writes=[ta.b])
            p.op('dve', lambda e: e.tensor_add(den.t[:], den.t[:], ta.t[:]), reads=[den.b, ta.b], writes=[den.b])
            p.op('dve', lambda e: e.reciprocal(den.t[:], den.t[:]), reads=[den.b], writes=[den.b])
            p.op('dve', lambda e: e.tensor_mul(ta.t[:], nr.t[:], are.t[:]), reads=[nr.b, are.b], writes=[ta.b])
            p.op('dve', lambda e: e.tensor_mul(tb_.t[:], ni.t[:], aim.t[:]), reads=[ni.b, aim.b], writes=[tb_.b])
            p.op('dve', lambda e: e.tensor_add(fre.t[:], ta.t[:], tb_.t[:]), reads=[ta.b, tb_.b], writes=[fre.b])
            p.op('dve', lambda e: e.tensor_mul(fre.t[:], fre.t[:], den.t[:]), reads=[fre.b, den.b], writes=[fre.b])
            p.op('dve', lambda e: e.tensor_mul(ta.t[:], ni.t[:], are.t[:]), reads=[ni.b, are.b], writes=[ta.b])
            p.op('dve', lambda e: e.tensor_mul(tb_.t[:], nr.t[:], aim.t[:]), reads=[nr.b, aim.b], writes=[tb_.b])
            p.op('dve', lambda e: e.tensor_sub(fim.t[:], ta.t[:], tb_.t[:]), reads=[ta.b, tb_.b], writes=[fim.b])
            p.op('dve', lambda e: e.tensor_mul(fim.t[:], fim.t[:], den.t[:]), reads=[fim.b, den.b], writes=[fim.b])
            Br = [sbt(st, f"s5Br{i}", [128, 64, 16], F32) for i in range(2)]
            for i_, nm in enumerate(("ssm_b_re", "ssm_b_im")):
                src = I[nm].rearrange("d (q g) p h -> g p (d q) h", g=2)
                for g2 in range(2):
                    p.dma(Br[i_].t[64 * g2:64 * g2 + 64, :, :], src[g2], writes=[Br[i_].b])
            p.op('pool', lambda e: e.memset(Bz.t[:].rearrange("p a b c d -> p (a b c d)"), 0.0), writes=[Bz.b])
            m1 = sbt(st, "s5m1", [128, 64, 16], F32)
            m2 = sbt(st, "s5m2", [128, 64, 16], F32)
            fre_b = bcast_last(fre.t[:], 16)
            fim_b = bcast_last(fim.t[:], 16)
            p.op('dve', lambda e: e.tensor_mul(m1.t[:], Br[0].t[:], fre_b), reads=[Br[0].b, fre.b], writes=[m1.b])
            p.op('dve', lambda e: e.tensor_mul(m2.t[:], Br[1].t[:], fim_b), reads=[Br[1].b, fim.b], writes=[m2.b])
            for g2 in range(2):
                p.op('dve', lambda e, g2=g2: e.tensor_sub(Bz.t[64 * g2:64 * g2 + 64, 0, :, g2, :], m1.t[64 * g2:64 * g2 + 64], m2.t[64 * g2:64 * g2 + 64]),
                     reads=[m1.b, m2.b], writes=[Bz.b])
            p.op('dve', lambda e: e.tensor_mul(m1.t[:], Br[1].t[:], fre_b), reads=[Br[1].b, fre.b], writes=[m1.b])
            p.op('dve', lambda e: e.tensor_mul(m2.t[:], Br[0].t[:], fim_b), reads=[Br[0].b, fim.b], writes=[m2.b])
            for g2 in range(2):
                p.op('dve', lambda e, g2=g2: e.tensor_add(Bz.t[64 * g2:64 * g2 + 64, 1, :, g2, :], m1.t[64 * g2:64 * g2 + 64], m2.t[64 * g2:64 * g2 + 64]),
                     reads=[m1.b, m2.b], writes=[Bz.b])
            pts = [pst(st, f"s5ptb{i}", [128, 128]) for i in range(2)]
            n_ = 0
            p.op('pool', lambda e: e.memset(Cw.t[:].rearrange("p a b c -> p (a b c)"), 0.0), writes=[Cw.b])
            Cn = [sbt(st, f"s5Cn{i}", [128, 2, 64], F32) for i in range(2)]
            for ri, nm in enumerate(("ssm_c_re", "ssm_c_im")):
                for dQ in range(16):
                    d_, Q_ = dQ // 8, dQ % 8
                    cnb = Cn[n_ % 2]
                    pp = pts[n_ % 2]
                    n_ += 1
                    src = I[nm][d_, Q_ * 4 * 2:(Q_ * 4 + 4) * 2].rearrange("g h p -> (g h) p")
                    p.dma(cnb.t[:, 0, :], src, writes=[cnb.b])
                    p.dma(cnb.t[:, 1, :], src, writes=[cnb.b])
                    p.op('pe', lambda e, pp=pp, cnb=cnb: e.transpose(pp.t[:, :], cnb.t[:, :, :].rearrange("p a b -> p (a b)"), identf.t[:, :]),
                         reads=[cnb.b, identf.b], writes=[pp.b])
                    sgn = 1.0 if ri == 0 else -1.0
                    for g2 in range(2):
                        p.op('act', lambda e, pp=pp, g2=g2, ri=ri, dQ=dQ, sgn=sgn: e.activation(
                            Cw.t[64 * g2:64 * g2 + 64, dQ * 4:(dQ + 1) * 4, ri, 16 * g2:16 * g2 + 16],
                            pp.t[64 * g2:64 * g2 + 64, :].rearrange("p (q g h) -> p q g h", q=4, g=2)[:, :, g2, :], AF.Identity, scale=sgn),
                            reads=[pp.b], writes=[Cw.b])
            p.barrier()

        if dbg is not None:
            dbg(PRM, Cw, nT)
            return

        with ExitStack() as st:
            yacc = sbt(st, "s5yacc", [128, T // 2], F32)
            Ec = sbt(st, "s5Ec", [128, 8, NCM], F32)
            Es = sbt(st, "s5Es", [128, 8, NCM], F32)
            wc = sbt(st, "s5wc", [128, 8], F32)
            ws = sbt(st, "s5ws", [128, 8], F32)
            wt1 = sbt(st, "s5wt1", [128, 8], F32)
            wt2 = sbt(st, "s5wt2", [128, 8], F32)
            et1 = sbt(st, "s5et1", [128, 8, NCM // 2], F32)
            et2 = sbt(st, "s5et2", [128, 8, NCM // 2], F32)
            hp = sbt(st, "s5hp", [128, 8, 2], F32)
            ini = sbt(st, "s5ini", [128, 8, 2], F32)
            itmp = sbt(st, "s5itmp", [128, 8, 2], F32)
            nsth = sbt(st, "s5nsth", [128, 64], F32)
            p.op('dve', lambda e: e.tensor_scalar(nsth.t[:], PRM4.t[:, 2, :], -1.0, None, ALU.mult), reads=[PRM4.b], writes=[nsth.b])
            hpb = [Buf() for _ in range(8)]
            CwP = sbt(st, "s5CwP", [128, LC + 1, 8, 2, 32], F32)
            ctm = [sbt(st, f"s5ctm{i}", [128, 4, 32], F32) for i in range(2)]
            BP = sbt(st, "s5BP", [128, LC, 2, 2, 4, 32], F32)
            Win = sbt(st, "s5Win", [128, 2, 2, LC, 128], BF16)
            Mw = sbt(st, "s5Mw", [128, 2, LC, 32], BF16)
            Wout = sbt(st, "s5Wout", [128, 8, LC, 2, 32], BF16)
            pts = [pst(st, f"s5ptq{i}", [128, 128]) for i in range(1)]
            pk = pst(st, "s5pk", [128, 16, 32])
            NS = 8
            Wk_ = [[sbt(st, f"s5w{j}_{i}", [128, 2, NCM], F32) for i in range(3)] for j in range(NS)]
            Hb = [sbt(st, f"s5hb{j}", [128, 2, NCM + 4], BF16) for j in range(NS)]
            Esg = sbt(st, "s5Esg", [128, 8, 2, NCM], F32)

            def swap2(tb_, ncn):
                b1 = tb_.t[:, 1, 0:ncn]
                apl = [list(a_) for a_ in b1.ap]
                return bass.AP(b1.tensor, b1.offset, [apl[0], [-NCM, 2], apl[-1]])

            def bc2(ap2):
                apl = [list(a_) for a_ in ap2.ap]
                return bass.AP(ap2.tensor, ap2.offset, [apl[0], [0, 2], apl[-1]])
            pS = [pst(st, f"s5pS{j}", [128, 512]) for j in range(4)]
            py = [pst(st, f"s5py{i}", [128, LC, NCM]) for i in range(2)]
            ygb = [sbt(st, f"s5yg{i}", [128, 512], BF16) for i in range(2)]
            gtb = [sbt(st, f"s5gt{i}", [128, 512], F32) for i in range(2)]
            tn_ = 0
            blkc = 0
            yi_ = 0
            psi = 0
            for Q in range(8):
                for d_ in range(2):
                    us = slice(d_ * 32 + Q * 4, d_ * 32 + Q * 4 + 4)
                    ts_ = slice(d_ * 4, d_ * 4 + 4)
                    c0_ = Cw.t[:, us, 0, :]
                    c1_ = Cw.t[:, us, 1, :]
                    for k in range(LC + 1):
                        pr = bcast_last(PW.t[:, k, 0, us], 32)
                        pi_ = bcast_last(PW.t[:, k, 1, us], 32)
                        t1_, t2_ = ctm
                        p.op('dve', lambda e, t1_=t1_, c0_=c0_, pr=pr: e.tensor_mul(t1_.t[:], c0_, pr), reads=[Cw.b, PW.b], writes=[t1_.b])
                        p.op('pool', lambda e, t2_=t2_, c1_=c1_, pi_=pi_: e.tensor_mul(t2_.t[:], c1_, pi_), reads=[Cw.b, PW.b], writes=[t2_.b])
                        p.op('dve', lambda e, t1_=t1_, t2_=t2_, k=k, ts_=ts_: e.tensor_add(CwP.t[:, k, ts_, 0, :], t1_.t[:], t2_.t[:]), reads=[t1_.b, t2_.b], writes=[CwP.b])
                        p.op('dve', lambda e, t1_=t1_, c1_=c1_, pr=pr: e.tensor_mul(t1_.t[:], c1_, pr), reads=[Cw.b, PW.b], writes=[t1_.b])
                        p.op('pool', lambda e, t2_=t2_, c0_=c0_, pi_=pi_: e.tensor_mul(t2_.t[:], c0_, pi_), reads=[Cw.b, PW.b], writes=[t2_.b])
                        p.op('dve', lambda e, t1_=t1_, t2_=t2_, k=k, ts_=ts_: e.tensor_sub(CwP.t[:, k, ts_, 1, :], t1_.t[:], t2_.t[:]), reads=[t1_.b, t2_.b], writes=[CwP.b])
                    b0_ = Bz.t[:, 0, us, :, :].rearrange("p a b c -> p a (b c)")
                    b1_ = Bz.t[:, 1, us, :, :].rearrange("p a b c -> p a (b c)")
                    for s_ in range(LC):
                        k = LC - 1 - s_
                        pr = bcast_last(PW.t[:, k, 0, us], 32)
                        pi_ = bcast_last(PW.t[:, k, 1, us], 32)
                        t1_, t2_ = ctm
                        p.op('dve', lambda e, t1_=t1_, b0_=b0_, pr=pr: e.tensor_mul(t1_.t[:], b0_, pr), reads=[Bz.b, PW.b], writes=[t1_.b])
                        p.op('pool', lambda e, t2_=t2_, b1_=b1_, pi_=pi_: e.tensor_mul(t2_.t[:], b1_, pi_), reads=[Bz.b, PW.b], writes=[t2_.b])
                        p.op('dve', lambda e, t1_=t1_, t2_=t2_, s_=s_, d_=d_: e.tensor_sub(BP.t[:, s_, 0, d_, :, :], t1_.t[:], t2_.t[:]), reads=[t1_.b, t2_.b], writes=[BP.b])
                        p.op('dve', lambda e, t1_=t1_, b0_=b0_, pi_=pi_: e.tensor_mul(t1_.t[:], b0_, pi_), reads=[Bz.b, PW.b], writes=[t1_.b])
                        p.op('pool', lambda e, t2_=t2_, b1_=b1_, pr=pr: e.tensor_mul(t2_.t[:], b1_, pr), reads=[Bz.b, PW.b], writes=[t2_.b])
                        p.op('dve', lambda e, t1_=t1_, t2_=t2_, s_=s_, d_=d_: e.tensor_add(BP.t[:, s_, 1, d_, :, :], t1_.t[:], t2_.t[:]), reads=[t1_.b, t2_.b], writes=[BP.b])
                p.op('act', lambda e: e.activation(Wout.t[:].rearrange("p x t r h -> p t x r h"), CwP.t[:, 1:LC + 1, :, :, :], AF.Identity), reads=[CwP.b], writes=[Wout.b])
                for d_ in range(2):
                    for ri in range(2):
                        for s_ in range(LC):
                            pp = pts[0]
                            tn_ += 1
                            p.op('pe', lambda e, pp=pp, s_=s_, ri=ri, d_=d_: e.transpose(pp.t[:, :], BP.t[:, s_, ri, d_, :, :].rearrange("p a b -> p (a b)"), identf.t[:, :]),
                                 reads=[BP.b, identf.b], writes=[pp.b])
                            p.op('act', lambda e, pp=pp, s_=s_, ri=ri, d_=d_: e.activation(Win.t[:, d_, ri, s_, :], pp.t[:, :], AF.Identity), reads=[pp.b], writes=[Win.b])
                    for thf in range(LC // 4):
                        for tq in range(4):
                            tau = thf * 4 + tq
                            for ql in range(4):
                                tix = d_ * 4 + ql
                                for ri in range(2):
                                    p.op('pe', lambda e, d_=d_, tau=tau, tq=tq, ql=ql, tix=tix, ri=ri, us=slice(d_ * 32 + Q * 4, d_ * 32 + Q * 4 + 4): e.matmul(
                                        pk.t[:, tq * 4 + ql, :], Bz.t[:, ri, us, :, :].rearrange("p q a b -> p (q a b)"), CwP.t[:, tau, tix, ri, :],
                                        start=(ri == 0), stop=(ri == 1)), reads=[Bz.b, CwP.b], writes=[pk.b])
                        for ql in range(4):
                            p.op('dve', lambda e, d_=d_, ql=ql, thf=thf: e.tensor_copy(Mw.t[32 * ql:32 * ql + 32, d_, thf * 4:thf * 4 + 4, :], pk.t[32 * ql:32 * ql + 32, ql:16:4, :]),
                                 reads=[pk.b], writes=[Mw.b])
                for d_ in range(2):
                    cols = slice(d_ * 32 + Q * 4, d_ * 32 + Q * 4 + 4)
                    p.op('dve', lambda e, d_=d_, cols=cols: e.tensor_copy(wc.t[:, d_ * 4:d_ * 4 + 4], PRM4.t[:, 1, cols]), reads=[PRM4.b], writes=[wc.b])
                    p.op('dve', lambda e, d_=d_, cols=cols: e.tensor_copy(ws.t[:, d_ * 4:d_ * 4 + 4], PRM4.t[:, 2, cols]), reads=[PRM4.b], writes=[ws.b])
                p.op('pool', lambda e: e.memset(Ec.t[:, :, 0:1], 1.0), writes=[Ec.b])
                p.op('pool', lambda e: e.memset(Es.t[:, :, 0:1], 0.0), writes=[Es.b])
                n = 1
                while n < NCM:
                    wcb = bcast_last(wc.t[:, :], n)
                    wsb = bcast_last(ws.t[:, :], n)
                    p.op('dve', lambda e, n=n, wcb=wcb: e.tensor_mul(et1.t[:, :, 0:n], Ec.t[:, :, 0:n], wcb), reads=[Ec.b, wc.b], writes=[et1.b])
                    p.op('pool', lambda e, n=n, wsb=wsb: e.tensor_mul(et2.t[:, :, 0:n], Es.t[:, :, 0:n], wsb), reads=[Es.b, ws.b], writes=[et2.b])
                    p.op('dve', lambda e, n=n: e.tensor_sub(Ec.t[:, :, n:2 * n], et1.t[:, :, 0:n], et2.t[:, :, 0:n]), reads=[et1.b, et2.b], writes=[Ec.b])
                    p.op('dve', lambda e, n=n, wcb=wcb: e.tensor_mul(et1.t[:, :, 0:n], Es.t[:, :, 0:n], wcb), reads=[Es.b, wc.b], writes=[et1.b])
                    p.op('pool', lambda e, n=n, wsb=wsb: e.tensor_mul(et2.t[:, :, 0:n], Ec.t[:, :, 0:n], wsb), reads=[Ec.b, ws.b], writes=[et2.b])
                    p.op('dve', lambda e, n=n: e.tensor_add(Es.t[:, :, n:2 * n], et1.t[:, :, 0:n], et2.t[:, :, 0:n]), reads=[et1.b, et2.b], writes=[Es.b])
                    p.op('dve', lambda e: e.tensor_mul(wt1.t[:], wc.t[:], wc.t[:]), reads=[wc.b], writes=[wt1.b])
                    p.op('dve', lambda e: e.tensor_mul(wt2.t[:], ws.t[:], ws.t[:]), reads=[ws.b], writes=[wt2.b])
                    p.op('dve', lambda e: e.scalar_tensor_tensor(ws.t[:], wc.t[:], 2.0, ws.t[:], ALU.mult, ALU.mult), reads=[wc.b, ws.b], writes=[ws.b])
                    p.op('dve', lambda e: e.tensor_sub(wc.t[:], wt1.t[:], wt2.t[:]), reads=[wt1.b, wt2.b], writes=[wc.b])
                    n *= 2
                p.op('dve', lambda e: e.tensor_copy(Esg.t[:, :, 0, :], Es.t[:, :, :]), reads=[Es.b], writes=[Esg.b])
                p.op('dve', lambda e: e.tensor_scalar(Esg.t[:, :, 1, :], Es.t[:, :, :], -1.0, None, ALU.mult), reads=[Es.b], writes=[Esg.b])
                blist = []
                for d_ in range(2):
                    if d_ == 0:
                        blocks = [(T, TC, False)] + [(b * 512, 512, True) for b in range(4)]
                    else:
                        blocks = [(T, TC, False)] + [(b * 512, 512, False) for b in (7, 6, 5, 4)] + [(b * 512, 512, True) for b in (3, 2, 1, 0)]
                    for bi_, (c0, n, islat) in enumerate(blocks):
                        blist.append((d_, bi_, c0, n, islat))
                bctx = {}

                def front_pe(k, Q=Q):
                    d_, bi_, c0, n, islat = blist[k]
                    gk = Q * 18 + k
                    ncn = n // LC
                    sets = [(gk % 2) * 4 + ql for ql in range(4)]
                    rhs_all = []
                    for ql in range(4):
                        base = nT.t[32 * ql:32 * ql + 32, Q, c0:c0 + n]
                        apl = [list(a_) for a_ in base.ap]
                        lst = []
                        for s_ in range(LC):
                            ap2 = [list(a_) for a_ in apl]
                            if d_ == 0:
                                ap2[-1] = [apl[-1][0] * LC, ncn]
                                lst.append(bass.AP(base.tensor, base.offset + s_ * apl[-1][0], ap2))
                            else:
                                ap2[-1] = [-apl[-1][0] * LC, ncn]
                                lst.append(bass.AP(base.tensor, base.offset + (n - 1 - s_) * apl[-1][0], ap2))
                        rhs_all.append(lst)
                    pss = [pS[ql] for ql in range(4)]
                    for ri in range(2):
                        for s_ in range(LC):
                            for ql in range(4):
                                ps_ = pss[ql]
                                p.op('pe', lambda e, ps_=ps_, ri=ri, s_=s_, ql=ql, d_=d_, ncn=ncn, r_=rhs_all[ql][s_]: e.matmul(
                                    ps_.t[:, ri * 128:ri * 128 + ncn], Win.t[32 * ql:32 * ql + 32, d_, ri, s_, :], r_, start=(s_ == 0), stop=(s_ == LC - 1), tile_position=(32 * ql, 0)),
                                    reads=[Win.b, nT.b], writes=[ps_.b])
                    bctx[k] = (ncn, sets, rhs_all)

                def front_ev(k, Q=Q):
                    d_, bi_, c0, n, islat = blist[k]
                    ncn, sets, rhs_all = bctx[k]
                    pss = [pS[ql] for ql in range(4)]
                    for ql in range(4):
                        ps_ = pss[ql]
                        X, P1, P2 = Wk_[sets[ql]]
                        p.op('act', lambda e, X=X, ps_=ps_, ncn=ncn: e.activation(X.t[:, :, 0:ncn], ps_.t[:, 0:256].rearrange("p (r c) -> p r c", r=2)[:, :, 0:ncn], AF.Identity),
                             reads=[ps_.b], writes=[X.b])
                    for ql in range(4):
                        X, P1, P2 = Wk_[sets[ql]]
                        tix = d_ * 4 + ql
                        ecb = bc2(Ec.t[:, tix, 0:ncn])
                        p.op('dve', lambda e, X=X, P1=P1, ecb=ecb, ncn=ncn: e.tensor_mul(P1.t[:, :, 0:ncn], X.t[:, :, 0:ncn], ecb), reads=[X.b, Ec.b], writes=[P1.b])
                        p.op('pool', lambda e, X=X, P2=P2, tix=tix, ncn=ncn: e.tensor_mul(P2.t[:, :, 0:ncn], swap2(X, ncn), Esg.t[:, tix, :, 0:ncn]), reads=[X.b, Esg.b], writes=[P2.b])
                    for ql in range(4):
                        X, P1, P2 = Wk_[sets[ql]]
                        p.op('dve', lambda e, P1=P1, P2=P2, ncn=ncn: e.tensor_add(P1.t[:, :, 0:ncn], P1.t[:, :, 0:ncn], P2.t[:, :, 0:ncn]), reads=[P1.b, P2.b], writes=[P1.b])

                def back(k, Q=Q):
                    d_, bi_, c0, n, islat = blist[k]
                    gk = Q * 18 + k
                    ncn, sets, rhs_all = bctx.pop(k)
                    pyt = py[gk % 2]
                    for ql in range(4):
                        tix = d_ * 4 + ql
                        u = d_ * 32 + Q * 4 + ql
                        hb_ = hpb[tix]
                        hbs = Hb[sets[ql]]
                        if bi_ > 0:
                            p.op('act', lambda e, tix=tix, u=u: e.activation(itmp.t[:, tix, 0:1], hp.t[:, tix, 1:2], AF.Identity, scale=nsth.t[:, u:u + 1]), reads=[hb_, nsth.b], writes=[hb_])
                            p.op('act', lambda e, tix=tix, u=u: e.activation(ini.t[:, tix, 0:1], hp.t[:, tix, 0:1], AF.Identity, scale=PRM4.t[:, 1, u:u + 1], bias=itmp.t[:, tix, 0:1]),
                                 reads=[hb_, PRM4.b], writes=[hb_])
                            p.op('act', lambda e, tix=tix, u=u: e.activation(itmp.t[:, tix, 1:2], hp.t[:, tix, 0:1], AF.Identity, scale=PRM4.t[:, 2, u:u + 1]), reads=[hb_, PRM4.b], writes=[hb_])
                            p.op('act', lambda e, tix=tix, u=u: e.activation(ini.t[:, tix, 1:2], hp.t[:, tix, 1:2], AF.Identity, scale=PRM4.t[:, 1, u:u + 1], bias=itmp.t[:, tix, 1:2]),
                                 reads=[hb_, PRM4.b], writes=[hb_])
                            if islat:
                                p.op('act', lambda e, hbs=hbs, tix=tix: e.activation(hbs.t[:, :, 0], hp.t[:, tix, :], AF.Identity), reads=[hb_], writes=[hbs.b])
                    for ql in range(4):
                        X, P1, P2 = Wk_[sets[ql]]
                        tix = d_ * 4 + ql
                        u = d_ * 32 + Q * 4 + ql
                        hb_ = hpb[tix]
                        rb = PRM4.t[:, 0, u:u + 1].to_broadcast([128, ncn])
                        for ri in range(2):
                            if bi_ == 0:
                                init_, irds = 0.0, []
                            else:
                                init_, irds = ini.t[:, tix, ri:ri + 1], [hb_]
                            p.op('dve', lambda e, X=X, P1=P1, rb=rb, init_=init_, ri=ri, ncn=ncn: e.tensor_tensor_scan(X.t[:, ri, 0:ncn], rb, P1.t[:, ri, 0:ncn], init_, ALU.mult, ALU.add),
                                 reads=[P1.b, PRM4.b] + irds, writes=[X.b])
                    for ql in range(4):
                        X, P1, P2 = Wk_[sets[ql]]
                        tix = d_ * 4 + ql
                        ecb = bc2(Ec.t[:, tix, 0:ncn])
                        p.op('dve', lambda e, X=X, P1=P1, ecb=ecb, ncn=ncn: e.tensor_mul(P1.t[:, :, 0:ncn], X.t[:, :, 0:ncn], ecb), reads=[X.b, Ec.b], writes=[P1.b])
                        p.op('pool', lambda e, X=X, P2=P2, tix=tix, ncn=ncn: e.tensor_mul(P2.t[:, :, 0:ncn], swap2(X, ncn), Esg.t[:, tix, :, 0:ncn]), reads=[X.b, Esg.b], writes=[P2.b])
                    for ql in range(4):
                        X, P1, P2 = Wk_[sets[ql]]
                        p.op('dve', lambda e, P1=P1, P2=P2, ncn=ncn: e.tensor_sub(P1.t[:, :, 0:ncn], P1.t[:, :, 0:ncn], P2.t[:, :, 0:ncn]), reads=[P1.b, P2.b], writes=[P1.b])
                    for ql in range(4):
                        X, P1, P2 = Wk_[sets[ql]]
                        tix = d_ * 4 + ql
                        hb_ = hpb[tix]
                        hbs = Hb[sets[ql]]
                        p.op('act', lambda e, tix=tix, P1=P1, ncn=ncn: e.activation(hp.t[:, tix, :], P1.t[:, :, ncn - 1], AF.Identity), reads=[P1.b], writes=[hb_])
                        if islat:
                            p.op('act', lambda e, hbs=hbs, P1=P1, ncn=ncn: e.activation(hbs.t[:, :, 1:ncn], P1.t[:, :, 0:ncn - 1], AF.Identity), reads=[P1.b], writes=[hbs.b])
                    if islat:
                        for t_ in range(LC):
                            for s_ in range(t_ + 1):
                                for ql in range(4):
                                    p.op('pe', lambda e, pyt=pyt, ql=ql, t_=t_, s_=s_, d_=d_, ncn=ncn, r_=rhs_all[ql][s_]: e.matmul(
                                        pyt.t[32 * ql:32 * ql + 32, t_, 0:ncn], Mw.t[32 * ql:32 * ql + 32, d_, t_ - s_, :], r_, start=(s_ == 0 and t_ == 0), stop=False,
                                        tile_position=(32 * ql, 32 * ql)), reads=[Mw.b, nT.b], writes=[pyt.b])
                        for t_ in range(LC):
                            for ri in range(2):
                                for ql in range(4):
                                    tix = d_ * 4 + ql
                                    hh_ = Hb[sets[ql]]
                                    p.op('pe', lambda e, pyt=pyt, ql=ql, t_=t_, ri=ri, tix=tix, hh_=hh_, ncn=ncn: e.matmul(
                                        pyt.t[32 * ql:32 * ql + 32, t_, 0:ncn], Wout.t[:, tix, t_, ri, :], hh_.t[:, ri, 0:ncn], start=False, stop=(ri == 1 and t_ == LC - 1),
                                        tile_position=(0, 32 * ql)), reads=[Wout.b, hh_.b], writes=[pyt.b])

                def yevac(k, Q=Q):
                    d_, bi_, c0, n, islat = blist[k]
                    if not islat:
                        return
                    gk = Q * 18 + k
                    ncn = n // LC
                    pyt = py[gk % 2]
                    if d_ == 0:
                        yv = yacc.t[:, c0:c0 + n].rearrange("p (c t) -> p t c", t=LC)
                        nv = nT.t[:, Q, c0:c0 + n].rearrange("p (c t) -> p t c", t=LC)
                        p.op('dve', lambda e, pyt=pyt, yv=yv, nv=nv, Q=Q: e.scalar_tensor_tensor(yv, nv, dT.t[:, Q:Q + 1], pyt.t[:, :, :], ALU.mult, ALU.add),
                             reads=[nT.b, dT.b, pyt.b], writes=[yacc.b])
                    else:
                        base = yacc.t[:, c0:c0 + n]
                        apl = [list(a_) for a_ in base.ap]
                        stp = apl[-1][0]
                        yv = bass.AP(base.tensor, base.offset + (n - 1) * stp, apl[:-1] + [[-stp, LC], [-stp * LC, ncn]])
                        p.op('dve', lambda e, pyt=pyt, yv=yv: e.tensor_add(yv, yv, pyt.t[:, :, :]), reads=[pyt.b, yacc.b], writes=[yacc.b])

                NB_ = len(blist)
                front_pe(0)
                front_ev(0)
                if NB_ > 1:
                    front_pe(1)
                for k in range(NB_):
                    if k + 1 < NB_:
                        front_ev(k + 1)
                    if k + 2 < NB_:
                        front_pe(k + 2)
                    back(k)
                    if k >= 1:
                        yevac(k - 1)
                yevac(NB_ - 1)
                for b in range(4):
                    yg = ygb[b % 2]
                    gt = gtb[b % 2]
                    ysl = yacc.t[:, b * 512:(b + 1) * 512]
                    p.op('dve', lambda e, gt=gt, ysl=ysl: e.tensor_mul(gt.t[:, :], ysl, ysl), reads=[yacc.b], writes=[gt.b])
                    p.op('dve', lambda e, gt=gt: e.tensor_scalar(gt.t[:, :], gt.t[:, :], 0.044715, 1.0, ALU.mult, ALU.add), reads=[gt.b], writes=[gt.b])
                    p.op('dve', lambda e, gt=gt, ysl=ysl: e.tensor_mul(gt.t[:, :], gt.t[:, :], ysl), reads=[gt.b, yacc.b], writes=[gt.b])
                    p.op('act', lambda e, gt=gt: e.activation(gt.t[:, :], gt.t[:, :], AF.Sigmoid, scale=2.0 * math.sqrt(2.0 / math.pi)), reads=[gt.b], writes=[gt.b])
                    p.op('dve', lambda e, gt=gt, ysl=ysl, yg=yg: e.tensor_mul(yg.t[:, :], gt.t[:, :], ysl), reads=[gt.b, yacc.b], writes=[yg.b])
                    p.dma(YGs[Q, :, b * 512:(b + 1) * 512], yg.t[:, :], reads=[yg.b])
            p.barrier()
    with ExitStack() as st:
        Wgl = sbt(st, "s5Wgl", [128, 8, 2 * D], BF16)
        with ExitStack() as st2:
            stg = [sbt(st2, f"s5gst{i}", [128, 2 * D], F32) for i in range(2)]
            for k in range(8):
                s_ = stg[k % 2]
                p.dma(s_.t[:, 0:D], I["ssm_glu_w"][k * 128:(k + 1) * 128, 0:D], writes=[s_.b])
                p.dma(s_.t[:, D:2 * D], I["ssm_glu_w"][k * 128:(k + 1) * 128, D:2 * D], writes=[s_.b])
                ew(lambda e, k=k, s_=s_: e.tensor_copy(Wgl.t[:, k, 0:D], s_.t[:, 0:D]), [s_.b], [Wgl.b])
                ew(lambda e, k=k, s_=s_: e.tensor_copy(Wgl.t[:, k, D:2 * D], s_.t[:, D:2 * D]), [s_.b], [Wgl.b])
            p.barrier()
        g1 = load_gate(st, "s5g1", 1, 0, 0)
        ygt = [sbt(st, f"s5ygt{i}", [128, 8, 128], BF16) for i in range(2)]
        h2 = [sbt(st, f"s5h2{i}", [128, D], F32) for i in range(2)]
        h3 = [sbt(st, f"s5h3{i}", [128, D], F32) for i in range(2)]
        sg_ = [sbt(st, f"s5sg{i}", [128, 512], F32) for i in range(2)]
        pa = [pst(st, f"s5pa{i}", [128, 512]) for i in range(2)]
        pgl = [pst(st, f"s5pg{i}", [128, 512]) for i in range(2)]
        c_ = 0
        YGs4 = YGs.rearrange("q p (r t) -> q p r t", r=2)
        for ti in range(16):
            yt = ygt[ti % 2]
            p.dma(yt.t[:], YGs[:, :, ti * 128:(ti + 1) * 128].rearrange("q p t -> p q t"), writes=[yt.b])
            hh = h2[ti % 2]
            ho_ = h3[ti % 2]
            p.dma(hh.t[:], Hloc[ti * 128:(ti + 1) * 128, :], writes=[hh.b])
            for half in range(2):
                a = pa[c_ % 2]
                g = pgl[c_ % 2]
                sg2 = sg_[c_ % 2]
                c_ += 1
                for k in range(8):
                    p.op('pe', lambda e, a=a, k=k, yt=yt, half=half: e.matmul(a.t[:, :], yt.t[:, k, :], Wgl.t[:, k, half * 512:(half + 1) * 512], start=(k == 0), stop=(k == 7)),
                         reads=[yt.b, Wgl.b], writes=[a.b])
                for k in range(8):
                    p.op('pe', lambda e, g=g, k=k, yt=yt, half=half: e.matmul(g.t[:, :], yt.t[:, k, :], Wgl.t[:, k, D + half * 512:D + (half + 1) * 512], start=(k == 0), stop=(k == 7)),
                         reads=[yt.b, Wgl.b], writes=[g.b])
                p.op('act', lambda e, g=g, sg2=sg2: e.activation(sg2.t[:, :], g.t[:, :], AF.Sigmoid), reads=[g.b], writes=[sg2.b])
                p.op('dve', lambda e, a=a, sg2=sg2: e.tensor_mul(sg2.t[:, :], sg2.t[:, :], a.t[:, :]), reads=[a.b, sg2.b], writes=[sg2.b])
                p.op('pool', lambda e, sg2=sg2, half=half: e.tensor_mul(sg2.t[:, :], sg2.t[:, :], g1.t[:, half * 512:(half + 1) * 512]), reads=[sg2.b, g1.b], writes=[sg2.b])
                p.op('pool', lambda e, sg2=sg2, hh=hh, ho_=ho_, half=half: e.tensor_add(ho_.t[:, half * 512:(half + 1) * 512], hh.t[:, half * 512:(half + 1) * 512], sg2.t[:, :]),
                     reads=[sg2.b, hh.b], writes=[ho_.b])
            p.dma(Hloc[ti * 128:(ti + 1) * 128, :], ho_.t[:], reads=[ho_.b])
        p.barrier()


_CACHE = {}


def kernel(**inputs):
    stage = int(inputs.pop("_stage", 99))
    if "nc" not in _CACHE or _CACHE.get("stage") != stage:
        _CACHE["nc"] = build(stage)
        _CACHE["stage"] = stage
        _CACHE["consts"] = host_consts()
    nc = _CACHE["nc"]
    if "consts_rev" not in _CACHE:
        _CACHE["consts_rev"] = host_consts(rev=True)
    f = lambda a: np.ascontiguousarray(np.asarray(a, dtype=np.float32))
    in_maps = []
    for core in range(8):
        b = core // 2
        rv = (core % 2 == 1) and stage >= 99
        cs = _CACHE["consts_rev"] if rv else _CACHE["consts"]
        sd = (lambda a: np.asarray(a)[0][::-1]) if rv else (lambda a: np.asarray(a)[0])
        xb = np.asarray(inputs["x"][b])
        cb = np.asarray(inputs["ctx"][b])
        if rv:
            xb = xb[::-1]
            cb = cb[::-1]
        m = {
            "x": f(xb), "c": f(inputs["c"][b]).reshape(8, 128), "ctx": f(cb),
            "c_ctx": f(inputs["c_ctx"]).reshape(8, 128),
            "mod_w": f(inputs["mod_w"]), "mod_b": f(inputs["mod_b"]).reshape(2, 48, 128),
            "norm_g": f(inputs["norm_g"]).reshape(2, 2, 8, 128),
            "ffn_w_gate": f(inputs["ffn_w_gate"]), "ffn_w_up": f(inputs["ffn_w_up"]), "ffn_w_down": f(inputs["ffn_w_down"]),
            "mix_w_in": f(inputs["mix_w_in"][0]), "mix_w_out": f(inputs["mix_w_out"][0]), "attn_sink": f(inputs["attn_sink"]).reshape(1, 8),
            "ssm_a_re": f(sd(inputs["ssm_a_re"])).reshape(128, 64), "ssm_a_im": f(sd(inputs["ssm_a_im"])).reshape(128, 64),
            "ssm_log_dt": f(sd(inputs["ssm_log_dt"])).reshape(1, 128),
            "ssm_b_re": f(sd(inputs["ssm_b_re"])), "ssm_b_im": f(sd(inputs["ssm_b_im"])),
            "ssm_c_re": f(sd(inputs["ssm_c_re"])), "ssm_c_im": f(sd(inputs["ssm_c_im"])),
            "ssm_d": f(inputs["ssm_d"][0]).reshape(8, 128), "ssm_glu_w": f(inputs["ssm_glu_w"][0]), "final_g": f(inputs["final_g"]).reshape(1, D),
        }
        m.update(cs)
        m["rk"] = np.array([[0]], np.int32)
        in_maps.append(m)
    res = run_bass_kernel_spmd(nc, in_maps, core_ids=list(range(8)))
    if stage < 99:
        return np.stack([np.asarray(res.results[2 * b]["out"], dtype=np.float32) for b in range(4)], axis=0)
    outp = np.stack([np.concatenate([np.asarray(res.results[2 * b]["out"], dtype=np.float32),
                                     np.asarray(res.results[2 * b + 1]["out"], dtype=np.float32)[::-1]], axis=0) for b in range(4)], axis=0)
    return outp
```

```python
import math
import numpy as np
import ml_dtypes
import concourse.bass as bass
import concourse.mybir as mybir
from concourse.bass_utils import run_bass_kernel_spmd
from contextlib import ExitStack

F32 = mybir.dt.float32
BF16 = mybir.dt.bfloat16
AF = mybir.ActivationFunctionType
ALU = mybir.AluOpType
NPBF = ml_dtypes.bfloat16

D = 1024
T = 4096
TC = 256
FF = 2816
NFC = 22
EPS = 1e-6


class Buf:
    def __init__(self, name=""):
        self.name = name
        self.last_w = None
        self.readers = []


class Prog:
    ENG = ['pe', 'act', 'dve', 'pool', 'sp']

    def __init__(self, nc, ndma_sems=10):
        self.nc = nc
        self.ops = {e: [] for e in self.ENG}
        self.cnt = {e: 0 for e in self.ENG}
        self.known = {e: {} for e in self.ENG}
        self.es = ExitStack()
        self.sem = {e: self.es.enter_context(nc.semaphore('s_' + e)) for e in self.ENG}
        self.dma_sems = {e: [self.es.enter_context(nc.semaphore(f'd_{e}{i}')) for i in range(ndma_sems)]
                         for e in ['sp', 'act', 'pool']}
        self.dma_val = {e: [0] * ndma_sems for e in self.dma_sems}
        self.dma_rr = {e: 0 for e in self.dma_sems}
        self.semobj = {}
        for e in self.ENG:
            self.semobj[('c', e)] = self.sem[e]
        for e in self.dma_sems:
            for i, s in enumerate(self.dma_sems[e]):
                self.semobj[('d', e, i)] = s
        self.q = 0
        self.rank_ap = None

    def _waits(self, eng, toks):
        need = {}
        for t in toks:
            if t is None:
                continue
            k, v = t
            if k == ('c', eng) and eng == 'pe':
                continue
            if self.known[eng].get(k, 0) >= v:
                continue
            if need.get(k, 0) < v:
                need[k] = v
        for k, v in need.items():
            self.known[eng][k] = v
        return list(need.items())

    def _deps(self, reads, writes):
        toks = []
        for b in reads:
            toks.append(b.last_w)
        for b in writes:
            toks.append(b.last_w)
            toks.extend(b.readers)
        return toks

    def _commit(self, tok, reads, writes):
        for b in reads:
            b.readers.append(tok)
            if len(b.readers) > 64:
                b.readers = b.readers[-64:]
        for b in writes:
            b.last_w = tok
            b.readers = []

    def op(self, eng, fn, reads=(), writes=()):
        waits = self._waits(eng, self._deps(reads, writes))
        self.cnt[eng] += 1
        tok = (('c', eng), self.cnt[eng])
        self.ops[eng].append((waits, fn, (self.sem[eng], 1)))
        self._commit(tok, reads, writes)
        return tok

    def dma(self, out, in_, reads=(), writes=(), eng=None, **kw):
        if eng is None:
            is_store = (not callable(out)) and ('DRam' in type(out.tensor).__name__)
            eng = 'pool' if is_store else 'sp'
        toks = self._deps(reads, writes)
        i = self.dma_rr[eng]
        self.dma_rr[eng] = (i + 1) % len(self.dma_sems[eng])
        key = ('d', eng, i)
        prev = self.dma_val[eng][i]
        if prev > 0:
            toks.append((key, prev))
        waits = self._waits(eng, toks)
        self.dma_val[eng][i] = prev + 16
        tok = (key, prev + 16)
        def issue(e, out=out, in_=in_, eng=eng):
            o = out(self.dyn[eng]) if callable(out) else out
            i2 = in_(self.dyn[eng]) if callable(in_) else in_
            return e.dma_start(out=o, in_=i2, **kw)
        self.ops[eng].append((waits, issue, (self.dma_sems[eng][i], 16)))
        self._commit(tok, reads, writes)
        return tok

    def all_tokens(self):
        allt = []
        for e in self.ENG:
            if self.cnt[e]:
                allt.append((('c', e), self.cnt[e]))
        for e in self.dma_sems:
            for i, v in enumerate(self.dma_val[e]):
                if v:
                    allt.append((('d', e, i), v))
        return allt

    def barrier(self):
        allt = self.all_tokens()
        for e in self.ENG:
            w = self._waits(e, allt)
            if w:
                self.ops[e].append((w, None, None))

    def emit(self):
        nc = self.nc
        fin = self._waits('sp', self.all_tokens())
        self.ops['sp'].append((fin, None, None))
        self.dyn = {}
        with nc.Block() as block:
            def mk(eng):
                def run(e):
                    for waits, fn, inc in self.ops[eng]:
                        for k, v in waits:
                            e.wait_ge(self.semobj[k], v)
                        if fn is not None:
                            fn(e).then_inc(inc[0], inc[1])

                def body(e):
                    if eng in ('sp', 'act') and self.rank_ap is not None:
                        with e.register("rk_" + eng) as reg:
                            e.reg_load(reg, self.rank_ap)
                            self.dyn[eng] = e.snap(reg, min_val=0, max_val=2048)
                            run(e)
                    else:
                        run(e)
                return body
            block.tensor(mk('pe'))
            block.scalar(mk('act'))
            block.vector(mk('dve'))
            block.gpsimd(mk('pool'))
            block.sync(mk('sp'))
        self.es.close()


class TB:
    def __init__(self, t, name=""):
        self.t = t
        self.b = Buf(name)


def host_consts(rev=False):
    cs = {}
    cs["identf"] = np.eye(128, dtype=np.float32)
    cs["identb"] = np.eye(128, dtype=np.float32).astype(NPBF)
    t = np.arange(T)
    row = (t // 64).astype(np.float64)
    col = (t % 64).astype(np.float64)
    nf = 16
    inv = 10000.0 ** (-np.arange(nf, dtype=np.float64) / nf)
    inv = inv.astype(np.float32).astype(np.float64)
    ang = np.concatenate([(row[:, None].astype(np.float32) * inv[None].astype(np.float32)),
                          (col[:, None].astype(np.float32) * inv[None].astype(np.float32))], axis=-1).astype(np.float32)
    cosv = np.cos(ang).astype(np.float32)
    sinv = np.sin(ang).astype(np.float32)
    C = np.zeros((128, T), np.float32)
    S = np.zeros((128, T), np.float32)
    for p in range(128):
        d = p % 64
        i = d // 2
        C[p] = cosv[:, i]
        S[p] = sinv[:, i] * (-1.0 if d % 2 == 0 else 1.0)
    if rev:
        C = np.ascontiguousarray(C[:, ::-1])
        S = np.ascontiguousarray(S[:, ::-1])
    cs["ropeC"] = C
    cs["ropeS"] = S
    j = np.arange(128)[:, None]
    i = np.arange(128)[None, :]
    mp = np.where(j >= i, 0.0, -30000.0).astype(np.float32)
    mn = np.where(j <= i, 0.0, -30000.0).astype(np.float32)
    cs["maskP"] = np.tile(mp, (1, 4)).astype(NPBF)
    cs["maskN"] = np.tile(mn, (1, 4)).astype(NPBF)
    tt = np.arange(T, dtype=np.int64)
    tk = (tt[:, None] * tt[None, :]) % T
    angT = 2.0 * np.pi * tk / T
    ct = (np.cos(angT) / math.sqrt(T)).astype(np.float32)
    stt = (np.sin(angT) / math.sqrt(T)).astype(np.float32)
    if rev:
        ct = np.ascontiguousarray(ct[::-1, ::-1])
        stt = np.ascontiguousarray(stt[::-1, ::-1])
    cs["CT"] = np.ascontiguousarray(ct.reshape(32, 128, 16, 256).transpose(2, 1, 0, 3)).astype(NPBF)
    cs["ST"] = np.ascontiguousarray(stt.reshape(32, 128, 16, 256).transpose(2, 1, 0, 3)).astype(NPBF)
    t2 = np.arange(TC, dtype=np.int64)
    a2 = 2.0 * np.pi * ((t2[:, None] * t2[None, :]) % TC) / TC
    c2 = (np.cos(a2) / math.sqrt(TC)).astype(np.float32)
    s2 = (np.sin(a2) / math.sqrt(TC)).astype(np.float32)
    if rev:
        c2 = np.ascontiguousarray(c2[::-1, ::-1])
        s2 = np.ascontiguousarray(s2[::-1, ::-1])
    cs["C256"] = np.ascontiguousarray(c2.reshape(2, 128, 256).transpose(1, 0, 2)).astype(NPBF)
    cs["S256"] = np.ascontiguousarray(s2.reshape(2, 128, 256).transpose(1, 0, 2)).astype(NPBF)
    c64 = np.arange(64)
    a3 = 2.0 * np.pi * ((c64[:, None] * c64[None, :]) % 64) / 64
    cc = np.zeros((128, 128), np.float32)
    sc = np.zeros((128, 128), np.float32)
    for g in range(2):
        cc[g * 64:(g + 1) * 64, g * 64:(g + 1) * 64] = np.cos(a3) / 8.0
        sc[g * 64:(g + 1) * 64, g * 64:(g + 1) * 64] = np.sin(a3) / 8.0
    cs["Cc"] = cc.astype(NPBF)
    cs["Sc"] = sc.astype(NPBF)
    return cs


CONST_SHAPES = {
    "identf": ([128, 128], F32), "identb": ([128, 128], BF16), "ropeC": ([128, T], F32), "ropeS": ([128, T], F32),
    "maskP": ([128, 512], BF16), "maskN": ([128, 512], BF16),
    "CT": ([16, 128, 32, 256], BF16), "ST": ([16, 128, 32, 256], BF16),
    "C256": ([128, 2, 256], BF16), "S256": ([128, 2, 256], BF16), "Cc": ([128, 128], BF16), "Sc": ([128, 128], BF16),
}

IN_SHAPES = {
    "x": [T, D], "c": [8, 128], "ctx": [TC, D], "c_ctx": [8, 128],
    "mod_w": [2, D, 6 * D], "mod_b": [2, 48, 128], "norm_g": [2, 2, 8, 128],
    "ffn_w_gate": [2, D, FF], "ffn_w_up": [2, D, FF], "ffn_w_down": [2, FF, D],
    "mix_w_in": [D, 1280], "mix_w_out": [D, D], "attn_sink": [1, 8],
    "ssm_a_re": [128, 64], "ssm_a_im": [128, 64], "ssm_log_dt": [1, 128],
    "ssm_b_re": [2, 64, 64, 16], "ssm_b_im": [2, 64, 64, 16], "ssm_c_re": [2, 64, 16, 64], "ssm_c_im": [2, 64, 16, 64],
    "ssm_d": [8, 128], "ssm_glu_w": [D, 2 * D], "final_g": [1, D],
}


def build(stage=99):
    nc = bass.Bass("TRN2", target_bir_lowering=False)
    I = {n: nc.dram_tensor(n, sh, F32, kind="ExternalInput").ap() for n, sh in IN_SHAPES.items()}
    K = {n: nc.dram_tensor(n, sh, dt, kind="ExternalInput").ap() for n, (sh, dt) in CONST_SHAPES.items()}
    rk_in = nc.dram_tensor("rk", [1, 1], mybir.dt.int32, kind="ExternalInput").ap()
    HT = T // 2
    out = nc.dram_tensor("out", [T if stage < 99 else HT, D], F32, kind="ExternalOutput").ap()
    Hs = nc.dram_tensor("Hs", [T + TC, D], F32).ap()
    mod_b_flat = I["mod_b"].rearrange("l j p -> l (j p)")

    p = Prog(nc)
    p.rank_ap = None
    top = ExitStack()
    Hloc = nc.dram_tensor("Hloc", [T // 2, D], F32).ap()
    p.Hloc = Hloc

    uid = {"n": 0}

    def sbt(st, name, shape, dt):
        uid["n"] += 1
        return TB(st.enter_context(nc.sbuf_tensor(f"s{uid['n']}_{name}", shape, dt)), name)

    def pst(st, name, shape, dt=F32):
        uid["n"] += 1
        return TB(st.enter_context(nc.psum_tensor(f"p{uid['n']}_{name}", shape, dt)), name)

    identf = sbt(top, "identf", [128, 128], F32)
    identb = sbt(top, "identb", [128, 128], BF16)
    p.dma(identf.t[:], K["identf"], writes=[identf.b])
    p.dma(identb.t[:], K["identb"], writes=[identb.b])
    ones_b = sbt(top, "ones_b", [128, 128], BF16)
    p.op('pool', lambda e: e.memset(ones_b.t[:], 1.0), writes=[ones_b.b])
    AB = sbt(top, "AB", [128, 2, 2, 4, 8], F32)
    Gs = nc.dram_tensor("Gs", [8, 128, D], F32).ap()
    ATs = nc.dram_tensor("ATs", [34, 128, 4, 128], BF16).ap()

    def load_gate(st, nm, l, col, gi):
        g = sbt(st, nm, [128, D], F32)
        p.dma(g.t[:], Gs[(l * 2 + col) * 2 + gi], writes=[g.b])
        return g
    rr = {"i": 0}

    def ew(fn, reads, writes, engs=('dve', 'pool')):
        e = engs[rr["i"] % len(engs)]
        rr["i"] += 1
        return p.op(e, fn, reads=reads, writes=writes)

    with ExitStack() as st:
        gates = sbt(st, "gates", [128, 2, 2, 2, D], F32)
        crow = sbt(st, "crow", [16, 128], F32)
        p.dma(crow.t[0:8, :], I["c"], writes=[crow.b])
        p.dma(crow.t[8:16, :], I["c_ctx"], writes=[crow.b])
        pT = pst(st, "pT", [128, 96])
        scT = sbt(st, "scT", [128, 16], F32)
        p.op('pe', lambda e: e.transpose(pT.t[:, 0:16], crow.t[:, :], identf.t[0:16, 0:16]), reads=[crow.b, identf.b], writes=[pT.b])
        p.op('act', lambda e: e.activation(scT.t[:], pT.t[:, 0:16], AF.Silu), reads=[pT.b], writes=[scT.b])
        scbc = sbt(st, "scbc", [128, 16, 128], F32)
        for ck in range(16):
            ew(lambda e, ck=ck: e.tensor_copy(scbc.t[:, ck, :], scT.t[:, ck:ck + 1].to_broadcast([128, 128])), [scT.b], [scbc.b])
        mbrow = sbt(st, "mbrow", [48, 2, 128], F32)
        ngrow = sbt(st, "ngrow", [32, 128], F32)
        p.dma(mbrow.t[:, 0, :], I["mod_b"][0], writes=[mbrow.b])
        p.dma(mbrow.t[:, 1, :], I["mod_b"][1], writes=[mbrow.b])
        p.dma(ngrow.t[:], I["norm_g"].rearrange("l i k p -> (l i k) p"), writes=[ngrow.b])
        mbT = sbt(st, "mbT", [128, 2, 48], F32)
        ngT = sbt(st, "ngT", [128, 32], F32)
        for l in range(2):
            p.op('pe', lambda e, l=l: e.transpose(pT.t[:, 0:48], mbrow.t[:, l, :], identf.t[0:48, 0:48]), reads=[mbrow.b, identf.b], writes=[pT.b])
            p.op('dve', lambda e, l=l: e.tensor_copy(mbT.t[:, l, :], pT.t[:, 0:48]), reads=[pT.b], writes=[mbT.b])
        p.op('pe', lambda e: e.transpose(pT.t[:, 0:32], ngrow.t[:, :], identf.t[0:32, 0:32]), reads=[ngrow.b, identf.b], writes=[pT.b])
        p.op('dve', lambda e: e.tensor_copy(ngT.t[:], pT.t[:, 0:32]), reads=[pT.b], writes=[ngT.b])
        macc = sbt(st, "macc", [128, 2, 48, 2], F32)
        Wk = [sbt(st, f"Wk{i}", [128, 6 * D], F32) for i in range(2)]
        pg = [pst(st, f"pg{i}", [128, 512]) for i in range(2)]
        pm = pst(st, "pm", [128, 96])
        it = 0
        for l in range(2):
            for k in range(8):
                w = Wk[it % 2]
                it += 1
                for q3 in range(3):
                    p.dma(w.t[:, q3 * 2048:(q3 + 1) * 2048], I["mod_w"][l, k * 128:(k + 1) * 128, q3 * 2048:(q3 + 1) * 2048], writes=[w.b])
                for j in range(48):
                    p.op('pe', lambda e, j=j, w=w, k=k: e.matmul(pm.t[:, 2 * j:2 * j + 2], w.t[:, j * 128:(j + 1) * 128],
                                                                 scT.t[:, k:16:8], start=True, stop=True),
                         reads=[w.b, scT.b], writes=[pm.b])
                if k == 0:
                    p.op('dve', lambda e, l=l: e.tensor_copy(macc.t[:, l].rearrange("p j c -> p (j c)"), pm.t[:, :]), reads=[pm.b], writes=[macc.b])
                else:
                    p.op('dve', lambda e, l=l: e.tensor_add(macc.t[:, l].rearrange("p j c -> p (j c)"), macc.t[:, l].rearrange("p j c -> p (j c)"), pm.t[:, :]),
                         reads=[pm.b, macc.b], writes=[macc.b])
                gi = 0
                for col in range(2):
                    for g_i, which in enumerate((2, 5)):
                        for half in range(2):
                            pgt = pg[gi % 2]
                            gi += 1
                            p.op('pe', lambda e, pgt=pgt, col=col, k=k, w=w, which=which, half=half: e.matmul(
                                pgt.t[:, :], scbc.t[:, col * 8 + k, :], w.t[:, which * D + half * 512: which * D + half * 512 + 512], start=True, stop=True),
                                reads=[scbc.b, w.b], writes=[pgt.b])
                            dst = gates.t[:, l, col, g_i, half * 512:(half + 1) * 512]
                            if k == 0:
                                p.op('dve', lambda e, dst=dst, pgt=pgt: e.tensor_copy(dst, pgt.t[:, :]), reads=[pgt.b], writes=[gates.b])
                            else:
                                p.op('dve', lambda e, dst=dst, pgt=pgt: e.tensor_add(dst, dst, pgt.t[:, :]), reads=[pgt.b, gates.b], writes=[gates.b])
        gb = sbt(st, "gb", [128, D], F32)
        for l in range(2):
            for col in range(2):
                p.op('dve', lambda e, l=l, col=col: e.tensor_add(macc.t[:, l, :, col], macc.t[:, l, :, col], mbT.t[:, l, :]), reads=[macc.b, mbT.b], writes=[macc.b])
            for g_i, which in enumerate((2, 5)):
                p.dma(gb.t[:], mod_b_flat[l:l + 1, which * D:(which + 1) * D].partition_broadcast(128), writes=[gb.b])
                for col in range(2):
                    p.op('dve', lambda e, l=l, col=col, g_i=g_i: e.tensor_add(gates.t[:, l, col, g_i, :], gates.t[:, l, col, g_i, :], gb.t[:]),
                         reads=[gb.b, gates.b], writes=[gates.b])
            for col in range(2):
                for i2 in range(2):
                    sh = macc.t[:, l, (3 * i2) * 8:(3 * i2) * 8 + 8, col]
                    scl = macc.t[:, l, (3 * i2 + 1) * 8:(3 * i2 + 1) * 8 + 8, col]
                    gn = ngT.t[:, (l * 2 + i2) * 8:(l * 2 + i2) * 8 + 8]
                    p.op('dve', lambda e, l=l, col=col, i2=i2, scl=scl, gn=gn: e.scalar_tensor_tensor(
                        AB.t[:, l, col, 2 * i2, :], scl, 1.0, gn, ALU.add, ALU.mult), reads=[macc.b, ngT.b], writes=[AB.b])
                    p.op('dve', lambda e, l=l, col=col, i2=i2, sh=sh: e.tensor_copy(AB.t[:, l, col, 2 * i2 + 1, :], sh), reads=[macc.b], writes=[AB.b])
        for l in range(2):
            for col in range(2):
                for g_i in range(2):
                    p.dma(Gs[(l * 2 + col) * 2 + g_i], gates.t[:, l, col, g_i, :], reads=[gates.b])
        p.barrier()

    if stage == 0:
        with ExitStack() as st:
            g0 = load_gate(st, "g0dbg", 0, 0, 0)
            p.dma(out[0:128, :], g0.t[:], reads=[g0.b])
        p.dma(out[128:256, 0:128], AB.t[:].rearrange("p a b c d -> p (a b c d)"), reads=[AB.b])
        p.emit()
        top.close()
        return nc

    def make_norm(st, nm, nxt=2):
        ctxn = {}
        ctxn["xt"] = [sbt(st, f"{nm}xt{i}", [128, D], F32) for i in range(nxt)]
        ctxn["junk"] = sbt(st, f"{nm}junk", [128, D], BF16)
        ctxn["xn"] = [sbt(st, f"{nm}xn{i}", [128, D], BF16) for i in range(2)]
        ctxn["ss"] = [sbt(st, f"{nm}ss{i}", [128, 1], F32) for i in range(3)]
        ctxn["tp"] = [pst(st, f"{nm}tp{i}", [128, 8, 128], BF16) for i in range(2)]
        ctxn["n"] = 0
        return ctxn

    def norm_tile(cn, src, l, col, which, dst_fn, dst_buf, xt_fixed=None, ident=None):
        n = cn["n"]
        cn["n"] += 1
        xt = xt_fixed if xt_fixed is not None else cn["xt"][n % len(cn["xt"])]
        ss = cn["ss"][n % 3]
        xn = cn["xn"][n % 2]
        tp = cn["tp"][n % 2]
        junk = cn["junk"]
        idt = ident if ident is not None else identb
        p.dma(xt.t[:], src, writes=[xt.b])
        p.op('act', lambda e: e.activation(junk.t[:], xt.t[:], AF.Square, accum_out=ss.t[:]), reads=[xt.b], writes=[junk.b, ss.b])
        p.op('dve', lambda e: e.tensor_scalar(ss.t[:], ss.t[:], 1.0 / D, EPS, ALU.mult, ALU.add), reads=[ss.b], writes=[ss.b])
        p.op('act', lambda e: e.activation(ss.t[:], ss.t[:], AF.Sqrt), reads=[ss.b], writes=[ss.b])
        p.op('dve', lambda e: e.reciprocal(ss.t[:], ss.t[:]), reads=[ss.b], writes=[ss.b])
        p.op('dve', lambda e: e.tensor_scalar(xn.t[:], xt.t[:], ss.t[:, 0:1], None, ALU.mult), reads=[xt.b, ss.b], writes=[xn.b])
        for k in range(8):
            p.op('pe', lambda e, k=k: e.transpose(tp.t[:, k, :], xn.t[:, k * 128:(k + 1) * 128], idt.t[:]), reads=[xn.b, idt.b], writes=[tp.b])
        for k in range(8):
            p.op('act', lambda e, k=k: e.activation(dst_fn(k), tp.t[:, k, :], AF.Identity,
                                                    scale=AB.t[:, l, col, 2 * which, k:k + 1], bias=AB.t[:, l, col, 2 * which + 1, k:k + 1]),
                 reads=[tp.b, AB.b], writes=[dst_buf])
        return xt, ss

    def tile_src(ti):
        return I["x"][ti * 128:(ti + 1) * 128, :] if ti < 32 else I["ctx"][(ti - 32) * 128:(ti - 31) * 128, :]

    NT = 34
    L0 = ExitStack()
    Fm = sbt(L0, "Fm", [128, NT, 512], BF16)
    with ExitStack() as stBC:
        QT = sbt(stBC, "QT", [128, 4, NT * 128], BF16)
        KT = sbt(stBC, "KT", [128, NT * 128], BF16)
        Vm = sbt(stBC, "Vm", [128, NT, 128], BF16)
        with ExitStack() as st:
            wb = sbt(st, "wb", [128, 8, 1920], BF16)
            wst = [sbt(st, f"wst{i}", [128, 1280], F32) for i in range(1)]
            for k in range(8):
                s_ = wst[0]
                p.dma(s_.t[:], I["mix_w_in"][k * 128:(k + 1) * 128, :], writes=[s_.b])
                S = s_.t
                W = wb.t
                ew(lambda e, k=k, S=S, W=W: e.tensor_copy(W[:, k, 0:512], S[:, 0:512]), [s_.b], [wb.b])
                ew(lambda e, k=k, S=S, W=W: e.tensor_copy(W[:, k, 512:640], S[:, 1152:1280]), [s_.b], [wb.b])
                ew(lambda e, k=k, S=S, W=W: e.tensor_copy(W[:, k, 640:1152].rearrange("p (j h d) -> p j h d", j=4, h=2),
                                                          S[:, 512:1024].rearrange("p (h j d) -> p j h d", h=2, j=4)), [s_.b], [wb.b])
                ew(lambda e, k=k, S=S, W=W: e.tensor_copy(W[:, k, 1152:1280], S[:, 1024:1152]), [s_.b], [wb.b])
                for two in range(2):
                    ew(lambda e, k=k, S=S, W=W, two=two: e.tensor_copy(
                        W[:, k, 1280:1792].rearrange("p (j h i t) -> p j h i t", j=4, h=2, t=2)[:, :, :, :, two],
                        S[:, 512:1024].rearrange("p (h j i t) -> p j h i t", h=2, j=4, t=2)[:, :, :, :, 1 - two]), [s_.b], [wb.b])
                    ew(lambda e, k=k, S=S, W=W, two=two: e.tensor_copy(
                        W[:, k, 1792:1920].rearrange("p (i t) -> p i t", t=2)[:, :, two],
                        S[:, 1024:1152].rearrange("p (i t) -> p i t", t=2)[:, :, 1 - two]), [s_.b], [wb.b])
            ropeCb = [sbt(st, f"ropeC{i}", [128, 512], F32) for i in range(2)]
            ropeSb = [sbt(st, f"ropeS{i}", [128, 512], F32) for i in range(2)]
            cn = make_norm(st, "b")
            nTb = [sbt(st, f"nTb{i}", [128, 8, 512], BF16) for i in range(2)]
            pf = pst(st, "pf", [128, 512])
            pv = pst(st, "pv", [128, 128])
            pq = [pst(st, f"pq{i}", [128, 512]) for i in range(2)]
            pqp = [pst(st, f"pqp{i}", [128, 512]) for i in range(2)]
            t1 = [sbt(st, f"t1{i}", [128, 512], F32) for i in range(2)]
            t2 = [sbt(st, f"t2{i}", [128, 512], F32) for i in range(2)]
            qi_ = 0
            for blk in range(9):
                ntile = 4 if blk < 8 else 2
                ncol = ntile * 128
                nT = nTb[blk % 2]
                col = 0 if blk < 8 else 1
                for tt in range(ntile):
                    ti = blk * 4 + tt
                    norm_tile(cn, tile_src(ti), 0, col, 0, lambda k, tt=tt, nT=nT: nT.t[:, k, tt * 128:(tt + 1) * 128], nT.b)
                    for k in range(8):
                        p.op('pe', lambda e, k=k, tt=tt, nT=nT: e.matmul(pf.t[:, :], nT.t[:, k, tt * 128:(tt + 1) * 128], wb.t[:, k, 0:512],
                                                                      start=(k == 0), stop=(k == 7)), reads=[nT.b, wb.b], writes=[pf.b])
                    p.op('act', lambda e, ti=ti: e.activation(Fm.t[:, ti, :], pf.t[:, :], AF.Identity), reads=[pf.b], writes=[Fm.b])
                    for k in range(8):
                        p.op('pe', lambda e, k=k, tt=tt, nT=nT: e.matmul(pv.t[:, :], nT.t[:, k, tt * 128:(tt + 1) * 128], wb.t[:, k, 512:640],
                                                                      start=(k == 0), stop=(k == 7)), reads=[nT.b, wb.b], writes=[pv.b])
                    p.op('dve', lambda e, ti=ti: e.tensor_copy(Vm.t[:, ti, :], pv.t[:, :]), reads=[pv.b], writes=[Vm.b])
                c0 = blk * 512
                ropeC = ropeCb[blk % 2]
                ropeS = ropeSb[blk % 2]
                if blk < 8:
                    p.dma(ropeC.t[:], K["ropeC"][:, c0:c0 + 512], writes=[ropeC.b])
                    p.dma(ropeS.t[:], K["ropeS"][:, c0:c0 + 512], writes=[ropeS.b])
                for oc in range(5):
                    a = pq[qi_ % 2]
                    bq = pqp[qi_ % 2]
                    u1 = t1[qi_ % 2]
                    u2 = t2[qi_ % 2]
                    qi_ += 1
                    for k in range(8):
                        p.op('pe', lambda e, k=k, oc=oc, a=a, nT=nT, ncol=ncol: e.matmul(a.t[:, 0:ncol], wb.t[:, k, 640 + 128 * oc: 768 + 128 * oc], nT.t[:, k, 0:ncol],
                                                                                   start=(k == 0), stop=(k == 7)), reads=[nT.b, wb.b], writes=[a.b])
                    dstb = QT.b if oc < 4 else KT.b
                    dst = QT.t[:, oc, c0:c0 + ncol] if oc < 4 else KT.t[:, c0:c0 + ncol]
                    if blk < 8:
                        for k in range(8):
                            p.op('pe', lambda e, k=k, oc=oc, bq=bq, nT=nT, ncol=ncol: e.matmul(bq.t[:, 0:ncol], wb.t[:, k, 1280 + 128 * oc: 1408 + 128 * oc], nT.t[:, k, 0:ncol],
                                                                                        start=(k == 0), stop=(k == 7)), reads=[nT.b, wb.b], writes=[bq.b])
                        p.op('dve', lambda e, a=a, u1=u1, ropeC=ropeC: e.tensor_mul(u1.t[:, :], a.t[:, :], ropeC.t[:, :]), reads=[a.b, ropeC.b], writes=[u1.b])
                        p.op('dve', lambda e, bq=bq, u2=u2, ropeS=ropeS: e.tensor_mul(u2.t[:, :], bq.t[:, :], ropeS.t[:, :]), reads=[bq.b, ropeS.b], writes=[u2.b])
                        p.op('pool', lambda e, dst=dst, u1=u1, u2=u2: e.tensor_add(dst, u1.t[:, :], u2.t[:, :]), reads=[u1.b, u2.b], writes=[dstb])
                    else:
                        p.op('dve', lambda e, dst=dst, a=a, ncol=ncol: e.tensor_copy(dst, a.t[:, 0:ncol]), reads=[a.b], writes=[dstb])
            p.barrier()
        with ExitStack() as st:
            maskP = sbt(st, "maskP", [128, 512], BF16)
            maskN = sbt(st, "maskN", [128, 512], BF16)
            p.dma(maskP.t[:], K["maskP"], writes=[maskP.b])
            p.dma(maskN.t[:], K["maskN"], writes=[maskN.b])
            sk = sbt(st, "sk", [128, 8], F32)
            p.dma(sk.t[:], I["attn_sink"].partition_broadcast(128), writes=[sk.b])
            p.op('act', lambda e: e.activation(sk.t[:], sk.t[:], AF.Exp), reads=[sk.b], writes=[sk.b])
            skf = sbt(st, "skf", [128, 512], F32)
            for g in range(2):
                for hh in range(4):
                    p.op('dve', lambda e, g=g, hh=hh: e.tensor_copy(skf.t[64 * g:64 * g + 64, hh * 128:(hh + 1) * 128],
                                                                  sk.t[64 * g:64 * g + 64, 4 * g + hh:4 * g + hh + 1].to_broadcast([64, 128])), reads=[sk.b], writes=[skf.b])
            pS = [pst(st, f"pS{i}", [128, 512]) for i in range(3)]
            pO = [pst(st, f"pO{i}", [128, 512]) for i in range(2)]
            pZ = [pst(st, f"pZ{i}", [128, 512]) for i in range(2)]
            PT = [sbt(st, f"PT{i}", [128, 512], BF16) for i in range(3)]
            den = [sbt(st, f"den{i}", [128, 512], F32) for i in range(2)]
            ATt = [sbt(st, f"ATt{i}", [128, 4, 128], BF16) for i in range(2)]
            OB = [[Buf() for _ in range(2)] for _ in range(2)]
            ZB = [[Buf() for _ in range(2)] for _ in range(2)]
            DB = [[Buf() for _ in range(2)] for _ in range(2)]
            si = 0
            pend = {"t": None}
            for qi in range(NT):
                AT = ATt[qi % 2]
                for g in range(2):
                    lo, hi = 64 * g, 64 * g + 64
                    keys = [(32, None), (33, None)]
                    if qi < 32:
                        if qi > 0:
                            keys.append((qi - 1, maskP))
                        keys.append((qi, None))
                        if qi < 31:
                            keys.append((qi + 1, maskN))
                    O = pO[qi % 2]
                    Z = pZ[qi % 2]
                    Ob = OB[qi % 2][g]
                    Zb = ZB[qi % 2][g]
                    Db = DB[qi % 2][g]
                    SP = []
                    for ki in range(len(keys)):
                        SP.append((pS[si % 3], PT[si % 3]))
                        si += 1

                    def emitS(ki):
                        kt, msk = keys[ki]
                        S_ = SP[ki][0]
                        p.op('pe', lambda e, S_=S_, kt=kt, qi=qi, lo=lo, hi=hi, msk=msk: e.matmul(
                            S_.t[:, :].rearrange("p (j q) -> p j q", j=4), KT.t[lo:hi, kt * 128:(kt + 1) * 128], QT.t[lo:hi, :, qi * 128:(qi + 1) * 128],
                            start=True, stop=(msk is None)), reads=[KT.b, QT.b], writes=[S_.b])
                        if msk is not None:
                            p.op('pe', lambda e, S_=S_, msk=msk: e.matmul(S_.t[:, :], identb.t[:, :], msk.t[:, :], start=False, stop=True),
                                 reads=[identb.b, msk.b], writes=[S_.b])
                    emitS(0)
                    if len(keys) > 1:
                        emitS(1)
                    for ki, (kt, msk) in enumerate(keys):
                        S_, P_ = SP[ki]
                        p.op('act', lambda e, S_=S_, P_=P_: e.activation(P_.t[:, :], S_.t[:, :], AF.Exp, scale=0.125), reads=[S_.b], writes=[P_.b])
                        if ki == 1 and pend["t"] is not None:
                            pend["t"]()
                            pend["t"] = None
                        if ki + 2 < len(keys):
                            emitS(ki + 2)
                        first = ki == 0
                        last = ki == len(keys) - 1
                        p.op('pe', lambda e, O=O, kt=kt, lo=lo, hi=hi, P_=P_, first=first, last=last: e.matmul(
                            O.t[lo:hi, :], Vm.t[:, kt, lo:hi], P_.t[:, :], start=first, stop=last), reads=[Vm.b, P_.b], writes=[Ob])
                        p.op('pe', lambda e, Z=Z, lo=lo, hi=hi, P_=P_, first=first, last=last: e.matmul(
                            Z.t[lo:hi, :], ones_b.t[:, 0:64], P_.t[:, :], start=first, stop=last), reads=[ones_b.b, P_.b], writes=[Zb])
                    dn = den[qi % 2]

                    def tail(dn=dn, Z=Z, O=O, lo=lo, hi=hi, AT=AT, Zb=Zb, Ob=Ob, Db=Db, g=g, qi=qi):
                        p.op('dve', lambda e: e.tensor_add(dn.t[lo:hi, :], Z.t[lo:hi, :], skf.t[lo:hi, :]), reads=[Zb, skf.b], writes=[Db])
                        p.op('act', lambda e: e.activation(dn.t[lo:hi, :], dn.t[lo:hi, :], AF.Ln), reads=[Db], writes=[Db])
                        p.op('act', lambda e: e.activation(dn.t[lo:hi, :], dn.t[lo:hi, :], AF.Exp, scale=-1.0), reads=[Db], writes=[Db])
                        p.op('dve', lambda e: e.tensor_mul(
                            AT.t[lo:hi, :, :], O.t[lo:hi, :].rearrange("p (j q) -> p j q", j=4),
                            dn.t[lo:hi, :].rearrange("p (j q) -> p j q", j=4)), reads=[Ob, Db], writes=[AT.b])
                        if g == 1:
                            p.dma(ATs[qi], AT.t[:], reads=[AT.b])
                    pend["t"] = tail
            if pend["t"] is not None:
                pend["t"]()
                pend["t"] = None
            p.barrier()

    with ExitStack() as st:
        wob = sbt(st, "wob", [128, 8, D], BF16)
        wst = [sbt(st, f"wost{i}", [128, D], F32) for i in range(2)]
        for j in range(8):
            s_ = wst[j % 2]
            if j < 4:
                p.dma(s_.t[:], I["mix_w_out"][j * 128:(j + 1) * 128, :], writes=[s_.b])
            else:
                jj = j - 4
                p.dma(s_.t[0:64, :], I["mix_w_out"][512 + 64 * jj:512 + 64 * jj + 64, :], writes=[s_.b])
                p.dma(s_.t[64:128, :], I["mix_w_out"][512 + 64 * (4 + jj):512 + 64 * (4 + jj) + 64, :], writes=[s_.b])
            ew(lambda e, j=j, s_=s_: e.tensor_copy(wob.t[:, j, :], s_.t[:]), [s_.b], [wob.b])
        Cc = sbt(st, "Cc", [128, 128], BF16)
        Sc = sbt(st, "Sc", [128, 128], BF16)
        p.dma(Cc.t[:], K["Cc"], writes=[Cc.b])
        p.dma(Sc.t[:], K["Sc"], writes=[Sc.b])
        CTb = [sbt(st, f"CTb{i}", [128, 32, 256], BF16) for i in range(2)]
        STb = [sbt(st, f"STb{i}", [128, 32, 256], BF16) for i in range(2)]
        pP = [pst(st, f"pP{i}", [128, 256]) for i in range(2)]
        pQ = [pst(st, f"pQ{i}", [128, 256]) for i in range(2)]
        pY = pst(st, "pY", [128, 256])
        pOo = [pst(st, f"pOo{i}", [128, 512]) for i in range(2)]
        Pb = [sbt(st, f"Pb{i}", [128, 256], BF16) for i in range(2)]
        Qb = [sbt(st, f"Qb{i}", [128, 256], BF16) for i in range(2)]
        YT = [sbt(st, f"YT{i}", [128, 4, 256], BF16) for i in range(2)]
        xres = [sbt(st, f"xres{i}", [128, D], F32) for i in range(2)]
        tmpo = [sbt(st, f"tmpo{i}", [128, 512], F32) for i in range(2)]
        hn = [sbt(st, f"hn{i}", [128, D], F32) for i in range(2)]
        ATl = [sbt(st, f"ATl{i}", [128, 4, 128], BF16) for i in range(2)]
        g1l = [load_gate(st, "g1lat", 0, 0, 0), load_gate(st, "g1ctx", 0, 1, 0)]
        cnt_ = 0
        oi = 0
        for kb in range(17):
            lat = kb < 16
            if lat:
                cb = CTb[kb % 2]
                sb_ = STb[kb % 2]
                for h4 in range(4):
                    p.dma(cb.t[:, h4 * 8:(h4 + 1) * 8, :], K["CT"][kb, :, h4 * 8:(h4 + 1) * 8, :], writes=[cb.b])
                    p.dma(sb_.t[:, h4 * 8:(h4 + 1) * 8, :], K["ST"][kb, :, h4 * 8:(h4 + 1) * 8, :], writes=[sb_.b])
                nti = 32
                t0 = 0
            else:
                cb = CTb[kb % 2]
                sb_ = STb[kb % 2]
                p.dma(cb.t[:, 0:2, :], K["C256"], writes=[cb.b])
                p.dma(sb_.t[:, 0:2, :], K["S256"], writes=[sb_.b])
                nti = 2
                t0 = 32
            Y = YT[kb % 2]
            for cc in range(4):
                a = pP[cnt_ % 2]
                b = pQ[cnt_ % 2]
                ab = Pb[cnt_ % 2]
                bb = Qb[cnt_ % 2]
                cnt_ += 1
                for i in range(nti):
                    p.op('pe', lambda e, a=a, i=i, cc=cc, cb=cb, t0=t0, nti=nti: e.matmul(a.t[:, :], Fm.t[:, t0 + i, cc * 128:(cc + 1) * 128], cb.t[:, i, :],
                                                                                     start=(i == 0), stop=(i == nti - 1)), reads=[Fm.b, cb.b], writes=[a.b])
                for i in range(nti):
                    p.op('pe', lambda e, b=b, i=i, cc=cc, sb_=sb_, t0=t0, nti=nti: e.matmul(b.t[:, :], Fm.t[:, t0 + i, cc * 128:(cc + 1) * 128], sb_.t[:, i, :],
                                                                                      start=(i == 0), stop=(i == nti - 1)), reads=[Fm.b, sb_.b], writes=[b.b])
                p.op('act', lambda e, a=a, ab=ab: e.activation(ab.t[:, :], a.t[:, :], AF.Identity), reads=[a.b], writes=[ab.b])
                p.op('act', lambda e, b=b, bb=bb: e.activation(bb.t[:, :], b.t[:, :], AF.Identity, scale=-1.0), reads=[b.b], writes=[bb.b])
                p.op('pe', lambda e, ab=ab: e.matmul(pY.t[:, :], Cc.t[:, :], ab.t[:, :], start=True, stop=False), reads=[Cc.b, ab.b], writes=[pY.b])
                p.op('pe', lambda e, bb=bb: e.matmul(pY.t[:, :], Sc.t[:, :], bb.t[:, :], start=False, stop=True), reads=[Sc.b, bb.b], writes=[pY.b])
                p.op('dve', lambda e, Y=Y, cc=cc: e.tensor_copy(Y.t[:, cc, :], pY.t[:, :]), reads=[pY.b], writes=[Y.b])
            for tt in range(2):
                ti = (kb * 2 + tt) if lat else 32 + tt
                col = 0 if lat else 1
                xr = xres[ti % 2]
                h_ = hn[ti % 2]
                p.dma(xr.t[:], tile_src(ti), writes=[xr.b])
                AT = ATl[ti % 2]
                p.dma(AT.t[:], ATs[ti], writes=[AT.b])
                for half in range(2):
                    po = pOo[oi % 2]
                    tm = tmpo[oi % 2]
                    oi += 1
                    for j in range(8):
                        lhs = Y.t[:, j, tt * 128:(tt + 1) * 128] if j < 4 else AT.t[:, j - 4, :]
                        rb = Y.b if j < 4 else AT.b
                        p.op('pe', lambda e, po=po, lhs=lhs, j=j, half=half: e.matmul(po.t[:, :], lhs, wob.t[:, j, half * 512:(half + 1) * 512],
                                                                                 start=(j == 0), stop=(j == 7)), reads=[rb, wob.b], writes=[po.b])
                    p.op('dve', lambda e, po=po, tm=tm, half=half, col=col: e.tensor_mul(tm.t[:, :], po.t[:, :], g1l[col].t[:, half * 512:(half + 1) * 512]),
                         reads=[po.b, g1l[col].b], writes=[tm.b])
                    p.op('pool', lambda e, tm=tm, xr=xr, h_=h_, half=half: e.tensor_add(h_.t[:, half * 512:(half + 1) * 512], xr.t[:, half * 512:(half + 1) * 512], tm.t[:, :]),
                         reads=[tm.b, xr.b], writes=[h_.b])
                p.dma(Hs[ti * 128:(ti + 1) * 128, :], h_.t[:], reads=[h_.b])
        p.barrier()
    L0.close()

    if stage == 1:
        for ti in range(32):
            pass
        with ExitStack() as st:
            bb_ = [sbt(st, f"bo{i}", [128, D], F32) for i in range(2)]
            for ti in range(32):
                b_ = bb_[ti % 2]
                p.dma(b_.t[:], Hs[ti * 128:(ti + 1) * 128, :], writes=[b_.b])
                p.dma(out[ti * 128:(ti + 1) * 128, :], b_.t[:], reads=[b_.b])
        p.emit()
        top.close()
        return nc

    def ffn(l, tiles, final):
        with ExitStack() as st:
            Wg = sbt(st, "Wg", [128, 8, FF], BF16)
            Wu = sbt(st, "Wu", [128, 8, FF], BF16)
            Wd = sbt(st, "Wd", [128, NFC, D], BF16)
            with ExitStack() as st2:
                stg = [sbt(st2, f"fst{i}", [128, FF], F32) for i in range(4)]
                si_ = 0
                for nm, Wt in (("ffn_w_gate", Wg), ("ffn_w_up", Wu)):
                    for k in range(8):
                        s_ = stg[si_ % 4]
                        si_ += 1
                        p.dma(s_.t[:, 0:1408], I[nm][l, k * 128:(k + 1) * 128, 0:1408], writes=[s_.b])
                        p.dma(s_.t[:, 1408:FF], I[nm][l, k * 128:(k + 1) * 128, 1408:FF], writes=[s_.b])
                        p.op('dve', lambda e, Wt=Wt, k=k, s_=s_: e.tensor_copy(Wt.t[:, k, 0:1408], s_.t[:, 0:1408]), reads=[s_.b], writes=[Wt.b])
                        p.op('act', lambda e, Wt=Wt, k=k, s_=s_: e.activation(Wt.t[:, k, 1408:FF], s_.t[:, 1408:FF], AF.Identity), reads=[s_.b], writes=[Wt.b])
                for fc in range(0, NFC, 2):
                    s_ = stg[si_ % 4]
                    si_ += 1
                    for q2 in range(2):
                        p.dma(s_.t[:, q2 * D:(q2 + 1) * D], I["ffn_w_down"][l, (fc + q2) * 128:(fc + q2 + 1) * 128, :], writes=[s_.b])
                    p.op('dve' if (fc // 2) % 2 == 0 else 'act', (lambda e, fc=fc, s_=s_: e.tensor_copy(Wd.t[:, fc:fc + 2, :].rearrange("p a d -> p (a d)"), s_.t[:, 0:2 * D])) if (fc // 2) % 2 == 0 else (lambda e, fc=fc, s_=s_: e.activation(Wd.t[:, fc:fc + 2, :].rearrange("p a d -> p (a d)"), s_.t[:, 0:2 * D], AF.Identity)), reads=[s_.b], writes=[Wd.b])
                p.barrier()
            cn = make_norm(st, "f", nxt=0)
            g2l = [load_gate(st, "g2lat", l, 0, 1)] + ([load_gate(st, "g2ctx", l, 1, 1)] if not final else [])
            hx = [sbt(st, f"hx{i}", [128, D], F32) for i in range(4)]
            nTf = [sbt(st, f"nTf{i}", [128, 8, 256], BF16) for i in range(2)]
            actT = [sbt(st, f"actT{i}", [128, NFC, 256], BF16) for i in range(1)]
            sg = [sbt(st, f"sg{i}", [128, 256], F32) for i in range(2)]
            pgt = [pst(st, f"fpg{i}", [128, 256]) for i in range(2)]
            put = [pst(st, f"fpu{i}", [128, 256]) for i in range(2)]
            pdn = [pst(st, f"fpd{i}", [128, 512]) for i in range(2)]
            tm_ = [sbt(st, f"ftm{i}", [128, 512], F32) for i in range(2)]
            ho = [sbt(st, f"ho{i}", [128, D], F32) for i in range(2)]
            fss = [sbt(st, f"fss{i}", [128, 1], F32) for i in range(2)]
            fj = cn["junk"]
            fingt = None
            if final:
                fingt = sbt(st, "fing", [128, D], F32)
                p.dma(fingt.t[:], I["final_g"].partition_broadcast(128), writes=[fingt.b])
            cnt = {"gi": 0, "di": 0, "hi": 0}
            blks = [tiles[b0:b0 + 2] for b0 in range(0, len(tiles), 2)]

            def do_norm(bi):
                tl = blks[bi]
                nT = nTf[bi % 2]
                xts = []
                for tt, ti in enumerate(tl):
                    col = 0 if ti < 32 else 1
                    hxt = hx[cnt['hi'] % 4]
                    cnt['hi'] += 1
                    src_ = Hs[ti * 128:(ti + 1) * 128, :]
                    norm_tile(cn, src_, l, col, 1, lambda k, tt=tt, nT=nT: nT.t[:, k, tt * 128:(tt + 1) * 128], nT.b, xt_fixed=hxt)
                    xts.append(hxt)

                return xts

            def gate_up(bi):
                nT = nTf[bi % 2]
                aT = actT[0]
                for fc in range(NFC):
                    a = pgt[cnt['gi'] % 2]
                    b = put[cnt['gi'] % 2]
                    s2 = sg[cnt['gi'] % 2]
                    cnt['gi'] += 1
                    for k in range(8):
                        p.op('pe', lambda e, a=a, k=k, fc=fc, nT=nT: e.matmul(a.t[:, :], Wg.t[:, k, fc * 128:(fc + 1) * 128], nT.t[:, k, :], start=(k == 0), stop=(k == 7)),
                             reads=[Wg.b, nT.b], writes=[a.b])
                    for k in range(8):
                        p.op('pe', lambda e, b=b, k=k, fc=fc, nT=nT: e.matmul(b.t[:, :], Wu.t[:, k, fc * 128:(fc + 1) * 128], nT.t[:, k, :], start=(k == 0), stop=(k == 7)),
                             reads=[Wu.b, nT.b], writes=[b.b])
                    p.op('act', lambda e, a=a, s2=s2: e.activation(s2.t[:, :], a.t[:, :], AF.Silu), reads=[a.b], writes=[s2.b])
                    p.op('dve', lambda e, b=b, s2=s2, aT=aT, fc=fc: e.tensor_mul(aT.t[:, fc, :], s2.t[:, :], b.t[:, :]), reads=[b.b, s2.b], writes=[aT.b])


            def down(bi, xts):
                tl = blks[bi]
                aT = actT[0]
                for tt, ti in enumerate(tl):
                    col = 0 if ti < 32 else 1
                    hxt = xts[tt]
                    h_ = ho[ti % 2]
                    for half in range(2):
                        po = pdn[cnt['di'] % 2]
                        tm = tm_[cnt['di'] % 2]
                        cnt['di'] += 1
                        for fc in range(NFC):
                            p.op('pe', lambda e, po=po, fc=fc, tt=tt, half=half, aT=aT: e.matmul(po.t[:, :], aT.t[:, fc, tt * 128:(tt + 1) * 128], Wd.t[:, fc, half * 512:(half + 1) * 512],
                                                                                         start=(fc == 0), stop=(fc == NFC - 1)), reads=[aT.b, Wd.b], writes=[po.b])
                        p.op('dve', lambda e, po=po, tm=tm, half=half, col=col: e.tensor_mul(tm.t[:, :], po.t[:, :], g2l[col].t[:, half * 512:(half + 1) * 512]),
                             reads=[po.b, g2l[col].b], writes=[tm.b])
                        p.op('pool', lambda e, tm=tm, hxt=hxt, h_=h_, half=half: e.tensor_add(h_.t[:, half * 512:(half + 1) * 512], hxt.t[:, half * 512:(half + 1) * 512], tm.t[:, :]),
                             reads=[tm.b, hxt.b], writes=[h_.b])
                    if not final:
                        p.dma(Hs[ti * 128:(ti + 1) * 128, :], h_.t[:], reads=[h_.b])
                    else:
                        ss = fss[ti % 2]
                        p.op('act', lambda e, h_=h_, ss=ss: e.activation(fj.t[:], h_.t[:], AF.Square, accum_out=ss.t[:]), reads=[h_.b], writes=[fj.b, ss.b])
                        p.op('dve', lambda e, ss=ss: e.tensor_scalar(ss.t[:], ss.t[:], 1.0 / D, EPS, ALU.mult, ALU.add), reads=[ss.b], writes=[ss.b])
                        p.op('act', lambda e, ss=ss: e.activation(ss.t[:], ss.t[:], AF.Sqrt), reads=[ss.b], writes=[ss.b])
                        p.op('dve', lambda e, ss=ss: e.reciprocal(ss.t[:], ss.t[:]), reads=[ss.b], writes=[ss.b])
                        p.op('dve', lambda e, h_=h_, ss=ss: e.scalar_tensor_tensor(h_.t[:], h_.t[:], ss.t[:, 0:1], fingt.t[:], ALU.mult, ALU.mult),
                             reads=[h_.b, ss.b, fingt.b], writes=[h_.b])
                        p.dma(out[ti * 128:(ti + 1) * 128, :], h_.t[:], reads=[h_.b])

            xts_next = do_norm(0)
            for bi in range(len(blks)):
                xts_cur = xts_next
                gate_up(bi)
                if bi + 1 < len(blks):
                    xts_next = do_norm(bi + 1)
                down(bi, xts_cur)
            p.barrier()

    ffn(0, list(range(34)), False)

    if stage == 2:
        with ExitStack() as st:
            bb_ = [sbt(st, f"bo{i}", [128, D], F32) for i in range(2)]
            for ti in range(32):
                b_ = bb_[ti % 2]
                p.dma(b_.t[:], Hs[ti * 128:(ti + 1) * 128, :], writes=[b_.b])
                p.dma(out[ti * 128:(ti + 1) * 128, :], b_.t[:], reads=[b_.b])
        p.emit()
        top.close()
        return nc

    s5_layer(nc, p, I, K, Hs, AB, load_gate, identf, identb, sbt, pst, ew, make_norm, norm_tile)
    if stage == 3:
        with ExitStack() as st:
            bb_ = [sbt(st, f"bo3{i}", [128, D], F32) for i in range(2)]
            for ti in range(32):
                b_ = bb_[ti % 2]
                p.dma(b_.t[:], Hs[ti * 128:(ti + 1) * 128, :], writes=[b_.b])
                p.dma(out[ti * 128:(ti + 1) * 128, :], b_.t[:], reads=[b_.b])
        p.emit()
        top.close()
        return nc
    ffn(1, list(range(16)), True)
    p.emit()
    top.close()
    return nc


def rev(ap):
    apl = [list(a) for a in ap.ap]
    n = apl[-1][1]
    stp = apl[-1][0]
    apl[-1][0] = -stp
    return bass.AP(ap.tensor, ap.offset + (n - 1) * stp, apl)


def bcast_last(ap, n):
    apl = [list(a) for a in ap.ap] + [[0, n]]
    return bass.AP(ap.tensor, ap.offset, apl)


def s5_layer(nc, p, I, K, Hs, AB, load_gate, identf, identb, sbt, pst, ew, make_norm, norm_tile, dbg=None):
    Hloc = Hs
    YGs = nc.dram_tensor("YGs", [8, 128, T // 2], BF16).ap()
    NTOK = T + TC
    with ExitStack() as S:
        nT = sbt(S, "s5nT", [128, 8, NTOK], BF16)
        with ExitStack() as st:
            cn = make_norm(st, "s5n", nxt=3)
            for ti in range(34):
                col = 0 if ti < 32 else 1
                norm_tile(cn, Hs[ti * 128:(ti + 1) * 128, :], 1, col, 0, lambda k, ti=ti: nT.t[:, k, ti * 128:(ti + 1) * 128], nT.b)
            p.barrier()
        PRM = sbt(S, "s5prm", [128, 3, 64], F32)
        Cw = sbt(S, "s5Cw", [128, 64, 2, 32], F32)
        Bz = sbt(S, "s5Bz", [128, 2, 64, 2, 16], F32)
        LC = 4
        NCM = 512 // LC
        PW = sbt(S, "s5PW", [128, LC + 1, 2, 64], F32)
        PRM4 = sbt(S, "s5prm4", [128, 3, 64], F32)
        dT = sbt(S, "s5dT", [128, 8], F32)
        with ExitStack() as st:
            pt = pst(st, "s5pt", [128, 128])
            def V(nm):
                return sbt(st, "s5v_" + nm, [128, 64], F32)
            arow = sbt(st, "s5arow", [64, 2, 128], F32)
            p.dma(arow.t[:, 0, :], I["ssm_a_re"].rearrange("(dq g) p -> dq (g p)", g=2), writes=[arow.b])
            p.dma(arow.t[:, 1, :], I["ssm_a_im"].rearrange("(dq g) p -> dq (g p)", g=2), writes=[arow.b])
            are, aim = V("are"), V("aim")
            for i_, dst in enumerate((are, aim)):
                p.op('pe', lambda e, i_=i_: e.transpose(pt.t[:, 0:64], arow.t[:, i_, :], identf.t[0:64, 0:64]), reads=[arow.b, identf.b], writes=[pt.b])
                p.op('dve', lambda e, dst=dst: e.tensor_copy(dst.t[:], pt.t[:, 0:64]), reads=[pt.b], writes=[dst.b])
            drow = sbt(st, "s5drow", [8, 128], F32)
            p.dma(drow.t[:], I["ssm_d"], writes=[drow.b])
            p.op('pe', lambda e: e.transpose(pt.t[:, 0:8], drow.t[:, :], identf.t[0:8, 0:8]), reads=[drow.b, identf.b], writes=[pt.b])
            p.op('dve', lambda e: e.tensor_copy(dT.t[:], pt.t[:, 0:8]), reads=[pt.b], writes=[dT.b])
            ldb = sbt(st, "s5ldb", [128, 128], F32)
            p.dma(ldb.t[:], I["ssm_log_dt"].partition_broadcast(128), writes=[ldb.b])
            dt = V("dt")
            for g2 in range(2):
                p.op('dve', lambda e, g2=g2: e.tensor_copy(dt.t[64 * g2:64 * g2 + 64, :], ldb.t[64 * g2:64 * g2 + 64, g2:128:2]), reads=[ldb.b], writes=[dt.b])
            p.op('act', lambda e: e.activation(dt.t[:], dt.t[:], AF.Exp), reads=[dt.b], writes=[dt.b])
            xr, th, mag = V("xr"), V("th"), V("mag")
            p.op('dve', lambda e: e.tensor_mul(xr.t[:], are.t[:], dt.t[:]), reads=[are.b, dt.b], writes=[xr.b])
            p.op('dve', lambda e: e.tensor_mul(th.t[:], aim.t[:], dt.t[:]), reads=[aim.b, dt.b], writes=[th.b])
            p.op('act', lambda e: e.activation(mag.t[:], xr.t[:], AF.Exp), reads=[xr.b], writes=[mag.b])
            kf = V("kf")
            ki = sbt(st, "s5ki", [128, 64], mybir.dt.int32)
            p.op('dve', lambda e: e.tensor_scalar(kf.t[:], th.t[:], 1.0 / (2 * math.pi), None, ALU.mult), reads=[th.b], writes=[kf.b])
            p.op('dve', lambda e: e.tensor_copy(ki.t[:], kf.t[:]), reads=[kf.b], writes=[ki.b])
            p.op('dve', lambda e: e.tensor_copy(kf.t[:], ki.t[:]), reads=[ki.b], writes=[kf.b])
            C1 = 6.28125
            C2 = 2 * math.pi - 6.28125
            thm = V("thm")
            p.op('dve', lambda e: e.scalar_tensor_tensor(thm.t[:], kf.t[:], -C1, th.t[:], ALU.mult, ALU.add), reads=[kf.b, th.b], writes=[thm.b])
            p.op('dve', lambda e: e.scalar_tensor_tensor(thm.t[:], kf.t[:], -C2, thm.t[:], ALU.mult, ALU.add), reads=[kf.b, thm.b], writes=[thm.b])
            xq, u2, qs, qc = V("xq"), V("u2"), V("qs"), V("qc")
            p.op('dve', lambda e: e.tensor_scalar(xq.t[:], thm.t[:], 0.25, None, ALU.mult), reads=[thm.b], writes=[xq.b])
            p.op('dve', lambda e: e.tensor_mul(u2.t[:], xq.t[:], xq.t[:]), reads=[xq.b], writes=[u2.b])
            sc_ = [(-1.0) ** k / math.factorial(2 * k + 1) for k in range(9)]
            cc_ = [(-1.0) ** k / math.factorial(2 * k) for k in range(9)]
            p.op('dve', lambda e: e.tensor_scalar(qs.t[:], u2.t[:], sc_[8], None, ALU.mult), reads=[u2.b], writes=[qs.b])
            p.op('dve', lambda e: e.tensor_scalar(qc.t[:], u2.t[:], cc_[8], None, ALU.mult), reads=[u2.b], writes=[qc.b])
            for k in range(7, 0, -1):
                p.op('dve', lambda e, k=k: e.scalar_tensor_tensor(qs.t[:], qs.t[:], sc_[k], u2.t[:], ALU.add, ALU.mult), reads=[qs.b, u2.b], writes=[qs.b])
                p.op('dve', lambda e, k=k: e.scalar_tensor_tensor(qc.t[:], qc.t[:], cc_[k], u2.t[:], ALU.add, ALU.mult), reads=[qc.b, u2.b], writes=[qc.b])
            sn, cs = V("sn"), V("cs")
            p.op('dve', lambda e: e.scalar_tensor_tensor(sn.t[:], qs.t[:], 1.0, xq.t[:], ALU.add, ALU.mult), reads=[qs.b, xq.b], writes=[sn.b])
            p.op('dve', lambda e: e.tensor_scalar(cs.t[:], qc.t[:], 1.0, None, ALU.add), reads=[qc.b], writes=[cs.b])
            ta, tb_ = V("ta"), V("tb")
            for _ in range(2):
                p.op('dve', lambda e: e.tensor_mul(ta.t[:], cs.t[:], cs.t[:]), reads=[cs.b], writes=[ta.b])
                p.op('dve', lambda e: e.tensor_mul(tb_.t[:], sn.t[:], sn.t[:]), reads=[sn.b], writes=[tb_.b])
                p.op('dve', lambda e: e.scalar_tensor_tensor(sn.t[:], cs.t[:], 2.0, sn.t[:], ALU.mult, ALU.mult), reads=[cs.b, sn.b], writes=[sn.b])
                p.op('dve', lambda e: e.tensor_sub(cs.t[:], ta.t[:], tb_.t[:]), reads=[ta.b, tb_.b], writes=[cs.b])
            p.op('dve', lambda e: e.tensor_copy(PRM.t[:, 0, :], mag.t[:]), reads=[mag.b], writes=[PRM.b])
            p.op('dve', lambda e: e.tensor_copy(PRM.t[:, 1, :], cs.t[:]), reads=[cs.b], writes=[PRM.b])
            p.op('dve', lambda e: e.tensor_copy(PRM.t[:, 2, :], sn.t[:]), reads=[sn.b], writes=[PRM.b])
            p.op('pool', lambda e: e.memset(PW.t[:, 0, 0, :], 1.0), writes=[PW.b])
            p.op('pool', lambda e: e.memset(PW.t[:, 0, 1, :], 0.0), writes=[PW.b])
            p.op('dve', lambda e: e.tensor_mul(PW.t[:, 1, 0, :], mag.t[:], cs.t[:]), reads=[mag.b, cs.b], writes=[PW.b])
            p.op('dve', lambda e: e.tensor_mul(PW.t[:, 1, 1, :], mag.t[:], sn.t[:]), reads=[mag.b, sn.b], writes=[PW.b])
            for k in range(2, LC + 1):
                p.op('dve', lambda e, k=k: e.tensor_mul(ta.t[:], PW.t[:, k - 1, 0, :], PW.t[:, 1, 0, :]), reads=[PW.b], writes=[ta.b])
                p.op('dve', lambda e, k=k: e.tensor_mul(tb_.t[:], PW.t[:, k - 1, 1, :], PW.t[:, 1, 1, :]), reads=[PW.b], writes=[tb_.b])
                p.op('dve', lambda e, k=k: e.tensor_sub(PW.t[:, k, 0, :], ta.t[:], tb_.t[:]), reads=[ta.b, tb_.b], writes=[PW.b])
                p.op('dve', lambda e, k=k: e.tensor_mul(ta.t[:], PW.t[:, k - 1, 0, :], PW.t[:, 1, 1, :]), reads=[PW.b], writes=[ta.b])
                p.op('dve', lambda e, k=k: e.tensor_mul(tb_.t[:], PW.t[:, k - 1, 1, :], PW.t[:, 1, 0, :]), reads=[PW.b], writes=[tb_.b])
                p.op('dve', lambda e, k=k: e.tensor_add(PW.t[:, k, 1, :], ta.t[:], tb_.t[:]), reads=[ta.b, tb_.b], writes=[PW.b])
            c4, s4, r4 = V("c4"), V("s4"), V("r4")
            p.op('dve', lambda e: e.tensor_copy(c4.t[:], cs.t[:]), reads=[cs.b], writes=[c4.b])
            p.op('dve', lambda e: e.tensor_copy(s4.t[:], sn.t[:]), reads=[sn.b], writes=[s4.b])
            p.op('dve', lambda e: e.tensor_copy(r4.t[:], mag.t[:]), reads=[mag.b], writes=[r4.b])
            for _ in range(int(round(math.log2(LC)))):
                p.op('dve', lambda e: e.tensor_mul(ta.t[:], c4.t[:], c4.t[:]), reads=[c4.b], writes=[ta.b])
                p.op('dve', lambda e: e.tensor_mul(tb_.t[:], s4.t[:], s4.t[:]), reads=[s4.b], writes=[tb_.b])
                p.op('dve', lambda e: e.scalar_tensor_tensor(s4.t[:], c4.t[:], 2.0, s4.t[:], ALU.mult, ALU.mult), reads=[c4.b, s4.b], writes=[s4.b])
                p.op('dve', lambda e: e.tensor_sub(c4.t[:], ta.t[:], tb_.t[:]), reads=[ta.b, tb_.b], writes=[c4.b])
                p.op('dve', lambda e: e.tensor_mul(r4.t[:], r4.t[:], r4.t[:]), reads=[r4.b], writes=[r4.b])
            p.op('dve', lambda e: e.tensor_copy(PRM4.t[:, 0, :], r4.t[:]), reads=[r4.b], writes=[PRM4.b])
            p.op('dve', lambda e: e.tensor_copy(PRM4.t[:, 1, :], c4.t[:]), reads=[c4.b], writes=[PRM4.b])
            p.op('dve', lambda e: e.tensor_copy(PRM4.t[:, 2, :], s4.t[:]), reads=[s4.b], writes=[PRM4.b])
            nr, ni, den, fre, fim = V("nr"), V("ni"), V("den"), V("fre"), V("fim")
            p.op('dve', lambda e: e.tensor_mul(nr.t[:], mag.t[:], cs.t[:]), reads=[mag.b, cs.b], writes=[nr.b])
            p.op('dve', lambda e: e.tensor_scalar(nr.t[:], nr.t[:], -1.0, None, ALU.add), reads=[nr.b], writes=[nr.b])
            p.op('dve', lambda e: e.tensor_mul(ni.t[:], mag.t[:], sn.t[:]), reads=[mag.b, sn.b], writes=[ni.b])
            p.op('dve', lambda e: e.tensor_mul(den.t[:], are.t[:], are.t[:]), reads=[are.b], writes=[den.b])
            p.op('dve', lambda e: e.tensor_mul(ta.t[:], aim.t[:], aim.t[:]), reads=[aim.b], writes=[ta.b])
            p.op('dve', lambda e: e.tensor_add(den.t[:], den.t[:], ta.t[:]), reads=[den.b, ta.b], writes=[den.b])
            p.op('dve', lambda e: e.reciprocal(den.t[:], den.t[:]), reads=[den.b], writes=[den.b])
            p.op('dve', lambda e: e.tensor_mul(ta.t[:], nr.t[:], are.t[:]), reads=[nr.b, are.b], writes=[ta.b])
            p.op('dve', lambda e: e.tensor_mul(tb_.t[:], ni.t[:], aim.t[:]), reads=[ni.b, aim.b], writes=[tb_.b])
            p.op('dve', lambda e: e.tensor_add(fre.t[:], ta.t[:], tb_.t[:]), reads=[ta.b, tb_.b], writes=[fre.b])
            p.op('dve', lambda e: e.tensor_mul(fre.t[:], fre.t[:], den.t[:]), reads=[fre.b, den.b], writes=[fre.b])
            p.op('dve', lambda e: e.tensor_mul(ta.t[:], ni.t[:], are.t[:]), reads=[ni.b, are.b], writes=[ta.b])
            p.op('dve', lambda e: e.tensor_mul(tb_.t[:], nr.t[:], aim.t[:]), reads=[nr.b, aim.b], writes=[tb_.b])
            p.op('dve', lambda e: e.tensor_sub(fim.t[:], ta.t[:], tb_.t[:]), reads=[ta.b, tb_.b], writes=[fim.b])
            p.op('dve', lambda e: e.tensor_mul(fim.t[:], fim.t[:], den.t[:]), reads=[fim.b, den.b], writes=[fim.b])
            Br = [sbt(st, f"s5Br{i}", [128, 64, 16], F32) for i in range(2)]
            for i_, nm in enumerate(("ssm_b_re", "ssm_b_im")):
                src = I[nm].rearrange("d (q g) p h -> g p (d q) h", g=2)
                for g2 in range(2):
                    p.dma(Br[i_].t[64 * g2:64 * g2 + 64, :, :], src[g2], writes=[Br[i_].b])
            p.op('pool', lambda e: e.memset(Bz.t[:].rearrange("p a b c d -> p (a b c d)"), 0.0), writes=[Bz.b])
            m1 = sbt(st, "s5m1", [128, 64, 16], F32)
            m2 = sbt(st, "s5m2", [128, 64, 16], F32)
            fre_b = bcast_last(fre.t[:], 16)
            fim_b = bcast_last(fim.t[:], 16)
            p.op('dve', lambda e: e.tensor_mul(m1.t[:], Br[0].t[:], fre_b), reads=[Br[0].b, fre.b], writes=[m1.b])
            p.op('dve', lambda e: e.tensor_mul(m2.t[:], Br[1].t[:], fim_b), reads=[Br[1].b, fim.b], writes=[m2.b])
            for g2 in range(2):
                p.op('dve', lambda e, g2=g2: e.tensor_sub(Bz.t[64 * g2:64 * g2 + 64, 0, :, g2, :], m1.t[64 * g2:64 * g2 + 64], m2.t[64 * g2:64 * g2 + 64]),
                     reads=[m1.b, m2.b], writes=[Bz.b])
            p.op('dve', lambda e: e.tensor_mul(m1.t[:], Br[1].t[:], fre_b), reads=[Br[1].b, fre.b], writes=[m1.b])
            p.op('dve', lambda e: e.tensor_mul(m2.t[:], Br[0].t[:], fim_b), reads=[Br[0].b, fim.b], writes=[m2.b])
            for g2 in range(2):
                p.op('dve', lambda e, g2=g2: e.tensor_add(Bz.t[64 * g2:64 * g2 + 64, 1, :, g2, :], m1.t[64 * g2:64 * g2 + 64], m2.t[64 * g2:64 * g2 + 64]),
                     reads=[m1.b, m2.b], writes=[Bz.b])
            pts = [pst(st, f"s5ptb{i}", [128, 128]) for i in range(2)]
            n_ = 0
            p.op('pool', lambda e: e.memset(Cw.t[:].rearrange("p a b c -> p (a b c)"), 0.0), writes=[Cw.b])
            Cn = [sbt(st, f"s5Cn{i}", [128, 2, 64], F32) for i in range(2)]
            for ri, nm in enumerate(("ssm_c_re", "ssm_c_im")):
                for dQ in range(16):
                    d_, Q_ = dQ // 8, dQ % 8
                    cnb = Cn[n_ % 2]
                    pp = pts[n_ % 2]
                    n_ += 1
                    src = I[nm][d_, Q_ * 4 * 2:(Q_ * 4 + 4) * 2].rearrange("g h p -> (g h) p")
                    p.dma(cnb.t[:, 0, :], src, writes=[cnb.b])
                    p.dma(cnb.t[:, 1, :], src, writes=[cnb.b])
                    p.op('pe', lambda e, pp=pp, cnb=cnb: e.transpose(pp.t[:, :], cnb.t[:, :, :].rearrange("p a b -> p (a b)"), identf.t[:, :]),
                         reads=[cnb.b, identf.b], writes=[pp.b])
                    sgn = 1.0 if ri == 0 else -1.0
                    for g2 in range(2):
                        p.op('act', lambda e, pp=pp, g2=g2, ri=ri, dQ=dQ, sgn=sgn: e.activation(
                            Cw.t[64 * g2:64 * g2 + 64, dQ * 4:(dQ + 1) * 4, ri, 16 * g2:16 * g2 + 16],
                            pp.t[64 * g2:64 * g2 + 64, :].rearrange("p (q g h) -> p q g h", q=4, g=2)[:, :, g2, :], AF.Identity, scale=sgn),
                            reads=[pp.b], writes=[Cw.b])
            p.barrier()

        if dbg is not None:
            dbg(PRM, Cw, nT)
            return

        with ExitStack() as st:
            yacc = sbt(st, "s5yacc", [128, T // 2], F32)
            Ec = sbt(st, "s5Ec", [128, 8, NCM], F32)
            Es = sbt(st, "s5Es", [128, 8, NCM], F32)
            wc = sbt(st, "s5wc", [128, 8], F32)
            ws = sbt(st, "s5ws", [128, 8], F32)
            wt1 = sbt(st, "s5wt1", [128, 8], F32)
            wt2 = sbt(st, "s5wt2", [128, 8], F32)
            et1 = sbt(st, "s5et1", [128, 8, NCM // 2], F32)
            et2 = sbt(st, "s5et2", [128, 8, NCM // 2], F32)
            hp = sbt(st, "s5hp", [128, 8, 2], F32)
            ini = sbt(st, "s5ini", [128, 8, 2], F32)
            itmp = sbt(st, "s5itmp", [128, 8, 2], F32)
            nsth = sbt(st, "s5nsth", [128, 64], F32)
            p.op('dve', lambda e: e.tensor_scalar(nsth.t[:], PRM4.t[:, 2, :], -1.0, None, ALU.mult), reads=[PRM4.b], writes=[nsth.b])
            hpb = [Buf() for _ in range(8)]
            CwP = sbt(st, "s5CwP", [128, LC + 1, 8, 2, 32], F32)
            ctm = [sbt(st, f"s5ctm{i}", [128, 4, 32], F32) for i in range(2)]
            BP = sbt(st, "s5BP", [128, LC, 2, 2, 4, 32], F32)
            Win = sbt(st, "s5Win", [128, 2, 2, LC, 128], BF16)
            Mw = sbt(st, "s5Mw", [128, 2, LC, 32], BF16)
            Wout = sbt(st, "s5Wout", [128, 8, LC, 2, 32], BF16)
            pts = [pst(st, f"s5ptq{i}", [128, 128]) for i in range(1)]
            pk = pst(st, "s5pk", [128, 16, 32])
            NS = 8
            Wk_ = [[sbt(st, f"s5w{j}_{i}", [128, 2, NCM], F32) for i in range(3)] for j in range(NS)]
            Hb = [sbt(st, f"s5hb{j}", [128, 2, NCM + 4], BF16) for j in range(NS)]
            Esg = sbt(st, "s5Esg", [128, 8, 2, NCM], F32)

            def swap2(tb_, ncn):
                b1 = tb_.t[:, 1, 0:ncn]
                apl = [list(a_) for a_ in b1.ap]
                return bass.AP(b1.tensor, b1.offset, [apl[0], [-NCM, 2], apl[-1]])

            def bc2(ap2):
                apl = [list(a_) for a_ in ap2.ap]
                return bass.AP(ap2.tensor, ap2.offset, [apl[0], [0, 2], apl[-1]])
            pS = [pst(st, f"s5pS{j}", [128, 512]) for j in range(4)]
            py = [pst(st, f"s5py{i}", [128, LC, NCM]) for i in range(2)]
            ygb = [sbt(st, f"s5yg{i}", [128, 512], BF16) for i in range(2)]
            gtb = [sbt(st, f"s5gt{i}", [128, 512], F32) for i in range(2)]
            tn_ = 0
            blkc = 0
            yi_ = 0
            psi = 0
            for Q in range(8):
                for d_ in range(2):
                    us = slice(d_ * 32 + Q * 4, d_ * 32 + Q * 4 + 4)
                    ts_ = slice(d_ * 4, d_ * 4 + 4)
                    c0_ = Cw.t[:, us, 0, :]
                    c1_ = Cw.t[:, us, 1, :]
                    for k in range(LC + 1):
                        pr = bcast_last(PW.t[:, k, 0, us], 32)
                        pi_ = bcast_last(PW.t[:, k, 1, us], 32)
                        t1_, t2_ = ctm
                        p.op('dve', lambda e, t1_=t1_, c0_=c0_, pr=pr: e.tensor_mul(t1_.t[:], c0_, pr), reads=[Cw.b, PW.b], writes=[t1_.b])
                        p.op('pool', lambda e, t2_=t2_, c1_=c1_, pi_=pi_: e.tensor_mul(t2_.t[:], c1_, pi_), reads=[Cw.b, PW.b], writes=[t2_.b])
                        p.op('dve', lambda e, t1_=t1_, t2_=t2_, k=k, ts_=ts_: e.tensor_add(CwP.t[:, k, ts_, 0, :], t1_.t[:], t2_.t[:]), reads=[t1_.b, t2_.b], writes=[CwP.b])
                        p.op('dve', lambda e, t1_=t1_, c1_=c1_, pr=pr: e.tensor_mul(t1_.t[:], c1_, pr), reads=[Cw.b, PW.b], writes=[t1_.b])
                        p.op('pool', lambda e, t2_=t2_, c0_=c0_, pi_=pi_: e.tensor_mul(t2_.t[:], c0_, pi_), reads=[Cw.b, PW.b], writes=[t2_.b])
                        p.op('dve', lambda e, t1_=t1_, t2_=t2_, k=k, ts_=ts_: e.tensor_sub(CwP.t[:, k, ts_, 1, :], t1_.t[:], t2_.t[:]), reads=[t1_.b, t2_.b], writes=[CwP.b])
                    b0_ = Bz.t[:, 0, us, :, :].rearrange("p a b c -> p a (b c)")
                    b1_ = Bz.t[:, 1, us, :, :].rearrange("p a b c -> p a (b c)")
                    for s_ in range(LC):
                        k = LC - 1 - s_
                        pr = bcast_last(PW.t[:, k, 0, us], 32)
                        pi_ = bcast_last(PW.t[:, k, 1, us], 32)
                        t1_, t2_ = ctm
                        p.op('dve', lambda e, t1_=t1_, b0_=b0_, pr=pr: e.tensor_mul(t1_.t[:], b0_, pr), reads=[Bz.b, PW.b], writes=[t1_.b])
                        p.op('pool', lambda e, t2_=t2_, b1_=b1_, pi_=pi_: e.tensor_mul(t2_.t[:], b1_, pi_), reads=[Bz.b, PW.b], writes=[t2_.b])
                        p.op('dve', lambda e, t1_=t1_, t2_=t2_, s_=s_, d_=d_: e.tensor_sub(BP.t[:, s_, 0, d_, :, :], t1_.t[:], t2_.t[:]), reads=[t1_.b, t2_.b], writes=[BP.b])
                        p.op('dve', lambda e, t1_=t1_, b0_=b0_, pi_=pi_: e.tensor_mul(t1_.t[:], b0_, pi_), reads=[Bz.b, PW.b], writes=[t1_.b])
                        p.op('pool', lambda e, t2_=t2_, b1_=b1_, pr=pr: e.tensor_mul(t2_.t[:], b1_, pr), reads=[Bz.b, PW.b], writes=[t2_.b])
                        p.op('dve', lambda e, t1_=t1_, t2_=t2_, s_=s_, d_=d_: e.tensor_add(BP.t[:, s_, 1, d_, :, :], t1_.t[:], t2_.t[:]), reads=[t1_.b, t2_.b], writes=[BP.b])
                p.op('act', lambda e: e.activation(Wout.t[:].rearrange("p x t r h -> p t x r h"), CwP.t[:, 1:LC + 1, :, :, :], AF.Identity), reads=[CwP.b], writes=[Wout.b])
                for d_ in range(2):
                    for ri in range(2):
                        for s_ in range(LC):
                            pp = pts[0]
                            tn_ += 1
                            p.op('pe', lambda e, pp=pp, s_=s_, ri=ri, d_=d_: e.transpose(pp.t[:, :], BP.t[:, s_, ri, d_, :, :].rearrange("p a b -> p (a b)"), identf.t[:, :]),
                                 reads=[BP.b, identf.b], writes=[pp.b])
                            p.op('act', lambda e, pp=pp, s_=s_, ri=ri, d_=d_: e.activation(Win.t[:, d_, ri, s_, :], pp.t[:, :], AF.Identity), reads=[pp.b], writes=[Win.b])
                    for thf in range(LC // 4):
                        for tq in range(4):
                            tau = thf * 4 + tq
                            for ql in range(4):
                                tix = d_ * 4 + ql
                                for ri in range(2):
                                    p.op('pe', lambda e, d_=d_, tau=tau, tq=tq, ql=ql, tix=tix, ri=ri, us=slice(d_ * 32 + Q * 4, d_ * 32 + Q * 4 + 4): e.matmul(
                                        pk.t[:, tq * 4 + ql, :], Bz.t[:, ri, us, :, :].rearrange("p q a b -> p (q a b)"), CwP.t[:, tau, tix, ri, :],
                                        start=(ri == 0), stop=(ri == 1)), reads=[Bz.b, CwP.b], writes=[pk.b])
                        for ql in range(4):
                            p.op('dve', lambda e, d_=d_, ql=ql, thf=thf: e.tensor_copy(Mw.t[32 * ql:32 * ql + 32, d_, thf * 4:thf * 4 + 4, :], pk.t[32 * ql:32 * ql + 32, ql:16:4, :]),
                                 reads=[pk.b], writes=[Mw.b])
                for d_ in range(2):
                    cols = slice(d_ * 32 + Q * 4, d_ * 32 + Q * 4 + 4)
                    p.op('dve', lambda e, d_=d_, cols=cols: e.tensor_copy(wc.t[:, d_ * 4:d_ * 4 + 4], PRM4.t[:, 1, cols]), reads=[PRM4.b], writes=[wc.b])
                    p.op('dve', lambda e, d_=d_, cols=cols: e.tensor_copy(ws.t[:, d_ * 4:d_ * 4 + 4], PRM4.t[:, 2, cols]), reads=[PRM4.b], writes=[ws.b])
                p.op('pool', lambda e: e.memset(Ec.t[:, :, 0:1], 1.0), writes=[Ec.b])
                p.op('pool', lambda e: e.memset(Es.t[:, :, 0:1], 0.0), writes=[Es.b])
                n = 1
                while n < NCM:
                    wcb = bcast_last(wc.t[:, :], n)
                    wsb = bcast_last(ws.t[:, :], n)
                    p.op('dve', lambda e, n=n, wcb=wcb: e.tensor_mul(et1.t[:, :, 0:n], Ec.t[:, :, 0:n], wcb), reads=[Ec.b, wc.b], writes=[et1.b])
                    p.op('pool', lambda e, n=n, wsb=wsb: e.tensor_mul(et2.t[:, :, 0:n], Es.t[:, :, 0:n], wsb), reads=[Es.b, ws.b], writes=[et2.b])
                    p.op('dve', lambda e, n=n: e.tensor_sub(Ec.t[:, :, n:2 * n], et1.t[:, :, 0:n], et2.t[:, :, 0:n]), reads=[et1.b, et2.b], writes=[Ec.b])
                    p.op('dve', lambda e, n=n, wcb=wcb: e.tensor_mul(et1.t[:, :, 0:n], Es.t[:, :, 0:n], wcb), reads=[Es.b, wc.b], writes=[et1.b])
                    p.op('pool', lambda e, n=n, wsb=wsb: e.tensor_mul(et2.t[:, :, 0:n], Ec.t[:, :, 0:n], wsb), reads=[Ec.b, ws.b], writes=[et2.b])
                    p.op('dve', lambda e, n=n: e.tensor_add(Es.t[:, :, n:2 * n], et1.t[:, :, 0:n], et2.t[:, :, 0:n]), reads=[et1.b, et2.b], writes=[Es.b])
                    p.op('dve', lambda e: e.tensor_mul(wt1.t[:], wc.t[:], wc.t[:]), reads=[wc.b], writes=[wt1.b])
                    p.op('dve', lambda e: e.tensor_mul(wt2.t[:], ws.t[:], ws.t[:]), reads=[ws.b], writes=[wt2.b])
                    p.op('dve', lambda e: e.scalar_tensor_tensor(ws.t[:], wc.t[:], 2.0, ws.t[:], ALU.mult, ALU.mult), reads=[wc.b, ws.b], writes=[ws.b])
                    p.op('dve', lambda e: e.tensor_sub(wc.t[:], wt1.t[:], wt2.t[:]), reads=[wt1.b, wt2.b], writes=[wc.b])
                    n *= 2
                p.op('dve', lambda e: e.tensor_copy(Esg.t[:, :, 0, :], Es.t[:, :, :]), reads=[Es.b], writes=[Esg.b])
                p.op('dve', lambda e: e.tensor_scalar(Esg.t[:, :, 1, :], Es.t[:, :, :], -1.0, None, ALU.mult), reads=[Es.b], writes=[Esg.b])
                blist = []
                for d_ in range(2):
                    if d_ == 0:
                        blocks = [(T, TC, False)] + [(b * 512, 512, True) for b in range(4)]
                    else:
                        blocks = [(T, TC, False)] + [(b * 512, 512, False) for b in (7, 6, 5, 4)] + [(b * 512, 512, True) for b in (3, 2, 1, 0)]
                    for bi_, (c0, n, islat) in enumerate(blocks):
                        blist.append((d_, bi_, c0, n, islat))
                bctx = {}

                def front_pe(k, Q=Q):
                    d_, bi_, c0, n, islat = blist[k]
                    gk = Q * 18 + k
                    ncn = n // LC
                    sets = [(gk % 2) * 4 + ql for ql in range(4)]
                    rhs_all = []
                    for ql in range(4):
                        base = nT.t[32 * ql:32 * ql + 32, Q, c0:c0 + n]
                        apl = [list(a_) for a_ in base.ap]
                        lst = []
                        for s_ in range(LC):
                            ap2 = [list(a_) for a_ in apl]
                            if d_ == 0:
                                ap2[-1] = [apl[-1][0] * LC, ncn]
                                lst.append(bass.AP(base.tensor, base.offset + s_ * apl[-1][0], ap2))
                            else:
                                ap2[-1] = [-apl[-1][0] * LC, ncn]
                                lst.append(bass.AP(base.tensor, base.offset + (n - 1 - s_) * apl[-1][0], ap2))
                        rhs_all.append(lst)
                    pss = [pS[ql] for ql in range(4)]
                    for ri in range(2):
                        for s_ in range(LC):
                            for ql in range(4):
                                ps_ = pss[ql]
                                p.op('pe', lambda e, ps_=ps_, ri=ri, s_=s_, ql=ql, d_=d_, ncn=ncn, r_=rhs_all[ql][s_]: e.matmul(
                                    ps_.t[:, ri * 128:ri * 128 + ncn], Win.t[32 * ql:32 * ql + 32, d_, ri, s_, :], r_, start=(s_ == 0), stop=(s_ == LC - 1), tile_position=(32 * ql, 0)),
                                    reads=[Win.b, nT.b], writes=[ps_.b])
                    bctx[k] = (ncn, sets, rhs_all)

                def front_ev(k, Q=Q):
                    d_, bi_, c0, n, islat = blist[k]
                    ncn, sets, rhs_all = bctx[k]
                    pss = [pS[ql] for ql in range(4)]
                    for ql in range(4):
                        ps_ = pss[ql]
                        X, P1, P2 = Wk_[sets[ql]]
                        p.op('act', lambda e, X=X, ps_=ps_, ncn=ncn: e.activation(X.t[:, :, 0:ncn], ps_.t[:, 0:256].rearrange("p (r c) -> p r c", r=2)[:, :, 0:ncn], AF.Identity),
                             reads=[ps_.b], writes=[X.b])
                    for ql in range(4):
                        X, P1, P2 = Wk_[sets[ql]]
                        tix = d_ * 4 + ql
                        ecb = bc2(Ec.t[:, tix, 0:ncn])
                        p.op('dve', lambda e, X=X, P1=P1, ecb=ecb, ncn=ncn: e.tensor_mul(P1.t[:, :, 0:ncn], X.t[:, :, 0:ncn], ecb), reads=[X.b, Ec.b], writes=[P1.b])
                        p.op('pool', lambda e, X=X, P2=P2, tix=tix, ncn=ncn: e.tensor_mul(P2.t[:, :, 0:ncn], swap2(X, ncn), Esg.t[:, tix, :, 0:ncn]), reads=[X.b, Esg.b], writes=[P2.b])
                    for ql in range(4):
                        X, P1, P2 = Wk_[sets[ql]]
                        p.op('dve', lambda e, P1=P1, P2=P2, ncn=ncn: e.tensor_add(P1.t[:, :, 0:ncn], P1.t[:, :, 0:ncn], P2.t[:, :, 0:ncn]), reads=[P1.b, P2.b], writes=[P1.b])

                def back(k, Q=Q):
                    d_, bi_, c0, n, islat = blist[k]
                    gk = Q * 18 + k
                    ncn, sets, rhs_all = bctx.pop(k)
                    pyt = py[gk % 2]
                    for ql in range(4):
                        tix = d_ * 4 + ql
                        u = d_ * 32 + Q * 4 + ql
                        hb_ = hpb[tix]
                        hbs = Hb[sets[ql]]
                        if bi_ > 0:
                            p.op('act', lambda e, tix=tix, u=u: e.activation(itmp.t[:, tix, 0:1], hp.t[:, tix, 1:2], AF.Identity, scale=nsth.t[:, u:u + 1]), reads=[hb_, nsth.b], writes=[hb_])
                            p.op('act', lambda e, tix=tix, u=u: e.activation(ini.t[:, tix, 0:1], hp.t[:, tix, 0:1], AF.Identity, scale=PRM4.t[:, 1, u:u + 1], bias=itmp.t[:, tix, 0:1]),
                                 reads=[hb_, PRM4.b], writes=[hb_])
                            p.op('act', lambda e, tix=tix, u=u: e.activation(itmp.t[:, tix, 1:2], hp.t[:, tix, 0:1], AF.Identity, scale=PRM4.t[:, 2, u:u + 1]), reads=[hb_, PRM4.b], writes=[hb_])
                            p.op('act', lambda e, tix=tix, u=u: e.activation(ini.t[:, tix, 1:2], hp.t[:, tix, 1:2], AF.Identity, scale=PRM4.t[:, 1, u:u + 1], bias=itmp.t[:, tix, 1:2]),
                                 reads=[hb_, PRM4.b], writes=[hb_])
                            if islat:
                                p.op('act', lambda e, hbs=hbs, tix=tix: e.activation(hbs.t[:, :, 0], hp.t[:, tix, :], AF.Identity), reads=[hb_], writes=[hbs.b])
                    for ql in range(4):
                        X, P1, P2 = Wk_[sets[ql]]
                        tix = d_ * 4 + ql
                        u = d_ * 32 + Q * 4 + ql
                        hb_ = hpb[tix]
                        rb = PRM4.t[:, 0, u:u + 1].to_broadcast([128, ncn])
                        for ri in range(2):
                            if bi_ == 0:
                                init_, irds = 0.0, []
                            else:
                                init_, irds = ini.t[:, tix, ri:ri + 1], [hb_]
                            p.op('dve', lambda e, X=X, P1=P1, rb=rb, init_=init_, ri=ri, ncn=ncn: e.tensor_tensor_scan(X.t[:, ri, 0:ncn], rb, P1.t[:, ri, 0:ncn], init_, ALU.mult, ALU.add),
                                 reads=[P1.b, PRM4.b] + irds, writes=[X.b])
                    for ql in range(4):
                        X, P1, P2 = Wk_[sets[ql]]
                        tix = d_ * 4 + ql
                        ecb = bc2(Ec.t[:, tix, 0:ncn])
                        p.op('dve', lambda e, X=X, P1=P1, ecb=ecb, ncn=ncn: e.tensor_mul(P1.t[:, :, 0:ncn], X.t[:, :, 0:ncn], ecb), reads=[X.b, Ec.b], writes=[P1.b])
                        p.op('pool', lambda e, X=X, P2=P2, tix=tix, ncn=ncn: e.tensor_mul(P2.t[:, :, 0:ncn], swap2(X, ncn), Esg.t[:, tix, :, 0:ncn]), reads=[X.b, Esg.b], writes=[P2.b])
                    for ql in range(4):
                        X, P1, P2 = Wk_[sets[ql]]
                        p.op('dve', lambda e, P1=P1, P2=P2, ncn=ncn: e.tensor_sub(P1.t[:, :, 0:ncn], P1.t[:, :, 0:ncn], P2.t[:, :, 0:ncn]), reads=[P1.b, P2.b], writes=[P1.b])
                    for ql in range(4):
                        X, P1, P2 = Wk_[sets[ql]]
                        tix = d_ * 4 + ql
                        hb_ = hpb[tix]
                        hbs = Hb[sets[ql]]
                        p.op('act', lambda e, tix=tix, P1=P1, ncn=ncn: e.activation(hp.t[:, tix, :], P1.t[:, :, ncn - 1], AF.Identity), reads=[P1.b], writes=[hb_])
                        if islat:
                            p.op('act', lambda e, hbs=hbs, P1=P1, ncn=ncn: e.activation(hbs.t[:, :, 1:ncn], P1.t[:, :, 0:ncn - 1], AF.Identity), reads=[P1.b], writes=[hbs.b])
                    if islat:
                        for t_ in range(LC):
                            for s_ in range(t_ + 1):
                                for ql in range(4):
                                    p.op('pe', lambda e, pyt=pyt, ql=ql, t_=t_, s_=s_, d_=d_, ncn=ncn, r_=rhs_all[ql][s_]: e.matmul(
                                        pyt.t[32 * ql:32 * ql + 32, t_, 0:ncn], Mw.t[32 * ql:32 * ql + 32, d_, t_ - s_, :], r_, start=(s_ == 0 and t_ == 0), stop=False,
                                        tile_position=(32 * ql, 32 * ql)), reads=[Mw.b, nT.b], writes=[pyt.b])
                        for t_ in range(LC):
                            for ri in range(2):
                                for ql in range(4):
                                    tix = d_ * 4 + ql
                                    hh_ = Hb[sets[ql]]
                                    p.op('pe', lambda e, pyt=pyt, ql=ql, t_=t_, ri=ri, tix=tix, hh_=hh_, ncn=ncn: e.matmul(
                                        pyt.t[32 * ql:32 * ql + 32, t_, 0:ncn], Wout.t[:, tix, t_, ri, :], hh_.t[:, ri, 0:ncn], start=False, stop=(ri == 1 and t_ == LC - 1),
                                        tile_position=(0, 32 * ql)), reads=[Wout.b, hh_.b], writes=[pyt.b])

                def yevac(k, Q=Q):
                    d_, bi_, c0, n, islat = blist[k]
                    if not islat:
                        return
                    gk = Q * 18 + k
                    ncn = n // LC
                    pyt = py[gk % 2]
                    if d_ == 0:
                        yv = yacc.t[:, c0:c0 + n].rearrange("p (c t) -> p t c", t=LC)
                        nv = nT.t[:, Q, c0:c0 + n].rearrange("p (c t) -> p t c", t=LC)
                        p.op('dve', lambda e, pyt=pyt, yv=yv, nv=nv, Q=Q: e.scalar_tensor_tensor(yv, nv, dT.t[:, Q:Q + 1], pyt.t[:, :, :], ALU.mult, ALU.add),
                             reads=[nT.b, dT.b, pyt.b], writes=[yacc.b])
                    else:
                        base = yacc.t[:, c0:c0 + n]
                        apl = [list(a_) for a_ in base.ap]
                        stp = apl[-1][0]
                        yv = bass.AP(base.tensor, base.offset + (n - 1) * stp, apl[:-1] + [[-stp, LC], [-stp * LC, ncn]])
                        p.op('dve', lambda e, pyt=pyt, yv=yv: e.tensor_add(yv, yv, pyt.t[:, :, :]), reads=[pyt.b, yacc.b], writes=[yacc.b])

                NB_ = len(blist)
                front_pe(0)
                front_ev(0)
                if NB_ > 1:
                    front_pe(1)
                for k in range(NB_):
                    if k + 1 < NB_:
                        front_ev(k + 1)
                    if k + 2 < NB_:
                        front_pe(k + 2)
                    back(k)
                    if k >= 1:
                        yevac(k - 1)
                yevac(NB_ - 1)
                for b in range(4):
                    yg = ygb[b % 2]
                    gt = gtb[b % 2]
                    ysl = yacc.t[:, b * 512:(b + 1) * 512]
                    p.op('dve', lambda e, gt=gt, ysl=ysl: e.tensor_mul(gt.t[:, :], ysl, ysl), reads=[yacc.b], writes=[gt.b])
                    p.op('dve', lambda e, gt=gt: e.tensor_scalar(gt.t[:, :], gt.t[:, :], 0.044715, 1.0, ALU.mult, ALU.add), reads=[gt.b], writes=[gt.b])
                    p.op('dve', lambda e, gt=gt, ysl=ysl: e.tensor_mul(gt.t[:, :], gt.t[:, :], ysl), reads=[gt.b, yacc.b], writes=[gt.b])
                    p.op('act', lambda e, gt=gt: e.activation(gt.t[:, :], gt.t[:, :], AF.Sigmoid, scale=2.0 * math.sqrt(2.0 / math.pi)), reads=[gt.b], writes=[gt.b])
                    p.op('dve', lambda e, gt=gt, ysl=ysl, yg=yg: e.tensor_mul(yg.t[:, :], gt.t[:, :], ysl), reads=[gt.b, yacc.b], writes=[yg.b])
                    p.dma(YGs[Q, :, b * 512:(b + 1) * 512], yg.t[:, :], reads=[yg.b])
            p.barrier()
    with ExitStack() as st:
        Wgl = sbt(st, "s5Wgl", [128, 8, 2 * D], BF16)
        with ExitStack() as st2:
            stg = [sbt(st2, f"s5gst{i}", [128, 2 * D], F32) for i in range(2)]
            for k in range(8):
                s_ = stg[k % 2]
                p.dma(s_.t[:, 0:D], I["ssm_glu_w"][k * 128:(k + 1) * 128, 0:D], writes=[s_.b])
                p.dma(s_.t[:, D:2 * D], I["ssm_glu_w"][k * 128:(k + 1) * 128, D:2 * D], writes=[s_.b])
                ew(lambda e, k=k, s_=s_: e.tensor_copy(Wgl.t[:, k, 0:D], s_.t[:, 0:D]), [s_.b], [Wgl.b])
                ew(lambda e, k=k, s_=s_: e.tensor_copy(Wgl.t[:, k, D:2 * D], s_.t[:, D:2 * D]), [s_.b], [Wgl.b])
            p.barrier()
        g1 = load_gate(st, "s5g1", 1, 0, 0)
        ygt = [sbt(st, f"s5ygt{i}", [128, 8, 128], BF16) for i in range(2)]
        h2 = [sbt(st, f"s5h2{i}", [128, D], F32) for i in range(2)]
        h3 = [sbt(st, f"s5h3{i}", [128, D], F32) for i in range(2)]
        sg_ = [sbt(st, f"s5sg{i}", [128, 512], F32) for i in range(2)]
        pa = [pst(st, f"s5pa{i}", [128, 512]) for i in range(2)]
        pgl = [pst(st, f"s5pg{i}", [128, 512]) for i in range(2)]
        c_ = 0
        YGs4 = YGs.rearrange("q p (r t) -> q p r t", r=2)
        for ti in range(16):
            yt = ygt[ti % 2]
            p.dma(yt.t[:], YGs[:, :, ti * 128:(ti + 1) * 128].rearrange("q p t -> p q t"), writes=[yt.b])
            hh = h2[ti % 2]
            ho_ = h3[ti % 2]
            p.dma(hh.t[:], Hloc[ti * 128:(ti + 1) * 128, :], writes=[hh.b])
            for half in range(2):
                a = pa[c_ % 2]
                g = pgl[c_ % 2]
                sg2 = sg_[c_ % 2]
                c_ += 1
                for k in range(8):
                    p.op('pe', lambda e, a=a, k=k, yt=yt, half=half: e.matmul(a.t[:, :], yt.t[:, k, :], Wgl.t[:, k, half * 512:(half + 1) * 512], start=(k == 0), stop=(k == 7)),
                         reads=[yt.b, Wgl.b], writes=[a.b])
                for k in range(8):
                    p.op('pe', lambda e, g=g, k=k, yt=yt, half=half: e.matmul(g.t[:, :], yt.t[:, k, :], Wgl.t[:, k, D + half * 512:D + (half + 1) * 512], start=(k == 0), stop=(k == 7)),
                         reads=[yt.b, Wgl.b], writes=[g.b])
                p.op('act', lambda e, g=g, sg2=sg2: e.activation(sg2.t[:, :], g.t[:, :], AF.Sigmoid), reads=[g.b], writes=[sg2.b])
                p.op('dve', lambda e, a=a, sg2=sg2: e.tensor_mul(sg2.t[:, :], sg2.t[:, :], a.t[:, :]), reads=[a.b, sg2.b], writes=[sg2.b])
                p.op('pool', lambda e, sg2=sg2, half=half: e.tensor_mul(sg2.t[:, :], sg2.t[:, :], g1.t[:, half * 512:(half + 1) * 512]), reads=[sg2.b, g1.b], writes=[sg2.b])
                p.op('pool', lambda e, sg2=sg2, hh=hh, ho_=ho_, half=half: e.tensor_add(ho_.t[:, half * 512:(half + 1) * 512], hh.t[:, half * 512:(half + 1) * 512], sg2.t[:, :]),
                     reads=[sg2.b, hh.b], writes=[ho_.b])
            p.dma(Hloc[ti * 128:(ti + 1) * 128, :], ho_.t[:], reads=[ho_.b])
        p.barrier()


_CACHE = {}


def kernel(**inputs):
    stage = int(inputs.pop("_stage", 99))
    if "nc" not in _CACHE or _CACHE.get("stage") != stage:
        _CACHE["nc"] = build(stage)
        _CACHE["stage"] = stage
        _CACHE["consts"] = host_consts()
    nc = _CACHE["nc"]
    if "consts_rev" not in _CACHE:
        _CACHE["consts_rev"] = host_consts(rev=True)
    f = lambda a: np.ascontiguousarray(np.asarray(a, dtype=np.float32))
    in_maps = []
    for core in range(8):
        b = core // 2
        rv = (core % 2 == 1) and stage >= 99
        cs = _CACHE["consts_rev"] if rv else _CACHE["consts"]
        sd = (lambda a: np.asarray(a)[0][::-1]) if rv else (lambda a: np.asarray(a)[0])
        xb = np.asarray(inputs["x"][b])
        cb = np.asarray(inputs["ctx"][b])
        if rv:
            xb = xb[::-1]
            cb = cb[::-1]
        m = {
            "x": f(xb), "c": f(inputs["c"][b]).reshape(8, 128), "ctx": f(cb),
            "c_ctx": f(inputs["c_ctx"]).reshape(8, 128),
            "mod_w": f(inputs["mod_w"]), "mod_b": f(inputs["mod_b"]).reshape(2, 48, 128),
            "norm_g": f(inputs["norm_g"]).reshape(2, 2, 8, 128),
            "ffn_w_gate": f(inputs["ffn_w_gate"]), "ffn_w_up": f(inputs["ffn_w_up"]), "ffn_w_down": f(inputs["ffn_w_down"]),
            "mix_w_in": f(inputs["mix_w_in"][0]), "mix_w_out": f(inputs["mix_w_out"][0]), "attn_sink": f(inputs["attn_sink"]).reshape(1, 8),
            "ssm_a_re": f(sd(inputs["ssm_a_re"])).reshape(128, 64), "ssm_a_im": f(sd(inputs["ssm_a_im"])).reshape(128, 64),
            "ssm_log_dt": f(sd(inputs["ssm_log_dt"])).reshape(1, 128),
            "ssm_b_re": f(sd(inputs["ssm_b_re"])), "ssm_b_im": f(sd(inputs["ssm_b_im"])),
            "ssm_c_re": f(sd(inputs["ssm_c_re"])), "ssm_c_im": f(sd(inputs["ssm_c_im"])),
            "ssm_d": f(inputs["ssm_d"][0]).reshape(8, 128), "ssm_glu_w": f(inputs["ssm_glu_w"][0]), "final_g": f(inputs["final_g"]).reshape(1, D),
        }
        m.update(cs)
        m["rk"] = np.array([[0]], np.int32)
        in_maps.append(m)
    res = run_bass_kernel_spmd(nc, in_maps, core_ids=list(range(8)))
    if stage < 99:
        return np.stack([np.asarray(res.results[2 * b]["out"], dtype=np.float32) for b in range(4)], axis=0)
    outp = np.stack([np.concatenate([np.asarray(res.results[2 * b]["out"], dtype=np.float32),
                                     np.asarray(res.results[2 * b + 1]["out"], dtype=np.float32)[::-1]], axis=0) for b in range(4)], axis=0)
    return outp
```

```python
import math
import numpy as np
import ml_dtypes
import concourse.bass as bass
import concourse.mybir as mybir
from concourse.bass_utils import run_bass_kernel_spmd
from contextlib import ExitStack

F32 = mybir.dt.float32
BF16 = mybir.dt.bfloat16
AF = mybir.ActivationFunctionType
ALU = mybir.AluOpType
NPBF = ml_dtypes.bfloat16

D = 1024
T = 4096
TC = 256
FF = 2816
NFC = 22
EPS = 1e-6


class Buf:
    def __init__(self, name=""):
        self.name = name
        self.last_w = None
        self.readers = []


class Prog:
    ENG = ['pe', 'act', 'dve', 'pool', 'sp']

    def __init__(self, nc, ndma_sems=10):
        self.nc = nc
        self.ops = {e: [] for e in self.ENG}
        self.cnt = {e: 0 for e in self.ENG}
        self.known = {e: {} for e in self.ENG}
        self.es = ExitStack()
        self.sem = {e: self.es.enter_context(nc.semaphore('s_' + e)) for e in self.ENG}
        self.dma_sems = {e: [self.es.enter_context(nc.semaphore(f'd_{e}{i}')) for i in range(ndma_sems)]
                         for e in ['sp', 'act', 'pool']}
        self.dma_val = {e: [0] * ndma_sems for e in self.dma_sems}
        self.dma_rr = {e: 0 for e in self.dma_sems}
        self.semobj = {}
        for e in self.ENG:
            self.semobj[('c', e)] = self.sem[e]
        for e in self.dma_sems:
            for i, s in enumerate(self.dma_sems[e]):
                self.semobj[('d', e, i)] = s
        self.q = 0
        self.rank_ap = None

    def _waits(self, eng, toks):
        need = {}
        for t in toks:
            if t is None:
                continue
            k, v = t
            if k == ('c', eng) and eng == 'pe':
                continue
            if self.known[eng].get(k, 0) >= v:
                continue
            if need.get(k, 0) < v:
                need[k] = v
        for k, v in need.items():
            self.known[eng][k] = v
        return list(need.items())

    def _deps(self, reads, writes):
        toks = []
        for b in reads:
            toks.append(b.last_w)
        for b in writes:
            toks.append(b.last_w)
            toks.extend(b.readers)
        return toks

    def _commit(self, tok, reads, writes):
        for b in reads:
            b.readers.append(tok)
            if len(b.readers) > 64:
                b.readers = b.readers[-64:]
        for b in writes:
            b.last_w = tok
            b.readers = []

    def op(self, eng, fn, reads=(), writes=()):
        waits = self._waits(eng, self._deps(reads, writes))
        self.cnt[eng] += 1
        tok = (('c', eng), self.cnt[eng])
        self.ops[eng].append((waits, fn, (self.sem[eng], 1)))
        self._commit(tok, reads, writes)
        return tok

    def dma(self, out, in_, reads=(), writes=(), eng=None, **kw):
        if eng is None:
            is_store = (not callable(out)) and ('DRam' in type(out.tensor).__name__)
            eng = 'pool' if is_store else 'sp'
        toks = self._deps(reads, writes)
        i = self.dma_rr[eng]
        self.dma_rr[eng] = (i + 1) % len(self.dma_sems[eng])
        key = ('d', eng, i)
        prev = self.dma_val[eng][i]
        if prev > 0:
            toks.append((key, prev))
        waits = self._waits(eng, toks)
        self.dma_val[eng][i] = prev + 16
        tok = (key, prev + 16)
        def issue(e, out=out, in_=in_, eng=eng):
            o = out(self.dyn[eng]) if callable(out) else out
            i2 = in_(self.dyn[eng]) if callable(in_) else in_
            return e.dma_start(out=o, in_=i2, **kw)
        self.ops[eng].append((waits, issue, (self.dma_sems[eng][i], 16)))
        self._commit(tok, reads, writes)
        return tok

    def all_tokens(self):
        allt = []
        for e in self.ENG:
            if self.cnt[e]:
                allt.append((('c', e), self.cnt[e]))
        for e in self.dma_sems:
            for i, v in enumerate(self.dma_val[e]):
                if v:
                    allt.append((('d', e, i), v))
        return allt

    def barrier(self):
        allt = self.all_tokens()
        for e in self.ENG:
            w = self._waits(e, allt)
            if w:
                self.ops[e].append((w, None, None))

    def emit(self):
        nc = self.nc
        fin = self._waits('sp', self.all_tokens())
        self.ops['sp'].append((fin, None, None))
        self.dyn = {}
        with nc.Block() as block:
            def mk(eng):
                def run(e):
                    for waits, fn, inc in self.ops[eng]:
                        for k, v in waits:
                            e.wait_ge(self.semobj[k], v)
                        if fn is not None:
                            fn(e).then_inc(inc[0], inc[1])

                def body(e):
                    if eng in ('sp', 'act') and self.rank_ap is not None:
                        with e.register("rk_" + eng) as reg:
                            e.reg_load(reg, self.rank_ap)
                            self.dyn[eng] = e.snap(reg, min_val=0, max_val=2048)
                            run(e)
                    else:
                        run(e)
                return body
            block.tensor(mk('pe'))
            block.scalar(mk('act'))
            block.vector(mk('dve'))
            block.gpsimd(mk('pool'))
            block.sync(mk('sp'))
        self.es.close()


class TB:
    def __init__(self, t, name=""):
        self.t = t
        self.b = Buf(name)


def host_consts(rev=False):
    cs = {}
    cs["identf"] = np.eye(128, dtype=np.float32)
    cs["identb"] = np.eye(128, dtype=np.float32).astype(NPBF)
    t = np.arange(T)
    row = (t // 64).astype(np.float64)
    col = (t % 64).astype(np.float64)
    nf = 16
    inv = 10000.0 ** (-np.arange(nf, dtype=np.float64) / nf)
    inv = inv.astype(np.float32).astype(np.float64)
    ang = np.concatenate([(row[:, None].astype(np.float32) * inv[None].astype(np.float32)),
                          (col[:, None].astype(np.float32) * inv[None].astype(np.float32))], axis=-1).astype(np.float32)
    cosv = np.cos(ang).astype(np.float32)
    sinv = np.sin(ang).astype(np.float32)
    C = np.zeros((128, T), np.float32)
    S = np.zeros((128, T), np.float32)
    for p in range(128):
        d = p % 64
        i = d // 2
        C[p] = cosv[:, i]
        S[p] = sinv[:, i] * (-1.0 if d % 2 == 0 else 1.0)
    if rev:
        C = np.ascontiguousarray(C[:, ::-1])
        S = np.ascontiguousarray(S[:, ::-1])
    cs["ropeC"] = C
    cs["ropeS"] = S
    j = np.arange(128)[:, None]
    i = np.arange(128)[None, :]
    mp = np.where(j >= i, 0.0, -30000.0).astype(np.float32)
    mn = np.where(j <= i, 0.0, -30000.0).astype(np.float32)
    cs["maskP"] = np.tile(mp, (1, 4)).astype(NPBF)
    cs["maskN"] = np.tile(mn, (1, 4)).astype(NPBF)
    tt = np.arange(T, dtype=np.int64)
    tk = (tt[:, None] * tt[None, :]) % T
    angT = 2.0 * np.pi * tk / T
    ct = (np.cos(angT) / math.sqrt(T)).astype(np.float32)
    stt = (np.sin(angT) / math.sqrt(T)).astype(np.float32)
    if rev:
        ct = np.ascontiguousarray(ct[::-1, ::-1])
        stt = np.ascontiguousarray(stt[::-1, ::-1])
    cs["CT"] = np.ascontiguousarray(ct.reshape(32, 128, 16, 256).transpose(2, 1, 0, 3)).astype(NPBF)
    cs["ST"] = np.ascontiguousarray(stt.reshape(32, 128, 16, 256).transpose(2, 1, 0, 3)).astype(NPBF)
    t2 = np.arange(TC, dtype=np.int64)
    a2 = 2.0 * np.pi * ((t2[:, None] * t2[None, :]) % TC) / TC
    c2 = (np.cos(a2) / math.sqrt(TC)).astype(np.float32)
    s2 = (np.sin(a2) / math.sqrt(TC)).astype(np.float32)
    if rev:
        c2 = np.ascontiguousarray(c2[::-1, ::-1])
        s2 = np.ascontiguousarray(s2[::-1, ::-1])
    cs["C256"] = np.ascontiguousarray(c2.reshape(2, 128, 256).transpose(1, 0, 2)).astype(NPBF)
    cs["S256"] = np.ascontiguousarray(s2.reshape(2, 128, 256).transpose(1, 0, 2)).astype(NPBF)
    c64 = np.arange(64)
    a3 = 2.0 * np.pi * ((c64[:, None] * c64[None, :]) % 64) / 64
    cc = np.zeros((128, 128), np.float32)
    sc = np.zeros((128, 128), np.float32)
    for g in range(2):
        cc[g * 64:(g + 1) * 64, g * 64:(g + 1) * 64] = np.cos(a3) / 8.0
        sc[g * 64:(g + 1) * 64, g * 64:(g + 1) * 64] = np.sin(a3) / 8.0
    cs["Cc"] = cc.astype(NPBF)
    cs["Sc"] = sc.astype(NPBF)
    return cs


CONST_SHAPES = {
    "identf": ([128, 128], F32), "identb": ([128, 128], BF16), "ropeC": ([128, T], F32), "ropeS": ([128, T], F32),
    "maskP": ([128, 512], BF16), "maskN": ([128, 512], BF16),
    "CT": ([16, 128, 32, 256], BF16), "ST": ([16, 128, 32, 256], BF16),
    "C256": ([128, 2, 256], BF16), "S256": ([128, 2, 256], BF16), "Cc": ([128, 128], BF16), "Sc": ([128, 128], BF16),
}

IN_SHAPES = {
    "x": [T, D], "c": [8, 128], "ctx": [TC, D], "c_ctx": [8, 128],
    "mod_w": [2, D, 6 * D], "mod_b": [2, 48, 128], "norm_g": [2, 2, 8, 128],
    "ffn_w_gate": [2, D, FF], "ffn_w_up": [2, D, FF], "ffn_w_down": [2, FF, D],
    "mix_w_in": [D, 1280], "mix_w_out": [D, D], "attn_sink": [1, 8],
    "ssm_a_re": [128, 64], "ssm_a_im": [128, 64], "ssm_log_dt": [1, 128],
    "ssm_b_re": [2, 64, 64, 16], "ssm_b_im": [2, 64, 64, 16], "ssm_c_re": [2, 64, 16, 64], "ssm_c_im": [2, 64, 16, 64],
    "ssm_d": [8, 128], "ssm_glu_w": [D, 2 * D], "final_g": [1, D],
}


def build(stage=99):
    nc = bass.Bass("TRN2", target_bir_lowering=False)
    I = {n: nc.dram_tensor(n, sh, F32, kind="ExternalInput").ap() for n, sh in IN_SHAPES.items()}
    K = {n: nc.dram_tensor(n, sh, dt, kind="ExternalInput").ap() for n, (sh, dt) in CONST_SHAPES.items()}
    rk_in = nc.dram_tensor("rk", [1, 1], mybir.dt.int32, kind="ExternalInput").ap()
    HT = T // 2
    out = nc.dram_tensor("out", [T if stage < 99 else HT, D], F32, kind="ExternalOutput").ap()
    Hs = nc.dram_tensor("Hs", [T + TC, D], F32).ap()
    mod_b_flat = I["mod_b"].rearrange("l j p -> l (j p)")

    p = Prog(nc)
    p.rank_ap = None
    top = ExitStack()
    Hloc = nc.dram_tensor("Hloc", [T // 2, D], F32).ap()
    p.Hloc = Hloc

    uid = {"n": 0}

    def sbt(st, name, shape, dt):
        uid["n"] += 1
        return TB(st.enter_context(nc.sbuf_tensor(f"s{uid['n']}_{name}", shape, dt)), name)

    def pst(st, name, shape, dt=F32):
        uid["n"] += 1
        return TB(st.enter_context(nc.psum_tensor(f"p{uid['n']}_{name}", shape, dt)), name)

    identf = sbt(top, "identf", [128, 128], F32)
    identb = sbt(top, "identb", [128, 128], BF16)
    p.dma(identf.t[:], K["identf"], writes=[identf.b])
    p.dma(identb.t[:], K["identb"], writes=[identb.b])
    ones_b = sbt(top, "ones_b", [128, 128], BF16)
    p.op('pool', lambda e: e.memset(ones_b.t[:], 1.0), writes=[ones_b.b])
    AB = sbt(top, "AB", [128, 2, 2, 4, 8], F32)
    Gs = nc.dram_tensor("Gs", [8, 128, D], F32).ap()
    ATs = nc.dram_tensor("ATs", [34, 128, 4, 128], BF16).ap()

    def load_gate(st, nm, l, col, gi):
        g = sbt(st, nm, [128, D], F32)
        p.dma(g.t[:], Gs[(l * 2 + col) * 2 + gi], writes=[g.b])
        return g
    rr = {"i": 0}

    def ew(fn, reads, writes, engs=('dve', 'pool')):
        e = engs[rr["i"] % len(engs)]
        rr["i"] += 1
        return p.op(e, fn, reads=reads, writes=writes)

    with ExitStack() as st:
        gates = sbt(st, "gates", [128, 2, 2, 2, D], F32)
        crow = sbt(st, "crow", [16, 128], F32)
        p.dma(crow.t[0:8, :], I["c"], writes=[crow.b])
        p.dma(crow.t[8:16, :], I["c_ctx"], writes=[crow.b])
        pT = pst(st, "pT", [128, 96])
        scT = sbt(st, "scT", [128, 16], F32)
        p.op('pe', lambda e: e.transpose(pT.t[:, 0:16], crow.t[:, :], identf.t[0:16, 0:16]), reads=[crow.b, identf.b], writes=[pT.b])
        p.op('act', lambda e: e.activation(scT.t[:], pT.t[:, 0:16], AF.Silu), reads=[pT.b], writes=[scT.b])
        scbc = sbt(st, "scbc", [128, 16, 128], F32)
        for ck in range(16):
            ew(lambda e, ck=ck: e.tensor_copy(scbc.t[:, ck, :], scT.t[:, ck:ck + 1].to_broadcast([128, 128])), [scT.b], [scbc.b])
        mbrow = sbt(st, "mbrow", [48, 2, 128], F32)
        ngrow = sbt(st, "ngrow", [32, 128], F32)
        p.dma(mbrow.t[:, 0, :], I["mod_b"][0], writes=[mbrow.b])
        p.dma(mbrow.t[:, 1, :], I["mod_b"][1], writes=[mbrow.b])
        p.dma(ngrow.t[:], I["norm_g"].rearrange("l i k p -> (l i k) p"), writes=[ngrow.b])
        mbT = sbt(st, "mbT", [128, 2, 48], F32)
        ngT = sbt(st, "ngT", [128, 32], F32)
        for l in range(2):
            p.op('pe', lambda e, l=l: e.transpose(pT.t[:, 0:48], mbrow.t[:, l, :], identf.t[0:48, 0:48]), reads=[mbrow.b, identf.b], writes=[pT.b])
            p.op('dve', lambda e, l=l: e.tensor_copy(mbT.t[:, l, :], pT.t[:, 0:48]), reads=[pT.b], writes=[mbT.b])
        p.op('pe', lambda e: e.transpose(pT.t[:, 0:32], ngrow.t[:, :], identf.t[0:32, 0:32]), reads=[ngrow.b, identf.b], writes=[pT.b])
        p.op('dve', lambda e: e.tensor_copy(ngT.t[:], pT.t[:, 0:32]), reads=[pT.b], writes=[ngT.b])
        macc = sbt(st, "macc", [128, 2, 48, 2], F32)
        Wk = [sbt(st, f"Wk{i}", [128, 6 * D], F32) for i in range(2)]
        pg = [pst(st, f"pg{i}", [128, 512]) for i in range(2)]
        pm = pst(st, "pm", [128, 96])
        it = 0
        for l in range(2):
            for k in range(8):
                w = Wk[it % 2]
                it += 1
                for q3 in range(3):
                    p.dma(w.t[:, q3 * 2048:(q3 + 1) * 2048], I["mod_w"][l, k * 128:(k + 1) * 128, q3 * 2048:(q3 + 1) * 2048], writes=[w.b])
                for j in range(48):
                    p.op('pe', lambda e, j=j, w=w, k=k: e.matmul(pm.t[:, 2 * j:2 * j + 2], w.t[:, j * 128:(j + 1) * 128],
                                                                 scT.t[:, k:16:8], start=True, stop=True),
                         reads=[w.b, scT.b], writes=[pm.b])
                if k == 0:
                    p.op('dve', lambda e, l=l: e.tensor_copy(macc.t[:, l].rearrange("p j c -> p (j c)"), pm.t[:, :]), reads=[pm.b], writes=[macc.b])
                else:
                    p.op('dve', lambda e, l=l: e.tensor_add(macc.t[:, l].rearrange("p j c -> p (j c)"), macc.t[:, l].rearrange("p j c -> p (j c)"), pm.t[:, :]),
                         reads=[pm.b, macc.b], writes=[macc.b])
                gi = 0
                for col in range(2):
                    for g_i, which in enumerate((2, 5)):
                        for half in range(2):
                            pgt = pg[gi % 2]
                            gi += 1
                            p.op('pe', lambda e, pgt=pgt, col=col, k=k, w=w, which=which, half=half: e.matmul(
                                pgt.t[:, :], scbc.t[:, col * 8 + k, :], w.t[:, which * D + half * 512: which * D + half * 512 + 512], start=True, stop=True),
                                reads=[scbc.b, w.b], writes=[pgt.b])
                            dst = gates.t[:, l, col, g_i, half * 512:(half + 1) * 512]
                            if k == 0:
                                p.op('dve', lambda e, dst=dst, pgt=pgt: e.tensor_copy(dst, pgt.t[:, :]), reads=[pgt.b], writes=[gates.b])
                            else:
                                p.op('dve', lambda e, dst=dst, pgt=pgt: e.tensor_add(dst, dst, pgt.t[:, :]), reads=[pgt.b, gates.b], writes=[gates.b])
        gb = sbt(st, "gb", [128, D], F32)
        for l in range(2):
            for col in range(2):
                p.op('dve', lambda e, l=l, col=col: e.tensor_add(macc.t[:, l, :, col], macc.t[:, l, :, col], mbT.t[:, l, :]), reads=[macc.b, mbT.b], writes=[macc.b])
            for g_i, which in enumerate((2, 5)):
                p.dma(gb.t[:], mod_b_flat[l:l + 1, which * D:(which + 1) * D].partition_broadcast(128), writes=[gb.b])
                for col in range(2):
                    p.op('dve', lambda e, l=l, col=col, g_i=g_i: e.tensor_add(gates.t[:, l, col, g_i, :], gates.t[:, l, col, g_i, :], gb.t[:]),
                         reads=[gb.b, gates.b], writes=[gates.b])
            for col in range(2):
                for i2 in range(2):
                    sh = macc.t[:, l, (3 * i2) * 8:(3 * i2) * 8 + 8, col]
                    scl = macc.t[:, l, (3 * i2 + 1) * 8:(3 * i2 + 1) * 8 + 8, col]
                    gn = ngT.t[:, (l * 2 + i2) * 8:(l * 2 + i2) * 8 + 8]
                    p.op('dve', lambda e, l=l, col=col, i2=i2, scl=scl, gn=gn: e.scalar_tensor_tensor(
                        AB.t[:, l, col, 2 * i2, :], scl, 1.0, gn, ALU.add, ALU.mult), reads=[macc.b, ngT.b], writes=[AB.b])
                    p.op('dve', lambda e, l=l, col=col, i2=i2, sh=sh: e.tensor_copy(AB.t[:, l, col, 2 * i2 + 1, :], sh), reads=[macc.b], writes=[AB.b])
        for l in range(2):
            for col in range(2):
                for g_i in range(2):
                    p.dma(Gs[(l * 2 + col) * 2 + g_i], gates.t[:, l, col, g_i, :], reads=[gates.b])
        p.barrier()

    if stage == 0:
        with ExitStack() as st:
            g0 = load_gate(st, "g0dbg", 0, 0, 0)
            p.dma(out[0:128, :], g0.t[:], reads=[g0.b])
        p.dma(out[128:256, 0:128], AB.t[:].rearrange("p a b c d -> p (a b c d)"), reads=[AB.b])
        p.emit()
        top.close()
        return nc

    def make_norm(st, nm, nxt=2):
        ctxn = {}
        ctxn["xt"] = [sbt(st, f"{nm}xt{i}", [128, D], F32) for i in range(nxt)]
        ctxn["junk"] = sbt(st, f"{nm}junk", [128, D], BF16)
        ctxn["xn"] = [sbt(st, f"{nm}xn{i}", [128, D], BF16) for i in range(2)]
        ctxn["ss"] = [sbt(st, f"{nm}ss{i}", [128, 1], F32) for i in range(3)]
        ctxn["tp"] = [pst(st, f"{nm}tp{i}", [128, 8, 128], BF16) for i in range(2)]
        ctxn["n"] = 0
        return ctxn

    def norm_tile(cn, src, l, col, which, dst_fn, dst_buf, xt_fixed=None, ident=None):
        n = cn["n"]
        cn["n"] += 1
        xt = xt_fixed if xt_fixed is not None else cn["xt"][n % len(cn["xt"])]
        ss = cn["ss"][n % 3]
        xn = cn["xn"][n % 2]
        tp = cn["tp"][n % 2]
        junk = cn["junk"]
        idt = ident if ident is not None else identb
        p.dma(xt.t[:], src, writes=[xt.b])
        p.op('act', lambda e: e.activation(junk.t[:], xt.t[:], AF.Square, accum_out=ss.t[:]), reads=[xt.b], writes=[junk.b, ss.b])
        p.op('dve', lambda e: e.tensor_scalar(ss.t[:], ss.t[:], 1.0 / D, EPS, ALU.mult, ALU.add), reads=[ss.b], writes=[ss.b])
        p.op('act', lambda e: e.activation(ss.t[:], ss.t[:], AF.Sqrt), reads=[ss.b], writes=[ss.b])
        p.op('dve', lambda e: e.reciprocal(ss.t[:], ss.t[:]), reads=[ss.b], writes=[ss.b])
        p.op('dve', lambda e: e.tensor_scalar(xn.t[:], xt.t[:], ss.t[:, 0:1], None, ALU.mult), reads=[xt.b, ss.b], writes=[xn.b])
        for k in range(8):
            p.op('pe', lambda e, k=k: e.transpose(tp.t[:, k, :], xn.t[:, k * 128:(k + 1) * 128], idt.t[:]), reads=[xn.b, idt.b], writes=[tp.b])
        for k in range(8):
            p.op('act', lambda e, k=k: e.activation(dst_fn(k), tp.t[:, k, :], AF.Identity,
                                                    scale=AB.t[:, l, col, 2 * which, k:k + 1], bias=AB.t[:, l, col, 2 * which + 1, k:k + 1]),
                 reads=[tp.b, AB.b], writes=[dst_buf])
        return xt, ss

    def tile_src(ti):
        return I["x"][ti * 128:(ti + 1) * 128, :] if ti < 32 else I["ctx"][(ti - 32) * 128:(ti - 31) * 128, :]

    NT = 34
    L0 = ExitStack()
    Fm = sbt(L0, "Fm", [128, NT, 512], BF16)
    with ExitStack() as stBC:
        QT = sbt(stBC, "QT", [128, 4, NT * 128], BF16)
        KT = sbt(stBC, "KT", [128, NT * 128], BF16)
        Vm = sbt(stBC, "Vm", [128, NT, 128], BF16)
        with ExitStack() as st:
            wb = sbt(st, "wb", [128, 8, 1920], BF16)
            wst = [sbt(st, f"wst{i}", [128, 1280], F32) for i in range(1)]
            for k in range(8):
                s_ = wst[0]
                p.dma(s_.t[:], I["mix_w_in"][k * 128:(k + 1) * 128, :], writes=[s_.b])
                S = s_.t
                W = wb.t
                ew(lambda e, k=k, S=S, W=W: e.tensor_copy(W[:, k, 0:512], S[:, 0:512]), [s_.b], [wb.b])
                ew(lambda e, k=k, S=S, W=W: e.tensor_copy(W[:, k, 512:640], S[:, 1152:1280]), [s_.b], [wb.b])
                ew(lambda e, k=k, S=S, W=W: e.tensor_copy(W[:, k, 640:1152].rearrange("p (j h d) -> p j h d", j=4, h=2),
                                                          S[:, 512:1024].rearrange("p (h j d) -> p j h d", h=2, j=4)), [s_.b], [wb.b])
                ew(lambda e, k=k, S=S, W=W: e.tensor_copy(W[:, k, 1152:1280], S[:, 1024:1152]), [s_.b], [wb.b])
                for two in range(2):
                    ew(lambda e, k=k, S=S, W=W, two=two: e.tensor_copy(
                        W[:, k, 1280:1792].rearrange("p (j h i t) -> p j h i t", j=4, h=2, t=2)[:, :, :, :, two],
                        S[:, 512:1024].rearrange("p (h j i t) -> p j h i t", h=2, j=4, t=2)[:, :, :, :, 1 - two]), [s_.b], [wb.b])
                    ew(lambda e, k=k, S=S, W=W, two=two: e.tensor_copy(
                        W[:, k, 1792:1920].rearrange("p (i t) -> p i t", t=2)[:, :, two],
                        S[:, 1024:1152].rearrange("p (i t) -> p i t", t=2)[:, :, 1 - two]), [s_.b], [wb.b])
            ropeCb = [sbt(st, f"ropeC{i}", [128, 512], F32) for i in range(2)]
            ropeSb = [sbt(st, f"ropeS{i}", [128, 512], F32) for i in range(2)]
            cn = make_norm(st, "b")
            nTb = [sbt(st, f"nTb{i}", [128, 8, 512], BF16) for i in range(2)]
            pf = pst(st, "pf", [128, 512])
            pv = pst(st, "pv", [128, 128])
            pq = [pst(st, f"pq{i}", [128, 512]) for i in range(2)]
            pqp = [pst(st, f"pqp{i}", [128, 512]) for i in range(2)]
            t1 = [sbt(st, f"t1{i}", [128, 512], F32) for i in range(2)]
            t2 = [sbt(st, f"t2{i}", [128, 512], F32) for i in range(2)]
            qic = {"n": 0}

            def partN(blk):
                ntile = 4 if blk < 8 else 2
                nT = nTb[blk % 2]
                col = 0 if blk < 8 else 1
                for tt in range(ntile):
                    ti = blk * 4 + tt
                    norm_tile(cn, tile_src(ti), 0, col, 0, lambda k, tt=tt, nT=nT: nT.t[:, k, tt * 128:(tt + 1) * 128], nT.b)

            def partFV(blk):
                ntile = 4 if blk < 8 else 2
                nT = nTb[blk % 2]
                for tt in range(ntile):
                    ti = blk * 4 + tt
                    for k in range(8):
                        p.op('pe', lambda e, k=k, tt=tt, nT=nT: e.matmul(pf.t[:, :], nT.t[:, k, tt * 128:(tt + 1) * 128], wb.t[:, k, 0:512],
                                                                      start=(k == 0), stop=(k == 7)), reads=[nT.b, wb.b], writes=[pf.b])
                    p.op('act', lambda e, ti=ti: e.activation(Fm.t[:, ti, :], pf.t[:, :], AF.Identity), reads=[pf.b], writes=[Fm.b])
                    for k in range(8):
                        p.op('pe', lambda e, k=k, tt=tt, nT=nT: e.matmul(pv.t[:, :], nT.t[:, k, tt * 128:(tt + 1) * 128], wb.t[:, k, 512:640],
                                                                      start=(k == 0), stop=(k == 7)), reads=[nT.b, wb.b], writes=[pv.b])
                    p.op('dve', lambda e, ti=ti: e.tensor_copy(Vm.t[:, ti, :], pv.t[:, :]), reads=[pv.b], writes=[Vm.b])

            def partQK(blk):
                ntile = 4 if blk < 8 else 2
                ncol = ntile * 128
                nT = nTb[blk % 2]
                c0 = blk * 512
                ropeC = ropeCb[blk % 2]
                ropeS = ropeSb[blk % 2]
                if blk < 8:
                    p.dma(ropeC.t[:], K["ropeC"][:, c0:c0 + 512], writes=[ropeC.b])
                    p.dma(ropeS.t[:], K["ropeS"][:, c0:c0 + 512], writes=[ropeS.b])
                for oc in range(5):
                    qi_ = qic["n"]
                    qic["n"] += 1
                    a = pq[qi_ % 2]
                    bq = pqp[qi_ % 2]
                    u1 = t1[qi_ % 2]
                    u2 = t2[qi_ % 2]
                    for k in range(8):
                        p.op('pe', lambda e, k=k, oc=oc, a=a, nT=nT, ncol=ncol: e.matmul(a.t[:, 0:ncol], wb.t[:, k, 640 + 128 * oc: 768 + 128 * oc], nT.t[:, k, 0:ncol],
                                                                                   start=(k == 0), stop=(k == 7)), reads=[nT.b, wb.b], writes=[a.b])
                    dstb = QT.b if oc < 4 else KT.b
                    dst = QT.t[:, oc, c0:c0 + ncol] if oc < 4 else KT.t[:, c0:c0 + ncol]
                    if blk < 8:
                        for k in range(8):
                            p.op('pe', lambda e, k=k, oc=oc, bq=bq, nT=nT, ncol=ncol: e.matmul(bq.t[:, 0:ncol], wb.t[:, k, 1280 + 128 * oc: 1408 + 128 * oc], nT.t[:, k, 0:ncol],
                                                                                        start=(k == 0), stop=(k == 7)), reads=[nT.b, wb.b], writes=[bq.b])
                        p.op('dve', lambda e, a=a, u1=u1, ropeC=ropeC: e.tensor_mul(u1.t[:, :], a.t[:, :], ropeC.t[:, :]), reads=[a.b, ropeC.b], writes=[u1.b])
                        p.op('dve', lambda e, bq=bq, u2=u2, ropeS=ropeS: e.tensor_mul(u2.t[:, :], bq.t[:, :], ropeS.t[:, :]), reads=[bq.b, ropeS.b], writes=[u2.b])
                        p.op('pool', lambda e, dst=dst, u1=u1, u2=u2: e.tensor_add(dst, u1.t[:, :], u2.t[:, :]), reads=[u1.b, u2.b], writes=[dstb])
                    else:
                        p.op('dve', lambda e, dst=dst, a=a, ncol=ncol: e.tensor_copy(dst, a.t[:, 0:ncol]), reads=[a.b], writes=[dstb])

            partN(0)
            partFV(0)
            for blk in range(9):
                if blk + 1 < 9:
                    partN(blk + 1)
                partQK(blk)
                if blk + 1 < 9:
                    partFV(blk + 1)
            p.barrier()
        with ExitStack() as st:
            maskP = sbt(st, "maskP", [128, 512], BF16)
            maskN = sbt(st, "maskN", [128, 512], BF16)
            p.dma(maskP.t[:], K["maskP"], writes=[maskP.b])
            p.dma(maskN.t[:], K["maskN"], writes=[maskN.b])
            sk = sbt(st, "sk", [128, 8], F32)
            p.dma(sk.t[:], I["attn_sink"].partition_broadcast(128), writes=[sk.b])
            p.op('act', lambda e: e.activation(sk.t[:], sk.t[:], AF.Exp), reads=[sk.b], writes=[sk.b])
            skf = sbt(st, "skf", [128, 512], F32)
            for g in range(2):
                for hh in range(4):
                    p.op('dve', lambda e, g=g, hh=hh: e.tensor_copy(skf.t[64 * g:64 * g + 64, hh * 128:(hh + 1) * 128],
                                                                  sk.t[64 * g:64 * g + 64, 4 * g + hh:4 * g + hh + 1].to_broadcast([64, 128])), reads=[sk.b], writes=[skf.b])
            pS = [pst(st, f"pS{i}", [128, 512]) for i in range(3)]
            pO = [pst(st, f"pO{i}", [128, 512]) for i in range(2)]
            pZ = [pst(st, f"pZ{i}", [128, 512]) for i in range(2)]
            PT = [sbt(st, f"PT{i}", [128, 512], BF16) for i in range(3)]
            den = [sbt(st, f"den{i}", [128, 512], F32) for i in range(2)]
            ATt = [sbt(st, f"ATt{i}", [128, 4, 128], BF16) for i in range(2)]
            OB = [[Buf() for _ in range(2)] for _ in range(2)]
            ZB = [[Buf() for _ in range(2)] for _ in range(2)]
            DB = [[Buf() for _ in range(2)] for _ in range(2)]
            si = 0
            pend = {"t": None}
            for qi in range(NT):
                AT = ATt[qi % 2]
                for g in range(2):
                    lo, hi = 64 * g, 64 * g + 64
                    keys = [(32, None), (33, None)]
                    if qi < 32:
                        if qi > 0:
                            keys.append((qi - 1, maskP))
                        keys.append((qi, None))
                        if qi < 31:
                            keys.append((qi + 1, maskN))
                    O = pO[qi % 2]
                    Z = pZ[qi % 2]
                    Ob = OB[qi % 2][g]
                    Zb = ZB[qi % 2][g]
                    Db = DB[qi % 2][g]
                    SP = []
                    for ki in range(len(keys)):
                        SP.append((pS[si % 3], PT[si % 3]))
                        si += 1

                    def emitS(ki):
                        kt, msk = keys[ki]
                        S_ = SP[ki][0]
                        p.op('pe', lambda e, S_=S_, kt=kt, qi=qi, lo=lo, hi=hi, msk=msk: e.matmul(
                            S_.t[:, :].rearrange("p (j q) -> p j q", j=4), KT.t[lo:hi, kt * 128:(kt + 1) * 128], QT.t[lo:hi, :, qi * 128:(qi + 1) * 128],
                            start=True, stop=(msk is None)), reads=[KT.b, QT.b], writes=[S_.b])
                        if msk is not None:
                            p.op('pe', lambda e, S_=S_, msk=msk: e.matmul(S_.t[:, :], identb.t[:, :], msk.t[:, :], start=False, stop=True),
                                 reads=[identb.b, msk.b], writes=[S_.b])
                    emitS(0)
                    if len(keys) > 1:
                        emitS(1)
                    for ki, (kt, msk) in enumerate(keys):
                        S_, P_ = SP[ki]
                        p.op('act', lambda e, S_=S_, P_=P_: e.activation(P_.t[:, :], S_.t[:, :], AF.Exp, scale=0.125), reads=[S_.b], writes=[P_.b])
                        if ki == 1 and pend["t"] is not None:
                            pend["t"]()
                            pend["t"] = None
                        if ki + 2 < len(keys):
                            emitS(ki + 2)
                        first = ki == 0
                        last = ki == len(keys) - 1
                        p.op('pe', lambda e, O=O, kt=kt, lo=lo, hi=hi, P_=P_, first=first, last=last: e.matmul(
                            O.t[lo:hi, :], Vm.t[:, kt, lo:hi], P_.t[:, :], start=first, stop=last), reads=[Vm.b, P_.b], writes=[Ob])
                        p.op('pe', lambda e, Z=Z, lo=lo, hi=hi, P_=P_, first=first, last=last: e.matmul(
                            Z.t[lo:hi, :], ones_b.t[:, 0:64], P_.t[:, :], start=first, stop=last), reads=[ones_b.b, P_.b], writes=[Zb])
                    dn = den[qi % 2]

                    def tail(dn=dn, Z=Z, O=O, lo=lo, hi=hi, AT=AT, Zb=Zb, Ob=Ob, Db=Db, g=g, qi=qi):
                        p.op('dve', lambda e: e.tensor_add(dn.t[lo:hi, :], Z.t[lo:hi, :], skf.t[lo:hi, :]), reads=[Zb, skf.b], writes=[Db])
                        p.op('act', lambda e: e.activation(dn.t[lo:hi, :], dn.t[lo:hi, :], AF.Ln), reads=[Db], writes=[Db])
                        p.op('act', lambda e: e.activation(dn.t[lo:hi, :], dn.t[lo:hi, :], AF.Exp, scale=-1.0), reads=[Db], writes=[Db])
                        p.op('dve', lambda e: e.tensor_mul(
                            AT.t[lo:hi, :, :], O.t[lo:hi, :].rearrange("p (j q) -> p j q", j=4),
                            dn.t[lo:hi, :].rearrange("p (j q) -> p j q", j=4)), reads=[Ob, Db], writes=[AT.b])
                        if g == 1:
                            p.dma(ATs[qi], AT.t[:], reads=[AT.b])
                    pend["t"] = tail
            if pend["t"] is not None:
                pend["t"]()
                pend["t"] = None
            p.barrier()

    with ExitStack() as st:
        wob = sbt(st, "wob", [128, 8, D], BF16)
        wst = [sbt(st, f"wost{i}", [128, D], F32) for i in range(2)]
        for j in range(8):
            s_ = wst[j % 2]
            if j < 4:
                p.dma(s_.t[:], I["mix_w_out"][j * 128:(j + 1) * 128, :], writes=[s_.b])
            else:
                jj = j - 4
                p.dma(s_.t[0:64, :], I["mix_w_out"][512 + 64 * jj:512 + 64 * jj + 64, :], writes=[s_.b])
                p.dma(s_.t[64:128, :], I["mix_w_out"][512 + 64 * (4 + jj):512 + 64 * (4 + jj) + 64, :], writes=[s_.b])
            ew(lambda e, j=j, s_=s_: e.tensor_copy(wob.t[:, j, :], s_.t[:]), [s_.b], [wob.b])
        Cc = sbt(st, "Cc", [128, 128], BF16)
        Sc = sbt(st, "Sc", [128, 128], BF16)
        p.dma(Cc.t[:], K["Cc"], writes=[Cc.b])
        p.dma(Sc.t[:], K["Sc"], writes=[Sc.b])
        CTb = [sbt(st, f"CTb{i}", [128, 32, 256], BF16) for i in range(2)]
        STb = [sbt(st, f"STb{i}", [128, 32, 256], BF16) for i in range(2)]
        pP = [pst(st, f"pP{i}", [128, 256]) for i in range(2)]
        pQ = [pst(st, f"pQ{i}", [128, 256]) for i in range(2)]
        pY = pst(st, "pY", [128, 256])
        pOo = [pst(st, f"pOo{i}", [128, 512]) for i in range(2)]
        Pb = [sbt(st, f"Pb{i}", [128, 256], BF16) for i in range(2)]
        Qb = [sbt(st, f"Qb{i}", [128, 256], BF16) for i in range(2)]
        YT = [sbt(st, f"YT{i}", [128, 4, 256], BF16) for i in range(2)]
        xres = [sbt(st, f"xres{i}", [128, D], F32) for i in range(2)]
        tmpo = [sbt(st, f"tmpo{i}", [128, 512], F32) for i in range(2)]
        hn = [sbt(st, f"hn{i}", [128, D], F32) for i in range(2)]
        ATl = [sbt(st, f"ATl{i}", [128, 4, 128], BF16) for i in range(2)]
        g1l = [load_gate(st, "g1lat", 0, 0, 0), load_gate(st, "g1ctx", 0, 1, 0)]
        cnt_ = 0
        oi = 0
        for kb in range(17):
            lat = kb < 16
            if lat:
                cb = CTb[kb % 2]
                sb_ = STb[kb % 2]
                for h4 in range(4):
                    p.dma(cb.t[:, h4 * 8:(h4 + 1) * 8, :], K["CT"][kb, :, h4 * 8:(h4 + 1) * 8, :], writes=[cb.b])
                    p.dma(sb_.t[:, h4 * 8:(h4 + 1) * 8, :], K["ST"][kb, :, h4 * 8:(h4 + 1) * 8, :], writes=[sb_.b])
                nti = 32
                t0 = 0
            else:
                cb = CTb[kb % 2]
                sb_ = STb[kb % 2]
                p.dma(cb.t[:, 0:2, :], K["C256"], writes=[cb.b])
                p.dma(sb_.t[:, 0:2, :], K["S256"], writes=[sb_.b])
                nti = 2
                t0 = 32
            Y = YT[kb % 2]
            for cc in range(4):
                a = pP[cnt_ % 2]
                b = pQ[cnt_ % 2]
                ab = Pb[cnt_ % 2]
                bb = Qb[cnt_ % 2]
                cnt_ += 1
                for i in range(nti):
                    p.op('pe', lambda e, a=a, i=i, cc=cc, cb=cb, t0=t0, nti=nti: e.matmul(a.t[:, :], Fm.t[:, t0 + i, cc * 128:(cc + 1) * 128], cb.t[:, i, :],
                                                                                     start=(i == 0), stop=(i == nti - 1)), reads=[Fm.b, cb.b], writes=[a.b])
                for i in range(nti):
                    p.op('pe', lambda e, b=b, i=i, cc=cc, sb_=sb_, t0=t0, nti=nti: e.matmul(b.t[:, :], Fm.t[:, t0 + i, cc * 128:(cc + 1) * 128], sb_.t[:, i, :],
                                                                                      start=(i == 0), stop=(i == nti - 1)), reads=[Fm.b, sb_.b], writes=[b.b])
                p.op('act', lambda e, a=a, ab=ab: e.activation(ab.t[:, :], a.t[:, :], AF.Identity), reads=[a.b], writes=[ab.b])
                p.op('act', lambda e, b=b, bb=bb: e.activation(bb.t[:, :], b.t[:, :], AF.Identity, scale=-1.0), reads=[b.b], writes=[bb.b])
                p.op('pe', lambda e, ab=ab: e.matmul(pY.t[:, :], Cc.t[:, :], ab.t[:, :], start=True, stop=False), reads=[Cc.b, ab.b], writes=[pY.b])
                p.op('pe', lambda e, bb=bb: e.matmul(pY.t[:, :], Sc.t[:, :], bb.t[:, :], start=False, stop=True), reads=[Sc.b, bb.b], writes=[pY.b])
                p.op('dve', lambda e, Y=Y, cc=cc: e.tensor_copy(Y.t[:, cc, :], pY.t[:, :]), reads=[pY.b], writes=[Y.b])
            for tt in range(2):
                ti = (kb * 2 + tt) if lat else 32 + tt
                col = 0 if lat else 1
                xr = xres[ti % 2]
                h_ = hn[ti % 2]
                p.dma(xr.t[:], tile_src(ti), writes=[xr.b])
                AT = ATl[ti % 2]
                p.dma(AT.t[:], ATs[ti], writes=[AT.b])
                for half in range(2):
                    po = pOo[oi % 2]
                    tm = tmpo[oi % 2]
                    oi += 1
                    for j in range(8):
                        lhs = Y.t[:, j, tt * 128:(tt + 1) * 128] if j < 4 else AT.t[:, j - 4, :]
                        rb = Y.b if j < 4 else AT.b
                        p.op('pe', lambda e, po=po, lhs=lhs, j=j, half=half: e.matmul(po.t[:, :], lhs, wob.t[:, j, half * 512:(half + 1) * 512],
                                                                                 start=(j == 0), stop=(j == 7)), reads=[rb, wob.b], writes=[po.b])
                    p.op('dve', lambda e, po=po, tm=tm, half=half, col=col: e.tensor_mul(tm.t[:, :], po.t[:, :], g1l[col].t[:, half * 512:(half + 1) * 512]),
                         reads=[po.b, g1l[col].b], writes=[tm.b])
                    p.op('pool', lambda e, tm=tm, xr=xr, h_=h_, half=half: e.tensor_add(h_.t[:, half * 512:(half + 1) * 512], xr.t[:, half * 512:(half + 1) * 512], tm.t[:, :]),
                         reads=[tm.b, xr.b], writes=[h_.b])
                p.dma(Hs[ti * 128:(ti + 1) * 128, :], h_.t[:], reads=[h_.b])
        p.barrier()
    L0.close()

    if stage == 1:
        for ti in range(32):
            pass
        with ExitStack() as st:
            bb_ = [sbt(st, f"bo{i}", [128, D], F32) for i in range(2)]
            for ti in range(32):
                b_ = bb_[ti % 2]
                p.dma(b_.t[:], Hs[ti * 128:(ti + 1) * 128, :], writes=[b_.b])
                p.dma(out[ti * 128:(ti + 1) * 128, :], b_.t[:], reads=[b_.b])
        p.emit()
        top.close()
        return nc

    def ffn(l, tiles, final):
        with ExitStack() as st:
            Wg = sbt(st, "Wg", [128, 8, FF], BF16)
            Wu = sbt(st, "Wu", [128, 8, FF], BF16)
            Wd = sbt(st, "Wd", [128, NFC, D], BF16)
            with ExitStack() as st2:
                stg = [sbt(st2, f"fst{i}", [128, FF], F32) for i in range(4)]
                si_ = 0
                for nm, Wt in (("ffn_w_gate", Wg), ("ffn_w_up", Wu)):
                    for k in range(8):
                        s_ = stg[si_ % 4]
                        si_ += 1
                        p.dma(s_.t[:, 0:1408], I[nm][l, k * 128:(k + 1) * 128, 0:1408], writes=[s_.b])
                        p.dma(s_.t[:, 1408:FF], I[nm][l, k * 128:(k + 1) * 128, 1408:FF], writes=[s_.b])
                        p.op('dve', lambda e, Wt=Wt, k=k, s_=s_: e.tensor_copy(Wt.t[:, k, 0:1408], s_.t[:, 0:1408]), reads=[s_.b], writes=[Wt.b])
                        p.op('act', lambda e, Wt=Wt, k=k, s_=s_: e.activation(Wt.t[:, k, 1408:FF], s_.t[:, 1408:FF], AF.Identity), reads=[s_.b], writes=[Wt.b])
                for fc in range(0, NFC, 2):
                    s_ = stg[si_ % 4]
                    si_ += 1
                    for q2 in range(2):
                        p.dma(s_.t[:, q2 * D:(q2 + 1) * D], I["ffn_w_down"][l, (fc + q2) * 128:(fc + q2 + 1) * 128, :], writes=[s_.b])
                    p.op('dve' if (fc // 2) % 2 == 0 else 'act', (lambda e, fc=fc, s_=s_: e.tensor_copy(Wd.t[:, fc:fc + 2, :].rearrange("p a d -> p (a d)"), s_.t[:, 0:2 * D])) if (fc // 2) % 2 == 0 else (lambda e, fc=fc, s_=s_: e.activation(Wd.t[:, fc:fc + 2, :].rearrange("p a d -> p (a d)"), s_.t[:, 0:2 * D], AF.Identity)), reads=[s_.b], writes=[Wd.b])
                p.barrier()
            cn = make_norm(st, "f", nxt=0)
            g2l = [load_gate(st, "g2lat", l, 0, 1)] + ([load_gate(st, "g2ctx", l, 1, 1)] if not final else [])
            hx = [sbt(st, f"hx{i}", [128, D], F32) for i in range(4)]
            nTf = [sbt(st, f"nTf{i}", [128, 8, 256], BF16) for i in range(2)]
            actT = [sbt(st, f"actT{i}", [128, NFC, 256], BF16) for i in range(1)]
            sg = [sbt(st, f"sg{i}", [128, 256], F32) for i in range(2)]
            pgt = [pst(st, f"fpg{i}", [128, 256]) for i in range(2)]
            put = [pst(st, f"fpu{i}", [128, 256]) for i in range(2)]
            pdn = [pst(st, f"fpd{i}", [128, 512]) for i in range(2)]
            tm_ = [sbt(st, f"ftm{i}", [128, 512], F32) for i in range(2)]
            ho = [sbt(st, f"ho{i}", [128, D], F32) for i in range(2)]
            fss = [sbt(st, f"fss{i}", [128, 1], F32) for i in range(2)]
            fj = cn["junk"]
            fingt = None
            if final:
                fingt = sbt(st, "fing", [128, D], F32)
                p.dma(fingt.t[:], I["final_g"].partition_broadcast(128), writes=[fingt.b])
            cnt = {"gi": 0, "di": 0, "hi": 0}
            blks = [tiles[b0:b0 + 2] for b0 in range(0, len(tiles), 2)]

            def do_norm(bi):
                tl = blks[bi]
                nT = nTf[bi % 2]
                xts = []
                for tt, ti in enumerate(tl):
                    col = 0 if ti < 32 else 1
                    hxt = hx[cnt['hi'] % 4]
                    cnt['hi'] += 1
                    src_ = Hs[ti * 128:(ti + 1) * 128, :]
                    norm_tile(cn, src_, l, col, 1, lambda k, tt=tt, nT=nT: nT.t[:, k, tt * 128:(tt + 1) * 128], nT.b, xt_fixed=hxt)
                    xts.append(hxt)

                return xts

            def gate_up(bi):
                nT = nTf[bi % 2]
                aT = actT[0]
                for fc in range(NFC):
                    a = pgt[cnt['gi'] % 2]
                    b = put[cnt['gi'] % 2]
                    s2 = sg[cnt['gi'] % 2]
                    cnt['gi'] += 1
                    for k in range(8):
                        p.op('pe', lambda e, a=a, k=k, fc=fc, nT=nT: e.matmul(a.t[:, :], Wg.t[:, k, fc * 128:(fc + 1) * 128], nT.t[:, k, :], start=(k == 0), stop=(k == 7)),
                             reads=[Wg.b, nT.b], writes=[a.b])
                    for k in range(8):
                        p.op('pe', lambda e, b=b, k=k, fc=fc, nT=nT: e.matmul(b.t[:, :], Wu.t[:, k, fc * 128:(fc + 1) * 128], nT.t[:, k, :], start=(k == 0), stop=(k == 7)),
                             reads=[Wu.b, nT.b], writes=[b.b])
                    p.op('act', lambda e, a=a, s2=s2: e.activation(s2.t[:, :], a.t[:, :], AF.Silu), reads=[a.b], writes=[s2.b])
                    p.op('dve', lambda e, b=b, s2=s2, aT=aT, fc=fc: e.tensor_mul(aT.t[:, fc, :], s2.t[:, :], b.t[:, :]), reads=[b.b, s2.b], writes=[aT.b])


            def down(bi, xts):
                tl = blks[bi]
                aT = actT[0]
                for tt, ti in enumerate(tl):
                    col = 0 if ti < 32 else 1
                    hxt = xts[tt]
                    h_ = ho[ti % 2]
                    for half in range(2):
                        po = pdn[cnt['di'] % 2]
                        tm = tm_[cnt['di'] % 2]
                        cnt['di'] += 1
                        for fc in range(NFC):
                            p.op('pe', lambda e, po=po, fc=fc, tt=tt, half=half, aT=aT: e.matmul(po.t[:, :], aT.t[:, fc, tt * 128:(tt + 1) * 128], Wd.t[:, fc, half * 512:(half + 1) * 512],
                                                                                         start=(fc == 0), stop=(fc == NFC - 1)), reads=[aT.b, Wd.b], writes=[po.b])
                        p.op('dve', lambda e, po=po, tm=tm, half=half, col=col: e.tensor_mul(tm.t[:, :], po.t[:, :], g2l[col].t[:, half * 512:(half + 1) * 512]),
                             reads=[po.b, g2l[col].b], writes=[tm.b])
                        p.op('pool', lambda e, tm=tm, hxt=hxt, h_=h_, half=half: e.tensor_add(h_.t[:, half * 512:(half + 1) * 512], hxt.t[:, half * 512:(half + 1) * 512], tm.t[:, :]),
                             reads=[tm.b, hxt.b], writes=[h_.b])
                    if not final:
                        p.dma(Hs[ti * 128:(ti + 1) * 128, :], h_.t[:], reads=[h_.b])
                    else:
                        ss = fss[ti % 2]
                        p.op('act', lambda e, h_=h_, ss=ss: e.activation(fj.t[:], h_.t[:], AF.Square, accum_out=ss.t[:]), reads=[h_.b], writes=[fj.b, ss.b])
                        p.op('dve', lambda e, ss=ss: e.tensor_scalar(ss.t[:], ss.t[:], 1.0 / D, EPS, ALU.mult, ALU.add), reads=[ss.b], writes=[ss.b])
                        p.op('act', lambda e, ss=ss: e.activation(ss.t[:], ss.t[:], AF.Sqrt), reads=[ss.b], writes=[ss.b])
                        p.op('dve', lambda e, ss=ss: e.reciprocal(ss.t[:], ss.t[:]), reads=[ss.b], writes=[ss.b])
                        p.op('dve', lambda e, h_=h_, ss=ss: e.scalar_tensor_tensor(h_.t[:], h_.t[:], ss.t[:, 0:1], fingt.t[:], ALU.mult, ALU.mult),
                             reads=[h_.b, ss.b, fingt.b], writes=[h_.b])
                        p.dma(out[ti * 128:(ti + 1) * 128, :], h_.t[:], reads=[h_.b])

            xts_next = do_norm(0)
            for bi in range(len(blks)):
                xts_cur = xts_next
                gate_up(bi)
                if bi + 1 < len(blks):
                    xts_next = do_norm(bi + 1)
                down(bi, xts_cur)
            p.barrier()

    ffn(0, list(range(34)), False)

    if stage == 2:
        with ExitStack() as st:
            bb_ = [sbt(st, f"bo{i}", [128, D], F32) for i in range(2)]
            for ti in range(32):
                b_ = bb_[ti % 2]
                p.dma(b_.t[:], Hs[ti * 128:(ti + 1) * 128, :], writes=[b_.b])
                p.dma(out[ti * 128:(ti + 1) * 128, :], b_.t[:], reads=[b_.b])
        p.emit()
        top.close()
        return nc

    s5_layer(nc, p, I, K, Hs, AB, load_gate, identf, identb, sbt, pst, ew, make_norm, norm_tile)
    if stage == 3:
        with ExitStack() as st:
            bb_ = [sbt(st, f"bo3{i}", [128, D], F32) for i in range(2)]
            for ti in range(32):
                b_ = bb_[ti % 2]
                p.dma(b_.t[:], Hs[ti * 128:(ti + 1) * 128, :], writes=[b_.b])
                p.dma(out[ti * 128:(ti + 1) * 128, :], b_.t[:], reads=[b_.b])
        p.emit()
        top.close()
        return nc
    ffn(1, list(range(16)), True)
    p.emit()
    top.close()
    return nc


def rev(ap):
    apl = [list(a) for a in ap.ap]
    n = apl[-1][1]
    stp = apl[-1][0]
    apl[-1][0] = -stp
    return bass.AP(ap.tensor, ap.offset + (n - 1) * stp, apl)


def bcast_last(ap, n):
    apl = [list(a) for a in ap.ap] + [[0, n]]
    return bass.AP(ap.tensor, ap.offset, apl)


def s5_layer(nc, p, I, K, Hs, AB, load_gate, identf, identb, sbt, pst, ew, make_norm, norm_tile, dbg=None):
    Hloc = Hs
    YGs = nc.dram_tensor("YGs", [8, 128, T // 2], BF16).ap()
    NTOK = T + TC
    with ExitStack() as S:
        nT = sbt(S, "s5nT", [128, 8, NTOK], BF16)
        with ExitStack() as st:
            cn = make_norm(st, "s5n", nxt=3)
            for ti in range(34):
                col = 0 if ti < 32 else 1
                norm_tile(cn, Hs[ti * 128:(ti + 1) * 128, :], 1, col, 0, lambda k, ti=ti: nT.t[:, k, ti * 128:(ti + 1) * 128], nT.b)
            p.barrier()
        PRM = sbt(S, "s5prm", [128, 3, 64], F32)
        Cw = sbt(S, "s5Cw", [128, 64, 2, 32], F32)
        Bz = sbt(S, "s5Bz", [128, 2, 64, 2, 16], F32)
        LC = 4
        NCM = 512 // LC
        PW = sbt(S, "s5PW", [128, LC + 1, 2, 64], F32)
        PRM4 = sbt(S, "s5prm4", [128, 3, 64], F32)
        dT = sbt(S, "s5dT", [128, 8], F32)
        with ExitStack() as st:
            pt = pst(st, "s5pt", [128, 128])
            def V(nm):
                return sbt(st, "s5v_" + nm, [128, 64], F32)
            arow = sbt(st, "s5arow", [64, 2, 128], F32)
            p.dma(arow.t[:, 0, :], I["ssm_a_re"].rearrange("(dq g) p -> dq (g p)", g=2), writes=[arow.b])
            p.dma(arow.t[:, 1, :], I["ssm_a_im"].rearrange("(dq g) p -> dq (g p)", g=2), writes=[arow.b])
            are, aim = V("are"), V("aim")
            for i_, dst in enumerate((are, aim)):
                p.op('pe', lambda e, i_=i_: e.transpose(pt.t[:, 0:64], arow.t[:, i_, :], identf.t[0:64, 0:64]), reads=[arow.b, identf.b], writes=[pt.b])
                p.op('dve', lambda e, dst=dst: e.tensor_copy(dst.t[:], pt.t[:, 0:64]), reads=[pt.b], writes=[dst.b])
            drow = sbt(st, "s5drow", [8, 128], F32)
            p.dma(drow.t[:], I["ssm_d"], writes=[drow.b])
            p.op('pe', lambda e: e.transpose(pt.t[:, 0:8], drow.t[:, :], identf.t[0:8, 0:8]), reads=[drow.b, identf.b], writes=[pt.b])
            p.op('dve', lambda e: e.tensor_copy(dT.t[:], pt.t[:, 0:8]), reads=[pt.b], writes=[dT.b])
            ldb = sbt(st, "s5ldb", [128, 128], F32)
            p.dma(ldb.t[:], I["ssm_log_dt"].partition_broadcast(128), writes=[ldb.b])
            dt = V("dt")
            for g2 in range(2):
                p.op('dve', lambda e, g2=g2: e.tensor_copy(dt.t[64 * g2:64 * g2 + 64, :], ldb.t[64 * g2:64 * g2 + 64, g2:128:2]), reads=[ldb.b], writes=[dt.b])
            p.op('act', lambda e: e.activation(dt.t[:], dt.t[:], AF.Exp), reads=[dt.b], writes=[dt.b])
            xr, th, mag = V("xr"), V("th"), V("mag")
            p.op('dve', lambda e: e.tensor_mul(xr.t[:], are.t[:], dt.t[:]), reads=[are.b, dt.b], writes=[xr.b])
            p.op('dve', lambda e: e.tensor_mul(th.t[:], aim.t[:], dt.t[:]), reads=[aim.b, dt.b], writes=[th.b])
            p.op('act', lambda e: e.activation(mag.t[:], xr.t[:], AF.Exp), reads=[xr.b], writes=[mag.b])
            kf = V("kf")
            ki = sbt(st, "s5ki", [128, 64], mybir.dt.int32)
            p.op('dve', lambda e: e.tensor_scalar(kf.t[:], th.t[:], 1.0 / (2 * math.pi), None, ALU.mult), reads=[th.b], writes=[kf.b])
            p.op('dve', lambda e: e.tensor_copy(ki.t[:], kf.t[:]), reads=[kf.b], writes=[ki.b])
            p.op('dve', lambda e: e.tensor_copy(kf.t[:], ki.t[:]), reads=[ki.b], writes=[kf.b])
            C1 = 6.28125
            C2 = 2 * math.pi - 6.28125
            thm = V("thm")
            p.op('dve', lambda e: e.scalar_tensor_tensor(thm.t[:], kf.t[:], -C1, th.t[:], ALU.mult, ALU.add), reads=[kf.b, th.b], writes=[thm.b])
            p.op('dve', lambda e: e.scalar_tensor_tensor(thm.t[:], kf.t[:], -C2, thm.t[:], ALU.mult, ALU.add), reads=[kf.b, thm.b], writes=[thm.b])
            xq, u2, qs, qc = V("xq"), V("u2"), V("qs"), V("qc")
            p.op('dve', lambda e: e.tensor_scalar(xq.t[:], thm.t[:], 0.25, None, ALU.mult), reads=[thm.b], writes=[xq.b])
            p.op('dve', lambda e: e.tensor_mul(u2.t[:], xq.t[:], xq.t[:]), reads=[xq.b], writes=[u2.b])
            sc_ = [(-1.0) ** k / math.factorial(2 * k + 1) for k in range(9)]
            cc_ = [(-1.0) ** k / math.factorial(2 * k) for k in range(9)]
            p.op('dve', lambda e: e.tensor_scalar(qs.t[:], u2.t[:], sc_[8], None, ALU.mult), reads=[u2.b], writes=[qs.b])
            p.op('dve', lambda e: e.tensor_scalar(qc.t[:], u2.t[:], cc_[8], None, ALU.mult), reads=[u2.b], writes=[qc.b])
            for k in range(7, 0, -1):
                p.op('dve', lambda e, k=k: e.scalar_tensor_tensor(qs.t[:], qs.t[:], sc_[k], u2.t[:], ALU.add, ALU.mult), reads=[qs.b, u2.b], writes=[qs.b])
                p.op('dve', lambda e, k=k: e.scalar_tensor_tensor(qc.t[:], qc.t[:], cc_[k], u2.t[:], ALU.add, ALU.mult), reads=[qc.b, u2.b], writes=[qc.b])
            sn, cs = V("sn"), V("cs")
            p.op('dve', lambda e: e.scalar_tensor_tensor(sn.t[:], qs.t[:], 1.0, xq.t[:], ALU.add, ALU.mult), reads=[qs.b, xq.b], writes=[sn.b])
            p.op('dve', lambda e: e.tensor_scalar(cs.t[:], qc.t[:], 1.0, None, ALU.add), reads=[qc.b], writes=[cs.b])
            ta, tb_ = V("ta"), V("tb")
            for _ in range(2):
                p.op('dve', lambda e: e.tensor_mul(ta.t[:], cs.t[:], cs.t[:]), reads=[cs.b], writes=[ta.b])
                p.op('dve', lambda e: e.tensor_mul(tb_.t[:], sn.t[:], sn.t[:]), reads=[sn.b], writes=[tb_.b])
                p.op('dve', lambda e: e.scalar_tensor_tensor(sn.t[:], cs.t[:], 2.0, sn.t[:], ALU.mult, ALU.mult), reads=[cs.b, sn.b], writes=[sn.b])
                p.op('dve', lambda e: e.tensor_sub(cs.t[:], ta.t[:], tb_.t[:]), reads=[ta.b, tb_.b], writes=[cs.b])
            p.op('dve', lambda e: e.tensor_copy(PRM.t[:, 0, :], mag.t[:]), reads=[mag.b], writes=[PRM.b])
            p.op('dve', lambda e: e.tensor_copy(PRM.t[:, 1, :], cs.t[:]), reads=[cs.b], writes=[PRM.b])
            p.op('dve', lambda e: e.tensor_copy(PRM.t[:, 2, :], sn.t[:]), reads=[sn.b], writes=[PRM.b])
            p.op('pool', lambda e: e.memset(PW.t[:, 0, 0, :], 1.0), writes=[PW.b])
            p.op('pool', lambda e: e.memset(PW.t[:, 0, 1, :], 0.0), writes=[PW.b])
            p.op('dve', lambda e: e.tensor_mul(PW.t[:, 1, 0, :], mag.t[:], cs.t[:]), reads=[mag.b, cs.b], writes=[PW.b])
            p.op('dve', lambda e: e.tensor_mul(PW.t[:, 1, 1, :], mag.t[:], sn.t[:]), reads=[mag.b, sn.b], writes=[PW.b])
            for k in range(2, LC + 1):
                p.op('dve', lambda e, k=k: e.tensor_mul(ta.t[:], PW.t[:, k - 1, 0, :], PW.t[:, 1, 0, :]), reads=[PW.b], writes=[ta.b])
                p.op('dve', lambda e, k=k: e.tensor_mul(tb_.t[:], PW.t[:, k - 1, 1, :], PW.t[:, 1, 1, :]), reads=[PW.b], writes=[tb_.b])
                p.op('dve', lambda e, k=k: e.tensor_sub(PW.t[:, k, 0, :], ta.t[:], tb_.t[:]), reads=[ta.b, tb_.b], writes=[PW.b])
                p.op('dve', lambda e, k=k: e.tensor_mul(ta.t[:], PW.t[:, k - 1, 0, :], PW.t[:, 1, 1, :]), reads=[PW.b], writes=[ta.b])
                p.op('dve', lambda e, k=k: e.tensor_mul(tb_.t[:], PW.t[:, k - 1, 1, :], PW.t[:, 1, 0, :]), reads=[PW.b], writes=[tb_.b])
                p.op('dve', lambda e, k=k: e.tensor_add(PW.t[:, k, 1, :], ta.t[:], tb_.t[:]), reads=[ta.b, tb_.b], writes=[PW.b])
            c4, s4, r4 = V("c4"), V("s4"), V("r4")
            p.op('dve', lambda e: e.tensor_copy(c4.t[:], cs.t[:]), reads=[cs.b], writes=[c4.b])
            p.op('dve', lambda e: e.tensor_copy(s4.t[:], sn.t[:]), reads=[sn.b], writes=[s4.b])
            p.op('dve', lambda e: e.tensor_copy(r4.t[:], mag.t[:]), reads=[mag.b], writes=[r4.b])
            for _ in range(int(round(math.log2(LC)))):
                p.op('dve', lambda e: e.tensor_mul(ta.t[:], c4.t[:], c4.t[:]), reads=[c4.b], writes=[ta.b])
                p.op('dve', lambda e: e.tensor_mul(tb_.t[:], s4.t[:], s4.t[:]), reads=[s4.b], writes=[tb_.b])
                p.op('dve', lambda e: e.scalar_tensor_tensor(s4.t[:], c4.t[:], 2.0, s4.t[:], ALU.mult, ALU.mult), reads=[c4.b, s4.b], writes=[s4.b])
                p.op('dve', lambda e: e.tensor_sub(c4.t[:], ta.t[:], tb_.t[:]), reads=[ta.b, tb_.b], writes=[c4.b])
                p.op('dve', lambda e: e.tensor_mul(r4.t[:], r4.t[:], r4.t[:]), reads=[r4.b], writes=[r4.b])
            p.op('dve', lambda e: e.tensor_copy(PRM4.t[:, 0, :], r4.t[:]), reads=[r4.b], writes=[PRM4.b])
            p.op('dve', lambda e: e.tensor_copy(PRM4.t[:, 1, :], c4.t[:]), reads=[c4.b], writes=[PRM4.b])
            p.op('dve', lambda e: e.tensor_copy(PRM4.t[:, 2, :], s4.t[:]), reads=[s4.b], writes=[PRM4.b])
            nr, ni, den, fre, fim = V("nr"), V("ni"), V("den"), V("fre"), V("fim")
            p.op('dve', lambda e: e.tensor_mul(nr.t[:], mag.t[:], cs.t[:]), reads=[mag.b, cs.b], writes=[nr.b])
            p.op('dve', lambda e: e.tensor_scalar(nr.t[:], nr.t[:], -1.0, None, ALU.add), reads=[nr.b], writes=[nr.b])
            p.op('dve', lambda e: e.tensor_mul(ni.t[:], mag.t[:], sn.t[:]), reads=[mag.b, sn.b], writes=[ni.b])
            p.op('dve', lambda e: e.tensor_mul(den.t[:], are.t[:], are.t[:]), reads=[are.b], writes=[den.b])
            p.op('dve', lambda e: e.tensor_mul(ta.t[:], aim.t[:], aim.t[:]), reads=[aim.b], writes=[ta.b])
            p.op('dve', lambda e: e.tensor_add(den.t[:], den.t[:], ta.t[:]), reads=[den.b, ta.b], writes=[den.b])
            p.op('dve', lambda e: e.reciprocal(den.t[:], den.t[:]), reads=[den.b], writes=[den.b])
            p.op('dve', lambda e: e.tensor_mul(ta.t[:], nr.t[:], are.t[:]), reads=[nr.b, are.b], writes=[ta.b])
            p.op('dve', lambda e: e.tensor_mul(tb_.t[:], ni.t[:], aim.t[:]), reads=[ni.b, aim.b], writes=[tb_.b])
            p.op('dve', lambda e: e.tensor_add(fre.t[:], ta.t[:], tb_.t[:]), reads=[ta.b, tb_.b], writes=[fre.b])
            p.op('dve', lambda e: e.tensor_mul(fre.t[:], fre.t[:], den.t[:]), reads=[fre.b, den.b], writes=[fre.b])
            p.op('dve', lambda e: e.tensor_mul(ta.t[:], ni.t[:], are.t[:]), reads=[ni.b, are.b], writes=[ta.b])
            p.op('dve', lambda e: e.tensor_mul(tb_.t[:], nr.t[:], aim.t[:]), reads=[nr.b, aim.b], writes=[tb_.b])
            p.op('dve', lambda e: e.tensor_sub(fim.t[:], ta.t[:], tb_.t[:]), reads=[ta.b, tb_.b], writes=[fim.b])
            p.op('dve', lambda e: e.tensor_mul(fim.t[:], fim.t[:], den.t[:]), reads=[fim.b, den.b], writes=[fim.b])
            Br = [sbt(st, f"s5Br{i}", [128, 64, 16], F32) for i in range(2)]
            for i_, nm in enumerate(("ssm_b_re", "ssm_b_im")):
                src = I[nm].rearrange("d (q g) p h -> g p (d q) h", g=2)
                for g2 in range(2):
                    p.dma(Br[i_].t[64 * g2:64 * g2 + 64, :, :], src[g2], writes=[Br[i_].b])
            p.op('pool', lambda e: e.memset(Bz.t[:].rearrange("p a b c d -> p (a b c d)"), 0.0), writes=[Bz.b])
            m1 = sbt(st, "s5m1", [128, 64, 16], F32)
            m2 = sbt(st, "s5m2", [128, 64, 16], F32)
            fre_b = bcast_last(fre.t[:], 16)
            fim_b = bcast_last(fim.t[:], 16)
            p.op('dve', lambda e: e.tensor_mul(m1.t[:], Br[0].t[:], fre_b), reads=[Br[0].b, fre.b], writes=[m1.b])
            p.op('dve', lambda e: e.tensor_mul(m2.t[:], Br[1].t[:], fim_b), reads=[Br[1].b, fim.b], writes=[m2.b])
            for g2 in range(2):
                p.op('dve', lambda e, g2=g2: e.tensor_sub(Bz.t[64 * g2:64 * g2 + 64, 0, :, g2, :], m1.t[64 * g2:64 * g2 + 64], m2.t[64 * g2:64 * g2 + 64]),
                     reads=[m1.b, m2.b], writes=[Bz.b])
            p.op('dve', lambda e: e.tensor_mul(m1.t[:], Br[1].t[:], fre_b), reads=[Br[1].b, fre.b], writes=[m1.b])
            p.op('dve', lambda e: e.tensor_mul(m2.t[:], Br[0].t[:], fim_b), reads=[Br[0].b, fim.b], writes=[m2.b])
            for g2 in range(2):
                p.op('dve', lambda e, g2=g2: e.tensor_add(Bz.t[64 * g2:64 * g2 + 64, 1, :, g2, :], m1.t[64 * g2:64 * g2 + 64], m2.t[64 * g2:64 * g2 + 64]),
                     reads=[m1.b, m2.b], writes=[Bz.b])
            pts = [pst(st, f"s5ptb{i}", [128, 128]) for i in range(2)]
            n_ = 0
            p.op('pool', lambda e: e.memset(Cw.t[:].rearrange("p a b c -> p (a b c)"), 0.0), writes=[Cw.b])
            Cn = [sbt(st, f"s5Cn{i}", [128, 2, 64], F32) for i in range(2)]
            for ri, nm in enumerate(("ssm_c_re", "ssm_c_im")):
                for dQ in range(16):
                    d_, Q_ = dQ // 8, dQ % 8
                    cnb = Cn[n_ % 2]
                    pp = pts[n_ % 2]
                    n_ += 1
                    src = I[nm][d_, Q_ * 4 * 2:(Q_ * 4 + 4) * 2].rearrange("g h p -> (g h) p")
                    p.dma(cnb.t[:, 0, :], src, writes=[cnb.b])
                    p.dma(cnb.t[:, 1, :], src, writes=[cnb.b])
                    p.op('pe', lambda e, pp=pp, cnb=cnb: e.transpose(pp.t[:, :], cnb.t[:, :, :].rearrange("p a b -> p (a b)"), identf.t[:, :]),
                         reads=[cnb.b, identf.b], writes=[pp.b])
                    sgn = 1.0 if ri == 0 else -1.0
                    for g2 in range(2):
                        p.op('act', lambda e, pp=pp, g2=g2, ri=ri, dQ=dQ, sgn=sgn: e.activation(
                            Cw.t[64 * g2:64 * g2 + 64, dQ * 4:(dQ + 1) * 4, ri, 16 * g2:16 * g2 + 16],
                            pp.t[64 * g2:64 * g2 + 64, :].rearrange("p (q g h) -> p q g h", q=4, g=2)[:, :, g2, :], AF.Identity, scale=sgn),
                            reads=[pp.b], writes=[Cw.b])
            p.barrier()

        if dbg is not None:
            dbg(PRM, Cw, nT)
            return

        with ExitStack() as st:
            yacc = sbt(st, "s5yacc", [128, T // 2], F32)
            Ec = sbt(st, "s5Ec", [128, 8, NCM], F32)
            Es = sbt(st, "s5Es", [128, 8, NCM], F32)
            wc = sbt(st, "s5wc", [128, 8], F32)
            ws = sbt(st, "s5ws", [128, 8], F32)
            wt1 = sbt(st, "s5wt1", [128, 8], F32)
            wt2 = sbt(st, "s5wt2", [128, 8], F32)
            et1 = sbt(st, "s5et1", [128, 8, NCM // 2], F32)
            et2 = sbt(st, "s5et2", [128, 8, NCM // 2], F32)
            hp = sbt(st, "s5hp", [128, 8, 2], F32)
            ini = sbt(st, "s5ini", [128, 8, 2], F32)
            itmp = sbt(st, "s5itmp", [128, 8, 2], F32)
            nsth = sbt(st, "s5nsth", [128, 64], F32)
            p.op('dve', lambda e: e.tensor_scalar(nsth.t[:], PRM4.t[:, 2, :], -1.0, None, ALU.mult), reads=[PRM4.b], writes=[nsth.b])
            hpb = [Buf() for _ in range(8)]
            CwP = sbt(st, "s5CwP", [128, LC + 1, 8, 2, 32], F32)
            ctm = [sbt(st, f"s5ctm{i}", [128, 4, 32], F32) for i in range(2)]
            BP = sbt(st, "s5BP", [128, LC, 2, 2, 4, 32], F32)
            Win = sbt(st, "s5Win", [128, 2, 2, LC, 128], BF16)
            Mw = sbt(st, "s5Mw", [128, 2, LC, 32], BF16)
            Wout = sbt(st, "s5Wout", [128, 8, LC, 2, 32], BF16)
            pts = [pst(st, f"s5ptq{i}", [128, 128]) for i in range(1)]
            pk = pst(st, "s5pk", [128, 16, 32])
            NS = 8
            Wk_ = [[sbt(st, f"s5w{j}_{i}", [128, 2, NCM], F32) for i in range(3)] for j in range(NS)]
            Hb = [sbt(st, f"s5hb{j}", [128, 2, NCM + 4], BF16) for j in range(NS)]
            Esg = sbt(st, "s5Esg", [128, 8, 2, NCM], F32)

            def swap2(tb_, ncn):
                b1 = tb_.t[:, 1, 0:ncn]
                apl = [list(a_) for a_ in b1.ap]
                return bass.AP(b1.tensor, b1.offset, [apl[0], [-NCM, 2], apl[-1]])

            def bc2(ap2):
                apl = [list(a_) for a_ in ap2.ap]
                return bass.AP(ap2.tensor, ap2.offset, [apl[0], [0, 2], apl[-1]])
            pS = [pst(st, f"s5pS{j}", [128, 512]) for j in range(4)]
            py = [pst(st, f"s5py{i}", [128, LC, NCM]) for i in range(2)]
            ygb = [sbt(st, f"s5yg{i}", [128, 512], BF16) for i in range(2)]
            gtb = [sbt(st, f"s5gt{i}", [128, 512], F32) for i in range(2)]
            tn_ = 0
            blkc = 0
            yi_ = 0
            psi = 0
            for Q in range(8):
                for d_ in range(2):
                    us = slice(d_ * 32 + Q * 4, d_ * 32 + Q * 4 + 4)
                    ts_ = slice(d_ * 4, d_ * 4 + 4)
                    c0_ = Cw.t[:, us, 0, :]
                    c1_ = Cw.t[:, us, 1, :]
                    for k in range(LC + 1):
                        pr = bcast_last(PW.t[:, k, 0, us], 32)
                        pi_ = bcast_last(PW.t[:, k, 1, us], 32)
                        t1_, t2_ = ctm
                        p.op('dve', lambda e, t1_=t1_, c0_=c0_, pr=pr: e.tensor_mul(t1_.t[:], c0_, pr), reads=[Cw.b, PW.b], writes=[t1_.b])
                        p.op('pool', lambda e, t2_=t2_, c1_=c1_, pi_=pi_: e.tensor_mul(t2_.t[:], c1_, pi_), reads=[Cw.b, PW.b], writes=[t2_.b])
                        p.op('dve', lambda e, t1_=t1_, t2_=t2_, k=k, ts_=ts_: e.tensor_add(CwP.t[:, k, ts_, 0, :], t1_.t[:], t2_.t[:]), reads=[t1_.b, t2_.b], writes=[CwP.b])
                        p.op('dve', lambda e, t1_=t1_, c1_=c1_, pr=pr: e.tensor_mul(t1_.t[:], c1_, pr), reads=[Cw.b, PW.b], writes=[t1_.b])
                        p.op('pool', lambda e, t2_=t2_, c0_=c0_, pi_=pi_: e.tensor_mul(t2_.t[:], c0_, pi_), reads=[Cw.b, PW.b], writes=[t2_.b])
                        p.op('dve', lambda e, t1_=t1_, t2_=t2_, k=k, ts_=ts_: e.tensor_sub(CwP.t[:, k, ts_, 1, :], t1_.t[:], t2_.t[:]), reads=[t1_.b, t2_.b], writes=[CwP.b])
                    b0_ = Bz.t[:, 0, us, :, :].rearrange("p a b c -> p a (b c)")
                    b1_ = Bz.t[:, 1, us, :, :].rearrange("p a b c -> p a (b c)")
                    for s_ in range(LC):
                        k = LC - 1 - s_
                        pr = bcast_last(PW.t[:, k, 0, us], 32)
                        pi_ = bcast_last(PW.t[:, k, 1, us], 32)
                        t1_, t2_ = ctm
                        p.op('dve', lambda e, t1_=t1_, b0_=b0_, pr=pr: e.tensor_mul(t1_.t[:], b0_, pr), reads=[Bz.b, PW.b], writes=[t1_.b])
                        p.op('pool', lambda e, t2_=t2_, b1_=b1_, pi_=pi_: e.tensor_mul(t2_.t[:], b1_, pi_), reads=[Bz.b, PW.b], writes=[t2_.b])
                        p.op('dve', lambda e, t1_=t1_, t2_=t2_, s_=s_, d_=d_: e.tensor_sub(BP.t[:, s_, 0, d_, :, :], t1_.t[:], t2_.t[:]), reads=[t1_.b, t2_.b], writes=[BP.b])
                        p.op('dve', lambda e, t1_=t1_, b0_=b0_, pi_=pi_: e.tensor_mul(t1_.t[:], b0_, pi_), reads=[Bz.b, PW.b], writes=[t1_.b])
                        p.op('pool', lambda e, t2_=t2_, b1_=b1_, pr=pr: e.tensor_mul(t2_.t[:], b1_, pr), reads=[Bz.b, PW.b], writes=[t2_.b])
                        p.op('dve', lambda e, t1_=t1_, t2_=t2_, s_=s_, d_=d_: e.tensor_add(BP.t[:, s_, 1, d_, :, :], t1_.t[:], t2_.t[:]), reads=[t1_.b, t2_.b], writes=[BP.b])
                p.op('act', lambda e: e.activation(Wout.t[:].rearrange("p x t r h -> p t x r h"), CwP.t[:, 1:LC + 1, :, :, :], AF.Identity), reads=[CwP.b], writes=[Wout.b])
                for d_ in range(2):
                    for ri in range(2):
                        for s_ in range(LC):
                            pp = pts[0]
                            tn_ += 1
                            p.op('pe', lambda e, pp=pp, s_=s_, ri=ri, d_=d_: e.transpose(pp.t[:, :], BP.t[:, s_, ri, d_, :, :].rearrange("p a b -> p (a b)"), identf.t[:, :]),
                                 reads=[BP.b, identf.b], writes=[pp.b])
                            p.op('act', lambda e, pp=pp, s_=s_, ri=ri, d_=d_: e.activation(Win.t[:, d_, ri, s_, :], pp.t[:, :], AF.Identity), reads=[pp.b], writes=[Win.b])
                    for thf in range(LC // 4):
                        for tq in range(4):
                            tau = thf * 4 + tq
                            for ql in range(4):
                                tix = d_ * 4 + ql
                                for ri in range(2):
                                    p.op('pe', lambda e, d_=d_, tau=tau, tq=tq, ql=ql, tix=tix, ri=ri, us=slice(d_ * 32 + Q * 4, d_ * 32 + Q * 4 + 4): e.matmul(
                                        pk.t[:, tq * 4 + ql, :], Bz.t[:, ri, us, :, :].rearrange("p q a b -> p (q a b)"), CwP.t[:, tau, tix, ri, :],
                                        start=(ri == 0), stop=(ri == 1)), reads=[Bz.b, CwP.b], writes=[pk.b])
                        for ql in range(4):
                            p.op('dve', lambda e, d_=d_, ql=ql, thf=thf: e.tensor_copy(Mw.t[32 * ql:32 * ql + 32, d_, thf * 4:thf * 4 + 4, :], pk.t[32 * ql:32 * ql + 32, ql:16:4, :]),
                                 reads=[pk.b], writes=[Mw.b])
                for d_ in range(2):
                    cols = slice(d_ * 32 + Q * 4, d_ * 32 + Q * 4 + 4)
                    p.op('dve', lambda e, d_=d_, cols=cols: e.tensor_copy(wc.t[:, d_ * 4:d_ * 4 + 4], PRM4.t[:, 1, cols]), reads=[PRM4.b], writes=[wc.b])
                    p.op('dve', lambda e, d_=d_, cols=cols: e.tensor_copy(ws.t[:, d_ * 4:d_ * 4 + 4], PRM4.t[:, 2, cols]), reads=[PRM4.b], writes=[ws.b])
                p.op('pool', lambda e: e.memset(Ec.t[:, :, 0:1], 1.0), writes=[Ec.b])
                p.op('pool', lambda e: e.memset(Es.t[:, :, 0:1], 0.0), writes=[Es.b])
                n = 1
                while n < NCM:
                    wcb = bcast_last(wc.t[:, :], n)
                    wsb = bcast_last(ws.t[:, :], n)
                    p.op('dve', lambda e, n=n, wcb=wcb: e.tensor_mul(et1.t[:, :, 0:n], Ec.t[:, :, 0:n], wcb), reads=[Ec.b, wc.b], writes=[et1.b])
                    p.op('pool', lambda e, n=n, wsb=wsb: e.tensor_mul(et2.t[:, :, 0:n], Es.t[:, :, 0:n], wsb), reads=[Es.b, ws.b], writes=[et2.b])
                    p.op('dve', lambda e, n=n: e.tensor_sub(Ec.t[:, :, n:2 * n], et1.t[:, :, 0:n], et2.t[:, :, 0:n]), reads=[et1.b, et2.b], writes=[Ec.b])
                    p.op('dve', lambda e, n=n, wcb=wcb: e.tensor_mul(et1.t[:, :, 0:n], Es.t[:, :, 0:n], wcb), reads=[Es.b, wc.b], writes=[et1.b])
                    p.op('pool', lambda e, n=n, wsb=wsb: e.tensor_mul(et2.t[:, :, 0:n], Ec.t[:, :, 0:n], wsb), reads=[Ec.b, ws.b], writes=[et2.b])
                    p.op('dve', lambda e, n=n: e.tensor_add(Es.t[:, :, n:2 * n], et1.t[:, :, 0:n], et2.t[:, :, 0:n]), reads=[et1.b, et2.b], writes=[Es.b])
                    p.op('dve', lambda e: e.tensor_mul(wt1.t[:], wc.t[:], wc.t[:]), reads=[wc.b], writes=[wt1.b])
                    p.op('dve', lambda e: e.tensor_mul(wt2.t[:], ws.t[:], ws.t[:]), reads=[ws.b], writes=[wt2.b])
                    p.op('dve', lambda e: e.scalar_tensor_tensor(ws.t[:], wc.t[:], 2.0, ws.t[:], ALU.mult, ALU.mult), reads=[wc.b, ws.b], writes=[ws.b])
                    p.op('dve', lambda e: e.tensor_sub(wc.t[:], wt1.t[:], wt2.t[:]), reads=[wt1.b, wt2.b], writes=[wc.b])
                    n *= 2
                p.op('dve', lambda e: e.tensor_copy(Esg.t[:, :, 0, :], Es.t[:, :, :]), reads=[Es.b], writes=[Esg.b])
                p.op('dve', lambda e: e.tensor_scalar(Esg.t[:, :, 1, :], Es.t[:, :, :], -1.0, None, ALU.mult), reads=[Es.b], writes=[Esg.b])
                blist = []
                for d_ in range(2):
                    if d_ == 0:
                        blocks = [(T, TC, False)] + [(b * 512, 512, True) for b in range(4)]
                    else:
                        blocks = [(T, TC, False)] + [(b * 512, 512, False) for b in (7, 6, 5, 4)] + [(b * 512, 512, True) for b in (3, 2, 1, 0)]
                    for bi_, (c0, n, islat) in enumerate(blocks):
                        blist.append((d_, bi_, c0, n, islat))
                bctx = {}

                def front_pe(k, Q=Q):
                    d_, bi_, c0, n, islat = blist[k]
                    gk = Q * 18 + k
                    ncn = n // LC
                    sets = [(gk % 2) * 4 + ql for ql in range(4)]
                    rhs_all = []
                    for ql in range(4):
                        base = nT.t[32 * ql:32 * ql + 32, Q, c0:c0 + n]
                        apl = [list(a_) for a_ in base.ap]
                        lst = []
                        for s_ in range(LC):
                            ap2 = [list(a_) for a_ in apl]
                            if d_ == 0:
                                ap2[-1] = [apl[-1][0] * LC, ncn]
                                lst.append(bass.AP(base.tensor, base.offset + s_ * apl[-1][0], ap2))
                            else:
                                ap2[-1] = [-apl[-1][0] * LC, ncn]
                                lst.append(bass.AP(base.tensor, base.offset + (n - 1 - s_) * apl[-1][0], ap2))
                        rhs_all.append(lst)
                    pss = [pS[ql] for ql in range(4)]
                    for ri in range(2):
                        for s_ in range(LC):
                            for ql in range(4):
                                ps_ = pss[ql]
                                p.op('pe', lambda e, ps_=ps_, ri=ri, s_=s_, ql=ql, d_=d_, ncn=ncn, r_=rhs_all[ql][s_]: e.matmul(
                                    ps_.t[:, ri * 128:ri * 128 + ncn], Win.t[32 * ql:32 * ql + 32, d_, ri, s_, :], r_, start=(s_ == 0), stop=(s_ == LC - 1), tile_position=(32 * ql, 0)),
                                    reads=[Win.b, nT.b], writes=[ps_.b])
                    bctx[k] = (ncn, sets, rhs_all)

                def front_ev(k, Q=Q):
                    d_, bi_, c0, n, islat = blist[k]
                    ncn, sets, rhs_all = bctx[k]
                    pss = [pS[ql] for ql in range(4)]
                    for ql in range(4):
                        ps_ = pss[ql]
                        X, P1, P2 = Wk_[sets[ql]]
                        p.op('act', lambda e, X=X, ps_=ps_, ncn=ncn: e.activation(X.t[:, :, 0:ncn], ps_.t[:, 0:256].rearrange("p (r c) -> p r c", r=2)[:, :, 0:ncn], AF.Identity),
                             reads=[ps_.b], writes=[X.b])
                    for ql in range(4):
                        X, P1, P2 = Wk_[sets[ql]]
                        tix = d_ * 4 + ql
                        ecb = bc2(Ec.t[:, tix, 0:ncn])
                        p.op('dve', lambda e, X=X, P1=P1, ecb=ecb, ncn=ncn: e.tensor_mul(P1.t[:, :, 0:ncn], X.t[:, :, 0:ncn], ecb), reads=[X.b, Ec.b], writes=[P1.b])
                        p.op('pool', lambda e, X=X, P2=P2, tix=tix, ncn=ncn: e.tensor_mul(P2.t[:, :, 0:ncn], swap2(X, ncn), Esg.t[:, tix, :, 0:ncn]), reads=[X.b, Esg.b], writes=[P2.b])
                    for ql in range(4):
                        X, P1, P2 = Wk_[sets[ql]]
                        p.op('dve', lambda e, P1=P1, P2=P2, ncn=ncn: e.tensor_add(P1.t[:, :, 0:ncn], P1.t[:, :, 0:ncn], P2.t[:, :, 0:ncn]), reads=[P1.b, P2.b], writes=[P1.b])

                def back(k, Q=Q):
                    d_, bi_, c0, n, islat = blist[k]
                    gk = Q * 18 + k
                    ncn, sets, rhs_all = bctx.pop(k)
                    pyt = py[gk % 2]
                    for ql in range(4):
                        tix = d_ * 4 + ql
                        u = d_ * 32 + Q * 4 + ql
                        hb_ = hpb[tix]
                        hbs = Hb[sets[ql]]
                        if bi_ > 0:
                            p.op('act', lambda e, tix=tix, u=u: e.activation(itmp.t[:, tix, 0:1], hp.t[:, tix, 1:2], AF.Identity, scale=nsth.t[:, u:u + 1]), reads=[hb_, nsth.b], writes=[hb_])
                            p.op('act', lambda e, tix=tix, u=u: e.activation(ini.t[:, tix, 0:1], hp.t[:, tix, 0:1], AF.Identity, scale=PRM4.t[:, 1, u:u + 1], bias=itmp.t[:, tix, 0:1]),
                                 reads=[hb_, PRM4.b], writes=[hb_])
                            p.op('act', lambda e, tix=tix, u=u: e.activation(itmp.t[:, tix, 1:2], hp.t[:, tix, 0:1], AF.Identity, scale=PRM4.t[:, 2, u:u + 1]), reads=[hb_, PRM4.b], writes=[hb_])
                            p.op('act', lambda e, tix=tix, u=u: e.activation(ini.t[:, tix, 1:2], hp.t[:, tix, 1:2], AF.Identity, scale=PRM4.t[:, 1, u:u + 1], bias=itmp.t[:, tix, 1:2]),
                                 reads=[hb_, PRM4.b], writes=[hb_])
                            if islat:
                                p.op('act', lambda e, hbs=hbs, tix=tix: e.activation(hbs.t[:, :, 0], hp.t[:, tix, :], AF.Identity), reads=[hb_], writes=[hbs.b])
                    for ql in range(4):
                        X, P1, P2 = Wk_[sets[ql]]
                        tix = d_ * 4 + ql
                        u = d_ * 32 + Q * 4 + ql
                        hb_ = hpb[tix]
                        rb = PRM4.t[:, 0, u:u + 1].to_broadcast([128, ncn])
                        for ri in range(2):
                            if bi_ == 0:
                                init_, irds = 0.0, []
                            else:
                                init_, irds = ini.t[:, tix, ri:ri + 1], [hb_]
                            p.op('dve', lambda e, X=X, P1=P1, rb=rb, init_=init_, ri=ri, ncn=ncn: e.tensor_tensor_scan(X.t[:, ri, 0:ncn], rb, P1.t[:, ri, 0:ncn], init_, ALU.mult, ALU.add),
                                 reads=[P1.b, PRM4.b] + irds, writes=[X.b])
                    for ql in range(4):
                        X, P1, P2 = Wk_[sets[ql]]
                        tix = d_ * 4 + ql
                        ecb = bc2(Ec.t[:, tix, 0:ncn])
                        p.op('dve', lambda e, X=X, P1=P1, ecb=ecb, ncn=ncn: e.tensor_mul(P1.t[:, :, 0:ncn], X.t[:, :, 0:ncn], ecb), reads=[X.b, Ec.b], writes=[P1.b])
                        p.op('pool', lambda e, X=X, P2=P2, tix=tix, ncn=ncn: e.tensor_mul(P2.t[:, :, 0:ncn], swap2(X, ncn), Esg.t[:, tix, :, 0:ncn]), reads=[X.b, Esg.b], writes=[P2.b])
                    for ql in range(4):
                        X, P1, P2 = Wk_[sets[ql]]
                        p.op('dve', lambda e, P1=P1, P2=P2, ncn=ncn: e.tensor_sub(P1.t[:, :, 0:ncn], P1.t[:, :, 0:ncn], P2.t[:, :, 0:ncn]), reads=[P1.b, P2.b], writes=[P1.b])
                    for ql in range(4):
                        X, P1, P2 = Wk_[sets[ql]]
                        tix = d_ * 4 + ql
                        hb_ = hpb[tix]
                        hbs = Hb[sets[ql]]
                        p.op('act', lambda e, tix=tix, P1=P1, ncn=ncn: e.activation(hp.t[:, tix, :], P1.t[:, :, ncn - 1], AF.Identity), reads=[P1.b], writes=[hb_])
                        if islat:
                            p.op('act', lambda e, hbs=hbs, P1=P1, ncn=ncn: e.activation(hbs.t[:, :, 1:ncn], P1.t[:, :, 0:ncn - 1], AF.Identity), reads=[P1.b], writes=[hbs.b])
                    if islat:
                        for t_ in range(LC):
                            for s_ in range(t_ + 1):
                                for ql in range(4):
                                    p.op('pe', lambda e, pyt=pyt, ql=ql, t_=t_, s_=s_, d_=d_, ncn=ncn, r_=rhs_all[ql][s_]: e.matmul(
                                        pyt.t[32 * ql:32 * ql + 32, t_, 0:ncn], Mw.t[32 * ql:32 * ql + 32, d_, t_ - s_, :], r_, start=(s_ == 0 and t_ == 0), stop=False,
                                        tile_position=(32 * ql, 32 * ql)), reads=[Mw.b, nT.b], writes=[pyt.b])
                        for t_ in range(LC):
                            for ri in range(2):
                                for ql in range(4):
                                    tix = d_ * 4 + ql
                                    hh_ = Hb[sets[ql]]
                                    p.op('pe', lambda e, pyt=pyt, ql=ql, t_=t_, ri=ri, tix=tix, hh_=hh_, ncn=ncn: e.matmul(
                                        pyt.t[32 * ql:32 * ql + 32, t_, 0:ncn], Wout.t[:, tix, t_, ri, :], hh_.t[:, ri, 0:ncn], start=False, stop=(ri == 1 and t_ == LC - 1),
                                        tile_position=(0, 32 * ql)), reads=[Wout.b, hh_.b], writes=[pyt.b])

                def yevac(k, Q=Q):
                    d_, bi_, c0, n, islat = blist[k]
                    if not islat:
                        return
                    gk = Q * 18 + k
                    ncn = n // LC
                    pyt = py[gk % 2]
                    if d_ == 0:
                        yv = yacc.t[:, c0:c0 + n].rearrange("p (c t) -> p t c", t=LC)
                        nv = nT.t[:, Q, c0:c0 + n].rearrange("p (c t) -> p t c", t=LC)
                        p.op('dve', lambda e, pyt=pyt, yv=yv, nv=nv, Q=Q: e.scalar_tensor_tensor(yv, nv, dT.t[:, Q:Q + 1], pyt.t[:, :, :], ALU.mult, ALU.add),
                             reads=[nT.b, dT.b, pyt.b], writes=[yacc.b])
                    else:
                        base = yacc.t[:, c0:c0 + n]
                        apl = [list(a_) for a_ in base.ap]
                        stp = apl[-1][0]
                        yv = bass.AP(base.tensor, base.offset + (n - 1) * stp, apl[:-1] + [[-stp, LC], [-stp * LC, ncn]])
                        p.op('dve', lambda e, pyt=pyt, yv=yv: e.tensor_add(yv, yv, pyt.t[:, :, :]), reads=[pyt.b, yacc.b], writes=[yacc.b])

                NB_ = len(blist)
                front_pe(0)
                front_ev(0)
                if NB_ > 1:
                    front_pe(1)
                for k in range(NB_):
                    if k + 1 < NB_:
                        front_ev(k + 1)
                    if k + 2 < NB_:
                        front_pe(k + 2)
                    back(k)
                    if k >= 1:
                        yevac(k - 1)
                yevac(NB_ - 1)
                for b in range(4):
                    yg = ygb[b % 2]
                    gt = gtb[b % 2]
                    ysl = yacc.t[:, b * 512:(b + 1) * 512]
                    p.op('dve', lambda e, gt=gt, ysl=ysl: e.tensor_mul(gt.t[:, :], ysl, ysl), reads=[yacc.b], writes=[gt.b])
                    p.op('dve', lambda e, gt=gt: e.tensor_scalar(gt.t[:, :], gt.t[:, :], 0.044715, 1.0, ALU.mult, ALU.add), reads=[gt.b], writes=[gt.b])
                    p.op('dve', lambda e, gt=gt, ysl=ysl: e.tensor_mul(gt.t[:, :], gt.t[:, :], ysl), reads=[gt.b, yacc.b], writes=[gt.b])
                    p.op('act', lambda e, gt=gt: e.activation(gt.t[:, :], gt.t[:, :], AF.Sigmoid, scale=2.0 * math.sqrt(2.0 / math.pi)), reads=[gt.b], writes=[gt.b])
                    p.op('dve', lambda e, gt=gt, ysl=ysl, yg=yg: e.tensor_mul(yg.t[:, :], gt.t[:, :], ysl), reads=[gt.b, yacc.b], writes=[yg.b])
                    p.dma(YGs[Q, :, b * 512:(b + 1) * 512], yg.t[:, :], reads=[yg.b])
            p.barrier()
    with ExitStack() as st:
        Wgl = sbt(st, "s5Wgl", [128, 8, 2 * D], BF16)
        with ExitStack() as st2:
            stg = [sbt(st2, f"s5gst{i}", [128, 2 * D], F32) for i in range(2)]
            for k in range(8):
                s_ = stg[k % 2]
                p.dma(s_.t[:, 0:D], I["ssm_glu_w"][k * 128:(k + 1) * 128, 0:D], writes=[s_.b])
                p.dma(s_.t[:, D:2 * D], I["ssm_glu_w"][k * 128:(k + 1) * 128, D:2 * D], writes=[s_.b])
                ew(lambda e, k=k, s_=s_: e.tensor_copy(Wgl.t[:, k, 0:D], s_.t[:, 0:D]), [s_.b], [Wgl.b])
                ew(lambda e, k=k, s_=s_: e.tensor_copy(Wgl.t[:, k, D:2 * D], s_.t[:, D:2 * D]), [s_.b], [Wgl.b])
            p.barrier()
        g1 = load_gate(st, "s5g1", 1, 0, 0)
        ygt = [sbt(st, f"s5ygt{i}", [128, 8, 128], BF16) for i in range(2)]
        h2 = [sbt(st, f"s5h2{i}", [128, D], F32) for i in range(2)]
        h3 = [sbt(st, f"s5h3{i}", [128, D], F32) for i in range(2)]
        sg_ = [sbt(st, f"s5sg{i}", [128, 512], F32) for i in range(2)]
        pa = [pst(st, f"s5pa{i}", [128, 512]) for i in range(2)]
        pgl = [pst(st, f"s5pg{i}", [128, 512]) for i in range(2)]
        c_ = 0
        YGs4 = YGs.rearrange("q p (r t) -> q p r t", r=2)
        for ti in range(16):
            yt = ygt[ti % 2]
            p.dma(yt.t[:], YGs[:, :, ti * 128:(ti + 1) * 128].rearrange("q p t -> p q t"), writes=[yt.b])
            hh = h2[ti % 2]
            ho_ = h3[ti % 2]
            p.dma(hh.t[:], Hloc[ti * 128:(ti + 1) * 128, :], writes=[hh.b])
            for half in range(2):
                a = pa[c_ % 2]
                g = pgl[c_ % 2]
                sg2 = sg_[c_ % 2]
                c_ += 1
                for k in range(8):
                    p.op('pe', lambda e, a=a, k=k, yt=yt, half=half: e.matmul(a.t[:, :], yt.t[:, k, :], Wgl.t[:, k, half * 512:(half + 1) * 512], start=(k == 0), stop=(k == 7)),
                         reads=[yt.b, Wgl.b], writes=[a.b])
                for k in range(8):
                    p.op('pe', lambda e, g=g, k=k, yt=yt, half=half: e.matmul(g.t[:, :], yt.t[:, k, :], Wgl.t[:, k, D + half * 512:D + (half + 1) * 512], start=(k == 0), stop=(k == 7)),
                         reads=[yt.b, Wgl.b], writes=[g.b])
                p.op('act', lambda e, g=g, sg2=sg2: e.activation(sg2.t[:, :], g.t[:, :], AF.Sigmoid), reads=[g.b], writes=[sg2.b])
                p.op('dve', lambda e, a=a, sg2=sg2: e.tensor_mul(sg2.t[:, :], sg2.t[:, :], a.t[:, :]), reads=[a.b, sg2.b], writes=[sg2.b])
                p.op('pool', lambda e, sg2=sg2, half=half: e.tensor_mul(sg2.t[:, :], sg2.t[:, :], g1.t[:, half * 512:(half + 1) * 512]), reads=[sg2.b, g1.b], writes=[sg2.b])
                p.op('pool', lambda e, sg2=sg2, hh=hh, ho_=ho_, half=half: e.tensor_add(ho_.t[:, half * 512:(half + 1) * 512], hh.t[:, half * 512:(half + 1) * 512], sg2.t[:, :]),
                     reads=[sg2.b, hh.b], writes=[ho_.b])
            p.dma(Hloc[ti * 128:(ti + 1) * 128, :], ho_.t[:], reads=[ho_.b])
        p.barrier()


_CACHE = {}


def kernel(**inputs):
    stage = int(inputs.pop("_stage", 99))
    if "nc" not in _CACHE or _CACHE.get("stage") != stage:
        _CACHE["nc"] = build(stage)
        _CACHE["stage"] = stage
        _CACHE["consts"] = host_consts()
    nc = _CACHE["nc"]
    if "consts_rev" not in _CACHE:
        _CACHE["consts_rev"] = host_consts(rev=True)
    f = lambda a: np.ascontiguousarray(np.asarray(a, dtype=np.float32))
    in_maps = []
    for core in range(8):
        b = core // 2
        rv = (core % 2 == 1) and stage >= 99
        cs = _CACHE["consts_rev"] if rv else _CACHE["consts"]
        sd = (lambda a: np.asarray(a)[0][::-1]) if rv else (lambda a: np.asarray(a)[0])
        xb = np.asarray(inputs["x"][b])
        cb = np.asarray(inputs["ctx"][b])
        if rv:
            xb = xb[::-1]
            cb = cb[::-1]
        m = {
            "x": f(xb), "c": f(inputs["c"][b]).reshape(8, 128), "ctx": f(cb),
            "c_ctx": f(inputs["c_ctx"]).reshape(8, 128),
            "mod_w": f(inputs["mod_w"]), "mod_b": f(inputs["mod_b"]).reshape(2, 48, 128),
            "norm_g": f(inputs["norm_g"]).reshape(2, 2, 8, 128),
            "ffn_w_gate": f(inputs["ffn_w_gate"]), "ffn_w_up": f(inputs["ffn_w_up"]), "ffn_w_down": f(inputs["ffn_w_down"]),
            "mix_w_in": f(inputs["mix_w_in"][0]), "mix_w_out": f(inputs["mix_w_out"][0]), "attn_sink": f(inputs["attn_sink"]).reshape(1, 8),
            "ssm_a_re": f(sd(inputs["ssm_a_re"])).reshape(128, 64), "ssm_a_im": f(sd(inputs["ssm_a_im"])).reshape(128, 64),
            "ssm_log_dt": f(sd(inputs["ssm_log_dt"])).reshape(1, 128),
            "ssm_b_re": f(sd(inputs["ssm_b_re"])), "ssm_b_im": f(sd(inputs["ssm_b_im"])),
            "ssm_c_re": f(sd(inputs["ssm_c_re"])), "ssm_c_im": f(sd(inputs["ssm_c_im"])),
            "ssm_d": f(inputs["ssm_d"][0]).reshape(8, 128), "ssm_glu_w": f(inputs["ssm_glu_w"][0]), "final_g": f(inputs["final_g"]).reshape(1, D),
        }
        m.update(cs)
        m["rk"] = np.array([[0]], np.int32)
        in_maps.append(m)
    res = run_bass_kernel_spmd(nc, in_maps, core_ids=list(range(8)))
    if stage < 99:
        return np.stack([np.asarray(res.results[2 * b]["out"], dtype=np.float32) for b in range(4)], axis=0)
    outp = np.stack([np.concatenate([np.asarray(res.results[2 * b]["out"], dtype=np.float32),
                                     np.asarray(res.results[2 * b + 1]["out"], dtype=np.float32)[::-1]], axis=0) for b in range(4)], axis=0)
    return outp
```
